# Optimizing a Trainium2 kernel written in Bass

```python
import math
import jax, jax.numpy as jnp
from jax import lax
import numpy as np


D_MODEL = 1024
BATCH = 4
SEQ = 4096
DEPTH = 2
DEC_BATCH = 8
DEC_SEQ = 2048
PAST_LEN = 128

GRID_W = 64
BLOCK = 128
EPS = 1e-6
ROPE_THETA = 10000.0

A_HEADS = 8
A_KV_HEADS = 2
A_HEAD_DIM = 64
A_GROUP = A_HEADS // A_KV_HEADS
A_WIDTH = A_HEADS * A_HEAD_DIM
A_KV_WIDTH = A_KV_HEADS * A_HEAD_DIM

B_HEADS = 4
B_KEY_DIM = 64
B_VAL_DIM = 128
RET_CHUNK = 128
B_QK_WIDTH = B_HEADS * B_KEY_DIM
B_V_WIDTH = B_HEADS * B_VAL_DIM

C_HEADS = 16
C_KV_HEADS = 2
C_HEAD_DIM = 64
C_GROUP = C_HEADS // C_KV_HEADS
C_WIDTH = C_HEADS * C_HEAD_DIM
C_KV_WIDTH = C_KV_HEADS * C_HEAD_DIM
WINDOW = 128
REL_BUCKETS = 32
REL_MAX_DIST = 128

AB_SPLITS = [A_WIDTH, A_KV_WIDTH, A_KV_WIDTH, A_WIDTH, B_QK_WIDTH, B_QK_WIDTH, B_V_WIDTH, B_V_WIDTH]
AB_IN = sum(AB_SPLITS)
AB_OUT = A_WIDTH + B_V_WIDTH
C_IN = 2 * C_WIDTH + 2 * C_KV_WIDTH
C_OUT = C_WIDTH
N_EVEN = (DEPTH + 1) // 2
N_ODD = DEPTH // 2

kernel_name = 'hybrid_axial_retention_window_encoder'

F32 = jnp.float32


def rmsnorm(x, g):
    xf = x.astype(F32)
    y = xf * lax.rsqrt(jnp.mean(xf * xf, axis=-1, keepdims=True) + EPS)
    if g is not None:
        y = y * g.astype(F32)
    return y.astype(x.dtype)


def rotate_pairs(x, ang):
    d = x.shape[-1]
    xf = x.astype(F32).reshape(*x.shape[:-1], d // 2, 2)
    cos = jnp.cos(ang)[:, None, :]
    sin = jnp.sin(ang)[:, None, :]
    x0, x1 = xf[..., 0], xf[..., 1]
    out = jnp.stack([x0 * cos - x1 * sin, x0 * sin + x1 * cos], axis=-1)
    return out.reshape(x.shape).astype(x.dtype)


def axial_angles(n):
    rows = n // GRID_W
    row = jnp.repeat(jnp.arange(rows, dtype=F32), GRID_W)
    col = jnp.tile(jnp.arange(GRID_W, dtype=F32), rows)
    quarter = A_HEAD_DIM // 4
    freqs = ROPE_THETA ** (-jnp.arange(quarter, dtype=F32) / quarter)
    return jnp.concatenate([row[:, None] * freqs, col[:, None] * freqs], axis=-1)


def linear_angles(n, d):
    half = d // 2
    freqs = ROPE_THETA ** (-jnp.arange(half, dtype=F32) / half)
    return jnp.arange(n, dtype=F32)[:, None] * freqs


def dense_gqa(q, k, v):
    b, n = q.shape[:2]
    nb = n // BLOCK
    qb = q.reshape(b, nb, BLOCK, *q.shape[2:]).swapaxes(0, 1)

    def attend(qblk):
        s = jnp.einsum('bqkgd,bskd->bkgqs', qblk, k).astype(F32)
        p = jax.nn.softmax(s, axis=-1).astype(v.dtype)
        return jnp.einsum('bkgqs,bskd->bqkgd', p, v)

    o = lax.map(attend, qb)
    return o.swapaxes(0, 1).reshape(b, n, -1)


def t5_bucket(rel):
    half = REL_BUCKETS // 2
    max_exact = half // 2
    ret = (rel > 0).astype(np.int32) * half
    dist = np.abs(rel)
    large = max_exact + (np.log(np.maximum(dist, 1) / max_exact) / np.log(REL_MAX_DIST / max_exact)
                         * (half - max_exact)).astype(np.int32)
    large = np.minimum(large, half - 1)
    return ret + np.where(dist < max_exact, dist, large)


def banded_sink_gqa(q, k, v, sink, rel_bias):
    b, n, kvh, grp, hd = q.shape
    nb = n // BLOCK
    pad = ((0, 0), (BLOCK, BLOCK), (0, 0), (0, 0))
    kp = jnp.pad(k, pad).reshape(b, nb + 2, BLOCK, kvh, hd)
    vp = jnp.pad(v, pad).reshape(b, nb + 2, BLOCK, kvh, hd)
    kwin = jnp.concatenate([kp[:, j:j + nb] for j in range(3)], axis=2)
    vwin = jnp.concatenate([vp[:, j:j + nb] for j in range(3)], axis=2)
    rel = np.arange(3 * BLOCK)[None, :] - BLOCK - np.arange(BLOCK)[:, None]
    in_win = jnp.asarray(np.abs(rel) <= WINDOW)
    bias = rel_bias[jnp.asarray(t5_bucket(rel))].astype(F32)
    bias = bias.transpose(2, 0, 1).reshape(kvh, grp, BLOCK, 3 * BLOCK)
    kpos = (jnp.arange(nb)[:, None] - 1) * BLOCK + jnp.arange(3 * BLOCK)[None, :]
    kvalid = (kpos >= 0) & (kpos < n)
    mask = in_win[None] & kvalid[:, None, :]
    sink_l = sink.reshape(kvh, grp)[..., None, None].astype(F32)
    qb = q.reshape(b, nb, BLOCK, kvh, grp, hd).swapaxes(0, 1)

    def attend(args):
        qblk, kblk, vblk, m = args
        s = jnp.einsum('bqkgd,bskd->bkgqs', qblk, kblk).astype(F32) + bias
        s = jnp.where(m, s, -jnp.inf)
        mx = jnp.maximum(jnp.max(s, axis=-1, keepdims=True), sink_l)
        e = jnp.exp(s - mx)
        denom = jnp.sum(e, axis=-1, keepdims=True) + jnp.exp(sink_l - mx)
        p = (e / denom).astype(vblk.dtype)
        return jnp.einsum('bkgqs,bskd->bqkgd', p, vblk)

    o = lax.map(attend, (qb, kwin.swapaxes(0, 1), vwin.swapaxes(0, 1), mask))
    return o.swapaxes(0, 1).reshape(b, n, -1)


def retention(q, k, v, log_gamma, inclusive):
    b, n, h, dk = q.shape
    dv = v.shape[-1]
    c = RET_CHUNK
    nc = n // c
    qc = q.reshape(b, nc, c, h, dk)
    kc = k.reshape(b, nc, c, h, dk)
    vc = v.reshape(b, nc, c, h, dv)
    idx = jnp.arange(c, dtype=F32)
    diff = idx[:, None] - idx[None, :]
    mask = (diff >= 0) if inclusive else (diff > 0)
    intra_decay = jnp.where(mask, jnp.exp(log_gamma[:, None, None] * jnp.where(mask, diff, 0.0)), 0.0)
    s = jnp.einsum('bnihd,bnjhd->bnhij', qc, kc) * intra_decay
    intra = jnp.einsum('bnhij,bnjhe->bnihe', s, vc)
    q_decay = jnp.exp((idx[:, None] + 1.0) * log_gamma[None, :])
    k_decay = jnp.exp((c - 1.0 - idx)[:, None] * log_gamma[None, :])
    chunk_decay = jnp.exp(c * log_gamma)
    kv = jnp.einsum('bnjhd,bnjhe->nbhde', kc * k_decay[:, :, None], vc)

    def step(state, kv_n):
        return state * chunk_decay[:, None, None] + kv_n, state

    _, states = lax.scan(step, jnp.zeros((b, h, dk, dv), F32), kv)
    cross = jnp.einsum('bnihd,nbhde->bnihe', qc * q_decay[:, :, None], states)
    return (intra + cross).reshape(b, n, h, dv)


def attn_retention_layer(x, g, w_in, qk_gain, ret_decay, w_out):
    b, n, _ = x.shape
    z = rmsnorm(x, g) @ w_in
    cuts = [int(i) for i in np.cumsum(AB_SPLITS)[:-1]]
    qa, ka, va, ga, qb, kb, vb, gb = jnp.split(z, cuts, axis=-1)
    ang = axial_angles(n)
    qa = rotate_pairs(rmsnorm(qa.reshape(b, n, A_HEADS, A_HEAD_DIM), qk_gain[0]), ang) * (A_HEAD_DIM ** -0.5)
    ka = rotate_pairs(rmsnorm(ka.reshape(b, n, A_KV_HEADS, A_HEAD_DIM), qk_gain[1]), ang)
    va = va.reshape(b, n, A_KV_HEADS, A_HEAD_DIM)
    oa = dense_gqa(qa.reshape(b, n, A_KV_HEADS, A_GROUP, A_HEAD_DIM), ka, va)
    ang_b = linear_angles(n, B_KEY_DIM)
    qb = rotate_pairs(qb.reshape(b, n, B_HEADS, B_KEY_DIM), ang_b).astype(F32)
    kb = rotate_pairs(kb.reshape(b, n, B_HEADS, B_KEY_DIM), ang_b).astype(F32) * (B_KEY_DIM ** -0.5)
    vb = vb.reshape(b, n, B_HEADS, B_VAL_DIM).astype(F32)
    log_gamma = -jnp.exp(ret_decay.astype(F32))
    fwd = retention(qb, kb, vb, log_gamma[0], True)
    bwd = jnp.flip(retention(jnp.flip(qb, 1), jnp.flip(kb, 1), jnp.flip(vb, 1), log_gamma[1], False), 1)
    ob = rmsnorm(fwd + bwd, None).reshape(b, n, B_V_WIDTH).astype(x.dtype)
    mixed = jnp.concatenate([jax.nn.silu(ga) * oa, jax.nn.silu(gb) * ob], axis=-1)
    return x + mixed @ w_out


def windowed_layer(x, g, w_in, sink, rel_bias, w_out):
    b, n, _ = x.shape
    z = rmsnorm(x, g) @ w_in
    q, k, v, gate = jnp.split(z, [C_WIDTH, C_WIDTH + C_KV_WIDTH, C_WIDTH + 2 * C_KV_WIDTH], axis=-1)
    q = q.reshape(b, n, C_KV_HEADS, C_GROUP, C_HEAD_DIM) * (C_HEAD_DIM ** -0.5)
    k = k.reshape(b, n, C_KV_HEADS, C_HEAD_DIM)
    v = v.reshape(b, n, C_KV_HEADS, C_HEAD_DIM)
    o = banded_sink_gqa(q, k, v, sink, rel_bias)
    return x + (jax.nn.silu(gate) * o) @ w_out


def trunk(x, norm_g, w_in_ab, qk_norm_a, ret_decay, w_out_ab, w_in_c, sink_c, w_out_c, rel_bias, final_norm):
    for layer in range(DEPTH):
        i = layer // 2
        if layer % 2 == 0:
            x = attn_retention_layer(x, norm_g[layer], w_in_ab[i], qk_norm_a[i], ret_decay[i], w_out_ab[i])
        else:
            x = windowed_layer(x, norm_g[layer], w_in_c[i], sink_c[i], rel_bias, w_out_c[i])
    return rmsnorm(x, final_norm)


def setup_inputs(seed: int = 0) -> dict:
    key = jax.random.key(seed)
    ks = jax.random.split(key, 12)
    nrm = jax.random.normal
    base_decay = np.log(-np.log(1.0 - 2.0 ** (-5.0 - np.arange(B_HEADS)))).astype(np.float32)
    return {
        'x_prompt': nrm(ks[0], (BATCH, SEQ, D_MODEL), F32),
        'x_sample': nrm(ks[1], (DEC_BATCH, DEC_SEQ, D_MODEL), F32),
        'norm_g': 1.0 + 0.02 * nrm(ks[2], (DEPTH, D_MODEL), F32),
        'w_in_ab': nrm(ks[3], (N_EVEN, D_MODEL, AB_IN), F32) * (D_MODEL ** -0.5),
        'qk_norm_a': 1.0 + 0.02 * nrm(ks[4], (N_EVEN, 2, A_HEAD_DIM), F32),
        'ret_decay': jnp.asarray(base_decay) + 0.05 * nrm(ks[5], (N_EVEN, 2, B_HEADS), F32),
        'w_out_ab': nrm(ks[6], (N_EVEN, AB_OUT, D_MODEL), F32) * (AB_OUT ** -0.5),
        'w_in_c': nrm(ks[7], (N_ODD, D_MODEL, C_IN), F32) * (D_MODEL ** -0.5),
        'sink_c': 0.5 * nrm(ks[8], (N_ODD, C_HEADS), F32),
        'w_out_c': nrm(ks[9], (N_ODD, C_OUT, D_MODEL), F32) * (C_OUT ** -0.5),
        'rel_bias': 0.5 * nrm(ks[10], (REL_BUCKETS, C_HEADS), F32),
        'final_norm': 1.0 + 0.02 * nrm(ks[11], (D_MODEL,), F32),
    }


def reference(x_prompt, x_sample, norm_g, w_in_ab, qk_norm_a, ret_decay, w_out_ab, w_in_c, sink_c, w_out_c, rel_bias, final_norm):
    y_prompt = trunk(x_prompt, norm_g, w_in_ab, qk_norm_a, ret_decay, w_out_ab, w_in_c, sink_c, w_out_c, rel_bias, final_norm)
    y_sample = trunk(x_sample, norm_g, w_in_ab, qk_norm_a, ret_decay, w_out_ab, w_in_c, sink_c, w_out_c, rel_bias, final_norm)
    return (y_prompt, y_sample)
```

```python
import numpy as np
import concourse.bass as bass
import concourse.mybir as mybir
from concourse.bass_utils import run_bass_kernel_spmd

F32 = mybir.dt.float32
BF16 = mybir.dt.bfloat16
AF = mybir.ActivationFunctionType
ALU = mybir.AluOpType

NT = 4096
NB = 32
CH = 512
NCH = 8
EPS = 1e-6
NEG = -30000.0


class Prod:
    def __init__(self, sem, inc):
        self.sem = sem
        self.inc = inc
        self.cnt = 0


class Res:
    def __init__(self):
        self.w = {}
        self.r = {}
        self.excl = False


class V:
    def __init__(self, ap, res=None):
        self.ap = ap
        self.res = res if res is not None else Res()

    def __getitem__(self, k):
        return V(self.ap[k], self.res)

    def re(self, pat, **kw):
        return V(self.ap.rearrange(pat, **kw), self.res)

    def bc(self, shape):
        return V(self.ap.to_broadcast(shape), self.res)


class Ker:
    def __init__(self, nc):
        self.nc = nc
        self.eng = {"pe": nc.tensor, "act": nc.scalar, "dve": nc.vector, "pool": nc.gpsimd, "sp": nc.sync}
        self.prod = {}
        for n in ("pe", "act", "dve", "pool"):
            self.prod[n] = Prod(nc.alloc_semaphore("s_" + n), 1)
        self.seen = {n: {} for n in self.eng}
        self.nslot = 0
        self.ninstr = 0

    def slot(self):
        self.nslot += 1
        p = Prod(self.nc.alloc_semaphore("d%d" % self.nslot), 16)
        if hasattr(self, "slots"):
            self.slots.append(p)
        return p

    def _wait(self, en, reads, writes):
        deps = {}
        for v in reads:
            for p, i in v.res.w.items():
                deps[p] = max(deps.get(p, 0), i)
        for v in writes:
            for p, i in v.res.w.items():
                deps[p] = max(deps.get(p, 0), i)
            for p, i in v.res.r.items():
                deps[p] = max(deps.get(p, 0), i)
        e = self.eng[en]
        seen = self.seen[en]
        own = self.prod.get(en)
        for p, i in deps.items():
            if p is own and en == "pe":
                continue
            if seen.get(p, 0) >= i:
                continue
            e.wait_ge(p.sem, i)
            seen[p] = i

    def op(self, en, fn, reads, writes):
        writes = list(writes) + [r for r in reads if r.res.excl]
        self._wait(en, reads, writes)
        ins = fn(self.eng[en])
        p = self.prod[en]
        p.cnt += 1
        ins.then_inc(p.sem, 1)
        for v in reads:
            v.res.r[p] = p.cnt
        for v in writes:
            v.res.w[p] = p.cnt
        self.ninstr += 1

    def dma(self, q, out, in_, slot):
        self._wait(q, [in_], [out])
        ins = self.eng[q].dma_start(out=out.ap, in_=in_.ap)
        slot.cnt += 16
        ins.then_inc(slot.sem, 16)
        in_.res.r[slot] = slot.cnt
        out.res.w[slot] = slot.cnt

    def mm(self, out, lhsT, rhs, start=True, stop=True, tp=None):
        kw = {}
        if tp is not None:
            kw["tile_position"] = tp
        self.op("pe", lambda e: e.matmul(out.ap, lhsT.ap, rhs.ap, start=start, stop=stop, **kw), [lhsT, rhs], [out])

    def tr(self, out, in_, ident):
        self.op("pe", lambda e: e.transpose(out.ap, in_.ap, ident.ap), [in_, ident], [out])

    def act(self, out, in_, func, bias=None, scale=1.0, accum=None):
        reads = [in_]
        kw = {}
        if bias is not None:
            if isinstance(bias, V):
                reads.append(bias)
                kw["bias"] = bias.ap
            else:
                kw["bias"] = bias
        if isinstance(scale, V):
            reads.append(scale)
            kw["scale"] = scale.ap
        else:
            kw["scale"] = scale
        writes = [out]
        if accum is not None:
            writes.append(accum)
            kw["accum_out"] = accum.ap
        self.op("act", lambda e: e.activation(out.ap, in_.ap, func, **kw), reads, writes)

    def tt(self, en, out, a, b, op):
        self.op(en, lambda e: e.tensor_tensor(out.ap, a.ap, b.ap, op), [a, b], [out])

    def stt(self, en, out, in0, scalar, in1, op0, op1):
        reads = [in0, in1]
        s = scalar
        if isinstance(scalar, V):
            reads.append(scalar)
            s = scalar.ap
        self.op(en, lambda e: e.scalar_tensor_tensor(out.ap, in0.ap, s, in1.ap, op0, op1), reads, [out])

    def ts(self, en, out, in0, s1, op0, s2=None, op1=None):
        reads = [in0]
        a1 = s1
        if isinstance(s1, V):
            reads.append(s1)
            a1 = s1.ap
        a2 = s2
        if isinstance(s2, V):
            reads.append(s2)
            a2 = s2.ap
        if op1 is None:
            self.op(en, lambda e: e.tensor_scalar(out.ap, in0.ap, a1, None, op0), reads, [out])
        else:
            self.op(en, lambda e: e.tensor_scalar(out.ap, in0.ap, a1, a2, op0, op1), reads, [out])

    def cp(self, en, out, in_):
        if en == "act":
            self.op("act", lambda e: e.copy(out.ap, in_.ap), [in_], [out])
        else:
            self.op(en, lambda e: e.tensor_copy(out.ap, in_.ap), [in_], [out])

    def recip(self, out, in_):
        self.op("dve", lambda e: e.reciprocal(out.ap, in_.ap), [in_], [out])

    def memset(self, en, out, val):
        self.op(en, lambda e: e.memset(out.ap, val), [], [out])


class StopBuild(Exception):
    pass


import contextlib


class Scope(contextlib.ExitStack):
    def __init__(self, K):
        super().__init__()
        self.K = K
        self.tiles = []

    def __exit__(self, *a):
        fr = self.K.freed
        for v in self.tiles:
            for d in (v.res.w, v.res.r):
                for p, i in d.items():
                    fr[p] = max(fr.get(p, 0), i)
        self.tiles = []
        return super().__exit__(*a)

    def close(self):
        self.__exit__(None, None, None)


def build_program(stop=None, taps=()):
    nc = bass.Bass("TRN2", target_bir_lowering=False)
    K = Ker(nc)
    K.slots = []
    K.freed = {}
    K.tapped = {}

    def checkpoint(name):
        if stop == name:
            raise StopBuild()

    def tap(name, v, shape, dt=F32):
        if name not in taps or name in K.tapped:
            return
        d = V(nc.dram_tensor("dbg_" + name, list(shape), dt, kind="ExternalOutput").ap())
        K.tapped[name] = d
        K.dma("sp", d, v, K.slot())

    def din(name, shape, dt=F32):
        return V(nc.dram_tensor(name, list(shape), dt, kind="ExternalInput").ap())

    x_d = din("x", [NT, 1024])
    wG_d = din("wG", [1024, 1664])
    wLB_d = din("wLB", [1024, 2048])
    wLA_d = din("wLA", [1024, 1536])
    woab_d = din("woab", [1024, 1024])
    wG1_d = din("wG1", [1024, 384])
    wL1_d = din("wL1", [1024, 2048])
    woc_d = din("woc", [1024, 1024])
    gcol_d = din("gcol", [128, 16])
    fn_d = din("fn", [1, 1024])
    gqk_d = din("gqk", [128, 4])
    rdec_d = din("rdec", [128, 12])
    sink_d = din("sinkl", [128, 8])
    relb_d = din("relb", [32, 16])
    ident_d = din("ident", [128, 128])
    onesblk_d = din("onesblk", [128, 128])
    mm_d = din("mmat", [128, 4, 128])
    iot_d = din("iot", [128, 4, 512])
    oh_d = din("oh", [32, 640])
    inwin_d = din("inwin", [16, 640])
    tabA_d = din("tabA", [2, 128, NT])
    tabB_d = din("tabB", [2, 128, NT])
    maskA_d = din("maskA", [128, 256])
    maskW_d = din("maskW", [128, 96])
    rfb_d = din("rfb", [128, 64])
    y_d = V(nc.dram_tensor("y", [NT, 1024], F32, kind="ExternalOutput").ap())
    x1_d = V(nc.dram_tensor("x1s", [NT, 1024], F32, kind="Internal").ap())
    mixb_d = V(nc.dram_tensor("mixbs", [4, 128, NT], BF16, kind="Internal").ap())
    vec_d = V(nc.dram_tensor("vecs", [16, 640], F32, kind="Internal").ap())

    es = Scope(K)
    uid = [0]

    def sb(name, shape, dt=F32, stack=None):
        uid[0] += 1
        st_ = stack if stack is not None else es
        t = st_.enter_context(nc.sbuf_tensor("sb%d_%s" % (uid[0], name), list(shape), dt))
        v = V(t[:])
        v.res.w = dict(K.freed)
        st_.tiles.append(v)
        return v

    def ps(name, shape, dt=F32, stack=None):
        uid[0] += 1
        st_ = stack if stack is not None else es
        t = st_.enter_context(nc.psum_tensor("ps%d_%s" % (uid[0], name), list(shape), dt))
        v = V(t[:])
        v.res.excl = True
        v.res.w = dict(K.freed)
        st_.tiles.append(v)
        return v

    try:
        with es:
            cslot = K.slot()
            consts = []

            def cload(name, src, shape, dt=F32, q="sp"):
                t = sb(name, shape, dt)
                K.dma(q, t, src, cslot)
                consts.append(t)
                return t

            gcol = cload("gcol", gcol_d, [128, 16])
            fnbc = sb("fnbc", [128, 1024])
            K.dma("sp", fnbc, V(fn_d.ap.to_broadcast([128, 1024]), fn_d.res), cslot)
            consts.append(fnbc)
            gqk = cload("gqk", gqk_d, [128, 4])
            rdec = cload("rdec", rdec_d, [128, 12])
            sinkl = cload("sinkl", sink_d, [128, 8])
            maskA = cload("maskA", maskA_d, [128, 256])
            maskW = cload("maskW", maskW_d, [128, 96])
            rfb = cload("rfb", rfb_d, [128, 64])
            ident32 = cload("ident32", ident_d, [128, 128])
            onesblk32 = cload("onesblk32", onesblk_d, [128, 128])
            iot = cload("iot", iot_d, [128, 4, CH])
            for c in consts:
                c.res.w[cslot] = cslot.cnt
            ident = sb("ident", [128, 128], BF16)
            onesblk = sb("onesblk", [128, 128], BF16)
            ones = sb("ones", [128, 128], BF16)
            epsb = sb("epsb", [128, 1])
            K.cp("dve", ident, ident32)
            K.cp("dve", onesblk, onesblk32)
            K.memset("dve", ones, 1.0)
            K.memset("dve", epsb, EPS)

            checkpoint("c0")
            wbf = sb("wbf", [128, 8, 2048], BF16)
            wobf = sb("wobf", [128, 8, 1024], BF16)
            wst = [sb("wst%d" % i, [128, 1024]) for i in range(2)]
            wst_slot = [K.slot() for _ in range(2)]
            xch = sb("xch", [128, 4, 1024])
            xch_slot = [K.slot() for _ in range(4)]
            xn = [sb("xn%d" % i, [128, 1024], BF16) for i in range(2)]
            junk = sb("junk", [128, 1024], BF16)
            xnT = sb("xnT", [128, 8, CH], BF16)
            ss = sb("ss", [128, 4])
            lnv4 = sb("lnv4", [128, 4])
            rstd4 = sb("rstd4", [128, 4])
            wcount = [0]

            def load_w(dst, src_d, ncols, layer_g):
                for kc in range(8):
                    for c0 in range(0, ncols, 1024):
                        c1 = min(ncols, c0 + 1024)
                        i = wcount[0] % 2
                        wcount[0] += 1
                        K.dma("sp", wst[i][:, 0:c1 - c0], src_d[kc * 128:(kc + 1) * 128, c0:c1], wst_slot[i])
                        if layer_g is None:
                            K.cp("pool", dst[:, kc, c0:c1], wst[i][:, 0:c1 - c0])
                        else:
                            K.ts("pool", dst[:, kc, c0:c1], wst[i][:, 0:c1 - c0], gcol[:, layer_g * 8 + kc:layer_g * 8 + kc + 1], ALU.mult)

            def make_xnT(src_d, C, pT):
                for b in range(4):
                    r0 = C * CH + b * 128
                    K.dma("sp", xch[:, b, :], src_d[r0:r0 + 128, :], xch_slot[b])
                    K.act(junk, xch[:, b, :], AF.Square, accum=ss[:, b:b + 1])
                K.act(lnv4, ss, AF.Ln, bias=epsb[:, 0:1], scale=1.0 / 1024.0)
                K.act(rstd4, lnv4, AF.Exp, scale=-0.5)
                for b in range(4):
                    xb = xn[b % 2]
                    K.ts("dve", xb, xch[:, b, :], rstd4[:, b:b + 1], ALU.mult)
                    for kc in range(8):
                        K.tr(pT[:, kc, :], xb[:, kc * 128:(kc + 1) * 128], ident)
                    K.cp("act", xnT[:, :, b * 128:(b + 1) * 128], pT)

            def proj(dst, c0):
                for kc in range(8):
                    K.mm(dst, wbf[:, kc, c0:c0 + 128], xnT[:, kc, :], start=(kc == 0), stop=(kc == 7))

            def rsq_bcast(dst, src_ps, nfeat, sq, psn, lnv, lhs_ones):
                K.act(sq, src_ps, AF.Square)
                K.mm(psn, lhs_ones, sq)
                K.act(lnv, psn, AF.Ln, bias=epsb[:, 0:1], scale=1.0 / nfeat)
                K.act(dst, lnv, AF.Exp, scale=-0.5)

            with Scope(K) as L0:
                LR = Scope(K)
                KaT = sb("KaT", [128, 2, NT], BF16, L0)
                Va = sb("Va", [128, NB, 128], BF16, L0)
                tabc = sb("tabc", [128, 2, CH], F32, L0)
                tab_slot = K.slot()
                tabd_slot = K.slot()
                sq = sb("sq", [128, CH], BF16, L0)
                lnv = sb("lnv", [128, CH], F32, L0)
                rs = sb("rs", [128, CH], F32, L0)
                t1 = sb("t1", [128, CH], F32, L0)
                t2 = sb("t2", [128, CH], F32, L0)
                tabg = sb("tabg", [128, 2, CH], F32, L0)
                SbAll = sb("SbAll", [128, 2, NB, 128], BF16, LR)
                tabd = sb("tabd", [128, 2, CH], F32, LR)
                vbtm = sb("vbtm", [128, 4, 512], BF16, LR)
                lg = sb("lg", [128, 12], F32, LR)
                K.act(lg, rdec, AF.Exp)
                K.ts("dve", lg, lg, -1.0, ALU.mult)
                cd = sb("cd", [128, 4], F32, LR)
                K.act(cd, lg[:, 0:4], AF.Exp, scale=128.0)
                cdr = sb("cdr", [128, 4, NB], F32, LR)
                for j in range(4):
                    off = 0 if j < 2 else 32
                    K.ts("dve", cdr[:, j, :], rfb[:, off:off + 32], cd[:, j:j + 1], ALU.mult)
                checkpoint("c1")
                QF4 = sb("QF4", [128, 2, CH], F32, LR)
                QB4 = sb("QB4", [128, 2, CH], F32, LR)
                KF4 = sb("KF4", [128, 2, CH], F32, LR)
                KB4 = sb("KB4", [128, 2, CH], F32, LR)
                for p in range(2):
                    K.act(QF4[:, p, :], iot[:, 0, :], AF.Exp, scale=lg[:, p:p + 1])
                    K.act(QB4[:, p, :], iot[:, 1, :], AF.Exp, scale=lg[:, 2 + p:3 + p])
                    K.act(KF4[:, p, :], iot[:, 2, :], AF.Exp, scale=lg[:, p:p + 1])
                    K.act(KB4[:, p, :], iot[:, 3, :], AF.Exp, scale=lg[:, 2 + p:3 + p])
                K.ts("dve", KF4, KF4, 0.125, ALU.mult)
                K.ts("dve", KB4, KB4, 0.125, ALU.mult)
                checkpoint("c2")
                DT = sb("DT", [128, 4, 128], F32, LR)
                with Scope(K) as S0:
                    mmat = sb("mmat", [128, 4, 128], F32, S0)
                    K.dma("sp", mmat, mm_d, tab_slot)
                    d1 = sb("d1", [128, 128], F32, S0)
                    d2 = sb("d2", [128, 128], F32, S0)
                    for h in range(4):
                        checkpoint("d0")
                        K.act(d1, mmat[:, 0, :], AF.Exp, scale=lg[:, 4 + h:5 + h])
                        checkpoint("d1")
                        K.tt("dve", d1, d1, mmat[:, 1, :], ALU.mult)
                        checkpoint("d2")
                        K.act(d2, mmat[:, 2, :], AF.Exp, scale=lg[:, 8 + h:9 + h])
                        K.tt("dve", d2, d2, mmat[:, 3, :], ALU.mult)
                        K.tt("dve", d1, d1, d2, ALU.add)
                        checkpoint("d3")
                        K.ts("dve", DT[:, h, :], d1, 0.125, ALU.mult)
                        checkpoint("d4")

                def load_tab(dst, slot, src_d, C):
                    K.dma("sp", dst, V(src_d.ap[:, :, C * CH:(C + 1) * CH].rearrange("t p c -> p t c"), src_d.res), slot)

                def rope(psa, psb, tab, out32, ga=None, gb=None):
                    if ga is None:
                        K.tt("dve", t1, psa, tab[:, 0, :], ALU.mult)
                        K.tt("dve", t2, psb, tab[:, 1, :], ALU.mult)
                    else:
                        K.ts("pool", tabg[:, 0, :], tab[:, 0, :], ga, ALU.mult)
                        K.ts("pool", tabg[:, 1, :], tab[:, 1, :], gb, ALU.mult)
                        K.tt("dve", t1, psa, tabg[:, 0, :], ALU.mult)
                        K.tt("dve", t2, psb, tabg[:, 1, :], ALU.mult)
                    K.tt("pool", out32, t1, t2, ALU.add)

                checkpoint("setup0")
                load_w(wbf, wG_d, 1664, 0)
                with Scope(K) as PG:
                    pT = ps("pT", [128, 8, 128], BF16, PG)
                    pa = ps("pa", [128, CH], F32, PG)
                    pb = ps("pb", [128, CH], F32, PG)
                    pn = ps("pn", [128, CH], F32, PG)
                    pk = ps("pk", [128, 4, 2, 128], BF16, PG)
                    pva = ps("pva", [128, 512], F32, PG)[:, 0:128]
                    pvb = ps("pvb", [128, 512], F32, PG)
                    pkv = ps("pkv", [128, 4, 128], F32, PG)[:, 0:2, :]
                    kdbT = sb("kdbT", [128, 2, CH], BF16, PG)
                    kdbtm = sb("kdbtm", [128, 4, 2, 128], BF16, PG)
                    Rb = sb("Rb", [128, 2, 128], F32, PG)
                    K.memset("dve", Rb, 0.0)
                    checkpoint("g_w")
                    for C in range(NCH - 1, -1, -1):
                        make_xnT(x_d, C, pT)
                        checkpoint("g_x"); checkpoint("G%d_x" % C)
                        load_tab(tabc, tab_slot, tabA_d, C)
                        load_tab(tabd, tabd_slot, tabB_d, C)
                        checkpoint("g_t"); checkpoint("G%d_t" % C)
                        for t in range(2):
                            proj(pa, t * 128)
                            proj(pb, 256 + t * 128)
                            checkpoint("k0"); checkpoint("G%d_%d_k0" % (C, t))
                            rsq_bcast(rs, pa, 64.0, sq, pn, lnv, onesblk)
                            checkpoint("k1"); checkpoint("G%d_%d_k1" % (C, t))
                            rope(pa, pb, tabc, t1, gqk[:, 2:3], gqk[:, 3:4])
                            checkpoint("k2"); checkpoint("G%d_%d_k2" % (C, t))
                            K.tt("pool", KaT[:, t, C * CH:(C + 1) * CH], t1, rs, ALU.mult)
                            checkpoint("k3"); checkpoint("G%d_%d_k3" % (C, t))
                        checkpoint("g_ka"); checkpoint("G%d_ka" % C)
                        for t in range(2):
                            proj(pa, 512 + t * 128)
                            proj(pb, 768 + t * 128)
                            rope(pa, pb, tabd, t1)
                            K.tt("pool", kdbT[:, t, :], t1, KB4[:, t, :], ALU.mult)
                            for cj in range(4):
                                K.tr(pk[:, cj, t, :], kdbT[:, t, cj * 128:(cj + 1) * 128], ident)
                        K.cp("act", kdbtm, pk)
                        checkpoint("g_kb"); checkpoint("G%d_kb" % C)
                        for b in range(4):
                            for kc in range(8):
                                K.mm(pva, xnT[:, kc, b * 128:(b + 1) * 128], wbf[:, kc, 1024:1152], start=(kc == 0), stop=(kc == 7))
                            for kc in range(8):
                                K.mm(pvb, xnT[:, kc, b * 128:(b + 1) * 128], wbf[:, kc, 1152:1664], start=(kc == 0), stop=(kc == 7))
                            K.cp("act", Va[:, C * 4 + b, :], pva)
                            K.cp("dve", vbtm[:, b, :], pvb)
                        checkpoint("g_v"); checkpoint("G%d_v" % C)
                        for cj in range(3, -1, -1):
                            n = C * 4 + cj
                            for p in range(2):
                                K.mm(pkv[0:64, p, :], kdbtm[:, cj, p, 0:64], vbtm[:, cj, (2 * p) * 128:(2 * p + 1) * 128])
                                K.mm(pkv[64:128, p, :], kdbtm[:, cj, p, 64:128], vbtm[:, cj, (2 * p + 1) * 128:(2 * p + 2) * 128], tp=(0, 64))
                            checkpoint("s0")
                            K.ts("dve", SbAll[:, :, n, :], Rb, rfb[:, 32 + n:33 + n], ALU.mult)
                            checkpoint("s1")
                            for p in range(2):
                                K.ts("dve", Rb[:, p, :], Rb[:, p, :], cdr[:, 2 + p, n:n + 1], ALU.mult)
                                checkpoint("s2")
                                K.tt("dve", Rb[:, p, :], pkv[:, p, :], Rb[:, p, :], ALU.add)
                                checkpoint("s3")
                            checkpoint("s4")
                        checkpoint("g_c1"); checkpoint("G%d_end" % C)

                tap("KaT", KaT, [128, 2, NT], BF16)
                tap("Va", Va, [128, NB, 128], BF16)
                tap("SbAll", SbAll, [128, 2, NB, 128], BF16)
                checkpoint("G")
                load_w(wbf, wLB_d, 2048, 0)
                with Scope(K) as PB:
                    pT = ps("pT", [128, 8, 128], BF16, PB)
                    pa = ps("pa", [128, CH], F32, PB)
                    pb = ps("pb", [128, CH], F32, PB)
                    pk = pT.re("p (c t) q -> p c t q", t=2)
                    pss = ps("pss", [128, 512], F32, PB)
                    po = ps("po", [128, 4, CH], F32, PB)
                    qrT = sb("qrT", [128, 2, CH], BF16, PB)
                    qdf = sb("qdf", [128, 2, CH], BF16, PB)
                    qdb = sb("qdb", [128, 2, CH], BF16, PB)
                    krT = sb("krT", [128, 2, CH], BF16, PB)
                    kdfT = sb("kdfT", [128, 2, CH], BF16, PB)
                    kdftm = sb("kdftm", [128, 4, 2, 128], BF16, PB)
                    sg = sb("sg", [128, 4, CH], BF16, PB)
                    AT = sb("AT", [128, 4, 128], BF16, PB)
                    Sf = sb("Sf", [128, 2, 128], BF16, PB)
                    Rf = sb("Rf", [128, 2, 128], F32, PB)
                    mixBc = [sb("mixBc%d" % i, [128, 4, CH], BF16, PB) for i in range(1)]
                    mixB_slot = [K.slot() for _ in range(1)]
                    K.memset("dve", Rf, 0.0)
                    for C in range(NCH):
                        make_xnT(x_d, C, pT)
                        load_tab(tabd, tabd_slot, tabB_d, C)
                        for t in range(2):
                            proj(pa, t * 128)
                            proj(pb, 256 + t * 128)
                            rope(pa, pb, tabd, t1)
                            K.cp("act", qrT[:, t, :], t1)
                            K.tt("pool", qdf[:, t, :], t1, QF4[:, t, :], ALU.mult)
                            K.tt("pool", qdb[:, t, :], t1, QB4[:, t, :], ALU.mult)
                        for t in range(2):
                            proj(pa, 512 + t * 128)
                            proj(pb, 768 + t * 128)
                            rope(pa, pb, tabd, t1)
                            K.cp("act", krT[:, t, :], t1)
                            K.tt("pool", kdfT[:, t, :], t1, KF4[:, t, :], ALU.mult)
                            for cj in range(4):
                                K.tr(pk[:, cj, t, :], kdfT[:, t, cj * 128:(cj + 1) * 128], ident)
                        K.cp("act", kdftm, pk)
                        for h in range(4):
                            proj(pa, 1024 + h * 128)
                            K.act(sg[:, h, :], pa, AF.Silu)
                        for b in range(4):
                            for kc in range(8):
                                K.mm(pb, xnT[:, kc, b * 128:(b + 1) * 128], wbf[:, kc, 1536:2048], start=(kc == 0), stop=(kc == 7))
                            K.cp("dve", vbtm[:, b, :], pb)
                        for cj in range(4):
                            n = C * 4 + cj
                            cs = slice(cj * 128, (cj + 1) * 128)
                            K.ts("dve", Sf, Rf, rfb[:, n:n + 1], ALU.mult)
                            for p in range(2):
                                K.mm(pa[0:64, p * 128:(p + 1) * 128], kdftm[:, cj, p, 0:64], vbtm[:, cj, (2 * p) * 128:(2 * p + 1) * 128])
                                K.mm(pa[64:128, p * 128:(p + 1) * 128], kdftm[:, cj, p, 64:128], vbtm[:, cj, (2 * p + 1) * 128:(2 * p + 2) * 128], tp=(0, 64))
                            for p in range(2):
                                K.ts("dve", Rf[:, p, :], Rf[:, p, :], cdr[:, p, n:n + 1], ALU.mult)
                                K.tt("dve", Rf[:, p, :], pa[:, p * 128:(p + 1) * 128], Rf[:, p, :], ALU.add)
                            for h in range(4):
                                t, r0 = h // 2, (h % 2) * 64
                                pdst = pss if (h % 2 == 0) else pb
                                K.mm(pdst[:, t * 128:(t + 1) * 128], krT[r0:r0 + 64, t, cs], qrT[r0:r0 + 64, t, cs])
                            ATv = AT.re("p (t hp) i -> p hp t i", hp=2)
                            DTv = DT.re("p (t hp) i -> p hp t i", hp=2)
                            K.tt("dve", ATv[:, 0, :, :], pss[:, 0:256].re("p (t i) -> p t i", t=2), DTv[:, 0, :, :], ALU.mult)
                            K.tt("dve", ATv[:, 1, :, :], pb[:, 0:256].re("p (t i) -> p t i", t=2), DTv[:, 1, :, :], ALU.mult)
                            for h in range(4):
                                t, r0 = h // 2, (h % 2) * 64
                                K.mm(po[:, h, cs], vbtm[:, cj, h * 128:(h + 1) * 128], AT[:, h, :], start=True, stop=False)
                                K.mm(po[:, h, cs], Sf[r0:r0 + 64, t, :], qdf[r0:r0 + 64, t, cs], start=False, stop=False)
                                K.mm(po[:, h, cs], SbAll[r0:r0 + 64, t, n, :], qdb[r0:r0 + 64, t, cs], start=False, stop=True)
                        mb = mixBc[0]
                        for h in range(4):
                            rsq_bcast(rs, po[:, h, :], 128.0, sq, pb, lnv, ones)
                            K.tt("dve", t1, po[:, h, :], rs, ALU.mult)
                            K.tt("pool", mb[:, h, :], t1, sg[:, h, :], ALU.mult)
                        K.dma("pool", V(mixb_d.ap[:, :, C * CH:(C + 1) * CH].rearrange("h p c -> p h c"), mixb_d.res), mb, mixB_slot[0])

                tap("mixb", mixb_d, [4, 128, NT], BF16)
                checkpoint("LB")
                LR.close()
                load_w(wbf, wLA_d, 1536, 0)
                load_w(wobf, woab_d, 1024, None)
                with Scope(K) as PA:
                    pT = ps("pT", [128, 8, 128], BF16, PA)
                    pa = ps("pa", [128, CH], F32, PA)
                    pb = ps("pb", [128, CH], F32, PA)
                    psc = [ps("psc%d" % i, [128, 2, CH], F32, PA) for i in range(2)]
                    pnum = ps("pnum", [128, CH], F32, PA)
                    pden = pb
                    qaT = sb("qaT", [128, 4, CH], BF16, PA)
                    sga = sb("sga", [128, 4, CH], BF16, PA)
                    mixA = sb("mixA", [128, 4, CH], BF16, PA)
                    mixBl = sb("mixBl", [128, 4, CH], BF16, PA)
                    mixBl_slot = K.slot()
                    pTs = [sb("pTs%d" % i, [128, 2, CH], BF16, PA) for i in range(2)]
                    x1b = [sb("x1b%d" % i, [128, 1024], F32, PA) for i in range(2)]
                    x1b_slot = [K.slot() for _ in range(2)]
                    it = 0
                    for C in range(NCH):
                        make_xnT(x_d, C, pT)
                        load_tab(tabc, tab_slot, tabA_d, C)
                        K.dma("pool", mixBl, V(mixb_d.ap[:, :, C * CH:(C + 1) * CH].rearrange("h p c -> p h c"), mixb_d.res), mixBl_slot)
                        for t in range(4):
                            proj(pa, t * 128)
                            proj(pb, 512 + t * 128)
                            rsq_bcast(rs, pa, 64.0, sq, pnum, lnv, onesblk)
                            rope(pa, pb, tabc, t1, gqk[:, 0:1], gqk[:, 1:2])
                            K.tt("pool", qaT[:, t, :], t1, rs, ALU.mult)
                        for t in range(4):
                            proj(pa, 1024 + t * 128)
                            K.act(sga[:, t, :], pa, AF.Silu)
                        for t in range(4):
                            kv = t // 2
                            for kb in range(NB):
                                sc = psc[it % 2]
                                pt = pTs[it % 2]
                                it += 1
                                ks = slice(kb * 128, (kb + 1) * 128)
                                K.mm(sc[:, 0, :], KaT[0:64, kv, ks], qaT[0:64, t, :])
                                K.mm(sc[:, 1, :], KaT[64:128, kv, ks], qaT[64:128, t, :])
                                K.act(pt, sc, AF.Exp, bias=maskA[:, C * NB + kb:C * NB + kb + 1], scale=0.125)
                                st, sp_ = (kb == 0), (kb == NB - 1)
                                K.mm(pnum[0:64, :], Va[:, kb, kv * 64:(kv + 1) * 64], pt[:, 0, :], start=st, stop=sp_)
                                K.mm(pnum[64:128, :], Va[:, kb, kv * 64:(kv + 1) * 64], pt[:, 1, :], start=st, stop=sp_, tp=(0, 64))
                                K.mm(pden[0:64, :], ones[:, 0:64], pt[:, 0, :], start=st, stop=sp_)
                                K.mm(pden[64:128, :], ones[:, 0:64], pt[:, 1, :], start=st, stop=sp_, tp=(0, 64))
                            K.recip(rs, pden)
                            K.tt("dve", t1, pnum, rs, ALU.mult)
                            K.tt("pool", mixA[:, t, :], t1, sga[:, t, :], ALU.mult)
                        for b in range(4):
                            bs = slice(b * 128, (b + 1) * 128)
                            py = psc[b % 2]
                            for half in range(2):
                                for f in range(8):
                                    src = mixA[:, f, bs] if f < 4 else mixBl[:, f - 4, bs]
                                    K.mm(py[:, half, :], src, wobf[:, f, half * 512:(half + 1) * 512], start=(f == 0), stop=(f == 7))
                            xo = x1b[b % 2]
                            K.tt("dve", xo, py.re("p a c -> p (a c)"), xch[:, b, :], ALU.add)
                            r0 = C * CH + b * 128
                            K.dma("pool", x1_d[r0:r0 + 128, :], xo, x1b_slot[b % 2])

            tap("x1", x1_d, [NT, 1024], F32)
            checkpoint("LA")
            with Scope(K) as L1:
                KcT = sb("KcT", [128, 2, (NB + 2) * 128], BF16, L1)
                Vc = sb("Vc", [128, NB + 2, 128], BF16, L1)
                EBT = sb("EBT", [128, 16, 3, 128], BF16, L1)
                esk = sb("esk", [128, 8], F32, L1)
                K.act(esk, sinkl, AF.Exp)
                K.memset("pool", KcT[:, :, 0:128], 0.0)
                K.memset("pool", KcT[:, :, (NB + 1) * 128:(NB + 2) * 128], 0.0)
                K.memset("pool", Vc[:, 0, :], 0.0)
                K.memset("pool", Vc[:, NB + 1, :], 0.0)
                with Scope(K) as S1:
                    relb = sb("relb", [32, 16], F32, S1)
                    oh = sb("oh", [32, 640], F32, S1)
                    inw = sb("inw", [16, 640], F32, S1)
                    e_slot = K.slot()
                    K.dma("sp", relb, relb_d, e_slot)
                    K.dma("sp", oh, oh_d, e_slot)
                    K.dma("sp", inw, inwin_d, e_slot)
                    for t_ in (relb, oh, inw):
                        t_.res.w[e_slot] = e_slot.cnt
                    pv = ps("pv", [16, 1024], F32, S1)[:, 0:640]
                    vec = sb("vec", [16, 640], F32, S1)
                    K.mm(pv[:, 0:512], relb, oh[:, 0:512])
                    K.mm(pv[:, 512:640], relb, oh[:, 512:640])
                    K.act(vec, pv, AF.Exp)
                    K.tt("dve", vec, vec, inw, ALU.mult)
                    v_slot = K.slot()
                    K.dma("sp", vec_d, vec, v_slot)
                    tap("vec", vec, [16, 640])
                    tap("vecd", vec_d, [16, 640])
                    tap("ohs", oh, [32, 640])
                    tap("relbs", relb, [32, 16])
                    EB32 = sb("EB32", [128, 16, 3, 128], F32, S1)
                    g_slot = [K.slot() for _ in range(2)]
                    for k in range(128):
                        qn = "sp" if (k % 2 == 0) else "pool"
                        K.dma(qn, EB32[k:k + 1, :, :, :], V(vec_d.ap[:, 127 - k:127 - k + 384].rearrange("(a h) (o q) -> a h o q", a=1, o=3), vec_d.res), g_slot[k % 2])
                    tap("EB32", EB32, [128, 16, 3, 128])
                    K.cp("dve", EBT.re("p h o q -> p (h o q)"), EB32.re("p h o q -> p (h o q)"))

                tap("EBT", EBT, [128, 16, 3, 128], BF16)
                checkpoint("EBT")
                load_w(wbf, wG1_d, 384, 1)
                with Scope(K) as PG1:
                    pT = ps("pT", [128, 8, 128], BF16, PG1)
                    pa = ps("pa", [128, CH], F32, PG1)
                    pva = ps("pva", [128, 512], F32, PG1)[:, 0:128]
                    for C in range(NCH):
                        make_xnT(x1_d, C, pT)
                        for t in range(2):
                            proj(pa, t * 128)
                            K.cp("act", KcT[:, t, (C * 4 + 1) * 128:(C * 4 + 5) * 128], pa)
                        for b in range(4):
                            for kc in range(8):
                                K.mm(pva, xnT[:, kc, b * 128:(b + 1) * 128], wbf[:, kc, 256:384], start=(kc == 0), stop=(kc == 7))
                            K.cp("dve", Vc[:, C * 4 + b + 1, :], pva)

                tap("KcT", KcT, [128, 2, (NB + 2) * 128], BF16)
                tap("Vc", Vc, [128, NB + 2, 128], BF16)
                checkpoint("G1")
                load_w(wbf, wL1_d, 2048, 1)
                load_w(wobf, woc_d, 1024, None)
                with Scope(K) as PL1:
                    pT = ps("pT", [128, 8, 128], BF16, PL1)
                    pa = ps("pa", [128, CH], F32, PL1)
                    pw = [ps("pw%d" % i, [128, 2, CH], F32, PL1) for i in range(2)]
                    pnum = ps("pnum", [128, CH], F32, PL1)
                    pden = ps("pden", [128, CH], F32, PL1)
                    qcT = sb("qcT", [128, 8, CH], BF16, PL1)
                    sgc = sb("sgc", [128, 8, CH], BF16, PL1)
                    mixC = sb("mixC", [128, 8, CH], BF16, PL1)
                    pws = [sb("pws%d" % i, [128, 2, 3, 128], BF16, PL1) for i in range(2)]
                    pw2 = [sb("pw2%d" % i, [128, 2, 3, 128], BF16, PL1) for i in range(2)]
                    rs = sb("rs1", [128, CH], F32, PL1)
                    t1 = sb("t11", [128, CH], F32, PL1)
                    x2 = [sb("x2%d" % i, [128, 1024], F32, PL1) for i in range(2)]
                    yo = [sb("yo%d" % i, [128, 1024], F32, PL1) for i in range(2)]
                    yo_slot = [K.slot() for _ in range(2)]
                    ss2 = sb("ss2", [128, 2], F32, PL1)
                    ln2 = sb("ln2", [128, 2], F32, PL1)
                    r2 = sb("r2", [128, 2], F32, PL1)
                    it = 0
                    for C in range(NCH):
                        make_xnT(x1_d, C, pT)
                        for t in range(8):
                            proj(pa, t * 128)
                            K.cp("act", qcT[:, t, :], pa)
                        for t in range(8):
                            proj(pa, 1024 + t * 128)
                            K.act(sgc[:, t, :], pa, AF.Silu)
                        for t in range(8):
                            kv = t // 4
                            for qi in range(4):
                                i = C * 4 + qi
                                qs = slice(qi * 128, (qi + 1) * 128)
                                w = pw[it % 2]
                                s1 = pws[it % 2]
                                s2 = pw2[it % 2]
                                it += 1
                                for o in range(3):
                                    sl = 2 - o
                                    ks = slice((i + o) * 128, (i + o + 1) * 128)
                                    K.mm(w[:, 0, sl * 128:(sl + 1) * 128], KcT[0:64, kv, ks], qcT[0:64, t, qs])
                                    K.mm(w[:, 1, sl * 128:(sl + 1) * 128], KcT[64:128, kv, ks], qcT[64:128, t, qs])
                                for o in range(3):
                                    sl = 2 - o
                                    K.act(s1[:, :, sl, :], w[:, :, sl * 128:(sl + 1) * 128], AF.Exp, bias=maskW[:, i * 3 + o:i * 3 + o + 1], scale=0.125)
                                K.tt("dve", s2, s1, EBT[:, 2 * t:2 * t + 2, :, :], ALU.mult)
                                for o in range(3):
                                    sl = 2 - o
                                    st, sp_ = (o == 0), (o == 2)
                                    vv = Vc[:, i + o, kv * 64:(kv + 1) * 64]
                                    K.mm(pnum[0:64, qs], vv, s2[:, 0, sl, :], start=st, stop=sp_)
                                    K.mm(pnum[64:128, qs], vv, s2[:, 1, sl, :], start=st, stop=sp_, tp=(0, 64))
                                    K.mm(pden[0:64, qs], ones[:, 0:64], s2[:, 0, sl, :], start=st, stop=sp_)
                                    K.mm(pden[64:128, qs], ones[:, 0:64], s2[:, 1, sl, :], start=st, stop=sp_, tp=(0, 64))
                            K.ts("dve", rs, pden, esk[:, t:t + 1], ALU.add)
                            K.recip(rs, rs)
                            K.tt("dve", t1, pnum, rs, ALU.mult)
                            K.tt("pool", mixC[:, t, :], t1, sgc[:, t, :], ALU.mult)
                        for b in range(4):
                            bs = slice(b * 128, (b + 1) * 128)
                            py = pw[b % 2]
                            for half in range(2):
                                for f in range(8):
                                    K.mm(py[:, half, :], mixC[:, f, bs], wobf[:, f, half * 512:(half + 1) * 512], start=(f == 0), stop=(f == 7))
                            xo = x2[b % 2]
                            K.tt("dve", xo, py.re("p a c -> p (a c)"), xch[:, b, :], ALU.add)
                            K.act(junk, xo, AF.Square, accum=ss2[:, b % 2:b % 2 + 1])
                            K.act(ln2[:, b % 2:b % 2 + 1], ss2[:, b % 2:b % 2 + 1], AF.Ln, bias=epsb[:, 0:1], scale=1.0 / 1024.0)
                            K.act(r2[:, b % 2:b % 2 + 1], ln2[:, b % 2:b % 2 + 1], AF.Exp, scale=-0.5)
                            yb = yo[b % 2]
                            K.ts("dve", yb, xo, r2[:, b % 2:b % 2 + 1], ALU.mult)
                            K.tt("pool", yb, yb, fnbc, ALU.mult)
                            r0 = C * CH + b * 128
                            K.dma("pool", y_d[r0:r0 + 128, :], yb, yo_slot[b % 2])
    except StopBuild:
        pass
    for s_ in K.slots:
        if s_.cnt:
            nc.gpsimd.wait_ge(s_.sem, s_.cnt)
    return nc, K


def _t5_bucket(rel):
    half = 16
    max_exact = 8
    ret = (rel > 0).astype(np.int32) * half
    dist = np.abs(rel)
    large = max_exact + (np.log(np.maximum(dist, 1) / max_exact) / np.log(128 / max_exact) * (half - max_exact)).astype(np.int32)
    large = np.minimum(large, half - 1)
    return ret + np.where(dist < max_exact, dist, large)


def _static_tables():
    f32 = np.float32
    st = {}
    st["ident"] = np.eye(128, dtype=f32)
    ob = np.zeros((128, 128), f32)
    ob[:64, :64] = 1
    ob[64:, 64:] = 1
    st["onesblk"] = ob
    j = np.arange(128)[:, None]
    i = np.arange(128)[None, :]
    mmat = np.zeros((128, 4, 128), f32)
    mmat[:, 0, :] = np.maximum(i - j, 0)
    mmat[:, 1, :] = (i >= j)
    mmat[:, 2, :] = np.maximum(j - i, 0)
    mmat[:, 3, :] = (j > i)
    st["mmat"] = mmat
    c = np.arange(512) % 128
    iot = np.zeros((128, 4, 512), f32)
    iot[:, 0, :] = c + 1
    iot[:, 1, :] = 128 - c
    iot[:, 2, :] = 127 - c
    iot[:, 3, :] = c
    st["iot"] = iot
    m = np.arange(640)
    rel = 255 - m
    bk = _t5_bucket(rel)
    oh = np.zeros((32, 640), f32)
    oh[bk, m] = 1
    st["oh"] = oh
    st["inwin"] = np.broadcast_to((np.abs(rel) <= 128).astype(f32)[None, :], (16, 640)).copy()
    return st


def _core_tables(is_prompt):
    f32 = np.float32
    seqlen = 4096 if is_prompt else 2048
    t = np.arange(NT) % seqlen
    d = np.arange(128) % 64
    pair = d // 2
    sgn = np.where(d % 2 == 0, -1.0, 1.0)
    quarter = 16
    freqs = (np.float32(10000.0) ** (-np.arange(quarter, dtype=f32) / quarter)).astype(f32)
    row = (t // 64).astype(f32)
    col = (t % 64).astype(f32)
    ang = np.concatenate([row[:, None] * freqs, col[:, None] * freqs], axis=-1).astype(f32)
    angd = ang[:, pair].T.astype(np.float64)
    tabA = np.stack([np.cos(angd), np.sin(angd) * sgn[:, None]]).astype(f32)
    half = 32
    freqs_b = (np.float32(10000.0) ** (-np.arange(half, dtype=f32) / half)).astype(f32)
    angb = (t.astype(f32)[:, None] * freqs_b).astype(f32)
    angbd = angb[:, pair].T.astype(np.float64)
    tabB = np.stack([np.cos(angbd), np.sin(angbd) * sgn[:, None]]).astype(f32)
    seq_of_blk = (np.arange(NB) * 128) // seqlen
    maskA = np.zeros((NCH, NB), f32)
    for C in range(NCH):
        sq = (C * CH) // seqlen
        maskA[C, :] = np.where(seq_of_blk == sq, 0.0, NEG)
    maskA = np.broadcast_to(maskA.reshape(1, -1), (128, NCH * NB)).copy()
    maskW = np.zeros((NB, 3), f32)
    for i in range(NB):
        for o in range(3):
            jb = i + o - 1
            if jb < 0 or jb >= NB or seq_of_blk[jb] != seq_of_blk[i]:
                maskW[i, o] = NEG
    maskW = np.broadcast_to(maskW.reshape(1, -1), (128, NB * 3)).copy()
    cps = seqlen // 128
    rf = np.array([0.0 if (n % cps == 0) else 1.0 for n in range(NB)], f32)
    rb = np.array([0.0 if (n % cps == cps - 1) else 1.0 for n in range(NB)], f32)
    rfb = np.broadcast_to(np.concatenate([rf, rb])[None, :], (128, 64)).copy()
    return {"tabA": tabA, "tabB": tabB, "maskA": maskA, "maskW": maskW, "rfb": rfb}


def _swap(cols):
    cols = np.asarray(cols)
    return cols ^ 1


def _prep_common(norm_g, w_in_ab, qk_norm_a, ret_decay, w_out_ab, w_in_c, sink_c, w_out_c, rel_bias, final_norm):
    f32 = np.float32
    W = np.asarray(w_in_ab[0], f32)
    qa = np.arange(0, 512)
    ka = np.arange(512, 640)
    va = np.arange(640, 768)
    ga = np.arange(768, 1280)
    qb = np.arange(1280, 1536)
    kb = np.arange(1536, 1792)
    vb = np.arange(1792, 2304)
    gb = np.arange(2304, 2816)
    kadup = np.concatenate([ka[0:64], ka[0:64], ka[64:128], ka[64:128]])
    cm = {}
    cm["wG"] = np.ascontiguousarray(W[:, np.concatenate([kadup, _swap(kadup), kb, _swap(kb), va, vb])])
    cm["wLB"] = np.ascontiguousarray(W[:, np.concatenate([qb, _swap(qb), kb, _swap(kb), gb, vb])])
    cm["wLA"] = np.ascontiguousarray(W[:, np.concatenate([qa, _swap(qa), ga])])
    cm["woab"] = np.ascontiguousarray(np.asarray(w_out_ab[0], f32))
    Wc = np.asarray(w_in_c[0], f32)
    kc = np.arange(1024, 1152)
    kcdup = np.concatenate([kc[0:64], kc[0:64], kc[64:128], kc[64:128]])
    cm["wG1"] = np.ascontiguousarray(Wc[:, np.concatenate([kcdup, np.arange(1152, 1280)])])
    cm["wL1"] = np.ascontiguousarray(Wc[:, np.concatenate([np.arange(0, 1024), np.arange(1280, 2304)])])
    cm["woc"] = np.ascontiguousarray(np.asarray(w_out_c[0], f32))
    ng = np.asarray(norm_g, f32)
    cm["gcol"] = np.ascontiguousarray(ng.reshape(2, 8, 128).transpose(2, 0, 1).reshape(128, 16))
    cm["fn"] = np.asarray(final_norm, f32).reshape(1, 1024).copy()
    g = np.asarray(qk_norm_a[0], f32)
    d = np.arange(128) % 64
    cm["gqk"] = np.stack([g[0][d], g[0][d ^ 1], g[1][d], g[1][d ^ 1]], axis=1).astype(f32).copy()
    rd = np.asarray(ret_decay[0], f32)
    hp = (np.arange(128) // 64)
    rdec = np.zeros((128, 12), f32)
    for p in range(2):
        rdec[:, p] = rd[0][2 * p + hp]
        rdec[:, 2 + p] = rd[1][2 * p + hp]
    for h in range(4):
        rdec[:, 4 + h] = rd[0][h]
        rdec[:, 8 + h] = rd[1][h]
    cm["rdec"] = rdec
    sk = np.asarray(sink_c[0], f32)
    sinkl = np.zeros((128, 8), f32)
    for t in range(8):
        sinkl[:, t] = sk[2 * t + hp]
    cm["sinkl"] = sinkl
    cm["relb"] = np.ascontiguousarray(np.asarray(rel_bias, f32))
    cm.update(_static_tables())
    return cm


_CACHE = {}


def kernel(x_prompt, x_sample, norm_g, w_in_ab, qk_norm_a, ret_decay, w_out_ab, w_in_c, sink_c, w_out_c, rel_bias, final_norm):
    xp = np.asarray(x_prompt, np.float32)
    xs = np.asarray(x_sample, np.float32)
    cm = _prep_common(norm_g, w_in_ab, qk_norm_a, ret_decay, w_out_ab, w_in_c, sink_c, w_out_c, rel_bias, final_norm)
    tp = _core_tables(True)
    tsm = _core_tables(False)
    in_maps = []
    for c in range(8):
        m = dict(cm)
        if c < 4:
            m["x"] = np.ascontiguousarray(xp[c])
            m.update(tp)
        else:
            m["x"] = np.ascontiguousarray(xs[2 * (c - 4):2 * (c - 4) + 2].reshape(NT, 1024))
            m.update(tsm)
        in_maps.append(m)
    if "nc" not in _CACHE:
        _CACHE["nc"] = build_program()[0]
    nc = _CACHE["nc"]
    res = run_bass_kernel_spmd(nc, in_maps, core_ids=list(range(8)))
    outs = [np.asarray(r["y"], np.float32) for r in res.results]
    y_prompt = np.stack(outs[0:4], axis=0)
    y_sample = np.stack(outs[4:8], axis=0).reshape(8, 2048, 1024)
    return (y_prompt, y_sample)
```

```python
import numpy as np
import concourse.bass as bass
import concourse.mybir as mybir
from concourse.bass_utils import run_bass_kernel_spmd

F32 = mybir.dt.float32
BF16 = mybir.dt.bfloat16
AF = mybir.ActivationFunctionType
ALU = mybir.AluOpType

NT = 4096
NB = 32
CH = 512
NCH = 8
EPS = 1e-6
NEG = -30000.0


class Prod:
    def __init__(self, sem, inc):
        self.sem = sem
        self.inc = inc
        self.cnt = 0


class Res:
    def __init__(self):
        self.w = {}
        self.r = {}
        self.excl = False


class V:
    def __init__(self, ap, res=None):
        self.ap = ap
        self.res = res if res is not None else Res()

    def __getitem__(self, k):
        return V(self.ap[k], self.res)

    def re(self, pat, **kw):
        return V(self.ap.rearrange(pat, **kw), self.res)

    def bc(self, shape):
        return V(self.ap.to_broadcast(shape), self.res)


class Ker:
    def __init__(self, nc):
        self.nc = nc
        self.eng = {"pe": nc.tensor, "act": nc.scalar, "dve": nc.vector, "pool": nc.gpsimd, "sp": nc.sync}
        self.prod = {}
        for n in ("pe", "act", "dve", "pool"):
            self.prod[n] = Prod(nc.alloc_semaphore("s_" + n), 1)
        self.seen = {n: {} for n in self.eng}
        self.nslot = 0
        self.ninstr = 0

    def slot(self):
        self.nslot += 1
        p = Prod(self.nc.alloc_semaphore("d%d" % self.nslot), 16)
        if hasattr(self, "slots"):
            self.slots.append(p)
        return p

    def _wait(self, en, reads, writes):
        deps = {}
        for v in reads:
            for p, i in v.res.w.items():
                deps[p] = max(deps.get(p, 0), i)
        for v in writes:
            for p, i in v.res.w.items():
                deps[p] = max(deps.get(p, 0), i)
            for p, i in v.res.r.items():
                deps[p] = max(deps.get(p, 0), i)
        e = self.eng[en]
        seen = self.seen[en]
        own = self.prod.get(en)
        for p, i in deps.items():
            if p is own and en == "pe":
                continue
            if seen.get(p, 0) >= i:
                continue
            e.wait_ge(p.sem, i)
            seen[p] = i

    def op(self, en, fn, reads, writes):
        writes = list(writes) + [r for r in reads if r.res.excl]
        self._wait(en, reads, writes)
        ins = fn(self.eng[en])
        p = self.prod[en]
        p.cnt += 1
        ins.then_inc(p.sem, 1)
        for v in reads:
            v.res.r[p] = p.cnt
        for v in writes:
            v.res.w[p] = p.cnt
        self.ninstr += 1

    def dma(self, q, out, in_, slot):
        self._wait(q, [in_], [out])
        ins = self.eng[q].dma_start(out=out.ap, in_=in_.ap)
        slot.cnt += 16
        ins.then_inc(slot.sem, 16)
        in_.res.r[slot] = slot.cnt
        out.res.w[slot] = slot.cnt

    def mm(self, out, lhsT, rhs, start=True, stop=True, tp=None):
        kw = {}
        if tp is not None:
            kw["tile_position"] = tp
        self.op("pe", lambda e: e.matmul(out.ap, lhsT.ap, rhs.ap, start=start, stop=stop, **kw), [lhsT, rhs], [out])

    def tr(self, out, in_, ident):
        self.op("pe", lambda e: e.transpose(out.ap, in_.ap, ident.ap), [in_, ident], [out])

    def act(self, out, in_, func, bias=None, scale=1.0, accum=None):
        reads = [in_]
        kw = {}
        if bias is not None:
            if isinstance(bias, V):
                reads.append(bias)
                kw["bias"] = bias.ap
            else:
                kw["bias"] = bias
        if isinstance(scale, V):
            reads.append(scale)
            kw["scale"] = scale.ap
        else:
            kw["scale"] = scale
        writes = [out]
        if accum is not None:
            writes.append(accum)
            kw["accum_out"] = accum.ap
        self.op("act", lambda e: e.activation(out.ap, in_.ap, func, **kw), reads, writes)

    def tt(self, en, out, a, b, op):
        self.op(en, lambda e: e.tensor_tensor(out.ap, a.ap, b.ap, op), [a, b], [out])

    def stt(self, en, out, in0, scalar, in1, op0, op1):
        reads = [in0, in1]
        s = scalar
        if isinstance(scalar, V):
            reads.append(scalar)
            s = scalar.ap
        self.op(en, lambda e: e.scalar_tensor_tensor(out.ap, in0.ap, s, in1.ap, op0, op1), reads, [out])

    def ts(self, en, out, in0, s1, op0, s2=None, op1=None):
        reads = [in0]
        a1 = s1
        if isinstance(s1, V):
            reads.append(s1)
            a1 = s1.ap
        a2 = s2
        if isinstance(s2, V):
            reads.append(s2)
            a2 = s2.ap
        if op1 is None:
            self.op(en, lambda e: e.tensor_scalar(out.ap, in0.ap, a1, None, op0), reads, [out])
        else:
            self.op(en, lambda e: e.tensor_scalar(out.ap, in0.ap, a1, a2, op0, op1), reads, [out])

    def cp(self, en, out, in_):
        if en == "act":
            self.op("act", lambda e: e.copy(out.ap, in_.ap), [in_], [out])
        else:
            self.op(en, lambda e: e.tensor_copy(out.ap, in_.ap), [in_], [out])

    def recip(self, out, in_):
        self.op("dve", lambda e: e.reciprocal(out.ap, in_.ap), [in_], [out])

    def memset(self, en, out, val):
        self.op(en, lambda e: e.memset(out.ap, val), [], [out])


class StopBuild(Exception):
    pass


import contextlib


class Scope(contextlib.ExitStack):
    def __init__(self, K):
        super().__init__()
        self.K = K
        self.tiles = []

    def __exit__(self, *a):
        fr = self.K.freed
        for v in self.tiles:
            for d in (v.res.w, v.res.r):
                for p, i in d.items():
                    fr[p] = max(fr.get(p, 0), i)
        self.tiles = []
        return super().__exit__(*a)

    def close(self):
        self.__exit__(None, None, None)


def build_program(stop=None, taps=()):
    nc = bass.Bass("TRN2", target_bir_lowering=False)
    K = Ker(nc)
    K.slots = []
    K.freed = {}
    K.tapped = {}

    def checkpoint(name):
        if stop == name:
            raise StopBuild()

    def tap(name, v, shape, dt=F32):
        if name not in taps or name in K.tapped:
            return
        d = V(nc.dram_tensor("dbg_" + name, list(shape), dt, kind="ExternalOutput").ap())
        K.tapped[name] = d
        K.dma("sp", d, v, K.slot())

    def din(name, shape, dt=F32):
        return V(nc.dram_tensor(name, list(shape), dt, kind="ExternalInput").ap())

    x_d = din("x", [NT, 1024])
    wG_d = din("wG", [1024, 1664])
    wLB_d = din("wLB", [1024, 2048])
    wLA_d = din("wLA", [1024, 1536])
    woab_d = din("woab", [1024, 1024])
    wG1_d = din("wG1", [1024, 384])
    wL1_d = din("wL1", [1024, 2048])
    woc_d = din("woc", [1024, 1024])
    gcol_d = din("gcol", [128, 16])
    fn_d = din("fn", [1, 1024])
    gqk_d = din("gqk", [128, 4])
    rdec_d = din("rdec", [128, 12])
    sink_d = din("sinkl", [128, 8])
    relb_d = din("relb", [32, 16])
    ident_d = din("ident", [128, 128])
    onesblk_d = din("onesblk", [128, 128])
    mm_d = din("mmat", [128, 4, 128])
    iot_d = din("iot", [128, 4, 512])
    oh_d = din("oh", [32, 640])
    inwin_d = din("inwin", [16, 640])
    tabA_d = din("tabA", [2, 128, NT])
    tabB_d = din("tabB", [2, 128, NT])
    maskA_d = din("maskA", [128, 256])
    maskW_d = din("maskW", [128, 96])
    rfb_d = din("rfb", [128, 64])
    y_d = V(nc.dram_tensor("y", [NT, 1024], F32, kind="ExternalOutput").ap())
    x1_d = V(nc.dram_tensor("x1s", [NT, 1024], F32, kind="Internal").ap())
    mixb_d = V(nc.dram_tensor("mixbs", [4, 128, NT], BF16, kind="Internal").ap())
    vec_d = V(nc.dram_tensor("vecs", [16, 640], BF16, kind="Internal").ap())
    xnT0_d = V(nc.dram_tensor("xnT0s", [NCH, 128, 8, CH], BF16, kind="Internal").ap())
    xnT1_d = V(nc.dram_tensor("xnT1s", [NCH, 128, 8, CH], BF16, kind="Internal").ap())

    es = Scope(K)
    uid = [0]

    def sb(name, shape, dt=F32, stack=None):
        uid[0] += 1
        st_ = stack if stack is not None else es
        t = st_.enter_context(nc.sbuf_tensor("sb%d_%s" % (uid[0], name), list(shape), dt))
        v = V(t[:])
        v.res.w = dict(K.freed)
        st_.tiles.append(v)
        return v

    def ps(name, shape, dt=F32, stack=None):
        uid[0] += 1
        st_ = stack if stack is not None else es
        t = st_.enter_context(nc.psum_tensor("ps%d_%s" % (uid[0], name), list(shape), dt))
        v = V(t[:])
        v.res.excl = True
        v.res.w = dict(K.freed)
        st_.tiles.append(v)
        return v

    try:
        with es:
            cslot = K.slot()
            consts = []

            def cload(name, src, shape, dt=F32, q="sp"):
                t = sb(name, shape, dt)
                K.dma(q, t, src, cslot)
                consts.append(t)
                return t

            gcol = cload("gcol", gcol_d, [128, 16])
            gqk = cload("gqk", gqk_d, [128, 4])
            rdec = cload("rdec", rdec_d, [128, 12])
            sinkl = cload("sinkl", sink_d, [128, 8])
            maskA = cload("maskA", maskA_d, [128, 256])
            maskW = cload("maskW", maskW_d, [128, 96])
            rfb = cload("rfb", rfb_d, [128, 64])
            ident32 = cload("ident32", ident_d, [128, 128])
            onesblk32 = cload("onesblk32", onesblk_d, [128, 128])
            for c in consts:
                c.res.w[cslot] = cslot.cnt
            ident = sb("ident", [128, 128], BF16)
            onesblk = sb("onesblk", [128, 128], BF16)
            ones = sb("ones", [128, 128], BF16)
            epsb = sb("epsb", [128, 1])
            K.cp("dve", ident, ident32)
            K.cp("dve", onesblk, onesblk32)
            K.memset("dve", ones, 1.0)
            K.memset("dve", epsb, EPS)

            checkpoint("c0")
            wbf = sb("wbf", [128, 8, 2048], BF16)
            wobf = sb("wobf", [128, 8, 1024], BF16)
            wst = [sb("wst%d" % i, [128, 1024]) for i in range(2)]
            wst_slot = [K.slot() for _ in range(2)]
            xnTs = [sb("xnT%d" % i, [128, 8, CH], BF16) for i in range(2)]
            xnT_slot = [K.slot() for _ in range(2)]
            cur = {"xnT": xnTs[0]}
            xch_slot = [K.slot() for _ in range(4)]
            xst_slot = [K.slot() for _ in range(2)]
            EBT = sb("EBT", [128, 16, 3, 128], BF16)
            esk = sb("esk", [128, 8], F32)
            K.act(esk, sinkl, AF.Exp)
            with Scope(K) as S1:
                relb = sb("relb", [32, 16], F32, S1)
                oh = sb("oh", [32, 640], F32, S1)
                inw = sb("inw", [16, 640], F32, S1)
                e_slot = K.slot()
                K.dma("sp", relb, relb_d, e_slot)
                K.dma("sp", oh, oh_d, e_slot)
                K.dma("sp", inw, inwin_d, e_slot)
                for t_ in (relb, oh, inw):
                    t_.res.w[e_slot] = e_slot.cnt
                pv = ps("pv", [16, 1024], F32, S1)[:, 0:640]
                vec = sb("vec", [16, 640], F32, S1)
                vecb = sb("vecb", [16, 640], BF16, S1)
                K.mm(pv[:, 0:512], relb, oh[:, 0:512])
                K.mm(pv[:, 512:640], relb, oh[:, 512:640])
                K.act(vec, pv, AF.Exp)
                K.tt("dve", vecb, vec, inw, ALU.mult)
                v_slot = K.slot()
                K.dma("sp", vec_d, vecb, v_slot)
                g_slot = [K.slot() for _ in range(2)]
                for k in range(128):
                    qn = "sp" if (k % 2 == 0) else "pool"
                    K.dma(qn, EBT[k:k + 1, :, :, :], V(vec_d.ap[:, 127 - k:127 - k + 384].rearrange("(a h) (o q) -> a h o q", a=1, o=3), vec_d.res), g_slot[k % 2])
            wcount = [0]

            class MX:
                pass

            def alloc_mx(scope, full=True):
                m = MX()
                m.xch = sb("xch", [128, 4, 1024], F32, scope)
                if full:
                    m.xn = [sb("xn%d" % i, [128, 1024], BF16, scope) for i in range(2)]
                    m.junk = sb("junk", [128, 1024], BF16, scope)
                    m.ss = sb("ss", [128, 4], F32, scope)
                    m.lnv4 = sb("lnv4", [128, 4], F32, scope)
                    m.rstd4 = sb("rstd4", [128, 4], F32, scope)
                return m

            def load_x(m, src_d, C):
                for b in range(4):
                    r0 = C * CH + b * 128
                    K.dma("sp", m.xch[:, b, :], src_d[r0:r0 + 128, :], xch_slot[b])

            def store_xnT(dst_d, C, slot_i):
                K.dma("pool", dst_d[C], cur["xnT"], xst_slot[slot_i])

            def load_xnT(src_d, C):
                i = C % 2
                K.dma("sp", xnTs[i], src_d[C], xnT_slot[i])

            def use_xnT(C):
                cur["xnT"] = xnTs[C % 2]

            def load_w(dst, src_d, ncols, layer_g):
                for kc in range(8):
                    for c0 in range(0, ncols, 1024):
                        c1 = min(ncols, c0 + 1024)
                        i = wcount[0] % 2
                        wcount[0] += 1
                        K.dma("sp", wst[i][:, 0:c1 - c0], src_d[kc * 128:(kc + 1) * 128, c0:c1], wst_slot[i])
                        if layer_g is None:
                            K.cp("pool", dst[:, kc, c0:c1], wst[i][:, 0:c1 - c0])
                        else:
                            K.ts("pool", dst[:, kc, c0:c1], wst[i][:, 0:c1 - c0], gcol[:, layer_g * 8 + kc:layer_g * 8 + kc + 1], ALU.mult)

            def make_xnT(m, src_d, C, pT):
                use_xnT(C)
                xnT = cur["xnT"]
                load_x(m, src_d, C)
                for b in range(4):
                    K.act(m.junk, m.xch[:, b, :], AF.Square, accum=m.ss[:, b:b + 1])
                K.act(m.lnv4, m.ss, AF.Ln, bias=epsb[:, 0:1], scale=1.0 / 1024.0)
                K.act(m.rstd4, m.lnv4, AF.Exp, scale=-0.5)
                for b in range(4):
                    xb = m.xn[b % 2]
                    K.ts("dve", xb, m.xch[:, b, :], m.rstd4[:, b:b + 1], ALU.mult)
                    for kc in range(8):
                        K.tr(pT[:, kc, :], xb[:, kc * 128:(kc + 1) * 128], ident)
                    K.cp("act", xnT[:, :, b * 128:(b + 1) * 128], pT)

            def proj(dst, c0):
                for kc in range(8):
                    K.mm(dst, wbf[:, kc, c0:c0 + 128], cur["xnT"][:, kc, :], start=(kc == 0), stop=(kc == 7))

            def rsq_bcast(dst, src_ps, nfeat, sq, psn, lnv, lhs_ones):
                K.act(sq, src_ps, AF.Square)
                K.mm(psn, lhs_ones, sq)
                K.act(lnv, psn, AF.Ln, bias=epsb[:, 0:1], scale=1.0 / nfeat)
                K.act(dst, lnv, AF.Exp, scale=-0.5)

            with Scope(K) as L0:
                LR = Scope(K)
                KaT = sb("KaT", [128, 2, NT], BF16, L0)
                Va = sb("Va", [128, NB, 128], BF16, L0)
                tabc = sb("tabc", [128, 2, CH], F32, L0)
                tab_slot = K.slot()
                tabd_slot = K.slot()
                sq = sb("sq", [128, CH], BF16, L0)
                lnv = sb("lnv", [128, CH], F32, L0)
                rs = sb("rs", [128, CH], F32, L0)
                t1 = sb("t1", [128, CH], F32, L0)
                t2 = sb("t2", [128, CH], F32, L0)
                tabg = sb("tabg", [128, 2, CH], F32, L0)
                SbAll = sb("SbAll", [128, 2, NB, 128], BF16, LR)
                tabd = sb("tabd", [128, 2, CH], F32, LR)
                vbtm = sb("vbtm", [128, 4, 512], BF16, LR)
                lg = sb("lg", [128, 12], F32, LR)
                K.act(lg, rdec, AF.Exp)
                K.ts("dve", lg, lg, -1.0, ALU.mult)
                cd = sb("cd", [128, 4], F32, LR)
                K.act(cd, lg[:, 0:4], AF.Exp, scale=128.0)
                cdr = sb("cdr", [128, 4, NB], F32, LR)
                for j in range(4):
                    off = 0 if j < 2 else 32
                    K.ts("dve", cdr[:, j, :], rfb[:, off:off + 32], cd[:, j:j + 1], ALU.mult)
                checkpoint("c1")
                QF4 = sb("QF4", [128, 2, CH], F32, LR)
                QB4 = sb("QB4", [128, 2, CH], F32, LR)
                KF4 = sb("KF4", [128, 2, CH], F32, LR)
                KB4 = sb("KB4", [128, 2, CH], F32, LR)
                DT = sb("DT", [128, 4, 128], F32, LR)
                with Scope(K) as S0:
                    iot = sb("iot", [128, 4, CH], F32, S0)
                    K.dma("sp", iot, iot_d, tabd_slot)
                    for p in range(2):
                        K.act(QF4[:, p, :], iot[:, 0, :], AF.Exp, scale=lg[:, p:p + 1])
                        K.act(QB4[:, p, :], iot[:, 1, :], AF.Exp, scale=lg[:, 2 + p:3 + p])
                        K.act(KF4[:, p, :], iot[:, 2, :], AF.Exp, scale=lg[:, p:p + 1])
                        K.act(KB4[:, p, :], iot[:, 3, :], AF.Exp, scale=lg[:, 2 + p:3 + p])
                    K.ts("dve", KF4, KF4, 0.125, ALU.mult)
                    K.ts("dve", KB4, KB4, 0.125, ALU.mult)
                    mmat = sb("mmat", [128, 4, 128], F32, S0)
                    K.dma("sp", mmat, mm_d, tab_slot)
                    d1 = sb("d1", [128, 128], F32, S0)
                    d2 = sb("d2", [128, 128], F32, S0)
                    for h in range(4):
                        checkpoint("d0")
                        K.act(d1, mmat[:, 0, :], AF.Exp, scale=lg[:, 4 + h:5 + h])
                        checkpoint("d1")
                        K.tt("dve", d1, d1, mmat[:, 1, :], ALU.mult)
                        checkpoint("d2")
                        K.act(d2, mmat[:, 2, :], AF.Exp, scale=lg[:, 8 + h:9 + h])
                        K.tt("dve", d2, d2, mmat[:, 3, :], ALU.mult)
                        K.tt("dve", d1, d1, d2, ALU.add)
                        checkpoint("d3")
                        K.ts("dve", DT[:, h, :], d1, 0.125, ALU.mult)
                        checkpoint("d4")

                def load_tab(dst, slot, src_d, C):
                    K.dma("sp", dst, V(src_d.ap[:, :, C * CH:(C + 1) * CH].rearrange("t p c -> p t c"), src_d.res), slot)

                def rope(psa, psb, tab, out32, ga=None, gb=None):
                    if ga is None:
                        K.tt("dve", t1, psa, tab[:, 0, :], ALU.mult)
                        K.tt("dve", t2, psb, tab[:, 1, :], ALU.mult)
                    else:
                        K.ts("pool", tabg[:, 0, :], tab[:, 0, :], ga, ALU.mult)
                        K.ts("pool", tabg[:, 1, :], tab[:, 1, :], gb, ALU.mult)
                        K.tt("dve", t1, psa, tabg[:, 0, :], ALU.mult)
                        K.tt("dve", t2, psb, tabg[:, 1, :], ALU.mult)
                    K.tt("pool", out32, t1, t2, ALU.add)

                checkpoint("setup0")
                load_w(wbf, wG_d, 1664, 0)
                with Scope(K) as PG:
                    pT = ps("pT", [128, 8, 128], BF16, PG)
                    pa = ps("pa", [128, CH], F32, PG)
                    pb = ps("pb", [128, CH], F32, PG)
                    pn = ps("pn", [128, CH], F32, PG)
                    pk = ps("pk", [128, 4, 2, 128], BF16, PG)
                    pva = ps("pva", [128, 512], F32, PG)[:, 0:128]
                    pvb = ps("pvb", [128, 512], F32, PG)
                    pkv = ps("pkv", [128, 4, 128], F32, PG)[:, 0:2, :]
                    kdbT = sb("kdbT", [128, 2, CH], BF16, PG)
                    kdbtm = sb("kdbtm", [128, 4, 2, 128], BF16, PG)
                    Rb = sb("Rb", [128, 2, 128], F32, PG)
                    mxg = alloc_mx(PG)
                    K.memset("dve", Rb, 0.0)
                    checkpoint("g_w")
                    for C in range(NCH - 1, -1, -1):
                        make_xnT(mxg, x_d, C, pT)
                        store_xnT(xnT0_d, C, C % 2)
                        checkpoint("g_x"); checkpoint("G%d_x" % C)
                        load_tab(tabc, tab_slot, tabA_d, C)
                        load_tab(tabd, tabd_slot, tabB_d, C)
                        checkpoint("g_t"); checkpoint("G%d_t" % C)
                        for t in range(2):
                            proj(pa, t * 128)
                            proj(pb, 256 + t * 128)
                            checkpoint("k0"); checkpoint("G%d_%d_k0" % (C, t))
                            rsq_bcast(rs, pa, 64.0, sq, pn, lnv, onesblk)
                            checkpoint("k1"); checkpoint("G%d_%d_k1" % (C, t))
                            rope(pa, pb, tabc, t1, gqk[:, 2:3], gqk[:, 3:4])
                            checkpoint("k2"); checkpoint("G%d_%d_k2" % (C, t))
                            K.tt("pool", KaT[:, t, C * CH:(C + 1) * CH], t1, rs, ALU.mult)
                            checkpoint("k3"); checkpoint("G%d_%d_k3" % (C, t))
                        checkpoint("g_ka"); checkpoint("G%d_ka" % C)
                        for t in range(2):
                            proj(pa, 512 + t * 128)
                            proj(pb, 768 + t * 128)
                            rope(pa, pb, tabd, t1)
                            K.tt("pool", kdbT[:, t, :], t1, KB4[:, t, :], ALU.mult)
                            for cj in range(4):
                                K.tr(pk[:, cj, t, :], kdbT[:, t, cj * 128:(cj + 1) * 128], ident)
                        K.cp("act", kdbtm, pk)
                        checkpoint("g_kb"); checkpoint("G%d_kb" % C)
                        for b in range(4):
                            for kc in range(8):
                                K.mm(pva, cur["xnT"][:, kc, b * 128:(b + 1) * 128], wbf[:, kc, 1024:1152], start=(kc == 0), stop=(kc == 7))
                            for kc in range(8):
                                K.mm(pvb, cur["xnT"][:, kc, b * 128:(b + 1) * 128], wbf[:, kc, 1152:1664], start=(kc == 0), stop=(kc == 7))
                            K.cp("act", Va[:, C * 4 + b, :], pva)
                            K.cp("dve", vbtm[:, b, :], pvb)
                        checkpoint("g_v"); checkpoint("G%d_v" % C)
                        for cj in range(3, -1, -1):
                            n = C * 4 + cj
                            for p in range(2):
                                K.mm(pkv[0:64, p, :], kdbtm[:, cj, p, 0:64], vbtm[:, cj, (2 * p) * 128:(2 * p + 1) * 128])
                                K.mm(pkv[64:128, p, :], kdbtm[:, cj, p, 64:128], vbtm[:, cj, (2 * p + 1) * 128:(2 * p + 2) * 128], tp=(0, 64))
                            checkpoint("s0")
                            K.ts("dve", SbAll[:, :, n, :], Rb, rfb[:, 32 + n:33 + n], ALU.mult)
                            checkpoint("s1")
                            for p in range(2):
                                K.ts("dve", Rb[:, p, :], Rb[:, p, :], cdr[:, 2 + p, n:n + 1], ALU.mult)
                                checkpoint("s2")
                                K.tt("dve", Rb[:, p, :], pkv[:, p, :], Rb[:, p, :], ALU.add)
                                checkpoint("s3")
                            checkpoint("s4")
                        checkpoint("g_c1"); checkpoint("G%d_end" % C)

                tap("KaT", KaT, [128, 2, NT], BF16)
                tap("Va", Va, [128, NB, 128], BF16)
                tap("SbAll", SbAll, [128, 2, NB, 128], BF16)
                checkpoint("G")
                load_w(wbf, wLB_d, 2048, 0)
                with Scope(K) as PB:
                    pT = ps("pT", [128, 8, 128], BF16, PB)
                    pa = ps("pa", [128, CH], F32, PB)
                    pb = ps("pb", [128, CH], F32, PB)
                    pk = pT.re("p (c t) q -> p c t q", t=2)
                    pss = ps("pss", [128, 512], F32, PB)
                    po = ps("po", [128, 4, CH], F32, PB)
                    qrT = sb("qrT", [128, 2, CH], BF16, PB)
                    qdf = sb("qdf", [128, 2, CH], BF16, PB)
                    qdb = sb("qdb", [128, 2, CH], BF16, PB)
                    krT = sb("krT", [128, 2, CH], BF16, PB)
                    kdfT = sb("kdfT", [128, 2, CH], BF16, PB)
                    kdftm = sb("kdftm", [128, 4, 2, 128], BF16, PB)
                    sg = sb("sg", [128, 4, CH], BF16, PB)
                    AT = sb("AT", [128, 4, 128], BF16, PB)
                    Sf = sb("Sf", [128, 2, 128], BF16, PB)
                    Rf = sb("Rf", [128, 2, 128], F32, PB)
                    mixBc = [sb("mixBc%d" % i, [128, 4, CH], BF16, PB) for i in range(1)]
                    mixB_slot = [K.slot() for _ in range(1)]
                    K.memset("dve", Rf, 0.0)
                    load_xnT(xnT0_d, 0)
                    for C in range(NCH):
                        use_xnT(C)
                        if C + 1 < NCH:
                            load_xnT(xnT0_d, C + 1)
                        load_tab(tabd, tabd_slot, tabB_d, C)
                        for t in range(2):
                            proj(pa, t * 128)
                            proj(pb, 256 + t * 128)
                            rope(pa, pb, tabd, t1)
                            K.cp("act", qrT[:, t, :], t1)
                            K.tt("pool", qdf[:, t, :], t1, QF4[:, t, :], ALU.mult)
                            K.tt("pool", qdb[:, t, :], t1, QB4[:, t, :], ALU.mult)
                        for t in range(2):
                            proj(pa, 512 + t * 128)
                            proj(pb, 768 + t * 128)
                            rope(pa, pb, tabd, t1)
                            K.cp("act", krT[:, t, :], t1)
                            K.tt("pool", kdfT[:, t, :], t1, KF4[:, t, :], ALU.mult)
                            for cj in range(4):
                                K.tr(pk[:, cj, t, :], kdfT[:, t, cj * 128:(cj + 1) * 128], ident)
                        K.cp("act", kdftm, pk)
                        for h in range(4):
                            proj(pa, 1024 + h * 128)
                            K.act(sg[:, h, :], pa, AF.Silu)
                        for b in range(4):
                            for kc in range(8):
                                K.mm(pb, cur["xnT"][:, kc, b * 128:(b + 1) * 128], wbf[:, kc, 1536:2048], start=(kc == 0), stop=(kc == 7))
                            K.cp("dve", vbtm[:, b, :], pb)
                        for cj in range(4):
                            n = C * 4 + cj
                            cs = slice(cj * 128, (cj + 1) * 128)
                            K.ts("dve", Sf, Rf, rfb[:, n:n + 1], ALU.mult)
                            for p in range(2):
                                K.mm(pa[0:64, p * 128:(p + 1) * 128], kdftm[:, cj, p, 0:64], vbtm[:, cj, (2 * p) * 128:(2 * p + 1) * 128])
                                K.mm(pa[64:128, p * 128:(p + 1) * 128], kdftm[:, cj, p, 64:128], vbtm[:, cj, (2 * p + 1) * 128:(2 * p + 2) * 128], tp=(0, 64))
                            for p in range(2):
                                K.ts("dve", Rf[:, p, :], Rf[:, p, :], cdr[:, p, n:n + 1], ALU.mult)
                                K.tt("dve", Rf[:, p, :], pa[:, p * 128:(p + 1) * 128], Rf[:, p, :], ALU.add)
                            for h in range(4):
                                t, r0 = h // 2, (h % 2) * 64
                                pdst = pss if (h % 2 == 0) else pb
                                K.mm(pdst[:, t * 128:(t + 1) * 128], krT[r0:r0 + 64, t, cs], qrT[r0:r0 + 64, t, cs])
                            ATv = AT.re("p (t hp) i -> p hp t i", hp=2)
                            DTv = DT.re("p (t hp) i -> p hp t i", hp=2)
                            K.tt("dve", ATv[:, 0, :, :], pss[:, 0:256].re("p (t i) -> p t i", t=2), DTv[:, 0, :, :], ALU.mult)
                            K.tt("dve", ATv[:, 1, :, :], pb[:, 0:256].re("p (t i) -> p t i", t=2), DTv[:, 1, :, :], ALU.mult)
                            for h in range(4):
                                t, r0 = h // 2, (h % 2) * 64
                                K.mm(po[:, h, cs], vbtm[:, cj, h * 128:(h + 1) * 128], AT[:, h, :], start=True, stop=False)
                                K.mm(po[:, h, cs], Sf[r0:r0 + 64, t, :], qdf[r0:r0 + 64, t, cs], start=False, stop=False)
                                K.mm(po[:, h, cs], SbAll[r0:r0 + 64, t, n, :], qdb[r0:r0 + 64, t, cs], start=False, stop=True)
                        mb = mixBc[0]
                        for h in range(4):
                            rsq_bcast(rs, po[:, h, :], 128.0, sq, pb, lnv, ones)
                            K.tt("dve", t1, po[:, h, :], rs, ALU.mult)
                            K.tt("pool", mb[:, h, :], t1, sg[:, h, :], ALU.mult)
                        K.dma("pool", V(mixb_d.ap[:, :, C * CH:(C + 1) * CH].rearrange("h p c -> p h c"), mixb_d.res), mb, mixB_slot[0])

                tap("mixb", mixb_d, [4, 128, NT], BF16)
                checkpoint("LB")
                LR.close()
                load_w(wbf, wLA_d, 1536, 0)
                load_w(wobf, woab_d, 1024, None)
                with Scope(K) as PA:
                    pT = ps("pT", [128, 8, 128], BF16, PA)
                    pa = ps("pa", [128, CH], F32, PA)
                    pb = ps("pb", [128, CH], F32, PA)
                    psc = [ps("psc%d" % i, [128, 2, CH], F32, PA) for i in range(2)]
                    pnum = ps("pnum", [128, CH], F32, PA)
                    pden = pb
                    qaT = sb("qaT", [128, 4, CH], BF16, PA)
                    sga = sb("sga", [128, 4, CH], BF16, PA)
                    mixA = sb("mixA", [128, 4, CH], BF16, PA)
                    mixBl = sb("mixBl", [128, 4, CH], BF16, PA)
                    mixBl_slot = K.slot()
                    pTs = [sb("pTs%d" % i, [128, 2, CH], BF16, PA) for i in range(2)]
                    x1b = [sb("x1b%d" % i, [128, 1024], F32, PA) for i in range(2)]
                    x1b_slot = [K.slot() for _ in range(2)]
                    it = 0
                    mxa = alloc_mx(PA, full=False)
                    xch = mxa.xch
                    load_xnT(xnT0_d, 0)
                    for C in range(NCH):
                        use_xnT(C)
                        if C + 1 < NCH:
                            load_xnT(xnT0_d, C + 1)
                        load_x(mxa, x_d, C)
                        load_tab(tabc, tab_slot, tabA_d, C)
                        K.dma("pool", mixBl, V(mixb_d.ap[:, :, C * CH:(C + 1) * CH].rearrange("h p c -> p h c"), mixb_d.res), mixBl_slot)
                        for t in range(4):
                            proj(pa, t * 128)
                            proj(pb, 512 + t * 128)
                            rsq_bcast(rs, pa, 64.0, sq, pnum, lnv, onesblk)
                            rope(pa, pb, tabc, t1, gqk[:, 0:1], gqk[:, 1:2])
                            K.tt("pool", qaT[:, t, :], t1, rs, ALU.mult)
                        for t in range(4):
                            proj(pa, 1024 + t * 128)
                            K.act(sga[:, t, :], pa, AF.Silu)
                        for t in range(4):
                            kv = t // 2
                            def qk(kb_, sc_):
                                ks = slice(kb_ * 128, (kb_ + 1) * 128)
                                K.mm(sc_[:, 0, :], KaT[0:64, kv, ks], qaT[0:64, t, :])
                                K.mm(sc_[:, 1, :], KaT[64:128, kv, ks], qaT[64:128, t, :])

                            qk(0, psc[it % 2])
                            for kb in range(NB):
                                sc = psc[it % 2]
                                pt = pTs[it % 2]
                                it += 1
                                K.act(pt, sc, AF.Exp, bias=maskA[:, C * NB + kb:C * NB + kb + 1], scale=0.125)
                                if kb + 1 < NB:
                                    qk(kb + 1, psc[it % 2])
                                st, sp_ = (kb == 0), (kb == NB - 1)
                                K.mm(pnum[0:64, :], Va[:, kb, kv * 64:(kv + 1) * 64], pt[:, 0, :], start=st, stop=sp_)
                                K.mm(pnum[64:128, :], Va[:, kb, kv * 64:(kv + 1) * 64], pt[:, 1, :], start=st, stop=sp_, tp=(0, 64))
                                K.mm(pden[0:64, :], ones[:, 0:64], pt[:, 0, :], start=st, stop=sp_)
                                K.mm(pden[64:128, :], ones[:, 0:64], pt[:, 1, :], start=st, stop=sp_, tp=(0, 64))
                            K.recip(rs, pden)
                            K.tt("dve", t1, pnum, rs, ALU.mult)
                            K.tt("pool", mixA[:, t, :], t1, sga[:, t, :], ALU.mult)
                        for b in range(4):
                            bs = slice(b * 128, (b + 1) * 128)
                            py = psc[b % 2]
                            for half in range(2):
                                for f in range(8):
                                    src = mixA[:, f, bs] if f < 4 else mixBl[:, f - 4, bs]
                                    K.mm(py[:, half, :], src, wobf[:, f, half * 512:(half + 1) * 512], start=(f == 0), stop=(f == 7))
                            xo = x1b[b % 2]
                            K.tt("dve", xo, py.re("p a c -> p (a c)"), xch[:, b, :], ALU.add)
                            r0 = C * CH + b * 128
                            K.dma("pool", x1_d[r0:r0 + 128, :], xo, x1b_slot[b % 2])

            tap("x1", x1_d, [NT, 1024], F32)
            checkpoint("LA")
            with Scope(K) as L1:
                KcT = sb("KcT", [128, 2, (NB + 2) * 128], BF16, L1)
                Vc = sb("Vc", [128, NB + 2, 128], BF16, L1)
                K.memset("pool", KcT[:, :, 0:128], 0.0)
                K.memset("pool", KcT[:, :, (NB + 1) * 128:(NB + 2) * 128], 0.0)
                K.memset("pool", Vc[:, 0, :], 0.0)
                K.memset("pool", Vc[:, NB + 1, :], 0.0)
                tap("EBT", EBT, [128, 16, 3, 128], BF16)
                checkpoint("EBT")
                load_w(wbf, wG1_d, 384, 1)
                with Scope(K) as PG1:
                    pT = ps("pT", [128, 8, 128], BF16, PG1)
                    pa = ps("pa", [128, CH], F32, PG1)
                    pva = ps("pva", [128, 512], F32, PG1)[:, 0:128]
                    mxg1 = alloc_mx(PG1)
                    for C in range(NCH):
                        make_xnT(mxg1, x1_d, C, pT)
                        store_xnT(xnT1_d, C, C % 2)
                        for t in range(2):
                            proj(pa, t * 128)
                            K.cp("act", KcT[:, t, (C * 4 + 1) * 128:(C * 4 + 5) * 128], pa)
                        for b in range(4):
                            for kc in range(8):
                                K.mm(pva, cur["xnT"][:, kc, b * 128:(b + 1) * 128], wbf[:, kc, 256:384], start=(kc == 0), stop=(kc == 7))
                            K.cp("dve", Vc[:, C * 4 + b + 1, :], pva)

                tap("KcT", KcT, [128, 2, (NB + 2) * 128], BF16)
                tap("Vc", Vc, [128, NB + 2, 128], BF16)
                checkpoint("G1")
                load_w(wbf, wL1_d, 2048, 1)
                load_w(wobf, woc_d, 1024, None)
                with Scope(K) as PL1:
                    pT = ps("pT", [128, 8, 128], BF16, PL1)
                    pa = ps("pa", [128, CH], F32, PL1)
                    pw = [ps("pw%d" % i, [128, 2, CH], F32, PL1) for i in range(2)]
                    pnum = ps("pnum", [128, CH], F32, PL1)
                    pden = ps("pden", [128, CH], F32, PL1)
                    qcT = sb("qcT", [128, 8, CH], BF16, PL1)
                    sgc = sb("sgc", [128, 8, CH], BF16, PL1)
                    mixC = sb("mixC", [128, 8, CH], BF16, PL1)
                    pws = [sb("pws%d" % i, [128, 2, 3, 128], BF16, PL1) for i in range(2)]
                    pw2 = [sb("pw2%d" % i, [128, 2, 3, 128], BF16, PL1) for i in range(2)]
                    rs = sb("rs1", [128, CH], F32, PL1)
                    t1 = sb("t11", [128, CH], F32, PL1)
                    x2 = [sb("x2%d" % i, [128, 1024], F32, PL1) for i in range(2)]
                    yo = [sb("yo%d" % i, [128, 1024], F32, PL1) for i in range(2)]
                    yo_slot = [K.slot() for _ in range(2)]
                    ss2 = sb("ss2", [128, 2], F32, PL1)
                    ln2 = sb("ln2", [128, 2], F32, PL1)
                    r2 = sb("r2", [128, 2], F32, PL1)
                    it = 0
                    mxl = alloc_mx(PL1, full=False)
                    xch = mxl.xch
                    junk = sb("junk1", [128, 1024], BF16, PL1)
                    fnbc = sb("fnbc", [128, 1024], F32, PL1)
                    fn_slot = K.slot()
                    K.dma("sp", fnbc, V(fn_d.ap.to_broadcast([128, 1024]), fn_d.res), fn_slot)
                    load_xnT(xnT1_d, 0)
                    for C in range(NCH):
                        use_xnT(C)
                        if C + 1 < NCH:
                            load_xnT(xnT1_d, C + 1)
                        load_x(mxl, x1_d, C)
                        for t in range(8):
                            proj(pa, t * 128)
                            K.cp("act", qcT[:, t, :], pa)
                        for t in range(8):
                            proj(pa, 1024 + t * 128)
                            K.act(sgc[:, t, :], pa, AF.Silu)
                        items = [(t, qi) for t in range(8) for qi in range(4)]

                        def wqk(t, qi, w):
                            kv = t // 4
                            i = C * 4 + qi
                            qs = slice(qi * 128, (qi + 1) * 128)
                            for o in range(3):
                                sl = 2 - o
                                ks = slice((i + o) * 128, (i + o + 1) * 128)
                                K.mm(w[:, 0, sl * 128:(sl + 1) * 128], KcT[0:64, kv, ks], qcT[0:64, t, qs])
                                K.mm(w[:, 1, sl * 128:(sl + 1) * 128], KcT[64:128, kv, ks], qcT[64:128, t, qs])

                        wqk(items[0][0], items[0][1], pw[it % 2])
                        for idx, (t, qi) in enumerate(items):
                            kv = t // 4
                            i = C * 4 + qi
                            qs = slice(qi * 128, (qi + 1) * 128)
                            w = pw[it % 2]
                            s1 = pws[it % 2]
                            s2 = pw2[it % 2]
                            it += 1
                            if i in (0, NB // 2 - 1, NB // 2, NB - 1):
                                for o in range(3):
                                    sl = 2 - o
                                    K.act(s1[:, :, sl, :], w[:, :, sl * 128:(sl + 1) * 128], AF.Exp, bias=maskW[:, i * 3 + o:i * 3 + o + 1], scale=0.125)
                            else:
                                K.act(s1, w[:, :, 0:384].re("p h (o q) -> p h o q", o=3), AF.Exp, scale=0.125)
                            K.tt("dve", s2, s1, EBT[:, 2 * t:2 * t + 2, :, :], ALU.mult)
                            if idx + 1 < len(items):
                                wqk(items[idx + 1][0], items[idx + 1][1], pw[it % 2])
                            for o in range(3):
                                sl = 2 - o
                                st, sp_ = (o == 0), (o == 2)
                                vv = Vc[:, i + o, kv * 64:(kv + 1) * 64]
                                K.mm(pnum[0:64, qs], vv, s2[:, 0, sl, :], start=st, stop=sp_)
                                K.mm(pnum[64:128, qs], vv, s2[:, 1, sl, :], start=st, stop=sp_, tp=(0, 64))
                                K.mm(pden[0:64, qs], ones[:, 0:64], s2[:, 0, sl, :], start=st, stop=sp_)
                                K.mm(pden[64:128, qs], ones[:, 0:64], s2[:, 1, sl, :], start=st, stop=sp_, tp=(0, 64))
                            if qi == 3:
                                K.ts("dve", rs, pden, esk[:, t:t + 1], ALU.add)
                                K.recip(rs, rs)
                                K.tt("dve", t1, pnum, rs, ALU.mult)
                                K.tt("pool", mixC[:, t, :], t1, sgc[:, t, :], ALU.mult)
                        for b in range(4):
                            bs = slice(b * 128, (b + 1) * 128)
                            py = pw[b % 2]
                            for half in range(2):
                                for f in range(8):
                                    K.mm(py[:, half, :], mixC[:, f, bs], wobf[:, f, half * 512:(half + 1) * 512], start=(f == 0), stop=(f == 7))
                            xo = x2[b % 2]
                            K.tt("dve", xo, py.re("p a c -> p (a c)"), xch[:, b, :], ALU.add)
                            K.act(junk, xo, AF.Square, accum=ss2[:, b % 2:b % 2 + 1])
                            K.act(ln2[:, b % 2:b % 2 + 1], ss2[:, b % 2:b % 2 + 1], AF.Ln, bias=epsb[:, 0:1], scale=1.0 / 1024.0)
                            K.act(r2[:, b % 2:b % 2 + 1], ln2[:, b % 2:b % 2 + 1], AF.Exp, scale=-0.5)
                            yb = yo[b % 2]
                            K.ts("dve", yb, xo, r2[:, b % 2:b % 2 + 1], ALU.mult)
                            K.tt("pool", yb, yb, fnbc, ALU.mult)
                            r0 = C * CH + b * 128
                            K.dma("pool", y_d[r0:r0 + 128, :], yb, yo_slot[b % 2])
    except StopBuild:
        pass
    for s_ in K.slots:
        if s_.cnt:
            nc.gpsimd.wait_ge(s_.sem, s_.cnt)
    return nc, K


def _t5_bucket(rel):
    half = 16
    max_exact = 8
    ret = (rel > 0).astype(np.int32) * half
    dist = np.abs(rel)
    large = max_exact + (np.log(np.maximum(dist, 1) / max_exact) / np.log(128 / max_exact) * (half - max_exact)).astype(np.int32)
    large = np.minimum(large, half - 1)
    return ret + np.where(dist < max_exact, dist, large)


def _static_tables():
    f32 = np.float32
    st = {}
    st["ident"] = np.eye(128, dtype=f32)
    ob = np.zeros((128, 128), f32)
    ob[:64, :64] = 1
    ob[64:, 64:] = 1
    st["onesblk"] = ob
    j = np.arange(128)[:, None]
    i = np.arange(128)[None, :]
    mmat = np.zeros((128, 4, 128), f32)
    mmat[:, 0, :] = np.maximum(i - j, 0)
    mmat[:, 1, :] = (i >= j)
    mmat[:, 2, :] = np.maximum(j - i, 0)
    mmat[:, 3, :] = (j > i)
    st["mmat"] = mmat
    c = np.arange(512) % 128
    iot = np.zeros((128, 4, 512), f32)
    iot[:, 0, :] = c + 1
    iot[:, 1, :] = 128 - c
    iot[:, 2, :] = 127 - c
    iot[:, 3, :] = c
    st["iot"] = iot
    m = np.arange(640)
    rel = 255 - m
    bk = _t5_bucket(rel)
    oh = np.zeros((32, 640), f32)
    oh[bk, m] = 1
    st["oh"] = oh
    st["inwin"] = np.broadcast_to((np.abs(rel) <= 128).astype(f32)[None, :], (16, 640)).copy()
    return st


def _core_tables(is_prompt):
    f32 = np.float32
    seqlen = 4096 if is_prompt else 2048
    t = np.arange(NT) % seqlen
    d = np.arange(128) % 64
    pair = d // 2
    sgn = np.where(d % 2 == 0, -1.0, 1.0)
    quarter = 16
    freqs = (np.float32(10000.0) ** (-np.arange(quarter, dtype=f32) / quarter)).astype(f32)
    row = (t // 64).astype(f32)
    col = (t % 64).astype(f32)
    ang = np.concatenate([row[:, None] * freqs, col[:, None] * freqs], axis=-1).astype(f32)
    angd = ang[:, pair].T.astype(np.float64)
    tabA = np.stack([np.cos(angd), np.sin(angd) * sgn[:, None]]).astype(f32)
    half = 32
    freqs_b = (np.float32(10000.0) ** (-np.arange(half, dtype=f32) / half)).astype(f32)
    angb = (t.astype(f32)[:, None] * freqs_b).astype(f32)
    angbd = angb[:, pair].T.astype(np.float64)
    tabB = np.stack([np.cos(angbd), np.sin(angbd) * sgn[:, None]]).astype(f32)
    seq_of_blk = (np.arange(NB) * 128) // seqlen
    maskA = np.zeros((NCH, NB), f32)
    for C in range(NCH):
        sq = (C * CH) // seqlen
        maskA[C, :] = np.where(seq_of_blk == sq, 0.0, NEG)
    maskA = np.broadcast_to(maskA.reshape(1, -1), (128, NCH * NB)).copy()
    maskW = np.zeros((NB, 3), f32)
    for i in range(NB):
        for o in range(3):
            jb = i + o - 1
            if jb < 0 or jb >= NB or seq_of_blk[jb] != seq_of_blk[i]:
                maskW[i, o] = NEG
    maskW = np.broadcast_to(maskW.reshape(1, -1), (128, NB * 3)).copy()
    cps = seqlen // 128
    rf = np.array([0.0 if (n % cps == 0) else 1.0 for n in range(NB)], f32)
    rb = np.array([0.0 if (n % cps == cps - 1) else 1.0 for n in range(NB)], f32)
    rfb = np.broadcast_to(np.concatenate([rf, rb])[None, :], (128, 64)).copy()
    return {"tabA": tabA, "tabB": tabB, "maskA": maskA, "maskW": maskW, "rfb": rfb}


def _swap(cols):
    cols = np.asarray(cols)
    return cols ^ 1


def _prep_common(norm_g, w_in_ab, qk_norm_a, ret_decay, w_out_ab, w_in_c, sink_c, w_out_c, rel_bias, final_norm):
    f32 = np.float32
    W = np.asarray(w_in_ab[0], f32)
    qa = np.arange(0, 512)
    ka = np.arange(512, 640)
    va = np.arange(640, 768)
    ga = np.arange(768, 1280)
    qb = np.arange(1280, 1536)
    kb = np.arange(1536, 1792)
    vb = np.arange(1792, 2304)
    gb = np.arange(2304, 2816)
    kadup = np.concatenate([ka[0:64], ka[0:64], ka[64:128], ka[64:128]])
    cm = {}
    cm["wG"] = np.ascontiguousarray(W[:, np.concatenate([kadup, _swap(kadup), kb, _swap(kb), va, vb])])
    cm["wLB"] = np.ascontiguousarray(W[:, np.concatenate([qb, _swap(qb), kb, _swap(kb), gb, vb])])
    cm["wLA"] = np.ascontiguousarray(W[:, np.concatenate([qa, _swap(qa), ga])])
    cm["woab"] = np.ascontiguousarray(np.asarray(w_out_ab[0], f32))
    Wc = np.asarray(w_in_c[0], f32)
    kc = np.arange(1024, 1152)
    kcdup = np.concatenate([kc[0:64], kc[0:64], kc[64:128], kc[64:128]])
    cm["wG1"] = np.ascontiguousarray(Wc[:, np.concatenate([kcdup, np.arange(1152, 1280)])])
    cm["wL1"] = np.ascontiguousarray(Wc[:, np.concatenate([np.arange(0, 1024), np.arange(1280, 2304)])])
    cm["woc"] = np.ascontiguousarray(np.asarray(w_out_c[0], f32))
    ng = np.asarray(norm_g, f32)
    cm["gcol"] = np.ascontiguousarray(ng.reshape(2, 8, 128).transpose(2, 0, 1).reshape(128, 16))
    cm["fn"] = np.asarray(final_norm, f32).reshape(1, 1024).copy()
    g = np.asarray(qk_norm_a[0], f32)
    d = np.arange(128) % 64
    cm["gqk"] = np.stack([g[0][d], g[0][d ^ 1], g[1][d], g[1][d ^ 1]], axis=1).astype(f32).copy()
    rd = np.asarray(ret_decay[0], f32)
    hp = (np.arange(128) // 64)
    rdec = np.zeros((128, 12), f32)
    for p in range(2):
        rdec[:, p] = rd[0][2 * p + hp]
        rdec[:, 2 + p] = rd[1][2 * p + hp]
    for h in range(4):
        rdec[:, 4 + h] = rd[0][h]
        rdec[:, 8 + h] = rd[1][h]
    cm["rdec"] = rdec
    sk = np.asarray(sink_c[0], f32)
    sinkl = np.zeros((128, 8), f32)
    for t in range(8):
        sinkl[:, t] = sk[2 * t + hp]
    cm["sinkl"] = sinkl
    cm["relb"] = np.ascontiguousarray(np.asarray(rel_bias, f32))
    cm.update(_static_tables())
    return cm


_CACHE = {}


def kernel(x_prompt, x_sample, norm_g, w_in_ab, qk_norm_a, ret_decay, w_out_ab, w_in_c, sink_c, w_out_c, rel_bias, final_norm):
    xp = np.asarray(x_prompt, np.float32)
    xs = np.asarray(x_sample, np.float32)
    cm = _prep_common(norm_g, w_in_ab, qk_norm_a, ret_decay, w_out_ab, w_in_c, sink_c, w_out_c, rel_bias, final_norm)
    tp = _core_tables(True)
    tsm = _core_tables(False)
    in_maps = []
    for c in range(8):
        m = dict(cm)
        if c < 4:
            m["x"] = np.ascontiguousarray(xp[c])
            m.update(tp)
        else:
            m["x"] = np.ascontiguousarray(xs[2 * (c - 4):2 * (c - 4) + 2].reshape(NT, 1024))
            m.update(tsm)
        in_maps.append(m)
    if "nc" not in _CACHE:
        _CACHE["nc"] = build_program()[0]
    nc = _CACHE["nc"]
    res = run_bass_kernel_spmd(nc, in_maps, core_ids=list(range(8)))
    outs = [np.asarray(r["y"], np.float32) for r in res.results]
    y_prompt = np.stack(outs[0:4], axis=0)
    y_sample = np.stack(outs[4:8], axis=0).reshape(8, 2048, 1024)
    return (y_prompt, y_sample)
```

```python
import numpy as np
import concourse.bass as bass
import concourse.mybir as mybir
from concourse.bass_utils import run_bass_kernel_spmd

F32 = mybir.dt.float32
BF16 = mybir.dt.bfloat16
AF = mybir.ActivationFunctionType
ALU = mybir.AluOpType

NT = 4096
NB = 32
CH = 512
NCH = 8
EPS = 1e-6
NEG = -30000.0


class Prod:
    def __init__(self, sem, inc):
        self.sem = sem
        self.inc = inc
        self.cnt = 0


class Res:
    def __init__(self):
        self.w = {}
        self.r = {}
        self.excl = False


class V:
    def __init__(self, ap, res=None):
        self.ap = ap
        self.res = res if res is not None else Res()

    def __getitem__(self, k):
        return V(self.ap[k], self.res)

    def re(self, pat, **kw):
        return V(self.ap.rearrange(pat, **kw), self.res)

    def bc(self, shape):
        return V(self.ap.to_broadcast(shape), self.res)


class Ker:
    def __init__(self, nc):
        self.nc = nc
        self.eng = {"pe": nc.tensor, "act": nc.scalar, "dve": nc.vector, "pool": nc.gpsimd, "sp": nc.sync}
        self.prod = {}
        for n in ("pe", "act", "dve", "pool"):
            self.prod[n] = Prod(nc.alloc_semaphore("s_" + n), 1)
        self.seen = {n: {} for n in self.eng}
        self.nslot = 0
        self.ninstr = 0

    def slot(self):
        self.nslot += 1
        p = Prod(self.nc.alloc_semaphore("d%d" % self.nslot), 16)
        if hasattr(self, "slots"):
            self.slots.append(p)
        return p

    def _wait(self, en, reads, writes):
        deps = {}
        for v in reads:
            for p, i in v.res.w.items():
                deps[p] = max(deps.get(p, 0), i)
        for v in writes:
            for p, i in v.res.w.items():
                deps[p] = max(deps.get(p, 0), i)
            for p, i in v.res.r.items():
                deps[p] = max(deps.get(p, 0), i)
        e = self.eng[en]
        seen = self.seen[en]
        own = self.prod.get(en)
        for p, i in deps.items():
            if p is own and en == "pe":
                continue
            if seen.get(p, 0) >= i:
                continue
            e.wait_ge(p.sem, i)
            seen[p] = i

    def op(self, en, fn, reads, writes):
        writes = list(writes) + [r for r in reads if r.res.excl]
        self._wait(en, reads, writes)
        ins = fn(self.eng[en])
        p = self.prod[en]
        p.cnt += 1
        ins.then_inc(p.sem, 1)
        for v in reads:
            v.res.r[p] = p.cnt
        for v in writes:
            v.res.w[p] = p.cnt
        self.ninstr += 1

    def dma(self, q, out, in_, slot):
        self._wait(q, [in_], [out])
        ins = self.eng[q].dma_start(out=out.ap, in_=in_.ap)
        slot.cnt += 16
        ins.then_inc(slot.sem, 16)
        in_.res.r[slot] = slot.cnt
        out.res.w[slot] = slot.cnt

    def mm(self, out, lhsT, rhs, start=True, stop=True, tp=None):
        kw = {}
        if tp is not None:
            kw["tile_position"] = tp
        self.op("pe", lambda e: e.matmul(out.ap, lhsT.ap, rhs.ap, start=start, stop=stop, **kw), [lhsT, rhs], [out])

    def tr(self, out, in_, ident):
        self.op("pe", lambda e: e.transpose(out.ap, in_.ap, ident.ap), [in_, ident], [out])

    def act(self, out, in_, func, bias=None, scale=1.0, accum=None):
        reads = [in_]
        kw = {}
        if bias is not None:
            if isinstance(bias, V):
                reads.append(bias)
                kw["bias"] = bias.ap
            else:
                kw["bias"] = bias
        if isinstance(scale, V):
            reads.append(scale)
            kw["scale"] = scale.ap
        else:
            kw["scale"] = scale
        writes = [out]
        if accum is not None:
            writes.append(accum)
            kw["accum_out"] = accum.ap
        self.op("act", lambda e: e.activation(out.ap, in_.ap, func, **kw), reads, writes)

    def tt(self, en, out, a, b, op):
        self.op(en, lambda e: e.tensor_tensor(out.ap, a.ap, b.ap, op), [a, b], [out])

    def stt(self, en, out, in0, scalar, in1, op0, op1):
        reads = [in0, in1]
        s = scalar
        if isinstance(scalar, V):
            reads.append(scalar)
            s = scalar.ap
        self.op(en, lambda e: e.scalar_tensor_tensor(out.ap, in0.ap, s, in1.ap, op0, op1), reads, [out])

    def ts(self, en, out, in0, s1, op0, s2=None, op1=None):
        reads = [in0]
        a1 = s1
        if isinstance(s1, V):
            reads.append(s1)
            a1 = s1.ap
        a2 = s2
        if isinstance(s2, V):
            reads.append(s2)
            a2 = s2.ap
        if op1 is None:
            self.op(en, lambda e: e.tensor_scalar(out.ap, in0.ap, a1, None, op0), reads, [out])
        else:
            self.op(en, lambda e: e.tensor_scalar(out.ap, in0.ap, a1, a2, op0, op1), reads, [out])

    def cp(self, en, out, in_):
        if en == "act":
            self.op("act", lambda e: e.copy(out.ap, in_.ap), [in_], [out])
        else:
            self.op(en, lambda e: e.tensor_copy(out.ap, in_.ap), [in_], [out])

    def amul(self, out, in_, m):
        self.op("act", lambda e: e.mul(out.ap, in_.ap, m.ap), [in_, m], [out])

    def recip(self, out, in_):
        self.op("dve", lambda e: e.reciprocal(out.ap, in_.ap), [in_], [out])

    def memset(self, en, out, val):
        self.op(en, lambda e: e.memset(out.ap, val), [], [out])


class StopBuild(Exception):
    pass


import contextlib


class Scope(contextlib.ExitStack):
    def __init__(self, K):
        super().__init__()
        self.K = K
        self.tiles = []

    def __exit__(self, *a):
        fr = self.K.freed
        for v in self.tiles:
            for d in (v.res.w, v.res.r):
                for p, i in d.items():
                    fr[p] = max(fr.get(p, 0), i)
        self.tiles = []
        return super().__exit__(*a)

    def close(self):
        self.__exit__(None, None, None)


def build_program(stop=None, taps=()):
    nc = bass.Bass("TRN2", target_bir_lowering=False)
    K = Ker(nc)
    K.slots = []
    K.freed = {}
    K.tapped = {}

    def checkpoint(name):
        if stop == name:
            raise StopBuild()

    def tap(name, v, shape, dt=F32):
        if name not in taps or name in K.tapped:
            return
        d = V(nc.dram_tensor("dbg_" + name, list(shape), dt, kind="ExternalOutput").ap())
        K.tapped[name] = d
        K.dma("sp", d, v, K.slot())

    def din(name, shape, dt=F32):
        return V(nc.dram_tensor(name, list(shape), dt, kind="ExternalInput").ap())

    x_d = din("x", [NT, 1024])
    wG_d = din("wG", [1024, 1664])
    wLB_d = din("wLB", [1024, 2048])
    wLA_d = din("wLA", [1024, 1536])
    woab_d = din("woab", [1024, 1024])
    wG1_d = din("wG1", [1024, 384])
    wL1_d = din("wL1", [1024, 2048])
    woc_d = din("woc", [1024, 1024])
    gcol_d = din("gcol", [128, 16])
    fn_d = din("fn", [1, 1024])
    gqk_d = din("gqk", [128, 4])
    rdec_d = din("rdec", [128, 12])
    sink_d = din("sinkl", [128, 8])
    relb_d = din("relb", [32, 16])
    ident_d = din("ident", [128, 128])
    onesblk_d = din("onesblk", [128, 128])
    mm_d = din("mmat", [128, 4, 128])
    iot_d = din("iot", [128, 4, 512])
    oh_d = din("oh", [32, 640])
    inwin_d = din("inwin", [16, 640])
    tabA_d = din("tabA", [2, 128, NT])
    tabB_d = din("tabB", [2, 128, NT])
    maskA_d = din("maskA", [128, 256])
    maskW_d = din("maskW", [128, 96])
    rfb_d = din("rfb", [128, 64])
    y_d = V(nc.dram_tensor("y", [NT, 1024], F32, kind="ExternalOutput").ap())
    x1_d = V(nc.dram_tensor("x1s", [NT, 1024], F32, kind="Internal").ap())
    mixb_d = V(nc.dram_tensor("mixbs", [4, 128, NT], BF16, kind="Internal").ap())
    vec_d = V(nc.dram_tensor("vecs", [16, 640], BF16, kind="Internal").ap())
    xnT0_d = V(nc.dram_tensor("xnT0s", [NCH, 128, 8, CH], BF16, kind="Internal").ap())
    xnT1_d = V(nc.dram_tensor("xnT1s", [NCH, 128, 8, CH], BF16, kind="Internal").ap())

    es = Scope(K)
    uid = [0]

    def sb(name, shape, dt=F32, stack=None):
        uid[0] += 1
        st_ = stack if stack is not None else es
        t = st_.enter_context(nc.sbuf_tensor("sb%d_%s" % (uid[0], name), list(shape), dt))
        v = V(t[:])
        v.res.w = dict(K.freed)
        st_.tiles.append(v)
        return v

    def ps(name, shape, dt=F32, stack=None):
        uid[0] += 1
        st_ = stack if stack is not None else es
        t = st_.enter_context(nc.psum_tensor("ps%d_%s" % (uid[0], name), list(shape), dt))
        v = V(t[:])
        v.res.excl = True
        v.res.w = dict(K.freed)
        st_.tiles.append(v)
        return v

    try:
        with es:
            cslot = K.slot()
            consts = []

            def cload(name, src, shape, dt=F32, q="sp"):
                t = sb(name, shape, dt)
                K.dma(q, t, src, cslot)
                consts.append(t)
                return t

            gcol = cload("gcol", gcol_d, [128, 16])
            gqk = cload("gqk", gqk_d, [128, 4])
            rdec = cload("rdec", rdec_d, [128, 12])
            sinkl = cload("sinkl", sink_d, [128, 8])
            maskA = cload("maskA", maskA_d, [128, 256])
            maskW = cload("maskW", maskW_d, [128, 96])
            rfb = cload("rfb", rfb_d, [128, 64])
            ident32 = cload("ident32", ident_d, [128, 128])
            onesblk32 = cload("onesblk32", onesblk_d, [128, 128])
            for c in consts:
                c.res.w[cslot] = cslot.cnt
            ident = sb("ident", [128, 128], BF16)
            onesblk = sb("onesblk", [128, 128], BF16)
            ones = sb("ones", [128, 128], BF16)
            epsb = sb("epsb", [128, 1])
            K.cp("dve", ident, ident32)
            K.cp("dve", onesblk, onesblk32)
            K.memset("dve", ones, 1.0)
            K.memset("dve", epsb, EPS)

            checkpoint("c0")
            wbf = sb("wbf", [128, 8, 2048], BF16)
            wobf = sb("wobf", [128, 8, 1024], BF16)
            wst = [sb("wst%d" % i, [128, 1024]) for i in range(2)]
            wst_slot = [K.slot() for _ in range(2)]
            xnTs = [sb("xnT%d" % i, [128, 8, CH], BF16) for i in range(2)]
            xnT_slot = [K.slot() for _ in range(2)]
            cur = {"xnT": xnTs[0]}
            xch_slot = [K.slot() for _ in range(4)]
            xst_slot = [K.slot() for _ in range(2)]
            EBT = sb("EBT", [128, 16, 3, 128], BF16)
            esk = sb("esk", [128, 8], F32)
            K.act(esk, sinkl, AF.Exp)
            with Scope(K) as S1:
                relb = sb("relb", [32, 16], F32, S1)
                oh = sb("oh", [32, 640], F32, S1)
                inw = sb("inw", [16, 640], F32, S1)
                e_slot = K.slot()
                K.dma("sp", relb, relb_d, e_slot)
                K.dma("sp", oh, oh_d, e_slot)
                K.dma("sp", inw, inwin_d, e_slot)
                for t_ in (relb, oh, inw):
                    t_.res.w[e_slot] = e_slot.cnt
                pv = ps("pv", [16, 1024], F32, S1)[:, 0:640]
                vec = sb("vec", [16, 640], F32, S1)
                vecb = sb("vecb", [16, 640], BF16, S1)
                K.mm(pv[:, 0:512], relb, oh[:, 0:512])
                K.mm(pv[:, 512:640], relb, oh[:, 512:640])
                K.act(vec, pv, AF.Exp)
                K.tt("dve", vecb, vec, inw, ALU.mult)
                v_slot = K.slot()
                K.dma("sp", vec_d, vecb, v_slot)
                g_slot = [K.slot() for _ in range(2)]
                for k in range(128):
                    qn = "sp" if (k % 2 == 0) else "pool"
                    K.dma(qn, EBT[k:k + 1, :, :, :], V(vec_d.ap[:, 127 - k:127 - k + 384].rearrange("(a h) (o q) -> a h o q", a=1, o=3), vec_d.res), g_slot[k % 2])
            wcount = [0]

            def handoff(srcs, dsts):
                for d_ in dsts:
                    for s_ in srcs:
                        for dd in (s_.res.w, s_.res.r):
                            for p_, i_ in dd.items():
                                d_.res.w[p_] = max(d_.res.w.get(p_, 0), i_)

            def subview(parent, ap):
                v = V(ap)
                v.res.excl = parent.res.excl
                v.res.w = dict(parent.res.w)
                return v

            class MX:
                pass

            def alloc_mx(scope, full=True):
                m = MX()
                m.xch = sb("xch", [128, 4, 1024], F32, scope)
                if full:
                    m.xn = [sb("xn%d" % i, [128, 1024], BF16, scope) for i in range(2)]
                    m.junk = sb("junk", [128, 1024], BF16, scope)
                    m.ss = sb("ss", [128, 4], F32, scope)
                    m.lnv4 = sb("lnv4", [128, 4], F32, scope)
                    m.rstd4 = sb("rstd4", [128, 4], F32, scope)
                return m

            def load_x(m, src_d, C):
                for b in range(4):
                    r0 = C * CH + b * 128
                    K.dma("sp", m.xch[:, b, :], src_d[r0:r0 + 128, :], xch_slot[b])

            def store_xnT(dst_d, C, slot_i):
                K.dma("pool", dst_d[C], cur["xnT"], xst_slot[slot_i])

            def load_xnT(src_d, C):
                i = C % 2
                K.dma("sp", xnTs[i], src_d[C], xnT_slot[i])

            def use_xnT(C):
                cur["xnT"] = xnTs[C % 2]

            def load_w(dst, src_d, ncols, layer_g):
                for kc in range(8):
                    for c0 in range(0, ncols, 1024):
                        c1 = min(ncols, c0 + 1024)
                        i = wcount[0] % 2
                        wcount[0] += 1
                        K.dma("sp", wst[i][:, 0:c1 - c0], src_d[kc * 128:(kc + 1) * 128, c0:c1], wst_slot[i])
                        en = "act" if (wcount[0] % 2 == 0) else "dve"
                        if layer_g is None:
                            K.cp(en, dst[:, kc, c0:c1], wst[i][:, 0:c1 - c0])
                        elif en == "act":
                            K.amul(dst[:, kc, c0:c1], wst[i][:, 0:c1 - c0], gcol[:, layer_g * 8 + kc:layer_g * 8 + kc + 1])
                        else:
                            K.ts("dve", dst[:, kc, c0:c1], wst[i][:, 0:c1 - c0], gcol[:, layer_g * 8 + kc:layer_g * 8 + kc + 1], ALU.mult)

            def make_xnT(m, src_d, C, pT):
                use_xnT(C)
                xnT = cur["xnT"]
                load_x(m, src_d, C)
                for b in range(4):
                    K.act(m.junk, m.xch[:, b, :], AF.Square, accum=m.ss[:, b:b + 1])
                K.act(m.lnv4, m.ss, AF.Ln, bias=epsb[:, 0:1], scale=1.0 / 1024.0)
                K.act(m.rstd4, m.lnv4, AF.Exp, scale=-0.5)
                for b in range(4):
                    xb = m.xn[b % 2]
                    K.ts("dve", xb, m.xch[:, b, :], m.rstd4[:, b:b + 1], ALU.mult)
                    for kc in range(8):
                        K.tr(pT[:, kc, :], xb[:, kc * 128:(kc + 1) * 128], ident)
                    K.cp("act", xnT[:, :, b * 128:(b + 1) * 128], pT)

            def proj(dst, c0):
                for kc in range(8):
                    K.mm(dst, wbf[:, kc, c0:c0 + 128], cur["xnT"][:, kc, :], start=(kc == 0), stop=(kc == 7))

            def rsq_bcast(dst, src_ps, nfeat, sq, psn, lnv, lhs_ones):
                K.act(sq, src_ps, AF.Square)
                K.mm(psn, lhs_ones, sq)
                K.act(lnv, psn, AF.Ln, bias=epsb[:, 0:1], scale=1.0 / nfeat)
                K.act(dst, lnv, AF.Exp, scale=-0.5)

            with Scope(K) as L0:
                LR = Scope(K)
                KaT = sb("KaT", [128, 2, NT], BF16, L0)
                Va = sb("Va", [128, NB, 128], BF16, L0)
                tabc = sb("tabc", [128, 2, CH], F32, L0)
                tab_slot = K.slot()
                tabd_slot = K.slot()
                sq = sb("sq", [128, CH], BF16, L0)
                lnv = sb("lnv", [128, CH], F32, L0)
                rs = sb("rs", [128, CH], F32, L0)
                t1 = sb("t1", [128, CH], F32, L0)
                t2 = sb("t2", [128, CH], F32, L0)
                tabg = sb("tabg", [128, 2, CH], F32, L0)
                SbAll = sb("SbAll", [128, 2, NB, 128], BF16, LR)
                tabd = sb("tabd", [128, 2, CH], F32, LR)
                vbtm = sb("vbtm", [128, 4, 512], BF16, LR)
                lg = sb("lg", [128, 12], F32, LR)
                K.act(lg, rdec, AF.Exp)
                K.ts("dve", lg, lg, -1.0, ALU.mult)
                cd = sb("cd", [128, 4], F32, LR)
                K.act(cd, lg[:, 0:4], AF.Exp, scale=128.0)
                cdr = sb("cdr", [128, 4, NB], F32, LR)
                for j in range(4):
                    off = 0 if j < 2 else 32
                    K.ts("dve", cdr[:, j, :], rfb[:, off:off + 32], cd[:, j:j + 1], ALU.mult)
                checkpoint("c1")
                QF4 = sb("QF4", [128, 2, CH], F32, LR)
                QB4 = sb("QB4", [128, 2, CH], F32, LR)
                KF4 = sb("KF4", [128, 2, CH], F32, LR)
                KB4 = sb("KB4", [128, 2, CH], F32, LR)
                DT = sb("DT", [128, 4, 128], F32, LR)
                with Scope(K) as S0:
                    iot = sb("iot", [128, 4, CH], F32, S0)
                    K.dma("sp", iot, iot_d, tabd_slot)
                    for p in range(2):
                        K.act(QF4[:, p, :], iot[:, 0, :], AF.Exp, scale=lg[:, p:p + 1])
                        K.act(QB4[:, p, :], iot[:, 1, :], AF.Exp, scale=lg[:, 2 + p:3 + p])
                        K.act(KF4[:, p, :], iot[:, 2, :], AF.Exp, scale=lg[:, p:p + 1])
                        K.act(KB4[:, p, :], iot[:, 3, :], AF.Exp, scale=lg[:, 2 + p:3 + p])
                    K.ts("dve", KF4, KF4, 0.125, ALU.mult)
                    K.ts("dve", KB4, KB4, 0.125, ALU.mult)
                    mmat = sb("mmat", [128, 4, 128], F32, S0)
                    K.dma("sp", mmat, mm_d, tab_slot)
                    d1 = sb("d1", [128, 128], F32, S0)
                    d2 = sb("d2", [128, 128], F32, S0)
                    for h in range(4):
                        checkpoint("d0")
                        K.act(d1, mmat[:, 0, :], AF.Exp, scale=lg[:, 4 + h:5 + h])
                        checkpoint("d1")
                        K.tt("dve", d1, d1, mmat[:, 1, :], ALU.mult)
                        checkpoint("d2")
                        K.act(d2, mmat[:, 2, :], AF.Exp, scale=lg[:, 8 + h:9 + h])
                        K.tt("dve", d2, d2, mmat[:, 3, :], ALU.mult)
                        K.tt("dve", d1, d1, d2, ALU.add)
                        checkpoint("d3")
                        K.ts("dve", DT[:, h, :], d1, 0.125, ALU.mult)
                        checkpoint("d4")

                def load_tab(dst, slot, src_d, C):
                    K.dma("sp", dst, V(src_d.ap[:, :, C * CH:(C + 1) * CH].rearrange("t p c -> p t c"), src_d.res), slot)

                def rope(psa, psb, tab, out32, ga=None, gb=None):
                    if ga is None:
                        K.tt("dve", t1, psa, tab[:, 0, :], ALU.mult)
                        K.tt("dve", t2, psb, tab[:, 1, :], ALU.mult)
                    else:
                        K.amul(tabg[:, 0, :], tab[:, 0, :], ga)
                        K.amul(tabg[:, 1, :], tab[:, 1, :], gb)
                        K.tt("dve", t1, psa, tabg[:, 0, :], ALU.mult)
                        K.tt("dve", t2, psb, tabg[:, 1, :], ALU.mult)
                    K.tt("pool", out32, t1, t2, ALU.add)

                checkpoint("setup0")
                load_w(wbf, wG_d, 1664, 0)
                with Scope(K) as PG:
                    pT = ps("pT", [128, 8, 128], BF16, PG)
                    pa = ps("pa", [128, CH], F32, PG)
                    pb = ps("pb", [128, CH], F32, PG)
                    pn = ps("pn", [128, CH], F32, PG)
                    pk = ps("pk", [128, 4, 2, 128], BF16, PG)
                    pva = ps("pva", [128, 512], F32, PG)[:, 0:128]
                    pvb = ps("pvb", [128, 512], F32, PG)
                    pkv = ps("pkv", [128, 4, 128], F32, PG)[:, 0:2, :]
                    kdbT = sb("kdbT", [128, 2, CH], BF16, PG)
                    kdbtm = sb("kdbtm", [128, 4, 2, 128], BF16, PG)
                    Rb = sb("Rb", [128, 2, 128], F32, PG)
                    mxg = alloc_mx(PG)
                    K.memset("dve", Rb, 0.0)
                    checkpoint("g_w")
                    for C in range(NCH - 1, -1, -1):
                        make_xnT(mxg, x_d, C, pT)
                        store_xnT(xnT0_d, C, C % 2)
                        checkpoint("g_x"); checkpoint("G%d_x" % C)
                        load_tab(tabc, tab_slot, tabA_d, C)
                        load_tab(tabd, tabd_slot, tabB_d, C)
                        checkpoint("g_t"); checkpoint("G%d_t" % C)
                        for t in range(2):
                            proj(pa, t * 128)
                            proj(pb, 256 + t * 128)
                            checkpoint("k0"); checkpoint("G%d_%d_k0" % (C, t))
                            rsq_bcast(rs, pa, 64.0, sq, pn, lnv, onesblk)
                            checkpoint("k1"); checkpoint("G%d_%d_k1" % (C, t))
                            rope(pa, pb, tabc, t1, gqk[:, 2:3], gqk[:, 3:4])
                            checkpoint("k2"); checkpoint("G%d_%d_k2" % (C, t))
                            K.tt("pool", KaT[:, t, C * CH:(C + 1) * CH], t1, rs, ALU.mult)
                            checkpoint("k3"); checkpoint("G%d_%d_k3" % (C, t))
                        checkpoint("g_ka"); checkpoint("G%d_ka" % C)
                        for t in range(2):
                            proj(pa, 512 + t * 128)
                            proj(pb, 768 + t * 128)
                            rope(pa, pb, tabd, t1)
                            K.tt("pool", kdbT[:, t, :], t1, KB4[:, t, :], ALU.mult)
                            for cj in range(4):
                                K.tr(pk[:, cj, t, :], kdbT[:, t, cj * 128:(cj + 1) * 128], ident)
                        K.cp("act", kdbtm, pk)
                        checkpoint("g_kb"); checkpoint("G%d_kb" % C)
                        for b in range(4):
                            for kc in range(8):
                                K.mm(pva, cur["xnT"][:, kc, b * 128:(b + 1) * 128], wbf[:, kc, 1024:1152], start=(kc == 0), stop=(kc == 7))
                            for kc in range(8):
                                K.mm(pvb, cur["xnT"][:, kc, b * 128:(b + 1) * 128], wbf[:, kc, 1152:1664], start=(kc == 0), stop=(kc == 7))
                            K.cp("act", Va[:, C * 4 + b, :], pva)
                            K.cp("dve", vbtm[:, b, :], pvb)
                        checkpoint("g_v"); checkpoint("G%d_v" % C)
                        for cj in range(3, -1, -1):
                            n = C * 4 + cj
                            for p in range(2):
                                K.mm(pkv[0:64, p, :], kdbtm[:, cj, p, 0:64], vbtm[:, cj, (2 * p) * 128:(2 * p + 1) * 128])
                                K.mm(pkv[64:128, p, :], kdbtm[:, cj, p, 64:128], vbtm[:, cj, (2 * p + 1) * 128:(2 * p + 2) * 128], tp=(0, 64))
                            checkpoint("s0")
                            K.ts("dve", SbAll[:, :, n, :], Rb, rfb[:, 32 + n:33 + n], ALU.mult)
                            checkpoint("s1")
                            for p in range(2):
                                K.ts("dve", Rb[:, p, :], Rb[:, p, :], cdr[:, 2 + p, n:n + 1], ALU.mult)
                                checkpoint("s2")
                                K.tt("dve", Rb[:, p, :], pkv[:, p, :], Rb[:, p, :], ALU.add)
                                checkpoint("s3")
                            checkpoint("s4")
                        checkpoint("g_c1"); checkpoint("G%d_end" % C)

                tap("KaT", KaT, [128, 2, NT], BF16)
                tap("Va", Va, [128, NB, 128], BF16)
                tap("SbAll", SbAll, [128, 2, NB, 128], BF16)
                checkpoint("G")
                load_w(wbf, wLB_d, 2048, 0)
                with Scope(K) as PB:
                    pT = ps("pT", [128, 8, 128], BF16, PB)
                    pa = ps("pa", [128, CH], F32, PB)
                    pb = ps("pb", [128, CH], F32, PB)
                    pk = pT.re("p (c t) q -> p c t q", t=2)
                    pss = ps("pss", [128, 512], F32, PB)
                    po = ps("po", [128, 4, CH], F32, PB)
                    qrT = sb("qrT", [128, 2, CH], BF16, PB)
                    qdf = sb("qdf", [128, 2, CH], BF16, PB)
                    qdb = sb("qdb", [128, 2, CH], BF16, PB)
                    krT = sb("krT", [128, 2, CH], BF16, PB)
                    kdfT = sb("kdfT", [128, 2, CH], BF16, PB)
                    kdftm = sb("kdftm", [128, 4, 2, 128], BF16, PB)
                    sg = sb("sg", [128, 4, CH], BF16, PB)
                    AT = sb("AT", [128, 4, 128], BF16, PB)
                    Sf = sb("Sf", [128, 2, 128], BF16, PB)
                    Rf = sb("Rf", [128, 2, 128], F32, PB)
                    mixBc = [sb("mixBc%d" % i, [128, 4, CH], BF16, PB) for i in range(1)]
                    mixB_slot = [K.slot() for _ in range(1)]
                    K.memset("dve", Rf, 0.0)
                    load_xnT(xnT0_d, 0)
                    for C in range(NCH):
                        use_xnT(C)
                        if C + 1 < NCH:
                            load_xnT(xnT0_d, C + 1)
                        load_tab(tabd, tabd_slot, tabB_d, C)
                        for t in range(2):
                            proj(pa, t * 128)
                            proj(pb, 256 + t * 128)
                            rope(pa, pb, tabd, t1)
                            K.cp("act", qrT[:, t, :], t1)
                            K.tt("pool", qdf[:, t, :], t1, QF4[:, t, :], ALU.mult)
                            K.tt("pool", qdb[:, t, :], t1, QB4[:, t, :], ALU.mult)
                        for t in range(2):
                            proj(pa, 512 + t * 128)
                            proj(pb, 768 + t * 128)
                            rope(pa, pb, tabd, t1)
                            K.cp("act", krT[:, t, :], t1)
                            K.tt("pool", kdfT[:, t, :], t1, KF4[:, t, :], ALU.mult)
                            for cj in range(4):
                                K.tr(pk[:, cj, t, :], kdfT[:, t, cj * 128:(cj + 1) * 128], ident)
                        K.cp("act", kdftm, pk)
                        for h in range(4):
                            proj(pa, 1024 + h * 128)
                            K.act(sg[:, h, :], pa, AF.Silu)
                        for b in range(4):
                            for kc in range(8):
                                K.mm(pb, cur["xnT"][:, kc, b * 128:(b + 1) * 128], wbf[:, kc, 1536:2048], start=(kc == 0), stop=(kc == 7))
                            K.cp("dve", vbtm[:, b, :], pb)
                        for cj in range(4):
                            n = C * 4 + cj
                            cs = slice(cj * 128, (cj + 1) * 128)
                            K.ts("dve", Sf, Rf, rfb[:, n:n + 1], ALU.mult)
                            for p in range(2):
                                K.mm(pa[0:64, p * 128:(p + 1) * 128], kdftm[:, cj, p, 0:64], vbtm[:, cj, (2 * p) * 128:(2 * p + 1) * 128])
                                K.mm(pa[64:128, p * 128:(p + 1) * 128], kdftm[:, cj, p, 64:128], vbtm[:, cj, (2 * p + 1) * 128:(2 * p + 2) * 128], tp=(0, 64))
                            for p in range(2):
                                K.ts("dve", Rf[:, p, :], Rf[:, p, :], cdr[:, p, n:n + 1], ALU.mult)
                                K.tt("dve", Rf[:, p, :], pa[:, p * 128:(p + 1) * 128], Rf[:, p, :], ALU.add)
                            for h in range(4):
                                t, r0 = h // 2, (h % 2) * 64
                                pdst = pss if (h % 2 == 0) else pb
                                K.mm(pdst[:, t * 128:(t + 1) * 128], krT[r0:r0 + 64, t, cs], qrT[r0:r0 + 64, t, cs])
                            ATv = AT.re("p (t hp) i -> p hp t i", hp=2)
                            DTv = DT.re("p (t hp) i -> p hp t i", hp=2)
                            K.tt("dve", ATv[:, 0, :, :], pss[:, 0:256].re("p (t i) -> p t i", t=2), DTv[:, 0, :, :], ALU.mult)
                            K.tt("dve", ATv[:, 1, :, :], pb[:, 0:256].re("p (t i) -> p t i", t=2), DTv[:, 1, :, :], ALU.mult)
                            for h in range(4):
                                t, r0 = h // 2, (h % 2) * 64
                                K.mm(po[:, h, cs], vbtm[:, cj, h * 128:(h + 1) * 128], AT[:, h, :], start=True, stop=False)
                                K.mm(po[:, h, cs], Sf[r0:r0 + 64, t, :], qdf[r0:r0 + 64, t, cs], start=False, stop=False)
                                K.mm(po[:, h, cs], SbAll[r0:r0 + 64, t, n, :], qdb[r0:r0 + 64, t, cs], start=False, stop=True)
                        mb = mixBc[0]
                        for h in range(4):
                            rsq_bcast(rs, po[:, h, :], 128.0, sq, pb, lnv, ones)
                            K.tt("dve", t1, po[:, h, :], rs, ALU.mult)
                            K.tt("pool", mb[:, h, :], t1, sg[:, h, :], ALU.mult)
                        K.dma("pool", V(mixb_d.ap[:, :, C * CH:(C + 1) * CH].rearrange("h p c -> p h c"), mixb_d.res), mb, mixB_slot[0])

                tap("mixb", mixb_d, [4, 128, NT], BF16)
                checkpoint("LB")
                LR.close()
                load_w(wbf, wLA_d, 1536, 0)
                load_w(wobf, woab_d, 1024, None)
                with Scope(K) as PA:
                    pbig = ps("pbig", [128, 8, CH], F32, PA)
                    psc = [subview(pbig, pbig.ap[:, 2 * i:2 * i + 2, :]) for i in range(3)]
                    pnum = subview(pbig, pbig.ap[:, 6, :])
                    pden = subview(pbig, pbig.ap[:, 7, :])
                    pa = subview(pbig, pbig.ap[:, 0, :])
                    pb = subview(pbig, pbig.ap[:, 1, :])
                    qaT = sb("qaT", [128, 4, CH], BF16, PA)
                    sga = sb("sga", [128, 4, CH], BF16, PA)
                    mixA = sb("mixA", [128, 4, CH], BF16, PA)
                    mixBl = sb("mixBl", [128, 4, CH], BF16, PA)
                    mixBl_slot = K.slot()
                    pTs = [sb("pTs%d" % i, [128, 2, CH], BF16, PA) for i in range(3)]
                    x1b = [sb("x1b%d" % i, [128, 1024], F32, PA) for i in range(2)]
                    dcp = sb("dcp", [128, CH], F32, PA)
                    ncp = sb("ncp", [128, CH], F32, PA)
                    x1b_slot = [K.slot() for _ in range(2)]
                    it = 0
                    mxa = alloc_mx(PA, full=False)
                    xch = mxa.xch
                    load_xnT(xnT0_d, 0)
                    for C in range(NCH):
                        use_xnT(C)
                        if C + 1 < NCH:
                            load_xnT(xnT0_d, C + 1)
                        load_x(mxa, x_d, C)
                        handoff([psc[0]], [pa, pb])
                        load_tab(tabc, tab_slot, tabA_d, C)
                        K.dma("pool", mixBl, V(mixb_d.ap[:, :, C * CH:(C + 1) * CH].rearrange("h p c -> p h c"), mixb_d.res), mixBl_slot)
                        for t in range(4):
                            proj(pa, t * 128)
                            proj(pb, 512 + t * 128)
                            rsq_bcast(rs, pa, 64.0, sq, pnum, lnv, onesblk)
                            rope(pa, pb, tabc, t1, gqk[:, 0:1], gqk[:, 1:2])
                            K.tt("pool", qaT[:, t, :], t1, rs, ALU.mult)
                        for t in range(4):
                            proj(pa, 1024 + t * 128)
                            K.act(sga[:, t, :], pa, AF.Silu)
                        handoff([pa, pb], [psc[0]])
                        for t in range(4):
                            kv = t // 2
                            def qk(kb_, sc_):
                                ks = slice(kb_ * 128, (kb_ + 1) * 128)
                                K.mm(sc_[:, 0, :], KaT[0:64, kv, ks], qaT[0:64, t, :])
                                K.mm(sc_[:, 1, :], KaT[64:128, kv, ks], qaT[64:128, t, :])

                            qk(0, psc[it % 3])
                            qk(1, psc[(it + 1) % 3])
                            for kb in range(NB):
                                sc = psc[it % 3]
                                pt = pTs[it % 3]
                                it += 1
                                K.act(pt, sc, AF.Exp, bias=maskA[:, C * NB + kb:C * NB + kb + 1], scale=0.125)
                                if kb + 2 < NB:
                                    qk(kb + 2, psc[(it + 1) % 3])
                                st, sp_ = (kb == 0), (kb == NB - 1)
                                K.mm(pnum[0:64, :], Va[:, kb, kv * 64:(kv + 1) * 64], pt[:, 0, :], start=st, stop=sp_)
                                K.mm(pnum[64:128, :], Va[:, kb, kv * 64:(kv + 1) * 64], pt[:, 1, :], start=st, stop=sp_, tp=(0, 64))
                                K.mm(pden[0:64, :], ones[:, 0:64], pt[:, 0, :], start=st, stop=sp_)
                                K.mm(pden[64:128, :], ones[:, 0:64], pt[:, 1, :], start=st, stop=sp_, tp=(0, 64))
                            K.cp("dve", dcp, pden)
                            K.cp("dve", ncp, pnum)
                            K.recip(dcp, dcp)
                            K.tt("dve", ncp, ncp, dcp, ALU.mult)
                            K.tt("pool", mixA[:, t, :], ncp, sga[:, t, :], ALU.mult)
                        for b in range(4):
                            bs = slice(b * 128, (b + 1) * 128)
                            py = psc[1 + b % 2]
                            for half in range(2):
                                for f in range(8):
                                    src = mixA[:, f, bs] if f < 4 else mixBl[:, f - 4, bs]
                                    K.mm(py[:, half, :], src, wobf[:, f, half * 512:(half + 1) * 512], start=(f == 0), stop=(f == 7))
                            xo = x1b[b % 2]
                            K.tt("dve", xo, py.re("p a c -> p (a c)"), xch[:, b, :], ALU.add)
                            r0 = C * CH + b * 128
                            K.dma("pool", x1_d[r0:r0 + 128, :], xo, x1b_slot[b % 2])

            tap("x1", x1_d, [NT, 1024], F32)
            checkpoint("LA")
            with Scope(K) as L1:
                KcT = sb("KcT", [128, 2, (NB + 2) * 128], BF16, L1)
                Vc = sb("Vc", [128, NB + 2, 128], BF16, L1)
                K.memset("pool", KcT[:, :, 0:128], 0.0)
                K.memset("pool", KcT[:, :, (NB + 1) * 128:(NB + 2) * 128], 0.0)
                K.memset("pool", Vc[:, 0, :], 0.0)
                K.memset("pool", Vc[:, NB + 1, :], 0.0)
                tap("EBT", EBT, [128, 16, 3, 128], BF16)
                checkpoint("EBT")
                load_w(wbf, wG1_d, 384, 1)
                with Scope(K) as PG1:
                    pT = ps("pT", [128, 8, 128], BF16, PG1)
                    pa = ps("pa", [128, CH], F32, PG1)
                    pva = ps("pva", [128, 512], F32, PG1)[:, 0:128]
                    mxg1 = alloc_mx(PG1)
                    for C in range(NCH):
                        make_xnT(mxg1, x1_d, C, pT)
                        store_xnT(xnT1_d, C, C % 2)
                        for t in range(2):
                            proj(pa, t * 128)
                            K.cp("act", KcT[:, t, (C * 4 + 1) * 128:(C * 4 + 5) * 128], pa)
                        for b in range(4):
                            for kc in range(8):
                                K.mm(pva, cur["xnT"][:, kc, b * 128:(b + 1) * 128], wbf[:, kc, 256:384], start=(kc == 0), stop=(kc == 7))
                            K.cp("dve", Vc[:, C * 4 + b + 1, :], pva)

                tap("KcT", KcT, [128, 2, (NB + 2) * 128], BF16)
                tap("Vc", Vc, [128, NB + 2, 128], BF16)
                checkpoint("G1")
                load_w(wbf, wL1_d, 2048, 1)
                load_w(wobf, woc_d, 1024, None)
                with Scope(K) as PL1:
                    pT = ps("pT", [128, 8, 128], BF16, PL1)
                    pa = ps("pa", [128, CH], F32, PL1)
                    pw = [ps("pw%d" % i, [128, 2, CH], F32, PL1) for i in range(2)]
                    pnum = ps("pnum", [128, CH], F32, PL1)
                    pden = ps("pden", [128, CH], F32, PL1)
                    qcT = sb("qcT", [128, 8, CH], BF16, PL1)
                    sgc = sb("sgc", [128, 8, CH], BF16, PL1)
                    mixC = sb("mixC", [128, 8, CH], BF16, PL1)
                    pws = [sb("pws%d" % i, [128, 2, 3, 128], BF16, PL1) for i in range(2)]
                    pw2 = [sb("pw2%d" % i, [128, 2, 3, 128], BF16, PL1) for i in range(2)]
                    rs = sb("rs1", [128, CH], F32, PL1)
                    lnr = sb("lnr", [128, CH], F32, PL1)
                    t1 = sb("t11", [128, CH], F32, PL1)
                    x2 = [sb("x2%d" % i, [128, 1024], F32, PL1) for i in range(2)]
                    yo = [sb("yo%d" % i, [128, 1024], F32, PL1) for i in range(2)]
                    yo_slot = [K.slot() for _ in range(2)]
                    ss2 = sb("ss2", [128, 2], F32, PL1)
                    ln2 = sb("ln2", [128, 2], F32, PL1)
                    r2 = sb("r2", [128, 2], F32, PL1)
                    it = 0
                    mxl = alloc_mx(PL1, full=False)
                    xch = mxl.xch
                    junk = sb("junk1", [128, 1024], BF16, PL1)
                    fnbc = sb("fnbc", [128, 1024], F32, PL1)
                    fn_slot = K.slot()
                    K.dma("sp", fnbc, V(fn_d.ap.to_broadcast([128, 1024]), fn_d.res), fn_slot)
                    load_xnT(xnT1_d, 0)
                    for C in range(NCH):
                        use_xnT(C)
                        if C + 1 < NCH:
                            load_xnT(xnT1_d, C + 1)
                        load_x(mxl, x1_d, C)
                        for t in range(8):
                            proj(pa, t * 128)
                            K.cp("act", qcT[:, t, :], pa)
                        for t in range(8):
                            proj(pa, 1024 + t * 128)
                            K.act(sgc[:, t, :], pa, AF.Silu)
                        items = [(t, qi) for t in range(8) for qi in range(4)]

                        def wqk(t, qi, w):
                            kv = t // 4
                            i = C * 4 + qi
                            qs = slice(qi * 128, (qi + 1) * 128)
                            for o in range(3):
                                sl = 2 - o
                                ks = slice((i + o) * 128, (i + o + 1) * 128)
                                K.mm(w[:, 0, sl * 128:(sl + 1) * 128], KcT[0:64, kv, ks], qcT[0:64, t, qs])
                                K.mm(w[:, 1, sl * 128:(sl + 1) * 128], KcT[64:128, kv, ks], qcT[64:128, t, qs])

                        wqk(items[0][0], items[0][1], pw[it % 2])
                        for idx, (t, qi) in enumerate(items):
                            kv = t // 4
                            i = C * 4 + qi
                            qs = slice(qi * 128, (qi + 1) * 128)
                            w = pw[it % 2]
                            s1 = pws[it % 2]
                            s2 = pw2[it % 2]
                            it += 1
                            if i in (0, NB // 2 - 1, NB // 2, NB - 1):
                                for o in range(3):
                                    sl = 2 - o
                                    K.act(s1[:, :, sl, :], w[:, :, sl * 128:(sl + 1) * 128], AF.Exp, bias=maskW[:, i * 3 + o:i * 3 + o + 1], scale=0.125)
                            else:
                                K.act(s1, w[:, :, 0:384].re("p h (o q) -> p h o q", o=3), AF.Exp, scale=0.125)
                            K.tt("dve", s2, s1, EBT[:, 2 * t:2 * t + 2, :, :], ALU.mult)
                            if idx + 1 < len(items):
                                wqk(items[idx + 1][0], items[idx + 1][1], pw[it % 2])
                            for o in range(3):
                                sl = 2 - o
                                st, sp_ = (o == 0), (o == 2)
                                vv = Vc[:, i + o, kv * 64:(kv + 1) * 64]
                                K.mm(pnum[0:64, qs], vv, s2[:, 0, sl, :], start=st, stop=sp_)
                                K.mm(pnum[64:128, qs], vv, s2[:, 1, sl, :], start=st, stop=sp_, tp=(0, 64))
                                K.mm(pden[0:64, qs], ones[:, 0:64], s2[:, 0, sl, :], start=st, stop=sp_)
                                K.mm(pden[64:128, qs], ones[:, 0:64], s2[:, 1, sl, :], start=st, stop=sp_, tp=(0, 64))
                            if qi == 3:
                                K.ts("dve", rs, pden, esk[:, t:t + 1], ALU.add)
                                K.cp("dve", t1, pnum)
                                K.act(lnr, rs, AF.Ln)
                                K.act(rs, lnr, AF.Exp, scale=-1.0)
                                K.tt("pool", t1, t1, rs, ALU.mult)
                                K.tt("pool", mixC[:, t, :], t1, sgc[:, t, :], ALU.mult)
                        for b in range(4):
                            bs = slice(b * 128, (b + 1) * 128)
                            py = pw[b % 2]
                            for half in range(2):
                                for f in range(8):
                                    K.mm(py[:, half, :], mixC[:, f, bs], wobf[:, f, half * 512:(half + 1) * 512], start=(f == 0), stop=(f == 7))
                            xo = x2[b % 2]
                            K.tt("dve", xo, py.re("p a c -> p (a c)"), xch[:, b, :], ALU.add)
                            K.act(junk, xo, AF.Square, accum=ss2[:, b % 2:b % 2 + 1])
                            K.act(ln2[:, b % 2:b % 2 + 1], ss2[:, b % 2:b % 2 + 1], AF.Ln, bias=epsb[:, 0:1], scale=1.0 / 1024.0)
                            K.act(r2[:, b % 2:b % 2 + 1], ln2[:, b % 2:b % 2 + 1], AF.Exp, scale=-0.5)
                            yb = yo[b % 2]
                            K.ts("dve", yb, xo, r2[:, b % 2:b % 2 + 1], ALU.mult)
                            K.tt("pool", yb, yb, fnbc, ALU.mult)
                            r0 = C * CH + b * 128
                            K.dma("pool", y_d[r0:r0 + 128, :], yb, yo_slot[b % 2])
    except StopBuild:
        pass
    for s_ in K.slots:
        if s_.cnt:
            nc.gpsimd.wait_ge(s_.sem, s_.cnt)
    return nc, K


def _t5_bucket(rel):
    half = 16
    max_exact = 8
    ret = (rel > 0).astype(np.int32) * half
    dist = np.abs(rel)
    large = max_exact + (np.log(np.maximum(dist, 1) / max_exact) / np.log(128 / max_exact) * (half - max_exact)).astype(np.int32)
    large = np.minimum(large, half - 1)
    return ret + np.where(dist < max_exact, dist, large)


def _static_tables():
    f32 = np.float32
    st = {}
    st["ident"] = np.eye(128, dtype=f32)
    ob = np.zeros((128, 128), f32)
    ob[:64, :64] = 1
    ob[64:, 64:] = 1
    st["onesblk"] = ob
    j = np.arange(128)[:, None]
    i = np.arange(128)[None, :]
    mmat = np.zeros((128, 4, 128), f32)
    mmat[:, 0, :] = np.maximum(i - j, 0)
    mmat[:, 1, :] = (i >= j)
    mmat[:, 2, :] = np.maximum(j - i, 0)
    mmat[:, 3, :] = (j > i)
    st["mmat"] = mmat
    c = np.arange(512) % 128
    iot = np.zeros((128, 4, 512), f32)
    iot[:, 0, :] = c + 1
    iot[:, 1, :] = 128 - c
    iot[:, 2, :] = 127 - c
    iot[:, 3, :] = c
    st["iot"] = iot
    m = np.arange(640)
    rel = 255 - m
    bk = _t5_bucket(rel)
    oh = np.zeros((32, 640), f32)
    oh[bk, m] = 1
    st["oh"] = oh
    st["inwin"] = np.broadcast_to((np.abs(rel) <= 128).astype(f32)[None, :], (16, 640)).copy()
    return st


def _core_tables(is_prompt):
    f32 = np.float32
    seqlen = 4096 if is_prompt else 2048
    t = np.arange(NT) % seqlen
    d = np.arange(128) % 64
    pair = d // 2
    sgn = np.where(d % 2 == 0, -1.0, 1.0)
    quarter = 16
    freqs = (np.float32(10000.0) ** (-np.arange(quarter, dtype=f32) / quarter)).astype(f32)
    row = (t // 64).astype(f32)
    col = (t % 64).astype(f32)
    ang = np.concatenate([row[:, None] * freqs, col[:, None] * freqs], axis=-1).astype(f32)
    angd = ang[:, pair].T.astype(np.float64)
    tabA = np.stack([np.cos(angd), np.sin(angd) * sgn[:, None]]).astype(f32)
    half = 32
    freqs_b = (np.float32(10000.0) ** (-np.arange(half, dtype=f32) / half)).astype(f32)
    angb = (t.astype(f32)[:, None] * freqs_b).astype(f32)
    angbd = angb[:, pair].T.astype(np.float64)
    tabB = np.stack([np.cos(angbd), np.sin(angbd) * sgn[:, None]]).astype(f32)
    seq_of_blk = (np.arange(NB) * 128) // seqlen
    maskA = np.zeros((NCH, NB), f32)
    for C in range(NCH):
        sq = (C * CH) // seqlen
        maskA[C, :] = np.where(seq_of_blk == sq, 0.0, NEG)
    maskA = np.broadcast_to(maskA.reshape(1, -1), (128, NCH * NB)).copy()
    maskW = np.zeros((NB, 3), f32)
    for i in range(NB):
        for o in range(3):
            jb = i + o - 1
            if jb < 0 or jb >= NB or seq_of_blk[jb] != seq_of_blk[i]:
                maskW[i, o] = NEG
    maskW = np.broadcast_to(maskW.reshape(1, -1), (128, NB * 3)).copy()
    cps = seqlen // 128
    rf = np.array([0.0 if (n % cps == 0) else 1.0 for n in range(NB)], f32)
    rb = np.array([0.0 if (n % cps == cps - 1) else 1.0 for n in range(NB)], f32)
    rfb = np.broadcast_to(np.concatenate([rf, rb])[None, :], (128, 64)).copy()
    return {"tabA": tabA, "tabB": tabB, "maskA": maskA, "maskW": maskW, "rfb": rfb}


def _swap(cols):
    cols = np.asarray(cols)
    return cols ^ 1


def _prep_common(norm_g, w_in_ab, qk_norm_a, ret_decay, w_out_ab, w_in_c, sink_c, w_out_c, rel_bias, final_norm):
    f32 = np.float32
    W = np.asarray(w_in_ab[0], f32)
    qa = np.arange(0, 512)
    ka = np.arange(512, 640)
    va = np.arange(640, 768)
    ga = np.arange(768, 1280)
    qb = np.arange(1280, 1536)
    kb = np.arange(1536, 1792)
    vb = np.arange(1792, 2304)
    gb = np.arange(2304, 2816)
    kadup = np.concatenate([ka[0:64], ka[0:64], ka[64:128], ka[64:128]])
    cm = {}
    cm["wG"] = np.ascontiguousarray(W[:, np.concatenate([kadup, _swap(kadup), kb, _swap(kb), va, vb])])
    cm["wLB"] = np.ascontiguousarray(W[:, np.concatenate([qb, _swap(qb), kb, _swap(kb), gb, vb])])
    cm["wLA"] = np.ascontiguousarray(W[:, np.concatenate([qa, _swap(qa), ga])])
    cm["woab"] = np.ascontiguousarray(np.asarray(w_out_ab[0], f32))
    Wc = np.asarray(w_in_c[0], f32)
    kc = np.arange(1024, 1152)
    kcdup = np.concatenate([kc[0:64], kc[0:64], kc[64:128], kc[64:128]])
    cm["wG1"] = np.ascontiguousarray(Wc[:, np.concatenate([kcdup, np.arange(1152, 1280)])])
    cm["wL1"] = np.ascontiguousarray(Wc[:, np.concatenate([np.arange(0, 1024), np.arange(1280, 2304)])])
    cm["woc"] = np.ascontiguousarray(np.asarray(w_out_c[0], f32))
    ng = np.asarray(norm_g, f32)
    cm["gcol"] = np.ascontiguousarray(ng.reshape(2, 8, 128).transpose(2, 0, 1).reshape(128, 16))
    cm["fn"] = np.asarray(final_norm, f32).reshape(1, 1024).copy()
    g = np.asarray(qk_norm_a[0], f32)
    d = np.arange(128) % 64
    cm["gqk"] = np.stack([g[0][d], g[0][d ^ 1], g[1][d], g[1][d ^ 1]], axis=1).astype(f32).copy()
    rd = np.asarray(ret_decay[0], f32)
    hp = (np.arange(128) // 64)
    rdec = np.zeros((128, 12), f32)
    for p in range(2):
        rdec[:, p] = rd[0][2 * p + hp]
        rdec[:, 2 + p] = rd[1][2 * p + hp]
    for h in range(4):
        rdec[:, 4 + h] = rd[0][h]
        rdec[:, 8 + h] = rd[1][h]
    cm["rdec"] = rdec
    sk = np.asarray(sink_c[0], f32)
    sinkl = np.zeros((128, 8), f32)
    for t in range(8):
        sinkl[:, t] = sk[2 * t + hp]
    cm["sinkl"] = sinkl
    cm["relb"] = np.ascontiguousarray(np.asarray(rel_bias, f32))
    cm.update(_static_tables())
    return cm


_CACHE = {}


def kernel(x_prompt, x_sample, norm_g, w_in_ab, qk_norm_a, ret_decay, w_out_ab, w_in_c, sink_c, w_out_c, rel_bias, final_norm):
    xp = np.asarray(x_prompt, np.float32)
    xs = np.asarray(x_sample, np.float32)
    cm = _prep_common(norm_g, w_in_ab, qk_norm_a, ret_decay, w_out_ab, w_in_c, sink_c, w_out_c, rel_bias, final_norm)
    tp = _core_tables(True)
    tsm = _core_tables(False)
    in_maps = []
    for c in range(8):
        m = dict(cm)
        if c < 4:
            m["x"] = np.ascontiguousarray(xp[c])
            m.update(tp)
        else:
            m["x"] = np.ascontiguousarray(xs[2 * (c - 4):2 * (c - 4) + 2].reshape(NT, 1024))
            m.update(tsm)
        in_maps.append(m)
    if "nc" not in _CACHE:
        _CACHE["nc"] = build_program()[0]
    nc = _CACHE["nc"]
    res = run_bass_kernel_spmd(nc, in_maps, core_ids=list(range(8)))
    outs = [np.asarray(r["y"], np.float32) for r in res.results]
    y_prompt = np.stack(outs[0:4], axis=0)
    y_sample = np.stack(outs[4:8], axis=0).reshape(8, 2048, 1024)
    return (y_prompt, y_sample)
```

```python
import numpy as np
import concourse.bass as bass
import concourse.mybir as mybir
from concourse.bass_utils import run_bass_kernel_spmd

F32 = mybir.dt.float32
BF16 = mybir.dt.bfloat16
AF = mybir.ActivationFunctionType
ALU = mybir.AluOpType

NT = 4096
NB = 32
CH = 512
NCH = 8
EPS = 1e-6
NEG = -30000.0


class Prod:
    def __init__(self, sem, inc):
        self.sem = sem
        self.inc = inc
        self.cnt = 0


class Res:
    def __init__(self):
        self.w = {}
        self.r = {}
        self.excl = False


class V:
    def __init__(self, ap, res=None):
        self.ap = ap
        self.res = res if res is not None else Res()

    def __getitem__(self, k):
        return V(self.ap[k], self.res)

    def re(self, pat, **kw):
        return V(self.ap.rearrange(pat, **kw), self.res)

    def bc(self, shape):
        return V(self.ap.to_broadcast(shape), self.res)


class Ker:
    def __init__(self, nc):
        self.nc = nc
        self.eng = {"pe": nc.tensor, "act": nc.scalar, "dve": nc.vector, "pool": nc.gpsimd, "sp": nc.sync}
        self.prod = {}
        for n in ("pe", "act", "dve", "pool"):
            self.prod[n] = Prod(nc.alloc_semaphore("s_" + n), 1)
        self.seen = {n: {} for n in self.eng}
        self.nslot = 0
        self.ninstr = 0

    def slot(self):
        self.nslot += 1
        p = Prod(self.nc.alloc_semaphore("d%d" % self.nslot), 16)
        if hasattr(self, "slots"):
            self.slots.append(p)
        return p

    def _wait(self, en, reads, writes):
        deps = {}
        for v in reads:
            for p, i in v.res.w.items():
                deps[p] = max(deps.get(p, 0), i)
        for v in writes:
            for p, i in v.res.w.items():
                deps[p] = max(deps.get(p, 0), i)
            for p, i in v.res.r.items():
                deps[p] = max(deps.get(p, 0), i)
        e = self.eng[en]
        seen = self.seen[en]
        own = self.prod.get(en)
        for p, i in deps.items():
            if p is own and en == "pe":
                continue
            if seen.get(p, 0) >= i:
                continue
            e.wait_ge(p.sem, i)
            seen[p] = i

    def op(self, en, fn, reads, writes):
        writes = list(writes) + [r for r in reads if r.res.excl]
        self._wait(en, reads, writes)
        ins = fn(self.eng[en])
        p = self.prod[en]
        p.cnt += 1
        ins.then_inc(p.sem, 1)
        for v in reads:
            v.res.r[p] = p.cnt
        for v in writes:
            v.res.w[p] = p.cnt
        self.ninstr += 1

    def dma(self, q, out, in_, slot):
        self._wait(q, [in_], [out])
        ins = self.eng[q].dma_start(out=out.ap, in_=in_.ap)
        slot.cnt += 16
        ins.then_inc(slot.sem, 16)
        in_.res.r[slot] = slot.cnt
        out.res.w[slot] = slot.cnt

    def mm(self, out, lhsT, rhs, start=True, stop=True, tp=None):
        kw = {}
        if tp is not None:
            kw["tile_position"] = tp
        self.op("pe", lambda e: e.matmul(out.ap, lhsT.ap, rhs.ap, start=start, stop=stop, **kw), [lhsT, rhs], [out])

    def tr(self, out, in_, ident):
        self.op("pe", lambda e: e.transpose(out.ap, in_.ap, ident.ap), [in_, ident], [out])

    def act(self, out, in_, func, bias=None, scale=1.0, accum=None):
        reads = [in_]
        kw = {}
        if bias is not None:
            if isinstance(bias, V):
                reads.append(bias)
                kw["bias"] = bias.ap
            else:
                kw["bias"] = bias
        if isinstance(scale, V):
            reads.append(scale)
            kw["scale"] = scale.ap
        else:
            kw["scale"] = scale
        writes = [out]
        if accum is not None:
            writes.append(accum)
            kw["accum_out"] = accum.ap
        self.op("act", lambda e: e.activation(out.ap, in_.ap, func, **kw), reads, writes)

    def tt(self, en, out, a, b, op):
        self.op(en, lambda e: e.tensor_tensor(out.ap, a.ap, b.ap, op), [a, b], [out])

    def stt(self, en, out, in0, scalar, in1, op0, op1):
        reads = [in0, in1]
        s = scalar
        if isinstance(scalar, V):
            reads.append(scalar)
            s = scalar.ap
        self.op(en, lambda e: e.scalar_tensor_tensor(out.ap, in0.ap, s, in1.ap, op0, op1), reads, [out])

    def ts(self, en, out, in0, s1, op0, s2=None, op1=None):
        reads = [in0]
        a1 = s1
        if isinstance(s1, V):
            reads.append(s1)
            a1 = s1.ap
        a2 = s2
        if isinstance(s2, V):
            reads.append(s2)
            a2 = s2.ap
        if op1 is None:
            self.op(en, lambda e: e.tensor_scalar(out.ap, in0.ap, a1, None, op0), reads, [out])
        else:
            self.op(en, lambda e: e.tensor_scalar(out.ap, in0.ap, a1, a2, op0, op1), reads, [out])

    def cp(self, en, out, in_):
        if en == "act":
            self.op("act", lambda e: e.copy(out.ap, in_.ap), [in_], [out])
        else:
            self.op(en, lambda e: e.tensor_copy(out.ap, in_.ap), [in_], [out])

    def amul(self, out, in_, m):
        self.op("act", lambda e: e.mul(out.ap, in_.ap, m.ap), [in_, m], [out])

    def recip(self, out, in_):
        self.op("dve", lambda e: e.reciprocal(out.ap, in_.ap), [in_], [out])

    def memset(self, en, out, val):
        self.op(en, lambda e: e.memset(out.ap, val), [], [out])


class StopBuild(Exception):
    pass


import contextlib


class Scope(contextlib.ExitStack):
    def __init__(self, K):
        super().__init__()
        self.K = K
        self.tiles = []

    def __exit__(self, *a):
        fr = self.K.freed
        for v in self.tiles:
            for d in (v.res.w, v.res.r):
                for p, i in d.items():
                    fr[p] = max(fr.get(p, 0), i)
        self.tiles = []
        return super().__exit__(*a)

    def close(self):
        self.__exit__(None, None, None)


def build_program(stop=None, taps=()):
    nc = bass.Bass("TRN2", target_bir_lowering=False)
    K = Ker(nc)
    K.slots = []
    K.freed = {}
    K.tapped = {}

    def checkpoint(name):
        if stop == name:
            raise StopBuild()

    def tap(name, v, shape, dt=F32):
        if name not in taps or name in K.tapped:
            return
        d = V(nc.dram_tensor("dbg_" + name, list(shape), dt, kind="ExternalOutput").ap())
        K.tapped[name] = d
        K.dma("sp", d, v, K.slot())

    def din(name, shape, dt=F32):
        return V(nc.dram_tensor(name, list(shape), dt, kind="ExternalInput").ap())

    x_d = din("x", [NT, 1024])
    wG_d = din("wG", [1024, 1664])
    wLB_d = din("wLB", [1024, 2048])
    wLA_d = din("wLA", [1024, 1536])
    woab_d = din("woab", [1024, 1024])
    wG1_d = din("wG1", [1024, 384])
    wL1_d = din("wL1", [1024, 2048])
    woc_d = din("woc", [1024, 1024])
    gcol_d = din("gcol", [128, 16])
    fn_d = din("fn", [1, 1024])
    gqk_d = din("gqk", [128, 4])
    rdec_d = din("rdec", [128, 12])
    sink_d = din("sinkl", [128, 8])
    relb_d = din("relb", [32, 16])
    ident_d = din("ident", [128, 128])
    onesblk_d = din("onesblk", [128, 128])
    mm_d = din("mmat", [128, 4, 128])
    iot_d = din("iot", [128, 4, 512])
    oh_d = din("oh", [32, 640])
    inwin_d = din("inwin", [16, 640])
    tabA_d = din("tabA", [2, 128, NT])
    tabB_d = din("tabB", [2, 128, NT])
    maskA_d = din("maskA", [128, 256])
    maskW_d = din("maskW", [128, 96])
    rfb_d = din("rfb", [128, 64])
    y_d = V(nc.dram_tensor("y", [NT, 1024], F32, kind="ExternalOutput").ap())
    x1_d = V(nc.dram_tensor("x1s", [NT, 1024], F32, kind="Internal").ap())
    mixb_d = V(nc.dram_tensor("mixbs", [4, 128, NT], BF16, kind="Internal").ap())
    vec_d = V(nc.dram_tensor("vecs", [16, 640], BF16, kind="Internal").ap())
    xnT0_d = V(nc.dram_tensor("xnT0s", [NCH, 128, 8, CH], BF16, kind="Internal").ap())
    xnT1_d = V(nc.dram_tensor("xnT1s", [NCH, 128, 8, CH], BF16, kind="Internal").ap())

    es = Scope(K)
    uid = [0]

    def sb(name, shape, dt=F32, stack=None):
        uid[0] += 1
        st_ = stack if stack is not None else es
        t = st_.enter_context(nc.sbuf_tensor("sb%d_%s" % (uid[0], name), list(shape), dt))
        v = V(t[:])
        v.res.w = dict(K.freed)
        st_.tiles.append(v)
        return v

    def ps(name, shape, dt=F32, stack=None):
        uid[0] += 1
        st_ = stack if stack is not None else es
        t = st_.enter_context(nc.psum_tensor("ps%d_%s" % (uid[0], name), list(shape), dt))
        v = V(t[:])
        v.res.excl = True
        v.res.w = dict(K.freed)
        st_.tiles.append(v)
        return v

    try:
        with es:
            cslot = K.slot()
            consts = []

            def cload(name, src, shape, dt=F32, q="sp"):
                t = sb(name, shape, dt)
                K.dma(q, t, src, cslot)
                consts.append(t)
                return t

            gcol = cload("gcol", gcol_d, [128, 16])
            gqk = cload("gqk", gqk_d, [128, 4])
            rdec = cload("rdec", rdec_d, [128, 12])
            sinkl = cload("sinkl", sink_d, [128, 8])
            maskA = cload("maskA", maskA_d, [128, 256])
            maskW = cload("maskW", maskW_d, [128, 96])
            rfb = cload("rfb", rfb_d, [128, 64])
            ident32 = cload("ident32", ident_d, [128, 128])
            onesblk32 = cload("onesblk32", onesblk_d, [128, 128])
            for c in consts:
                c.res.w[cslot] = cslot.cnt
            ident = sb("ident", [128, 128], BF16)
            onesblk = sb("onesblk", [128, 128], BF16)
            ones = sb("ones", [128, 128], BF16)
            epsb = sb("epsb", [128, 1])
            K.cp("dve", ident, ident32)
            K.cp("dve", onesblk, onesblk32)
            K.memset("dve", ones, 1.0)
            K.memset("dve", epsb, EPS)

            checkpoint("c0")
            wbf = sb("wbf", [128, 8, 2048], BF16)
            wobf = sb("wobf", [128, 8, 1024], BF16)
            wst = [sb("wst%d" % i, [128, 1024]) for i in range(2)]
            wst_slot = [K.slot() for _ in range(2)]
            xnTs = [sb("xnT%d" % i, [128, 8, CH], BF16) for i in range(2)]
            xnT_slot = [K.slot() for _ in range(2)]
            cur = {"xnT": xnTs[0]}
            xch_slot = [K.slot() for _ in range(4)]
            xst_slot = [K.slot() for _ in range(2)]
            EBT = sb("EBT", [128, 16, 3, 128], BF16)
            esk = sb("esk", [128, 8], F32)
            K.act(esk, sinkl, AF.Exp)
            with Scope(K) as S1:
                relb = sb("relb", [32, 16], F32, S1)
                oh = sb("oh", [32, 640], F32, S1)
                inw = sb("inw", [16, 640], F32, S1)
                e_slot = K.slot()
                K.dma("sp", relb, relb_d, e_slot)
                K.dma("sp", oh, oh_d, e_slot)
                K.dma("sp", inw, inwin_d, e_slot)
                for t_ in (relb, oh, inw):
                    t_.res.w[e_slot] = e_slot.cnt
                pv = ps("pv", [16, 1024], F32, S1)[:, 0:640]
                vec = sb("vec", [16, 640], F32, S1)
                vecb = sb("vecb", [16, 640], BF16, S1)
                K.mm(pv[:, 0:512], relb, oh[:, 0:512])
                K.mm(pv[:, 512:640], relb, oh[:, 512:640])
                K.act(vec, pv, AF.Exp)
                K.tt("dve", vecb, vec, inw, ALU.mult)
                v_slot = K.slot()
                K.dma("sp", vec_d, vecb, v_slot)
                g_slot = [K.slot() for _ in range(2)]
                for k in range(128):
                    qn = "sp" if (k % 2 == 0) else "pool"
                    K.dma(qn, EBT[k:k + 1, :, :, :], V(vec_d.ap[:, 127 - k:127 - k + 384].rearrange("(a h) (o q) -> a h o q", a=1, o=3), vec_d.res), g_slot[k % 2])
            wcount = [0]

            def handoff(srcs, dsts):
                for d_ in dsts:
                    for s_ in srcs:
                        for dd in (s_.res.w, s_.res.r):
                            for p_, i_ in dd.items():
                                d_.res.w[p_] = max(d_.res.w.get(p_, 0), i_)

            def subview(parent, ap):
                v = V(ap)
                v.res.excl = parent.res.excl
                v.res.w = dict(parent.res.w)
                return v

            class MX:
                pass

            def alloc_mx(scope, full=True):
                m = MX()
                m.xch = sb("xch", [128, 4, 1024], F32, scope)
                if full:
                    m.xn = [sb("xn%d" % i, [128, 1024], BF16, scope) for i in range(2)]
                    m.junk = sb("junk", [128, 1024], BF16, scope)
                    m.ss = sb("ss", [128, 4], F32, scope)
                    m.lnv4 = sb("lnv4", [128, 4], F32, scope)
                    m.rstd4 = sb("rstd4", [128, 4], F32, scope)
                return m

            def load_x(m, src_d, C):
                for b in range(4):
                    r0 = C * CH + b * 128
                    K.dma("sp", m.xch[:, b, :], src_d[r0:r0 + 128, :], xch_slot[b])

            def store_xnT(dst_d, C, slot_i):
                K.dma("pool", dst_d[C], cur["xnT"], xst_slot[slot_i])

            def load_xnT(src_d, C):
                i = C % 2
                K.dma("sp", xnTs[i], src_d[C], xnT_slot[i])

            def use_xnT(C):
                cur["xnT"] = xnTs[C % 2]

            def load_w(dst, src_d, ncols, layer_g):
                for kc in range(8):
                    for c0 in range(0, ncols, 1024):
                        c1 = min(ncols, c0 + 1024)
                        i = wcount[0] % 2
                        wcount[0] += 1
                        K.dma("sp", wst[i][:, 0:c1 - c0], src_d[kc * 128:(kc + 1) * 128, c0:c1], wst_slot[i])
                        en = "act" if (wcount[0] % 2 == 0) else "dve"
                        if layer_g is None:
                            K.cp(en, dst[:, kc, c0:c1], wst[i][:, 0:c1 - c0])
                        elif en == "act":
                            K.amul(dst[:, kc, c0:c1], wst[i][:, 0:c1 - c0], gcol[:, layer_g * 8 + kc:layer_g * 8 + kc + 1])
                        else:
                            K.ts("dve", dst[:, kc, c0:c1], wst[i][:, 0:c1 - c0], gcol[:, layer_g * 8 + kc:layer_g * 8 + kc + 1], ALU.mult)

            def make_xnT(m, src_d, C, pT):
                use_xnT(C)
                xnT = cur["xnT"]
                load_x(m, src_d, C)
                for b in range(4):
                    K.act(m.junk, m.xch[:, b, :], AF.Square, accum=m.ss[:, b:b + 1])
                K.act(m.lnv4, m.ss, AF.Ln, bias=epsb[:, 0:1], scale=1.0 / 1024.0)
                K.act(m.rstd4, m.lnv4, AF.Exp, scale=-0.5)
                for b in range(4):
                    xb = m.xn[b % 2]
                    K.ts("dve", xb, m.xch[:, b, :], m.rstd4[:, b:b + 1], ALU.mult)
                    for kc in range(8):
                        K.tr(pT[:, kc, :], xb[:, kc * 128:(kc + 1) * 128], ident)
                    K.cp("act", xnT[:, :, b * 128:(b + 1) * 128], pT)

            def proj(dst, c0):
                for kc in range(8):
                    K.mm(dst, wbf[:, kc, c0:c0 + 128], cur["xnT"][:, kc, :], start=(kc == 0), stop=(kc == 7))

            def rsq_bcast(dst, src_ps, nfeat, sq, psn, lnv, lhs_ones):
                K.act(sq, src_ps, AF.Square)
                K.mm(psn, lhs_ones, sq)
                K.act(lnv, psn, AF.Ln, bias=epsb[:, 0:1], scale=1.0 / nfeat)
                K.act(dst, lnv, AF.Exp, scale=-0.5)

            with Scope(K) as L0:
                LR = Scope(K)
                KaT = sb("KaT", [128, 2, NT], BF16, L0)
                Va = sb("Va", [128, NB, 128], BF16, L0)
                tabc = sb("tabc", [128, 2, CH], F32, L0)
                tab_slot = K.slot()
                tabd_slot = K.slot()
                sq = sb("sq", [128, CH], BF16, L0)
                lnv = sb("lnv", [128, CH], F32, L0)
                rs = sb("rs", [128, CH], F32, L0)
                t1 = sb("t1", [128, CH], F32, L0)
                t2 = sb("t2", [128, CH], F32, L0)
                tabg = sb("tabg", [128, 2, CH], F32, L0)
                SbAll = sb("SbAll", [128, 2, NB, 128], BF16, LR)
                tabd = sb("tabd", [128, 2, CH], F32, LR)
                vbtm = sb("vbtm", [128, 4, 512], BF16, LR)
                lg = sb("lg", [128, 12], F32, LR)
                K.act(lg, rdec, AF.Exp)
                K.ts("dve", lg, lg, -1.0, ALU.mult)
                cd = sb("cd", [128, 4], F32, LR)
                K.act(cd, lg[:, 0:4], AF.Exp, scale=128.0)
                cdr = sb("cdr", [128, 4, NB], F32, LR)
                for j in range(4):
                    off = 0 if j < 2 else 32
                    K.ts("dve", cdr[:, j, :], rfb[:, off:off + 32], cd[:, j:j + 1], ALU.mult)
                checkpoint("c1")
                QF4 = sb("QF4", [128, 2, CH], F32, LR)
                QB4 = sb("QB4", [128, 2, CH], F32, LR)
                KF4 = sb("KF4", [128, 2, CH], F32, LR)
                KB4 = sb("KB4", [128, 2, CH], F32, LR)
                DT = sb("DT", [128, 4, 128], F32, LR)
                with Scope(K) as S0:
                    iot = sb("iot", [128, 4, CH], F32, S0)
                    K.dma("sp", iot, iot_d, tabd_slot)
                    for p in range(2):
                        K.act(QF4[:, p, :], iot[:, 0, :], AF.Exp, scale=lg[:, p:p + 1])
                        K.act(QB4[:, p, :], iot[:, 1, :], AF.Exp, scale=lg[:, 2 + p:3 + p])
                        K.act(KF4[:, p, :], iot[:, 2, :], AF.Exp, scale=lg[:, p:p + 1])
                        K.act(KB4[:, p, :], iot[:, 3, :], AF.Exp, scale=lg[:, 2 + p:3 + p])
                    K.ts("dve", KF4, KF4, 0.125, ALU.mult)
                    K.ts("dve", KB4, KB4, 0.125, ALU.mult)
                    mmat = sb("mmat", [128, 4, 128], F32, S0)
                    K.dma("sp", mmat, mm_d, tab_slot)
                    d1 = sb("d1", [128, 128], F32, S0)
                    d2 = sb("d2", [128, 128], F32, S0)
                    for h in range(4):
                        checkpoint("d0")
                        K.act(d1, mmat[:, 0, :], AF.Exp, scale=lg[:, 4 + h:5 + h])
                        checkpoint("d1")
                        K.tt("dve", d1, d1, mmat[:, 1, :], ALU.mult)
                        checkpoint("d2")
                        K.act(d2, mmat[:, 2, :], AF.Exp, scale=lg[:, 8 + h:9 + h])
                        K.tt("dve", d2, d2, mmat[:, 3, :], ALU.mult)
                        K.tt("dve", d1, d1, d2, ALU.add)
                        checkpoint("d3")
                        K.ts("dve", DT[:, h, :], d1, 0.125, ALU.mult)
                        checkpoint("d4")

                def load_tab(dst, slot, src_d, C):
                    K.dma("sp", dst, V(src_d.ap[:, :, C * CH:(C + 1) * CH].rearrange("t p c -> p t c"), src_d.res), slot)

                def rope(psa, psb, tab, out32, ga=None, gb=None):
                    if ga is None:
                        K.tt("dve", t1, psa, tab[:, 0, :], ALU.mult)
                        K.tt("dve", t2, psb, tab[:, 1, :], ALU.mult)
                    else:
                        K.amul(tabg[:, 0, :], tab[:, 0, :], ga)
                        K.amul(tabg[:, 1, :], tab[:, 1, :], gb)
                        K.tt("dve", t1, psa, tabg[:, 0, :], ALU.mult)
                        K.tt("dve", t2, psb, tabg[:, 1, :], ALU.mult)
                    K.tt("pool", out32, t1, t2, ALU.add)

                checkpoint("setup0")
                load_w(wbf, wG_d, 1664, 0)
                with Scope(K) as PG:
                    pT = ps("pT", [128, 8, 128], BF16, PG)
                    pk = pT.re("p (c t) q -> p c t q", t=2)
                    pbig = ps("pbigG", [128, 7, CH], F32, PG)
                    bk = [subview(pbig, pbig.ap[:, i, :]) for i in range(7)]
                    pn = bk[4]
                    pkv = V(bk[6].ap[:, 0:256].rearrange("p (a b) -> p a b", a=2), bk[6].res)
                    kdbT = sb("kdbT", [128, 2, CH], BF16, PG)
                    kdbtm = sb("kdbtm", [128, 4, 2, 128], BF16, PG)
                    Rb = sb("Rb", [128, 2, 128], F32, PG)
                    mxg = alloc_mx(PG)
                    WS = [dict(sq=sq, lnv=lnv, rs=rs, t1=t1, t2=t2),
                          dict(sq=sb("wsq", [128, CH], BF16, PG), lnv=sb("wlnv", [128, CH], F32, PG),
                               rs=sb("wrs", [128, CH], F32, PG), t1=sb("wt1", [128, CH], F32, PG),
                               t2=sb("wt2", [128, CH], F32, PG))]
                    K.memset("dve", Rb, 0.0)
                    for C in range(NCH - 1, -1, -1):
                        make_xnT(mxg, x_d, C, pT)
                        store_xnT(xnT0_d, C, C % 2)
                        load_tab(tabc, tab_slot, tabA_d, C)
                        load_tab(tabd, tabd_slot, tabB_d, C)
                        K.amul(tabg[:, 0, :], tabc[:, 0, :], gqk[:, 2:3])
                        K.amul(tabg[:, 1, :], tabc[:, 1, :], gqk[:, 3:4])
                        for t in range(2):
                            w_ = WS[t % 2]
                            pa_, pb_ = bk[2 * t], bk[2 * t + 1]
                            proj(pa_, t * 128)
                            proj(pb_, 256 + t * 128)
                            rsq_bcast(w_["rs"], pa_, 64.0, w_["sq"], pn, w_["lnv"], onesblk)
                            K.tt("dve", w_["t1"], pa_, tabg[:, 0, :], ALU.mult)
                            K.tt("dve", w_["t2"], pb_, tabg[:, 1, :], ALU.mult)
                            K.tt("pool", w_["t1"], w_["t1"], w_["t2"], ALU.add)
                            K.tt("pool", KaT[:, t, C * CH:(C + 1) * CH], w_["t1"], w_["rs"], ALU.mult)
                        for t in range(2):
                            w_ = WS[t % 2]
                            pa_, pb_ = bk[2 * t], bk[2 * t + 1]
                            proj(pa_, 512 + t * 128)
                            proj(pb_, 768 + t * 128)
                            K.tt("dve", w_["t1"], pa_, tabd[:, 0, :], ALU.mult)
                            K.tt("dve", w_["t2"], pb_, tabd[:, 1, :], ALU.mult)
                            K.tt("pool", w_["t1"], w_["t1"], w_["t2"], ALU.add)
                            K.tt("pool", kdbT[:, t, :], w_["t1"], KB4[:, t, :], ALU.mult)
                            for cj in range(4):
                                K.tr(pk[:, cj, t, :], kdbT[:, t, cj * 128:(cj + 1) * 128], ident)
                        K.cp("act", kdbtm, pk)
                        for b in range(4):
                            pva = (bk[4] if b % 2 == 0 else bk[2])[:, 0:128]
                            pvb = bk[5] if b % 2 == 0 else bk[3]
                            for kc in range(8):
                                K.mm(pva, cur["xnT"][:, kc, b * 128:(b + 1) * 128], wbf[:, kc, 1024:1152], start=(kc == 0), stop=(kc == 7))
                            for kc in range(8):
                                K.mm(pvb, cur["xnT"][:, kc, b * 128:(b + 1) * 128], wbf[:, kc, 1152:1664], start=(kc == 0), stop=(kc == 7))
                            K.cp("act", Va[:, C * 4 + b, :], pva)
                            K.cp("dve", vbtm[:, b, :], pvb)
                        for cj in range(3, -1, -1):
                            n = C * 4 + cj
                            for p in range(2):
                                K.mm(pkv[0:64, p, :], kdbtm[:, cj, p, 0:64], vbtm[:, cj, (2 * p) * 128:(2 * p + 1) * 128])
                                K.mm(pkv[64:128, p, :], kdbtm[:, cj, p, 64:128], vbtm[:, cj, (2 * p + 1) * 128:(2 * p + 2) * 128], tp=(0, 64))
                            K.ts("dve", SbAll[:, :, n, :], Rb, rfb[:, 32 + n:33 + n], ALU.mult)
                            for p in range(2):
                                K.ts("dve", Rb[:, p, :], Rb[:, p, :], cdr[:, 2 + p, n:n + 1], ALU.mult)
                                K.tt("dve", Rb[:, p, :], pkv[:, p, :], Rb[:, p, :], ALU.add)

                tap("KaT", KaT, [128, 2, NT], BF16)
                tap("Va", Va, [128, NB, 128], BF16)
                tap("SbAll", SbAll, [128, 2, NB, 128], BF16)
                checkpoint("G")
                load_w(wbf, wLB_d, 2048, 0)
                with Scope(K) as PB:
                    pT = ps("pT", [128, 8, 128], BF16, PB)
                    pk = pT.re("p (c t) q -> p c t q", t=2)
                    pbig = ps("pbigB", [128, 7, CH], F32, PB)
                    bk = [subview(pbig, pbig.ap[:, i, :]) for i in range(7)]
                    pa, pb, pss = bk[0], bk[1], bk[2]
                    po = subview(pbig, pbig.ap[:, 3:7, :])
                    qrT = sb("qrT", [128, 2, CH], BF16, PB)
                    qdf = sb("qdf", [128, 2, CH], BF16, PB)
                    qdb = sb("qdb", [128, 2, CH], BF16, PB)
                    krT = sb("krT", [128, 2, CH], BF16, PB)
                    kdfT = sb("kdfT", [128, 2, CH], BF16, PB)
                    kdftm = sb("kdftm", [128, 4, 2, 128], BF16, PB)
                    sg = sb("sg", [128, 4, CH], BF16, PB)
                    ATs = [sb("AT%d" % i, [128, 4, 128], BF16, PB) for i in range(2)]
                    Sfs = [sb("Sf%d" % i, [128, 2, 128], BF16, PB) for i in range(2)]
                    Rf = sb("Rf", [128, 2, 128], F32, PB)
                    mixBc = [sb("mixBc%d" % i, [128, 4, CH], BF16, PB) for i in range(1)]
                    mixB_slot = [K.slot() for _ in range(1)]
                    WS = [dict(sq=sq, lnv=lnv, rs=rs, t1=t1, t2=t2),
                          dict(sq=sb("wsq", [128, CH], BF16, PB), lnv=sb("wlnv", [128, CH], F32, PB),
                               rs=sb("wrs", [128, CH], F32, PB), t1=sb("wt1", [128, CH], F32, PB),
                               t2=sb("wt2", [128, CH], F32, PB))]
                    K.memset("dve", Rf, 0.0)
                    load_xnT(xnT0_d, 0)
                    pairs = [(bk[0], bk[1]), (bk[3], bk[4]), (bk[5], bk[6])]
                    for C in range(NCH):
                        use_xnT(C)
                        if C + 1 < NCH:
                            load_xnT(xnT0_d, C + 1)
                        load_tab(tabd, tabd_slot, tabB_d, C)
                        handoff([po], bk[3:7])
                        ip = 0
                        for t in range(2):
                            w_ = WS[ip % 2]
                            pa_, pb_ = pairs[ip % 3]
                            ip += 1
                            proj(pa_, t * 128)
                            proj(pb_, 256 + t * 128)
                            K.tt("dve", w_["t1"], pa_, tabd[:, 0, :], ALU.mult)
                            K.tt("dve", w_["t2"], pb_, tabd[:, 1, :], ALU.mult)
                            K.tt("pool", w_["t1"], w_["t1"], w_["t2"], ALU.add)
                            K.cp("act", qrT[:, t, :], w_["t1"])
                            K.tt("pool", qdf[:, t, :], w_["t1"], QF4[:, t, :], ALU.mult)
                            K.tt("pool", qdb[:, t, :], w_["t1"], QB4[:, t, :], ALU.mult)
                        for t in range(2):
                            w_ = WS[ip % 2]
                            pa_, pb_ = pairs[ip % 3]
                            ip += 1
                            proj(pa_, 512 + t * 128)
                            proj(pb_, 768 + t * 128)
                            K.tt("dve", w_["t1"], pa_, tabd[:, 0, :], ALU.mult)
                            K.tt("dve", w_["t2"], pb_, tabd[:, 1, :], ALU.mult)
                            K.tt("pool", w_["t1"], w_["t1"], w_["t2"], ALU.add)
                            K.cp("act", krT[:, t, :], w_["t1"])
                            K.tt("pool", kdfT[:, t, :], w_["t1"], KF4[:, t, :], ALU.mult)
                            for cj in range(4):
                                K.tr(pk[:, cj, t, :], kdfT[:, t, cj * 128:(cj + 1) * 128], ident)
                        K.cp("act", kdftm, pk)
                        for h in range(4):
                            pa_ = bk[3 + h]
                            proj(pa_, 1024 + h * 128)
                            K.act(sg[:, h, :], pa_, AF.Silu)
                        for b in range(4):
                            pv_ = bk[1 + b % 2]
                            for kc in range(8):
                                K.mm(pv_, cur["xnT"][:, kc, b * 128:(b + 1) * 128], wbf[:, kc, 1536:2048], start=(kc == 0), stop=(kc == 7))
                            K.cp("dve", vbtm[:, b, :], pv_)
                        handoff(bk[3:7], [po])
                        for cj in range(4):
                            n = C * 4 + cj
                            cs = slice(cj * 128, (cj + 1) * 128)
                            Sf = Sfs[cj % 2]
                            AT = ATs[cj % 2]
                            K.ts("dve", Sf, Rf, rfb[:, n:n + 1], ALU.mult)
                            for p in range(2):
                                K.mm(pa[0:64, p * 128:(p + 1) * 128], kdftm[:, cj, p, 0:64], vbtm[:, cj, (2 * p) * 128:(2 * p + 1) * 128])
                                K.mm(pa[64:128, p * 128:(p + 1) * 128], kdftm[:, cj, p, 64:128], vbtm[:, cj, (2 * p + 1) * 128:(2 * p + 2) * 128], tp=(0, 64))
                            for p in range(2):
                                K.ts("dve", Rf[:, p, :], Rf[:, p, :], cdr[:, p, n:n + 1], ALU.mult)
                                K.tt("dve", Rf[:, p, :], pa[:, p * 128:(p + 1) * 128], Rf[:, p, :], ALU.add)
                            for h in range(4):
                                t, r0 = h // 2, (h % 2) * 64
                                pdst = pss if (h % 2 == 0) else pb
                                K.mm(pdst[:, t * 128:(t + 1) * 128], krT[r0:r0 + 64, t, cs], qrT[r0:r0 + 64, t, cs])
                            ATv = AT.re("p (t hp) i -> p hp t i", hp=2)
                            DTv = DT.re("p (t hp) i -> p hp t i", hp=2)
                            K.tt("dve", ATv[:, 0, :, :], pss[:, 0:256].re("p (t i) -> p t i", t=2), DTv[:, 0, :, :], ALU.mult)
                            K.tt("dve", ATv[:, 1, :, :], pb[:, 0:256].re("p (t i) -> p t i", t=2), DTv[:, 1, :, :], ALU.mult)
                            for h in range(4):
                                t, r0 = h // 2, (h % 2) * 64
                                K.mm(po[:, h, cs], vbtm[:, cj, h * 128:(h + 1) * 128], AT[:, h, :], start=True, stop=False)
                                K.mm(po[:, h, cs], Sf[r0:r0 + 64, t, :], qdf[r0:r0 + 64, t, cs], start=False, stop=False)
                                K.mm(po[:, h, cs], SbAll[r0:r0 + 64, t, n, :], qdb[r0:r0 + 64, t, cs], start=False, stop=True)
                        mb = mixBc[0]
                        for h in range(4):
                            w_ = WS[h % 2]
                            psn_ = bk[h % 3]
                            rsq_bcast(w_["rs"], po[:, h, :], 128.0, w_["sq"], psn_, w_["lnv"], ones)
                            K.tt("dve", w_["t1"], po[:, h, :], w_["rs"], ALU.mult)
                            K.tt("pool", mb[:, h, :], w_["t1"], sg[:, h, :], ALU.mult)
                        K.dma("pool", V(mixb_d.ap[:, :, C * CH:(C + 1) * CH].rearrange("h p c -> p h c"), mixb_d.res), mb, mixB_slot[0])

                tap("mixb", mixb_d, [4, 128, NT], BF16)
                checkpoint("LB")
                LR.close()
                load_w(wbf, wLA_d, 1536, 0)
                load_w(wobf, woab_d, 1024, None)
                with Scope(K) as PA:
                    pbig = ps("pbig", [128, 8, CH], F32, PA)
                    psc = [subview(pbig, pbig.ap[:, 2 * i:2 * i + 2, :]) for i in range(3)]
                    pnum = subview(pbig, pbig.ap[:, 6, :])
                    pden = subview(pbig, pbig.ap[:, 7, :])
                    bk = [subview(pbig, pbig.ap[:, i, :]) for i in range(6)] + [pnum, pden]
                    WS = [dict(sq=sb("wsq%d" % i, [128, CH], BF16, PA), lnv=sb("wlnv%d" % i, [128, CH], F32, PA),
                               rs=sb("wrs%d" % i, [128, CH], F32, PA), t1=sb("wt1%d" % i, [128, CH], F32, PA),
                               t2=sb("wt2%d" % i, [128, CH], F32, PA)) for i in range(2)]
                    qaT = sb("qaT", [128, 4, CH], BF16, PA)
                    sga = sb("sga", [128, 4, CH], BF16, PA)
                    mixA = sb("mixA", [128, 4, CH], BF16, PA)
                    mixBl = sb("mixBl", [128, 4, CH], BF16, PA)
                    mixBl_slot = K.slot()
                    pTs = [sb("pTs%d" % i, [128, 2, CH], BF16, PA) for i in range(3)]
                    x1b = [sb("x1b%d" % i, [128, 1024], F32, PA) for i in range(2)]
                    dcp = sb("dcp", [128, CH], F32, PA)
                    ncp = sb("ncp", [128, CH], F32, PA)
                    x1b_slot = [K.slot() for _ in range(2)]
                    it = 0
                    mxa = alloc_mx(PA, full=False)
                    xch = mxa.xch
                    load_xnT(xnT0_d, 0)
                    for C in range(NCH):
                        use_xnT(C)
                        if C + 1 < NCH:
                            load_xnT(xnT0_d, C + 1)
                        load_x(mxa, x_d, C)
                        handoff(psc, bk[0:6])
                        load_tab(tabc, tab_slot, tabA_d, C)
                        K.dma("pool", mixBl, V(mixb_d.ap[:, :, C * CH:(C + 1) * CH].rearrange("h p c -> p h c"), mixb_d.res), mixBl_slot)
                        K.amul(tabg[:, 0, :], tabc[:, 0, :], gqk[:, 0:1])
                        K.amul(tabg[:, 1, :], tabc[:, 1, :], gqk[:, 1:2])
                        for t in range(4):
                            w_ = WS[t % 2]
                            pa_, pb_ = bk[(2 * t) % 6], bk[(2 * t + 1) % 6]
                            psn_ = bk[6 + t % 2]
                            proj(pa_, t * 128)
                            proj(pb_, 512 + t * 128)
                            rsq_bcast(w_["rs"], pa_, 64.0, w_["sq"], psn_, w_["lnv"], onesblk)
                            K.tt("dve", w_["t1"], pa_, tabg[:, 0, :], ALU.mult)
                            K.tt("dve", w_["t2"], pb_, tabg[:, 1, :], ALU.mult)
                            K.tt("pool", w_["t1"], w_["t1"], w_["t2"], ALU.add)
                            K.tt("pool", qaT[:, t, :], w_["t1"], w_["rs"], ALU.mult)
                        for t in range(4):
                            pa_ = bk[(2 + t) % 6]
                            proj(pa_, 1024 + t * 128)
                            K.act(sga[:, t, :], pa_, AF.Silu)
                        handoff(bk[0:6], psc)
                        handoff([pa, pb], [psc[0]])
                        for t in range(4):
                            kv = t // 2
                            def qk(kb_, sc_):
                                ks = slice(kb_ * 128, (kb_ + 1) * 128)
                                K.mm(sc_[:, 0, :], KaT[0:64, kv, ks], qaT[0:64, t, :])
                                K.mm(sc_[:, 1, :], KaT[64:128, kv, ks], qaT[64:128, t, :])

                            qk(0, psc[it % 3])
                            qk(1, psc[(it + 1) % 3])
                            for kb in range(NB):
                                sc = psc[it % 3]
                                pt = pTs[it % 3]
                                it += 1
                                K.act(pt, sc, AF.Exp, bias=maskA[:, C * NB + kb:C * NB + kb + 1], scale=0.125)
                                if kb + 2 < NB:
                                    qk(kb + 2, psc[(it + 1) % 3])
                                st, sp_ = (kb == 0), (kb == NB - 1)
                                K.mm(pnum[0:64, :], Va[:, kb, kv * 64:(kv + 1) * 64], pt[:, 0, :], start=st, stop=sp_)
                                K.mm(pnum[64:128, :], Va[:, kb, kv * 64:(kv + 1) * 64], pt[:, 1, :], start=st, stop=sp_, tp=(0, 64))
                                K.mm(pden[0:64, :], ones[:, 0:64], pt[:, 0, :], start=st, stop=sp_)
                                K.mm(pden[64:128, :], ones[:, 0:64], pt[:, 1, :], start=st, stop=sp_, tp=(0, 64))
                            K.cp("dve", dcp, pden)
                            K.cp("dve", ncp, pnum)
                            K.recip(dcp, dcp)
                            K.tt("dve", ncp, ncp, dcp, ALU.mult)
                            K.tt("pool", mixA[:, t, :], ncp, sga[:, t, :], ALU.mult)
                        for b in range(4):
                            bs = slice(b * 128, (b + 1) * 128)
                            py = psc[1 + b % 2]
                            for half in range(2):
                                for f in range(8):
                                    src = mixA[:, f, bs] if f < 4 else mixBl[:, f - 4, bs]
                                    K.mm(py[:, half, :], src, wobf[:, f, half * 512:(half + 1) * 512], start=(f == 0), stop=(f == 7))
                            xo = x1b[b % 2]
                            K.tt("dve", xo, py.re("p a c -> p (a c)"), xch[:, b, :], ALU.add)
                            r0 = C * CH + b * 128
                            K.dma("pool", x1_d[r0:r0 + 128, :], xo, x1b_slot[b % 2])

            tap("x1", x1_d, [NT, 1024], F32)
            checkpoint("LA")
            with Scope(K) as L1:
                KcT = sb("KcT", [128, 2, (NB + 2) * 128], BF16, L1)
                Vc = sb("Vc", [128, NB + 2, 128], BF16, L1)
                K.memset("pool", KcT[:, :, 0:128], 0.0)
                K.memset("pool", KcT[:, :, (NB + 1) * 128:(NB + 2) * 128], 0.0)
                K.memset("pool", Vc[:, 0, :], 0.0)
                K.memset("pool", Vc[:, NB + 1, :], 0.0)
                tap("EBT", EBT, [128, 16, 3, 128], BF16)
                checkpoint("EBT")
                load_w(wbf, wG1_d, 384, 1)
                with Scope(K) as PG1:
                    pT = ps("pT", [128, 8, 128], BF16, PG1)
                    pa = ps("pa", [128, CH], F32, PG1)
                    pva = ps("pva", [128, 512], F32, PG1)[:, 0:128]
                    mxg1 = alloc_mx(PG1)
                    for C in range(NCH):
                        make_xnT(mxg1, x1_d, C, pT)
                        store_xnT(xnT1_d, C, C % 2)
                        for t in range(2):
                            proj(pa, t * 128)
                            K.cp("act", KcT[:, t, (C * 4 + 1) * 128:(C * 4 + 5) * 128], pa)
                        for b in range(4):
                            for kc in range(8):
                                K.mm(pva, cur["xnT"][:, kc, b * 128:(b + 1) * 128], wbf[:, kc, 256:384], start=(kc == 0), stop=(kc == 7))
                            K.cp("dve", Vc[:, C * 4 + b + 1, :], pva)

                tap("KcT", KcT, [128, 2, (NB + 2) * 128], BF16)
                tap("Vc", Vc, [128, NB + 2, 128], BF16)
                checkpoint("G1")
                load_w(wbf, wL1_d, 2048, 1)
                load_w(wobf, woc_d, 1024, None)
                with Scope(K) as PL1:
                    pbig = ps("pbig1", [128, 8, CH], F32, PL1)
                    pw = [subview(pbig, pbig.ap[:, 2 * i:2 * i + 2, :]) for i in range(2)]
                    pnum = subview(pbig, pbig.ap[:, 6, :])
                    pden = subview(pbig, pbig.ap[:, 7, :])
                    bk = [subview(pbig, pbig.ap[:, i, :]) for i in range(6)]
                    qcT = sb("qcT", [128, 8, CH], BF16, PL1)
                    sgc = sb("sgc", [128, 8, CH], BF16, PL1)
                    mixC = sb("mixC", [128, 8, CH], BF16, PL1)
                    pws = [sb("pws%d" % i, [128, 2, 3, 128], BF16, PL1) for i in range(2)]
                    pw2 = [sb("pw2%d" % i, [128, 2, 3, 128], BF16, PL1) for i in range(2)]
                    rs = sb("rs1", [128, CH], F32, PL1)
                    lnr = sb("lnr", [128, CH], F32, PL1)
                    t1 = sb("t11", [128, CH], F32, PL1)
                    x2 = [sb("x2%d" % i, [128, 1024], F32, PL1) for i in range(2)]
                    yo = [sb("yo%d" % i, [128, 1024], F32, PL1) for i in range(2)]
                    yo_slot = [K.slot() for _ in range(2)]
                    ss2 = sb("ss2", [128, 2], F32, PL1)
                    ln2 = sb("ln2", [128, 2], F32, PL1)
                    r2 = sb("r2", [128, 2], F32, PL1)
                    it = 0
                    mxl = alloc_mx(PL1, full=False)
                    xch = mxl.xch
                    junk = sb("junk1", [128, 1024], BF16, PL1)
                    fnbc = sb("fnbc", [128, 1024], F32, PL1)
                    fn_slot = K.slot()
                    K.dma("sp", fnbc, V(fn_d.ap.to_broadcast([128, 1024]), fn_d.res), fn_slot)
                    load_xnT(xnT1_d, 0)
                    for C in range(NCH):
                        use_xnT(C)
                        if C + 1 < NCH:
                            load_xnT(xnT1_d, C + 1)
                        load_x(mxl, x1_d, C)
                        handoff(pw, bk[0:4])
                        for t in range(8):
                            pa_ = bk[t % 6]
                            proj(pa_, t * 128)
                            K.cp("act" if t % 2 == 0 else "dve", qcT[:, t, :], pa_)
                        for t in range(8):
                            pa_ = bk[(t + 2) % 6]
                            proj(pa_, 1024 + t * 128)
                            K.act(sgc[:, t, :], pa_, AF.Silu)
                        handoff(bk[0:4], pw)
                        items = [(t, qi) for t in range(8) for qi in range(4)]

                        def wqk(t, qi, w):
                            kv = t // 4
                            i = C * 4 + qi
                            qs = slice(qi * 128, (qi + 1) * 128)
                            for o in range(3):
                                sl = 2 - o
                                ks = slice((i + o) * 128, (i + o + 1) * 128)
                                K.mm(w[:, 0, sl * 128:(sl + 1) * 128], KcT[0:64, kv, ks], qcT[0:64, t, qs])
                                K.mm(w[:, 1, sl * 128:(sl + 1) * 128], KcT[64:128, kv, ks], qcT[64:128, t, qs])

                        wqk(items[0][0], items[0][1], pw[it % 2])
                        for idx, (t, qi) in enumerate(items):
                            kv = t // 4
                            i = C * 4 + qi
                            qs = slice(qi * 128, (qi + 1) * 128)
                            w = pw[it % 2]
                            s1 = pws[it % 2]
                            s2 = pw2[it % 2]
                            it += 1
                            if i in (0, NB // 2 - 1, NB // 2, NB - 1):
                                for o in range(3):
                                    sl = 2 - o
                                    K.act(s1[:, :, sl, :], w[:, :, sl * 128:(sl + 1) * 128], AF.Exp, bias=maskW[:, i * 3 + o:i * 3 + o + 1], scale=0.125)
                            else:
                                K.act(s1, w[:, :, 0:384].re("p h (o q) -> p h o q", o=3), AF.Exp, scale=0.125)
                            K.tt("dve", s2, s1, EBT[:, 2 * t:2 * t + 2, :, :], ALU.mult)
                            if idx + 1 < len(items):
                                wqk(items[idx + 1][0], items[idx + 1][1], pw[it % 2])
                            for o in range(3):
                                sl = 2 - o
                                st, sp_ = (o == 0), (o == 2)
                                vv = Vc[:, i + o, kv * 64:(kv + 1) * 64]
                                K.mm(pnum[0:64, qs], vv, s2[:, 0, sl, :], start=st, stop=sp_)
                                K.mm(pnum[64:128, qs], vv, s2[:, 1, sl, :], start=st, stop=sp_, tp=(0, 64))
                                K.mm(pden[0:64, qs], ones[:, 0:64], s2[:, 0, sl, :], start=st, stop=sp_)
                                K.mm(pden[64:128, qs], ones[:, 0:64], s2[:, 1, sl, :], start=st, stop=sp_, tp=(0, 64))
                            if qi == 3:
                                K.ts("dve", rs, pden, esk[:, t:t + 1], ALU.add)
                                K.cp("dve", t1, pnum)
                                K.act(lnr, rs, AF.Ln)
                                K.act(rs, lnr, AF.Exp, scale=-1.0)
                                K.tt("pool", t1, t1, rs, ALU.mult)
                                K.tt("pool", mixC[:, t, :], t1, sgc[:, t, :], ALU.mult)
                        for b in range(4):
                            bs = slice(b * 128, (b + 1) * 128)
                            py = pw[b % 2]
                            for half in range(2):
                                for f in range(8):
                                    K.mm(py[:, half, :], mixC[:, f, bs], wobf[:, f, half * 512:(half + 1) * 512], start=(f == 0), stop=(f == 7))
                            xo = x2[b % 2]
                            K.tt("dve", xo, py.re("p a c -> p (a c)"), xch[:, b, :], ALU.add)
                            K.act(junk, xo, AF.Square, accum=ss2[:, b % 2:b % 2 + 1])
                            K.act(ln2[:, b % 2:b % 2 + 1], ss2[:, b % 2:b % 2 + 1], AF.Ln, bias=epsb[:, 0:1], scale=1.0 / 1024.0)
                            K.act(r2[:, b % 2:b % 2 + 1], ln2[:, b % 2:b % 2 + 1], AF.Exp, scale=-0.5)
                            yb = yo[b % 2]
                            K.ts("dve", yb, xo, r2[:, b % 2:b % 2 + 1], ALU.mult)
                            K.tt("pool", yb, yb, fnbc, ALU.mult)
                            r0 = C * CH + b * 128
                            K.dma("pool", y_d[r0:r0 + 128, :], yb, yo_slot[b % 2])
    except StopBuild:
        pass
    for s_ in K.slots:
        if s_.cnt:
            nc.gpsimd.wait_ge(s_.sem, s_.cnt)
    return nc, K


def _t5_bucket(rel):
    half = 16
    max_exact = 8
    ret = (rel > 0).astype(np.int32) * half
    dist = np.abs(rel)
    large = max_exact + (np.log(np.maximum(dist, 1) / max_exact) / np.log(128 / max_exact) * (half - max_exact)).astype(np.int32)
    large = np.minimum(large, half - 1)
    return ret + np.where(dist < max_exact, dist, large)


def _static_tables():
    f32 = np.float32
    st = {}
    st["ident"] = np.eye(128, dtype=f32)
    ob = np.zeros((128, 128), f32)
    ob[:64, :64] = 1
    ob[64:, 64:] = 1
    st["onesblk"] = ob
    j = np.arange(128)[:, None]
    i = np.arange(128)[None, :]
    mmat = np.zeros((128, 4, 128), f32)
    mmat[:, 0, :] = np.maximum(i - j, 0)
    mmat[:, 1, :] = (i >= j)
    mmat[:, 2, :] = np.maximum(j - i, 0)
    mmat[:, 3, :] = (j > i)
    st["mmat"] = mmat
    c = np.arange(512) % 128
    iot = np.zeros((128, 4, 512), f32)
    iot[:, 0, :] = c + 1
    iot[:, 1, :] = 128 - c
    iot[:, 2, :] = 127 - c
    iot[:, 3, :] = c
    st["iot"] = iot
    m = np.arange(640)
    rel = 255 - m
    bk = _t5_bucket(rel)
    oh = np.zeros((32, 640), f32)
    oh[bk, m] = 1
    st["oh"] = oh
    st["inwin"] = np.broadcast_to((np.abs(rel) <= 128).astype(f32)[None, :], (16, 640)).copy()
    return st


def _core_tables(is_prompt):
    f32 = np.float32
    seqlen = 4096 if is_prompt else 2048
    t = np.arange(NT) % seqlen
    d = np.arange(128) % 64
    pair = d // 2
    sgn = np.where(d % 2 == 0, -1.0, 1.0)
    quarter = 16
    freqs = (np.float32(10000.0) ** (-np.arange(quarter, dtype=f32) / quarter)).astype(f32)
    row = (t // 64).astype(f32)
    col = (t % 64).astype(f32)
    ang = np.concatenate([row[:, None] * freqs, col[:, None] * freqs], axis=-1).astype(f32)
    angd = ang[:, pair].T.astype(np.float64)
    tabA = np.stack([np.cos(angd), np.sin(angd) * sgn[:, None]]).astype(f32)
    half = 32
    freqs_b = (np.float32(10000.0) ** (-np.arange(half, dtype=f32) / half)).astype(f32)
    angb = (t.astype(f32)[:, None] * freqs_b).astype(f32)
    angbd = angb[:, pair].T.astype(np.float64)
    tabB = np.stack([np.cos(angbd), np.sin(angbd) * sgn[:, None]]).astype(f32)
    seq_of_blk = (np.arange(NB) * 128) // seqlen
    maskA = np.zeros((NCH, NB), f32)
    for C in range(NCH):
        sq = (C * CH) // seqlen
        maskA[C, :] = np.where(seq_of_blk == sq, 0.0, NEG)
    maskA = np.broadcast_to(maskA.reshape(1, -1), (128, NCH * NB)).copy()
    maskW = np.zeros((NB, 3), f32)
    for i in range(NB):
        for o in range(3):
            jb = i + o - 1
            if jb < 0 or jb >= NB or seq_of_blk[jb] != seq_of_blk[i]:
                maskW[i, o] = NEG
    maskW = np.broadcast_to(maskW.reshape(1, -1), (128, NB * 3)).copy()
    cps = seqlen // 128
    rf = np.array([0.0 if (n % cps == 0) else 1.0 for n in range(NB)], f32)
    rb = np.array([0.0 if (n % cps == cps - 1) else 1.0 for n in range(NB)], f32)
    rfb = np.broadcast_to(np.concatenate([rf, rb])[None, :], (128, 64)).copy()
    return {"tabA": tabA, "tabB": tabB, "maskA": maskA, "maskW": maskW, "rfb": rfb}


def _swap(cols):
    cols = np.asarray(cols)
    return cols ^ 1


def _prep_common(norm_g, w_in_ab, qk_norm_a, ret_decay, w_out_ab, w_in_c, sink_c, w_out_c, rel_bias, final_norm):
    f32 = np.float32
    W = np.asarray(w_in_ab[0], f32)
    qa = np.arange(0, 512)
    ka = np.arange(512, 640)
    va = np.arange(640, 768)
    ga = np.arange(768, 1280)
    qb = np.arange(1280, 1536)
    kb = np.arange(1536, 1792)
    vb = np.arange(1792, 2304)
    gb = np.arange(2304, 2816)
    kadup = np.concatenate([ka[0:64], ka[0:64], ka[64:128], ka[64:128]])
    cm = {}
    cm["wG"] = np.ascontiguousarray(W[:, np.concatenate([kadup, _swap(kadup), kb, _swap(kb), va, vb])])
    cm["wLB"] = np.ascontiguousarray(W[:, np.concatenate([qb, _swap(qb), kb, _swap(kb), gb, vb])])
    cm["wLA"] = np.ascontiguousarray(W[:, np.concatenate([qa, _swap(qa), ga])])
    cm["woab"] = np.ascontiguousarray(np.asarray(w_out_ab[0], f32))
    Wc = np.asarray(w_in_c[0], f32)
    kc = np.arange(1024, 1152)
    kcdup = np.concatenate([kc[0:64], kc[0:64], kc[64:128], kc[64:128]])
    cm["wG1"] = np.ascontiguousarray(Wc[:, np.concatenate([kcdup, np.arange(1152, 1280)])])
    cm["wL1"] = np.ascontiguousarray(Wc[:, np.concatenate([np.arange(0, 1024), np.arange(1280, 2304)])])
    cm["woc"] = np.ascontiguousarray(np.asarray(w_out_c[0], f32))
    ng = np.asarray(norm_g, f32)
    cm["gcol"] = np.ascontiguousarray(ng.reshape(2, 8, 128).transpose(2, 0, 1).reshape(128, 16))
    cm["fn"] = np.asarray(final_norm, f32).reshape(1, 1024).copy()
    g = np.asarray(qk_norm_a[0], f32)
    d = np.arange(128) % 64
    cm["gqk"] = np.stack([g[0][d], g[0][d ^ 1], g[1][d], g[1][d ^ 1]], axis=1).astype(f32).copy()
    rd = np.asarray(ret_decay[0], f32)
    hp = (np.arange(128) // 64)
    rdec = np.zeros((128, 12), f32)
    for p in range(2):
        rdec[:, p] = rd[0][2 * p + hp]
        rdec[:, 2 + p] = rd[1][2 * p + hp]
    for h in range(4):
        rdec[:, 4 + h] = rd[0][h]
        rdec[:, 8 + h] = rd[1][h]
    cm["rdec"] = rdec
    sk = np.asarray(sink_c[0], f32)
    sinkl = np.zeros((128, 8), f32)
    for t in range(8):
        sinkl[:, t] = sk[2 * t + hp]
    cm["sinkl"] = sinkl
    cm["relb"] = np.ascontiguousarray(np.asarray(rel_bias, f32))
    cm.update(_static_tables())
    return cm


_CACHE = {}


def kernel(x_prompt, x_sample, norm_g, w_in_ab, qk_norm_a, ret_decay, w_out_ab, w_in_c, sink_c, w_out_c, rel_bias, final_norm):
    xp = np.asarray(x_prompt, np.float32)
    xs = np.asarray(x_sample, np.float32)
    cm = _prep_common(norm_g, w_in_ab, qk_norm_a, ret_decay, w_out_ab, w_in_c, sink_c, w_out_c, rel_bias, final_norm)
    tp = _core_tables(True)
    tsm = _core_tables(False)
    in_maps = []
    for c in range(8):
        m = dict(cm)
        if c < 4:
            m["x"] = np.ascontiguousarray(xp[c])
            m.update(tp)
        else:
            m["x"] = np.ascontiguousarray(xs[2 * (c - 4):2 * (c - 4) + 2].reshape(NT, 1024))
            m.update(tsm)
        in_maps.append(m)
    if "nc" not in _CACHE:
        _CACHE["nc"] = build_program()[0]
    nc = _CACHE["nc"]
    res = run_bass_kernel_spmd(nc, in_maps, core_ids=list(range(8)))
    outs = [np.asarray(r["y"], np.float32) for r in res.results]
    y_prompt = np.stack(outs[0:4], axis=0)
    y_sample = np.stack(outs[4:8], axis=0).reshape(8, 2048, 1024)
    return (y_prompt, y_sample)
```

```python
import numpy as np
import concourse.bass as bass
import concourse.mybir as mybir
from concourse.bass_utils import run_bass_kernel_spmd

F32 = mybir.dt.float32
BF16 = mybir.dt.bfloat16
AF = mybir.ActivationFunctionType
ALU = mybir.AluOpType

NT = 4096
NB = 32
CH = 512
NCH = 8
EPS = 1e-6
NEG = -30000.0


class Prod:
    def __init__(self, sem, inc):
        self.sem = sem
        self.inc = inc
        self.cnt = 0


class Res:
    def __init__(self):
        self.w = {}
        self.r = {}
        self.excl = False


class V:
    def __init__(self, ap, res=None):
        self.ap = ap
        self.res = res if res is not None else Res()

    def __getitem__(self, k):
        return V(self.ap[k], self.res)

    def re(self, pat, **kw):
        return V(self.ap.rearrange(pat, **kw), self.res)

    def bc(self, shape):
        return V(self.ap.to_broadcast(shape), self.res)


class Ker:
    def __init__(self, nc):
        self.nc = nc
        self.eng = {"pe": nc.tensor, "act": nc.scalar, "dve": nc.vector, "pool": nc.gpsimd, "sp": nc.sync}
        self.prod = {}
        for n in ("pe", "act", "dve", "pool"):
            self.prod[n] = Prod(nc.alloc_semaphore("s_" + n), 1)
        self.seen = {n: {} for n in self.eng}
        self.nslot = 0
        self.ninstr = 0

    def slot(self):
        self.nslot += 1
        p = Prod(self.nc.alloc_semaphore("d%d" % self.nslot), 16)
        if hasattr(self, "slots"):
            self.slots.append(p)
        return p

    def _wait(self, en, reads, writes):
        deps = {}
        for v in reads:
            for p, i in v.res.w.items():
                deps[p] = max(deps.get(p, 0), i)
        for v in writes:
            for p, i in v.res.w.items():
                deps[p] = max(deps.get(p, 0), i)
            for p, i in v.res.r.items():
                deps[p] = max(deps.get(p, 0), i)
        e = self.eng[en]
        seen = self.seen[en]
        own = self.prod.get(en)
        for p, i in deps.items():
            if p is own and en == "pe":
                continue
            if seen.get(p, 0) >= i:
                continue
            e.wait_ge(p.sem, i)
            seen[p] = i

    def op(self, en, fn, reads, writes):
        writes = list(writes) + [r for r in reads if r.res.excl]
        self._wait(en, reads, writes)
        ins = fn(self.eng[en])
        p = self.prod[en]
        p.cnt += 1
        ins.then_inc(p.sem, 1)
        for v in reads:
            v.res.r[p] = p.cnt
        for v in writes:
            v.res.w[p] = p.cnt
        self.ninstr += 1

    def dma(self, q, out, in_, slot):
        self._wait(q, [in_], [out])
        ins = self.eng[q].dma_start(out=out.ap, in_=in_.ap)
        slot.cnt += 16
        ins.then_inc(slot.sem, 16)
        in_.res.r[slot] = slot.cnt
        out.res.w[slot] = slot.cnt

    def mm(self, out, lhsT, rhs, start=True, stop=True, tp=None):
        kw = {}
        if tp is not None:
            kw["tile_position"] = tp
        self.op("pe", lambda e: e.matmul(out.ap, lhsT.ap, rhs.ap, start=start, stop=stop, **kw), [lhsT, rhs], [out])

    def tr(self, out, in_, ident):
        self.op("pe", lambda e: e.transpose(out.ap, in_.ap, ident.ap), [in_, ident], [out])

    def act(self, out, in_, func, bias=None, scale=1.0, accum=None):
        reads = [in_]
        kw = {}
        if bias is not None:
            if isinstance(bias, V):
                reads.append(bias)
                kw["bias"] = bias.ap
            else:
                kw["bias"] = bias
        if isinstance(scale, V):
            reads.append(scale)
            kw["scale"] = scale.ap
        else:
            kw["scale"] = scale
        writes = [out]
        if accum is not None:
            writes.append(accum)
            kw["accum_out"] = accum.ap
        self.op("act", lambda e: e.activation(out.ap, in_.ap, func, **kw), reads, writes)

    def tt(self, en, out, a, b, op):
        self.op(en, lambda e: e.tensor_tensor(out.ap, a.ap, b.ap, op), [a, b], [out])

    def stt(self, en, out, in0, scalar, in1, op0, op1):
        reads = [in0, in1]
        s = scalar
        if isinstance(scalar, V):
            reads.append(scalar)
            s = scalar.ap
        self.op(en, lambda e: e.scalar_tensor_tensor(out.ap, in0.ap, s, in1.ap, op0, op1), reads, [out])

    def ts(self, en, out, in0, s1, op0, s2=None, op1=None):
        reads = [in0]
        a1 = s1
        if isinstance(s1, V):
            reads.append(s1)
            a1 = s1.ap
        a2 = s2
        if isinstance(s2, V):
            reads.append(s2)
            a2 = s2.ap
        if op1 is None:
            self.op(en, lambda e: e.tensor_scalar(out.ap, in0.ap, a1, None, op0), reads, [out])
        else:
            self.op(en, lambda e: e.tensor_scalar(out.ap, in0.ap, a1, a2, op0, op1), reads, [out])

    def cp(self, en, out, in_):
        if en == "act":
            self.op("act", lambda e: e.copy(out.ap, in_.ap), [in_], [out])
        else:
            self.op(en, lambda e: e.tensor_copy(out.ap, in_.ap), [in_], [out])

    def amul(self, out, in_, m):
        self.op("act", lambda e: e.mul(out.ap, in_.ap, m.ap), [in_, m], [out])

    def recip(self, out, in_):
        self.op("dve", lambda e: e.reciprocal(out.ap, in_.ap), [in_], [out])

    def memset(self, en, out, val):
        self.op(en, lambda e: e.memset(out.ap, val), [], [out])


class StopBuild(Exception):
    pass


import contextlib


class Scope(contextlib.ExitStack):
    def __init__(self, K):
        super().__init__()
        self.K = K
        self.tiles = []

    def __exit__(self, *a):
        fr = self.K.freed
        for v in self.tiles:
            for d in (v.res.w, v.res.r):
                for p, i in d.items():
                    fr[p] = max(fr.get(p, 0), i)
        self.tiles = []
        return super().__exit__(*a)

    def close(self):
        self.__exit__(None, None, None)


def build_program(stop=None, taps=()):
    nc = bass.Bass("TRN2", target_bir_lowering=False)
    K = Ker(nc)
    K.slots = []
    K.freed = {}
    K.tapped = {}

    def checkpoint(name):
        if stop == name:
            raise StopBuild()

    def tap(name, v, shape, dt=F32):
        if name not in taps or name in K.tapped:
            return
        d = V(nc.dram_tensor("dbg_" + name, list(shape), dt, kind="ExternalOutput").ap())
        K.tapped[name] = d
        K.dma("sp", d, v, K.slot())

    def din(name, shape, dt=F32):
        return V(nc.dram_tensor(name, list(shape), dt, kind="ExternalInput").ap())

    x_d = din("x", [NT, 1024])
    wG_d = din("wG", [1024, 1664])
    wLB_d = din("wLB", [1024, 2048])
    wLA_d = din("wLA", [1024, 1536])
    woab_d = din("woab", [1024, 1024])
    wG1_d = din("wG1", [1024, 384])
    wL1_d = din("wL1", [1024, 2048])
    woc_d = din("woc", [1024, 1024])
    gcol_d = din("gcol", [128, 16])
    fn_d = din("fn", [1, 1024])
    gqk_d = din("gqk", [128, 4])
    rdec_d = din("rdec", [128, 12])
    sink_d = din("sinkl", [128, 8])
    relb_d = din("relb", [32, 16])
    ident_d = din("ident", [128, 128])
    onesblk_d = din("onesblk", [128, 128])
    mm_d = din("mmat", [128, 4, 128])
    iot_d = din("iot", [128, 4, 512])
    oh_d = din("oh", [32, 640])
    inwin_d = din("inwin", [16, 640])
    tabA_d = din("tabA", [2, 128, NT])
    tabB_d = din("tabB", [2, 128, NT])
    maskA_d = din("maskA", [128, 256])
    maskW_d = din("maskW", [128, 96])
    rfb_d = din("rfb", [128, 64])
    y_d = V(nc.dram_tensor("y", [NT, 1024], F32, kind="ExternalOutput").ap())
    x1_d = V(nc.dram_tensor("x1s", [NT, 1024], F32, kind="Internal").ap())
    mixb_d = V(nc.dram_tensor("mixbs", [4, 128, NT], BF16, kind="Internal").ap())
    vec_d = V(nc.dram_tensor("vecs", [16, 640], BF16, kind="Internal").ap())
    xnT0_d = V(nc.dram_tensor("xnT0s", [NCH, 128, 8, CH], BF16, kind="Internal").ap())
    xnT1_d = V(nc.dram_tensor("xnT1s", [NCH, 128, 8, CH], BF16, kind="Internal").ap())

    es = Scope(K)
    uid = [0]

    def sb(name, shape, dt=F32, stack=None):
        uid[0] += 1
        st_ = stack if stack is not None else es
        t = st_.enter_context(nc.sbuf_tensor("sb%d_%s" % (uid[0], name), list(shape), dt))
        v = V(t[:])
        v.res.w = dict(K.freed)
        st_.tiles.append(v)
        return v

    def ps(name, shape, dt=F32, stack=None):
        uid[0] += 1
        st_ = stack if stack is not None else es
        t = st_.enter_context(nc.psum_tensor("ps%d_%s" % (uid[0], name), list(shape), dt))
        v = V(t[:])
        v.res.excl = True
        v.res.w = dict(K.freed)
        st_.tiles.append(v)
        return v

    try:
        with es:
            cslot = K.slot()
            consts = []

            def cload(name, src, shape, dt=F32, q="sp"):
                t = sb(name, shape, dt)
                K.dma(q, t, src, cslot)
                consts.append(t)
                return t

            gcol = cload("gcol", gcol_d, [128, 16])
            gqk = cload("gqk", gqk_d, [128, 4])
            rdec = cload("rdec", rdec_d, [128, 12])
            sinkl = cload("sinkl", sink_d, [128, 8])
            maskA = cload("maskA", maskA_d, [128, 256])
            maskW = cload("maskW", maskW_d, [128, 96])
            rfb = cload("rfb", rfb_d, [128, 64])
            ident32 = cload("ident32", ident_d, [128, 128])
            onesblk32 = cload("onesblk32", onesblk_d, [128, 128])
            for c in consts:
                c.res.w[cslot] = cslot.cnt
            ident = sb("ident", [128, 128], BF16)
            onesblk = sb("onesblk", [128, 128], BF16)
            ones = sb("ones", [128, 128], BF16)
            epsb = sb("epsb", [128, 1])
            K.cp("dve", ident, ident32)
            K.cp("dve", onesblk, onesblk32)
            K.memset("dve", ones, 1.0)
            K.memset("dve", epsb, EPS)

            checkpoint("c0")
            wbf = sb("wbf", [128, 8, 2048], BF16)
            wobf = sb("wobf", [128, 8, 1024], BF16)
            wst = [sb("wst%d" % i, [128, 1024]) for i in range(2)]
            wst_slot = [K.slot() for _ in range(2)]
            xnTs = [sb("xnT%d" % i, [128, 8, CH], BF16) for i in range(2)]
            xnT_slot = [K.slot() for _ in range(2)]
            cur = {"xnT": xnTs[0]}
            xch_slot = [K.slot() for _ in range(4)]
            xst_slot = [K.slot() for _ in range(2)]
            EBT = sb("EBT", [128, 16, 3, 128], BF16)
            esk = sb("esk", [128, 8], F32)
            K.act(esk, sinkl, AF.Exp)
            with Scope(K) as S1:
                relb = sb("relb", [32, 16], F32, S1)
                oh = sb("oh", [32, 640], F32, S1)
                inw = sb("inw", [16, 640], F32, S1)
                e_slot = K.slot()
                K.dma("sp", relb, relb_d, e_slot)
                K.dma("sp", oh, oh_d, e_slot)
                K.dma("sp", inw, inwin_d, e_slot)
                for t_ in (relb, oh, inw):
                    t_.res.w[e_slot] = e_slot.cnt
                pv = ps("pv", [16, 1024], F32, S1)[:, 0:640]
                vec = sb("vec", [16, 640], F32, S1)
                vecb = sb("vecb", [16, 640], BF16, S1)
                K.mm(pv[:, 0:512], relb, oh[:, 0:512])
                K.mm(pv[:, 512:640], relb, oh[:, 512:640])
                K.act(vec, pv, AF.Exp)
                K.tt("dve", vecb, vec, inw, ALU.mult)
                v_slot = K.slot()
                K.dma("sp", vec_d, vecb, v_slot)
                g_slot = [K.slot() for _ in range(2)]
                for k in range(128):
                    qn = "sp" if (k % 2 == 0) else "pool"
                    K.dma(qn, EBT[k:k + 1, :, :, :], V(vec_d.ap[:, 127 - k:127 - k + 384].rearrange("(a h) (o q) -> a h o q", a=1, o=3), vec_d.res), g_slot[k % 2])
            wcount = [0]

            def handoff(srcs, dsts):
                for d_ in dsts:
                    for s_ in srcs:
                        for dd in (s_.res.w, s_.res.r):
                            for p_, i_ in dd.items():
                                d_.res.w[p_] = max(d_.res.w.get(p_, 0), i_)

            def subview(parent, ap):
                v = V(ap)
                v.res.excl = parent.res.excl
                v.res.w = dict(parent.res.w)
                return v

            class MX:
                pass

            def alloc_mx(scope, full=True):
                m = MX()
                m.xch = sb("xch", [128, 4, 1024], F32, scope)
                if full:
                    m.xn = [sb("xn%d" % i, [128, 1024], BF16, scope) for i in range(2)]
                    m.junk = sb("junk", [128, 1024], BF16, scope)
                    m.ss = sb("ss", [128, 4], F32, scope)
                    m.lnv4 = sb("lnv4", [128, 4], F32, scope)
                    m.rstd4 = sb("rstd4", [128, 4], F32, scope)
                return m

            def load_x(m, src_d, C):
                for b in range(4):
                    r0 = C * CH + b * 128
                    K.dma("sp", m.xch[:, b, :], src_d[r0:r0 + 128, :], xch_slot[b])

            def store_xnT(dst_d, C, slot_i):
                K.dma("pool", dst_d[C], cur["xnT"], xst_slot[slot_i])

            def load_xnT(src_d, C):
                i = C % 2
                K.dma("sp", xnTs[i], src_d[C], xnT_slot[i])

            def use_xnT(C):
                cur["xnT"] = xnTs[C % 2]

            def load_w(dst, src_d, ncols, layer_g):
                for kc in range(8):
                    for c0 in range(0, ncols, 1024):
                        c1 = min(ncols, c0 + 1024)
                        i = wcount[0] % 2
                        wcount[0] += 1
                        K.dma("sp", wst[i][:, 0:c1 - c0], src_d[kc * 128:(kc + 1) * 128, c0:c1], wst_slot[i])
                        en = "act" if (wcount[0] % 2 == 0) else "dve"
                        if layer_g is None:
                            K.cp(en, dst[:, kc, c0:c1], wst[i][:, 0:c1 - c0])
                        elif en == "act":
                            K.amul(dst[:, kc, c0:c1], wst[i][:, 0:c1 - c0], gcol[:, layer_g * 8 + kc:layer_g * 8 + kc + 1])
                        else:
                            K.ts("dve", dst[:, kc, c0:c1], wst[i][:, 0:c1 - c0], gcol[:, layer_g * 8 + kc:layer_g * 8 + kc + 1], ALU.mult)

            def make_xnT(m, src_d, C, pT):
                use_xnT(C)
                xnT = cur["xnT"]
                load_x(m, src_d, C)
                for b in range(4):
                    K.act(m.junk, m.xch[:, b, :], AF.Square, accum=m.ss[:, b:b + 1])
                K.act(m.lnv4, m.ss, AF.Ln, bias=epsb[:, 0:1], scale=1.0 / 1024.0)
                K.act(m.rstd4, m.lnv4, AF.Exp, scale=-0.5)
                for b in range(4):
                    xb = m.xn[b % 2]
                    K.ts("dve", xb, m.xch[:, b, :], m.rstd4[:, b:b + 1], ALU.mult)
                    for kc in range(8):
                        K.tr(pT[:, kc, :], xb[:, kc * 128:(kc + 1) * 128], ident)
                    K.cp("act", xnT[:, :, b * 128:(b + 1) * 128], pT)

            def proj(dst, c0):
                for kc in range(8):
                    K.mm(dst, wbf[:, kc, c0:c0 + 128], cur["xnT"][:, kc, :], start=(kc == 0), stop=(kc == 7))

            def rsq_bcast(dst, src_ps, nfeat, sq, psn, lnv, lhs_ones):
                K.act(sq, src_ps, AF.Square)
                K.mm(psn, lhs_ones, sq)
                K.act(lnv, psn, AF.Ln, bias=epsb[:, 0:1], scale=1.0 / nfeat)
                K.act(dst, lnv, AF.Exp, scale=-0.5)

            with Scope(K) as L0:
                LR = Scope(K)
                KaT = sb("KaT", [128, 2, NT], BF16, L0)
                Va = sb("Va", [128, NB, 128], BF16, L0)
                tabc = sb("tabc", [128, 2, CH], F32, L0)
                tab_slot = K.slot()
                tabd_slot = K.slot()
                sq = sb("sq", [128, CH], BF16, L0)
                lnv = sb("lnv", [128, CH], F32, L0)
                rs = sb("rs", [128, CH], F32, L0)
                t1 = sb("t1", [128, CH], F32, L0)
                t2 = sb("t2", [128, CH], F32, L0)
                tabg = sb("tabg", [128, 2, CH], F32, L0)
                SbAll = sb("SbAll", [128, 2, NB, 128], BF16, LR)
                tabd = sb("tabd", [128, 2, CH], F32, LR)
                vbtm = sb("vbtm", [128, 4, 512], BF16, LR)
                lg = sb("lg", [128, 12], F32, LR)
                K.act(lg, rdec, AF.Exp)
                K.ts("dve", lg, lg, -1.0, ALU.mult)
                cd = sb("cd", [128, 4], F32, LR)
                K.act(cd, lg[:, 0:4], AF.Exp, scale=128.0)
                cdr = sb("cdr", [128, 4, NB], F32, LR)
                for j in range(4):
                    off = 0 if j < 2 else 32
                    K.ts("dve", cdr[:, j, :], rfb[:, off:off + 32], cd[:, j:j + 1], ALU.mult)
                checkpoint("c1")
                QF4 = sb("QF4", [128, 2, CH], F32, LR)
                QB4 = sb("QB4", [128, 2, CH], F32, LR)
                KF4 = sb("KF4", [128, 2, CH], F32, LR)
                KB4 = sb("KB4", [128, 2, CH], F32, LR)
                DT = sb("DT", [128, 4, 128], F32, LR)
                with Scope(K) as S0:
                    iot = sb("iot", [128, 4, CH], F32, S0)
                    K.dma("sp", iot, iot_d, tabd_slot)
                    for p in range(2):
                        K.act(QF4[:, p, :], iot[:, 0, :], AF.Exp, scale=lg[:, p:p + 1])
                        K.act(QB4[:, p, :], iot[:, 1, :], AF.Exp, scale=lg[:, 2 + p:3 + p])
                        K.act(KF4[:, p, :], iot[:, 2, :], AF.Exp, scale=lg[:, p:p + 1])
                        K.act(KB4[:, p, :], iot[:, 3, :], AF.Exp, scale=lg[:, 2 + p:3 + p])
                    K.ts("dve", KF4, KF4, 0.125, ALU.mult)
                    K.ts("dve", KB4, KB4, 0.125, ALU.mult)
                    mmat = sb("mmat", [128, 4, 128], F32, S0)
                    K.dma("sp", mmat, mm_d, tab_slot)
                    d1 = sb("d1", [128, 128], F32, S0)
                    d2 = sb("d2", [128, 128], F32, S0)
                    for h in range(4):
                        checkpoint("d0")
                        K.act(d1, mmat[:, 0, :], AF.Exp, scale=lg[:, 4 + h:5 + h])
                        checkpoint("d1")
                        K.tt("dve", d1, d1, mmat[:, 1, :], ALU.mult)
                        checkpoint("d2")
                        K.act(d2, mmat[:, 2, :], AF.Exp, scale=lg[:, 8 + h:9 + h])
                        K.tt("dve", d2, d2, mmat[:, 3, :], ALU.mult)
                        K.tt("dve", d1, d1, d2, ALU.add)
                        checkpoint("d3")
                        K.ts("dve", DT[:, h, :], d1, 0.125, ALU.mult)
                        checkpoint("d4")

                def load_tab(dst, slot, src_d, C):
                    K.dma("sp", dst, V(src_d.ap[:, :, C * CH:(C + 1) * CH].rearrange("t p c -> p t c"), src_d.res), slot)

                def rope(psa, psb, tab, out32, ga=None, gb=None):
                    if ga is None:
                        K.tt("dve", t1, psa, tab[:, 0, :], ALU.mult)
                        K.tt("dve", t2, psb, tab[:, 1, :], ALU.mult)
                    else:
                        K.amul(tabg[:, 0, :], tab[:, 0, :], ga)
                        K.amul(tabg[:, 1, :], tab[:, 1, :], gb)
                        K.tt("dve", t1, psa, tabg[:, 0, :], ALU.mult)
                        K.tt("dve", t2, psb, tabg[:, 1, :], ALU.mult)
                    K.tt("pool", out32, t1, t2, ALU.add)

                checkpoint("setup0")
                load_w(wbf, wG_d, 1664, 0)
                with Scope(K) as PG:
                    pT = ps("pT", [128, 8, 128], BF16, PG)
                    pk = pT.re("p (c t) q -> p c t q", t=2)
                    pbig = ps("pbigG", [128, 7, CH], F32, PG)
                    bk = [subview(pbig, pbig.ap[:, i, :]) for i in range(7)]
                    pn = bk[4]
                    pkv = V(bk[6].ap[:, 0:256].rearrange("p (a b) -> p a b", a=2), bk[6].res)
                    kdbT = sb("kdbT", [128, 2, CH], BF16, PG)
                    kdbtm = sb("kdbtm", [128, 4, 2, 128], BF16, PG)
                    Rb = sb("Rb", [128, 2, 128], F32, PG)
                    mxg = alloc_mx(PG)
                    WS = [dict(sq=sq, lnv=lnv, rs=rs, t1=t1, t2=t2),
                          dict(sq=sb("wsq", [128, CH], BF16, PG), lnv=sb("wlnv", [128, CH], F32, PG),
                               rs=sb("wrs", [128, CH], F32, PG), t1=sb("wt1", [128, CH], F32, PG),
                               t2=sb("wt2", [128, CH], F32, PG))]
                    K.memset("dve", Rb, 0.0)
                    for C in range(NCH - 1, -1, -1):
                        make_xnT(mxg, x_d, C, pT)
                        store_xnT(xnT0_d, C, C % 2)
                        load_tab(tabc, tab_slot, tabA_d, C)
                        load_tab(tabd, tabd_slot, tabB_d, C)
                        K.amul(tabg[:, 0, :], tabc[:, 0, :], gqk[:, 2:3])
                        K.amul(tabg[:, 1, :], tabc[:, 1, :], gqk[:, 3:4])
                        for t in range(2):
                            w_ = WS[t % 2]
                            pa_, pb_ = bk[2 * t], bk[2 * t + 1]
                            proj(pa_, t * 128)
                            proj(pb_, 256 + t * 128)
                            rsq_bcast(w_["rs"], pa_, 64.0, w_["sq"], pn, w_["lnv"], onesblk)
                            K.tt("dve", w_["t1"], pa_, tabg[:, 0, :], ALU.mult)
                            K.tt("dve", w_["t2"], pb_, tabg[:, 1, :], ALU.mult)
                            K.tt("pool", w_["t1"], w_["t1"], w_["t2"], ALU.add)
                            K.tt("pool", KaT[:, t, C * CH:(C + 1) * CH], w_["t1"], w_["rs"], ALU.mult)
                        for t in range(2):
                            w_ = WS[t % 2]
                            pa_, pb_ = bk[2 * t], bk[2 * t + 1]
                            proj(pa_, 512 + t * 128)
                            proj(pb_, 768 + t * 128)
                            K.tt("dve", w_["t1"], pa_, tabd[:, 0, :], ALU.mult)
                            K.tt("dve", w_["t2"], pb_, tabd[:, 1, :], ALU.mult)
                            K.tt("pool", w_["t1"], w_["t1"], w_["t2"], ALU.add)
                            K.tt("pool", kdbT[:, t, :], w_["t1"], KB4[:, t, :], ALU.mult)
                            for cj in range(4):
                                K.tr(pk[:, cj, t, :], kdbT[:, t, cj * 128:(cj + 1) * 128], ident)
                        K.cp("act", kdbtm, pk)
                        for b in range(4):
                            pva = (bk[4] if b % 2 == 0 else bk[2])[:, 0:128]
                            pvb = bk[5] if b % 2 == 0 else bk[3]
                            for kc in range(8):
                                K.mm(pva, cur["xnT"][:, kc, b * 128:(b + 1) * 128], wbf[:, kc, 1024:1152], start=(kc == 0), stop=(kc == 7))
                            for kc in range(8):
                                K.mm(pvb, cur["xnT"][:, kc, b * 128:(b + 1) * 128], wbf[:, kc, 1152:1664], start=(kc == 0), stop=(kc == 7))
                            K.cp("act", Va[:, C * 4 + b, :], pva)
                            K.cp("dve", vbtm[:, b, :], pvb)
                        for cj in range(3, -1, -1):
                            n = C * 4 + cj
                            for p in range(2):
                                K.mm(pkv[0:64, p, :], kdbtm[:, cj, p, 0:64], vbtm[:, cj, (2 * p) * 128:(2 * p + 1) * 128])
                                K.mm(pkv[64:128, p, :], kdbtm[:, cj, p, 64:128], vbtm[:, cj, (2 * p + 1) * 128:(2 * p + 2) * 128], tp=(0, 64))
                            K.ts("dve", SbAll[:, :, n, :], Rb, rfb[:, 32 + n:33 + n], ALU.mult)
                            for p in range(2):
                                K.ts("dve", Rb[:, p, :], Rb[:, p, :], cdr[:, 2 + p, n:n + 1], ALU.mult)
                                K.tt("dve", Rb[:, p, :], pkv[:, p, :], Rb[:, p, :], ALU.add)

                tap("KaT", KaT, [128, 2, NT], BF16)
                tap("Va", Va, [128, NB, 128], BF16)
                tap("SbAll", SbAll, [128, 2, NB, 128], BF16)
                checkpoint("G")
                load_w(wbf, wLB_d, 2048, 0)
                with Scope(K) as PB:
                    pT = ps("pT", [128, 8, 128], BF16, PB)
                    pk = pT.re("p (c t) q -> p c t q", t=2)
                    pbig = ps("pbigB", [128, 7, CH], F32, PB)
                    bk = [subview(pbig, pbig.ap[:, i, :]) for i in range(7)]
                    pa, pb, pss = bk[0], bk[1], bk[2]
                    po = subview(pbig, pbig.ap[:, 3:7, :])
                    qrT = sb("qrT", [128, 2, CH], BF16, PB)
                    qdf = sb("qdf", [128, 2, CH], BF16, PB)
                    qdb = sb("qdb", [128, 2, CH], BF16, PB)
                    krT = sb("krT", [128, 2, CH], BF16, PB)
                    kdfT = sb("kdfT", [128, 2, CH], BF16, PB)
                    kdftm = sb("kdftm", [128, 4, 2, 128], BF16, PB)
                    sg = sb("sg", [128, 4, CH], BF16, PB)
                    ATs = [sb("AT%d" % i, [128, 4, 128], BF16, PB) for i in range(2)]
                    Sfs = [sb("Sf%d" % i, [128, 2, 128], BF16, PB) for i in range(2)]
                    Rf = sb("Rf", [128, 2, 128], F32, PB)
                    mixBc = [sb("mixBc%d" % i, [128, 4, CH], BF16, PB) for i in range(1)]
                    mixB_slot = [K.slot() for _ in range(1)]
                    WS = [dict(sq=sq, lnv=lnv, rs=rs, t1=t1, t2=t2),
                          dict(sq=sb("wsq", [128, CH], BF16, PB), lnv=sb("wlnv", [128, CH], F32, PB),
                               rs=sb("wrs", [128, CH], F32, PB), t1=sb("wt1", [128, CH], F32, PB),
                               t2=sb("wt2", [128, CH], F32, PB))]
                    K.memset("dve", Rf, 0.0)
                    load_xnT(xnT0_d, 0)
                    pairs = [(bk[0], bk[1]), (bk[3], bk[4]), (bk[5], bk[6])]
                    for C in range(NCH):
                        use_xnT(C)
                        if C + 1 < NCH:
                            load_xnT(xnT0_d, C + 1)
                        load_tab(tabd, tabd_slot, tabB_d, C)
                        handoff([po], bk[3:7])
                        ip = 0
                        for t in range(2):
                            w_ = WS[ip % 2]
                            pa_, pb_ = pairs[ip % 3]
                            ip += 1
                            proj(pa_, t * 128)
                            proj(pb_, 256 + t * 128)
                            K.tt("dve", w_["t1"], pa_, tabd[:, 0, :], ALU.mult)
                            K.tt("dve", w_["t2"], pb_, tabd[:, 1, :], ALU.mult)
                            K.tt("pool", w_["t1"], w_["t1"], w_["t2"], ALU.add)
                            K.cp("act", qrT[:, t, :], w_["t1"])
                            K.tt("pool", qdf[:, t, :], w_["t1"], QF4[:, t, :], ALU.mult)
                            K.tt("pool", qdb[:, t, :], w_["t1"], QB4[:, t, :], ALU.mult)
                        for t in range(2):
                            w_ = WS[ip % 2]
                            pa_, pb_ = pairs[ip % 3]
                            ip += 1
                            proj(pa_, 512 + t * 128)
                            proj(pb_, 768 + t * 128)
                            K.tt("dve", w_["t1"], pa_, tabd[:, 0, :], ALU.mult)
                            K.tt("dve", w_["t2"], pb_, tabd[:, 1, :], ALU.mult)
                            K.tt("pool", w_["t1"], w_["t1"], w_["t2"], ALU.add)
                            K.cp("act", krT[:, t, :], w_["t1"])
                            K.tt("pool", kdfT[:, t, :], w_["t1"], KF4[:, t, :], ALU.mult)
                            for cj in range(4):
                                K.tr(pk[:, cj, t, :], kdfT[:, t, cj * 128:(cj + 1) * 128], ident)
                        K.cp("act", kdftm, pk)
                        for h in range(4):
                            pa_ = bk[3 + h]
                            proj(pa_, 1024 + h * 128)
                            K.act(sg[:, h, :], pa_, AF.Silu)
                        for b in range(4):
                            pv_ = bk[1 + b % 2]
                            for kc in range(8):
                                K.mm(pv_, cur["xnT"][:, kc, b * 128:(b + 1) * 128], wbf[:, kc, 1536:2048], start=(kc == 0), stop=(kc == 7))
                            K.cp("dve", vbtm[:, b, :], pv_)
                        handoff(bk[3:7], [po])
                        for cj in range(4):
                            n = C * 4 + cj
                            cs = slice(cj * 128, (cj + 1) * 128)
                            Sf = Sfs[cj % 2]
                            AT = ATs[cj % 2]
                            K.ts("dve", Sf, Rf, rfb[:, n:n + 1], ALU.mult)
                            for p in range(2):
                                K.mm(pa[0:64, p * 128:(p + 1) * 128], kdftm[:, cj, p, 0:64], vbtm[:, cj, (2 * p) * 128:(2 * p + 1) * 128])
                                K.mm(pa[64:128, p * 128:(p + 1) * 128], kdftm[:, cj, p, 64:128], vbtm[:, cj, (2 * p + 1) * 128:(2 * p + 2) * 128], tp=(0, 64))
                            for p in range(2):
                                K.ts("dve", Rf[:, p, :], Rf[:, p, :], cdr[:, p, n:n + 1], ALU.mult)
                                K.tt("dve", Rf[:, p, :], pa[:, p * 128:(p + 1) * 128], Rf[:, p, :], ALU.add)
                            for h in range(4):
                                t, r0 = h // 2, (h % 2) * 64
                                pdst = pss if (h % 2 == 0) else pb
                                K.mm(pdst[:, t * 128:(t + 1) * 128], krT[r0:r0 + 64, t, cs], qrT[r0:r0 + 64, t, cs])
                            ATv = AT.re("p (t hp) i -> p hp t i", hp=2)
                            DTv = DT.re("p (t hp) i -> p hp t i", hp=2)
                            K.tt("dve", ATv[:, 0, :, :], pss[:, 0:256].re("p (t i) -> p t i", t=2), DTv[:, 0, :, :], ALU.mult)
                            K.tt("dve", ATv[:, 1, :, :], pb[:, 0:256].re("p (t i) -> p t i", t=2), DTv[:, 1, :, :], ALU.mult)
                            for h in range(4):
                                t, r0 = h // 2, (h % 2) * 64
                                K.mm(po[:, h, cs], vbtm[:, cj, h * 128:(h + 1) * 128], AT[:, h, :], start=True, stop=False)
                                K.mm(po[:, h, cs], Sf[r0:r0 + 64, t, :], qdf[r0:r0 + 64, t, cs], start=False, stop=False)
                                K.mm(po[:, h, cs], SbAll[r0:r0 + 64, t, n, :], qdb[r0:r0 + 64, t, cs], start=False, stop=True)
                        mb = mixBc[0]
                        for h in range(4):
                            w_ = WS[h % 2]
                            psn_ = bk[h % 3]
                            rsq_bcast(w_["rs"], po[:, h, :], 128.0, w_["sq"], psn_, w_["lnv"], ones)
                            K.tt("dve", w_["t1"], po[:, h, :], w_["rs"], ALU.mult)
                            K.tt("pool", mb[:, h, :], w_["t1"], sg[:, h, :], ALU.mult)
                        K.dma("pool", V(mixb_d.ap[:, :, C * CH:(C + 1) * CH].rearrange("h p c -> p h c"), mixb_d.res), mb, mixB_slot[0])

                tap("mixb", mixb_d, [4, 128, NT], BF16)
                checkpoint("LB")
                LR.close()
                load_w(wbf, wLA_d, 1536, 0)
                load_w(wobf, woab_d, 1024, None)
                with Scope(K) as PA:
                    pbig = ps("pbig", [128, 8, CH], F32, PA)
                    psc = [subview(pbig, pbig.ap[:, 2 * i:2 * i + 2, :]) for i in range(3)]
                    pnum = subview(pbig, pbig.ap[:, 6, :])
                    pden = subview(pbig, pbig.ap[:, 7, :])
                    bk = [subview(pbig, pbig.ap[:, i, :]) for i in range(6)] + [pnum, pden]
                    WS = [dict(sq=sq, lnv=lnv, rs=rs, t1=t1, t2=t2),
                          dict(sq=sb("wsq", [128, CH], BF16, PA), lnv=sb("wlnv", [128, CH], F32, PA),
                               rs=sb("wrs", [128, CH], F32, PA), t1=sb("wt1", [128, CH], F32, PA),
                               t2=sb("wt2", [128, CH], F32, PA))]
                    qaT = sb("qaT", [128, 4, CH], BF16, PA)
                    sga = sb("sga", [128, 4, CH], BF16, PA)
                    mixA = sb("mixA", [128, 4, CH], BF16, PA)
                    mixBl = sb("mixBl", [128, 4, CH], BF16, PA)
                    mixBl_slot = K.slot()
                    pTs = [sb("pTs%d" % i, [128, 2, CH], BF16, PA) for i in range(3)]
                    x1b = [sb("x1b%d" % i, [128, 1024], F32, PA) for i in range(2)]
                    dcp = sb("dcp", [128, CH], F32, PA)
                    ncp = sb("ncp", [128, CH], F32, PA)
                    x1b_slot = [K.slot() for _ in range(2)]
                    mxa = alloc_mx(PA, full=False)
                    xch = mxa.xch
                    qaTs = [qaT, sb("qaT1", [128, 4, CH], BF16, PA)]
                    sgas = [sga, sb("sga1", [128, 4, CH], BF16, PA)]
                    tabcs = [tabc, sb("tabc1", [128, 2, CH], F32, PA)]
                    tabgs = [tabg, sb("tabg1", [128, 2, CH], F32, PA)]
                    tabsl = [tab_slot, K.slot()]
                    nbuf = [0]
                    npt = [0]

                    held = set()

                    def take_buf():
                        while True:
                            i_ = nbuf[0] % 3
                            nbuf[0] += 1
                            if i_ not in held:
                                return psc[i_]

                    def hold(b_):
                        held.add(psc.index(b_))

                    def release(b_):
                        held.discard(psc.index(b_))

                    def projx(dst, c0, xT):
                        for kc in range(8):
                            K.mm(dst, wbf[:, kc, c0:c0 + 128], xT[:, kc, :], start=(kc == 0), stop=(kc == 7))

                    def proj_items(Cn):
                        q_, g_ = qaTs[Cn % 2], sgas[Cn % 2]
                        tc_, tg_ = tabcs[Cn % 2], tabgs[Cn % 2]
                        xT = xnTs[Cn % 2]

                        def prep():
                            load_tab(tc_, tabsl[Cn % 2], tabA_d, Cn)
                            K.amul(tg_[:, 0, :], tc_[:, 0, :], gqk[:, 0:1])
                            K.amul(tg_[:, 1, :], tc_[:, 1, :], gqk[:, 1:2])

                        items = []
                        for t in range(4):
                            def mk(t=t):
                                st = {}
                                w_ = WS[t % 2]

                                def s1():
                                    st["buf"] = take_buf()
                                    hold(st["buf"])
                                    projx(st["buf"][:, 0, :], t * 128, xT)
                                    projx(st["buf"][:, 1, :], 512 + t * 128, xT)

                                def s2():
                                    pa_, pb_ = st["buf"][:, 0, :], st["buf"][:, 1, :]
                                    K.tt("dve", w_["t1"], pa_, tg_[:, 0, :], ALU.mult)
                                    K.tt("dve", w_["t2"], pb_, tg_[:, 1, :], ALU.mult)
                                    K.act(w_["sq"], pa_, AF.Square)

                                def s3():
                                    K.mm(st["buf"][:, 1, :], onesblk, w_["sq"])

                                def s4():
                                    K.act(w_["lnv"], st["buf"][:, 1, :], AF.Ln, bias=epsb[:, 0:1], scale=1.0 / 64.0)
                                    K.act(w_["rs"], w_["lnv"], AF.Exp, scale=-0.5)
                                    K.tt("pool", w_["t1"], w_["t1"], w_["t2"], ALU.add)
                                    K.tt("pool", q_[:, t, :], w_["t1"], w_["rs"], ALU.mult)
                                    release(st["buf"])
                                return [(s1, 3), (s2, 1), (s3, 2), (s4, 0)]
                            items.append(mk())
                        for t2_ in range(2):
                            def mk(t2_=t2_):
                                st = {}

                                def s1():
                                    st["buf"] = take_buf()
                                    hold(st["buf"])
                                    for j in range(2):
                                        projx(st["buf"][:, j, :], 1024 + (2 * t2_ + j) * 128, xT)

                                def s2():
                                    for j in range(2):
                                        K.act(g_[:, 2 * t2_ + j, :], st["buf"][:, j, :], AF.Silu)
                                    release(st["buf"])
                                return [(s1, 3), (s2, 0)]
                            items.append(mk())
                        return prep, items

                    load_xnT(xnT0_d, 0)
                    prep0, items0 = proj_items(0)
                    prep0()
                    for it_ in items0:
                        for st_fn, _d in it_:
                            st_fn()
                    for C in range(NCH):
                        use_xnT(C)
                        qaT_c, sga_c = qaTs[C % 2], sgas[C % 2]
                        pending = []
                        if C + 1 < NCH:
                            load_xnT(xnT0_d, C + 1)
                            prepn, pending = proj_items(C + 1)
                            prepn()
                        load_x(mxa, x_d, C)
                        K.dma("pool", mixBl, V(mixb_d.ap[:, :, C * CH:(C + 1) * CH].rearrange("h p c -> p h c"), mixb_d.res), mixBl_slot)
                        nit = 0
                        active = [None]
                        for t in range(4):
                            kv = t // 2
                            fifo = []

                            def qk(kb_):
                                sc_ = take_buf()
                                fifo.append(sc_)
                                ks = slice(kb_ * 128, (kb_ + 1) * 128)
                                K.mm(sc_[:, 0, :], KaT[0:64, kv, ks], qaT_c[0:64, t, :])
                                K.mm(sc_[:, 1, :], KaT[64:128, kv, ks], qaT_c[64:128, t, :])

                            qk(0)
                            qk(1)
                            for kb in range(NB):
                                sc = fifo.pop(0)
                                pt = pTs[npt[0] % 3]
                                npt[0] += 1
                                nit += 1
                                K.act(pt, sc, AF.Exp, bias=maskA[:, C * NB + kb:C * NB + kb + 1], scale=0.125)
                                if kb + 2 < NB:
                                    qk(kb + 2)
                                if active[0] is None and pending and nit % 16 == 4:
                                    active[0] = [pending.pop(0), 0, nit]
                                if active[0] is not None and nit >= active[0][2]:
                                    stages_, si_, _due = active[0]
                                    fn_, delay_ = stages_[si_]
                                    fn_()
                                    if si_ + 1 < len(stages_):
                                        active[0] = [stages_, si_ + 1, nit + delay_]
                                    else:
                                        active[0] = None
                                st, sp_ = (kb == 0), (kb == NB - 1)
                                K.mm(pnum[0:64, :], Va[:, kb, kv * 64:(kv + 1) * 64], pt[:, 0, :], start=st, stop=sp_)
                                K.mm(pnum[64:128, :], Va[:, kb, kv * 64:(kv + 1) * 64], pt[:, 1, :], start=st, stop=sp_, tp=(0, 64))
                                K.mm(pden[0:64, :], ones[:, 0:64], pt[:, 0, :], start=st, stop=sp_)
                                K.mm(pden[64:128, :], ones[:, 0:64], pt[:, 1, :], start=st, stop=sp_, tp=(0, 64))
                            K.cp("dve", dcp, pden)
                            K.cp("dve", ncp, pnum)
                            K.recip(dcp, dcp)
                            K.tt("dve", ncp, ncp, dcp, ALU.mult)
                            K.tt("pool", mixA[:, t, :], ncp, sga_c[:, t, :], ALU.mult)
                        while active[0] is not None or pending:
                            if active[0] is None:
                                active[0] = [pending.pop(0), 0, 0]
                            stages_, si_, _due = active[0]
                            stages_[si_][0]()
                            active[0] = [stages_, si_ + 1, 0] if si_ + 1 < len(stages_) else None
                        for b in range(4):
                            bs = slice(b * 128, (b + 1) * 128)
                            py = take_buf()
                            for half in range(2):
                                for f in range(8):
                                    src = mixA[:, f, bs] if f < 4 else mixBl[:, f - 4, bs]
                                    K.mm(py[:, half, :], src, wobf[:, f, half * 512:(half + 1) * 512], start=(f == 0), stop=(f == 7))
                            xo = x1b[b % 2]
                            K.tt("dve", xo, py.re("p a c -> p (a c)"), xch[:, b, :], ALU.add)
                            r0 = C * CH + b * 128
                            K.dma("pool", x1_d[r0:r0 + 128, :], xo, x1b_slot[b % 2])

            tap("x1", x1_d, [NT, 1024], F32)
            checkpoint("LA")
            with Scope(K) as L1:
                KcT = sb("KcT", [128, 2, (NB + 2) * 128], BF16, L1)
                Vc = sb("Vc", [128, NB + 2, 128], BF16, L1)
                K.memset("pool", KcT[:, :, 0:128], 0.0)
                K.memset("pool", KcT[:, :, (NB + 1) * 128:(NB + 2) * 128], 0.0)
                K.memset("pool", Vc[:, 0, :], 0.0)
                K.memset("pool", Vc[:, NB + 1, :], 0.0)
                tap("EBT", EBT, [128, 16, 3, 128], BF16)
                checkpoint("EBT")
                load_w(wbf, wG1_d, 384, 1)
                with Scope(K) as PG1:
                    pT = ps("pT", [128, 8, 128], BF16, PG1)
                    pa = ps("pa", [128, CH], F32, PG1)
                    pva = ps("pva", [128, 512], F32, PG1)[:, 0:128]
                    mxg1 = alloc_mx(PG1)
                    for C in range(NCH):
                        make_xnT(mxg1, x1_d, C, pT)
                        store_xnT(xnT1_d, C, C % 2)
                        for t in range(2):
                            proj(pa, t * 128)
                            K.cp("act", KcT[:, t, (C * 4 + 1) * 128:(C * 4 + 5) * 128], pa)
                        for b in range(4):
                            for kc in range(8):
                                K.mm(pva, cur["xnT"][:, kc, b * 128:(b + 1) * 128], wbf[:, kc, 256:384], start=(kc == 0), stop=(kc == 7))
                            K.cp("dve", Vc[:, C * 4 + b + 1, :], pva)

                tap("KcT", KcT, [128, 2, (NB + 2) * 128], BF16)
                tap("Vc", Vc, [128, NB + 2, 128], BF16)
                checkpoint("G1")
                load_w(wbf, wL1_d, 2048, 1)
                load_w(wobf, woc_d, 1024, None)
                with Scope(K) as PL1:
                    pbig = ps("pbig1", [128, 8, CH], F32, PL1)
                    pw = [subview(pbig, pbig.ap[:, 2 * i:2 * i + 2, :]) for i in range(2)]
                    pnum = subview(pbig, pbig.ap[:, 6, :])
                    pden = subview(pbig, pbig.ap[:, 7, :])
                    bk = [subview(pbig, pbig.ap[:, i, :]) for i in range(6)]
                    qcT = sb("qcT", [128, 8, CH], BF16, PL1)
                    sgc = sb("sgc", [128, 8, CH], BF16, PL1)
                    mixC = sb("mixC", [128, 8, CH], BF16, PL1)
                    pws = [sb("pws%d" % i, [128, 2, 3, 128], BF16, PL1) for i in range(2)]
                    pw2 = [sb("pw2%d" % i, [128, 2, 3, 128], BF16, PL1) for i in range(2)]
                    rs = sb("rs1", [128, CH], F32, PL1)
                    lnr = sb("lnr", [128, CH], F32, PL1)
                    t1 = sb("t11", [128, CH], F32, PL1)
                    x2 = [sb("x2%d" % i, [128, 1024], F32, PL1) for i in range(2)]
                    yo = [sb("yo%d" % i, [128, 1024], F32, PL1) for i in range(2)]
                    yo_slot = [K.slot() for _ in range(2)]
                    ss2 = sb("ss2", [128, 2], F32, PL1)
                    ln2 = sb("ln2", [128, 2], F32, PL1)
                    r2 = sb("r2", [128, 2], F32, PL1)
                    it = 0
                    mxl = alloc_mx(PL1, full=False)
                    xch = mxl.xch
                    junk = sb("junk1", [128, 1024], BF16, PL1)
                    fnbc = sb("fnbc", [128, 1024], F32, PL1)
                    fn_slot = K.slot()
                    K.dma("sp", fnbc, V(fn_d.ap.to_broadcast([128, 1024]), fn_d.res), fn_slot)
                    load_xnT(xnT1_d, 0)
                    for C in range(NCH):
                        use_xnT(C)
                        if C + 1 < NCH:
                            load_xnT(xnT1_d, C + 1)
                        load_x(mxl, x1_d, C)
                        handoff(pw, bk[0:4])
                        for t in range(8):
                            pa_ = bk[t % 6]
                            proj(pa_, t * 128)
                            K.cp("act" if t % 2 == 0 else "dve", qcT[:, t, :], pa_)
                        for t in range(8):
                            pa_ = bk[(t + 2) % 6]
                            proj(pa_, 1024 + t * 128)
                            K.act(sgc[:, t, :], pa_, AF.Silu)
                        handoff(bk[0:4], pw)
                        items = [(t, qi) for t in range(8) for qi in range(4)]

                        def wqk(t, qi, w):
                            kv = t // 4
                            i = C * 4 + qi
                            qs = slice(qi * 128, (qi + 1) * 128)
                            for o in range(3):
                                sl = 2 - o
                                ks = slice((i + o) * 128, (i + o + 1) * 128)
                                K.mm(w[:, 0, sl * 128:(sl + 1) * 128], KcT[0:64, kv, ks], qcT[0:64, t, qs])
                                K.mm(w[:, 1, sl * 128:(sl + 1) * 128], KcT[64:128, kv, ks], qcT[64:128, t, qs])

                        wqk(items[0][0], items[0][1], pw[it % 2])
                        for idx, (t, qi) in enumerate(items):
                            kv = t // 4
                            i = C * 4 + qi
                            qs = slice(qi * 128, (qi + 1) * 128)
                            w = pw[it % 2]
                            s1 = pws[it % 2]
                            s2 = pw2[it % 2]
                            it += 1
                            if i in (0, NB // 2 - 1, NB // 2, NB - 1):
                                for o in range(3):
                                    sl = 2 - o
                                    K.act(s1[:, :, sl, :], w[:, :, sl * 128:(sl + 1) * 128], AF.Exp, bias=maskW[:, i * 3 + o:i * 3 + o + 1], scale=0.125)
                            else:
                                K.act(s1, w[:, :, 0:384].re("p h (o q) -> p h o q", o=3), AF.Exp, scale=0.125)
                            K.tt("dve", s2, s1, EBT[:, 2 * t:2 * t + 2, :, :], ALU.mult)
                            if idx + 1 < len(items):
                                wqk(items[idx + 1][0], items[idx + 1][1], pw[it % 2])
                            for o in range(3):
                                sl = 2 - o
                                st, sp_ = (o == 0), (o == 2)
                                vv = Vc[:, i + o, kv * 64:(kv + 1) * 64]
                                K.mm(pnum[0:64, qs], vv, s2[:, 0, sl, :], start=st, stop=sp_)
                                K.mm(pnum[64:128, qs], vv, s2[:, 1, sl, :], start=st, stop=sp_, tp=(0, 64))
                                K.mm(pden[0:64, qs], ones[:, 0:64], s2[:, 0, sl, :], start=st, stop=sp_)
                                K.mm(pden[64:128, qs], ones[:, 0:64], s2[:, 1, sl, :], start=st, stop=sp_, tp=(0, 64))
                            if qi == 3:
                                K.ts("dve", rs, pden, esk[:, t:t + 1], ALU.add)
                                K.cp("dve", t1, pnum)
                                K.act(lnr, rs, AF.Ln)
                                K.act(rs, lnr, AF.Exp, scale=-1.0)
                                K.tt("pool", t1, t1, rs, ALU.mult)
                                K.tt("pool", mixC[:, t, :], t1, sgc[:, t, :], ALU.mult)
                        for b in range(4):
                            bs = slice(b * 128, (b + 1) * 128)
                            py = pw[b % 2]
                            for half in range(2):
                                for f in range(8):
                                    K.mm(py[:, half, :], mixC[:, f, bs], wobf[:, f, half * 512:(half + 1) * 512], start=(f == 0), stop=(f == 7))
                            xo = x2[b % 2]
                            K.tt("dve", xo, py.re("p a c -> p (a c)"), xch[:, b, :], ALU.add)
                            K.act(junk, xo, AF.Square, accum=ss2[:, b % 2:b % 2 + 1])
                            K.act(ln2[:, b % 2:b % 2 + 1], ss2[:, b % 2:b % 2 + 1], AF.Ln, bias=epsb[:, 0:1], scale=1.0 / 1024.0)
                            K.act(r2[:, b % 2:b % 2 + 1], ln2[:, b % 2:b % 2 + 1], AF.Exp, scale=-0.5)
                            yb = yo[b % 2]
                            K.ts("dve", yb, xo, r2[:, b % 2:b % 2 + 1], ALU.mult)
                            K.tt("pool", yb, yb, fnbc, ALU.mult)
                            r0 = C * CH + b * 128
                            K.dma("pool", y_d[r0:r0 + 128, :], yb, yo_slot[b % 2])
    except StopBuild:
        pass
    for s_ in K.slots:
        if s_.cnt:
            nc.gpsimd.wait_ge(s_.sem, s_.cnt)
    return nc, K


def _t5_bucket(rel):
    half = 16
    max_exact = 8
    ret = (rel > 0).astype(np.int32) * half
    dist = np.abs(rel)
    large = max_exact + (np.log(np.maximum(dist, 1) / max_exact) / np.log(128 / max_exact) * (half - max_exact)).astype(np.int32)
    large = np.minimum(large, half - 1)
    return ret + np.where(dist < max_exact, dist, large)


def _static_tables():
    f32 = np.float32
    st = {}
    st["ident"] = np.eye(128, dtype=f32)
    ob = np.zeros((128, 128), f32)
    ob[:64, :64] = 1
    ob[64:, 64:] = 1
    st["onesblk"] = ob
    j = np.arange(128)[:, None]
    i = np.arange(128)[None, :]
    mmat = np.zeros((128, 4, 128), f32)
    mmat[:, 0, :] = np.maximum(i - j, 0)
    mmat[:, 1, :] = (i >= j)
    mmat[:, 2, :] = np.maximum(j - i, 0)
    mmat[:, 3, :] = (j > i)
    st["mmat"] = mmat
    c = np.arange(512) % 128
    iot = np.zeros((128, 4, 512), f32)
    iot[:, 0, :] = c + 1
    iot[:, 1, :] = 128 - c
    iot[:, 2, :] = 127 - c
    iot[:, 3, :] = c
    st["iot"] = iot
    m = np.arange(640)
    rel = 255 - m
    bk = _t5_bucket(rel)
    oh = np.zeros((32, 640), f32)
    oh[bk, m] = 1
    st["oh"] = oh
    st["inwin"] = np.broadcast_to((np.abs(rel) <= 128).astype(f32)[None, :], (16, 640)).copy()
    return st


def _core_tables(is_prompt):
    f32 = np.float32
    seqlen = 4096 if is_prompt else 2048
    t = np.arange(NT) % seqlen
    d = np.arange(128) % 64
    pair = d // 2
    sgn = np.where(d % 2 == 0, -1.0, 1.0)
    quarter = 16
    freqs = (np.float32(10000.0) ** (-np.arange(quarter, dtype=f32) / quarter)).astype(f32)
    row = (t // 64).astype(f32)
    col = (t % 64).astype(f32)
    ang = np.concatenate([row[:, None] * freqs, col[:, None] * freqs], axis=-1).astype(f32)
    angd = ang[:, pair].T.astype(np.float64)
    tabA = np.stack([np.cos(angd), np.sin(angd) * sgn[:, None]]).astype(f32)
    half = 32
    freqs_b = (np.float32(10000.0) ** (-np.arange(half, dtype=f32) / half)).astype(f32)
    angb = (t.astype(f32)[:, None] * freqs_b).astype(f32)
    angbd = angb[:, pair].T.astype(np.float64)
    tabB = np.stack([np.cos(angbd), np.sin(angbd) * sgn[:, None]]).astype(f32)
    seq_of_blk = (np.arange(NB) * 128) // seqlen
    maskA = np.zeros((NCH, NB), f32)
    for C in range(NCH):
        sq = (C * CH) // seqlen
        maskA[C, :] = np.where(seq_of_blk == sq, 0.0, NEG)
    maskA = np.broadcast_to(maskA.reshape(1, -1), (128, NCH * NB)).copy()
    maskW = np.zeros((NB, 3), f32)
    for i in range(NB):
        for o in range(3):
            jb = i + o - 1
            if jb < 0 or jb >= NB or seq_of_blk[jb] != seq_of_blk[i]:
                maskW[i, o] = NEG
    maskW = np.broadcast_to(maskW.reshape(1, -1), (128, NB * 3)).copy()
    cps = seqlen // 128
    rf = np.array([0.0 if (n % cps == 0) else 1.0 for n in range(NB)], f32)
    rb = np.array([0.0 if (n % cps == cps - 1) else 1.0 for n in range(NB)], f32)
    rfb = np.broadcast_to(np.concatenate([rf, rb])[None, :], (128, 64)).copy()
    return {"tabA": tabA, "tabB": tabB, "maskA": maskA, "maskW": maskW, "rfb": rfb}


def _swap(cols):
    cols = np.asarray(cols)
    return cols ^ 1


def _prep_common(norm_g, w_in_ab, qk_norm_a, ret_decay, w_out_ab, w_in_c, sink_c, w_out_c, rel_bias, final_norm):
    f32 = np.float32
    W = np.asarray(w_in_ab[0], f32)
    qa = np.arange(0, 512)
    ka = np.arange(512, 640)
    va = np.arange(640, 768)
    ga = np.arange(768, 1280)
    qb = np.arange(1280, 1536)
    kb = np.arange(1536, 1792)
    vb = np.arange(1792, 2304)
    gb = np.arange(2304, 2816)
    kadup = np.concatenate([ka[0:64], ka[0:64], ka[64:128], ka[64:128]])
    cm = {}
    cm["wG"] = np.ascontiguousarray(W[:, np.concatenate([kadup, _swap(kadup), kb, _swap(kb), va, vb])])
    cm["wLB"] = np.ascontiguousarray(W[:, np.concatenate([qb, _swap(qb), kb, _swap(kb), gb, vb])])
    cm["wLA"] = np.ascontiguousarray(W[:, np.concatenate([qa, _swap(qa), ga])])
    cm["woab"] = np.ascontiguousarray(np.asarray(w_out_ab[0], f32))
    Wc = np.asarray(w_in_c[0], f32)
    kc = np.arange(1024, 1152)
    kcdup = np.concatenate([kc[0:64], kc[0:64], kc[64:128], kc[64:128]])
    cm["wG1"] = np.ascontiguousarray(Wc[:, np.concatenate([kcdup, np.arange(1152, 1280)])])
    cm["wL1"] = np.ascontiguousarray(Wc[:, np.concatenate([np.arange(0, 1024), np.arange(1280, 2304)])])
    cm["woc"] = np.ascontiguousarray(np.asarray(w_out_c[0], f32))
    ng = np.asarray(norm_g, f32)
    cm["gcol"] = np.ascontiguousarray(ng.reshape(2, 8, 128).transpose(2, 0, 1).reshape(128, 16))
    cm["fn"] = np.asarray(final_norm, f32).reshape(1, 1024).copy()
    g = np.asarray(qk_norm_a[0], f32)
    d = np.arange(128) % 64
    cm["gqk"] = np.stack([g[0][d], g[0][d ^ 1], g[1][d], g[1][d ^ 1]], axis=1).astype(f32).copy()
    rd = np.asarray(ret_decay[0], f32)
    hp = (np.arange(128) // 64)
    rdec = np.zeros((128, 12), f32)
    for p in range(2):
        rdec[:, p] = rd[0][2 * p + hp]
        rdec[:, 2 + p] = rd[1][2 * p + hp]
    for h in range(4):
        rdec[:, 4 + h] = rd[0][h]
        rdec[:, 8 + h] = rd[1][h]
    cm["rdec"] = rdec
    sk = np.asarray(sink_c[0], f32)
    sinkl = np.zeros((128, 8), f32)
    for t in range(8):
        sinkl[:, t] = sk[2 * t + hp]
    cm["sinkl"] = sinkl
    cm["relb"] = np.ascontiguousarray(np.asarray(rel_bias, f32))
    cm.update(_static_tables())
    return cm


_CACHE = {}


def kernel(x_prompt, x_sample, norm_g, w_in_ab, qk_norm_a, ret_decay, w_out_ab, w_in_c, sink_c, w_out_c, rel_bias, final_norm):
    xp = np.asarray(x_prompt, np.float32)
    xs = np.asarray(x_sample, np.float32)
    cm = _prep_common(norm_g, w_in_ab, qk_norm_a, ret_decay, w_out_ab, w_in_c, sink_c, w_out_c, rel_bias, final_norm)
    tp = _core_tables(True)
    tsm = _core_tables(False)
    in_maps = []
    for c in range(8):
        m = dict(cm)
        if c < 4:
            m["x"] = np.ascontiguousarray(xp[c])
            m.update(tp)
        else:
            m["x"] = np.ascontiguousarray(xs[2 * (c - 4):2 * (c - 4) + 2].reshape(NT, 1024))
            m.update(tsm)
        in_maps.append(m)
    if "nc" not in _CACHE:
        _CACHE["nc"] = build_program()[0]
    nc = _CACHE["nc"]
    res = run_bass_kernel_spmd(nc, in_maps, core_ids=list(range(8)))
    outs = [np.asarray(r["y"], np.float32) for r in res.results]
    y_prompt = np.stack(outs[0:4], axis=0)
    y_sample = np.stack(outs[4:8], axis=0).reshape(8, 2048, 1024)
    return (y_prompt, y_sample)
```

```python
import numpy as np
import concourse.bass as bass
import concourse.mybir as mybir
from concourse.bass_utils import run_bass_kernel_spmd

F32 = mybir.dt.float32
BF16 = mybir.dt.bfloat16
AF = mybir.ActivationFunctionType
ALU = mybir.AluOpType

NT = 4096
NB = 32
CH = 512
NCH = 8
EPS = 1e-6
NEG = -30000.0


class Prod:
    def __init__(self, sem, inc):
        self.sem = sem
        self.inc = inc
        self.cnt = 0


class Res:
    def __init__(self):
        self.w = {}
        self.r = {}
        self.excl = False


class V:
    def __init__(self, ap, res=None):
        self.ap = ap
        self.res = res if res is not None else Res()

    def __getitem__(self, k):
        return V(self.ap[k], self.res)

    def re(self, pat, **kw):
        return V(self.ap.rearrange(pat, **kw), self.res)

    def bc(self, shape):
        return V(self.ap.to_broadcast(shape), self.res)


class Ker:
    def __init__(self, nc):
        self.nc = nc
        self.eng = {"pe": nc.tensor, "act": nc.scalar, "dve": nc.vector, "pool": nc.gpsimd, "sp": nc.sync}
        self.prod = {}
        for n in ("pe", "act", "dve", "pool"):
            self.prod[n] = Prod(nc.alloc_semaphore("s_" + n), 1)
        self.seen = {n: {} for n in self.eng}
        self.nslot = 0
        self.ninstr = 0

    def slot(self):
        self.nslot += 1
        p = Prod(self.nc.alloc_semaphore("d%d" % self.nslot), 16)
        if hasattr(self, "slots"):
            self.slots.append(p)
        return p

    def _wait(self, en, reads, writes):
        deps = {}
        for v in reads:
            for p, i in v.res.w.items():
                deps[p] = max(deps.get(p, 0), i)
        for v in writes:
            for p, i in v.res.w.items():
                deps[p] = max(deps.get(p, 0), i)
            for p, i in v.res.r.items():
                deps[p] = max(deps.get(p, 0), i)
        e = self.eng[en]
        seen = self.seen[en]
        own = self.prod.get(en)
        for p, i in deps.items():
            if p is own and en == "pe":
                continue
            if seen.get(p, 0) >= i:
                continue
            e.wait_ge(p.sem, i)
            seen[p] = i

    def op(self, en, fn, reads, writes):
        writes = list(writes) + [r for r in reads if r.res.excl]
        self._wait(en, reads, writes)
        ins = fn(self.eng[en])
        p = self.prod[en]
        p.cnt += 1
        ins.then_inc(p.sem, 1)
        for v in reads:
            v.res.r[p] = p.cnt
        for v in writes:
            v.res.w[p] = p.cnt
        self.ninstr += 1

    def dma(self, q, out, in_, slot):
        self._wait(q, [in_], [out])
        ins = self.eng[q].dma_start(out=out.ap, in_=in_.ap)
        slot.cnt += 16
        ins.then_inc(slot.sem, 16)
        in_.res.r[slot] = slot.cnt
        out.res.w[slot] = slot.cnt

    def mm(self, out, lhsT, rhs, start=True, stop=True, tp=None):
        kw = {}
        if tp is not None:
            kw["tile_position"] = tp
        self.op("pe", lambda e: e.matmul(out.ap, lhsT.ap, rhs.ap, start=start, stop=stop, **kw), [lhsT, rhs], [out])

    def tr(self, out, in_, ident):
        self.op("pe", lambda e: e.transpose(out.ap, in_.ap, ident.ap), [in_, ident], [out])

    def act(self, out, in_, func, bias=None, scale=1.0, accum=None):
        reads = [in_]
        kw = {}
        if bias is not None:
            if isinstance(bias, V):
                reads.append(bias)
                kw["bias"] = bias.ap
            else:
                kw["bias"] = bias
        if isinstance(scale, V):
            reads.append(scale)
            kw["scale"] = scale.ap
        else:
            kw["scale"] = scale
        writes = [out]
        if accum is not None:
            writes.append(accum)
            kw["accum_out"] = accum.ap
        self.op("act", lambda e: e.activation(out.ap, in_.ap, func, **kw), reads, writes)

    def tt(self, en, out, a, b, op):
        self.op(en, lambda e: e.tensor_tensor(out.ap, a.ap, b.ap, op), [a, b], [out])

    def stt(self, en, out, in0, scalar, in1, op0, op1):
        reads = [in0, in1]
        s = scalar
        if isinstance(scalar, V):
            reads.append(scalar)
            s = scalar.ap
        self.op(en, lambda e: e.scalar_tensor_tensor(out.ap, in0.ap, s, in1.ap, op0, op1), reads, [out])

    def ts(self, en, out, in0, s1, op0, s2=None, op1=None):
        reads = [in0]
        a1 = s1
        if isinstance(s1, V):
            reads.append(s1)
            a1 = s1.ap
        a2 = s2
        if isinstance(s2, V):
            reads.append(s2)
            a2 = s2.ap
        if op1 is None:
            self.op(en, lambda e: e.tensor_scalar(out.ap, in0.ap, a1, None, op0), reads, [out])
        else:
            self.op(en, lambda e: e.tensor_scalar(out.ap, in0.ap, a1, a2, op0, op1), reads, [out])

    def cp(self, en, out, in_):
        if en == "act":
            self.op("act", lambda e: e.copy(out.ap, in_.ap), [in_], [out])
        else:
            self.op(en, lambda e: e.tensor_copy(out.ap, in_.ap), [in_], [out])

    def amul(self, out, in_, m):
        self.op("act", lambda e: e.mul(out.ap, in_.ap, m.ap), [in_, m], [out])

    def recip(self, out, in_):
        self.op("dve", lambda e: e.reciprocal(out.ap, in_.ap), [in_], [out])

    def memset(self, en, out, val):
        self.op(en, lambda e: e.memset(out.ap, val), [], [out])


class StopBuild(Exception):
    pass


import contextlib


class Scope(contextlib.ExitStack):
    def __init__(self, K):
        super().__init__()
        self.K = K
        self.tiles = []

    def __exit__(self, *a):
        fr = self.K.freed
        for v in self.tiles:
            for d in (v.res.w, v.res.r):
                for p, i in d.items():
                    fr[p] = max(fr.get(p, 0), i)
        self.tiles = []
        return super().__exit__(*a)

    def close(self):
        self.__exit__(None, None, None)


def build_program(stop=None, taps=()):
    nc = bass.Bass("TRN2", target_bir_lowering=False)
    K = Ker(nc)
    K.slots = []
    K.freed = {}
    K.tapped = {}

    def checkpoint(name):
        if stop == name:
            raise StopBuild()

    def tap(name, v, shape, dt=F32):
        if name not in taps or name in K.tapped:
            return
        d = V(nc.dram_tensor("dbg_" + name, list(shape), dt, kind="ExternalOutput").ap())
        K.tapped[name] = d
        K.dma("sp", d, v, K.slot())

    def din(name, shape, dt=F32):
        return V(nc.dram_tensor(name, list(shape), dt, kind="ExternalInput").ap())

    x_d = din("x", [NT, 1024])
    wG_d = din("wG", [1024, 1664])
    wLB_d = din("wLB", [1024, 2048])
    wLA_d = din("wLA", [1024, 1536])
    woab_d = din("woab", [1024, 1024])
    wG1_d = din("wG1", [1024, 384])
    wL1_d = din("wL1", [1024, 2048])
    woc_d = din("woc", [1024, 1024])
    gcol_d = din("gcol", [128, 16])
    fn_d = din("fn", [1, 1024])
    gqk_d = din("gqk", [128, 4])
    rdec_d = din("rdec", [128, 12])
    sink_d = din("sinkl", [128, 8])
    relb_d = din("relb", [32, 16])
    ident_d = din("ident", [128, 128])
    onesblk_d = din("onesblk", [128, 128])
    mm_d = din("mmat", [128, 4, 128])
    iot_d = din("iot", [128, 4, 512])
    oh_d = din("oh", [32, 640])
    inwin_d = din("inwin", [16, 640])
    tabA_d = din("tabA", [2, 128, NT])
    tabB_d = din("tabB", [2, 128, NT])
    maskA_d = din("maskA", [128, 256])
    maskW_d = din("maskW", [128, 96])
    rfb_d = din("rfb", [128, 64])
    y_d = V(nc.dram_tensor("y", [NT, 1024], F32, kind="ExternalOutput").ap())
    x1_d = V(nc.dram_tensor("x1s", [NT, 1024], F32, kind="Internal").ap())
    mixb_d = V(nc.dram_tensor("mixbs", [4, 128, NT], BF16, kind="Internal").ap())
    vec_d = V(nc.dram_tensor("vecs", [16, 640], BF16, kind="Internal").ap())
    xnT0_d = V(nc.dram_tensor("xnT0s", [NCH, 128, 8, CH], BF16, kind="Internal").ap())
    xnT1_d = V(nc.dram_tensor("xnT1s", [NCH, 128, 8, CH], BF16, kind="Internal").ap())

    es = Scope(K)
    uid = [0]

    def sb(name, shape, dt=F32, stack=None):
        uid[0] += 1
        st_ = stack if stack is not None else es
        t = st_.enter_context(nc.sbuf_tensor("sb%d_%s" % (uid[0], name), list(shape), dt))
        v = V(t[:])
        v.res.w = dict(K.freed)
        st_.tiles.append(v)
        return v

    def ps(name, shape, dt=F32, stack=None):
        uid[0] += 1
        st_ = stack if stack is not None else es
        t = st_.enter_context(nc.psum_tensor("ps%d_%s" % (uid[0], name), list(shape), dt))
        v = V(t[:])
        v.res.excl = True
        v.res.w = dict(K.freed)
        st_.tiles.append(v)
        return v

    try:
        with es:
            cslot = K.slot()
            consts = []

            def cload(name, src, shape, dt=F32, q="sp"):
                t = sb(name, shape, dt)
                K.dma(q, t, src, cslot)
                consts.append(t)
                return t

            gcol = cload("gcol", gcol_d, [128, 16])
            gqk = cload("gqk", gqk_d, [128, 4])
            rdec = cload("rdec", rdec_d, [128, 12])
            sinkl = cload("sinkl", sink_d, [128, 8])
            maskA = cload("maskA", maskA_d, [128, 256])
            maskW = cload("maskW", maskW_d, [128, 96])
            rfb = cload("rfb", rfb_d, [128, 64])
            ident32 = cload("ident32", ident_d, [128, 128])
            onesblk32 = cload("onesblk32", onesblk_d, [128, 128])
            for c in consts:
                c.res.w[cslot] = cslot.cnt
            ident = sb("ident", [128, 128], BF16)
            onesblk = sb("onesblk", [128, 128], BF16)
            ones = sb("ones", [128, 128], BF16)
            epsb = sb("epsb", [128, 1])
            K.cp("dve", ident, ident32)
            K.cp("dve", onesblk, onesblk32)
            K.memset("dve", ones, 1.0)
            K.memset("dve", epsb, EPS)

            checkpoint("c0")
            wbf = sb("wbf", [128, 8, 2048], BF16)
            wobf = sb("wobf", [128, 8, 1024], BF16)
            wst = [sb("wst%d" % i, [128, 1024]) for i in range(2)]
            wst_slot = [K.slot() for _ in range(2)]
            xnTs = [sb("xnT%d" % i, [128, 8, CH], BF16) for i in range(2)]
            xnT_slot = [K.slot() for _ in range(2)]
            cur = {"xnT": xnTs[0]}
            xch_slot = [K.slot() for _ in range(4)]
            xst_slot = [K.slot() for _ in range(2)]
            EBT = sb("EBT", [128, 16, 3, 128], BF16)
            esk = sb("esk", [128, 8], F32)
            K.act(esk, sinkl, AF.Exp)
            with Scope(K) as S1:
                relb = sb("relb", [32, 16], F32, S1)
                oh = sb("oh", [32, 640], F32, S1)
                inw = sb("inw", [16, 640], F32, S1)
                e_slot = K.slot()
                K.dma("sp", relb, relb_d, e_slot)
                K.dma("sp", oh, oh_d, e_slot)
                K.dma("sp", inw, inwin_d, e_slot)
                for t_ in (relb, oh, inw):
                    t_.res.w[e_slot] = e_slot.cnt
                pv = ps("pv", [16, 1024], F32, S1)[:, 0:640]
                vec = sb("vec", [16, 640], F32, S1)
                vecb = sb("vecb", [16, 640], BF16, S1)
                K.mm(pv[:, 0:512], relb, oh[:, 0:512])
                K.mm(pv[:, 512:640], relb, oh[:, 512:640])
                K.act(vec, pv, AF.Exp)
                K.tt("dve", vecb, vec, inw, ALU.mult)
                v_slot = K.slot()
                K.dma("sp", vec_d, vecb, v_slot)
                g_slot = [K.slot() for _ in range(2)]
                for k in range(128):
                    qn = "sp" if (k % 2 == 0) else "pool"
                    K.dma(qn, EBT[k:k + 1, :, :, :], V(vec_d.ap[:, 127 - k:127 - k + 384].rearrange("(a h) (o q) -> a h o q", a=1, o=3), vec_d.res), g_slot[k % 2])
            wcount = [0]

            def handoff(srcs, dsts):
                for d_ in dsts:
                    for s_ in srcs:
                        for dd in (s_.res.w, s_.res.r):
                            for p_, i_ in dd.items():
                                d_.res.w[p_] = max(d_.res.w.get(p_, 0), i_)

            def subview(parent, ap):
                v = V(ap)
                v.res.excl = parent.res.excl
                v.res.w = dict(parent.res.w)
                return v

            class MX:
                pass

            def alloc_mx(scope, full=True):
                m = MX()
                m.xch = sb("xch", [128, 4, 1024], F32, scope)
                if full:
                    m.xn = [sb("xn%d" % i, [128, 1024], BF16, scope) for i in range(2)]
                    m.junk = sb("junk", [128, 1024], BF16, scope)
                    m.ss = sb("ss", [128, 4], F32, scope)
                    m.lnv4 = sb("lnv4", [128, 4], F32, scope)
                    m.rstd4 = sb("rstd4", [128, 4], F32, scope)
                return m

            def load_x(m, src_d, C):
                for b in range(4):
                    r0 = C * CH + b * 128
                    K.dma("sp", m.xch[:, b, :], src_d[r0:r0 + 128, :], xch_slot[b])

            def store_xnT(dst_d, C, slot_i):
                K.dma("pool", dst_d[C], cur["xnT"], xst_slot[slot_i])

            def load_xnT(src_d, C):
                i = C % 2
                K.dma("sp", xnTs[i], src_d[C], xnT_slot[i])

            def use_xnT(C):
                cur["xnT"] = xnTs[C % 2]

            def load_w(dst, src_d, ncols, layer_g):
                for kc in range(8):
                    for c0 in range(0, ncols, 1024):
                        c1 = min(ncols, c0 + 1024)
                        i = wcount[0] % 2
                        wcount[0] += 1
                        K.dma("sp", wst[i][:, 0:c1 - c0], src_d[kc * 128:(kc + 1) * 128, c0:c1], wst_slot[i])
                        en = "act" if (wcount[0] % 2 == 0) else "dve"
                        if layer_g is None:
                            K.cp(en, dst[:, kc, c0:c1], wst[i][:, 0:c1 - c0])
                        elif en == "act":
                            K.amul(dst[:, kc, c0:c1], wst[i][:, 0:c1 - c0], gcol[:, layer_g * 8 + kc:layer_g * 8 + kc + 1])
                        else:
                            K.ts("dve", dst[:, kc, c0:c1], wst[i][:, 0:c1 - c0], gcol[:, layer_g * 8 + kc:layer_g * 8 + kc + 1], ALU.mult)

            def make_xnT(m, src_d, C, pT):
                use_xnT(C)
                xnT = cur["xnT"]
                load_x(m, src_d, C)
                for b in range(4):
                    K.act(m.junk, m.xch[:, b, :], AF.Square, accum=m.ss[:, b:b + 1])
                K.act(m.lnv4, m.ss, AF.Ln, bias=epsb[:, 0:1], scale=1.0 / 1024.0)
                K.act(m.rstd4, m.lnv4, AF.Exp, scale=-0.5)
                for b in range(4):
                    xb = m.xn[b % 2]
                    K.ts("dve", xb, m.xch[:, b, :], m.rstd4[:, b:b + 1], ALU.mult)
                    for kc in range(8):
                        K.tr(pT[:, kc, :], xb[:, kc * 128:(kc + 1) * 128], ident)
                    K.cp("act", xnT[:, :, b * 128:(b + 1) * 128], pT)

            def proj(dst, c0):
                for kc in range(8):
                    K.mm(dst, wbf[:, kc, c0:c0 + 128], cur["xnT"][:, kc, :], start=(kc == 0), stop=(kc == 7))

            def rsq_bcast(dst, src_ps, nfeat, sq, psn, lnv, lhs_ones):
                K.act(sq, src_ps, AF.Square)
                K.mm(psn, lhs_ones, sq)
                K.act(lnv, psn, AF.Ln, bias=epsb[:, 0:1], scale=1.0 / nfeat)
                K.act(dst, lnv, AF.Exp, scale=-0.5)

            with Scope(K) as L0:
                LR = Scope(K)
                KaT = sb("KaT", [128, 2, NT], BF16, L0)
                Va = sb("Va", [128, NB, 128], BF16, L0)
                tabc = sb("tabc", [128, 2, CH], F32, L0)
                tab_slot = K.slot()
                tabd_slot = K.slot()
                sq = sb("sq", [128, CH], BF16, L0)
                lnv = sb("lnv", [128, CH], F32, L0)
                rs = sb("rs", [128, CH], F32, L0)
                t1 = sb("t1", [128, CH], F32, L0)
                t2 = sb("t2", [128, CH], F32, L0)
                tabg = sb("tabg", [128, 2, CH], F32, L0)
                SbAll = sb("SbAll", [128, 2, NB, 128], BF16, LR)
                tabd = sb("tabd", [128, 2, CH], F32, LR)
                vbtm = sb("vbtm", [128, 4, 512], BF16, LR)
                lg = sb("lg", [128, 12], F32, LR)
                K.act(lg, rdec, AF.Exp)
                K.ts("dve", lg, lg, -1.0, ALU.mult)
                cd = sb("cd", [128, 4], F32, LR)
                K.act(cd, lg[:, 0:4], AF.Exp, scale=128.0)
                cdr = sb("cdr", [128, 4, NB], F32, LR)
                for j in range(4):
                    off = 0 if j < 2 else 32
                    K.ts("dve", cdr[:, j, :], rfb[:, off:off + 32], cd[:, j:j + 1], ALU.mult)
                checkpoint("c1")
                QF4 = sb("QF4", [128, 2, CH], F32, LR)
                QB4 = sb("QB4", [128, 2, CH], F32, LR)
                KF4 = sb("KF4", [128, 2, CH], F32, LR)
                KB4 = sb("KB4", [128, 2, CH], F32, LR)
                DT = sb("DT", [128, 4, 128], F32, LR)
                with Scope(K) as S0:
                    iot = sb("iot", [128, 4, CH], F32, S0)
                    K.dma("sp", iot, iot_d, tabd_slot)
                    for p in range(2):
                        K.act(QF4[:, p, :], iot[:, 0, :], AF.Exp, scale=lg[:, p:p + 1])
                        K.act(QB4[:, p, :], iot[:, 1, :], AF.Exp, scale=lg[:, 2 + p:3 + p])
                        K.act(KF4[:, p, :], iot[:, 2, :], AF.Exp, scale=lg[:, p:p + 1])
                        K.act(KB4[:, p, :], iot[:, 3, :], AF.Exp, scale=lg[:, 2 + p:3 + p])
                    K.ts("dve", KF4, KF4, 0.125, ALU.mult)
                    K.ts("dve", KB4, KB4, 0.125, ALU.mult)
                    mmat = sb("mmat", [128, 4, 128], F32, S0)
                    K.dma("sp", mmat, mm_d, tab_slot)
                    d1 = sb("d1", [128, 128], F32, S0)
                    d2 = sb("d2", [128, 128], F32, S0)
                    for h in range(4):
                        checkpoint("d0")
                        K.act(d1, mmat[:, 0, :], AF.Exp, scale=lg[:, 4 + h:5 + h])
                        checkpoint("d1")
                        K.tt("dve", d1, d1, mmat[:, 1, :], ALU.mult)
                        checkpoint("d2")
                        K.act(d2, mmat[:, 2, :], AF.Exp, scale=lg[:, 8 + h:9 + h])
                        K.tt("dve", d2, d2, mmat[:, 3, :], ALU.mult)
                        K.tt("dve", d1, d1, d2, ALU.add)
                        checkpoint("d3")
                        K.ts("dve", DT[:, h, :], d1, 0.125, ALU.mult)
                        checkpoint("d4")

                def load_tab(dst, slot, src_d, C):
                    K.dma("sp", dst, V(src_d.ap[:, :, C * CH:(C + 1) * CH].rearrange("t p c -> p t c"), src_d.res), slot)

                def rope(psa, psb, tab, out32, ga=None, gb=None):
                    if ga is None:
                        K.tt("dve", t1, psa, tab[:, 0, :], ALU.mult)
                        K.tt("dve", t2, psb, tab[:, 1, :], ALU.mult)
                    else:
                        K.amul(tabg[:, 0, :], tab[:, 0, :], ga)
                        K.amul(tabg[:, 1, :], tab[:, 1, :], gb)
                        K.tt("dve", t1, psa, tabg[:, 0, :], ALU.mult)
                        K.tt("dve", t2, psb, tabg[:, 1, :], ALU.mult)
                    K.tt("pool", out32, t1, t2, ALU.add)

                checkpoint("setup0")
                load_w(wbf, wG_d, 1664, 0)
                with Scope(K) as PG:
                    pT = ps("pT", [128, 8, 128], BF16, PG)
                    pk = pT.re("p (c t) q -> p c t q", t=2)
                    pbig = ps("pbigG", [128, 7, CH], F32, PG)
                    bk = [subview(pbig, pbig.ap[:, i, :]) for i in range(7)]
                    pn = bk[4]
                    pkv = V(bk[6].ap[:, 0:256].rearrange("p (a b) -> p a b", a=2), bk[6].res)
                    kdbT = sb("kdbT", [128, 2, CH], BF16, PG)
                    kdbtm = sb("kdbtm", [128, 4, 2, 128], BF16, PG)
                    Rb = sb("Rb", [128, 2, 128], F32, PG)
                    mxg = alloc_mx(PG)
                    WS = [dict(sq=sq, lnv=lnv, rs=rs, t1=t1, t2=t2),
                          dict(sq=sb("wsq", [128, CH], BF16, PG), lnv=sb("wlnv", [128, CH], F32, PG),
                               rs=sb("wrs", [128, CH], F32, PG), t1=sb("wt1", [128, CH], F32, PG),
                               t2=sb("wt2", [128, CH], F32, PG))]
                    K.memset("dve", Rb, 0.0)
                    for C in range(NCH - 1, -1, -1):
                        make_xnT(mxg, x_d, C, pT)
                        store_xnT(xnT0_d, C, C % 2)
                        load_tab(tabc, tab_slot, tabA_d, C)
                        load_tab(tabd, tabd_slot, tabB_d, C)
                        K.amul(tabg[:, 0, :], tabc[:, 0, :], gqk[:, 2:3])
                        K.amul(tabg[:, 1, :], tabc[:, 1, :], gqk[:, 3:4])
                        for t in range(2):
                            w_ = WS[t % 2]
                            pa_, pb_ = bk[2 * t], bk[2 * t + 1]
                            proj(pa_, t * 128)
                            proj(pb_, 256 + t * 128)
                            rsq_bcast(w_["rs"], pa_, 64.0, w_["sq"], pn, w_["lnv"], onesblk)
                            K.tt("dve", w_["t1"], pa_, tabg[:, 0, :], ALU.mult)
                            K.tt("dve", w_["t2"], pb_, tabg[:, 1, :], ALU.mult)
                            K.tt("pool", w_["t1"], w_["t1"], w_["t2"], ALU.add)
                            K.tt("pool", KaT[:, t, C * CH:(C + 1) * CH], w_["t1"], w_["rs"], ALU.mult)
                        for t in range(2):
                            w_ = WS[t % 2]
                            pa_, pb_ = bk[2 * t], bk[2 * t + 1]
                            proj(pa_, 512 + t * 128)
                            proj(pb_, 768 + t * 128)
                            K.tt("dve", w_["t1"], pa_, tabd[:, 0, :], ALU.mult)
                            K.tt("dve", w_["t2"], pb_, tabd[:, 1, :], ALU.mult)
                            K.tt("pool", w_["t1"], w_["t1"], w_["t2"], ALU.add)
                            K.tt("pool", kdbT[:, t, :], w_["t1"], KB4[:, t, :], ALU.mult)
                            for cj in range(4):
                                K.tr(pk[:, cj, t, :], kdbT[:, t, cj * 128:(cj + 1) * 128], ident)
                        K.cp("act", kdbtm, pk)
                        for b in range(4):
                            pva = (bk[4] if b % 2 == 0 else bk[2])[:, 0:128]
                            pvb = bk[5] if b % 2 == 0 else bk[3]
                            for kc in range(8):
                                K.mm(pva, cur["xnT"][:, kc, b * 128:(b + 1) * 128], wbf[:, kc, 1024:1152], start=(kc == 0), stop=(kc == 7))
                            for kc in range(8):
                                K.mm(pvb, cur["xnT"][:, kc, b * 128:(b + 1) * 128], wbf[:, kc, 1152:1664], start=(kc == 0), stop=(kc == 7))
                            K.cp("act", Va[:, C * 4 + b, :], pva)
                            K.cp("dve", vbtm[:, b, :], pvb)
                        for cj in range(3, -1, -1):
                            n = C * 4 + cj
                            for p in range(2):
                                K.mm(pkv[0:64, p, :], kdbtm[:, cj, p, 0:64], vbtm[:, cj, (2 * p) * 128:(2 * p + 1) * 128])
                                K.mm(pkv[64:128, p, :], kdbtm[:, cj, p, 64:128], vbtm[:, cj, (2 * p + 1) * 128:(2 * p + 2) * 128], tp=(0, 64))
                            K.ts("dve", SbAll[:, :, n, :], Rb, rfb[:, 32 + n:33 + n], ALU.mult)
                            for p in range(2):
                                K.ts("dve", Rb[:, p, :], Rb[:, p, :], cdr[:, 2 + p, n:n + 1], ALU.mult)
                                K.tt("dve", Rb[:, p, :], pkv[:, p, :], Rb[:, p, :], ALU.add)

                tap("KaT", KaT, [128, 2, NT], BF16)
                tap("Va", Va, [128, NB, 128], BF16)
                tap("SbAll", SbAll, [128, 2, NB, 128], BF16)
                checkpoint("G")
                load_w(wbf, wLB_d, 2048, 0)
                with Scope(K) as PB:
                    pT = ps("pT", [128, 8, 128], BF16, PB)
                    pk = pT.re("p (c t) q -> p c t q", t=2)
                    pbig = ps("pbigB", [128, 7, CH], F32, PB)
                    bk = [subview(pbig, pbig.ap[:, i, :]) for i in range(7)]
                    pa, pb, pss = bk[0], bk[1], bk[2]
                    po = subview(pbig, pbig.ap[:, 3:7, :])
                    qrT = sb("qrT", [128, 2, CH], BF16, PB)
                    qdf = sb("qdf", [128, 2, CH], BF16, PB)
                    qdb = sb("qdb", [128, 2, CH], BF16, PB)
                    krT = sb("krT", [128, 2, CH], BF16, PB)
                    kdfT = sb("kdfT", [128, 2, CH], BF16, PB)
                    kdftm = sb("kdftm", [128, 4, 2, 128], BF16, PB)
                    sg = sb("sg", [128, 4, CH], BF16, PB)
                    ATs = [sb("AT%d" % i, [128, 4, 128], BF16, PB) for i in range(2)]
                    Sfs = [sb("Sf%d" % i, [128, 2, 128], BF16, PB) for i in range(2)]
                    Rf = sb("Rf", [128, 2, 128], F32, PB)
                    mixBc = [sb("mixBc%d" % i, [128, 4, CH], BF16, PB) for i in range(1)]
                    mixB_slot = [K.slot() for _ in range(1)]
                    WS = [dict(sq=sq, lnv=lnv, rs=rs, t1=t1, t2=t2),
                          dict(sq=sb("wsq", [128, CH], BF16, PB), lnv=sb("wlnv", [128, CH], F32, PB),
                               rs=sb("wrs", [128, CH], F32, PB), t1=sb("wt1", [128, CH], F32, PB),
                               t2=sb("wt2", [128, CH], F32, PB))]
                    K.memset("dve", Rf, 0.0)
                    load_xnT(xnT0_d, 0)
                    pairs = [(bk[0], bk[1]), (bk[3], bk[4]), (bk[5], bk[6])]
                    for C in range(NCH):
                        use_xnT(C)
                        if C + 1 < NCH:
                            load_xnT(xnT0_d, C + 1)
                        load_tab(tabd, tabd_slot, tabB_d, C)
                        handoff([po], bk[3:7])
                        ip = 0
                        for t in range(2):
                            w_ = WS[ip % 2]
                            pa_, pb_ = pairs[ip % 3]
                            ip += 1
                            proj(pa_, t * 128)
                            proj(pb_, 256 + t * 128)
                            K.tt("dve", w_["t1"], pa_, tabd[:, 0, :], ALU.mult)
                            K.tt("dve", w_["t2"], pb_, tabd[:, 1, :], ALU.mult)
                            K.tt("pool", w_["t1"], w_["t1"], w_["t2"], ALU.add)
                            K.cp("act", qrT[:, t, :], w_["t1"])
                            K.tt("pool", qdf[:, t, :], w_["t1"], QF4[:, t, :], ALU.mult)
                            K.tt("pool", qdb[:, t, :], w_["t1"], QB4[:, t, :], ALU.mult)
                        for t in range(2):
                            w_ = WS[ip % 2]
                            pa_, pb_ = pairs[ip % 3]
                            ip += 1
                            proj(pa_, 512 + t * 128)
                            proj(pb_, 768 + t * 128)
                            K.tt("dve", w_["t1"], pa_, tabd[:, 0, :], ALU.mult)
                            K.tt("dve", w_["t2"], pb_, tabd[:, 1, :], ALU.mult)
                            K.tt("pool", w_["t1"], w_["t1"], w_["t2"], ALU.add)
                            K.cp("act", krT[:, t, :], w_["t1"])
                            K.tt("pool", kdfT[:, t, :], w_["t1"], KF4[:, t, :], ALU.mult)
                            for cj in range(4):
                                K.tr(pk[:, cj, t, :], kdfT[:, t, cj * 128:(cj + 1) * 128], ident)
                        K.cp("act", kdftm, pk)
                        for h in range(4):
                            pa_ = bk[3 + h]
                            proj(pa_, 1024 + h * 128)
                            K.act(sg[:, h, :], pa_, AF.Silu)
                        for b in range(4):
                            pv_ = bk[1 + b % 2]
                            for kc in range(8):
                                K.mm(pv_, cur["xnT"][:, kc, b * 128:(b + 1) * 128], wbf[:, kc, 1536:2048], start=(kc == 0), stop=(kc == 7))
                            K.cp("dve", vbtm[:, b, :], pv_)
                        handoff(bk[3:7], [po])
                        for cj in range(4):
                            n = C * 4 + cj
                            cs = slice(cj * 128, (cj + 1) * 128)
                            Sf = Sfs[cj % 2]
                            AT = ATs[cj % 2]
                            K.ts("dve", Sf, Rf, rfb[:, n:n + 1], ALU.mult)
                            for p in range(2):
                                K.mm(pa[0:64, p * 128:(p + 1) * 128], kdftm[:, cj, p, 0:64], vbtm[:, cj, (2 * p) * 128:(2 * p + 1) * 128])
                                K.mm(pa[64:128, p * 128:(p + 1) * 128], kdftm[:, cj, p, 64:128], vbtm[:, cj, (2 * p + 1) * 128:(2 * p + 2) * 128], tp=(0, 64))
                            for p in range(2):
                                K.ts("dve", Rf[:, p, :], Rf[:, p, :], cdr[:, p, n:n + 1], ALU.mult)
                                K.tt("dve", Rf[:, p, :], pa[:, p * 128:(p + 1) * 128], Rf[:, p, :], ALU.add)
                            for h in range(4):
                                t, r0 = h // 2, (h % 2) * 64
                                pdst = pss if (h % 2 == 0) else pb
                                K.mm(pdst[:, t * 128:(t + 1) * 128], krT[r0:r0 + 64, t, cs], qrT[r0:r0 + 64, t, cs])
                            ATv = AT.re("p (t hp) i -> p hp t i", hp=2)
                            DTv = DT.re("p (t hp) i -> p hp t i", hp=2)
                            K.tt("dve", ATv[:, 0, :, :], pss[:, 0:256].re("p (t i) -> p t i", t=2), DTv[:, 0, :, :], ALU.mult)
                            K.tt("dve", ATv[:, 1, :, :], pb[:, 0:256].re("p (t i) -> p t i", t=2), DTv[:, 1, :, :], ALU.mult)
                            for h in range(4):
                                t, r0 = h // 2, (h % 2) * 64
                                K.mm(po[:, h, cs], vbtm[:, cj, h * 128:(h + 1) * 128], AT[:, h, :], start=True, stop=False)
                                K.mm(po[:, h, cs], Sf[r0:r0 + 64, t, :], qdf[r0:r0 + 64, t, cs], start=False, stop=False)
                                K.mm(po[:, h, cs], SbAll[r0:r0 + 64, t, n, :], qdb[r0:r0 + 64, t, cs], start=False, stop=True)
                        mb = mixBc[0]
                        for h in range(4):
                            w_ = WS[h % 2]
                            psn_ = bk[h % 3]
                            rsq_bcast(w_["rs"], po[:, h, :], 128.0, w_["sq"], psn_, w_["lnv"], ones)
                            K.tt("dve", w_["t1"], po[:, h, :], w_["rs"], ALU.mult)
                            K.tt("pool", mb[:, h, :], w_["t1"], sg[:, h, :], ALU.mult)
                        K.dma("pool", V(mixb_d.ap[:, :, C * CH:(C + 1) * CH].rearrange("h p c -> p h c"), mixb_d.res), mb, mixB_slot[0])

                tap("mixb", mixb_d, [4, 128, NT], BF16)
                checkpoint("LB")
                LR.close()
                load_w(wbf, wLA_d, 1536, 0)
                load_w(wobf, woab_d, 1024, None)
                with Scope(K) as PA:
                    pbig = ps("pbig", [128, 8, CH], F32, PA)
                    psc = [subview(pbig, pbig.ap[:, 2 * i:2 * i + 2, :]) for i in range(3)]
                    pnum = subview(pbig, pbig.ap[:, 6, :])
                    pden = subview(pbig, pbig.ap[:, 7, :])
                    bk = [subview(pbig, pbig.ap[:, i, :]) for i in range(6)] + [pnum, pden]
                    WS = [dict(sq=sq, lnv=lnv, rs=rs, t1=t1, t2=t2),
                          dict(sq=sb("wsq", [128, CH], BF16, PA), lnv=sb("wlnv", [128, CH], F32, PA),
                               rs=sb("wrs", [128, CH], F32, PA), t1=sb("wt1", [128, CH], F32, PA),
                               t2=sb("wt2", [128, CH], F32, PA))]
                    qaT = sb("qaT", [128, 4, CH], BF16, PA)
                    sga = sb("sga", [128, 4, CH], BF16, PA)
                    mixAs = [sb("mixA%d" % i, [128, 4, CH], BF16, PA) for i in range(2)]
                    mixBls = [sb("mixBl%d" % i, [128, 4, CH], BF16, PA) for i in range(2)]
                    mixBl_slots = [K.slot() for _ in range(2)]
                    xblk = [sb("xblk%d" % i, [128, 1024], F32, PA) for i in range(2)]
                    xblk_slot = [K.slot() for _ in range(2)]
                    nxb = [0]
                    pTs = [sb("pTs%d" % i, [128, 2, CH], BF16, PA) for i in range(3)]
                    x1b = [sb("x1b%d" % i, [128, 1024], F32, PA) for i in range(2)]
                    dcp = sb("dcp", [128, CH], F32, PA)
                    ncp = sb("ncp", [128, CH], F32, PA)
                    x1b_slot = [K.slot() for _ in range(2)]
                    qaTs = [qaT, sb("qaT1", [128, 4, CH], BF16, PA)]
                    sgas = [sga, sb("sga1", [128, 4, CH], BF16, PA)]
                    tabcs = [tabc, sb("tabc1", [128, 2, CH], F32, PA)]
                    tabgs = [tabg, sb("tabg1", [128, 2, CH], F32, PA)]
                    tabsl = [tab_slot, K.slot()]
                    nbuf = [0]
                    npt = [0]

                    held = set()

                    def take_buf():
                        while True:
                            i_ = nbuf[0] % 3
                            nbuf[0] += 1
                            if i_ not in held:
                                return psc[i_]

                    def hold(b_):
                        held.add(psc.index(b_))

                    def release(b_):
                        held.discard(psc.index(b_))

                    def projx(dst, c0, xT):
                        for kc in range(8):
                            K.mm(dst, wbf[:, kc, c0:c0 + 128], xT[:, kc, :], start=(kc == 0), stop=(kc == 7))

                    def proj_items(Cn):
                        q_, g_ = qaTs[Cn % 2], sgas[Cn % 2]
                        tc_, tg_ = tabcs[Cn % 2], tabgs[Cn % 2]
                        xT = xnTs[Cn % 2]

                        def prep():
                            load_tab(tc_, tabsl[Cn % 2], tabA_d, Cn)
                            K.amul(tg_[:, 0, :], tc_[:, 0, :], gqk[:, 0:1])
                            K.amul(tg_[:, 1, :], tc_[:, 1, :], gqk[:, 1:2])

                        items = []
                        for t in range(4):
                            def mk(t=t):
                                st = {}
                                w_ = WS[t % 2]

                                def s1():
                                    st["buf"] = take_buf()
                                    hold(st["buf"])
                                    projx(st["buf"][:, 0, :], t * 128, xT)
                                    projx(st["buf"][:, 1, :], 512 + t * 128, xT)

                                def s2():
                                    pa_, pb_ = st["buf"][:, 0, :], st["buf"][:, 1, :]
                                    K.tt("dve", w_["t1"], pa_, tg_[:, 0, :], ALU.mult)
                                    K.tt("dve", w_["t2"], pb_, tg_[:, 1, :], ALU.mult)
                                    K.act(w_["sq"], pa_, AF.Square)

                                def s3():
                                    K.mm(st["buf"][:, 1, :], onesblk, w_["sq"])

                                def s4():
                                    K.act(w_["lnv"], st["buf"][:, 1, :], AF.Ln, bias=epsb[:, 0:1], scale=1.0 / 64.0)
                                    K.act(w_["rs"], w_["lnv"], AF.Exp, scale=-0.5)
                                    K.tt("pool", w_["t1"], w_["t1"], w_["t2"], ALU.add)
                                    K.tt("pool", q_[:, t, :], w_["t1"], w_["rs"], ALU.mult)
                                    release(st["buf"])
                                return [(s1, 3), (s2, 1), (s3, 2), (s4, 0)]
                            items.append(mk())
                        for t2_ in range(2):
                            def mk(t2_=t2_):
                                st = {}

                                def s1():
                                    st["buf"] = take_buf()
                                    hold(st["buf"])
                                    for j in range(2):
                                        projx(st["buf"][:, j, :], 1024 + (2 * t2_ + j) * 128, xT)

                                def s2():
                                    for j in range(2):
                                        K.act(WS[j]["t1"], st["buf"][:, j, :], AF.Tanh, scale=0.5)

                                def s3():
                                    for j in range(2):
                                        K.ts("dve", WS[j]["t1"], WS[j]["t1"], 0.5, ALU.mult, 0.5, ALU.add)
                                        K.tt("dve", g_[:, 2 * t2_ + j, :], st["buf"][:, j, :], WS[j]["t1"], ALU.mult)
                                    release(st["buf"])
                                return [(s1, 3), (s2, 1), (s3, 0)]
                            items.append(mk())
                        return prep, items

                    def outproj_items(Cc):
                        mA, mB = mixAs[Cc % 2], mixBls[Cc % 2]
                        items = []
                        for b in range(4):
                            def mk(b=b):
                                st = {}
                                bs = slice(b * 128, (b + 1) * 128)
                                r0 = Cc * CH + b * 128

                                def s1():
                                    i_ = nxb[0] % 2
                                    nxb[0] += 1
                                    st["i"] = i_
                                    K.dma("sp", xblk[i_], x_d[r0:r0 + 128, :], xblk_slot[i_])
                                    st["buf"] = take_buf()
                                    hold(st["buf"])
                                    py = st["buf"]
                                    for half in range(2):
                                        for f in range(8):
                                            src = mA[:, f, bs] if f < 4 else mB[:, f - 4, bs]
                                            K.mm(py[:, half, :], src, wobf[:, f, half * 512:(half + 1) * 512], start=(f == 0), stop=(f == 7))

                                def s2():
                                    xo = x1b[st["i"]]
                                    K.tt("dve", xo, st["buf"].re("p a c -> p (a c)"), xblk[st["i"]], ALU.add)
                                    K.dma("pool", x1_d[r0:r0 + 128, :], xo, x1b_slot[st["i"]])
                                    release(st["buf"])
                                return [(s1, 4), (s2, 0)]
                            items.append(mk())
                        return items

                    load_xnT(xnT0_d, 0)
                    prep0, items0 = proj_items(0)
                    prep0()
                    for it_ in items0:
                        for st_fn, _d in it_:
                            st_fn()
                    carry = []
                    for C in range(NCH):
                        use_xnT(C)
                        qaT_c, sga_c = qaTs[C % 2], sgas[C % 2]
                        mixA = mixAs[C % 2]
                        pending = list(carry)
                        carry = []
                        if C + 1 < NCH:
                            load_xnT(xnT0_d, C + 1)
                            prepn, pitems = proj_items(C + 1)
                            prepn()
                            pending = pending + pitems
                        K.dma("pool", mixBls[C % 2], V(mixb_d.ap[:, :, C * CH:(C + 1) * CH].rearrange("h p c -> p h c"), mixb_d.res), mixBl_slots[C % 2])
                        nit = 0
                        active = [None]
                        for t in range(4):
                            kv = t // 2
                            fifo = []

                            def qk(kb_):
                                sc_ = take_buf()
                                fifo.append(sc_)
                                ks = slice(kb_ * 128, (kb_ + 1) * 128)
                                K.mm(sc_[:, 0, :], KaT[0:64, kv, ks], qaT_c[0:64, t, :])
                                K.mm(sc_[:, 1, :], KaT[64:128, kv, ks], qaT_c[64:128, t, :])

                            qk(0)
                            qk(1)
                            for kb in range(NB):
                                sc = fifo.pop(0)
                                pt = pTs[npt[0] % 3]
                                npt[0] += 1
                                nit += 1
                                K.act(pt, sc, AF.Exp, bias=maskA[:, C * NB + kb:C * NB + kb + 1], scale=0.125)
                                if kb + 2 < NB:
                                    qk(kb + 2)
                                if active[0] is None and pending and nit % 12 == 3:
                                    active[0] = [pending.pop(0), 0, nit]
                                if active[0] is not None and nit >= active[0][2]:
                                    stages_, si_, _due = active[0]
                                    fn_, delay_ = stages_[si_]
                                    fn_()
                                    if si_ + 1 < len(stages_):
                                        active[0] = [stages_, si_ + 1, nit + delay_]
                                    else:
                                        active[0] = None
                                st, sp_ = (kb == 0), (kb == NB - 1)
                                K.mm(pnum[0:64, :], Va[:, kb, kv * 64:(kv + 1) * 64], pt[:, 0, :], start=st, stop=sp_)
                                K.mm(pnum[64:128, :], Va[:, kb, kv * 64:(kv + 1) * 64], pt[:, 1, :], start=st, stop=sp_, tp=(0, 64))
                                K.mm(pden[0:64, :], ones[:, 0:64], pt[:, 0, :], start=st, stop=sp_)
                                K.mm(pden[64:128, :], ones[:, 0:64], pt[:, 1, :], start=st, stop=sp_, tp=(0, 64))
                            K.cp("dve", dcp, pden)
                            K.cp("dve", ncp, pnum)
                            K.recip(dcp, dcp)
                            K.tt("dve", ncp, ncp, dcp, ALU.mult)
                            K.tt("pool", mixA[:, t, :], ncp, sga_c[:, t, :], ALU.mult)
                        while active[0] is not None or pending:
                            if active[0] is None:
                                active[0] = [pending.pop(0), 0, 0]
                            stages_, si_, _due = active[0]
                            stages_[si_][0]()
                            active[0] = [stages_, si_ + 1, 0] if si_ + 1 < len(stages_) else None
                        carry = outproj_items(C)
                        if C == NCH - 1:
                            for it_ in carry:
                                for st_fn, _d in it_:
                                    st_fn()
                            carry = []

            tap("x1", x1_d, [NT, 1024], F32)
            checkpoint("LA")
            with Scope(K) as L1:
                KcT = sb("KcT", [128, 2, (NB + 2) * 128], BF16, L1)
                Vc = sb("Vc", [128, NB + 2, 128], BF16, L1)
                K.memset("pool", KcT[:, :, 0:128], 0.0)
                K.memset("pool", KcT[:, :, (NB + 1) * 128:(NB + 2) * 128], 0.0)
                K.memset("pool", Vc[:, 0, :], 0.0)
                K.memset("pool", Vc[:, NB + 1, :], 0.0)
                tap("EBT", EBT, [128, 16, 3, 128], BF16)
                checkpoint("EBT")
                load_w(wbf, wG1_d, 384, 1)
                with Scope(K) as PG1:
                    pT = ps("pT", [128, 8, 128], BF16, PG1)
                    pa = ps("pa", [128, CH], F32, PG1)
                    pva = ps("pva", [128, 512], F32, PG1)[:, 0:128]
                    mxg1 = alloc_mx(PG1)
                    for C in range(NCH):
                        make_xnT(mxg1, x1_d, C, pT)
                        store_xnT(xnT1_d, C, C % 2)
                        for t in range(2):
                            proj(pa, t * 128)
                            K.cp("act", KcT[:, t, (C * 4 + 1) * 128:(C * 4 + 5) * 128], pa)
                        for b in range(4):
                            for kc in range(8):
                                K.mm(pva, cur["xnT"][:, kc, b * 128:(b + 1) * 128], wbf[:, kc, 256:384], start=(kc == 0), stop=(kc == 7))
                            K.cp("dve", Vc[:, C * 4 + b + 1, :], pva)

                tap("KcT", KcT, [128, 2, (NB + 2) * 128], BF16)
                tap("Vc", Vc, [128, NB + 2, 128], BF16)
                checkpoint("G1")
                load_w(wbf, wL1_d, 2048, 1)
                load_w(wobf, woc_d, 1024, None)
                with Scope(K) as PL1:
                    pbig = ps("pbig1", [128, 8, CH], F32, PL1)
                    pw = [subview(pbig, pbig.ap[:, 2 * i:2 * i + 2, :]) for i in range(2)]
                    pnum = subview(pbig, pbig.ap[:, 6, :])
                    pden = subview(pbig, pbig.ap[:, 7, :])
                    bk = [subview(pbig, pbig.ap[:, i, :]) for i in range(6)]
                    qcT = sb("qcT", [128, 8, CH], BF16, PL1)
                    sgc = sb("sgc", [128, 8, CH], BF16, PL1)
                    mixC = sb("mixC", [128, 8, CH], BF16, PL1)
                    pws = [sb("pws%d" % i, [128, 2, 3, 128], BF16, PL1) for i in range(2)]
                    pw2 = [sb("pw2%d" % i, [128, 2, 3, 128], BF16, PL1) for i in range(2)]
                    rs = sb("rs1", [128, CH], F32, PL1)
                    lnr = sb("lnr", [128, CH], F32, PL1)
                    t1 = sb("t11", [128, CH], F32, PL1)
                    x2 = [sb("x2%d" % i, [128, 1024], F32, PL1) for i in range(2)]
                    yo = [sb("yo%d" % i, [128, 1024], F32, PL1) for i in range(2)]
                    yo_slot = [K.slot() for _ in range(2)]
                    ss2 = sb("ss2", [128, 2], F32, PL1)
                    ln2 = sb("ln2", [128, 2], F32, PL1)
                    r2 = sb("r2", [128, 2], F32, PL1)
                    it = 0
                    mxl = alloc_mx(PL1, full=False)
                    xch = mxl.xch
                    junk = sb("junk1", [128, 1024], BF16, PL1)
                    fnbc = sb("fnbc", [128, 1024], F32, PL1)
                    fn_slot = K.slot()
                    K.dma("sp", fnbc, V(fn_d.ap.to_broadcast([128, 1024]), fn_d.res), fn_slot)
                    load_xnT(xnT1_d, 0)
                    for C in range(NCH):
                        use_xnT(C)
                        if C + 1 < NCH:
                            load_xnT(xnT1_d, C + 1)
                        load_x(mxl, x1_d, C)
                        handoff(pw, bk[0:4])
                        for t in range(8):
                            pa_ = bk[t % 6]
                            proj(pa_, t * 128)
                            K.cp("act" if t % 2 == 0 else "dve", qcT[:, t, :], pa_)
                        for t in range(8):
                            pa_ = bk[(t + 2) % 6]
                            proj(pa_, 1024 + t * 128)
                            K.act(sgc[:, t, :], pa_, AF.Silu)
                        handoff(bk[0:4], pw)
                        items = [(t, qi) for t in range(8) for qi in range(4)]

                        def wqk(t, qi, w):
                            kv = t // 4
                            i = C * 4 + qi
                            qs = slice(qi * 128, (qi + 1) * 128)
                            for o in range(3):
                                sl = 2 - o
                                ks = slice((i + o) * 128, (i + o + 1) * 128)
                                K.mm(w[:, 0, sl * 128:(sl + 1) * 128], KcT[0:64, kv, ks], qcT[0:64, t, qs])
                                K.mm(w[:, 1, sl * 128:(sl + 1) * 128], KcT[64:128, kv, ks], qcT[64:128, t, qs])

                        wqk(items[0][0], items[0][1], pw[it % 2])
                        for idx, (t, qi) in enumerate(items):
                            kv = t // 4
                            i = C * 4 + qi
                            qs = slice(qi * 128, (qi + 1) * 128)
                            w = pw[it % 2]
                            s1 = pws[it % 2]
                            s2 = pw2[it % 2]
                            it += 1
                            if i in (0, NB // 2 - 1, NB // 2, NB - 1):
                                for o in range(3):
                                    sl = 2 - o
                                    K.act(s1[:, :, sl, :], w[:, :, sl * 128:(sl + 1) * 128], AF.Exp, bias=maskW[:, i * 3 + o:i * 3 + o + 1], scale=0.125)
                            else:
                                K.act(s1, w[:, :, 0:384].re("p h (o q) -> p h o q", o=3), AF.Exp, scale=0.125)
                            K.tt("dve", s2, s1, EBT[:, 2 * t:2 * t + 2, :, :], ALU.mult)
                            if idx + 1 < len(items):
                                wqk(items[idx + 1][0], items[idx + 1][1], pw[it % 2])
                            for o in range(3):
                                sl = 2 - o
                                st, sp_ = (o == 0), (o == 2)
                                vv = Vc[:, i + o, kv * 64:(kv + 1) * 64]
                                K.mm(pnum[0:64, qs], vv, s2[:, 0, sl, :], start=st, stop=sp_)
                                K.mm(pnum[64:128, qs], vv, s2[:, 1, sl, :], start=st, stop=sp_, tp=(0, 64))
                                K.mm(pden[0:64, qs], ones[:, 0:64], s2[:, 0, sl, :], start=st, stop=sp_)
                                K.mm(pden[64:128, qs], ones[:, 0:64], s2[:, 1, sl, :], start=st, stop=sp_, tp=(0, 64))
                            if qi == 3:
                                K.ts("dve", rs, pden, esk[:, t:t + 1], ALU.add)
                                K.cp("dve", t1, pnum)
                                K.act(lnr, rs, AF.Ln)
                                K.act(rs, lnr, AF.Exp, scale=-1.0)
                                K.tt("pool", t1, t1, rs, ALU.mult)
                                K.tt("pool", mixC[:, t, :], t1, sgc[:, t, :], ALU.mult)
                        for b in range(4):
                            bs = slice(b * 128, (b + 1) * 128)
                            py = pw[b % 2]
                            for half in range(2):
                                for f in range(8):
                                    K.mm(py[:, half, :], mixC[:, f, bs], wobf[:, f, half * 512:(half + 1) * 512], start=(f == 0), stop=(f == 7))
                            xo = x2[b % 2]
                            K.tt("dve", xo, py.re("p a c -> p (a c)"), xch[:, b, :], ALU.add)
                            K.act(junk, xo, AF.Square, accum=ss2[:, b % 2:b % 2 + 1])
                            K.act(ln2[:, b % 2:b % 2 + 1], ss2[:, b % 2:b % 2 + 1], AF.Ln, bias=epsb[:, 0:1], scale=1.0 / 1024.0)
                            K.act(r2[:, b % 2:b % 2 + 1], ln2[:, b % 2:b % 2 + 1], AF.Exp, scale=-0.5)
                            yb = yo[b % 2]
                            K.ts("dve", yb, xo, r2[:, b % 2:b % 2 + 1], ALU.mult)
                            K.tt("pool", yb, yb, fnbc, ALU.mult)
                            r0 = C * CH + b * 128
                            K.dma("pool", y_d[r0:r0 + 128, :], yb, yo_slot[b % 2])
    except StopBuild:
        pass
    for s_ in K.slots:
        if s_.cnt:
            nc.gpsimd.wait_ge(s_.sem, s_.cnt)
    return nc, K


def _t5_bucket(rel):
    half = 16
    max_exact = 8
    ret = (rel > 0).astype(np.int32) * half
    dist = np.abs(rel)
    large = max_exact + (np.log(np.maximum(dist, 1) / max_exact) / np.log(128 / max_exact) * (half - max_exact)).astype(np.int32)
    large = np.minimum(large, half - 1)
    return ret + np.where(dist < max_exact, dist, large)


def _static_tables():
    f32 = np.float32
    st = {}
    st["ident"] = np.eye(128, dtype=f32)
    ob = np.zeros((128, 128), f32)
    ob[:64, :64] = 1
    ob[64:, 64:] = 1
    st["onesblk"] = ob
    j = np.arange(128)[:, None]
    i = np.arange(128)[None, :]
    mmat = np.zeros((128, 4, 128), f32)
    mmat[:, 0, :] = np.maximum(i - j, 0)
    mmat[:, 1, :] = (i >= j)
    mmat[:, 2, :] = np.maximum(j - i, 0)
    mmat[:, 3, :] = (j > i)
    st["mmat"] = mmat
    c = np.arange(512) % 128
    iot = np.zeros((128, 4, 512), f32)
    iot[:, 0, :] = c + 1
    iot[:, 1, :] = 128 - c
    iot[:, 2, :] = 127 - c
    iot[:, 3, :] = c
    st["iot"] = iot
    m = np.arange(640)
    rel = 255 - m
    bk = _t5_bucket(rel)
    oh = np.zeros((32, 640), f32)
    oh[bk, m] = 1
    st["oh"] = oh
    st["inwin"] = np.broadcast_to((np.abs(rel) <= 128).astype(f32)[None, :], (16, 640)).copy()
    return st


def _core_tables(is_prompt):
    f32 = np.float32
    seqlen = 4096 if is_prompt else 2048
    t = np.arange(NT) % seqlen
    d = np.arange(128) % 64
    pair = d // 2
    sgn = np.where(d % 2 == 0, -1.0, 1.0)
    quarter = 16
    freqs = (np.float32(10000.0) ** (-np.arange(quarter, dtype=f32) / quarter)).astype(f32)
    row = (t // 64).astype(f32)
    col = (t % 64).astype(f32)
    ang = np.concatenate([row[:, None] * freqs, col[:, None] * freqs], axis=-1).astype(f32)
    angd = ang[:, pair].T.astype(np.float64)
    tabA = np.stack([np.cos(angd), np.sin(angd) * sgn[:, None]]).astype(f32)
    half = 32
    freqs_b = (np.float32(10000.0) ** (-np.arange(half, dtype=f32) / half)).astype(f32)
    angb = (t.astype(f32)[:, None] * freqs_b).astype(f32)
    angbd = angb[:, pair].T.astype(np.float64)
    tabB = np.stack([np.cos(angbd), np.sin(angbd) * sgn[:, None]]).astype(f32)
    seq_of_blk = (np.arange(NB) * 128) // seqlen
    maskA = np.zeros((NCH, NB), f32)
    for C in range(NCH):
        sq = (C * CH) // seqlen
        maskA[C, :] = np.where(seq_of_blk == sq, 0.0, NEG)
    maskA = np.broadcast_to(maskA.reshape(1, -1), (128, NCH * NB)).copy()
    maskW = np.zeros((NB, 3), f32)
    for i in range(NB):
        for o in range(3):
            jb = i + o - 1
            if jb < 0 or jb >= NB or seq_of_blk[jb] != seq_of_blk[i]:
                maskW[i, o] = NEG
    maskW = np.broadcast_to(maskW.reshape(1, -1), (128, NB * 3)).copy()
    cps = seqlen // 128
    rf = np.array([0.0 if (n % cps == 0) else 1.0 for n in range(NB)], f32)
    rb = np.array([0.0 if (n % cps == cps - 1) else 1.0 for n in range(NB)], f32)
    rfb = np.broadcast_to(np.concatenate([rf, rb])[None, :], (128, 64)).copy()
    return {"tabA": tabA, "tabB": tabB, "maskA": maskA, "maskW": maskW, "rfb": rfb}


def _swap(cols):
    cols = np.asarray(cols)
    return cols ^ 1


def _prep_common(norm_g, w_in_ab, qk_norm_a, ret_decay, w_out_ab, w_in_c, sink_c, w_out_c, rel_bias, final_norm):
    f32 = np.float32
    W = np.asarray(w_in_ab[0], f32)
    qa = np.arange(0, 512)
    ka = np.arange(512, 640)
    va = np.arange(640, 768)
    ga = np.arange(768, 1280)
    qb = np.arange(1280, 1536)
    kb = np.arange(1536, 1792)
    vb = np.arange(1792, 2304)
    gb = np.arange(2304, 2816)
    kadup = np.concatenate([ka[0:64], ka[0:64], ka[64:128], ka[64:128]])
    cm = {}
    cm["wG"] = np.ascontiguousarray(W[:, np.concatenate([kadup, _swap(kadup), kb, _swap(kb), va, vb])])
    cm["wLB"] = np.ascontiguousarray(W[:, np.concatenate([qb, _swap(qb), kb, _swap(kb), gb, vb])])
    cm["wLA"] = np.ascontiguousarray(W[:, np.concatenate([qa, _swap(qa), ga])])
    cm["woab"] = np.ascontiguousarray(np.asarray(w_out_ab[0], f32))
    Wc = np.asarray(w_in_c[0], f32)
    kc = np.arange(1024, 1152)
    kcdup = np.concatenate([kc[0:64], kc[0:64], kc[64:128], kc[64:128]])
    cm["wG1"] = np.ascontiguousarray(Wc[:, np.concatenate([kcdup, np.arange(1152, 1280)])])
    cm["wL1"] = np.ascontiguousarray(Wc[:, np.concatenate([np.arange(0, 1024), np.arange(1280, 2304)])])
    cm["woc"] = np.ascontiguousarray(np.asarray(w_out_c[0], f32))
    ng = np.asarray(norm_g, f32)
    cm["gcol"] = np.ascontiguousarray(ng.reshape(2, 8, 128).transpose(2, 0, 1).reshape(128, 16))
    cm["fn"] = np.asarray(final_norm, f32).reshape(1, 1024).copy()
    g = np.asarray(qk_norm_a[0], f32)
    d = np.arange(128) % 64
    cm["gqk"] = np.stack([g[0][d], g[0][d ^ 1], g[1][d], g[1][d ^ 1]], axis=1).astype(f32).copy()
    rd = np.asarray(ret_decay[0], f32)
    hp = (np.arange(128) // 64)
    rdec = np.zeros((128, 12), f32)
    for p in range(2):
        rdec[:, p] = rd[0][2 * p + hp]
        rdec[:, 2 + p] = rd[1][2 * p + hp]
    for h in range(4):
        rdec[:, 4 + h] = rd[0][h]
        rdec[:, 8 + h] = rd[1][h]
    cm["rdec"] = rdec
    sk = np.asarray(sink_c[0], f32)
    sinkl = np.zeros((128, 8), f32)
    for t in range(8):
        sinkl[:, t] = sk[2 * t + hp]
    cm["sinkl"] = sinkl
    cm["relb"] = np.ascontiguousarray(np.asarray(rel_bias, f32))
    cm.update(_static_tables())
    return cm


_CACHE = {}


def kernel(x_prompt, x_sample, norm_g, w_in_ab, qk_norm_a, ret_decay, w_out_ab, w_in_c, sink_c, w_out_c, rel_bias, final_norm):
    xp = np.asarray(x_prompt, np.float32)
    xs = np.asarray(x_sample, np.float32)
    cm = _prep_common(norm_g, w_in_ab, qk_norm_a, ret_decay, w_out_ab, w_in_c, sink_c, w_out_c, rel_bias, final_norm)
    tp = _core_tables(True)
    tsm = _core_tables(False)
    in_maps = []
    for c in range(8):
        m = dict(cm)
        if c < 4:
            m["x"] = np.ascontiguousarray(xp[c])
            m.update(tp)
        else:
            m["x"] = np.ascontiguousarray(xs[2 * (c - 4):2 * (c - 4) + 2].reshape(NT, 1024))
            m.update(tsm)
        in_maps.append(m)
    if "nc" not in _CACHE:
        _CACHE["nc"] = build_program()[0]
    nc = _CACHE["nc"]
    res = run_bass_kernel_spmd(nc, in_maps, core_ids=list(range(8)))
    outs = [np.asarray(r["y"], np.float32) for r in res.results]
    y_prompt = np.stack(outs[0:4], axis=0)
    y_sample = np.stack(outs[4:8], axis=0).reshape(8, 2048, 1024)
    return (y_prompt, y_sample)
```

```python
import numpy as np
import concourse.bass as bass
import concourse.mybir as mybir
from concourse.bass_utils import run_bass_kernel_spmd

F32 = mybir.dt.float32
BF16 = mybir.dt.bfloat16
AF = mybir.ActivationFunctionType
ALU = mybir.AluOpType

NT = 4096
NB = 32
CH = 512
NCH = 8
EPS = 1e-6
NEG = -30000.0


class Prod:
    def __init__(self, sem, inc):
        self.sem = sem
        self.inc = inc
        self.cnt = 0


class Res:
    def __init__(self):
        self.w = {}
        self.r = {}
        self.excl = False


class V:
    def __init__(self, ap, res=None):
        self.ap = ap
        self.res = res if res is not None else Res()

    def __getitem__(self, k):
        return V(self.ap[k], self.res)

    def re(self, pat, **kw):
        return V(self.ap.rearrange(pat, **kw), self.res)

    def bc(self, shape):
        return V(self.ap.to_broadcast(shape), self.res)


class Ker:
    def __init__(self, nc):
        self.nc = nc
        self.eng = {"pe": nc.tensor, "act": nc.scalar, "dve": nc.vector, "pool": nc.gpsimd, "sp": nc.sync}
        self.prod = {}
        for n in ("pe", "act", "dve", "pool"):
            self.prod[n] = Prod(nc.alloc_semaphore("s_" + n), 1)
        self.seen = {n: {} for n in self.eng}
        self.nslot = 0
        self.ninstr = 0

    def slot(self):
        self.nslot += 1
        p = Prod(self.nc.alloc_semaphore("d%d" % self.nslot), 16)
        if hasattr(self, "slots"):
            self.slots.append(p)
        return p

    def _wait(self, en, reads, writes):
        deps = {}
        for v in reads:
            for p, i in v.res.w.items():
                deps[p] = max(deps.get(p, 0), i)
        for v in writes:
            for p, i in v.res.w.items():
                deps[p] = max(deps.get(p, 0), i)
            for p, i in v.res.r.items():
                deps[p] = max(deps.get(p, 0), i)
        e = self.eng[en]
        seen = self.seen[en]
        own = self.prod.get(en)
        for p, i in deps.items():
            if p is own and en == "pe":
                continue
            if seen.get(p, 0) >= i:
                continue
            e.wait_ge(p.sem, i)
            seen[p] = i

    def op(self, en, fn, reads, writes):
        writes = list(writes) + [r for r in reads if r.res.excl]
        self._wait(en, reads, writes)
        ins = fn(self.eng[en])
        p = self.prod[en]
        p.cnt += 1
        ins.then_inc(p.sem, 1)
        for v in reads:
            v.res.r[p] = p.cnt
        for v in writes:
            v.res.w[p] = p.cnt
        self.ninstr += 1

    def dma(self, q, out, in_, slot):
        self._wait(q, [in_], [out])
        ins = self.eng[q].dma_start(out=out.ap, in_=in_.ap)
        slot.cnt += 16
        ins.then_inc(slot.sem, 16)
        in_.res.r[slot] = slot.cnt
        out.res.w[slot] = slot.cnt

    def mm(self, out, lhsT, rhs, start=True, stop=True, tp=None):
        kw = {}
        if tp is not None:
            kw["tile_position"] = tp
        self.op("pe", lambda e: e.matmul(out.ap, lhsT.ap, rhs.ap, start=start, stop=stop, **kw), [lhsT, rhs], [out])

    def tr(self, out, in_, ident):
        self.op("pe", lambda e: e.transpose(out.ap, in_.ap, ident.ap), [in_, ident], [out])

    def act(self, out, in_, func, bias=None, scale=1.0, accum=None):
        reads = [in_]
        kw = {}
        if bias is not None:
            if isinstance(bias, V):
                reads.append(bias)
                kw["bias"] = bias.ap
            else:
                kw["bias"] = bias
        if isinstance(scale, V):
            reads.append(scale)
            kw["scale"] = scale.ap
        else:
            kw["scale"] = scale
        writes = [out]
        if accum is not None:
            writes.append(accum)
            kw["accum_out"] = accum.ap
        self.op("act", lambda e: e.activation(out.ap, in_.ap, func, **kw), reads, writes)

    def tt(self, en, out, a, b, op):
        self.op(en, lambda e: e.tensor_tensor(out.ap, a.ap, b.ap, op), [a, b], [out])

    def stt(self, en, out, in0, scalar, in1, op0, op1):
        reads = [in0, in1]
        s = scalar
        if isinstance(scalar, V):
            reads.append(scalar)
            s = scalar.ap
        self.op(en, lambda e: e.scalar_tensor_tensor(out.ap, in0.ap, s, in1.ap, op0, op1), reads, [out])

    def ts(self, en, out, in0, s1, op0, s2=None, op1=None):
        reads = [in0]
        a1 = s1
        if isinstance(s1, V):
            reads.append(s1)
            a1 = s1.ap
        a2 = s2
        if isinstance(s2, V):
            reads.append(s2)
            a2 = s2.ap
        if op1 is None:
            self.op(en, lambda e: e.tensor_scalar(out.ap, in0.ap, a1, None, op0), reads, [out])
        else:
            self.op(en, lambda e: e.tensor_scalar(out.ap, in0.ap, a1, a2, op0, op1), reads, [out])

    def cp(self, en, out, in_):
        if en == "act":
            self.op("act", lambda e: e.copy(out.ap, in_.ap), [in_], [out])
        else:
            self.op(en, lambda e: e.tensor_copy(out.ap, in_.ap), [in_], [out])

    def amul(self, out, in_, m):
        self.op("act", lambda e: e.mul(out.ap, in_.ap, m.ap), [in_, m], [out])

    def recip(self, out, in_):
        self.op("dve", lambda e: e.reciprocal(out.ap, in_.ap), [in_], [out])

    def memset(self, en, out, val):
        self.op(en, lambda e: e.memset(out.ap, val), [], [out])


class StopBuild(Exception):
    pass


import contextlib


class Scope(contextlib.ExitStack):
    def __init__(self, K):
        super().__init__()
        self.K = K
        self.tiles = []

    def __exit__(self, *a):
        fr = self.K.freed
        for v in self.tiles:
            for d in (v.res.w, v.res.r):
                for p, i in d.items():
                    fr[p] = max(fr.get(p, 0), i)
        self.tiles = []
        return super().__exit__(*a)

    def close(self):
        self.__exit__(None, None, None)


def build_program(stop=None, taps=()):
    nc = bass.Bass("TRN2", target_bir_lowering=False)
    K = Ker(nc)
    K.slots = []
    K.freed = {}
    K.tapped = {}

    def checkpoint(name):
        if stop == name:
            raise StopBuild()

    def tap(name, v, shape, dt=F32):
        if name not in taps or name in K.tapped:
            return
        d = V(nc.dram_tensor("dbg_" + name, list(shape), dt, kind="ExternalOutput").ap())
        K.tapped[name] = d
        K.dma("sp", d, v, K.slot())

    def din(name, shape, dt=F32):
        return V(nc.dram_tensor(name, list(shape), dt, kind="ExternalInput").ap())

    x_d = din("x", [NT, 1024])
    wG_d = din("wG", [1024, 1664])
    wLB_d = din("wLB", [1024, 2048])
    wLA_d = din("wLA", [1024, 1536])
    woab_d = din("woab", [1024, 1024])
    wG1_d = din("wG1", [1024, 384])
    wL1_d = din("wL1", [1024, 2048])
    woc_d = din("woc", [1024, 1024])
    gcol_d = din("gcol", [128, 16])
    fn_d = din("fn", [1, 1024])
    gqk_d = din("gqk", [128, 4])
    rdec_d = din("rdec", [128, 12])
    sink_d = din("sinkl", [128, 8])
    relb_d = din("relb", [32, 16])
    ident_d = din("ident", [128, 128])
    onesblk_d = din("onesblk", [128, 128])
    mm_d = din("mmat", [128, 4, 128])
    iot_d = din("iot", [128, 4, 512])
    oh_d = din("oh", [32, 640])
    inwin_d = din("inwin", [16, 640])
    tabA_d = din("tabA", [2, 128, NT])
    tabB_d = din("tabB", [2, 128, NT])
    maskA_d = din("maskA", [128, 256])
    maskW_d = din("maskW", [128, 96])
    rfb_d = din("rfb", [128, 64])
    y_d = V(nc.dram_tensor("y", [NT, 1024], F32, kind="ExternalOutput").ap())
    x1_d = V(nc.dram_tensor("x1s", [NT, 1024], F32, kind="Internal").ap())
    mixb_d = V(nc.dram_tensor("mixbs", [4, 128, NT], BF16, kind="Internal").ap())
    vec_d = V(nc.dram_tensor("vecs", [16, 640], BF16, kind="Internal").ap())
    xnT0_d = V(nc.dram_tensor("xnT0s", [NCH, 128, 8, CH], BF16, kind="Internal").ap())
    xnT1_d = V(nc.dram_tensor("xnT1s", [NCH, 128, 8, CH], BF16, kind="Internal").ap())

    es = Scope(K)
    uid = [0]

    def sb(name, shape, dt=F32, stack=None):
        uid[0] += 1
        st_ = stack if stack is not None else es
        t = st_.enter_context(nc.sbuf_tensor("sb%d_%s" % (uid[0], name), list(shape), dt))
        v = V(t[:])
        v.res.w = dict(K.freed)
        st_.tiles.append(v)
        return v

    def ps(name, shape, dt=F32, stack=None):
        uid[0] += 1
        st_ = stack if stack is not None else es
        t = st_.enter_context(nc.psum_tensor("ps%d_%s" % (uid[0], name), list(shape), dt))
        v = V(t[:])
        v.res.excl = True
        v.res.w = dict(K.freed)
        st_.tiles.append(v)
        return v

    try:
        with es:
            cslot = K.slot()
            consts = []

            def cload(name, src, shape, dt=F32, q="sp"):
                t = sb(name, shape, dt)
                K.dma(q, t, src, cslot)
                consts.append(t)
                return t

            gcol = cload("gcol", gcol_d, [128, 16])
            gqk = cload("gqk", gqk_d, [128, 4])
            rdec = cload("rdec", rdec_d, [128, 12])
            sinkl = cload("sinkl", sink_d, [128, 8])
            maskA = cload("maskA", maskA_d, [128, 256])
            maskW = cload("maskW", maskW_d, [128, 96])
            rfb = cload("rfb", rfb_d, [128, 64])
            ident32 = cload("ident32", ident_d, [128, 128])
            onesblk32 = cload("onesblk32", onesblk_d, [128, 128])
            for c in consts:
                c.res.w[cslot] = cslot.cnt
            ident = sb("ident", [128, 128], BF16)
            onesblk = sb("onesblk", [128, 128], BF16)
            ones = sb("ones", [128, 128], BF16)
            epsb = sb("epsb", [128, 1])
            K.cp("dve", ident, ident32)
            K.cp("dve", onesblk, onesblk32)
            K.memset("dve", ones, 1.0)
            K.memset("dve", epsb, EPS)

            checkpoint("c0")
            wbf = sb("wbf", [128, 8, 2048], BF16)
            wobf = sb("wobf", [128, 8, 1024], BF16)
            wst = [sb("wst%d" % i, [128, 1024]) for i in range(2)]
            wst_slot = [K.slot() for _ in range(2)]
            xnTs = [sb("xnT%d" % i, [128, 8, CH], BF16) for i in range(2)]
            xnT_slot = [K.slot() for _ in range(2)]
            cur = {"xnT": xnTs[0]}
            xch_slot = [K.slot() for _ in range(4)]
            xst_slot = [K.slot() for _ in range(2)]
            EBT = sb("EBT", [128, 16, 3, 128], BF16)
            esk = sb("esk", [128, 8], F32)
            K.act(esk, sinkl, AF.Exp)
            with Scope(K) as S1:
                relb = sb("relb", [32, 16], F32, S1)
                oh = sb("oh", [32, 640], F32, S1)
                inw = sb("inw", [16, 640], F32, S1)
                e_slot = K.slot()
                K.dma("sp", relb, relb_d, e_slot)
                K.dma("sp", oh, oh_d, e_slot)
                K.dma("sp", inw, inwin_d, e_slot)
                for t_ in (relb, oh, inw):
                    t_.res.w[e_slot] = e_slot.cnt
                pv = ps("pv", [16, 1024], F32, S1)[:, 0:640]
                vec = sb("vec", [16, 640], F32, S1)
                vecb = sb("vecb", [16, 640], BF16, S1)
                K.mm(pv[:, 0:512], relb, oh[:, 0:512])
                K.mm(pv[:, 512:640], relb, oh[:, 512:640])
                K.act(vec, pv, AF.Exp)
                K.tt("dve", vecb, vec, inw, ALU.mult)
                v_slot = K.slot()
                K.dma("sp", vec_d, vecb, v_slot)
                g_slot = [K.slot() for _ in range(2)]
                for k in range(128):
                    qn = "sp" if (k % 2 == 0) else "pool"
                    K.dma(qn, EBT[k:k + 1, :, :, :], V(vec_d.ap[:, 127 - k:127 - k + 384].rearrange("(a h) (o q) -> a h o q", a=1, o=3), vec_d.res), g_slot[k % 2])
            wcount = [0]

            def handoff(srcs, dsts):
                for d_ in dsts:
                    for s_ in srcs:
                        for dd in (s_.res.w, s_.res.r):
                            for p_, i_ in dd.items():
                                d_.res.w[p_] = max(d_.res.w.get(p_, 0), i_)

            def subview(parent, ap):
                v = V(ap)
                v.res.excl = parent.res.excl
                v.res.w = dict(parent.res.w)
                return v

            class MX:
                pass

            def alloc_mx(scope, full=True):
                m = MX()
                m.xch = sb("xch", [128, 4, 1024], F32, scope)
                if full:
                    m.xn = [sb("xn%d" % i, [128, 1024], BF16, scope) for i in range(2)]
                    m.junk = sb("junk", [128, 1024], BF16, scope)
                    m.ss = sb("ss", [128, 4], F32, scope)
                    m.lnv4 = sb("lnv4", [128, 4], F32, scope)
                    m.rstd4 = sb("rstd4", [128, 4], F32, scope)
                return m

            def load_x(m, src_d, C):
                for b in range(4):
                    r0 = C * CH + b * 128
                    K.dma("sp", m.xch[:, b, :], src_d[r0:r0 + 128, :], xch_slot[b])

            def store_xnT(dst_d, C, slot_i):
                K.dma("pool", dst_d[C], cur["xnT"], xst_slot[slot_i])

            def load_xnT(src_d, C):
                i = C % 2
                K.dma("sp", xnTs[i], src_d[C], xnT_slot[i])

            def use_xnT(C):
                cur["xnT"] = xnTs[C % 2]

            def load_w_piece(dst, d0, src_d, kc, c0, c1, layer_g):
                i = wcount[0] % 2
                wcount[0] += 1
                n_ = c1 - c0
                K.dma("sp", wst[i][:, 0:n_], src_d[kc * 128:(kc + 1) * 128, c0:c1], wst_slot[i])
                en = "act" if (wcount[0] % 2 == 0) else "dve"
                if layer_g is None:
                    K.cp(en, dst[:, kc, d0:d0 + n_], wst[i][:, 0:n_])
                elif en == "act":
                    K.amul(dst[:, kc, d0:d0 + n_], wst[i][:, 0:n_], gcol[:, layer_g * 8 + kc:layer_g * 8 + kc + 1])
                else:
                    K.ts("dve", dst[:, kc, d0:d0 + n_], wst[i][:, 0:n_], gcol[:, layer_g * 8 + kc:layer_g * 8 + kc + 1], ALU.mult)

            def load_w(dst, src_d, ncols, layer_g):
                for kc in range(8):
                    for c0 in range(0, ncols, 1024):
                        c1 = min(ncols, c0 + 1024)
                        load_w_piece(dst, c0, src_d, kc, c0, c1, layer_g)

            wlo = V(wbf.ap[:, :, 0:1536])
            whi = V(wbf.ap[:, :, 1536:2048])
            wmode = {"split": False, "base": 0}

            def wcol(kc, c0, n_=128):
                c0 = c0 + wmode["base"]
                if not wmode["split"]:
                    return wbf[:, kc, c0:c0 + n_]
                if c0 + n_ <= 1536:
                    return wlo[:, kc, c0:c0 + n_]
                return whi[:, kc, c0 - 1536:c0 - 1536 + n_]

            def make_xnT(m, src_d, C, pT):
                use_xnT(C)
                xnT = cur["xnT"]
                load_x(m, src_d, C)
                for b in range(4):
                    K.act(m.junk, m.xch[:, b, :], AF.Square, accum=m.ss[:, b:b + 1])
                K.act(m.lnv4, m.ss, AF.Ln, bias=epsb[:, 0:1], scale=1.0 / 1024.0)
                K.act(m.rstd4, m.lnv4, AF.Exp, scale=-0.5)
                for b in range(4):
                    xb = m.xn[b % 2]
                    K.ts("dve", xb, m.xch[:, b, :], m.rstd4[:, b:b + 1], ALU.mult)
                    for kc in range(8):
                        K.tr(pT[:, kc, :], xb[:, kc * 128:(kc + 1) * 128], ident)
                    K.cp("act", xnT[:, :, b * 128:(b + 1) * 128], pT)

            def proj(dst, c0):
                for kc in range(8):
                    K.mm(dst, wcol(kc, c0), cur["xnT"][:, kc, :], start=(kc == 0), stop=(kc == 7))

            def rsq_bcast(dst, src_ps, nfeat, sq, psn, lnv, lhs_ones):
                K.act(sq, src_ps, AF.Square)
                K.mm(psn, lhs_ones, sq)
                K.act(lnv, psn, AF.Ln, bias=epsb[:, 0:1], scale=1.0 / nfeat)
                K.act(dst, lnv, AF.Exp, scale=-0.5)

            with Scope(K) as L0:
                LR = Scope(K)
                KaT = sb("KaT", [128, 2, NT], BF16, L0)
                Va = sb("Va", [128, NB, 128], BF16, L0)
                tabc = sb("tabc", [128, 2, CH], F32, L0)
                tab_slot = K.slot()
                tabd_slot = K.slot()
                sq = sb("sq", [128, CH], BF16, L0)
                lnv = sb("lnv", [128, CH], F32, L0)
                rs = sb("rs", [128, CH], F32, L0)
                t1 = sb("t1", [128, CH], F32, L0)
                t2 = sb("t2", [128, CH], F32, L0)
                tabg = sb("tabg", [128, 2, CH], F32, L0)
                SbAll = sb("SbAll", [128, 2, NB, 128], BF16, LR)
                tabd = sb("tabd", [128, 2, CH], F32, LR)
                vbtm = sb("vbtm", [128, 4, 512], BF16, LR)
                lg = sb("lg", [128, 12], F32, LR)
                K.act(lg, rdec, AF.Exp)
                K.ts("dve", lg, lg, -1.0, ALU.mult)
                cd = sb("cd", [128, 4], F32, LR)
                K.act(cd, lg[:, 0:4], AF.Exp, scale=128.0)
                cdr = sb("cdr", [128, 4, NB], F32, LR)
                for j in range(4):
                    off = 0 if j < 2 else 32
                    K.ts("dve", cdr[:, j, :], rfb[:, off:off + 32], cd[:, j:j + 1], ALU.mult)
                checkpoint("c1")
                QF4 = sb("QF4", [128, 2, CH], F32, LR)
                QB4 = sb("QB4", [128, 2, CH], F32, LR)
                KF4 = sb("KF4", [128, 2, CH], F32, LR)
                KB4 = sb("KB4", [128, 2, CH], F32, LR)
                DT = sb("DT", [128, 4, 128], F32, LR)
                with Scope(K) as S0:
                    iot = sb("iot", [128, 4, CH], F32, S0)
                    K.dma("sp", iot, iot_d, tabd_slot)
                    for p in range(2):
                        K.act(QF4[:, p, :], iot[:, 0, :], AF.Exp, scale=lg[:, p:p + 1])
                        K.act(QB4[:, p, :], iot[:, 1, :], AF.Exp, scale=lg[:, 2 + p:3 + p])
                        K.act(KF4[:, p, :], iot[:, 2, :], AF.Exp, scale=lg[:, p:p + 1])
                        K.act(KB4[:, p, :], iot[:, 3, :], AF.Exp, scale=lg[:, 2 + p:3 + p])
                    K.ts("dve", KF4, KF4, 0.125, ALU.mult)
                    K.ts("dve", KB4, KB4, 0.125, ALU.mult)
                    mmat = sb("mmat", [128, 4, 128], F32, S0)
                    K.dma("sp", mmat, mm_d, tab_slot)
                    d1 = sb("d1", [128, 128], F32, S0)
                    d2 = sb("d2", [128, 128], F32, S0)
                    for h in range(4):
                        checkpoint("d0")
                        K.act(d1, mmat[:, 0, :], AF.Exp, scale=lg[:, 4 + h:5 + h])
                        checkpoint("d1")
                        K.tt("dve", d1, d1, mmat[:, 1, :], ALU.mult)
                        checkpoint("d2")
                        K.act(d2, mmat[:, 2, :], AF.Exp, scale=lg[:, 8 + h:9 + h])
                        K.tt("dve", d2, d2, mmat[:, 3, :], ALU.mult)
                        K.tt("dve", d1, d1, d2, ALU.add)
                        checkpoint("d3")
                        K.ts("dve", DT[:, h, :], d1, 0.125, ALU.mult)
                        checkpoint("d4")

                def load_tab(dst, slot, src_d, C):
                    K.dma("sp", dst, V(src_d.ap[:, :, C * CH:(C + 1) * CH].rearrange("t p c -> p t c"), src_d.res), slot)

                def rope(psa, psb, tab, out32, ga=None, gb=None):
                    if ga is None:
                        K.tt("dve", t1, psa, tab[:, 0, :], ALU.mult)
                        K.tt("dve", t2, psb, tab[:, 1, :], ALU.mult)
                    else:
                        K.amul(tabg[:, 0, :], tab[:, 0, :], ga)
                        K.amul(tabg[:, 1, :], tab[:, 1, :], gb)
                        K.tt("dve", t1, psa, tabg[:, 0, :], ALU.mult)
                        K.tt("dve", t2, psb, tabg[:, 1, :], ALU.mult)
                    K.tt("pool", out32, t1, t2, ALU.add)

                checkpoint("setup0")
                load_w(wbf, wG_d, 1664, 0)
                with Scope(K) as PG:
                    pT = ps("pT", [128, 8, 128], BF16, PG)
                    pk = pT.re("p (c t) q -> p c t q", t=2)
                    pbig = ps("pbigG", [128, 7, CH], F32, PG)
                    bk = [subview(pbig, pbig.ap[:, i, :]) for i in range(7)]
                    pn = bk[4]
                    pkv = V(bk[6].ap[:, 0:256].rearrange("p (a b) -> p a b", a=2), bk[6].res)
                    kdbT = sb("kdbT", [128, 2, CH], BF16, PG)
                    kdbtm = sb("kdbtm", [128, 4, 2, 128], BF16, PG)
                    Rb = sb("Rb", [128, 2, 128], F32, PG)
                    mxg = alloc_mx(PG)
                    WS = [dict(sq=sq, lnv=lnv, rs=rs, t1=t1, t2=t2),
                          dict(sq=sb("wsq", [128, CH], BF16, PG), lnv=sb("wlnv", [128, CH], F32, PG),
                               rs=sb("wrs", [128, CH], F32, PG), t1=sb("wt1", [128, CH], F32, PG),
                               t2=sb("wt2", [128, CH], F32, PG))]
                    K.memset("dve", Rb, 0.0)
                    for C in range(NCH - 1, -1, -1):
                        make_xnT(mxg, x_d, C, pT)
                        store_xnT(xnT0_d, C, C % 2)
                        load_w_piece(wobf, 0, woab_d, C, 0, 1024, None)
                        load_tab(tabc, tab_slot, tabA_d, C)
                        load_tab(tabd, tabd_slot, tabB_d, C)
                        K.amul(tabg[:, 0, :], tabc[:, 0, :], gqk[:, 2:3])
                        K.amul(tabg[:, 1, :], tabc[:, 1, :], gqk[:, 3:4])
                        for t in range(2):
                            w_ = WS[t % 2]
                            pa_, pb_ = bk[2 * t], bk[2 * t + 1]
                            proj(pa_, t * 128)
                            proj(pb_, 256 + t * 128)
                            rsq_bcast(w_["rs"], pa_, 64.0, w_["sq"], pn, w_["lnv"], onesblk)
                            K.tt("dve", w_["t1"], pa_, tabg[:, 0, :], ALU.mult)
                            K.tt("dve", w_["t2"], pb_, tabg[:, 1, :], ALU.mult)
                            K.tt("pool", w_["t1"], w_["t1"], w_["t2"], ALU.add)
                            K.tt("pool", KaT[:, t, C * CH:(C + 1) * CH], w_["t1"], w_["rs"], ALU.mult)
                        for t in range(2):
                            w_ = WS[t % 2]
                            pa_, pb_ = bk[2 * t], bk[2 * t + 1]
                            proj(pa_, 512 + t * 128)
                            proj(pb_, 768 + t * 128)
                            K.tt("dve", w_["t1"], pa_, tabd[:, 0, :], ALU.mult)
                            K.tt("dve", w_["t2"], pb_, tabd[:, 1, :], ALU.mult)
                            K.tt("pool", w_["t1"], w_["t1"], w_["t2"], ALU.add)
                            K.tt("pool", kdbT[:, t, :], w_["t1"], KB4[:, t, :], ALU.mult)
                            for cj in range(4):
                                K.tr(pk[:, cj, t, :], kdbT[:, t, cj * 128:(cj + 1) * 128], ident)
                        K.cp("act", kdbtm, pk)
                        for b in range(4):
                            pva = (bk[4] if b % 2 == 0 else bk[2])[:, 0:128]
                            pvb = bk[5] if b % 2 == 0 else bk[3]
                            for kc in range(8):
                                K.mm(pva, cur["xnT"][:, kc, b * 128:(b + 1) * 128], wbf[:, kc, 1024:1152], start=(kc == 0), stop=(kc == 7))
                            for kc in range(8):
                                K.mm(pvb, cur["xnT"][:, kc, b * 128:(b + 1) * 128], wbf[:, kc, 1152:1664], start=(kc == 0), stop=(kc == 7))
                            K.cp("act", Va[:, C * 4 + b, :], pva)
                            K.cp("dve", vbtm[:, b, :], pvb)
                        for cj in range(3, -1, -1):
                            n = C * 4 + cj
                            for p in range(2):
                                K.mm(pkv[0:64, p, :], kdbtm[:, cj, p, 0:64], vbtm[:, cj, (2 * p) * 128:(2 * p + 1) * 128])
                                K.mm(pkv[64:128, p, :], kdbtm[:, cj, p, 64:128], vbtm[:, cj, (2 * p + 1) * 128:(2 * p + 2) * 128], tp=(0, 64))
                            K.ts("dve", SbAll[:, :, n, :], Rb, rfb[:, 32 + n:33 + n], ALU.mult)
                            for p in range(2):
                                K.ts("dve", Rb[:, p, :], Rb[:, p, :], cdr[:, 2 + p, n:n + 1], ALU.mult)
                                K.tt("dve", Rb[:, p, :], pkv[:, p, :], Rb[:, p, :], ALU.add)

                tap("KaT", KaT, [128, 2, NT], BF16)
                tap("Va", Va, [128, NB, 128], BF16)
                tap("SbAll", SbAll, [128, 2, NB, 128], BF16)
                checkpoint("G")
                load_w(wbf, wLB_d, 2048, 0)
                with Scope(K) as PB:
                    pT = ps("pT", [128, 8, 128], BF16, PB)
                    pk = pT.re("p (c t) q -> p c t q", t=2)
                    pbig = ps("pbigB", [128, 7, CH], F32, PB)
                    bk = [subview(pbig, pbig.ap[:, i, :]) for i in range(7)]
                    pa, pb, pss = bk[0], bk[1], bk[2]
                    po = subview(pbig, pbig.ap[:, 3:7, :])
                    qrT = sb("qrT", [128, 2, CH], BF16, PB)
                    qdf = sb("qdf", [128, 2, CH], BF16, PB)
                    qdb = sb("qdb", [128, 2, CH], BF16, PB)
                    krT = sb("krT", [128, 2, CH], BF16, PB)
                    kdfT = sb("kdfT", [128, 2, CH], BF16, PB)
                    kdftm = sb("kdftm", [128, 4, 2, 128], BF16, PB)
                    sg = sb("sg", [128, 4, CH], BF16, PB)
                    ATs = [sb("AT%d" % i, [128, 4, 128], BF16, PB) for i in range(2)]
                    Sfs = [sb("Sf%d" % i, [128, 2, 128], BF16, PB) for i in range(2)]
                    Rf = sb("Rf", [128, 2, 128], F32, PB)
                    mixBc = [sb("mixBc%d" % i, [128, 4, CH], BF16, PB) for i in range(1)]
                    mixB_slot = [K.slot() for _ in range(1)]
                    WS = [dict(sq=sq, lnv=lnv, rs=rs, t1=t1, t2=t2),
                          dict(sq=sb("wsq", [128, CH], BF16, PB), lnv=sb("wlnv", [128, CH], F32, PB),
                               rs=sb("wrs", [128, CH], F32, PB), t1=sb("wt1", [128, CH], F32, PB),
                               t2=sb("wt2", [128, CH], F32, PB))]
                    K.memset("dve", Rf, 0.0)
                    load_xnT(xnT0_d, 0)
                    pairs = [(bk[0], bk[1]), (bk[3], bk[4]), (bk[5], bk[6])]
                    for C in range(NCH):
                        use_xnT(C)
                        if C + 1 < NCH:
                            load_xnT(xnT0_d, C + 1)
                        load_tab(tabd, tabd_slot, tabB_d, C)
                        handoff([po], bk[3:7])
                        ip = 0
                        for t in range(2):
                            w_ = WS[ip % 2]
                            pa_, pb_ = pairs[ip % 3]
                            ip += 1
                            proj(pa_, t * 128)
                            proj(pb_, 256 + t * 128)
                            K.tt("dve", w_["t1"], pa_, tabd[:, 0, :], ALU.mult)
                            K.tt("dve", w_["t2"], pb_, tabd[:, 1, :], ALU.mult)
                            K.tt("pool", w_["t1"], w_["t1"], w_["t2"], ALU.add)
                            K.cp("act", qrT[:, t, :], w_["t1"])
                            K.tt("pool", qdf[:, t, :], w_["t1"], QF4[:, t, :], ALU.mult)
                            K.tt("pool", qdb[:, t, :], w_["t1"], QB4[:, t, :], ALU.mult)
                        for t in range(2):
                            w_ = WS[ip % 2]
                            pa_, pb_ = pairs[ip % 3]
                            ip += 1
                            proj(pa_, 512 + t * 128)
                            proj(pb_, 768 + t * 128)
                            K.tt("dve", w_["t1"], pa_, tabd[:, 0, :], ALU.mult)
                            K.tt("dve", w_["t2"], pb_, tabd[:, 1, :], ALU.mult)
                            K.tt("pool", w_["t1"], w_["t1"], w_["t2"], ALU.add)
                            K.cp("act", krT[:, t, :], w_["t1"])
                            K.tt("pool", kdfT[:, t, :], w_["t1"], KF4[:, t, :], ALU.mult)
                            for cj in range(4):
                                K.tr(pk[:, cj, t, :], kdfT[:, t, cj * 128:(cj + 1) * 128], ident)
                        K.cp("act", kdftm, pk)
                        for h in range(4):
                            pa_ = bk[3 + h]
                            proj(pa_, 1024 + h * 128)
                            K.act(sg[:, h, :], pa_, AF.Silu)
                        for b in range(4):
                            pv_ = bk[1 + b % 2]
                            for kc in range(8):
                                K.mm(pv_, cur["xnT"][:, kc, b * 128:(b + 1) * 128], wbf[:, kc, 1536:2048], start=(kc == 0), stop=(kc == 7))
                            K.cp("dve", vbtm[:, b, :], pv_)
                        handoff(bk[3:7], [po])
                        for cj in range(4):
                            n = C * 4 + cj
                            cs = slice(cj * 128, (cj + 1) * 128)
                            Sf = Sfs[cj % 2]
                            AT = ATs[cj % 2]
                            K.ts("dve", Sf, Rf, rfb[:, n:n + 1], ALU.mult)
                            for p in range(2):
                                K.mm(pa[0:64, p * 128:(p + 1) * 128], kdftm[:, cj, p, 0:64], vbtm[:, cj, (2 * p) * 128:(2 * p + 1) * 128])
                                K.mm(pa[64:128, p * 128:(p + 1) * 128], kdftm[:, cj, p, 64:128], vbtm[:, cj, (2 * p + 1) * 128:(2 * p + 2) * 128], tp=(0, 64))
                            for p in range(2):
                                K.ts("dve", Rf[:, p, :], Rf[:, p, :], cdr[:, p, n:n + 1], ALU.mult)
                                K.tt("dve", Rf[:, p, :], pa[:, p * 128:(p + 1) * 128], Rf[:, p, :], ALU.add)
                            for h in range(4):
                                t, r0 = h // 2, (h % 2) * 64
                                pdst = pss if (h % 2 == 0) else pb
                                K.mm(pdst[:, t * 128:(t + 1) * 128], krT[r0:r0 + 64, t, cs], qrT[r0:r0 + 64, t, cs])
                            ATv = AT.re("p (t hp) i -> p hp t i", hp=2)
                            DTv = DT.re("p (t hp) i -> p hp t i", hp=2)
                            K.tt("dve", ATv[:, 0, :, :], pss[:, 0:256].re("p (t i) -> p t i", t=2), DTv[:, 0, :, :], ALU.mult)
                            K.tt("dve", ATv[:, 1, :, :], pb[:, 0:256].re("p (t i) -> p t i", t=2), DTv[:, 1, :, :], ALU.mult)
                            for h in range(4):
                                t, r0 = h // 2, (h % 2) * 64
                                K.mm(po[:, h, cs], vbtm[:, cj, h * 128:(h + 1) * 128], AT[:, h, :], start=True, stop=False)
                                K.mm(po[:, h, cs], Sf[r0:r0 + 64, t, :], qdf[r0:r0 + 64, t, cs], start=False, stop=False)
                                K.mm(po[:, h, cs], SbAll[r0:r0 + 64, t, n, :], qdb[r0:r0 + 64, t, cs], start=False, stop=True)
                        mb = mixBc[0]
                        for h in range(4):
                            w_ = WS[h % 2]
                            psn_ = bk[h % 3]
                            rsq_bcast(w_["rs"], po[:, h, :], 128.0, w_["sq"], psn_, w_["lnv"], ones)
                            K.tt("dve", w_["t1"], po[:, h, :], w_["rs"], ALU.mult)
                            K.tt("pool", mb[:, h, :], w_["t1"], sg[:, h, :], ALU.mult)
                        K.dma("pool", V(mixb_d.ap[:, :, C * CH:(C + 1) * CH].rearrange("h p c -> p h c"), mixb_d.res), mb, mixB_slot[0])

                tap("mixb", mixb_d, [4, 128, NT], BF16)
                checkpoint("LB")
                LR.close()
                load_w(wbf, wLA_d, 1536, 0)
                handoff([wbf], [wlo, whi])
                wmode["split"] = True
                with Scope(K) as PA:
                    pbig = ps("pbig", [128, 8, CH], F32, PA)
                    psc = [subview(pbig, pbig.ap[:, 2 * i:2 * i + 2, :]) for i in range(3)]
                    pnum = subview(pbig, pbig.ap[:, 6, :])
                    pden = subview(pbig, pbig.ap[:, 7, :])
                    bk = [subview(pbig, pbig.ap[:, i, :]) for i in range(6)] + [pnum, pden]
                    WS = [dict(sq=sq, lnv=lnv, rs=rs, t1=t1, t2=t2),
                          dict(sq=sb("wsq", [128, CH], BF16, PA), lnv=sb("wlnv", [128, CH], F32, PA),
                               rs=sb("wrs", [128, CH], F32, PA), t1=sb("wt1", [128, CH], F32, PA),
                               t2=sb("wt2", [128, CH], F32, PA))]
                    qaT = sb("qaT", [128, 4, CH], BF16, PA)
                    sga = sb("sga", [128, 4, CH], BF16, PA)
                    mixAs = [sb("mixA%d" % i, [128, 4, CH], BF16, PA) for i in range(2)]
                    mixBls = [sb("mixBl%d" % i, [128, 4, CH], BF16, PA) for i in range(2)]
                    mixBl_slots = [K.slot() for _ in range(2)]
                    xblk = [sb("xblk%d" % i, [128, 1024], F32, PA) for i in range(2)]
                    xblk_slot = [K.slot() for _ in range(2)]
                    nxb = [0]
                    pTs = [sb("pTs%d" % i, [128, 2, CH], BF16, PA) for i in range(3)]
                    x1b = [sb("x1b%d" % i, [128, 1024], F32, PA) for i in range(2)]
                    dcp = sb("dcp", [128, CH], F32, PA)
                    ncp = sb("ncp", [128, CH], F32, PA)
                    x1b_slot = [K.slot() for _ in range(2)]
                    qaTs = [qaT, sb("qaT1", [128, 4, CH], BF16, PA)]
                    sgas = [sga, sb("sga1", [128, 4, CH], BF16, PA)]
                    tabcs = [tabc, sb("tabc1", [128, 2, CH], F32, PA)]
                    tabgs = [tabg, sb("tabg1", [128, 2, CH], F32, PA)]
                    tabsl = [tab_slot, K.slot()]
                    nbuf = [0]
                    npt = [0]

                    held = set()

                    def take_buf():
                        while True:
                            i_ = nbuf[0] % 3
                            nbuf[0] += 1
                            if i_ not in held:
                                return psc[i_]

                    def hold(b_):
                        held.add(psc.index(b_))

                    def release(b_):
                        held.discard(psc.index(b_))

                    def projx(dst, c0, xT):
                        for kc in range(8):
                            K.mm(dst, wcol(kc, c0), xT[:, kc, :], start=(kc == 0), stop=(kc == 7))

                    def proj_items(Cn):
                        q_, g_ = qaTs[Cn % 2], sgas[Cn % 2]
                        tc_, tg_ = tabcs[Cn % 2], tabgs[Cn % 2]
                        xT = xnTs[Cn % 2]

                        def prep():
                            load_tab(tc_, tabsl[Cn % 2], tabA_d, Cn)
                            K.amul(tg_[:, 0, :], tc_[:, 0, :], gqk[:, 0:1])
                            K.amul(tg_[:, 1, :], tc_[:, 1, :], gqk[:, 1:2])

                        items = []
                        for t in range(4):
                            def mk(t=t):
                                st = {}
                                w_ = WS[t % 2]

                                def s1():
                                    st["buf"] = take_buf()
                                    hold(st["buf"])
                                    projx(st["buf"][:, 0, :], t * 128, xT)
                                    projx(st["buf"][:, 1, :], 512 + t * 128, xT)

                                def s2():
                                    pa_, pb_ = st["buf"][:, 0, :], st["buf"][:, 1, :]
                                    K.tt("dve", w_["t1"], pa_, tg_[:, 0, :], ALU.mult)
                                    K.tt("dve", w_["t2"], pb_, tg_[:, 1, :], ALU.mult)
                                    K.act(w_["sq"], pa_, AF.Square)

                                def s3():
                                    K.mm(st["buf"][:, 1, :], onesblk, w_["sq"])

                                def s4():
                                    K.act(w_["lnv"], st["buf"][:, 1, :], AF.Ln, bias=epsb[:, 0:1], scale=1.0 / 64.0)
                                    K.act(w_["rs"], w_["lnv"], AF.Exp, scale=-0.5)
                                    K.tt("pool", w_["t1"], w_["t1"], w_["t2"], ALU.add)
                                    K.tt("pool", q_[:, t, :], w_["t1"], w_["rs"], ALU.mult)
                                    release(st["buf"])
                                return [(s1, 3), (s2, 1), (s3, 2), (s4, 0)]
                            items.append(mk())
                        for t2_ in range(2):
                            def mk(t2_=t2_):
                                st = {}

                                def s1():
                                    st["buf"] = take_buf()
                                    hold(st["buf"])
                                    for j in range(2):
                                        projx(st["buf"][:, j, :], 1024 + (2 * t2_ + j) * 128, xT)

                                def s2():
                                    for j in range(2):
                                        K.act(WS[j]["t1"], st["buf"][:, j, :], AF.Tanh, scale=0.5)

                                def s3():
                                    for j in range(2):
                                        K.ts("dve", WS[j]["t1"], WS[j]["t1"], 0.5, ALU.mult, 0.5, ALU.add)
                                        K.tt("dve", g_[:, 2 * t2_ + j, :], st["buf"][:, j, :], WS[j]["t1"], ALU.mult)
                                    release(st["buf"])
                                return [(s1, 3), (s2, 1), (s3, 0)]
                            items.append(mk())
                        return prep, items

                    def outproj_items(Cc):
                        mA, mB = mixAs[Cc % 2], mixBls[Cc % 2]
                        items = []
                        for b in range(4):
                            def mk(b=b):
                                st = {}
                                bs = slice(b * 128, (b + 1) * 128)
                                r0 = Cc * CH + b * 128

                                def s1():
                                    i_ = nxb[0] % 2
                                    nxb[0] += 1
                                    st["i"] = i_
                                    K.dma("sp", xblk[i_], x_d[r0:r0 + 128, :], xblk_slot[i_])
                                    st["buf"] = take_buf()
                                    hold(st["buf"])
                                    py = st["buf"]
                                    for half in range(2):
                                        for f in range(8):
                                            src = mA[:, f, bs] if f < 4 else mB[:, f - 4, bs]
                                            K.mm(py[:, half, :], src, wobf[:, f, half * 512:(half + 1) * 512], start=(f == 0), stop=(f == 7))

                                def s2():
                                    xo = x1b[st["i"]]
                                    K.tt("dve", xo, st["buf"].re("p a c -> p (a c)"), xblk[st["i"]], ALU.add)
                                    K.dma("pool", x1_d[r0:r0 + 128, :], xo, x1b_slot[st["i"]])
                                    release(st["buf"])
                                return [(s1, 4), (s2, 0)]
                            items.append(mk())
                        return items

                    load_xnT(xnT0_d, 0)
                    prep0, items0 = proj_items(0)
                    prep0()
                    for it_ in items0:
                        for st_fn, _d in it_:
                            st_fn()
                    carry = []
                    for C in range(NCH):
                        use_xnT(C)
                        qaT_c, sga_c = qaTs[C % 2], sgas[C % 2]
                        mixA = mixAs[C % 2]
                        load_w_piece(whi, 0, wG1_d, C, 0, 384, 1)
                        pending = list(carry)
                        carry = []
                        if C + 1 < NCH:
                            load_xnT(xnT0_d, C + 1)
                            prepn, pitems = proj_items(C + 1)
                            prepn()
                            pending = pending + pitems
                        K.dma("pool", mixBls[C % 2], V(mixb_d.ap[:, :, C * CH:(C + 1) * CH].rearrange("h p c -> p h c"), mixb_d.res), mixBl_slots[C % 2])
                        nit = 0
                        active = [None]
                        for t in range(4):
                            kv = t // 2
                            fifo = []

                            def qk(kb_):
                                sc_ = take_buf()
                                fifo.append(sc_)
                                ks = slice(kb_ * 128, (kb_ + 1) * 128)
                                K.mm(sc_[:, 0, :], KaT[0:64, kv, ks], qaT_c[0:64, t, :])
                                K.mm(sc_[:, 1, :], KaT[64:128, kv, ks], qaT_c[64:128, t, :])

                            qk(0)
                            qk(1)
                            for kb in range(NB):
                                sc = fifo.pop(0)
                                pt = pTs[npt[0] % 3]
                                npt[0] += 1
                                nit += 1
                                K.act(pt, sc, AF.Exp, bias=maskA[:, C * NB + kb:C * NB + kb + 1], scale=0.125)
                                if kb + 2 < NB:
                                    qk(kb + 2)
                                if active[0] is None and pending and nit % 12 == 3:
                                    active[0] = [pending.pop(0), 0, nit]
                                if active[0] is not None and nit >= active[0][2]:
                                    stages_, si_, _due = active[0]
                                    fn_, delay_ = stages_[si_]
                                    fn_()
                                    if si_ + 1 < len(stages_):
                                        active[0] = [stages_, si_ + 1, nit + delay_]
                                    else:
                                        active[0] = None
                                st, sp_ = (kb == 0), (kb == NB - 1)
                                K.mm(pnum[0:64, :], Va[:, kb, kv * 64:(kv + 1) * 64], pt[:, 0, :], start=st, stop=sp_)
                                K.mm(pnum[64:128, :], Va[:, kb, kv * 64:(kv + 1) * 64], pt[:, 1, :], start=st, stop=sp_, tp=(0, 64))
                                K.mm(pden[0:64, :], ones[:, 0:64], pt[:, 0, :], start=st, stop=sp_)
                                K.mm(pden[64:128, :], ones[:, 0:64], pt[:, 1, :], start=st, stop=sp_, tp=(0, 64))
                            K.cp("dve", dcp, pden)
                            K.cp("dve", ncp, pnum)
                            K.recip(dcp, dcp)
                            K.tt("dve", ncp, ncp, dcp, ALU.mult)
                            K.tt("pool", mixA[:, t, :], ncp, sga_c[:, t, :], ALU.mult)
                        while active[0] is not None or pending:
                            if active[0] is None:
                                active[0] = [pending.pop(0), 0, 0]
                            stages_, si_, _due = active[0]
                            stages_[si_][0]()
                            active[0] = [stages_, si_ + 1, 0] if si_ + 1 < len(stages_) else None
                        carry = outproj_items(C)
                        if C == NCH - 1:
                            for it_ in carry:
                                for st_fn, _d in it_:
                                    st_fn()
                            carry = []

            tap("x1", x1_d, [NT, 1024], F32)
            checkpoint("LA")
            with Scope(K) as L1:
                KcT = sb("KcT", [128, 2, (NB + 2) * 128], BF16, L1)
                Vc = sb("Vc", [128, NB + 2, 128], BF16, L1)
                K.memset("pool", KcT[:, :, 0:128], 0.0)
                K.memset("pool", KcT[:, :, (NB + 1) * 128:(NB + 2) * 128], 0.0)
                K.memset("pool", Vc[:, 0, :], 0.0)
                K.memset("pool", Vc[:, NB + 1, :], 0.0)
                tap("EBT", EBT, [128, 16, 3, 128], BF16)
                checkpoint("EBT")
                wmode["base"] = 1536
                with Scope(K) as PG1:
                    pT = ps("pT", [128, 8, 128], BF16, PG1)
                    pa = ps("pa", [128, CH], F32, PG1)
                    pva = ps("pva", [128, 512], F32, PG1)[:, 0:128]
                    mxg1 = alloc_mx(PG1)
                    for C in range(NCH):
                        make_xnT(mxg1, x1_d, C, pT)
                        store_xnT(xnT1_d, C, C % 2)
                        load_w_piece(wlo, 0, wL1_d, C, 0, 1024, 1)
                        load_w_piece(wlo, 1024, wL1_d, C, 1024, 1536, 1)
                        load_w_piece(wobf, 0, woc_d, C, 0, 1024, None)
                        for t in range(2):
                            proj(pa, t * 128)
                            K.cp("act", KcT[:, t, (C * 4 + 1) * 128:(C * 4 + 5) * 128], pa)
                        for b in range(4):
                            for kc in range(8):
                                K.mm(pva, cur["xnT"][:, kc, b * 128:(b + 1) * 128], wcol(kc, 256), start=(kc == 0), stop=(kc == 7))
                            K.cp("dve", Vc[:, C * 4 + b + 1, :], pva)

                tap("KcT", KcT, [128, 2, (NB + 2) * 128], BF16)
                tap("Vc", Vc, [128, NB + 2, 128], BF16)
                checkpoint("G1")
                wmode["base"] = 0
                for kc_ in range(8):
                    load_w_piece(whi, 0, wL1_d, kc_, 1536, 2048, 1)
                with Scope(K) as PL1:
                    pbig = ps("pbig1", [128, 8, CH], F32, PL1)
                    pw = [subview(pbig, pbig.ap[:, 2 * i:2 * i + 2, :]) for i in range(2)]
                    pnum = subview(pbig, pbig.ap[:, 6, :])
                    pden = subview(pbig, pbig.ap[:, 7, :])
                    bk = [subview(pbig, pbig.ap[:, i, :]) for i in range(6)]
                    qcT = sb("qcT", [128, 8, CH], BF16, PL1)
                    sgc = sb("sgc", [128, 8, CH], BF16, PL1)
                    mixC = sb("mixC", [128, 8, CH], BF16, PL1)
                    pws = [sb("pws%d" % i, [128, 2, 3, 128], BF16, PL1) for i in range(2)]
                    pw2 = [sb("pw2%d" % i, [128, 2, 3, 128], BF16, PL1) for i in range(2)]
                    rs = sb("rs1", [128, CH], F32, PL1)
                    lnr = sb("lnr", [128, CH], F32, PL1)
                    t1 = sb("t11", [128, CH], F32, PL1)
                    x2 = [sb("x2%d" % i, [128, 1024], F32, PL1) for i in range(2)]
                    yo = [sb("yo%d" % i, [128, 1024], F32, PL1) for i in range(2)]
                    yo_slot = [K.slot() for _ in range(2)]
                    ss2 = sb("ss2", [128, 2], F32, PL1)
                    ln2 = sb("ln2", [128, 2], F32, PL1)
                    r2 = sb("r2", [128, 2], F32, PL1)
                    it = 0
                    mxl = alloc_mx(PL1, full=False)
                    xch = mxl.xch
                    junk = sb("junk1", [128, 1024], BF16, PL1)
                    fnbc = sb("fnbc", [128, 1024], F32, PL1)
                    fn_slot = K.slot()
                    K.dma("sp", fnbc, V(fn_d.ap.to_broadcast([128, 1024]), fn_d.res), fn_slot)
                    load_xnT(xnT1_d, 0)
                    for C in range(NCH):
                        use_xnT(C)
                        if C + 1 < NCH:
                            load_xnT(xnT1_d, C + 1)
                        load_x(mxl, x1_d, C)
                        handoff(pw, bk[0:4])
                        for t in range(8):
                            pa_ = bk[t % 6]
                            proj(pa_, t * 128)
                            K.cp("act" if t % 2 == 0 else "dve", qcT[:, t, :], pa_)
                        for t in range(8):
                            pa_ = bk[(t + 2) % 6]
                            proj(pa_, 1024 + t * 128)
                            K.act(sgc[:, t, :], pa_, AF.Silu)
                        handoff(bk[0:4], pw)
                        items = [(t, qi) for t in range(8) for qi in range(4)]

                        def wqk(t, qi, w):
                            kv = t // 4
                            i = C * 4 + qi
                            qs = slice(qi * 128, (qi + 1) * 128)
                            for o in range(3):
                                sl = 2 - o
                                ks = slice((i + o) * 128, (i + o + 1) * 128)
                                K.mm(w[:, 0, sl * 128:(sl + 1) * 128], KcT[0:64, kv, ks], qcT[0:64, t, qs])
                                K.mm(w[:, 1, sl * 128:(sl + 1) * 128], KcT[64:128, kv, ks], qcT[64:128, t, qs])

                        wqk(items[0][0], items[0][1], pw[it % 2])
                        for idx, (t, qi) in enumerate(items):
                            kv = t // 4
                            i = C * 4 + qi
                            qs = slice(qi * 128, (qi + 1) * 128)
                            w = pw[it % 2]
                            s1 = pws[it % 2]
                            s2 = pw2[it % 2]
                            it += 1
                            if i in (0, NB // 2 - 1, NB // 2, NB - 1):
                                for o in range(3):
                                    sl = 2 - o
                                    K.act(s1[:, :, sl, :], w[:, :, sl * 128:(sl + 1) * 128], AF.Exp, bias=maskW[:, i * 3 + o:i * 3 + o + 1], scale=0.125)
                            else:
                                K.act(s1, w[:, :, 0:384].re("p h (o q) -> p h o q", o=3), AF.Exp, scale=0.125)
                            K.tt("dve", s2, s1, EBT[:, 2 * t:2 * t + 2, :, :], ALU.mult)
                            if idx + 1 < len(items):
                                wqk(items[idx + 1][0], items[idx + 1][1], pw[it % 2])
                            for o in range(3):
                                sl = 2 - o
                                st, sp_ = (o == 0), (o == 2)
                                vv = Vc[:, i + o, kv * 64:(kv + 1) * 64]
                                K.mm(pnum[0:64, qs], vv, s2[:, 0, sl, :], start=st, stop=sp_)
                                K.mm(pnum[64:128, qs], vv, s2[:, 1, sl, :], start=st, stop=sp_, tp=(0, 64))
                                K.mm(pden[0:64, qs], ones[:, 0:64], s2[:, 0, sl, :], start=st, stop=sp_)
                                K.mm(pden[64:128, qs], ones[:, 0:64], s2[:, 1, sl, :], start=st, stop=sp_, tp=(0, 64))
                            if qi == 3:
                                K.ts("dve", rs, pden, esk[:, t:t + 1], ALU.add)
                                K.cp("dve", t1, pnum)
                                K.act(lnr, rs, AF.Ln)
                                K.act(rs, lnr, AF.Exp, scale=-1.0)
                                K.tt("pool", t1, t1, rs, ALU.mult)
                                K.tt("pool", mixC[:, t, :], t1, sgc[:, t, :], ALU.mult)
                        for b in range(4):
                            bs = slice(b * 128, (b + 1) * 128)
                            py = pw[b % 2]
                            for half in range(2):
                                for f in range(8):
                                    K.mm(py[:, half, :], mixC[:, f, bs], wobf[:, f, half * 512:(half + 1) * 512], start=(f == 0), stop=(f == 7))
                            xo = x2[b % 2]
                            K.tt("dve", xo, py.re("p a c -> p (a c)"), xch[:, b, :], ALU.add)
                            K.act(junk, xo, AF.Square, accum=ss2[:, b % 2:b % 2 + 1])
                            K.act(ln2[:, b % 2:b % 2 + 1], ss2[:, b % 2:b % 2 + 1], AF.Ln, bias=epsb[:, 0:1], scale=1.0 / 1024.0)
                            K.act(r2[:, b % 2:b % 2 + 1], ln2[:, b % 2:b % 2 + 1], AF.Exp, scale=-0.5)
                            yb = yo[b % 2]
                            K.ts("dve", yb, xo, r2[:, b % 2:b % 2 + 1], ALU.mult)
                            K.tt("pool", yb, yb, fnbc, ALU.mult)
                            r0 = C * CH + b * 128
                            K.dma("pool", y_d[r0:r0 + 128, :], yb, yo_slot[b % 2])
    except StopBuild:
        pass
    for s_ in K.slots:
        if s_.cnt:
            nc.gpsimd.wait_ge(s_.sem, s_.cnt)
    return nc, K


def _t5_bucket(rel):
    half = 16
    max_exact = 8
    ret = (rel > 0).astype(np.int32) * half
    dist = np.abs(rel)
    large = max_exact + (np.log(np.maximum(dist, 1) / max_exact) / np.log(128 / max_exact) * (half - max_exact)).astype(np.int32)
    large = np.minimum(large, half - 1)
    return ret + np.where(dist < max_exact, dist, large)


def _static_tables():
    f32 = np.float32
    st = {}
    st["ident"] = np.eye(128, dtype=f32)
    ob = np.zeros((128, 128), f32)
    ob[:64, :64] = 1
    ob[64:, 64:] = 1
    st["onesblk"] = ob
    j = np.arange(128)[:, None]
    i = np.arange(128)[None, :]
    mmat = np.zeros((128, 4, 128), f32)
    mmat[:, 0, :] = np.maximum(i - j, 0)
    mmat[:, 1, :] = (i >= j)
    mmat[:, 2, :] = np.maximum(j - i, 0)
    mmat[:, 3, :] = (j > i)
    st["mmat"] = mmat
    c = np.arange(512) % 128
    iot = np.zeros((128, 4, 512), f32)
    iot[:, 0, :] = c + 1
    iot[:, 1, :] = 128 - c
    iot[:, 2, :] = 127 - c
    iot[:, 3, :] = c
    st["iot"] = iot
    m = np.arange(640)
    rel = 255 - m
    bk = _t5_bucket(rel)
    oh = np.zeros((32, 640), f32)
    oh[bk, m] = 1
    st["oh"] = oh
    st["inwin"] = np.broadcast_to((np.abs(rel) <= 128).astype(f32)[None, :], (16, 640)).copy()
    return st


def _core_tables(is_prompt):
    f32 = np.float32
    seqlen = 4096 if is_prompt else 2048
    t = np.arange(NT) % seqlen
    d = np.arange(128) % 64
    pair = d // 2
    sgn = np.where(d % 2 == 0, -1.0, 1.0)
    quarter = 16
    freqs = (np.float32(10000.0) ** (-np.arange(quarter, dtype=f32) / quarter)).astype(f32)
    row = (t // 64).astype(f32)
    col = (t % 64).astype(f32)
    ang = np.concatenate([row[:, None] * freqs, col[:, None] * freqs], axis=-1).astype(f32)
    angd = ang[:, pair].T.astype(np.float64)
    tabA = np.stack([np.cos(angd), np.sin(angd) * sgn[:, None]]).astype(f32)
    half = 32
    freqs_b = (np.float32(10000.0) ** (-np.arange(half, dtype=f32) / half)).astype(f32)
    angb = (t.astype(f32)[:, None] * freqs_b).astype(f32)
    angbd = angb[:, pair].T.astype(np.float64)
    tabB = np.stack([np.cos(angbd), np.sin(angbd) * sgn[:, None]]).astype(f32)
    seq_of_blk = (np.arange(NB) * 128) // seqlen
    maskA = np.zeros((NCH, NB), f32)
    for C in range(NCH):
        sq = (C * CH) // seqlen
        maskA[C, :] = np.where(seq_of_blk == sq, 0.0, NEG)
    maskA = np.broadcast_to(maskA.reshape(1, -1), (128, NCH * NB)).copy()
    maskW = np.zeros((NB, 3), f32)
    for i in range(NB):
        for o in range(3):
            jb = i + o - 1
            if jb < 0 or jb >= NB or seq_of_blk[jb] != seq_of_blk[i]:
                maskW[i, o] = NEG
    maskW = np.broadcast_to(maskW.reshape(1, -1), (128, NB * 3)).copy()
    cps = seqlen // 128
    rf = np.array([0.0 if (n % cps == 0) else 1.0 for n in range(NB)], f32)
    rb = np.array([0.0 if (n % cps == cps - 1) else 1.0 for n in range(NB)], f32)
    rfb = np.broadcast_to(np.concatenate([rf, rb])[None, :], (128, 64)).copy()
    return {"tabA": tabA, "tabB": tabB, "maskA": maskA, "maskW": maskW, "rfb": rfb}


def _swap(cols):
    cols = np.asarray(cols)
    return cols ^ 1


def _prep_common(norm_g, w_in_ab, qk_norm_a, ret_decay, w_out_ab, w_in_c, sink_c, w_out_c, rel_bias, final_norm):
    f32 = np.float32
    W = np.asarray(w_in_ab[0], f32)
    qa = np.arange(0, 512)
    ka = np.arange(512, 640)
    va = np.arange(640, 768)
    ga = np.arange(768, 1280)
    qb = np.arange(1280, 1536)
    kb = np.arange(1536, 1792)
    vb = np.arange(1792, 2304)
    gb = np.arange(2304, 2816)
    kadup = np.concatenate([ka[0:64], ka[0:64], ka[64:128], ka[64:128]])
    cm = {}
    cm["wG"] = np.ascontiguousarray(W[:, np.concatenate([kadup, _swap(kadup), kb, _swap(kb), va, vb])])
    cm["wLB"] = np.ascontiguousarray(W[:, np.concatenate([qb, _swap(qb), kb, _swap(kb), gb, vb])])
    cm["wLA"] = np.ascontiguousarray(W[:, np.concatenate([qa, _swap(qa), ga])])
    cm["woab"] = np.ascontiguousarray(np.asarray(w_out_ab[0], f32))
    Wc = np.asarray(w_in_c[0], f32)
    kc = np.arange(1024, 1152)
    kcdup = np.concatenate([kc[0:64], kc[0:64], kc[64:128], kc[64:128]])
    cm["wG1"] = np.ascontiguousarray(Wc[:, np.concatenate([kcdup, np.arange(1152, 1280)])])
    cm["wL1"] = np.ascontiguousarray(Wc[:, np.concatenate([np.arange(0, 1024), np.arange(1280, 2304)])])
    cm["woc"] = np.ascontiguousarray(np.asarray(w_out_c[0], f32))
    ng = np.asarray(norm_g, f32)
    cm["gcol"] = np.ascontiguousarray(ng.reshape(2, 8, 128).transpose(2, 0, 1).reshape(128, 16))
    cm["fn"] = np.asarray(final_norm, f32).reshape(1, 1024).copy()
    g = np.asarray(qk_norm_a[0], f32)
    d = np.arange(128) % 64
    cm["gqk"] = np.stack([g[0][d], g[0][d ^ 1], g[1][d], g[1][d ^ 1]], axis=1).astype(f32).copy()
    rd = np.asarray(ret_decay[0], f32)
    hp = (np.arange(128) // 64)
    rdec = np.zeros((128, 12), f32)
    for p in range(2):
        rdec[:, p] = rd[0][2 * p + hp]
        rdec[:, 2 + p] = rd[1][2 * p + hp]
    for h in range(4):
        rdec[:, 4 + h] = rd[0][h]
        rdec[:, 8 + h] = rd[1][h]
    cm["rdec"] = rdec
    sk = np.asarray(sink_c[0], f32)
    sinkl = np.zeros((128, 8), f32)
    for t in range(8):
        sinkl[:, t] = sk[2 * t + hp]
    cm["sinkl"] = sinkl
    cm["relb"] = np.ascontiguousarray(np.asarray(rel_bias, f32))
    cm.update(_static_tables())
    return cm


_CACHE = {}


def kernel(x_prompt, x_sample, norm_g, w_in_ab, qk_norm_a, ret_decay, w_out_ab, w_in_c, sink_c, w_out_c, rel_bias, final_norm):
    xp = np.asarray(x_prompt, np.float32)
    xs = np.asarray(x_sample, np.float32)
    cm = _prep_common(norm_g, w_in_ab, qk_norm_a, ret_decay, w_out_ab, w_in_c, sink_c, w_out_c, rel_bias, final_norm)
    tp = _core_tables(True)
    tsm = _core_tables(False)
    in_maps = []
    for c in range(8):
        m = dict(cm)
        if c < 4:
            m["x"] = np.ascontiguousarray(xp[c])
            m.update(tp)
        else:
            m["x"] = np.ascontiguousarray(xs[2 * (c - 4):2 * (c - 4) + 2].reshape(NT, 1024))
            m.update(tsm)
        in_maps.append(m)
    if "nc" not in _CACHE:
        _CACHE["nc"] = build_program()[0]
    nc = _CACHE["nc"]
    res = run_bass_kernel_spmd(nc, in_maps, core_ids=list(range(8)))
    outs = [np.asarray(r["y"], np.float32) for r in res.results]
    y_prompt = np.stack(outs[0:4], axis=0)
    y_sample = np.stack(outs[4:8], axis=0).reshape(8, 2048, 1024)
    return (y_prompt, y_sample)
```

```python
import numpy as np
import concourse.bass as bass
import concourse.mybir as mybir
from concourse.bass_utils import run_bass_kernel_spmd

F32 = mybir.dt.float32
BF16 = mybir.dt.bfloat16
AF = mybir.ActivationFunctionType
ALU = mybir.AluOpType

NT = 4096
NB = 32
CH = 512
NCH = 8
EPS = 1e-6
NEG = -30000.0


class Prod:
    def __init__(self, sem, inc):
        self.sem = sem
        self.inc = inc
        self.cnt = 0


class Res:
    def __init__(self):
        self.w = {}
        self.r = {}
        self.excl = False


class V:
    def __init__(self, ap, res=None):
        self.ap = ap
        self.res = res if res is not None else Res()

    def __getitem__(self, k):
        return V(self.ap[k], self.res)

    def re(self, pat, **kw):
        return V(self.ap.rearrange(pat, **kw), self.res)

    def bc(self, shape):
        return V(self.ap.to_broadcast(shape), self.res)


class Ker:
    def __init__(self, nc):
        self.nc = nc
        self.eng = {"pe": nc.tensor, "act": nc.scalar, "dve": nc.vector, "pool": nc.gpsimd, "sp": nc.sync}
        self.prod = {}
        for n in ("pe", "act", "dve", "pool"):
            self.prod[n] = Prod(nc.alloc_semaphore("s_" + n), 1)
        self.seen = {n: {} for n in self.eng}
        self.nslot = 0
        self.ninstr = 0

    def slot(self):
        self.nslot += 1
        p = Prod(self.nc.alloc_semaphore("d%d" % self.nslot), 16)
        if hasattr(self, "slots"):
            self.slots.append(p)
        return p

    def _wait(self, en, reads, writes):
        deps = {}
        for v in reads:
            for p, i in v.res.w.items():
                deps[p] = max(deps.get(p, 0), i)
        for v in writes:
            for p, i in v.res.w.items():
                deps[p] = max(deps.get(p, 0), i)
            for p, i in v.res.r.items():
                deps[p] = max(deps.get(p, 0), i)
        e = self.eng[en]
        seen = self.seen[en]
        own = self.prod.get(en)
        for p, i in deps.items():
            if p is own and en == "pe":
                continue
            if seen.get(p, 0) >= i:
                continue
            e.wait_ge(p.sem, i)
            seen[p] = i

    def op(self, en, fn, reads, writes):
        writes = list(writes) + [r for r in reads if r.res.excl]
        self._wait(en, reads, writes)
        ins = fn(self.eng[en])
        p = self.prod[en]
        p.cnt += 1
        ins.then_inc(p.sem, 1)
        for v in reads:
            v.res.r[p] = p.cnt
        for v in writes:
            v.res.w[p] = p.cnt
        self.ninstr += 1

    def dma(self, q, out, in_, slot):
        self._wait(q, [in_], [out])
        ins = self.eng[q].dma_start(out=out.ap, in_=in_.ap)
        slot.cnt += 16
        ins.then_inc(slot.sem, 16)
        in_.res.r[slot] = slot.cnt
        out.res.w[slot] = slot.cnt

    def mm(self, out, lhsT, rhs, start=True, stop=True, tp=None):
        kw = {}
        if tp is not None:
            kw["tile_position"] = tp
        self.op("pe", lambda e: e.matmul(out.ap, lhsT.ap, rhs.ap, start=start, stop=stop, **kw), [lhsT, rhs], [out])

    def tr(self, out, in_, ident):
        self.op("pe", lambda e: e.transpose(out.ap, in_.ap, ident.ap), [in_, ident], [out])

    def act(self, out, in_, func, bias=None, scale=1.0, accum=None):
        reads = [in_]
        kw = {}
        if bias is not None:
            if isinstance(bias, V):
                reads.append(bias)
                kw["bias"] = bias.ap
            else:
                kw["bias"] = bias
        if isinstance(scale, V):
            reads.append(scale)
            kw["scale"] = scale.ap
        else:
            kw["scale"] = scale
        writes = [out]
        if accum is not None:
            writes.append(accum)
            kw["accum_out"] = accum.ap
        self.op("act", lambda e: e.activation(out.ap, in_.ap, func, **kw), reads, writes)

    def tt(self, en, out, a, b, op):
        self.op(en, lambda e: e.tensor_tensor(out.ap, a.ap, b.ap, op), [a, b], [out])

    def stt(self, en, out, in0, scalar, in1, op0, op1):
        reads = [in0, in1]
        s = scalar
        if isinstance(scalar, V):
            reads.append(scalar)
            s = scalar.ap
        self.op(en, lambda e: e.scalar_tensor_tensor(out.ap, in0.ap, s, in1.ap, op0, op1), reads, [out])

    def ts(self, en, out, in0, s1, op0, s2=None, op1=None):
        reads = [in0]
        a1 = s1
        if isinstance(s1, V):
            reads.append(s1)
            a1 = s1.ap
        a2 = s2
        if isinstance(s2, V):
            reads.append(s2)
            a2 = s2.ap
        if op1 is None:
            self.op(en, lambda e: e.tensor_scalar(out.ap, in0.ap, a1, None, op0), reads, [out])
        else:
            self.op(en, lambda e: e.tensor_scalar(out.ap, in0.ap, a1, a2, op0, op1), reads, [out])

    def cp(self, en, out, in_):
        if en == "act":
            self.op("act", lambda e: e.copy(out.ap, in_.ap), [in_], [out])
        else:
            self.op(en, lambda e: e.tensor_copy(out.ap, in_.ap), [in_], [out])

    def amul(self, out, in_, m):
        self.op("act", lambda e: e.mul(out.ap, in_.ap, m.ap), [in_, m], [out])

    def recip(self, out, in_):
        self.op("dve", lambda e: e.reciprocal(out.ap, in_.ap), [in_], [out])

    def memset(self, en, out, val):
        self.op(en, lambda e: e.memset(out.ap, val), [], [out])


class StopBuild(Exception):
    pass


import contextlib


class Scope(contextlib.ExitStack):
    def __init__(self, K):
        super().__init__()
        self.K = K
        self.tiles = []

    def __exit__(self, *a):
        fr = self.K.freed
        for v in self.tiles:
            for d in (v.res.w, v.res.r):
                for p, i in d.items():
                    fr[p] = max(fr.get(p, 0), i)
        self.tiles = []
        return super().__exit__(*a)

    def close(self):
        self.__exit__(None, None, None)


def build_program(stop=None, taps=()):
    nc = bass.Bass("TRN2", target_bir_lowering=False)
    K = Ker(nc)
    K.slots = []
    K.freed = {}
    K.tapped = {}

    def checkpoint(name):
        if stop == name:
            raise StopBuild()

    def tap(name, v, shape, dt=F32):
        if name not in taps or name in K.tapped:
            return
        d = V(nc.dram_tensor("dbg_" + name, list(shape), dt, kind="ExternalOutput").ap())
        K.tapped[name] = d
        K.dma("sp", d, v, K.slot())

    def din(name, shape, dt=F32):
        return V(nc.dram_tensor(name, list(shape), dt, kind="ExternalInput").ap())

    x_d = din("x", [NT, 1024])
    wG_d = din("wG", [1024, 1664])
    wLB_d = din("wLB", [1024, 2048])
    wLA_d = din("wLA", [1024, 1536])
    woab_d = din("woab", [1024, 1024])
    wG1_d = din("wG1", [1024, 384])
    wL1_d = din("wL1", [1024, 2048])
    woc_d = din("woc", [1024, 1024])
    gcol_d = din("gcol", [128, 16])
    fn_d = din("fn", [1, 1024])
    gqk_d = din("gqk", [128, 4])
    rdec_d = din("rdec", [128, 12])
    sink_d = din("sinkl", [128, 8])
    relb_d = din("relb", [32, 16])
    ident_d = din("ident", [128, 128])
    onesblk_d = din("onesblk", [128, 128])
    mm_d = din("mmat", [128, 4, 128])
    iot_d = din("iot", [128, 4, 512])
    oh_d = din("oh", [32, 640])
    inwin_d = din("inwin", [16, 640])
    tabA_d = din("tabA", [2, 128, NT])
    tabB_d = din("tabB", [2, 128, NT])
    maskA_d = din("maskA", [128, 256])
    maskW_d = din("maskW", [128, 96])
    rfb_d = din("rfb", [128, 64])
    y_d = V(nc.dram_tensor("y", [NT, 1024], F32, kind="ExternalOutput").ap())
    x1_d = V(nc.dram_tensor("x1s", [NT, 1024], F32, kind="Internal").ap())
    mixb_d = V(nc.dram_tensor("mixbs", [4, 128, NT], BF16, kind="Internal").ap())
    vec_h = nc.dram_tensor("vecs", [16, 640], BF16, kind="Internal")
    vec_d = V(vec_h.ap())
    aident_d = din("aident", [128, 128])
    xnT0_d = V(nc.dram_tensor("xnT0s", [NCH, 128, 8, CH], BF16, kind="Internal").ap())
    xnT1_d = V(nc.dram_tensor("xnT1s", [NCH, 128, 8, CH], BF16, kind="Internal").ap())

    es = Scope(K)
    uid = [0]

    def sb(name, shape, dt=F32, stack=None):
        uid[0] += 1
        st_ = stack if stack is not None else es
        t = st_.enter_context(nc.sbuf_tensor("sb%d_%s" % (uid[0], name), list(shape), dt))
        v = V(t[:])
        v.res.w = dict(K.freed)
        st_.tiles.append(v)
        return v

    def ps(name, shape, dt=F32, stack=None):
        uid[0] += 1
        st_ = stack if stack is not None else es
        t = st_.enter_context(nc.psum_tensor("ps%d_%s" % (uid[0], name), list(shape), dt))
        v = V(t[:])
        v.res.excl = True
        v.res.w = dict(K.freed)
        st_.tiles.append(v)
        return v

    try:
        with es:
            cslot = K.slot()
            consts = []

            def cload(name, src, shape, dt=F32, q="sp"):
                t = sb(name, shape, dt)
                K.dma(q, t, src, cslot)
                consts.append(t)
                return t

            gcol = cload("gcol", gcol_d, [128, 16])
            gqk = cload("gqk", gqk_d, [128, 4])
            rdec = cload("rdec", rdec_d, [128, 12])
            sinkl = cload("sinkl", sink_d, [128, 8])
            maskA = cload("maskA", maskA_d, [128, 256])
            maskW = cload("maskW", maskW_d, [128, 96])
            rfb = cload("rfb", rfb_d, [128, 64])
            ident32 = cload("ident32", ident_d, [128, 128])
            onesblk32 = cload("onesblk32", onesblk_d, [128, 128])
            for c in consts:
                c.res.w[cslot] = cslot.cnt
            ident = sb("ident", [128, 128], BF16)
            onesblk = sb("onesblk", [128, 128], BF16)
            ones = sb("ones", [128, 128], BF16)
            epsb = sb("epsb", [128, 1])
            K.cp("dve", ident, ident32)
            K.cp("dve", onesblk, onesblk32)
            K.memset("dve", ones, 1.0)
            K.memset("dve", epsb, EPS)

            checkpoint("c0")
            wbf = sb("wbf", [128, 8, 2048], BF16)
            wobf = sb("wobf", [128, 8, 1024], BF16)
            wst = [sb("wst%d" % i, [128, 1024]) for i in range(2)]
            wst_slot = [K.slot() for _ in range(2)]
            xnTs = [sb("xnT%d" % i, [128, 8, CH], BF16) for i in range(2)]
            xnT_slot = [K.slot() for _ in range(2)]
            cur = {"xnT": xnTs[0]}
            xch_slot = [K.slot() for _ in range(4)]
            xst_slot = [K.slot() for _ in range(2)]
            EBT = sb("EBT", [128, 16, 3, 128], BF16)
            esk = sb("esk", [128, 8], F32)
            K.act(esk, sinkl, AF.Exp)
            with Scope(K) as S1:
                relb = sb("relb", [32, 16], F32, S1)
                oh = sb("oh", [32, 640], F32, S1)
                inw = sb("inw", [16, 640], F32, S1)
                e_slot = K.slot()
                K.dma("sp", relb, relb_d, e_slot)
                K.dma("sp", oh, oh_d, e_slot)
                K.dma("sp", inw, inwin_d, e_slot)
                for t_ in (relb, oh, inw):
                    t_.res.w[e_slot] = e_slot.cnt
                pv = ps("pv", [16, 1024], F32, S1)[:, 0:640]
                vec = sb("vec", [16, 640], F32, S1)
                vecb = sb("vecb", [16, 640], BF16, S1)
                K.mm(pv[:, 0:512], relb, oh[:, 0:512])
                K.mm(pv[:, 512:640], relb, oh[:, 512:640])
                K.act(vec, pv, AF.Exp)
                K.tt("dve", vecb, vec, inw, ALU.mult)
                v_slot = K.slot()
                K.dma("sp", vec_d, vecb, v_slot)
                g_slot = K.slot()
                aid32 = sb("aid32", [128, 128], F32, S1)
                K.dma("sp", aid32, aident_d, g_slot)
                aid = sb("aid", [128, 128], BF16, S1)
                K.cp("dve", aid, aid32)
                TT = sb("TT", [128, 16 * 384], BF16, S1)
                src = V(bass.AP(vec_h, 0, [[1, 128], [640, 16], [1, 384]]), vec_d.res)
                K.dma("sp", TT.re("p (h j) -> p h j", h=16), src, g_slot)
                prev = ps("prev", [128, 2, CH], F32, S1)
                EBTf = EBT.re("p h o q -> p (h o q)")
                for n_ in range(12):
                    K.mm(prev[:, n_ % 2, :], aid, TT[:, n_ * 512:(n_ + 1) * 512])
                    K.cp("act" if n_ % 2 == 0 else "dve", EBTf[:, n_ * 512:(n_ + 1) * 512], prev[:, n_ % 2, :])
            wcount = [0]

            def handoff(srcs, dsts):
                for d_ in dsts:
                    for s_ in srcs:
                        for dd in (s_.res.w, s_.res.r):
                            for p_, i_ in dd.items():
                                d_.res.w[p_] = max(d_.res.w.get(p_, 0), i_)

            def subview(parent, ap):
                v = V(ap)
                v.res.excl = parent.res.excl
                v.res.w = dict(parent.res.w)
                return v

            class MX:
                pass

            def alloc_mx(scope, full=True):
                m = MX()
                m.xch = sb("xch", [128, 4, 1024], F32, scope)
                if full:
                    m.xn = [sb("xn%d" % i, [128, 1024], BF16, scope) for i in range(2)]
                    m.junk = sb("junk", [128, 1024], BF16, scope)
                    m.ss = sb("ss", [128, 4], F32, scope)
                    m.lnv4 = sb("lnv4", [128, 4], F32, scope)
                    m.rstd4 = sb("rstd4", [128, 4], F32, scope)
                return m

            def load_x(m, src_d, C):
                for b in range(4):
                    r0 = C * CH + b * 128
                    K.dma("sp", m.xch[:, b, :], src_d[r0:r0 + 128, :], xch_slot[b])

            def store_xnT(dst_d, C, slot_i):
                K.dma("pool", dst_d[C], cur["xnT"], xst_slot[slot_i])

            def load_xnT(src_d, C):
                i = C % 2
                K.dma("sp", xnTs[i], src_d[C], xnT_slot[i])

            def use_xnT(C):
                cur["xnT"] = xnTs[C % 2]

            def load_w(dst, src_d, ncols, layer_g):
                for kc in range(8):
                    for c0 in range(0, ncols, 1024):
                        c1 = min(ncols, c0 + 1024)
                        i = wcount[0] % 2
                        wcount[0] += 1
                        K.dma("sp", wst[i][:, 0:c1 - c0], src_d[kc * 128:(kc + 1) * 128, c0:c1], wst_slot[i])
                        en = "act" if (wcount[0] % 2 == 0) else "dve"
                        if layer_g is None:
                            K.cp(en, dst[:, kc, c0:c1], wst[i][:, 0:c1 - c0])
                        elif en == "act":
                            K.amul(dst[:, kc, c0:c1], wst[i][:, 0:c1 - c0], gcol[:, layer_g * 8 + kc:layer_g * 8 + kc + 1])
                        else:
                            K.ts("dve", dst[:, kc, c0:c1], wst[i][:, 0:c1 - c0], gcol[:, layer_g * 8 + kc:layer_g * 8 + kc + 1], ALU.mult)

            def make_xnT(m, src_d, C, pT):
                use_xnT(C)
                xnT = cur["xnT"]
                load_x(m, src_d, C)
                for b in range(4):
                    K.act(m.junk, m.xch[:, b, :], AF.Square, accum=m.ss[:, b:b + 1])
                K.act(m.lnv4, m.ss, AF.Ln, bias=epsb[:, 0:1], scale=1.0 / 1024.0)
                K.act(m.rstd4, m.lnv4, AF.Exp, scale=-0.5)
                for b in range(4):
                    xb = m.xn[b % 2]
                    K.ts("dve", xb, m.xch[:, b, :], m.rstd4[:, b:b + 1], ALU.mult)
                    for kc in range(8):
                        K.tr(pT[:, kc, :], xb[:, kc * 128:(kc + 1) * 128], ident)
                    K.cp("act", xnT[:, :, b * 128:(b + 1) * 128], pT)

            def proj(dst, c0):
                for kc in range(8):
                    K.mm(dst, wbf[:, kc, c0:c0 + 128], cur["xnT"][:, kc, :], start=(kc == 0), stop=(kc == 7))

            def rsq_bcast(dst, src_ps, nfeat, sq, psn, lnv, lhs_ones):
                K.act(sq, src_ps, AF.Square)
                K.mm(psn, lhs_ones, sq)
                K.act(lnv, psn, AF.Ln, bias=epsb[:, 0:1], scale=1.0 / nfeat)
                K.act(dst, lnv, AF.Exp, scale=-0.5)

            with Scope(K) as L0:
                LR = Scope(K)
                KaT = sb("KaT", [128, 2, NT], BF16, L0)
                Va = sb("Va", [128, NB, 128], BF16, L0)
                tabc = sb("tabc", [128, 2, CH], F32, L0)
                tab_slot = K.slot()
                tabd_slot = K.slot()
                sq = sb("sq", [128, CH], BF16, L0)
                lnv = sb("lnv", [128, CH], F32, L0)
                rs = sb("rs", [128, CH], F32, L0)
                t1 = sb("t1", [128, CH], F32, L0)
                t2 = sb("t2", [128, CH], F32, L0)
                tabg = sb("tabg", [128, 2, CH], F32, L0)
                SbAll = sb("SbAll", [128, 2, NB, 128], BF16, LR)
                tabd = sb("tabd", [128, 2, CH], F32, LR)
                vbtm = sb("vbtm", [128, 4, 512], BF16, LR)
                lg = sb("lg", [128, 12], F32, LR)
                K.act(lg, rdec, AF.Exp)
                K.ts("dve", lg, lg, -1.0, ALU.mult)
                cd = sb("cd", [128, 4], F32, LR)
                K.act(cd, lg[:, 0:4], AF.Exp, scale=128.0)
                cdr = sb("cdr", [128, 4, NB], F32, LR)
                for j in range(4):
                    off = 0 if j < 2 else 32
                    K.ts("dve", cdr[:, j, :], rfb[:, off:off + 32], cd[:, j:j + 1], ALU.mult)
                checkpoint("c1")
                QF4 = sb("QF4", [128, 2, CH], F32, LR)
                QB4 = sb("QB4", [128, 2, CH], F32, LR)
                KF4 = sb("KF4", [128, 2, CH], F32, LR)
                KB4 = sb("KB4", [128, 2, CH], F32, LR)
                DT = sb("DT", [128, 4, 128], F32, LR)
                with Scope(K) as S0:
                    iot = sb("iot", [128, 4, CH], F32, S0)
                    K.dma("sp", iot, iot_d, tabd_slot)
                    for p in range(2):
                        K.act(QF4[:, p, :], iot[:, 0, :], AF.Exp, scale=lg[:, p:p + 1])
                        K.act(QB4[:, p, :], iot[:, 1, :], AF.Exp, scale=lg[:, 2 + p:3 + p])
                        K.act(KF4[:, p, :], iot[:, 2, :], AF.Exp, scale=lg[:, p:p + 1])
                        K.act(KB4[:, p, :], iot[:, 3, :], AF.Exp, scale=lg[:, 2 + p:3 + p])
                    K.ts("dve", KF4, KF4, 0.125, ALU.mult)
                    K.ts("dve", KB4, KB4, 0.125, ALU.mult)
                    mmat = sb("mmat", [128, 4, 128], F32, S0)
                    K.dma("sp", mmat, mm_d, tab_slot)
                    d1 = sb("d1", [128, 128], F32, S0)
                    d2 = sb("d2", [128, 128], F32, S0)
                    for h in range(4):
                        checkpoint("d0")
                        K.act(d1, mmat[:, 0, :], AF.Exp, scale=lg[:, 4 + h:5 + h])
                        checkpoint("d1")
                        K.tt("dve", d1, d1, mmat[:, 1, :], ALU.mult)
                        checkpoint("d2")
                        K.act(d2, mmat[:, 2, :], AF.Exp, scale=lg[:, 8 + h:9 + h])
                        K.tt("dve", d2, d2, mmat[:, 3, :], ALU.mult)
                        K.tt("dve", d1, d1, d2, ALU.add)
                        checkpoint("d3")
                        K.ts("dve", DT[:, h, :], d1, 0.125, ALU.mult)
                        checkpoint("d4")

                def load_tab(dst, slot, src_d, C):
                    K.dma("sp", dst, V(src_d.ap[:, :, C * CH:(C + 1) * CH].rearrange("t p c -> p t c"), src_d.res), slot)

                def rope(psa, psb, tab, out32, ga=None, gb=None):
                    if ga is None:
                        K.tt("dve", t1, psa, tab[:, 0, :], ALU.mult)
                        K.tt("dve", t2, psb, tab[:, 1, :], ALU.mult)
                    else:
                        K.amul(tabg[:, 0, :], tab[:, 0, :], ga)
                        K.amul(tabg[:, 1, :], tab[:, 1, :], gb)
                        K.tt("dve", t1, psa, tabg[:, 0, :], ALU.mult)
                        K.tt("dve", t2, psb, tabg[:, 1, :], ALU.mult)
                    K.tt("pool", out32, t1, t2, ALU.add)

                checkpoint("setup0")
                load_w(wbf, wG_d, 1664, 0)
                with Scope(K) as PG:
                    pT = ps("pT", [128, 8, 128], BF16, PG)
                    pk = pT.re("p (c t) q -> p c t q", t=2)
                    pbig = ps("pbigG", [128, 7, CH], F32, PG)
                    bk = [subview(pbig, pbig.ap[:, i, :]) for i in range(7)]
                    pn = bk[4]
                    pkv = V(bk[6].ap[:, 0:256].rearrange("p (a b) -> p a b", a=2), bk[6].res)
                    kdbT = sb("kdbT", [128, 2, CH], BF16, PG)
                    kdbtm = sb("kdbtm", [128, 4, 2, 128], BF16, PG)
                    Rb = sb("Rb", [128, 2, 128], F32, PG)
                    mxg = alloc_mx(PG)
                    WS = [dict(sq=sq, lnv=lnv, rs=rs, t1=t1, t2=t2),
                          dict(sq=sb("wsq", [128, CH], BF16, PG), lnv=sb("wlnv", [128, CH], F32, PG),
                               rs=sb("wrs", [128, CH], F32, PG), t1=sb("wt1", [128, CH], F32, PG),
                               t2=sb("wt2", [128, CH], F32, PG))]
                    K.memset("dve", Rb, 0.0)
                    for C in range(NCH - 1, -1, -1):
                        make_xnT(mxg, x_d, C, pT)
                        store_xnT(xnT0_d, C, C % 2)
                        load_tab(tabc, tab_slot, tabA_d, C)
                        load_tab(tabd, tabd_slot, tabB_d, C)
                        K.amul(tabg[:, 0, :], tabc[:, 0, :], gqk[:, 2:3])
                        K.amul(tabg[:, 1, :], tabc[:, 1, :], gqk[:, 3:4])
                        for t in range(2):
                            w_ = WS[t % 2]
                            pa_, pb_ = bk[2 * t], bk[2 * t + 1]
                            proj(pa_, t * 128)
                            proj(pb_, 256 + t * 128)
                            rsq_bcast(w_["rs"], pa_, 64.0, w_["sq"], pn, w_["lnv"], onesblk)
                            K.tt("dve", w_["t1"], pa_, tabg[:, 0, :], ALU.mult)
                            K.tt("dve", w_["t2"], pb_, tabg[:, 1, :], ALU.mult)
                            K.tt("pool", w_["t1"], w_["t1"], w_["t2"], ALU.add)
                            K.tt("pool", KaT[:, t, C * CH:(C + 1) * CH], w_["t1"], w_["rs"], ALU.mult)
                        for t in range(2):
                            w_ = WS[t % 2]
                            pa_, pb_ = bk[2 * t], bk[2 * t + 1]
                            proj(pa_, 512 + t * 128)
                            proj(pb_, 768 + t * 128)
                            K.tt("dve", w_["t1"], pa_, tabd[:, 0, :], ALU.mult)
                            K.tt("dve", w_["t2"], pb_, tabd[:, 1, :], ALU.mult)
                            K.tt("pool", w_["t1"], w_["t1"], w_["t2"], ALU.add)
                            K.tt("pool", kdbT[:, t, :], w_["t1"], KB4[:, t, :], ALU.mult)
                            for cj in range(4):
                                K.tr(pk[:, cj, t, :], kdbT[:, t, cj * 128:(cj + 1) * 128], ident)
                        K.cp("act", kdbtm, pk)
                        for b in range(4):
                            pva = (bk[4] if b % 2 == 0 else bk[2])[:, 0:128]
                            pvb = bk[5] if b % 2 == 0 else bk[3]
                            for kc in range(8):
                                K.mm(pva, cur["xnT"][:, kc, b * 128:(b + 1) * 128], wbf[:, kc, 1024:1152], start=(kc == 0), stop=(kc == 7))
                            for kc in range(8):
                                K.mm(pvb, cur["xnT"][:, kc, b * 128:(b + 1) * 128], wbf[:, kc, 1152:1664], start=(kc == 0), stop=(kc == 7))
                            K.cp("act", Va[:, C * 4 + b, :], pva)
                            K.cp("dve", vbtm[:, b, :], pvb)
                        for cj in range(3, -1, -1):
                            n = C * 4 + cj
                            for p in range(2):
                                K.mm(pkv[0:64, p, :], kdbtm[:, cj, p, 0:64], vbtm[:, cj, (2 * p) * 128:(2 * p + 1) * 128])
                                K.mm(pkv[64:128, p, :], kdbtm[:, cj, p, 64:128], vbtm[:, cj, (2 * p + 1) * 128:(2 * p + 2) * 128], tp=(0, 64))
                            K.ts("dve", SbAll[:, :, n, :], Rb, rfb[:, 32 + n:33 + n], ALU.mult)
                            for p in range(2):
                                K.ts("dve", Rb[:, p, :], Rb[:, p, :], cdr[:, 2 + p, n:n + 1], ALU.mult)
                                K.tt("dve", Rb[:, p, :], pkv[:, p, :], Rb[:, p, :], ALU.add)

                tap("KaT", KaT, [128, 2, NT], BF16)
                tap("Va", Va, [128, NB, 128], BF16)
                tap("SbAll", SbAll, [128, 2, NB, 128], BF16)
                checkpoint("G")
                load_w(wbf, wLB_d, 2048, 0)
                with Scope(K) as PB:
                    pT = ps("pT", [128, 8, 128], BF16, PB)
                    pk = pT.re("p (c t) q -> p c t q", t=2)
                    pbig = ps("pbigB", [128, 7, CH], F32, PB)
                    bk = [subview(pbig, pbig.ap[:, i, :]) for i in range(7)]
                    pa, pb, pss = bk[0], bk[1], bk[2]
                    po = subview(pbig, pbig.ap[:, 3:7, :])
                    qrT = sb("qrT", [128, 2, CH], BF16, PB)
                    qdf = sb("qdf", [128, 2, CH], BF16, PB)
                    qdb = sb("qdb", [128, 2, CH], BF16, PB)
                    krT = sb("krT", [128, 2, CH], BF16, PB)
                    kdfT = sb("kdfT", [128, 2, CH], BF16, PB)
                    kdftm = sb("kdftm", [128, 4, 2, 128], BF16, PB)
                    sg = sb("sg", [128, 4, CH], BF16, PB)
                    ATs = [sb("AT%d" % i, [128, 4, 128], BF16, PB) for i in range(2)]
                    Sfs = [sb("Sf%d" % i, [128, 2, 128], BF16, PB) for i in range(2)]
                    Rf = sb("Rf", [128, 2, 128], F32, PB)
                    mixBc = [sb("mixBc%d" % i, [128, 4, CH], BF16, PB) for i in range(1)]
                    mixB_slot = [K.slot() for _ in range(1)]
                    WS = [dict(sq=sq, lnv=lnv, rs=rs, t1=t1, t2=t2),
                          dict(sq=sb("wsq", [128, CH], BF16, PB), lnv=sb("wlnv", [128, CH], F32, PB),
                               rs=sb("wrs", [128, CH], F32, PB), t1=sb("wt1", [128, CH], F32, PB),
                               t2=sb("wt2", [128, CH], F32, PB))]
                    K.memset("dve", Rf, 0.0)
                    load_xnT(xnT0_d, 0)
                    pairs = [(bk[0], bk[1]), (bk[3], bk[4]), (bk[5], bk[6])]
                    for C in range(NCH):
                        use_xnT(C)
                        if C + 1 < NCH:
                            load_xnT(xnT0_d, C + 1)
                        load_tab(tabd, tabd_slot, tabB_d, C)
                        handoff([po], bk[3:7])
                        ip = 0
                        for t in range(2):
                            w_ = WS[ip % 2]
                            pa_, pb_ = pairs[ip % 3]
                            ip += 1
                            proj(pa_, t * 128)
                            proj(pb_, 256 + t * 128)
                            K.tt("dve", w_["t1"], pa_, tabd[:, 0, :], ALU.mult)
                            K.tt("dve", w_["t2"], pb_, tabd[:, 1, :], ALU.mult)
                            K.tt("pool", w_["t1"], w_["t1"], w_["t2"], ALU.add)
                            K.cp("act", qrT[:, t, :], w_["t1"])
                            K.tt("pool", qdf[:, t, :], w_["t1"], QF4[:, t, :], ALU.mult)
                            K.tt("pool", qdb[:, t, :], w_["t1"], QB4[:, t, :], ALU.mult)
                        for t in range(2):
                            w_ = WS[ip % 2]
                            pa_, pb_ = pairs[ip % 3]
                            ip += 1
                            proj(pa_, 512 + t * 128)
                            proj(pb_, 768 + t * 128)
                            K.tt("dve", w_["t1"], pa_, tabd[:, 0, :], ALU.mult)
                            K.tt("dve", w_["t2"], pb_, tabd[:, 1, :], ALU.mult)
                            K.tt("pool", w_["t1"], w_["t1"], w_["t2"], ALU.add)
                            K.cp("act", krT[:, t, :], w_["t1"])
                            K.tt("pool", kdfT[:, t, :], w_["t1"], KF4[:, t, :], ALU.mult)
                            for cj in range(4):
                                K.tr(pk[:, cj, t, :], kdfT[:, t, cj * 128:(cj + 1) * 128], ident)
                        K.cp("act", kdftm, pk)
                        for h in range(4):
                            pa_ = bk[3 + h]
                            proj(pa_, 1024 + h * 128)
                            K.act(sg[:, h, :], pa_, AF.Silu)
                        for b in range(4):
                            pv_ = bk[1 + b % 2]
                            for kc in range(8):
                                K.mm(pv_, cur["xnT"][:, kc, b * 128:(b + 1) * 128], wbf[:, kc, 1536:2048], start=(kc == 0), stop=(kc == 7))
                            K.cp("dve", vbtm[:, b, :], pv_)
                        handoff(bk[3:7], [po])
                        for cj in range(4):
                            n = C * 4 + cj
                            cs = slice(cj * 128, (cj + 1) * 128)
                            Sf = Sfs[cj % 2]
                            AT = ATs[cj % 2]
                            K.ts("dve", Sf, Rf, rfb[:, n:n + 1], ALU.mult)
                            for p in range(2):
                                K.mm(pa[0:64, p * 128:(p + 1) * 128], kdftm[:, cj, p, 0:64], vbtm[:, cj, (2 * p) * 128:(2 * p + 1) * 128])
                                K.mm(pa[64:128, p * 128:(p + 1) * 128], kdftm[:, cj, p, 64:128], vbtm[:, cj, (2 * p + 1) * 128:(2 * p + 2) * 128], tp=(0, 64))
                            for p in range(2):
                                K.ts("dve", Rf[:, p, :], Rf[:, p, :], cdr[:, p, n:n + 1], ALU.mult)
                                K.tt("dve", Rf[:, p, :], pa[:, p * 128:(p + 1) * 128], Rf[:, p, :], ALU.add)
                            for h in range(4):
                                t, r0 = h // 2, (h % 2) * 64
                                pdst = pss if (h % 2 == 0) else pb
                                K.mm(pdst[:, t * 128:(t + 1) * 128], krT[r0:r0 + 64, t, cs], qrT[r0:r0 + 64, t, cs])
                            ATv = AT.re("p (t hp) i -> p hp t i", hp=2)
                            DTv = DT.re("p (t hp) i -> p hp t i", hp=2)
                            K.tt("dve", ATv[:, 0, :, :], pss[:, 0:256].re("p (t i) -> p t i", t=2), DTv[:, 0, :, :], ALU.mult)
                            K.tt("dve", ATv[:, 1, :, :], pb[:, 0:256].re("p (t i) -> p t i", t=2), DTv[:, 1, :, :], ALU.mult)
                            for h in range(4):
                                t, r0 = h // 2, (h % 2) * 64
                                K.mm(po[:, h, cs], vbtm[:, cj, h * 128:(h + 1) * 128], AT[:, h, :], start=True, stop=False)
                                K.mm(po[:, h, cs], Sf[r0:r0 + 64, t, :], qdf[r0:r0 + 64, t, cs], start=False, stop=False)
                                K.mm(po[:, h, cs], SbAll[r0:r0 + 64, t, n, :], qdb[r0:r0 + 64, t, cs], start=False, stop=True)
                        mb = mixBc[0]
                        for h in range(4):
                            w_ = WS[h % 2]
                            psn_ = bk[h % 3]
                            rsq_bcast(w_["rs"], po[:, h, :], 128.0, w_["sq"], psn_, w_["lnv"], ones)
                            K.tt("dve", w_["t1"], po[:, h, :], w_["rs"], ALU.mult)
                            K.tt("pool", mb[:, h, :], w_["t1"], sg[:, h, :], ALU.mult)
                        K.dma("pool", V(mixb_d.ap[:, :, C * CH:(C + 1) * CH].rearrange("h p c -> p h c"), mixb_d.res), mb, mixB_slot[0])

                tap("mixb", mixb_d, [4, 128, NT], BF16)
                checkpoint("LB")
                LR.close()
                load_w(wbf, wLA_d, 1536, 0)
                load_w(wobf, woab_d, 1024, None)
                with Scope(K) as PA:
                    pbig = ps("pbig", [128, 8, CH], F32, PA)
                    psc = [subview(pbig, pbig.ap[:, 2 * i:2 * i + 2, :]) for i in range(3)]
                    pnum = subview(pbig, pbig.ap[:, 6, :])
                    pden = subview(pbig, pbig.ap[:, 7, :])
                    bk = [subview(pbig, pbig.ap[:, i, :]) for i in range(6)] + [pnum, pden]
                    WS = [dict(sq=sq, lnv=lnv, rs=rs, t1=t1, t2=t2),
                          dict(sq=sb("wsq", [128, CH], BF16, PA), lnv=sb("wlnv", [128, CH], F32, PA),
                               rs=sb("wrs", [128, CH], F32, PA), t1=sb("wt1", [128, CH], F32, PA),
                               t2=sb("wt2", [128, CH], F32, PA))]
                    qaT = sb("qaT", [128, 4, CH], BF16, PA)
                    sga = sb("sga", [128, 4, CH], BF16, PA)
                    mixAs = [sb("mixA%d" % i, [128, 4, CH], BF16, PA) for i in range(2)]
                    mixBls = [sb("mixBl%d" % i, [128, 4, CH], BF16, PA) for i in range(2)]
                    mixBl_slots = [K.slot() for _ in range(2)]
                    xblk = [sb("xblk%d" % i, [128, 1024], F32, PA) for i in range(2)]
                    xblk_slot = [K.slot() for _ in range(2)]
                    nxb = [0]
                    pTs = [sb("pTs%d" % i, [128, 2, CH], BF16, PA) for i in range(3)]
                    x1b = [sb("x1b%d" % i, [128, 1024], F32, PA) for i in range(2)]
                    dcp = sb("dcp", [128, CH], F32, PA)
                    ncp = sb("ncp", [128, CH], F32, PA)
                    x1b_slot = [K.slot() for _ in range(2)]
                    qaTs = [qaT, sb("qaT1", [128, 4, CH], BF16, PA)]
                    sgas = [sga, sb("sga1", [128, 4, CH], BF16, PA)]
                    tabcs = [tabc, sb("tabc1", [128, 2, CH], F32, PA)]
                    tabgs = [tabg, sb("tabg1", [128, 2, CH], F32, PA)]
                    tabsl = [tab_slot, K.slot()]
                    nbuf = [0]
                    npt = [0]

                    held = set()

                    def take_buf():
                        while True:
                            i_ = nbuf[0] % 3
                            nbuf[0] += 1
                            if i_ not in held:
                                return psc[i_]

                    def hold(b_):
                        held.add(psc.index(b_))

                    def release(b_):
                        held.discard(psc.index(b_))

                    def projx(dst, c0, xT):
                        for kc in range(8):
                            K.mm(dst, wbf[:, kc, c0:c0 + 128], xT[:, kc, :], start=(kc == 0), stop=(kc == 7))

                    def proj_items(Cn):
                        q_, g_ = qaTs[Cn % 2], sgas[Cn % 2]
                        tc_, tg_ = tabcs[Cn % 2], tabgs[Cn % 2]
                        xT = xnTs[Cn % 2]

                        def prep():
                            load_tab(tc_, tabsl[Cn % 2], tabA_d, Cn)
                            K.amul(tg_[:, 0, :], tc_[:, 0, :], gqk[:, 0:1])
                            K.amul(tg_[:, 1, :], tc_[:, 1, :], gqk[:, 1:2])

                        items = []
                        for t in range(4):
                            def mk(t=t):
                                st = {}
                                w_ = WS[t % 2]

                                def s1():
                                    st["buf"] = take_buf()
                                    hold(st["buf"])
                                    projx(st["buf"][:, 0, :], t * 128, xT)
                                    projx(st["buf"][:, 1, :], 512 + t * 128, xT)

                                def s2():
                                    pa_, pb_ = st["buf"][:, 0, :], st["buf"][:, 1, :]
                                    K.tt("dve", w_["t1"], pa_, tg_[:, 0, :], ALU.mult)
                                    K.tt("dve", w_["t2"], pb_, tg_[:, 1, :], ALU.mult)
                                    K.act(w_["sq"], pa_, AF.Square)

                                def s3():
                                    K.mm(st["buf"][:, 1, :], onesblk, w_["sq"])

                                def s4():
                                    K.act(w_["lnv"], st["buf"][:, 1, :], AF.Ln, bias=epsb[:, 0:1], scale=1.0 / 64.0)
                                    K.act(w_["rs"], w_["lnv"], AF.Exp, scale=-0.5)
                                    K.tt("pool", w_["t1"], w_["t1"], w_["t2"], ALU.add)
                                    K.tt("pool", q_[:, t, :], w_["t1"], w_["rs"], ALU.mult)
                                    release(st["buf"])
                                return [(s1, 3), (s2, 1), (s3, 2), (s4, 0)]
                            items.append(mk())
                        for t2_ in range(2):
                            def mk(t2_=t2_):
                                st = {}

                                def s1():
                                    st["buf"] = take_buf()
                                    hold(st["buf"])
                                    for j in range(2):
                                        projx(st["buf"][:, j, :], 1024 + (2 * t2_ + j) * 128, xT)

                                def s2():
                                    for j in range(2):
                                        K.act(WS[j]["t1"], st["buf"][:, j, :], AF.Tanh, scale=0.5)

                                def s3():
                                    for j in range(2):
                                        K.ts("dve", WS[j]["t1"], WS[j]["t1"], 0.5, ALU.mult, 0.5, ALU.add)
                                        K.tt("dve", g_[:, 2 * t2_ + j, :], st["buf"][:, j, :], WS[j]["t1"], ALU.mult)
                                    release(st["buf"])
                                return [(s1, 3), (s2, 1), (s3, 0)]
                            items.append(mk())
                        return prep, items

                    def outproj_items(Cc):
                        mA, mB = mixAs[Cc % 2], mixBls[Cc % 2]
                        items = []
                        for b in range(4):
                            def mk(b=b):
                                st = {}
                                bs = slice(b * 128, (b + 1) * 128)
                                r0 = Cc * CH + b * 128

                                def s1():
                                    i_ = nxb[0] % 2
                                    nxb[0] += 1
                                    st["i"] = i_
                                    K.dma("sp", xblk[i_], x_d[r0:r0 + 128, :], xblk_slot[i_])
                                    st["buf"] = take_buf()
                                    hold(st["buf"])
                                    py = st["buf"]
                                    for half in range(2):
                                        for f in range(8):
                                            src = mA[:, f, bs] if f < 4 else mB[:, f - 4, bs]
                                            K.mm(py[:, half, :], src, wobf[:, f, half * 512:(half + 1) * 512], start=(f == 0), stop=(f == 7))

                                def s2():
                                    xo = x1b[st["i"]]
                                    K.tt("dve", xo, st["buf"].re("p a c -> p (a c)"), xblk[st["i"]], ALU.add)
                                    K.dma("pool", x1_d[r0:r0 + 128, :], xo, x1b_slot[st["i"]])
                                    release(st["buf"])
                                return [(s1, 4), (s2, 0)]
                            items.append(mk())
                        return items

                    load_xnT(xnT0_d, 0)
                    prep0, items0 = proj_items(0)
                    prep0()
                    for it_ in items0:
                        for st_fn, _d in it_:
                            st_fn()
                    carry = []
                    for C in range(NCH):
                        use_xnT(C)
                        qaT_c, sga_c = qaTs[C % 2], sgas[C % 2]
                        mixA = mixAs[C % 2]
                        pending = list(carry)
                        carry = []
                        if C + 1 < NCH:
                            load_xnT(xnT0_d, C + 1)
                            prepn, pitems = proj_items(C + 1)
                            prepn()
                            pending = pending + pitems
                        K.dma("pool", mixBls[C % 2], V(mixb_d.ap[:, :, C * CH:(C + 1) * CH].rearrange("h p c -> p h c"), mixb_d.res), mixBl_slots[C % 2])
                        nit = 0
                        active = [None]
                        for t in range(4):
                            kv = t // 2
                            fifo = []

                            def qk(kb_):
                                sc_ = take_buf()
                                fifo.append(sc_)
                                ks = slice(kb_ * 128, (kb_ + 1) * 128)
                                K.mm(sc_[:, 0, :], KaT[0:64, kv, ks], qaT_c[0:64, t, :])
                                K.mm(sc_[:, 1, :], KaT[64:128, kv, ks], qaT_c[64:128, t, :])

                            qk(0)
                            qk(1)
                            for kb in range(NB):
                                sc = fifo.pop(0)
                                pt = pTs[npt[0] % 3]
                                npt[0] += 1
                                nit += 1
                                K.act(pt, sc, AF.Exp, bias=maskA[:, C * NB + kb:C * NB + kb + 1], scale=0.125)
                                if kb + 2 < NB:
                                    qk(kb + 2)
                                if active[0] is None and pending and nit % 12 == 3:
                                    active[0] = [pending.pop(0), 0, nit]
                                if active[0] is not None and nit >= active[0][2]:
                                    stages_, si_, _due = active[0]
                                    fn_, delay_ = stages_[si_]
                                    fn_()
                                    if si_ + 1 < len(stages_):
                                        active[0] = [stages_, si_ + 1, nit + delay_]
                                    else:
                                        active[0] = None
                                st, sp_ = (kb == 0), (kb == NB - 1)
                                K.mm(pnum[0:64, :], Va[:, kb, kv * 64:(kv + 1) * 64], pt[:, 0, :], start=st, stop=sp_)
                                K.mm(pnum[64:128, :], Va[:, kb, kv * 64:(kv + 1) * 64], pt[:, 1, :], start=st, stop=sp_, tp=(0, 64))
                                K.mm(pden[0:64, :], ones[:, 0:64], pt[:, 0, :], start=st, stop=sp_)
                                K.mm(pden[64:128, :], ones[:, 0:64], pt[:, 1, :], start=st, stop=sp_, tp=(0, 64))
                            K.cp("dve", dcp, pden)
                            K.cp("dve", ncp, pnum)
                            K.recip(dcp, dcp)
                            K.tt("dve", ncp, ncp, dcp, ALU.mult)
                            K.tt("pool", mixA[:, t, :], ncp, sga_c[:, t, :], ALU.mult)
                        while active[0] is not None or pending:
                            if active[0] is None:
                                active[0] = [pending.pop(0), 0, 0]
                            stages_, si_, _due = active[0]
                            stages_[si_][0]()
                            active[0] = [stages_, si_ + 1, 0] if si_ + 1 < len(stages_) else None
                        carry = outproj_items(C)
                        if C == NCH - 1:
                            for it_ in carry:
                                for st_fn, _d in it_:
                                    st_fn()
                            carry = []

            tap("x1", x1_d, [NT, 1024], F32)
            checkpoint("LA")
            with Scope(K) as L1:
                KcT = sb("KcT", [128, 2, (NB + 2) * 128], BF16, L1)
                Vc = sb("Vc", [128, NB + 2, 128], BF16, L1)
                K.memset("pool", KcT[:, :, 0:128], 0.0)
                K.memset("pool", KcT[:, :, (NB + 1) * 128:(NB + 2) * 128], 0.0)
                K.memset("pool", Vc[:, 0, :], 0.0)
                K.memset("pool", Vc[:, NB + 1, :], 0.0)
                tap("EBT", EBT, [128, 16, 3, 128], BF16)
                checkpoint("EBT")
                load_w(wbf, wG1_d, 384, 1)
                with Scope(K) as PG1:
                    pT = ps("pT", [128, 8, 128], BF16, PG1)
                    pa = ps("pa", [128, CH], F32, PG1)
                    pva = ps("pva", [128, 512], F32, PG1)[:, 0:128]
                    mxg1 = alloc_mx(PG1)
                    for C in range(NCH):
                        make_xnT(mxg1, x1_d, C, pT)
                        store_xnT(xnT1_d, C, C % 2)
                        for t in range(2):
                            proj(pa, t * 128)
                            K.cp("act", KcT[:, t, (C * 4 + 1) * 128:(C * 4 + 5) * 128], pa)
                        for b in range(4):
                            for kc in range(8):
                                K.mm(pva, cur["xnT"][:, kc, b * 128:(b + 1) * 128], wbf[:, kc, 256:384], start=(kc == 0), stop=(kc == 7))
                            K.cp("dve", Vc[:, C * 4 + b + 1, :], pva)

                tap("KcT", KcT, [128, 2, (NB + 2) * 128], BF16)
                tap("Vc", Vc, [128, NB + 2, 128], BF16)
                checkpoint("G1")
                load_w(wbf, wL1_d, 2048, 1)
                load_w(wobf, woc_d, 1024, None)
                with Scope(K) as PL1:
                    pbig = ps("pbig1", [128, 8, CH], F32, PL1)
                    pw = [subview(pbig, pbig.ap[:, 2 * i:2 * i + 2, :]) for i in range(2)]
                    pnum = subview(pbig, pbig.ap[:, 6, :])
                    pden = subview(pbig, pbig.ap[:, 7, :])
                    bk = [subview(pbig, pbig.ap[:, i, :]) for i in range(6)]
                    qcT = sb("qcT", [128, 8, CH], BF16, PL1)
                    sgc = sb("sgc", [128, 8, CH], BF16, PL1)
                    mixC = sb("mixC", [128, 8, CH], BF16, PL1)
                    pws = [sb("pws%d" % i, [128, 2, 3, 128], BF16, PL1) for i in range(2)]
                    pw2 = [sb("pw2%d" % i, [128, 2, 3, 128], BF16, PL1) for i in range(2)]
                    rs = sb("rs1", [128, CH], F32, PL1)
                    lnr = sb("lnr", [128, CH], F32, PL1)
                    t1 = sb("t11", [128, CH], F32, PL1)
                    x2 = [sb("x2%d" % i, [128, 1024], F32, PL1) for i in range(2)]
                    yo = [sb("yo%d" % i, [128, 1024], F32, PL1) for i in range(2)]
                    yo_slot = [K.slot() for _ in range(2)]
                    ss2 = sb("ss2", [128, 2], F32, PL1)
                    ln2 = sb("ln2", [128, 2], F32, PL1)
                    r2 = sb("r2", [128, 2], F32, PL1)
                    it = 0
                    mxl = alloc_mx(PL1, full=False)
                    xch = mxl.xch
                    junk = sb("junk1", [128, 1024], BF16, PL1)
                    fnbc = sb("fnbc", [128, 1024], F32, PL1)
                    fn_slot = K.slot()
                    K.dma("sp", fnbc, V(fn_d.ap.to_broadcast([128, 1024]), fn_d.res), fn_slot)
                    load_xnT(xnT1_d, 0)
                    for C in range(NCH):
                        use_xnT(C)
                        if C + 1 < NCH:
                            load_xnT(xnT1_d, C + 1)
                        load_x(mxl, x1_d, C)
                        handoff(pw, bk[0:4])
                        for t in range(8):
                            pa_ = bk[t % 6]
                            proj(pa_, t * 128)
                            K.cp("act" if t % 2 == 0 else "dve", qcT[:, t, :], pa_)
                        for t in range(8):
                            pa_ = bk[(t + 2) % 6]
                            proj(pa_, 1024 + t * 128)
                            K.act(sgc[:, t, :], pa_, AF.Silu)
                        handoff(bk[0:4], pw)
                        items = [(t, qi) for t in range(8) for qi in range(4)]

                        def wqk(t, qi, w):
                            kv = t // 4
                            i = C * 4 + qi
                            qs = slice(qi * 128, (qi + 1) * 128)
                            for o in range(3):
                                sl = 2 - o
                                ks = slice((i + o) * 128, (i + o + 1) * 128)
                                K.mm(w[:, 0, sl * 128:(sl + 1) * 128], KcT[0:64, kv, ks], qcT[0:64, t, qs])
                                K.mm(w[:, 1, sl * 128:(sl + 1) * 128], KcT[64:128, kv, ks], qcT[64:128, t, qs])

                        wqk(items[0][0], items[0][1], pw[it % 2])
                        for idx, (t, qi) in enumerate(items):
                            kv = t // 4
                            i = C * 4 + qi
                            qs = slice(qi * 128, (qi + 1) * 128)
                            w = pw[it % 2]
                            s1 = pws[it % 2]
                            s2 = pw2[it % 2]
                            it += 1
                            if i in (0, NB // 2 - 1, NB // 2, NB - 1):
                                for o in range(3):
                                    sl = 2 - o
                                    K.act(s1[:, :, sl, :], w[:, :, sl * 128:(sl + 1) * 128], AF.Exp, bias=maskW[:, i * 3 + o:i * 3 + o + 1], scale=0.125)
                            else:
                                K.act(s1, w[:, :, 0:384].re("p h (o q) -> p h o q", o=3), AF.Exp, scale=0.125)
                            K.tt("dve", s2, s1, EBT[:, 2 * t:2 * t + 2, :, :], ALU.mult)
                            if idx + 1 < len(items):
                                wqk(items[idx + 1][0], items[idx + 1][1], pw[it % 2])
                            for o in range(3):
                                sl = 2 - o
                                st, sp_ = (o == 0), (o == 2)
                                vv = Vc[:, i + o, kv * 64:(kv + 1) * 64]
                                K.mm(pnum[0:64, qs], vv, s2[:, 0, sl, :], start=st, stop=sp_)
                                K.mm(pnum[64:128, qs], vv, s2[:, 1, sl, :], start=st, stop=sp_, tp=(0, 64))
                                K.mm(pden[0:64, qs], ones[:, 0:64], s2[:, 0, sl, :], start=st, stop=sp_)
                                K.mm(pden[64:128, qs], ones[:, 0:64], s2[:, 1, sl, :], start=st, stop=sp_, tp=(0, 64))
                            if qi == 3:
                                K.ts("dve", rs, pden, esk[:, t:t + 1], ALU.add)
                                K.cp("dve", t1, pnum)
                                K.act(lnr, rs, AF.Ln)
                                K.act(rs, lnr, AF.Exp, scale=-1.0)
                                K.tt("pool", t1, t1, rs, ALU.mult)
                                K.tt("pool", mixC[:, t, :], t1, sgc[:, t, :], ALU.mult)
                        for b in range(4):
                            bs = slice(b * 128, (b + 1) * 128)
                            py = pw[b % 2]
                            for half in range(2):
                                for f in range(8):
                                    K.mm(py[:, half, :], mixC[:, f, bs], wobf[:, f, half * 512:(half + 1) * 512], start=(f == 0), stop=(f == 7))
                            xo = x2[b % 2]
                            K.tt("dve", xo, py.re("p a c -> p (a c)"), xch[:, b, :], ALU.add)
                            K.act(junk, xo, AF.Square, accum=ss2[:, b % 2:b % 2 + 1])
                            K.act(ln2[:, b % 2:b % 2 + 1], ss2[:, b % 2:b % 2 + 1], AF.Ln, bias=epsb[:, 0:1], scale=1.0 / 1024.0)
                            K.act(r2[:, b % 2:b % 2 + 1], ln2[:, b % 2:b % 2 + 1], AF.Exp, scale=-0.5)
                            yb = yo[b % 2]
                            K.ts("dve", yb, xo, r2[:, b % 2:b % 2 + 1], ALU.mult)
                            K.tt("pool", yb, yb, fnbc, ALU.mult)
                            r0 = C * CH + b * 128
                            K.dma("pool", y_d[r0:r0 + 128, :], yb, yo_slot[b % 2])
    except StopBuild:
        pass
    for s_ in K.slots:
        if s_.cnt:
            nc.gpsimd.wait_ge(s_.sem, s_.cnt)
    return nc, K


def _t5_bucket(rel):
    half = 16
    max_exact = 8
    ret = (rel > 0).astype(np.int32) * half
    dist = np.abs(rel)
    large = max_exact + (np.log(np.maximum(dist, 1) / max_exact) / np.log(128 / max_exact) * (half - max_exact)).astype(np.int32)
    large = np.minimum(large, half - 1)
    return ret + np.where(dist < max_exact, dist, large)


def _static_tables():
    f32 = np.float32
    st = {}
    st["ident"] = np.eye(128, dtype=f32)
    st["aident"] = np.ascontiguousarray(np.eye(128, dtype=f32)[::-1])
    ob = np.zeros((128, 128), f32)
    ob[:64, :64] = 1
    ob[64:, 64:] = 1
    st["onesblk"] = ob
    j = np.arange(128)[:, None]
    i = np.arange(128)[None, :]
    mmat = np.zeros((128, 4, 128), f32)
    mmat[:, 0, :] = np.maximum(i - j, 0)
    mmat[:, 1, :] = (i >= j)
    mmat[:, 2, :] = np.maximum(j - i, 0)
    mmat[:, 3, :] = (j > i)
    st["mmat"] = mmat
    c = np.arange(512) % 128
    iot = np.zeros((128, 4, 512), f32)
    iot[:, 0, :] = c + 1
    iot[:, 1, :] = 128 - c
    iot[:, 2, :] = 127 - c
    iot[:, 3, :] = c
    st["iot"] = iot
    m = np.arange(640)
    rel = 255 - m
    bk = _t5_bucket(rel)
    oh = np.zeros((32, 640), f32)
    oh[bk, m] = 1
    st["oh"] = oh
    st["inwin"] = np.broadcast_to((np.abs(rel) <= 128).astype(f32)[None, :], (16, 640)).copy()
    return st


def _core_tables(is_prompt):
    f32 = np.float32
    seqlen = 4096 if is_prompt else 2048
    t = np.arange(NT) % seqlen
    d = np.arange(128) % 64
    pair = d // 2
    sgn = np.where(d % 2 == 0, -1.0, 1.0)
    quarter = 16
    freqs = (np.float32(10000.0) ** (-np.arange(quarter, dtype=f32) / quarter)).astype(f32)
    row = (t // 64).astype(f32)
    col = (t % 64).astype(f32)
    ang = np.concatenate([row[:, None] * freqs, col[:, None] * freqs], axis=-1).astype(f32)
    angd = ang[:, pair].T.astype(np.float64)
    tabA = np.stack([np.cos(angd), np.sin(angd) * sgn[:, None]]).astype(f32)
    half = 32
    freqs_b = (np.float32(10000.0) ** (-np.arange(half, dtype=f32) / half)).astype(f32)
    angb = (t.astype(f32)[:, None] * freqs_b).astype(f32)
    angbd = angb[:, pair].T.astype(np.float64)
    tabB = np.stack([np.cos(angbd), np.sin(angbd) * sgn[:, None]]).astype(f32)
    seq_of_blk = (np.arange(NB) * 128) // seqlen
    maskA = np.zeros((NCH, NB), f32)
    for C in range(NCH):
        sq = (C * CH) // seqlen
        maskA[C, :] = np.where(seq_of_blk == sq, 0.0, NEG)
    maskA = np.broadcast_to(maskA.reshape(1, -1), (128, NCH * NB)).copy()
    maskW = np.zeros((NB, 3), f32)
    for i in range(NB):
        for o in range(3):
            jb = i + o - 1
            if jb < 0 or jb >= NB or seq_of_blk[jb] != seq_of_blk[i]:
                maskW[i, o] = NEG
    maskW = np.broadcast_to(maskW.reshape(1, -1), (128, NB * 3)).copy()
    cps = seqlen // 128
    rf = np.array([0.0 if (n % cps == 0) else 1.0 for n in range(NB)], f32)
    rb = np.array([0.0 if (n % cps == cps - 1) else 1.0 for n in range(NB)], f32)
    rfb = np.broadcast_to(np.concatenate([rf, rb])[None, :], (128, 64)).copy()
    return {"tabA": tabA, "tabB": tabB, "maskA": maskA, "maskW": maskW, "rfb": rfb}


def _swap(cols):
    cols = np.asarray(cols)
    return cols ^ 1


def _prep_common(norm_g, w_in_ab, qk_norm_a, ret_decay, w_out_ab, w_in_c, sink_c, w_out_c, rel_bias, final_norm):
    f32 = np.float32
    W = np.asarray(w_in_ab[0], f32)
    qa = np.arange(0, 512)
    ka = np.arange(512, 640)
    va = np.arange(640, 768)
    ga = np.arange(768, 1280)
    qb = np.arange(1280, 1536)
    kb = np.arange(1536, 1792)
    vb = np.arange(1792, 2304)
    gb = np.arange(2304, 2816)
    kadup = np.concatenate([ka[0:64], ka[0:64], ka[64:128], ka[64:128]])
    cm = {}
    cm["wG"] = np.ascontiguousarray(W[:, np.concatenate([kadup, _swap(kadup), kb, _swap(kb), va, vb])])
    cm["wLB"] = np.ascontiguousarray(W[:, np.concatenate([qb, _swap(qb), kb, _swap(kb), gb, vb])])
    cm["wLA"] = np.ascontiguousarray(W[:, np.concatenate([qa, _swap(qa), ga])])
    cm["woab"] = np.ascontiguousarray(np.asarray(w_out_ab[0], f32))
    Wc = np.asarray(w_in_c[0], f32)
    kc = np.arange(1024, 1152)
    kcdup = np.concatenate([kc[0:64], kc[0:64], kc[64:128], kc[64:128]])
    cm["wG1"] = np.ascontiguousarray(Wc[:, np.concatenate([kcdup, np.arange(1152, 1280)])])
    cm["wL1"] = np.ascontiguousarray(Wc[:, np.concatenate([np.arange(0, 1024), np.arange(1280, 2304)])])
    cm["woc"] = np.ascontiguousarray(np.asarray(w_out_c[0], f32))
    ng = np.asarray(norm_g, f32)
    cm["gcol"] = np.ascontiguousarray(ng.reshape(2, 8, 128).transpose(2, 0, 1).reshape(128, 16))
    cm["fn"] = np.asarray(final_norm, f32).reshape(1, 1024).copy()
    g = np.asarray(qk_norm_a[0], f32)
    d = np.arange(128) % 64
    cm["gqk"] = np.stack([g[0][d], g[0][d ^ 1], g[1][d], g[1][d ^ 1]], axis=1).astype(f32).copy()
    rd = np.asarray(ret_decay[0], f32)
    hp = (np.arange(128) // 64)
    rdec = np.zeros((128, 12), f32)
    for p in range(2):
        rdec[:, p] = rd[0][2 * p + hp]
        rdec[:, 2 + p] = rd[1][2 * p + hp]
    for h in range(4):
        rdec[:, 4 + h] = rd[0][h]
        rdec[:, 8 + h] = rd[1][h]
    cm["rdec"] = rdec
    sk = np.asarray(sink_c[0], f32)
    sinkl = np.zeros((128, 8), f32)
    for t in range(8):
        sinkl[:, t] = sk[2 * t + hp]
    cm["sinkl"] = sinkl
    cm["relb"] = np.ascontiguousarray(np.asarray(rel_bias, f32))
    cm.update(_static_tables())
    return cm


_CACHE = {}


def kernel(x_prompt, x_sample, norm_g, w_in_ab, qk_norm_a, ret_decay, w_out_ab, w_in_c, sink_c, w_out_c, rel_bias, final_norm):
    xp = np.asarray(x_prompt, np.float32)
    xs = np.asarray(x_sample, np.float32)
    cm = _prep_common(norm_g, w_in_ab, qk_norm_a, ret_decay, w_out_ab, w_in_c, sink_c, w_out_c, rel_bias, final_norm)
    tp = _core_tables(True)
    tsm = _core_tables(False)
    in_maps = []
    for c in range(8):
        m = dict(cm)
        if c < 4:
            m["x"] = np.ascontiguousarray(xp[c])
            m.update(tp)
        else:
            m["x"] = np.ascontiguousarray(xs[2 * (c - 4):2 * (c - 4) + 2].reshape(NT, 1024))
            m.update(tsm)
        in_maps.append(m)
    if "nc" not in _CACHE:
        _CACHE["nc"] = build_program()[0]
    nc = _CACHE["nc"]
    res = run_bass_kernel_spmd(nc, in_maps, core_ids=list(range(8)))
    outs = [np.asarray(r["y"], np.float32) for r in res.results]
    y_prompt = np.stack(outs[0:4], axis=0)
    y_sample = np.stack(outs[4:8], axis=0).reshape(8, 2048, 1024)
    return (y_prompt, y_sample)
```

```python
import numpy as np
import concourse.bass as bass
import concourse.mybir as mybir
from concourse.bass_utils import run_bass_kernel_spmd

F32 = mybir.dt.float32
BF16 = mybir.dt.bfloat16
AF = mybir.ActivationFunctionType
ALU = mybir.AluOpType

NT = 4096
NB = 32
CH = 512
NCH = 8
EPS = 1e-6
NEG = -30000.0


class Prod:
    def __init__(self, sem, inc):
        self.sem = sem
        self.inc = inc
        self.cnt = 0


class Res:
    def __init__(self):
        self.w = {}
        self.r = {}
        self.excl = False


class V:
    def __init__(self, ap, res=None):
        self.ap = ap
        self.res = res if res is not None else Res()

    def __getitem__(self, k):
        return V(self.ap[k], self.res)

    def re(self, pat, **kw):
        return V(self.ap.rearrange(pat, **kw), self.res)

    def bc(self, shape):
        return V(self.ap.to_broadcast(shape), self.res)


class Ker:
    def __init__(self, nc):
        self.nc = nc
        self.eng = {"pe": nc.tensor, "act": nc.scalar, "dve": nc.vector, "pool": nc.gpsimd, "sp": nc.sync}
        self.prod = {}
        for n in ("pe", "act", "dve", "pool"):
            self.prod[n] = Prod(nc.alloc_semaphore("s_" + n), 1)
        self.seen = {n: {} for n in self.eng}
        self.nslot = 0
        self.ninstr = 0

    def slot(self):
        self.nslot += 1
        p = Prod(self.nc.alloc_semaphore("d%d" % self.nslot), 16)
        if hasattr(self, "slots"):
            self.slots.append(p)
        return p

    def _wait(self, en, reads, writes):
        deps = {}
        for v in reads:
            for p, i in v.res.w.items():
                deps[p] = max(deps.get(p, 0), i)
        for v in writes:
            for p, i in v.res.w.items():
                deps[p] = max(deps.get(p, 0), i)
            for p, i in v.res.r.items():
                deps[p] = max(deps.get(p, 0), i)
        e = self.eng[en]
        seen = self.seen[en]
        own = self.prod.get(en)
        for p, i in deps.items():
            if p is own and en == "pe":
                continue
            if seen.get(p, 0) >= i:
                continue
            e.wait_ge(p.sem, i)
            seen[p] = i

    def op(self, en, fn, reads, writes):
        writes = list(writes) + [r for r in reads if r.res.excl]
        self._wait(en, reads, writes)
        ins = fn(self.eng[en])
        p = self.prod[en]
        p.cnt += 1
        ins.then_inc(p.sem, 1)
        for v in reads:
            v.res.r[p] = p.cnt
        for v in writes:
            v.res.w[p] = p.cnt
        self.ninstr += 1

    def dma(self, q, out, in_, slot):
        self._wait(q, [in_], [out])
        ins = self.eng[q].dma_start(out=out.ap, in_=in_.ap)
        slot.cnt += 16
        ins.then_inc(slot.sem, 16)
        in_.res.r[slot] = slot.cnt
        out.res.w[slot] = slot.cnt

    def mm(self, out, lhsT, rhs, start=True, stop=True, tp=None):
        kw = {}
        if tp is not None:
            kw["tile_position"] = tp
        self.op("pe", lambda e: e.matmul(out.ap, lhsT.ap, rhs.ap, start=start, stop=stop, **kw), [lhsT, rhs], [out])

    def tr(self, out, in_, ident):
        self.op("pe", lambda e: e.transpose(out.ap, in_.ap, ident.ap), [in_, ident], [out])

    def act(self, out, in_, func, bias=None, scale=1.0, accum=None):
        reads = [in_]
        kw = {}
        if bias is not None:
            if isinstance(bias, V):
                reads.append(bias)
                kw["bias"] = bias.ap
            else:
                kw["bias"] = bias
        if isinstance(scale, V):
            reads.append(scale)
            kw["scale"] = scale.ap
        else:
            kw["scale"] = scale
        writes = [out]
        if accum is not None:
            writes.append(accum)
            kw["accum_out"] = accum.ap
        self.op("act", lambda e: e.activation(out.ap, in_.ap, func, **kw), reads, writes)

    def tt(self, en, out, a, b, op):
        self.op(en, lambda e: e.tensor_tensor(out.ap, a.ap, b.ap, op), [a, b], [out])

    def stt(self, en, out, in0, scalar, in1, op0, op1):
        reads = [in0, in1]
        s = scalar
        if isinstance(scalar, V):
            reads.append(scalar)
            s = scalar.ap
        self.op(en, lambda e: e.scalar_tensor_tensor(out.ap, in0.ap, s, in1.ap, op0, op1), reads, [out])

    def ts(self, en, out, in0, s1, op0, s2=None, op1=None):
        reads = [in0]
        a1 = s1
        if isinstance(s1, V):
            reads.append(s1)
            a1 = s1.ap
        a2 = s2
        if isinstance(s2, V):
            reads.append(s2)
            a2 = s2.ap
        if op1 is None:
            self.op(en, lambda e: e.tensor_scalar(out.ap, in0.ap, a1, None, op0), reads, [out])
        else:
            self.op(en, lambda e: e.tensor_scalar(out.ap, in0.ap, a1, a2, op0, op1), reads, [out])

    def cp(self, en, out, in_):
        if en == "act":
            self.op("act", lambda e: e.copy(out.ap, in_.ap), [in_], [out])
        else:
            self.op(en, lambda e: e.tensor_copy(out.ap, in_.ap), [in_], [out])

    def amul(self, out, in_, m):
        self.op("act", lambda e: e.mul(out.ap, in_.ap, m.ap), [in_, m], [out])

    def recip(self, out, in_):
        self.op("dve", lambda e: e.reciprocal(out.ap, in_.ap), [in_], [out])

    def memset(self, en, out, val):
        self.op(en, lambda e: e.memset(out.ap, val), [], [out])


class StopBuild(Exception):
    pass


import contextlib


class Scope(contextlib.ExitStack):
    def __init__(self, K):
        super().__init__()
        self.K = K
        self.tiles = []

    def __exit__(self, *a):
        fr = self.K.freed
        for v in self.tiles:
            for d in (v.res.w, v.res.r):
                for p, i in d.items():
                    fr[p] = max(fr.get(p, 0), i)
        self.tiles = []
        return super().__exit__(*a)

    def close(self):
        self.__exit__(None, None, None)


def build_program(stop=None, taps=()):
    nc = bass.Bass("TRN2", target_bir_lowering=False)
    K = Ker(nc)
    K.slots = []
    K.freed = {}
    K.tapped = {}

    def checkpoint(name):
        if stop == name:
            raise StopBuild()

    def tap(name, v, shape, dt=F32):
        if name not in taps or name in K.tapped:
            return
        d = V(nc.dram_tensor("dbg_" + name, list(shape), dt, kind="ExternalOutput").ap())
        K.tapped[name] = d
        K.dma("sp", d, v, K.slot())

    def din(name, shape, dt=F32):
        return V(nc.dram_tensor(name, list(shape), dt, kind="ExternalInput").ap())

    x_d = din("x", [NT, 1024])
    wG_d = din("wG", [1024, 1664])
    wLB_d = din("wLB", [1024, 2048])
    wLA_d = din("wLA", [1024, 1536])
    woab_d = din("woab", [1024, 1024])
    wG1_d = din("wG1", [1024, 384])
    wL1_d = din("wL1", [1024, 2048])
    woc_d = din("woc", [1024, 1024])
    gcol_d = din("gcol", [128, 16])
    fn_d = din("fn", [1, 1024])
    gqk_d = din("gqk", [128, 4])
    rdec_d = din("rdec", [128, 12])
    sink_d = din("sinkl", [128, 8])
    relb_d = din("relb", [32, 16])
    ident_d = din("ident", [128, 128])
    onesblk_d = din("onesblk", [128, 128])
    mm_d = din("mmat", [128, 4, 128])
    iot_d = din("iot", [128, 4, 512])
    oh_d = din("oh", [32, 640])
    inwin_d = din("inwin", [16, 640])
    tabA_d = din("tabA", [2, 128, NT])
    tabB_d = din("tabB", [2, 128, NT])
    maskA_d = din("maskA", [128, 256])
    maskW_d = din("maskW", [128, 96])
    rfb_d = din("rfb", [128, 64])
    y_d = V(nc.dram_tensor("y", [NT, 1024], F32, kind="ExternalOutput").ap())
    x1_d = V(nc.dram_tensor("x1s", [NT, 1024], F32, kind="Internal").ap())
    mixb_d = V(nc.dram_tensor("mixbs", [4, 128, NT], BF16, kind="Internal").ap())
    vec_h = nc.dram_tensor("vecs", [16, 640], BF16, kind="Internal")
    vec_d = V(vec_h.ap())
    aident_d = din("aident", [128, 128])
    xnT0_d = V(nc.dram_tensor("xnT0s", [NCH, 128, 8, CH], BF16, kind="Internal").ap())
    xnT1_d = V(nc.dram_tensor("xnT1s", [NCH, 128, 8, CH], BF16, kind="Internal").ap())

    es = Scope(K)
    uid = [0]

    def sb(name, shape, dt=F32, stack=None):
        uid[0] += 1
        st_ = stack if stack is not None else es
        t = st_.enter_context(nc.sbuf_tensor("sb%d_%s" % (uid[0], name), list(shape), dt))
        v = V(t[:])
        v.res.w = dict(K.freed)
        st_.tiles.append(v)
        return v

    def ps(name, shape, dt=F32, stack=None):
        uid[0] += 1
        st_ = stack if stack is not None else es
        t = st_.enter_context(nc.psum_tensor("ps%d_%s" % (uid[0], name), list(shape), dt))
        v = V(t[:])
        v.res.excl = True
        v.res.w = dict(K.freed)
        st_.tiles.append(v)
        return v

    try:
        with es:
            cslot = K.slot()
            consts = []

            def cload(name, src, shape, dt=F32, q="sp"):
                t = sb(name, shape, dt)
                K.dma(q, t, src, cslot)
                consts.append(t)
                return t

            gcol = cload("gcol", gcol_d, [128, 16])
            gqk = cload("gqk", gqk_d, [128, 4])
            rdec = cload("rdec", rdec_d, [128, 12])
            sinkl = cload("sinkl", sink_d, [128, 8])
            maskA = cload("maskA", maskA_d, [128, 256])
            maskW = cload("maskW", maskW_d, [128, 96])
            rfb = cload("rfb", rfb_d, [128, 64])
            ident32 = cload("ident32", ident_d, [128, 128])
            onesblk32 = cload("onesblk32", onesblk_d, [128, 128])
            for c in consts:
                c.res.w[cslot] = cslot.cnt
            ident = sb("ident", [128, 128], BF16)
            onesblk = sb("onesblk", [128, 128], BF16)
            ones = sb("ones", [128, 128], BF16)
            epsb = sb("epsb", [128, 1])
            K.cp("dve", ident, ident32)
            K.cp("dve", onesblk, onesblk32)
            K.memset("dve", ones, 1.0)
            K.memset("dve", epsb, EPS)

            checkpoint("c0")
            wbf = sb("wbf", [128, 8, 2048], BF16)
            wobf = sb("wobf", [128, 8, 1024], BF16)
            wst = [sb("wst%d" % i, [128, 1024]) for i in range(2)]
            wst_slot = [K.slot() for _ in range(2)]
            xnTs = [sb("xnT%d" % i, [128, 8, CH], BF16) for i in range(2)]
            xnT_slot = [K.slot() for _ in range(2)]
            cur = {"xnT": xnTs[0]}
            xch_slot = [K.slot() for _ in range(4)]
            xst_slot = [K.slot() for _ in range(2)]
            EBT = sb("EBT", [128, 16, 3, 128], BF16)
            esk = sb("esk", [128, 8], F32)
            K.act(esk, sinkl, AF.Exp)
            with Scope(K) as S1:
                relb = sb("relb", [32, 16], F32, S1)
                oh = sb("oh", [32, 640], F32, S1)
                inw = sb("inw", [16, 640], F32, S1)
                e_slot = K.slot()
                K.dma("sp", relb, relb_d, e_slot)
                K.dma("sp", oh, oh_d, e_slot)
                K.dma("sp", inw, inwin_d, e_slot)
                for t_ in (relb, oh, inw):
                    t_.res.w[e_slot] = e_slot.cnt
                pv = ps("pv", [16, 1024], F32, S1)[:, 0:640]
                vec = sb("vec", [16, 640], F32, S1)
                vecb = sb("vecb", [16, 640], BF16, S1)
                K.mm(pv[:, 0:512], relb, oh[:, 0:512])
                K.mm(pv[:, 512:640], relb, oh[:, 512:640])
                K.act(vec, pv, AF.Exp)
                K.tt("dve", vecb, vec, inw, ALU.mult)
                v_slot = K.slot()
                K.dma("sp", vec_d, vecb, v_slot)
                g_slot = K.slot()
                aid32 = sb("aid32", [128, 128], F32, S1)
                K.dma("sp", aid32, aident_d, g_slot)
                aid = sb("aid", [128, 128], BF16, S1)
                K.cp("dve", aid, aid32)
                TT = sb("TT", [128, 16 * 384], BF16, S1)
                src = V(bass.AP(vec_h, 0, [[1, 128], [640, 16], [1, 384]]), vec_d.res)
                K.dma("sp", TT.re("p (h j) -> p h j", h=16), src, g_slot)
                prev = ps("prev", [128, 2, CH], F32, S1)
                EBTf = EBT.re("p h o q -> p (h o q)")
                for n_ in range(12):
                    K.mm(prev[:, n_ % 2, :], aid, TT[:, n_ * 512:(n_ + 1) * 512])
                    K.cp("act" if n_ % 2 == 0 else "dve", EBTf[:, n_ * 512:(n_ + 1) * 512], prev[:, n_ % 2, :])
            wcount = [0]

            def handoff(srcs, dsts):
                for d_ in dsts:
                    for s_ in srcs:
                        for dd in (s_.res.w, s_.res.r):
                            for p_, i_ in dd.items():
                                d_.res.w[p_] = max(d_.res.w.get(p_, 0), i_)

            def subview(parent, ap):
                v = V(ap)
                v.res.excl = parent.res.excl
                v.res.w = dict(parent.res.w)
                return v

            class MX:
                pass

            def alloc_mx(scope, full=True):
                m = MX()
                m.xch = sb("xch", [128, 4, 1024], F32, scope)
                if full:
                    m.xn = [sb("xn%d" % i, [128, 1024], BF16, scope) for i in range(2)]
                    m.junk = sb("junk", [128, 1024], BF16, scope)
                    m.ss = sb("ss", [128, 4], F32, scope)
                    m.lnv4 = sb("lnv4", [128, 4], F32, scope)
                    m.rstd4 = sb("rstd4", [128, 4], F32, scope)
                return m

            def load_x(m, src_d, C):
                for b in range(4):
                    r0 = C * CH + b * 128
                    K.dma("sp", m.xch[:, b, :], src_d[r0:r0 + 128, :], xch_slot[b])

            def store_xnT(dst_d, C, slot_i):
                K.dma("pool", dst_d[C], cur["xnT"], xst_slot[slot_i])

            def load_xnT(src_d, C):
                i = C % 2
                K.dma("sp", xnTs[i], src_d[C], xnT_slot[i])

            def use_xnT(C):
                cur["xnT"] = xnTs[C % 2]

            def load_w(dst, src_d, ncols, layer_g):
                for kc in range(8):
                    for c0 in range(0, ncols, 1024):
                        c1 = min(ncols, c0 + 1024)
                        i = wcount[0] % 2
                        wcount[0] += 1
                        K.dma("sp", wst[i][:, 0:c1 - c0], src_d[kc * 128:(kc + 1) * 128, c0:c1], wst_slot[i])
                        en = "act" if (wcount[0] % 2 == 0) else "dve"
                        if layer_g is None:
                            K.cp(en, dst[:, kc, c0:c1], wst[i][:, 0:c1 - c0])
                        elif en == "act":
                            K.amul(dst[:, kc, c0:c1], wst[i][:, 0:c1 - c0], gcol[:, layer_g * 8 + kc:layer_g * 8 + kc + 1])
                        else:
                            K.ts("dve", dst[:, kc, c0:c1], wst[i][:, 0:c1 - c0], gcol[:, layer_g * 8 + kc:layer_g * 8 + kc + 1], ALU.mult)

            def make_xnT(m, src_d, C, pT):
                use_xnT(C)
                xnT = cur["xnT"]
                load_x(m, src_d, C)
                for b in range(4):
                    K.act(m.junk, m.xch[:, b, :], AF.Square, accum=m.ss[:, b:b + 1])
                K.act(m.lnv4, m.ss, AF.Ln, bias=epsb[:, 0:1], scale=1.0 / 1024.0)
                K.act(m.rstd4, m.lnv4, AF.Exp, scale=-0.5)
                for b in range(4):
                    xb = m.xn[b % 2]
                    K.ts("dve", xb, m.xch[:, b, :], m.rstd4[:, b:b + 1], ALU.mult)
                    for kc in range(8):
                        K.tr(pT[:, kc, :], xb[:, kc * 128:(kc + 1) * 128], ident)
                    K.cp("act", xnT[:, :, b * 128:(b + 1) * 128], pT)

            def proj(dst, c0):
                for kc in range(8):
                    K.mm(dst, wbf[:, kc, c0:c0 + 128], cur["xnT"][:, kc, :], start=(kc == 0), stop=(kc == 7))

            def rsq_bcast(dst, src_ps, nfeat, sq, psn, lnv, lhs_ones):
                K.act(sq, src_ps, AF.Square)
                K.mm(psn, lhs_ones, sq)
                K.act(lnv, psn, AF.Ln, bias=epsb[:, 0:1], scale=1.0 / nfeat)
                K.act(dst, lnv, AF.Exp, scale=-0.5)

            with Scope(K) as L0:
                LR = Scope(K)
                KaT = sb("KaT", [128, 2, NT], BF16, L0)
                Va = sb("Va", [128, NB, 128], BF16, L0)
                tabc = sb("tabc", [128, 2, CH], F32, L0)
                tab_slot = K.slot()
                tabd_slot = K.slot()
                sq = sb("sq", [128, CH], BF16, L0)
                lnv = sb("lnv", [128, CH], F32, L0)
                rs = sb("rs", [128, CH], F32, L0)
                t1 = sb("t1", [128, CH], F32, L0)
                t2 = sb("t2", [128, CH], F32, L0)
                tabg = sb("tabg", [128, 2, CH], F32, L0)
                SbAll = sb("SbAll", [128, 2, NB, 128], BF16, LR)
                tabd = sb("tabd", [128, 2, CH], F32, LR)
                vbtm = sb("vbtm", [128, 4, 512], BF16, LR)
                lg = sb("lg", [128, 12], F32, LR)
                K.act(lg, rdec, AF.Exp)
                K.ts("dve", lg, lg, -1.0, ALU.mult)
                cd = sb("cd", [128, 4], F32, LR)
                K.act(cd, lg[:, 0:4], AF.Exp, scale=128.0)
                cdr = sb("cdr", [128, 4, NB], F32, LR)
                for j in range(4):
                    off = 0 if j < 2 else 32
                    K.ts("dve", cdr[:, j, :], rfb[:, off:off + 32], cd[:, j:j + 1], ALU.mult)
                checkpoint("c1")
                QF4 = sb("QF4", [128, 2, CH], F32, LR)
                QB4 = sb("QB4", [128, 2, CH], F32, LR)
                KF4 = sb("KF4", [128, 2, CH], F32, LR)
                KB4 = sb("KB4", [128, 2, CH], F32, LR)
                DT = sb("DT", [128, 4, 128], F32, LR)
                with Scope(K) as S0:
                    iot = sb("iot", [128, 4, CH], F32, S0)
                    K.dma("sp", iot, iot_d, tabd_slot)
                    for p in range(2):
                        K.act(QF4[:, p, :], iot[:, 0, :], AF.Exp, scale=lg[:, p:p + 1])
                        K.act(QB4[:, p, :], iot[:, 1, :], AF.Exp, scale=lg[:, 2 + p:3 + p])
                        K.act(KF4[:, p, :], iot[:, 2, :], AF.Exp, scale=lg[:, p:p + 1])
                        K.act(KB4[:, p, :], iot[:, 3, :], AF.Exp, scale=lg[:, 2 + p:3 + p])
                    K.ts("dve", KF4, KF4, 0.125, ALU.mult)
                    K.ts("dve", KB4, KB4, 0.125, ALU.mult)
                    mmat = sb("mmat", [128, 4, 128], F32, S0)
                    K.dma("sp", mmat, mm_d, tab_slot)
                    d1 = sb("d1", [128, 128], F32, S0)
                    d2 = sb("d2", [128, 128], F32, S0)
                    for h in range(4):
                        checkpoint("d0")
                        K.act(d1, mmat[:, 0, :], AF.Exp, scale=lg[:, 4 + h:5 + h])
                        checkpoint("d1")
                        K.tt("dve", d1, d1, mmat[:, 1, :], ALU.mult)
                        checkpoint("d2")
                        K.act(d2, mmat[:, 2, :], AF.Exp, scale=lg[:, 8 + h:9 + h])
                        K.tt("dve", d2, d2, mmat[:, 3, :], ALU.mult)
                        K.tt("dve", d1, d1, d2, ALU.add)
                        checkpoint("d3")
                        K.ts("dve", DT[:, h, :], d1, 0.125, ALU.mult)
                        checkpoint("d4")

                def load_tab(dst, slot, src_d, C):
                    K.dma("sp", dst, V(src_d.ap[:, :, C * CH:(C + 1) * CH].rearrange("t p c -> p t c"), src_d.res), slot)

                def rope(psa, psb, tab, out32, ga=None, gb=None):
                    if ga is None:
                        K.tt("dve", t1, psa, tab[:, 0, :], ALU.mult)
                        K.tt("dve", t2, psb, tab[:, 1, :], ALU.mult)
                    else:
                        K.amul(tabg[:, 0, :], tab[:, 0, :], ga)
                        K.amul(tabg[:, 1, :], tab[:, 1, :], gb)
                        K.tt("dve", t1, psa, tabg[:, 0, :], ALU.mult)
                        K.tt("dve", t2, psb, tabg[:, 1, :], ALU.mult)
                    K.tt("pool", out32, t1, t2, ALU.add)

                checkpoint("setup0")
                load_w(wbf, wG_d, 1664, 0)
                with Scope(K) as PG:
                    pT = ps("pT", [128, 8, 128], BF16, PG)
                    pk = pT.re("p (c t) q -> p c t q", t=2)
                    pbig = ps("pbigG", [128, 7, CH], F32, PG)
                    bk = [subview(pbig, pbig.ap[:, i, :]) for i in range(7)]
                    pn = bk[4]
                    pkv = V(bk[6].ap[:, 0:256].rearrange("p (a b) -> p a b", a=2), bk[6].res)
                    kdbT = sb("kdbT", [128, 2, CH], BF16, PG)
                    kdbtm = sb("kdbtm", [128, 4, 2, 128], BF16, PG)
                    Rb = sb("Rb", [128, 2, 128], F32, PG)
                    mxg = alloc_mx(PG)
                    WS = [dict(sq=sq, lnv=lnv, rs=rs, t1=t1, t2=t2),
                          dict(sq=sb("wsq", [128, CH], BF16, PG), lnv=sb("wlnv", [128, CH], F32, PG),
                               rs=sb("wrs", [128, CH], F32, PG), t1=sb("wt1", [128, CH], F32, PG),
                               t2=sb("wt2", [128, CH], F32, PG))]
                    K.memset("dve", Rb, 0.0)
                    for C in range(NCH - 1, -1, -1):
                        make_xnT(mxg, x_d, C, pT)
                        store_xnT(xnT0_d, C, C % 2)
                        load_tab(tabc, tab_slot, tabA_d, C)
                        load_tab(tabd, tabd_slot, tabB_d, C)
                        K.amul(tabg[:, 0, :], tabc[:, 0, :], gqk[:, 2:3])
                        K.amul(tabg[:, 1, :], tabc[:, 1, :], gqk[:, 3:4])
                        for t in range(2):
                            w_ = WS[t % 2]
                            pa_, pb_ = bk[2 * t], bk[2 * t + 1]
                            proj(pa_, t * 128)
                            proj(pb_, 256 + t * 128)
                            rsq_bcast(w_["rs"], pa_, 64.0, w_["sq"], pn, w_["lnv"], onesblk)
                            K.tt("dve", w_["t1"], pa_, tabg[:, 0, :], ALU.mult)
                            K.tt("dve", w_["t2"], pb_, tabg[:, 1, :], ALU.mult)
                            K.tt("pool", w_["t1"], w_["t1"], w_["t2"], ALU.add)
                            K.tt("pool", KaT[:, t, C * CH:(C + 1) * CH], w_["t1"], w_["rs"], ALU.mult)
                        for t in range(2):
                            w_ = WS[t % 2]
                            pa_, pb_ = bk[2 * t], bk[2 * t + 1]
                            proj(pa_, 512 + t * 128)
                            proj(pb_, 768 + t * 128)
                            K.tt("dve", w_["t1"], pa_, tabd[:, 0, :], ALU.mult)
                            K.tt("dve", w_["t2"], pb_, tabd[:, 1, :], ALU.mult)
                            K.tt("pool", w_["t1"], w_["t1"], w_["t2"], ALU.add)
                            K.tt("pool", kdbT[:, t, :], w_["t1"], KB4[:, t, :], ALU.mult)
                            for cj in range(4):
                                K.tr(pk[:, cj, t, :], kdbT[:, t, cj * 128:(cj + 1) * 128], ident)
                        K.cp("act", kdbtm, pk)
                        for b in range(4):
                            pva = (bk[4] if b % 2 == 0 else bk[2])[:, 0:128]
                            pvb = bk[5] if b % 2 == 0 else bk[3]
                            for kc in range(8):
                                K.mm(pva, cur["xnT"][:, kc, b * 128:(b + 1) * 128], wbf[:, kc, 1024:1152], start=(kc == 0), stop=(kc == 7))
                            for kc in range(8):
                                K.mm(pvb, cur["xnT"][:, kc, b * 128:(b + 1) * 128], wbf[:, kc, 1152:1664], start=(kc == 0), stop=(kc == 7))
                            K.cp("act", Va[:, C * 4 + b, :], pva)
                            K.cp("dve", vbtm[:, b, :], pvb)
                        for cj in range(3, -1, -1):
                            n = C * 4 + cj
                            for p in range(2):
                                K.mm(pkv[0:64, p, :], kdbtm[:, cj, p, 0:64], vbtm[:, cj, (2 * p) * 128:(2 * p + 1) * 128])
                                K.mm(pkv[64:128, p, :], kdbtm[:, cj, p, 64:128], vbtm[:, cj, (2 * p + 1) * 128:(2 * p + 2) * 128], tp=(0, 64))
                            K.ts("dve", SbAll[:, :, n, :], Rb, rfb[:, 32 + n:33 + n], ALU.mult)
                            for p in range(2):
                                K.ts("dve", Rb[:, p, :], Rb[:, p, :], cdr[:, 2 + p, n:n + 1], ALU.mult)
                                K.tt("dve", Rb[:, p, :], pkv[:, p, :], Rb[:, p, :], ALU.add)

                tap("KaT", KaT, [128, 2, NT], BF16)
                tap("Va", Va, [128, NB, 128], BF16)
                tap("SbAll", SbAll, [128, 2, NB, 128], BF16)
                checkpoint("G")
                load_w(wbf, wLB_d, 2048, 0)
                with Scope(K) as PB:
                    pT = ps("pT", [128, 8, 128], BF16, PB)
                    pk = pT.re("p (c t) q -> p c t q", t=2)
                    pbig = ps("pbigB", [128, 7, CH], F32, PB)
                    bk = [subview(pbig, pbig.ap[:, i, :]) for i in range(7)]
                    pa, pb, pss = bk[0], bk[1], bk[2]
                    po = subview(pbig, pbig.ap[:, 3:7, :])
                    qrT = sb("qrT", [128, 2, CH], BF16, PB)
                    qdf = sb("qdf", [128, 2, CH], BF16, PB)
                    qdb = sb("qdb", [128, 2, CH], BF16, PB)
                    krT = sb("krT", [128, 2, CH], BF16, PB)
                    kdfT = sb("kdfT", [128, 2, CH], BF16, PB)
                    kdftm = sb("kdftm", [128, 4, 2, 128], BF16, PB)
                    sg = sb("sg", [128, 4, CH], BF16, PB)
                    ATs = [sb("AT%d" % i, [128, 4, 128], BF16, PB) for i in range(2)]
                    Sfs = [sb("Sf%d" % i, [128, 2, 128], BF16, PB) for i in range(2)]
                    Rf = sb("Rf", [128, 2, 128], F32, PB)
                    mixBc = [sb("mixBc%d" % i, [128, 4, CH], BF16, PB) for i in range(1)]
                    mixB_slot = [K.slot() for _ in range(1)]
                    WS = [dict(sq=sq, lnv=lnv, rs=rs, t1=t1, t2=t2),
                          dict(sq=sb("wsq", [128, CH], BF16, PB), lnv=sb("wlnv", [128, CH], F32, PB),
                               rs=sb("wrs", [128, CH], F32, PB), t1=sb("wt1", [128, CH], F32, PB),
                               t2=sb("wt2", [128, CH], F32, PB))]
                    K.memset("dve", Rf, 0.0)
                    load_xnT(xnT0_d, 0)
                    pairs = [(bk[0], bk[1]), (bk[3], bk[4]), (bk[5], bk[6])]
                    for C in range(NCH):
                        use_xnT(C)
                        if C + 1 < NCH:
                            load_xnT(xnT0_d, C + 1)
                        load_tab(tabd, tabd_slot, tabB_d, C)
                        handoff([po], bk[3:7])
                        ip = 0
                        for t in range(2):
                            w_ = WS[ip % 2]
                            pa_, pb_ = pairs[ip % 3]
                            ip += 1
                            proj(pa_, t * 128)
                            proj(pb_, 256 + t * 128)
                            K.tt("dve", w_["t1"], pa_, tabd[:, 0, :], ALU.mult)
                            K.tt("dve", w_["t2"], pb_, tabd[:, 1, :], ALU.mult)
                            K.tt("pool", w_["t1"], w_["t1"], w_["t2"], ALU.add)
                            K.cp("act", qrT[:, t, :], w_["t1"])
                            K.tt("pool", qdf[:, t, :], w_["t1"], QF4[:, t, :], ALU.mult)
                            K.tt("pool", qdb[:, t, :], w_["t1"], QB4[:, t, :], ALU.mult)
                        for t in range(2):
                            w_ = WS[ip % 2]
                            pa_, pb_ = pairs[ip % 3]
                            ip += 1
                            proj(pa_, 512 + t * 128)
                            proj(pb_, 768 + t * 128)
                            K.tt("dve", w_["t1"], pa_, tabd[:, 0, :], ALU.mult)
                            K.tt("dve", w_["t2"], pb_, tabd[:, 1, :], ALU.mult)
                            K.tt("pool", w_["t1"], w_["t1"], w_["t2"], ALU.add)
                            K.cp("act", krT[:, t, :], w_["t1"])
                            K.tt("pool", kdfT[:, t, :], w_["t1"], KF4[:, t, :], ALU.mult)
                            for cj in range(4):
                                K.tr(pk[:, cj, t, :], kdfT[:, t, cj * 128:(cj + 1) * 128], ident)
                        K.cp("act", kdftm, pk)
                        for h in range(4):
                            pa_ = bk[3 + h]
                            proj(pa_, 1024 + h * 128)
                            K.act(sg[:, h, :], pa_, AF.Silu)
                        for b in range(4):
                            pv_ = bk[1 + b % 2]
                            for kc in range(8):
                                K.mm(pv_, cur["xnT"][:, kc, b * 128:(b + 1) * 128], wbf[:, kc, 1536:2048], start=(kc == 0), stop=(kc == 7))
                            K.cp("dve", vbtm[:, b, :], pv_)
                        handoff(bk[3:7], [po])
                        for cj in range(4):
                            n = C * 4 + cj
                            cs = slice(cj * 128, (cj + 1) * 128)
                            Sf = Sfs[cj % 2]
                            AT = ATs[cj % 2]
                            K.ts("dve", Sf, Rf, rfb[:, n:n + 1], ALU.mult)
                            for p in range(2):
                                K.mm(pa[0:64, p * 128:(p + 1) * 128], kdftm[:, cj, p, 0:64], vbtm[:, cj, (2 * p) * 128:(2 * p + 1) * 128])
                                K.mm(pa[64:128, p * 128:(p + 1) * 128], kdftm[:, cj, p, 64:128], vbtm[:, cj, (2 * p + 1) * 128:(2 * p + 2) * 128], tp=(0, 64))
                            for p in range(2):
                                K.ts("dve", Rf[:, p, :], Rf[:, p, :], cdr[:, p, n:n + 1], ALU.mult)
                                K.tt("dve", Rf[:, p, :], pa[:, p * 128:(p + 1) * 128], Rf[:, p, :], ALU.add)
                            for h in range(4):
                                t, r0 = h // 2, (h % 2) * 64
                                pdst = pss if (h % 2 == 0) else pb
                                K.mm(pdst[:, t * 128:(t + 1) * 128], krT[r0:r0 + 64, t, cs], qrT[r0:r0 + 64, t, cs])
                            ATv = AT.re("p (t hp) i -> p hp t i", hp=2)
                            DTv = DT.re("p (t hp) i -> p hp t i", hp=2)
                            K.tt("dve", ATv[:, 0, :, :], pss[:, 0:256].re("p (t i) -> p t i", t=2), DTv[:, 0, :, :], ALU.mult)
                            K.tt("dve", ATv[:, 1, :, :], pb[:, 0:256].re("p (t i) -> p t i", t=2), DTv[:, 1, :, :], ALU.mult)
                            for h in range(4):
                                t, r0 = h // 2, (h % 2) * 64
                                K.mm(po[:, h, cs], vbtm[:, cj, h * 128:(h + 1) * 128], AT[:, h, :], start=True, stop=False)
                                K.mm(po[:, h, cs], Sf[r0:r0 + 64, t, :], qdf[r0:r0 + 64, t, cs], start=False, stop=False)
                                K.mm(po[:, h, cs], SbAll[r0:r0 + 64, t, n, :], qdb[r0:r0 + 64, t, cs], start=False, stop=True)
                        mb = mixBc[0]
                        for h0 in (0, 2):
                            hs = (h0, h0 + 1)
                            for h in hs:
                                K.act(WS[h % 2]["sq"], po[:, h, :], AF.Square)
                            for h in hs:
                                K.mm(bk[h % 2], ones, WS[h % 2]["sq"])
                            for h in hs:
                                K.act(WS[h % 2]["lnv"], bk[h % 2], AF.Ln, bias=epsb[:, 0:1], scale=1.0 / 128.0)
                            for h in hs:
                                K.act(WS[h % 2]["rs"], WS[h % 2]["lnv"], AF.Exp, scale=-0.5)
                            for h in hs:
                                K.tt("dve", WS[h % 2]["t1"], po[:, h, :], WS[h % 2]["rs"], ALU.mult)
                                K.tt("pool", mb[:, h, :], WS[h % 2]["t1"], sg[:, h, :], ALU.mult)
                        K.dma("pool", V(mixb_d.ap[:, :, C * CH:(C + 1) * CH].rearrange("h p c -> p h c"), mixb_d.res), mb, mixB_slot[0])

                tap("mixb", mixb_d, [4, 128, NT], BF16)
                checkpoint("LB")
                LR.close()
                load_w(wbf, wLA_d, 1536, 0)
                load_w(wobf, woab_d, 1024, None)
                with Scope(K) as PA:
                    pbig = ps("pbig", [128, 8, CH], F32, PA)
                    psc = [subview(pbig, pbig.ap[:, 2 * i:2 * i + 2, :]) for i in range(3)]
                    pnum = subview(pbig, pbig.ap[:, 6, :])
                    pden = subview(pbig, pbig.ap[:, 7, :])
                    bk = [subview(pbig, pbig.ap[:, i, :]) for i in range(6)] + [pnum, pden]
                    WS = [dict(sq=sq, lnv=lnv, rs=rs, t1=t1, t2=t2),
                          dict(sq=sb("wsq", [128, CH], BF16, PA), lnv=sb("wlnv", [128, CH], F32, PA),
                               rs=sb("wrs", [128, CH], F32, PA), t1=sb("wt1", [128, CH], F32, PA),
                               t2=sb("wt2", [128, CH], F32, PA))]
                    qaT = sb("qaT", [128, 4, CH], BF16, PA)
                    sga = sb("sga", [128, 4, CH], BF16, PA)
                    mixAs = [sb("mixA%d" % i, [128, 4, CH], BF16, PA) for i in range(2)]
                    mixBls = [sb("mixBl%d" % i, [128, 4, CH], BF16, PA) for i in range(2)]
                    mixBl_slots = [K.slot() for _ in range(2)]
                    xblk = [sb("xblk%d" % i, [128, 1024], F32, PA) for i in range(2)]
                    xblk_slot = [K.slot() for _ in range(2)]
                    nxb = [0]
                    pTs = [sb("pTs%d" % i, [128, 2, CH], BF16, PA) for i in range(3)]
                    x1b = [sb("x1b%d" % i, [128, 1024], F32, PA) for i in range(2)]
                    dcp = sb("dcp", [128, CH], F32, PA)
                    ncp = sb("ncp", [128, CH], F32, PA)
                    x1b_slot = [K.slot() for _ in range(2)]
                    qaTs = [qaT, sb("qaT1", [128, 4, CH], BF16, PA)]
                    sgas = [sga, sb("sga1", [128, 4, CH], BF16, PA)]
                    tabcs = [tabc, sb("tabc1", [128, 2, CH], F32, PA)]
                    tabgs = [tabg, sb("tabg1", [128, 2, CH], F32, PA)]
                    tabsl = [tab_slot, K.slot()]
                    nbuf = [0]
                    npt = [0]

                    held = set()

                    def take_buf():
                        while True:
                            i_ = nbuf[0] % 3
                            nbuf[0] += 1
                            if i_ not in held:
                                return psc[i_]

                    def hold(b_):
                        held.add(psc.index(b_))

                    def release(b_):
                        held.discard(psc.index(b_))

                    def projx(dst, c0, xT):
                        for kc in range(8):
                            K.mm(dst, wbf[:, kc, c0:c0 + 128], xT[:, kc, :], start=(kc == 0), stop=(kc == 7))

                    def proj_items(Cn):
                        q_, g_ = qaTs[Cn % 2], sgas[Cn % 2]
                        tc_, tg_ = tabcs[Cn % 2], tabgs[Cn % 2]
                        xT = xnTs[Cn % 2]

                        def prep():
                            load_tab(tc_, tabsl[Cn % 2], tabA_d, Cn)
                            K.amul(tg_[:, 0, :], tc_[:, 0, :], gqk[:, 0:1])
                            K.amul(tg_[:, 1, :], tc_[:, 1, :], gqk[:, 1:2])

                        items = []
                        for t in range(4):
                            def mk(t=t):
                                st = {}
                                w_ = WS[t % 2]

                                def s1():
                                    st["buf"] = take_buf()
                                    hold(st["buf"])
                                    projx(st["buf"][:, 0, :], t * 128, xT)
                                    projx(st["buf"][:, 1, :], 512 + t * 128, xT)

                                def s2():
                                    pa_, pb_ = st["buf"][:, 0, :], st["buf"][:, 1, :]
                                    K.tt("dve", w_["t1"], pa_, tg_[:, 0, :], ALU.mult)
                                    K.tt("dve", w_["t2"], pb_, tg_[:, 1, :], ALU.mult)
                                    K.act(w_["sq"], pa_, AF.Square)

                                def s3():
                                    K.mm(st["buf"][:, 1, :], onesblk, w_["sq"])

                                def s4():
                                    K.act(w_["lnv"], st["buf"][:, 1, :], AF.Ln, bias=epsb[:, 0:1], scale=1.0 / 64.0)
                                    K.act(w_["rs"], w_["lnv"], AF.Exp, scale=-0.5)
                                    K.tt("pool", w_["t1"], w_["t1"], w_["t2"], ALU.add)
                                    K.tt("pool", q_[:, t, :], w_["t1"], w_["rs"], ALU.mult)
                                    release(st["buf"])
                                return [(s1, 3), (s2, 1), (s3, 2), (s4, 0)]
                            items.append(mk())
                        for t2_ in range(2):
                            def mk(t2_=t2_):
                                st = {}

                                def s1():
                                    st["buf"] = take_buf()
                                    hold(st["buf"])
                                    for j in range(2):
                                        projx(st["buf"][:, j, :], 1024 + (2 * t2_ + j) * 128, xT)

                                def s2():
                                    for j in range(2):
                                        K.act(WS[j]["t1"], st["buf"][:, j, :], AF.Tanh, scale=0.5)

                                def s3():
                                    for j in range(2):
                                        K.ts("dve", WS[j]["t1"], WS[j]["t1"], 0.5, ALU.mult, 0.5, ALU.add)
                                        K.tt("dve", g_[:, 2 * t2_ + j, :], st["buf"][:, j, :], WS[j]["t1"], ALU.mult)
                                    release(st["buf"])
                                return [(s1, 3), (s2, 1), (s3, 0)]
                            items.append(mk())
                        return prep, items

                    def outproj_items(Cc):
                        mA, mB = mixAs[Cc % 2], mixBls[Cc % 2]
                        items = []
                        for b in range(4):
                            def mk(b=b):
                                st = {}
                                bs = slice(b * 128, (b + 1) * 128)
                                r0 = Cc * CH + b * 128

                                def s1():
                                    i_ = nxb[0] % 2
                                    nxb[0] += 1
                                    st["i"] = i_
                                    K.dma("sp", xblk[i_], x_d[r0:r0 + 128, :], xblk_slot[i_])
                                    st["buf"] = take_buf()
                                    hold(st["buf"])
                                    py = st["buf"]
                                    for half in range(2):
                                        for f in range(8):
                                            src = mA[:, f, bs] if f < 4 else mB[:, f - 4, bs]
                                            K.mm(py[:, half, :], src, wobf[:, f, half * 512:(half + 1) * 512], start=(f == 0), stop=(f == 7))

                                def s2():
                                    xo = x1b[st["i"]]
                                    K.tt("dve", xo, st["buf"].re("p a c -> p (a c)"), xblk[st["i"]], ALU.add)
                                    K.dma("pool", x1_d[r0:r0 + 128, :], xo, x1b_slot[st["i"]])
                                    release(st["buf"])
                                return [(s1, 4), (s2, 0)]
                            items.append(mk())
                        return items

                    load_xnT(xnT0_d, 0)
                    prep0, items0 = proj_items(0)
                    prep0()
                    for it_ in items0:
                        for st_fn, _d in it_:
                            st_fn()
                    carry = []
                    for C in range(NCH):
                        use_xnT(C)
                        qaT_c, sga_c = qaTs[C % 2], sgas[C % 2]
                        mixA = mixAs[C % 2]
                        pending = list(carry)
                        carry = []
                        if C + 1 < NCH:
                            load_xnT(xnT0_d, C + 1)
                            prepn, pitems = proj_items(C + 1)
                            prepn()
                            pending = pending + pitems
                        K.dma("pool", mixBls[C % 2], V(mixb_d.ap[:, :, C * CH:(C + 1) * CH].rearrange("h p c -> p h c"), mixb_d.res), mixBl_slots[C % 2])
                        nit = 0
                        active = [None]
                        for t in range(4):
                            kv = t // 2
                            fifo = []

                            def qk(kb_):
                                sc_ = take_buf()
                                fifo.append(sc_)
                                ks = slice(kb_ * 128, (kb_ + 1) * 128)
                                K.mm(sc_[:, 0, :], KaT[0:64, kv, ks], qaT_c[0:64, t, :])
                                K.mm(sc_[:, 1, :], KaT[64:128, kv, ks], qaT_c[64:128, t, :])

                            qk(0)
                            qk(1)
                            for kb in range(NB):
                                sc = fifo.pop(0)
                                pt = pTs[npt[0] % 3]
                                npt[0] += 1
                                nit += 1
                                K.act(pt, sc, AF.Exp, bias=maskA[:, C * NB + kb:C * NB + kb + 1], scale=0.125)
                                if kb + 2 < NB:
                                    qk(kb + 2)
                                if active[0] is None and pending and nit % 12 == 3:
                                    active[0] = [pending.pop(0), 0, nit]
                                if active[0] is not None and nit >= active[0][2]:
                                    stages_, si_, _due = active[0]
                                    fn_, delay_ = stages_[si_]
                                    fn_()
                                    if si_ + 1 < len(stages_):
                                        active[0] = [stages_, si_ + 1, nit + delay_]
                                    else:
                                        active[0] = None
                                st, sp_ = (kb == 0), (kb == NB - 1)
                                K.mm(pnum[0:64, :], Va[:, kb, kv * 64:(kv + 1) * 64], pt[:, 0, :], start=st, stop=sp_)
                                K.mm(pnum[64:128, :], Va[:, kb, kv * 64:(kv + 1) * 64], pt[:, 1, :], start=st, stop=sp_, tp=(0, 64))
                                K.mm(pden[0:64, :], ones[:, 0:64], pt[:, 0, :], start=st, stop=sp_)
                                K.mm(pden[64:128, :], ones[:, 0:64], pt[:, 1, :], start=st, stop=sp_, tp=(0, 64))
                            K.cp("dve", dcp, pden)
                            K.cp("dve", ncp, pnum)
                            K.recip(dcp, dcp)
                            K.tt("dve", ncp, ncp, dcp, ALU.mult)
                            K.tt("pool", mixA[:, t, :], ncp, sga_c[:, t, :], ALU.mult)
                        while active[0] is not None or pending:
                            if active[0] is None:
                                active[0] = [pending.pop(0), 0, 0]
                            stages_, si_, _due = active[0]
                            stages_[si_][0]()
                            active[0] = [stages_, si_ + 1, 0] if si_ + 1 < len(stages_) else None
                        carry = outproj_items(C)
                        if C == NCH - 1:
                            for it_ in carry:
                                for st_fn, _d in it_:
                                    st_fn()
                            carry = []

            tap("x1", x1_d, [NT, 1024], F32)
            checkpoint("LA")
            with Scope(K) as L1:
                KcT = sb("KcT", [128, 2, (NB + 2) * 128], BF16, L1)
                Vc = sb("Vc", [128, NB + 2, 128], BF16, L1)
                K.memset("pool", KcT[:, :, 0:128], 0.0)
                K.memset("pool", KcT[:, :, (NB + 1) * 128:(NB + 2) * 128], 0.0)
                K.memset("pool", Vc[:, 0, :], 0.0)
                K.memset("pool", Vc[:, NB + 1, :], 0.0)
                tap("EBT", EBT, [128, 16, 3, 128], BF16)
                checkpoint("EBT")
                load_w(wbf, wG1_d, 384, 1)
                with Scope(K) as PG1:
                    pT = ps("pT", [128, 8, 128], BF16, PG1)
                    pa = ps("pa", [128, CH], F32, PG1)
                    pva = ps("pva", [128, 512], F32, PG1)[:, 0:128]
                    mxg1 = alloc_mx(PG1)
                    for C in range(NCH):
                        make_xnT(mxg1, x1_d, C, pT)
                        store_xnT(xnT1_d, C, C % 2)
                        for t in range(2):
                            proj(pa, t * 128)
                            K.cp("act", KcT[:, t, (C * 4 + 1) * 128:(C * 4 + 5) * 128], pa)
                        for b in range(4):
                            for kc in range(8):
                                K.mm(pva, cur["xnT"][:, kc, b * 128:(b + 1) * 128], wbf[:, kc, 256:384], start=(kc == 0), stop=(kc == 7))
                            K.cp("dve", Vc[:, C * 4 + b + 1, :], pva)

                tap("KcT", KcT, [128, 2, (NB + 2) * 128], BF16)
                tap("Vc", Vc, [128, NB + 2, 128], BF16)
                checkpoint("G1")
                load_w(wbf, wL1_d, 2048, 1)
                load_w(wobf, woc_d, 1024, None)
                with Scope(K) as PL1:
                    pbig = ps("pbig1", [128, 8, CH], F32, PL1)
                    pw = [subview(pbig, pbig.ap[:, 2 * i:2 * i + 2, :]) for i in range(2)]
                    pnum = subview(pbig, pbig.ap[:, 6, :])
                    pden = subview(pbig, pbig.ap[:, 7, :])
                    bk = [subview(pbig, pbig.ap[:, i, :]) for i in range(6)]
                    qcT = sb("qcT", [128, 8, CH], BF16, PL1)
                    sgc = sb("sgc", [128, 8, CH], BF16, PL1)
                    mixC = sb("mixC", [128, 8, CH], BF16, PL1)
                    pws = [sb("pws%d" % i, [128, 2, 3, 128], BF16, PL1) for i in range(2)]
                    pw2 = [sb("pw2%d" % i, [128, 2, 3, 128], BF16, PL1) for i in range(2)]
                    rs = sb("rs1", [128, CH], F32, PL1)
                    lnr = sb("lnr", [128, CH], F32, PL1)
                    t1 = sb("t11", [128, CH], F32, PL1)
                    EP = [dict(rs=rs, t1=t1, lnr=lnr),
                          dict(rs=sb("rs1b", [128, CH], F32, PL1), t1=sb("t11b", [128, CH], F32, PL1), lnr=sb("lnrb", [128, CH], F32, PL1))]
                    pnums = [pnum, bk[4]]
                    pdens = [pden, bk[5]]
                    x2 = [sb("x2%d" % i, [128, 1024], F32, PL1) for i in range(2)]
                    yo = [sb("yo%d" % i, [128, 1024], F32, PL1) for i in range(2)]
                    yo_slot = [K.slot() for _ in range(2)]
                    ss2 = sb("ss2", [128, 2], F32, PL1)
                    ln2 = sb("ln2", [128, 2], F32, PL1)
                    r2 = sb("r2", [128, 2], F32, PL1)
                    it = 0
                    mxl = alloc_mx(PL1, full=False)
                    xch = mxl.xch
                    junk = sb("junk1", [128, 1024], BF16, PL1)
                    fnbc = sb("fnbc", [128, 1024], F32, PL1)
                    fn_slot = K.slot()
                    K.dma("sp", fnbc, V(fn_d.ap.to_broadcast([128, 1024]), fn_d.res), fn_slot)
                    load_xnT(xnT1_d, 0)
                    for C in range(NCH):
                        use_xnT(C)
                        if C + 1 < NCH:
                            load_xnT(xnT1_d, C + 1)
                        load_x(mxl, x1_d, C)
                        handoff(pw, bk[0:4])
                        for t in range(8):
                            pa_ = bk[t % 6]
                            proj(pa_, t * 128)
                            K.cp("act" if t % 2 == 0 else "dve", qcT[:, t, :], pa_)
                        for t in range(8):
                            pa_ = bk[(t + 2) % 6]
                            proj(pa_, 1024 + t * 128)
                            K.act(sgc[:, t, :], pa_, AF.Silu)
                        handoff(bk[0:4], pw)
                        items = [(t, qi) for t in range(8) for qi in range(4)]

                        def wqk(t, qi, w):
                            kv = t // 4
                            i = C * 4 + qi
                            qs = slice(qi * 128, (qi + 1) * 128)
                            for o in range(3):
                                sl = 2 - o
                                ks = slice((i + o) * 128, (i + o + 1) * 128)
                                K.mm(w[:, 0, sl * 128:(sl + 1) * 128], KcT[0:64, kv, ks], qcT[0:64, t, qs])
                                K.mm(w[:, 1, sl * 128:(sl + 1) * 128], KcT[64:128, kv, ks], qcT[64:128, t, qs])

                        wqk(items[0][0], items[0][1], pw[it % 2])
                        deferred = []
                        for idx, (t, qi) in enumerate(items):
                            kv = t // 4
                            i = C * 4 + qi
                            qs = slice(qi * 128, (qi + 1) * 128)
                            w = pw[it % 2]
                            s1 = pws[it % 2]
                            s2 = pw2[it % 2]
                            it += 1
                            if i in (0, NB // 2 - 1, NB // 2, NB - 1):
                                for o in range(3):
                                    sl = 2 - o
                                    K.act(s1[:, :, sl, :], w[:, :, sl * 128:(sl + 1) * 128], AF.Exp, bias=maskW[:, i * 3 + o:i * 3 + o + 1], scale=0.125)
                            else:
                                K.act(s1, w[:, :, 0:384].re("p h (o q) -> p h o q", o=3), AF.Exp, scale=0.125)
                            K.tt("dve", s2, s1, EBT[:, 2 * t:2 * t + 2, :, :], ALU.mult)
                            if deferred:
                                deferred.pop(0)()
                            if idx + 1 < len(items):
                                wqk(items[idx + 1][0], items[idx + 1][1], pw[it % 2])
                            pnum_, pden_ = pnums[t % 2], pdens[t % 2]
                            for o in range(3):
                                sl = 2 - o
                                st, sp_ = (o == 0), (o == 2)
                                vv = Vc[:, i + o, kv * 64:(kv + 1) * 64]
                                K.mm(pnum_[0:64, qs], vv, s2[:, 0, sl, :], start=st, stop=sp_)
                                K.mm(pnum_[64:128, qs], vv, s2[:, 1, sl, :], start=st, stop=sp_, tp=(0, 64))
                                K.mm(pden_[0:64, qs], ones[:, 0:64], s2[:, 0, sl, :], start=st, stop=sp_)
                                K.mm(pden_[64:128, qs], ones[:, 0:64], s2[:, 1, sl, :], start=st, stop=sp_, tp=(0, 64))
                            if qi == 3:
                                def epi(t=t, pnum_=pnum_, pden_=pden_):
                                    e_ = EP[t % 2]
                                    K.ts("dve", e_["rs"], pden_, esk[:, t:t + 1], ALU.add)
                                    K.cp("dve", e_["t1"], pnum_)
                                    K.act(e_["lnr"], e_["rs"], AF.Ln)
                                    K.act(e_["rs"], e_["lnr"], AF.Exp, scale=-1.0)
                                    K.tt("pool", e_["t1"], e_["t1"], e_["rs"], ALU.mult)
                                    K.tt("pool", mixC[:, t, :], e_["t1"], sgc[:, t, :], ALU.mult)
                                deferred.append(epi)
                        while deferred:
                            deferred.pop(0)()
                        for b in range(4):
                            bs = slice(b * 128, (b + 1) * 128)
                            py = pw[b % 2]
                            for half in range(2):
                                for f in range(8):
                                    K.mm(py[:, half, :], mixC[:, f, bs], wobf[:, f, half * 512:(half + 1) * 512], start=(f == 0), stop=(f == 7))
                            xo = x2[b % 2]
                            K.tt("dve", xo, py.re("p a c -> p (a c)"), xch[:, b, :], ALU.add)
                            K.act(junk, xo, AF.Square, accum=ss2[:, b % 2:b % 2 + 1])
                            K.act(ln2[:, b % 2:b % 2 + 1], ss2[:, b % 2:b % 2 + 1], AF.Ln, bias=epsb[:, 0:1], scale=1.0 / 1024.0)
                            K.act(r2[:, b % 2:b % 2 + 1], ln2[:, b % 2:b % 2 + 1], AF.Exp, scale=-0.5)
                            yb = yo[b % 2]
                            K.ts("dve", yb, xo, r2[:, b % 2:b % 2 + 1], ALU.mult)
                            K.tt("pool", yb, yb, fnbc, ALU.mult)
                            r0 = C * CH + b * 128
                            K.dma("pool", y_d[r0:r0 + 128, :], yb, yo_slot[b % 2])
    except StopBuild:
        pass
    for s_ in K.slots:
        if s_.cnt:
            nc.gpsimd.wait_ge(s_.sem, s_.cnt)
    return nc, K


def _t5_bucket(rel):
    half = 16
    max_exact = 8
    ret = (rel > 0).astype(np.int32) * half
    dist = np.abs(rel)
    large = max_exact + (np.log(np.maximum(dist, 1) / max_exact) / np.log(128 / max_exact) * (half - max_exact)).astype(np.int32)
    large = np.minimum(large, half - 1)
    return ret + np.where(dist < max_exact, dist, large)


def _static_tables():
    f32 = np.float32
    st = {}
    st["ident"] = np.eye(128, dtype=f32)
    st["aident"] = np.ascontiguousarray(np.eye(128, dtype=f32)[::-1])
    ob = np.zeros((128, 128), f32)
    ob[:64, :64] = 1
    ob[64:, 64:] = 1
    st["onesblk"] = ob
    j = np.arange(128)[:, None]
    i = np.arange(128)[None, :]
    mmat = np.zeros((128, 4, 128), f32)
    mmat[:, 0, :] = np.maximum(i - j, 0)
    mmat[:, 1, :] = (i >= j)
    mmat[:, 2, :] = np.maximum(j - i, 0)
    mmat[:, 3, :] = (j > i)
    st["mmat"] = mmat
    c = np.arange(512) % 128
    iot = np.zeros((128, 4, 512), f32)
    iot[:, 0, :] = c + 1
    iot[:, 1, :] = 128 - c
    iot[:, 2, :] = 127 - c
    iot[:, 3, :] = c
    st["iot"] = iot
    m = np.arange(640)
    rel = 255 - m
    bk = _t5_bucket(rel)
    oh = np.zeros((32, 640), f32)
    oh[bk, m] = 1
    st["oh"] = oh
    st["inwin"] = np.broadcast_to((np.abs(rel) <= 128).astype(f32)[None, :], (16, 640)).copy()
    return st


def _core_tables(is_prompt):
    f32 = np.float32
    seqlen = 4096 if is_prompt else 2048
    t = np.arange(NT) % seqlen
    d = np.arange(128) % 64
    pair = d // 2
    sgn = np.where(d % 2 == 0, -1.0, 1.0)
    quarter = 16
    freqs = (np.float32(10000.0) ** (-np.arange(quarter, dtype=f32) / quarter)).astype(f32)
    row = (t // 64).astype(f32)
    col = (t % 64).astype(f32)
    ang = np.concatenate([row[:, None] * freqs, col[:, None] * freqs], axis=-1).astype(f32)
    angd = ang[:, pair].T.astype(np.float64)
    tabA = np.stack([np.cos(angd), np.sin(angd) * sgn[:, None]]).astype(f32)
    half = 32
    freqs_b = (np.float32(10000.0) ** (-np.arange(half, dtype=f32) / half)).astype(f32)
    angb = (t.astype(f32)[:, None] * freqs_b).astype(f32)
    angbd = angb[:, pair].T.astype(np.float64)
    tabB = np.stack([np.cos(angbd), np.sin(angbd) * sgn[:, None]]).astype(f32)
    seq_of_blk = (np.arange(NB) * 128) // seqlen
    maskA = np.zeros((NCH, NB), f32)
    for C in range(NCH):
        sq = (C * CH) // seqlen
        maskA[C, :] = np.where(seq_of_blk == sq, 0.0, NEG)
    maskA = np.broadcast_to(maskA.reshape(1, -1), (128, NCH * NB)).copy()
    maskW = np.zeros((NB, 3), f32)
    for i in range(NB):
        for o in range(3):
            jb = i + o - 1
            if jb < 0 or jb >= NB or seq_of_blk[jb] != seq_of_blk[i]:
                maskW[i, o] = NEG
    maskW = np.broadcast_to(maskW.reshape(1, -1), (128, NB * 3)).copy()
    cps = seqlen // 128
    rf = np.array([0.0 if (n % cps == 0) else 1.0 for n in range(NB)], f32)
    rb = np.array([0.0 if (n % cps == cps - 1) else 1.0 for n in range(NB)], f32)
    rfb = np.broadcast_to(np.concatenate([rf, rb])[None, :], (128, 64)).copy()
    return {"tabA": tabA, "tabB": tabB, "maskA": maskA, "maskW": maskW, "rfb": rfb}


def _swap(cols):
    cols = np.asarray(cols)
    return cols ^ 1


def _prep_common(norm_g, w_in_ab, qk_norm_a, ret_decay, w_out_ab, w_in_c, sink_c, w_out_c, rel_bias, final_norm):
    f32 = np.float32
    W = np.asarray(w_in_ab[0], f32)
    qa = np.arange(0, 512)
    ka = np.arange(512, 640)
    va = np.arange(640, 768)
    ga = np.arange(768, 1280)
    qb = np.arange(1280, 1536)
    kb = np.arange(1536, 1792)
    vb = np.arange(1792, 2304)
    gb = np.arange(2304, 2816)
    kadup = np.concatenate([ka[0:64], ka[0:64], ka[64:128], ka[64:128]])
    cm = {}
    cm["wG"] = np.ascontiguousarray(W[:, np.concatenate([kadup, _swap(kadup), kb, _swap(kb), va, vb])])
    cm["wLB"] = np.ascontiguousarray(W[:, np.concatenate([qb, _swap(qb), kb, _swap(kb), gb, vb])])
    cm["wLA"] = np.ascontiguousarray(W[:, np.concatenate([qa, _swap(qa), ga])])
    cm["woab"] = np.ascontiguousarray(np.asarray(w_out_ab[0], f32))
    Wc = np.asarray(w_in_c[0], f32)
    kc = np.arange(1024, 1152)
    kcdup = np.concatenate([kc[0:64], kc[0:64], kc[64:128], kc[64:128]])
    cm["wG1"] = np.ascontiguousarray(Wc[:, np.concatenate([kcdup, np.arange(1152, 1280)])])
    cm["wL1"] = np.ascontiguousarray(Wc[:, np.concatenate([np.arange(0, 1024), np.arange(1280, 2304)])])
    cm["woc"] = np.ascontiguousarray(np.asarray(w_out_c[0], f32))
    ng = np.asarray(norm_g, f32)
    cm["gcol"] = np.ascontiguousarray(ng.reshape(2, 8, 128).transpose(2, 0, 1).reshape(128, 16))
    cm["fn"] = np.asarray(final_norm, f32).reshape(1, 1024).copy()
    g = np.asarray(qk_norm_a[0], f32)
    d = np.arange(128) % 64
    cm["gqk"] = np.stack([g[0][d], g[0][d ^ 1], g[1][d], g[1][d ^ 1]], axis=1).astype(f32).copy()
    rd = np.asarray(ret_decay[0], f32)
    hp = (np.arange(128) // 64)
    rdec = np.zeros((128, 12), f32)
    for p in range(2):
        rdec[:, p] = rd[0][2 * p + hp]
        rdec[:, 2 + p] = rd[1][2 * p + hp]
    for h in range(4):
        rdec[:, 4 + h] = rd[0][h]
        rdec[:, 8 + h] = rd[1][h]
    cm["rdec"] = rdec
    sk = np.asarray(sink_c[0], f32)
    sinkl = np.zeros((128, 8), f32)
    for t in range(8):
        sinkl[:, t] = sk[2 * t + hp]
    cm["sinkl"] = sinkl
    cm["relb"] = np.ascontiguousarray(np.asarray(rel_bias, f32))
    cm.update(_static_tables())
    return cm


_CACHE = {}


def kernel(x_prompt, x_sample, norm_g, w_in_ab, qk_norm_a, ret_decay, w_out_ab, w_in_c, sink_c, w_out_c, rel_bias, final_norm):
    xp = np.asarray(x_prompt, np.float32)
    xs = np.asarray(x_sample, np.float32)
    cm = _prep_common(norm_g, w_in_ab, qk_norm_a, ret_decay, w_out_ab, w_in_c, sink_c, w_out_c, rel_bias, final_norm)
    tp = _core_tables(True)
    tsm = _core_tables(False)
    in_maps = []
    for c in range(8):
        m = dict(cm)
        if c < 4:
            m["x"] = np.ascontiguousarray(xp[c])
            m.update(tp)
        else:
            m["x"] = np.ascontiguousarray(xs[2 * (c - 4):2 * (c - 4) + 2].reshape(NT, 1024))
            m.update(tsm)
        in_maps.append(m)
    if "nc" not in _CACHE:
        _CACHE["nc"] = build_program()[0]
    nc = _CACHE["nc"]
    res = run_bass_kernel_spmd(nc, in_maps, core_ids=list(range(8)))
    outs = [np.asarray(r["y"], np.float32) for r in res.results]
    y_prompt = np.stack(outs[0:4], axis=0)
    y_sample = np.stack(outs[4:8], axis=0).reshape(8, 2048, 1024)
    return (y_prompt, y_sample)
```

```python
import numpy as np
import concourse.bass as bass
import concourse.mybir as mybir
from concourse.bass_utils import run_bass_kernel_spmd

F32 = mybir.dt.float32
BF16 = mybir.dt.bfloat16
AF = mybir.ActivationFunctionType
ALU = mybir.AluOpType

NT = 4096
NB = 32
CH = 512
NCH = 8
EPS = 1e-6
NEG = -30000.0


class Prod:
    def __init__(self, sem, inc):
        self.sem = sem
        self.inc = inc
        self.cnt = 0


class Res:
    def __init__(self):
        self.w = {}
        self.r = {}
        self.excl = False


class V:
    def __init__(self, ap, res=None):
        self.ap = ap
        self.res = res if res is not None else Res()

    def __getitem__(self, k):
        return V(self.ap[k], self.res)

    def re(self, pat, **kw):
        return V(self.ap.rearrange(pat, **kw), self.res)

    def bc(self, shape):
        return V(self.ap.to_broadcast(shape), self.res)


class Ker:
    def __init__(self, nc):
        self.nc = nc
        self.eng = {"pe": nc.tensor, "act": nc.scalar, "dve": nc.vector, "pool": nc.gpsimd, "sp": nc.sync}
        self.prod = {}
        for n in ("pe", "act", "dve", "pool"):
            self.prod[n] = Prod(nc.alloc_semaphore("s_" + n), 1)
        self.seen = {n: {} for n in self.eng}
        self.nslot = 0
        self.ninstr = 0

    def slot(self):
        self.nslot += 1
        p = Prod(self.nc.alloc_semaphore("d%d" % self.nslot), 16)
        if hasattr(self, "slots"):
            self.slots.append(p)
        return p

    def _wait(self, en, reads, writes):
        deps = {}
        for v in reads:
            for p, i in v.res.w.items():
                deps[p] = max(deps.get(p, 0), i)
        for v in writes:
            for p, i in v.res.w.items():
                deps[p] = max(deps.get(p, 0), i)
            for p, i in v.res.r.items():
                deps[p] = max(deps.get(p, 0), i)
        e = self.eng[en]
        seen = self.seen[en]
        own = self.prod.get(en)
        for p, i in deps.items():
            if p is own and en == "pe":
                continue
            if seen.get(p, 0) >= i:
                continue
            e.wait_ge(p.sem, i)
            seen[p] = i

    def op(self, en, fn, reads, writes):
        writes = list(writes) + [r for r in reads if r.res.excl]
        self._wait(en, reads, writes)
        ins = fn(self.eng[en])
        p = self.prod[en]
        p.cnt += 1
        ins.then_inc(p.sem, 1)
        for v in reads:
            v.res.r[p] = p.cnt
        for v in writes:
            v.res.w[p] = p.cnt
        self.ninstr += 1

    def dma(self, q, out, in_, slot):
        self._wait(q, [in_], [out])
        ins = self.eng[q].dma_start(out=out.ap, in_=in_.ap)
        slot.cnt += 16
        ins.then_inc(slot.sem, 16)
        in_.res.r[slot] = slot.cnt
        out.res.w[slot] = slot.cnt

    def mm(self, out, lhsT, rhs, start=True, stop=True, tp=None):
        kw = {}
        if tp is not None:
            kw["tile_position"] = tp
        self.op("pe", lambda e: e.matmul(out.ap, lhsT.ap, rhs.ap, start=start, stop=stop, **kw), [lhsT, rhs], [out])

    def tr(self, out, in_, ident):
        self.op("pe", lambda e: e.transpose(out.ap, in_.ap, ident.ap), [in_, ident], [out])

    def act(self, out, in_, func, bias=None, scale=1.0, accum=None):
        reads = [in_]
        kw = {}
        if bias is not None:
            if isinstance(bias, V):
                reads.append(bias)
                kw["bias"] = bias.ap
            else:
                kw["bias"] = bias
        if isinstance(scale, V):
            reads.append(scale)
            kw["scale"] = scale.ap
        else:
            kw["scale"] = scale
        writes = [out]
        if accum is not None:
            writes.append(accum)
            kw["accum_out"] = accum.ap
        self.op("act", lambda e: e.activation(out.ap, in_.ap, func, **kw), reads, writes)

    def tt(self, en, out, a, b, op):
        self.op(en, lambda e: e.tensor_tensor(out.ap, a.ap, b.ap, op), [a, b], [out])

    def stt(self, en, out, in0, scalar, in1, op0, op1):
        reads = [in0, in1]
        s = scalar
        if isinstance(scalar, V):
            reads.append(scalar)
            s = scalar.ap
        self.op(en, lambda e: e.scalar_tensor_tensor(out.ap, in0.ap, s, in1.ap, op0, op1), reads, [out])

    def ts(self, en, out, in0, s1, op0, s2=None, op1=None):
        reads = [in0]
        a1 = s1
        if isinstance(s1, V):
            reads.append(s1)
            a1 = s1.ap
        a2 = s2
        if isinstance(s2, V):
            reads.append(s2)
            a2 = s2.ap
        if op1 is None:
            self.op(en, lambda e: e.tensor_scalar(out.ap, in0.ap, a1, None, op0), reads, [out])
        else:
            self.op(en, lambda e: e.tensor_scalar(out.ap, in0.ap, a1, a2, op0, op1), reads, [out])

    def cp(self, en, out, in_):
        if en == "act":
            self.op("act", lambda e: e.copy(out.ap, in_.ap), [in_], [out])
        else:
            self.op(en, lambda e: e.tensor_copy(out.ap, in_.ap), [in_], [out])

    def amul(self, out, in_, m):
        self.op("act", lambda e: e.mul(out.ap, in_.ap, m.ap), [in_, m], [out])

    def recip(self, out, in_):
        self.op("dve", lambda e: e.reciprocal(out.ap, in_.ap), [in_], [out])

    def memset(self, en, out, val):
        self.op(en, lambda e: e.memset(out.ap, val), [], [out])


class StopBuild(Exception):
    pass


import contextlib


class Scope(contextlib.ExitStack):
    def __init__(self, K):
        super().__init__()
        self.K = K
        self.tiles = []

    def __exit__(self, *a):
        fr = self.K.freed
        for v in self.tiles:
            for d in (v.res.w, v.res.r):
                for p, i in d.items():
                    fr[p] = max(fr.get(p, 0), i)
        self.tiles = []
        return super().__exit__(*a)

    def close(self):
        self.__exit__(None, None, None)


def build_program(stop=None, taps=()):
    nc = bass.Bass("TRN2", target_bir_lowering=False)
    K = Ker(nc)
    K.slots = []
    K.freed = {}
    K.tapped = {}

    def checkpoint(name):
        if stop == name:
            raise StopBuild()

    def tap(name, v, shape, dt=F32):
        if name not in taps or name in K.tapped:
            return
        d = V(nc.dram_tensor("dbg_" + name, list(shape), dt, kind="ExternalOutput").ap())
        K.tapped[name] = d
        K.dma("sp", d, v, K.slot())

    def din(name, shape, dt=F32):
        return V(nc.dram_tensor(name, list(shape), dt, kind="ExternalInput").ap())

    x_d = din("x", [NT, 1024])
    wG_d = din("wG", [1024, 1664])
    wLB_d = din("wLB", [1024, 2048])
    wLA_d = din("wLA", [1024, 1536])
    woab_d = din("woab", [1024, 1024])
    wG1_d = din("wG1", [1024, 384])
    wL1_d = din("wL1", [1024, 2048])
    woc_d = din("woc", [1024, 1024])
    gcol_d = din("gcol", [128, 16])
    fn_d = din("fn", [1, 1024])
    gqk_d = din("gqk", [128, 4])
    rdec_d = din("rdec", [128, 12])
    sink_d = din("sinkl", [128, 8])
    relb_d = din("relb", [32, 16])
    ident_d = din("ident", [128, 128])
    onesblk_d = din("onesblk", [128, 128])
    mm_d = din("mmat", [128, 4, 128])
    iot_d = din("iot", [128, 4, 512])
    oh_d = din("oh", [32, 640])
    inwin_d = din("inwin", [16, 640])
    tabA_d = din("tabA", [2, 128, NT])
    tabB_d = din("tabB", [2, 128, NT])
    maskA_d = din("maskA", [128, 256])
    maskW_d = din("maskW", [128, 96])
    rfb_d = din("rfb", [128, 64])
    y_d = V(nc.dram_tensor("y", [NT, 1024], F32, kind="ExternalOutput").ap())
    x1_d = V(nc.dram_tensor("x1s", [NT, 1024], F32, kind="Internal").ap())
    mixb_d = V(nc.dram_tensor("mixbs", [4, 128, NT], BF16, kind="Internal").ap())
    vec_h = nc.dram_tensor("vecs", [16, 640], BF16, kind="Internal")
    vec_d = V(vec_h.ap())
    aident_d = din("aident", [128, 128])
    xnT0_d = V(nc.dram_tensor("xnT0s", [NCH, 128, 8, CH], BF16, kind="Internal").ap())
    xnT1_d = V(nc.dram_tensor("xnT1s", [NCH, 128, 8, CH], BF16, kind="Internal").ap())

    es = Scope(K)
    uid = [0]

    def sb(name, shape, dt=F32, stack=None):
        uid[0] += 1
        st_ = stack if stack is not None else es
        t = st_.enter_context(nc.sbuf_tensor("sb%d_%s" % (uid[0], name), list(shape), dt))
        v = V(t[:])
        v.res.w = dict(K.freed)
        st_.tiles.append(v)
        return v

    def ps(name, shape, dt=F32, stack=None):
        uid[0] += 1
        st_ = stack if stack is not None else es
        t = st_.enter_context(nc.psum_tensor("ps%d_%s" % (uid[0], name), list(shape), dt))
        v = V(t[:])
        v.res.excl = True
        v.res.w = dict(K.freed)
        st_.tiles.append(v)
        return v

    try:
        with es:
            cslot = K.slot()
            consts = []

            def cload(name, src, shape, dt=F32, q="sp"):
                t = sb(name, shape, dt)
                K.dma(q, t, src, cslot)
                consts.append(t)
                return t

            gcol = cload("gcol", gcol_d, [128, 16])
            gqk = cload("gqk", gqk_d, [128, 4])
            rdec = cload("rdec", rdec_d, [128, 12])
            sinkl = cload("sinkl", sink_d, [128, 8])
            maskA = cload("maskA", maskA_d, [128, 256])
            maskW = cload("maskW", maskW_d, [128, 96])
            rfb = cload("rfb", rfb_d, [128, 64])
            ident32 = cload("ident32", ident_d, [128, 128])
            onesblk32 = cload("onesblk32", onesblk_d, [128, 128])
            for c in consts:
                c.res.w[cslot] = cslot.cnt
            ident = sb("ident", [128, 128], BF16)
            onesblk = sb("onesblk", [128, 128], BF16)
            ones = sb("ones", [128, 128], BF16)
            epsb = sb("epsb", [128, 1])
            K.cp("dve", ident, ident32)
            K.cp("dve", onesblk, onesblk32)
            K.memset("dve", ones, 1.0)
            K.memset("dve", epsb, EPS)

            checkpoint("c0")
            wbf = sb("wbf", [128, 8, 2048], BF16)
            wobf = sb("wobf", [128, 8, 1024], BF16)
            wst = [sb("wst%d" % i, [128, 1024]) for i in range(2)]
            wst_slot = [K.slot() for _ in range(2)]
            xnTs = [sb("xnT%d" % i, [128, 8, CH], BF16) for i in range(2)]
            xnT_slot = [K.slot() for _ in range(2)]
            cur = {"xnT": xnTs[0]}
            xch_slot = [K.slot() for _ in range(4)]
            xst_slot = [K.slot() for _ in range(2)]
            EBT = sb("EBT", [128, 16, 3, 128], BF16)
            esk = sb("esk", [128, 8], F32)
            K.act(esk, sinkl, AF.Exp)
            with Scope(K) as S1:
                relb = sb("relb", [32, 16], F32, S1)
                oh = sb("oh", [32, 640], F32, S1)
                inw = sb("inw", [16, 640], F32, S1)
                e_slot = K.slot()
                K.dma("sp", relb, relb_d, e_slot)
                K.dma("sp", oh, oh_d, e_slot)
                K.dma("sp", inw, inwin_d, e_slot)
                for t_ in (relb, oh, inw):
                    t_.res.w[e_slot] = e_slot.cnt
                pv = ps("pv", [16, 1024], F32, S1)[:, 0:640]
                vec = sb("vec", [16, 640], F32, S1)
                vecb = sb("vecb", [16, 640], BF16, S1)
                K.mm(pv[:, 0:512], relb, oh[:, 0:512])
                K.mm(pv[:, 512:640], relb, oh[:, 512:640])
                K.act(vec, pv, AF.Exp)
                K.tt("dve", vecb, vec, inw, ALU.mult)
                v_slot = K.slot()
                K.dma("sp", vec_d, vecb, v_slot)
                g_slot = K.slot()
                aid32 = sb("aid32", [128, 128], F32, S1)
                K.dma("sp", aid32, aident_d, g_slot)
                aid = sb("aid", [128, 128], BF16, S1)
                K.cp("dve", aid, aid32)
                TT = sb("TT", [128, 16 * 384], BF16, S1)
                src = V(bass.AP(vec_h, 0, [[1, 128], [640, 16], [1, 384]]), vec_d.res)
                K.dma("sp", TT.re("p (h j) -> p h j", h=16), src, g_slot)
                prev = ps("prev", [128, 2, CH], F32, S1)
                EBTf = EBT.re("p h o q -> p (h o q)")
                for n_ in range(12):
                    K.mm(prev[:, n_ % 2, :], aid, TT[:, n_ * 512:(n_ + 1) * 512])
                    K.cp("act" if n_ % 2 == 0 else "dve", EBTf[:, n_ * 512:(n_ + 1) * 512], prev[:, n_ % 2, :])
            wcount = [0]

            def handoff(srcs, dsts):
                for d_ in dsts:
                    for s_ in srcs:
                        for dd in (s_.res.w, s_.res.r):
                            for p_, i_ in dd.items():
                                d_.res.w[p_] = max(d_.res.w.get(p_, 0), i_)

            def subview(parent, ap):
                v = V(ap)
                v.res.excl = parent.res.excl
                v.res.w = dict(parent.res.w)
                return v

            class MX:
                pass

            def alloc_mx(scope, full=True):
                m = MX()
                m.xch = sb("xch", [128, 4, 1024], F32, scope)
                if full:
                    m.xn = [sb("xn%d" % i, [128, 1024], BF16, scope) for i in range(2)]
                    m.junk = sb("junk", [128, 1024], BF16, scope)
                    m.ss = sb("ss", [128, 4], F32, scope)
                    m.lnv4 = sb("lnv4", [128, 4], F32, scope)
                    m.rstd4 = sb("rstd4", [128, 4], F32, scope)
                return m

            def load_x(m, src_d, C):
                for b in range(4):
                    r0 = C * CH + b * 128
                    K.dma("sp", m.xch[:, b, :], src_d[r0:r0 + 128, :], xch_slot[b])

            def store_xnT(dst_d, C, slot_i):
                K.dma("pool", dst_d[C], cur["xnT"], xst_slot[slot_i])

            def load_xnT(src_d, C):
                i = C % 2
                K.dma("sp", xnTs[i], src_d[C], xnT_slot[i])

            def use_xnT(C):
                cur["xnT"] = xnTs[C % 2]

            def load_w_piece(dst, d0, src_d, kc, c0, c1, layer_g):
                i = wcount[0] % 2
                wcount[0] += 1
                n_ = c1 - c0
                K.dma("sp", wst[i][:, 0:n_], src_d[kc * 128:(kc + 1) * 128, c0:c1], wst_slot[i])
                en = "act" if (wcount[0] % 2 == 0) else "dve"
                if layer_g is None:
                    K.cp(en, dst[:, kc, d0:d0 + n_], wst[i][:, 0:n_])
                elif en == "act":
                    K.amul(dst[:, kc, d0:d0 + n_], wst[i][:, 0:n_], gcol[:, layer_g * 8 + kc:layer_g * 8 + kc + 1])
                else:
                    K.ts("dve", dst[:, kc, d0:d0 + n_], wst[i][:, 0:n_], gcol[:, layer_g * 8 + kc:layer_g * 8 + kc + 1], ALU.mult)

            def load_w(dst, src_d, ncols, layer_g):
                for kc in range(8):
                    for c0 in range(0, ncols, 1024):
                        c1 = min(ncols, c0 + 1024)
                        load_w_piece(dst, c0, src_d, kc, c0, c1, layer_g)

            wlo = V(wbf.ap[:, :, 0:1536])
            whi = V(wbf.ap[:, :, 1536:2048])
            wmode = {"split": False, "base": 0}

            def wcol(kc, c0, n_=128):
                c0 = c0 + wmode["base"]
                if not wmode["split"]:
                    return wbf[:, kc, c0:c0 + n_]
                if c0 + n_ <= 1536:
                    return wlo[:, kc, c0:c0 + n_]
                return whi[:, kc, c0 - 1536:c0 - 1536 + n_]

            def make_xnT(m, src_d, C, pT):
                use_xnT(C)
                xnT = cur["xnT"]
                load_x(m, src_d, C)
                for b in range(4):
                    K.act(m.junk, m.xch[:, b, :], AF.Square, accum=m.ss[:, b:b + 1])
                K.act(m.lnv4, m.ss, AF.Ln, bias=epsb[:, 0:1], scale=1.0 / 1024.0)
                K.act(m.rstd4, m.lnv4, AF.Exp, scale=-0.5)
                for b in range(4):
                    xb = m.xn[b % 2]
                    K.ts("dve", xb, m.xch[:, b, :], m.rstd4[:, b:b + 1], ALU.mult)
                    for kc in range(8):
                        K.tr(pT[:, kc, :], xb[:, kc * 128:(kc + 1) * 128], ident)
                    K.cp("act", xnT[:, :, b * 128:(b + 1) * 128], pT)

            def proj(dst, c0):
                for kc in range(8):
                    K.mm(dst, wcol(kc, c0), cur["xnT"][:, kc, :], start=(kc == 0), stop=(kc == 7))

            def rsq_bcast(dst, src_ps, nfeat, sq, psn, lnv, lhs_ones):
                K.act(sq, src_ps, AF.Square)
                K.mm(psn, lhs_ones, sq)
                K.act(lnv, psn, AF.Ln, bias=epsb[:, 0:1], scale=1.0 / nfeat)
                K.act(dst, lnv, AF.Exp, scale=-0.5)

            with Scope(K) as L0:
                LR = Scope(K)
                KaT = sb("KaT", [128, 2, NT], BF16, L0)
                Va = sb("Va", [128, NB, 128], BF16, L0)
                tabc = sb("tabc", [128, 2, CH], F32, L0)
                tab_slot = K.slot()
                tabd_slot = K.slot()
                sq = sb("sq", [128, CH], BF16, L0)
                lnv = sb("lnv", [128, CH], F32, L0)
                rs = sb("rs", [128, CH], F32, L0)
                t1 = sb("t1", [128, CH], F32, L0)
                t2 = sb("t2", [128, CH], F32, L0)
                tabg = sb("tabg", [128, 2, CH], F32, L0)
                SbAll = sb("SbAll", [128, 2, NB, 128], BF16, LR)
                tabd = sb("tabd", [128, 2, CH], F32, LR)
                vbtm = sb("vbtm", [128, 4, 512], BF16, LR)
                lg = sb("lg", [128, 12], F32, LR)
                K.act(lg, rdec, AF.Exp)
                K.ts("dve", lg, lg, -1.0, ALU.mult)
                cd = sb("cd", [128, 4], F32, LR)
                K.act(cd, lg[:, 0:4], AF.Exp, scale=128.0)
                cdr = sb("cdr", [128, 4, NB], F32, LR)
                for j in range(4):
                    off = 0 if j < 2 else 32
                    K.ts("dve", cdr[:, j, :], rfb[:, off:off + 32], cd[:, j:j + 1], ALU.mult)
                checkpoint("c1")
                QF4 = sb("QF4", [128, 2, CH], F32, LR)
                QB4 = sb("QB4", [128, 2, CH], F32, LR)
                KF4 = sb("KF4", [128, 2, CH], F32, LR)
                KB4 = sb("KB4", [128, 2, CH], F32, LR)
                DT = sb("DT", [128, 4, 128], F32, LR)
                with Scope(K) as S0:
                    iot = sb("iot", [128, 4, CH], F32, S0)
                    K.dma("sp", iot, iot_d, tabd_slot)
                    for p in range(2):
                        K.act(QF4[:, p, :], iot[:, 0, :], AF.Exp, scale=lg[:, p:p + 1])
                        K.act(QB4[:, p, :], iot[:, 1, :], AF.Exp, scale=lg[:, 2 + p:3 + p])
                        K.act(KF4[:, p, :], iot[:, 2, :], AF.Exp, scale=lg[:, p:p + 1])
                        K.act(KB4[:, p, :], iot[:, 3, :], AF.Exp, scale=lg[:, 2 + p:3 + p])
                    K.ts("dve", KF4, KF4, 0.125, ALU.mult)
                    K.ts("dve", KB4, KB4, 0.125, ALU.mult)
                    mmat = sb("mmat", [128, 4, 128], F32, S0)
                    K.dma("sp", mmat, mm_d, tab_slot)
                    d1 = sb("d1", [128, 128], F32, S0)
                    d2 = sb("d2", [128, 128], F32, S0)
                    for h in range(4):
                        checkpoint("d0")
                        K.act(d1, mmat[:, 0, :], AF.Exp, scale=lg[:, 4 + h:5 + h])
                        checkpoint("d1")
                        K.tt("dve", d1, d1, mmat[:, 1, :], ALU.mult)
                        checkpoint("d2")
                        K.act(d2, mmat[:, 2, :], AF.Exp, scale=lg[:, 8 + h:9 + h])
                        K.tt("dve", d2, d2, mmat[:, 3, :], ALU.mult)
                        K.tt("dve", d1, d1, d2, ALU.add)
                        checkpoint("d3")
                        K.ts("dve", DT[:, h, :], d1, 0.125, ALU.mult)
                        checkpoint("d4")

                def load_tab(dst, slot, src_d, C):
                    K.dma("sp", dst, V(src_d.ap[:, :, C * CH:(C + 1) * CH].rearrange("t p c -> p t c"), src_d.res), slot)

                def rope(psa, psb, tab, out32, ga=None, gb=None):
                    if ga is None:
                        K.tt("dve", t1, psa, tab[:, 0, :], ALU.mult)
                        K.tt("dve", t2, psb, tab[:, 1, :], ALU.mult)
                    else:
                        K.amul(tabg[:, 0, :], tab[:, 0, :], ga)
                        K.amul(tabg[:, 1, :], tab[:, 1, :], gb)
                        K.tt("dve", t1, psa, tabg[:, 0, :], ALU.mult)
                        K.tt("dve", t2, psb, tabg[:, 1, :], ALU.mult)
                    K.tt("pool", out32, t1, t2, ALU.add)

                checkpoint("setup0")
                load_w(wbf, wG_d, 1664, 0)
                with Scope(K) as PG:
                    pT = ps("pT", [128, 8, 128], BF16, PG)
                    pk = pT.re("p (c t) q -> p c t q", t=2)
                    pbig = ps("pbigG", [128, 7, CH], F32, PG)
                    bk = [subview(pbig, pbig.ap[:, i, :]) for i in range(7)]
                    pn = bk[4]
                    pkv = V(bk[6].ap[:, 0:256].rearrange("p (a b) -> p a b", a=2), bk[6].res)
                    kdbT = sb("kdbT", [128, 2, CH], BF16, PG)
                    kdbtm = sb("kdbtm", [128, 4, 2, 128], BF16, PG)
                    Rb = sb("Rb", [128, 2, 128], F32, PG)
                    mxg = alloc_mx(PG)
                    WS = [dict(sq=sq, lnv=lnv, rs=rs, t1=t1, t2=t2),
                          dict(sq=sb("wsq", [128, CH], BF16, PG), lnv=sb("wlnv", [128, CH], F32, PG),
                               rs=sb("wrs", [128, CH], F32, PG), t1=sb("wt1", [128, CH], F32, PG),
                               t2=sb("wt2", [128, CH], F32, PG))]
                    K.memset("dve", Rb, 0.0)
                    for C in range(NCH - 1, -1, -1):
                        make_xnT(mxg, x_d, C, pT)
                        store_xnT(xnT0_d, C, C % 2)
                        load_w_piece(wobf, 0, woab_d, C, 0, 1024, None)
                        load_tab(tabc, tab_slot, tabA_d, C)
                        load_tab(tabd, tabd_slot, tabB_d, C)
                        K.amul(tabg[:, 0, :], tabc[:, 0, :], gqk[:, 2:3])
                        K.amul(tabg[:, 1, :], tabc[:, 1, :], gqk[:, 3:4])
                        for t in range(2):
                            w_ = WS[t % 2]
                            pa_, pb_ = bk[2 * t], bk[2 * t + 1]
                            proj(pa_, t * 128)
                            proj(pb_, 256 + t * 128)
                            rsq_bcast(w_["rs"], pa_, 64.0, w_["sq"], pn, w_["lnv"], onesblk)
                            K.tt("dve", w_["t1"], pa_, tabg[:, 0, :], ALU.mult)
                            K.tt("dve", w_["t2"], pb_, tabg[:, 1, :], ALU.mult)
                            K.tt("pool", w_["t1"], w_["t1"], w_["t2"], ALU.add)
                            K.tt("pool", KaT[:, t, C * CH:(C + 1) * CH], w_["t1"], w_["rs"], ALU.mult)
                        for t in range(2):
                            w_ = WS[t % 2]
                            pa_, pb_ = bk[2 * t], bk[2 * t + 1]
                            proj(pa_, 512 + t * 128)
                            proj(pb_, 768 + t * 128)
                            K.tt("dve", w_["t1"], pa_, tabd[:, 0, :], ALU.mult)
                            K.tt("dve", w_["t2"], pb_, tabd[:, 1, :], ALU.mult)
                            K.tt("pool", w_["t1"], w_["t1"], w_["t2"], ALU.add)
                            K.tt("pool", kdbT[:, t, :], w_["t1"], KB4[:, t, :], ALU.mult)
                            for cj in range(4):
                                K.tr(pk[:, cj, t, :], kdbT[:, t, cj * 128:(cj + 1) * 128], ident)
                        K.cp("act", kdbtm, pk)
                        for b in range(4):
                            pva = (bk[4] if b % 2 == 0 else bk[2])[:, 0:128]
                            pvb = bk[5] if b % 2 == 0 else bk[3]
                            for kc in range(8):
                                K.mm(pva, cur["xnT"][:, kc, b * 128:(b + 1) * 128], wbf[:, kc, 1024:1152], start=(kc == 0), stop=(kc == 7))
                            for kc in range(8):
                                K.mm(pvb, cur["xnT"][:, kc, b * 128:(b + 1) * 128], wbf[:, kc, 1152:1664], start=(kc == 0), stop=(kc == 7))
                            K.cp("act", Va[:, C * 4 + b, :], pva)
                            K.cp("dve", vbtm[:, b, :], pvb)
                        for cj in range(3, -1, -1):
                            n = C * 4 + cj
                            for p in range(2):
                                K.mm(pkv[0:64, p, :], kdbtm[:, cj, p, 0:64], vbtm[:, cj, (2 * p) * 128:(2 * p + 1) * 128])
                                K.mm(pkv[64:128, p, :], kdbtm[:, cj, p, 64:128], vbtm[:, cj, (2 * p + 1) * 128:(2 * p + 2) * 128], tp=(0, 64))
                            K.ts("dve", SbAll[:, :, n, :], Rb, rfb[:, 32 + n:33 + n], ALU.mult)
                            for p in range(2):
                                K.ts("dve", Rb[:, p, :], Rb[:, p, :], cdr[:, 2 + p, n:n + 1], ALU.mult)
                                K.tt("dve", Rb[:, p, :], pkv[:, p, :], Rb[:, p, :], ALU.add)

                tap("KaT", KaT, [128, 2, NT], BF16)
                tap("Va", Va, [128, NB, 128], BF16)
                tap("SbAll", SbAll, [128, 2, NB, 128], BF16)
                checkpoint("G")
                load_w(wbf, wLB_d, 2048, 0)
                with Scope(K) as PB:
                    pT = ps("pT", [128, 8, 128], BF16, PB)
                    pk = pT.re("p (c t) q -> p c t q", t=2)
                    pbig = ps("pbigB", [128, 7, CH], F32, PB)
                    bk = [subview(pbig, pbig.ap[:, i, :]) for i in range(7)]
                    pa, pb, pss = bk[0], bk[1], bk[2]
                    po = subview(pbig, pbig.ap[:, 3:7, :])
                    qrT = sb("qrT", [128, 2, CH], BF16, PB)
                    qdf = sb("qdf", [128, 2, CH], BF16, PB)
                    qdb = sb("qdb", [128, 2, CH], BF16, PB)
                    krT = sb("krT", [128, 2, CH], BF16, PB)
                    kdfT = sb("kdfT", [128, 2, CH], BF16, PB)
                    kdftm = sb("kdftm", [128, 4, 2, 128], BF16, PB)
                    sg = sb("sg", [128, 4, CH], BF16, PB)
                    ATs = [sb("AT%d" % i, [128, 4, 128], BF16, PB) for i in range(2)]
                    Sfs = [sb("Sf%d" % i, [128, 2, 128], BF16, PB) for i in range(2)]
                    Rf = sb("Rf", [128, 2, 128], F32, PB)
                    mixBc = [sb("mixBc%d" % i, [128, 4, CH], BF16, PB) for i in range(1)]
                    mixB_slot = [K.slot() for _ in range(1)]
                    WS = [dict(sq=sq, lnv=lnv, rs=rs, t1=t1, t2=t2),
                          dict(sq=sb("wsq", [128, CH], BF16, PB), lnv=sb("wlnv", [128, CH], F32, PB),
                               rs=sb("wrs", [128, CH], F32, PB), t1=sb("wt1", [128, CH], F32, PB),
                               t2=sb("wt2", [128, CH], F32, PB))]
                    K.memset("dve", Rf, 0.0)
                    load_xnT(xnT0_d, 0)
                    pairs = [(bk[0], bk[1]), (bk[3], bk[4]), (bk[5], bk[6])]
                    for C in range(NCH):
                        use_xnT(C)
                        if C + 1 < NCH:
                            load_xnT(xnT0_d, C + 1)
                        load_tab(tabd, tabd_slot, tabB_d, C)
                        handoff([po], bk[3:7])
                        ip = 0
                        for t in range(2):
                            w_ = WS[ip % 2]
                            pa_, pb_ = pairs[ip % 3]
                            ip += 1
                            proj(pa_, t * 128)
                            proj(pb_, 256 + t * 128)
                            K.tt("dve", w_["t1"], pa_, tabd[:, 0, :], ALU.mult)
                            K.tt("dve", w_["t2"], pb_, tabd[:, 1, :], ALU.mult)
                            K.tt("pool", w_["t1"], w_["t1"], w_["t2"], ALU.add)
                            K.cp("act", qrT[:, t, :], w_["t1"])
                            K.tt("pool", qdf[:, t, :], w_["t1"], QF4[:, t, :], ALU.mult)
                            K.tt("pool", qdb[:, t, :], w_["t1"], QB4[:, t, :], ALU.mult)
                        for t in range(2):
                            w_ = WS[ip % 2]
                            pa_, pb_ = pairs[ip % 3]
                            ip += 1
                            proj(pa_, 512 + t * 128)
                            proj(pb_, 768 + t * 128)
                            K.tt("dve", w_["t1"], pa_, tabd[:, 0, :], ALU.mult)
                            K.tt("dve", w_["t2"], pb_, tabd[:, 1, :], ALU.mult)
                            K.tt("pool", w_["t1"], w_["t1"], w_["t2"], ALU.add)
                            K.cp("act", krT[:, t, :], w_["t1"])
                            K.tt("pool", kdfT[:, t, :], w_["t1"], KF4[:, t, :], ALU.mult)
                            for cj in range(4):
                                K.tr(pk[:, cj, t, :], kdfT[:, t, cj * 128:(cj + 1) * 128], ident)
                        K.cp("act", kdftm, pk)
                        for h in range(4):
                            pa_ = bk[3 + h]
                            proj(pa_, 1024 + h * 128)
                            K.act(sg[:, h, :], pa_, AF.Silu)
                        for b in range(4):
                            pv_ = bk[1 + b % 2]
                            for kc in range(8):
                                K.mm(pv_, cur["xnT"][:, kc, b * 128:(b + 1) * 128], wbf[:, kc, 1536:2048], start=(kc == 0), stop=(kc == 7))
                            K.cp("dve", vbtm[:, b, :], pv_)
                        handoff(bk[3:7], [po])
                        for cj in range(4):
                            n = C * 4 + cj
                            cs = slice(cj * 128, (cj + 1) * 128)
                            Sf = Sfs[cj % 2]
                            AT = ATs[cj % 2]
                            K.ts("dve", Sf, Rf, rfb[:, n:n + 1], ALU.mult)
                            for p in range(2):
                                K.mm(pa[0:64, p * 128:(p + 1) * 128], kdftm[:, cj, p, 0:64], vbtm[:, cj, (2 * p) * 128:(2 * p + 1) * 128])
                                K.mm(pa[64:128, p * 128:(p + 1) * 128], kdftm[:, cj, p, 64:128], vbtm[:, cj, (2 * p + 1) * 128:(2 * p + 2) * 128], tp=(0, 64))
                            for p in range(2):
                                K.ts("dve", Rf[:, p, :], Rf[:, p, :], cdr[:, p, n:n + 1], ALU.mult)
                                K.tt("dve", Rf[:, p, :], pa[:, p * 128:(p + 1) * 128], Rf[:, p, :], ALU.add)
                            for h in range(4):
                                t, r0 = h // 2, (h % 2) * 64
                                pdst = pss if (h % 2 == 0) else pb
                                K.mm(pdst[:, t * 128:(t + 1) * 128], krT[r0:r0 + 64, t, cs], qrT[r0:r0 + 64, t, cs])
                            ATv = AT.re("p (t hp) i -> p hp t i", hp=2)
                            DTv = DT.re("p (t hp) i -> p hp t i", hp=2)
                            K.tt("dve", ATv[:, 0, :, :], pss[:, 0:256].re("p (t i) -> p t i", t=2), DTv[:, 0, :, :], ALU.mult)
                            K.tt("dve", ATv[:, 1, :, :], pb[:, 0:256].re("p (t i) -> p t i", t=2), DTv[:, 1, :, :], ALU.mult)
                            for h in range(4):
                                t, r0 = h // 2, (h % 2) * 64
                                K.mm(po[:, h, cs], vbtm[:, cj, h * 128:(h + 1) * 128], AT[:, h, :], start=True, stop=False)
                                K.mm(po[:, h, cs], Sf[r0:r0 + 64, t, :], qdf[r0:r0 + 64, t, cs], start=False, stop=False)
                                K.mm(po[:, h, cs], SbAll[r0:r0 + 64, t, n, :], qdb[r0:r0 + 64, t, cs], start=False, stop=True)
                        mb = mixBc[0]
                        for h0 in (0, 2):
                            hs = (h0, h0 + 1)
                            for h in hs:
                                K.act(WS[h % 2]["sq"], po[:, h, :], AF.Square)
                            for h in hs:
                                K.mm(bk[h % 2], ones, WS[h % 2]["sq"])
                            for h in hs:
                                K.act(WS[h % 2]["lnv"], bk[h % 2], AF.Ln, bias=epsb[:, 0:1], scale=1.0 / 128.0)
                            for h in hs:
                                K.act(WS[h % 2]["rs"], WS[h % 2]["lnv"], AF.Exp, scale=-0.5)
                            for h in hs:
                                K.tt("dve", WS[h % 2]["t1"], po[:, h, :], WS[h % 2]["rs"], ALU.mult)
                                K.tt("pool", mb[:, h, :], WS[h % 2]["t1"], sg[:, h, :], ALU.mult)
                        K.dma("pool", V(mixb_d.ap[:, :, C * CH:(C + 1) * CH].rearrange("h p c -> p h c"), mixb_d.res), mb, mixB_slot[0])

                tap("mixb", mixb_d, [4, 128, NT], BF16)
                checkpoint("LB")
                LR.close()
                load_w(wbf, wLA_d, 1536, 0)
                handoff([wbf], [wlo, whi])
                wmode["split"] = True
                with Scope(K) as PA:
                    pbig = ps("pbig", [128, 8, CH], F32, PA)
                    psc = [subview(pbig, pbig.ap[:, 2 * i:2 * i + 2, :]) for i in range(3)]
                    pnum = subview(pbig, pbig.ap[:, 6, :])
                    pden = subview(pbig, pbig.ap[:, 7, :])
                    bk = [subview(pbig, pbig.ap[:, i, :]) for i in range(6)] + [pnum, pden]
                    WS = [dict(sq=sq, lnv=lnv, rs=rs, t1=t1, t2=t2),
                          dict(sq=sb("wsq", [128, CH], BF16, PA), lnv=sb("wlnv", [128, CH], F32, PA),
                               rs=sb("wrs", [128, CH], F32, PA), t1=sb("wt1", [128, CH], F32, PA),
                               t2=sb("wt2", [128, CH], F32, PA))]
                    qaT = sb("qaT", [128, 4, CH], BF16, PA)
                    sga = sb("sga", [128, 4, CH], BF16, PA)
                    mixAs = [sb("mixA%d" % i, [128, 4, CH], BF16, PA) for i in range(2)]
                    mixBls = [sb("mixBl%d" % i, [128, 4, CH], BF16, PA) for i in range(2)]
                    mixBl_slots = [K.slot() for _ in range(2)]
                    xblk = [sb("xblk%d" % i, [128, 1024], F32, PA) for i in range(2)]
                    xblk_slot = [K.slot() for _ in range(2)]
                    nxb = [0]
                    pTs = [sb("pTs%d" % i, [128, 2, CH], BF16, PA) for i in range(3)]
                    x1b = [sb("x1b%d" % i, [128, 1024], F32, PA) for i in range(2)]
                    dcp = sb("dcp", [128, CH], F32, PA)
                    ncp = sb("ncp", [128, CH], F32, PA)
                    x1b_slot = [K.slot() for _ in range(2)]
                    qaTs = [qaT, sb("qaT1", [128, 4, CH], BF16, PA)]
                    sgas = [sga, sb("sga1", [128, 4, CH], BF16, PA)]
                    tabcs = [tabc, sb("tabc1", [128, 2, CH], F32, PA)]
                    tabgs = [tabg, sb("tabg1", [128, 2, CH], F32, PA)]
                    tabsl = [tab_slot, K.slot()]
                    nbuf = [0]
                    npt = [0]

                    held = set()

                    def take_buf():
                        while True:
                            i_ = nbuf[0] % 3
                            nbuf[0] += 1
                            if i_ not in held:
                                return psc[i_]

                    def hold(b_):
                        held.add(psc.index(b_))

                    def release(b_):
                        held.discard(psc.index(b_))

                    def projx(dst, c0, xT):
                        for kc in range(8):
                            K.mm(dst, wcol(kc, c0), xT[:, kc, :], start=(kc == 0), stop=(kc == 7))

                    def proj_items(Cn):
                        q_, g_ = qaTs[Cn % 2], sgas[Cn % 2]
                        tc_, tg_ = tabcs[Cn % 2], tabgs[Cn % 2]
                        xT = xnTs[Cn % 2]

                        def prep():
                            load_tab(tc_, tabsl[Cn % 2], tabA_d, Cn)
                            K.amul(tg_[:, 0, :], tc_[:, 0, :], gqk[:, 0:1])
                            K.amul(tg_[:, 1, :], tc_[:, 1, :], gqk[:, 1:2])

                        items = []
                        for t in range(4):
                            def mk(t=t):
                                st = {}
                                w_ = WS[t % 2]

                                def s1():
                                    st["buf"] = take_buf()
                                    hold(st["buf"])
                                    projx(st["buf"][:, 0, :], t * 128, xT)
                                    projx(st["buf"][:, 1, :], 512 + t * 128, xT)

                                def s2():
                                    pa_, pb_ = st["buf"][:, 0, :], st["buf"][:, 1, :]
                                    K.tt("dve", w_["t1"], pa_, tg_[:, 0, :], ALU.mult)
                                    K.tt("dve", w_["t2"], pb_, tg_[:, 1, :], ALU.mult)
                                    K.act(w_["sq"], pa_, AF.Square)

                                def s3():
                                    K.mm(st["buf"][:, 1, :], onesblk, w_["sq"])

                                def s4():
                                    K.act(w_["lnv"], st["buf"][:, 1, :], AF.Ln, bias=epsb[:, 0:1], scale=1.0 / 64.0)
                                    K.act(w_["rs"], w_["lnv"], AF.Exp, scale=-0.5)
                                    K.tt("pool", w_["t1"], w_["t1"], w_["t2"], ALU.add)
                                    K.tt("pool", q_[:, t, :], w_["t1"], w_["rs"], ALU.mult)
                                    release(st["buf"])
                                return [(s1, 3), (s2, 1), (s3, 2), (s4, 0)]
                            items.append(mk())
                        for t2_ in range(2):
                            def mk(t2_=t2_):
                                st = {}

                                def s1():
                                    st["buf"] = take_buf()
                                    hold(st["buf"])
                                    for j in range(2):
                                        projx(st["buf"][:, j, :], 1024 + (2 * t2_ + j) * 128, xT)

                                def s2():
                                    for j in range(2):
                                        K.act(WS[j]["t1"], st["buf"][:, j, :], AF.Tanh, scale=0.5)

                                def s3():
                                    for j in range(2):
                                        K.ts("dve", WS[j]["t1"], WS[j]["t1"], 0.5, ALU.mult, 0.5, ALU.add)
                                        K.tt("dve", g_[:, 2 * t2_ + j, :], st["buf"][:, j, :], WS[j]["t1"], ALU.mult)
                                    release(st["buf"])
                                return [(s1, 3), (s2, 1), (s3, 0)]
                            items.append(mk())
                        return prep, items

                    def outproj_items(Cc):
                        mA, mB = mixAs[Cc % 2], mixBls[Cc % 2]
                        items = []
                        for b in range(4):
                            def mk(b=b):
                                st = {}
                                bs = slice(b * 128, (b + 1) * 128)
                                r0 = Cc * CH + b * 128

                                def s1():
                                    i_ = nxb[0] % 2
                                    nxb[0] += 1
                                    st["i"] = i_
                                    K.dma("sp", xblk[i_], x_d[r0:r0 + 128, :], xblk_slot[i_])
                                    st["buf"] = take_buf()
                                    hold(st["buf"])
                                    py = st["buf"]
                                    for half in range(2):
                                        for f in range(8):
                                            src = mA[:, f, bs] if f < 4 else mB[:, f - 4, bs]
                                            K.mm(py[:, half, :], src, wobf[:, f, half * 512:(half + 1) * 512], start=(f == 0), stop=(f == 7))

                                def s2():
                                    xo = x1b[st["i"]]
                                    K.tt("dve", xo, st["buf"].re("p a c -> p (a c)"), xblk[st["i"]], ALU.add)
                                    K.dma("pool", x1_d[r0:r0 + 128, :], xo, x1b_slot[st["i"]])
                                    release(st["buf"])
                                return [(s1, 4), (s2, 0)]
                            items.append(mk())
                        return items

                    load_xnT(xnT0_d, 0)
                    prep0, items0 = proj_items(0)
                    prep0()
                    for it_ in items0:
                        for st_fn, _d in it_:
                            st_fn()
                    carry = []
                    for C in range(NCH):
                        use_xnT(C)
                        qaT_c, sga_c = qaTs[C % 2], sgas[C % 2]
                        mixA = mixAs[C % 2]
                        load_w_piece(whi, 0, wG1_d, C, 0, 384, 1)
                        pending = list(carry)
                        carry = []
                        if C + 1 < NCH:
                            load_xnT(xnT0_d, C + 1)
                            prepn, pitems = proj_items(C + 1)
                            prepn()
                            pending = pending + pitems
                        K.dma("pool", mixBls[C % 2], V(mixb_d.ap[:, :, C * CH:(C + 1) * CH].rearrange("h p c -> p h c"), mixb_d.res), mixBl_slots[C % 2])
                        nit = 0
                        active = [None]
                        for t in range(4):
                            kv = t // 2
                            fifo = []

                            def qk(kb_):
                                sc_ = take_buf()
                                fifo.append(sc_)
                                ks = slice(kb_ * 128, (kb_ + 1) * 128)
                                K.mm(sc_[:, 0, :], KaT[0:64, kv, ks], qaT_c[0:64, t, :])
                                K.mm(sc_[:, 1, :], KaT[64:128, kv, ks], qaT_c[64:128, t, :])

                            qk(0)
                            qk(1)
                            for kb in range(NB):
                                sc = fifo.pop(0)
                                pt = pTs[npt[0] % 3]
                                npt[0] += 1
                                nit += 1
                                K.act(pt, sc, AF.Exp, bias=maskA[:, C * NB + kb:C * NB + kb + 1], scale=0.125)
                                if kb + 2 < NB:
                                    qk(kb + 2)
                                if active[0] is None and pending and nit % 12 == 3:
                                    active[0] = [pending.pop(0), 0, nit]
                                if active[0] is not None and nit >= active[0][2]:
                                    stages_, si_, _due = active[0]
                                    fn_, delay_ = stages_[si_]
                                    fn_()
                                    if si_ + 1 < len(stages_):
                                        active[0] = [stages_, si_ + 1, nit + delay_]
                                    else:
                                        active[0] = None
                                st, sp_ = (kb == 0), (kb == NB - 1)
                                K.mm(pnum[0:64, :], Va[:, kb, kv * 64:(kv + 1) * 64], pt[:, 0, :], start=st, stop=sp_)
                                K.mm(pnum[64:128, :], Va[:, kb, kv * 64:(kv + 1) * 64], pt[:, 1, :], start=st, stop=sp_, tp=(0, 64))
                                K.mm(pden[0:64, :], ones[:, 0:64], pt[:, 0, :], start=st, stop=sp_)
                                K.mm(pden[64:128, :], ones[:, 0:64], pt[:, 1, :], start=st, stop=sp_, tp=(0, 64))
                            K.cp("dve", dcp, pden)
                            K.cp("dve", ncp, pnum)
                            K.recip(dcp, dcp)
                            K.tt("dve", ncp, ncp, dcp, ALU.mult)
                            K.tt("pool", mixA[:, t, :], ncp, sga_c[:, t, :], ALU.mult)
                        while active[0] is not None or pending:
                            if active[0] is None:
                                active[0] = [pending.pop(0), 0, 0]
                            stages_, si_, _due = active[0]
                            stages_[si_][0]()
                            active[0] = [stages_, si_ + 1, 0] if si_ + 1 < len(stages_) else None
                        carry = outproj_items(C)
                        if C == NCH - 1:
                            for it_ in carry:
                                for st_fn, _d in it_:
                                    st_fn()
                            carry = []

            tap("x1", x1_d, [NT, 1024], F32)
            checkpoint("LA")
            with Scope(K) as L1:
                KcT = sb("KcT", [128, 2, (NB + 2) * 128], BF16, L1)
                Vc = sb("Vc", [128, NB + 2, 128], BF16, L1)
                K.memset("pool", KcT[:, :, 0:128], 0.0)
                K.memset("pool", KcT[:, :, (NB + 1) * 128:(NB + 2) * 128], 0.0)
                K.memset("pool", Vc[:, 0, :], 0.0)
                K.memset("pool", Vc[:, NB + 1, :], 0.0)
                tap("EBT", EBT, [128, 16, 3, 128], BF16)
                checkpoint("EBT")
                wmode["base"] = 1536
                with Scope(K) as PG1:
                    pT = ps("pT", [128, 8, 128], BF16, PG1)
                    pa = ps("pa", [128, CH], F32, PG1)
                    pva = ps("pva", [128, 512], F32, PG1)[:, 0:128]
                    mxg1 = alloc_mx(PG1)
                    for C in range(NCH):
                        make_xnT(mxg1, x1_d, C, pT)
                        store_xnT(xnT1_d, C, C % 2)
                        load_w_piece(wlo, 0, wL1_d, C, 0, 1024, 1)
                        load_w_piece(wlo, 1024, wL1_d, C, 1024, 1536, 1)
                        load_w_piece(wobf, 0, woc_d, C, 0, 1024, None)
                        for t in range(2):
                            proj(pa, t * 128)
                            K.cp("act", KcT[:, t, (C * 4 + 1) * 128:(C * 4 + 5) * 128], pa)
                        for b in range(4):
                            for kc in range(8):
                                K.mm(pva, cur["xnT"][:, kc, b * 128:(b + 1) * 128], wcol(kc, 256), start=(kc == 0), stop=(kc == 7))
                            K.cp("dve", Vc[:, C * 4 + b + 1, :], pva)

                tap("KcT", KcT, [128, 2, (NB + 2) * 128], BF16)
                tap("Vc", Vc, [128, NB + 2, 128], BF16)
                checkpoint("G1")
                wmode["base"] = 0
                for kc_ in range(8):
                    load_w_piece(whi, 0, wL1_d, kc_, 1536, 2048, 1)
                with Scope(K) as PL1:
                    pbig = ps("pbig1", [128, 8, CH], F32, PL1)
                    pw = [subview(pbig, pbig.ap[:, 2 * i:2 * i + 2, :]) for i in range(2)]
                    pnum = subview(pbig, pbig.ap[:, 6, :])
                    pden = subview(pbig, pbig.ap[:, 7, :])
                    bk = [subview(pbig, pbig.ap[:, i, :]) for i in range(6)]
                    qcT = sb("qcT", [128, 8, CH], BF16, PL1)
                    sgc = sb("sgc", [128, 8, CH], BF16, PL1)
                    mixC = sb("mixC", [128, 8, CH], BF16, PL1)
                    pws = [sb("pws%d" % i, [128, 2, 3, 128], BF16, PL1) for i in range(2)]
                    pw2 = [sb("pw2%d" % i, [128, 2, 3, 128], BF16, PL1) for i in range(2)]
                    rs = sb("rs1", [128, CH], F32, PL1)
                    lnr = sb("lnr", [128, CH], F32, PL1)
                    t1 = sb("t11", [128, CH], F32, PL1)
                    EP = [dict(rs=rs, t1=t1, lnr=lnr),
                          dict(rs=sb("rs1b", [128, CH], F32, PL1), t1=sb("t11b", [128, CH], F32, PL1), lnr=sb("lnrb", [128, CH], F32, PL1))]
                    pnums = [pnum, bk[4]]
                    pdens = [pden, bk[5]]
                    x2 = [sb("x2%d" % i, [128, 1024], F32, PL1) for i in range(2)]
                    yo = [sb("yo%d" % i, [128, 1024], F32, PL1) for i in range(2)]
                    yo_slot = [K.slot() for _ in range(2)]
                    ss2 = sb("ss2", [128, 2], F32, PL1)
                    ln2 = sb("ln2", [128, 2], F32, PL1)
                    r2 = sb("r2", [128, 2], F32, PL1)
                    it = 0
                    mxl = alloc_mx(PL1, full=False)
                    xch = mxl.xch
                    junk = sb("junk1", [128, 1024], BF16, PL1)
                    fnbc = sb("fnbc", [128, 1024], F32, PL1)
                    fn_slot = K.slot()
                    K.dma("sp", fnbc, V(fn_d.ap.to_broadcast([128, 1024]), fn_d.res), fn_slot)
                    load_xnT(xnT1_d, 0)
                    for C in range(NCH):
                        use_xnT(C)
                        if C + 1 < NCH:
                            load_xnT(xnT1_d, C + 1)
                        load_x(mxl, x1_d, C)
                        handoff(pw, bk[0:4])
                        for t in range(8):
                            pa_ = bk[t % 6]
                            proj(pa_, t * 128)
                            K.cp("act" if t % 2 == 0 else "dve", qcT[:, t, :], pa_)
                        for t in range(8):
                            pa_ = bk[(t + 2) % 6]
                            proj(pa_, 1024 + t * 128)
                            K.act(sgc[:, t, :], pa_, AF.Silu)
                        handoff(bk[0:4], pw)
                        items = [(t, qi) for t in range(8) for qi in range(4)]

                        def wqk(t, qi, w):
                            kv = t // 4
                            i = C * 4 + qi
                            qs = slice(qi * 128, (qi + 1) * 128)
                            for o in range(3):
                                sl = 2 - o
                                ks = slice((i + o) * 128, (i + o + 1) * 128)
                                K.mm(w[:, 0, sl * 128:(sl + 1) * 128], KcT[0:64, kv, ks], qcT[0:64, t, qs])
                                K.mm(w[:, 1, sl * 128:(sl + 1) * 128], KcT[64:128, kv, ks], qcT[64:128, t, qs])

                        wqk(items[0][0], items[0][1], pw[it % 2])
                        deferred = []
                        for idx, (t, qi) in enumerate(items):
                            kv = t // 4
                            i = C * 4 + qi
                            qs = slice(qi * 128, (qi + 1) * 128)
                            w = pw[it % 2]
                            s1 = pws[it % 2]
                            s2 = pw2[it % 2]
                            it += 1
                            if i in (0, NB // 2 - 1, NB // 2, NB - 1):
                                for o in range(3):
                                    sl = 2 - o
                                    K.act(s1[:, :, sl, :], w[:, :, sl * 128:(sl + 1) * 128], AF.Exp, bias=maskW[:, i * 3 + o:i * 3 + o + 1], scale=0.125)
                            else:
                                K.act(s1, w[:, :, 0:384].re("p h (o q) -> p h o q", o=3), AF.Exp, scale=0.125)
                            K.tt("dve", s2, s1, EBT[:, 2 * t:2 * t + 2, :, :], ALU.mult)
                            if deferred:
                                deferred.pop(0)()
                            if idx + 1 < len(items):
                                wqk(items[idx + 1][0], items[idx + 1][1], pw[it % 2])
                            pnum_, pden_ = pnums[t % 2], pdens[t % 2]
                            for o in range(3):
                                sl = 2 - o
                                st, sp_ = (o == 0), (o == 2)
                                vv = Vc[:, i + o, kv * 64:(kv + 1) * 64]
                                K.mm(pnum_[0:64, qs], vv, s2[:, 0, sl, :], start=st, stop=sp_)
                                K.mm(pnum_[64:128, qs], vv, s2[:, 1, sl, :], start=st, stop=sp_, tp=(0, 64))
                                K.mm(pden_[0:64, qs], ones[:, 0:64], s2[:, 0, sl, :], start=st, stop=sp_)
                                K.mm(pden_[64:128, qs], ones[:, 0:64], s2[:, 1, sl, :], start=st, stop=sp_, tp=(0, 64))
                            if qi == 3:
                                def epi(t=t, pnum_=pnum_, pden_=pden_):
                                    e_ = EP[t % 2]
                                    K.ts("dve", e_["rs"], pden_, esk[:, t:t + 1], ALU.add)
                                    K.cp("dve", e_["t1"], pnum_)
                                    K.act(e_["lnr"], e_["rs"], AF.Ln)
                                    K.act(e_["rs"], e_["lnr"], AF.Exp, scale=-1.0)
                                    K.tt("pool", e_["t1"], e_["t1"], e_["rs"], ALU.mult)
                                    K.tt("pool", mixC[:, t, :], e_["t1"], sgc[:, t, :], ALU.mult)
                                deferred.append(epi)
                        while deferred:
                            deferred.pop(0)()
                        for b in range(4):
                            bs = slice(b * 128, (b + 1) * 128)
                            py = pw[b % 2]
                            for half in range(2):
                                for f in range(8):
                                    K.mm(py[:, half, :], mixC[:, f, bs], wobf[:, f, half * 512:(half + 1) * 512], start=(f == 0), stop=(f == 7))
                            xo = x2[b % 2]
                            K.tt("dve", xo, py.re("p a c -> p (a c)"), xch[:, b, :], ALU.add)
                            K.act(junk, xo, AF.Square, accum=ss2[:, b % 2:b % 2 + 1])
                            K.act(ln2[:, b % 2:b % 2 + 1], ss2[:, b % 2:b % 2 + 1], AF.Ln, bias=epsb[:, 0:1], scale=1.0 / 1024.0)
                            K.act(r2[:, b % 2:b % 2 + 1], ln2[:, b % 2:b % 2 + 1], AF.Exp, scale=-0.5)
                            yb = yo[b % 2]
                            K.ts("dve", yb, xo, r2[:, b % 2:b % 2 + 1], ALU.mult)
                            K.tt("pool", yb, yb, fnbc, ALU.mult)
                            r0 = C * CH + b * 128
                            K.dma("pool", y_d[r0:r0 + 128, :], yb, yo_slot[b % 2])
    except StopBuild:
        pass
    for s_ in K.slots:
        if s_.cnt:
            nc.gpsimd.wait_ge(s_.sem, s_.cnt)
    return nc, K


def _t5_bucket(rel):
    half = 16
    max_exact = 8
    ret = (rel > 0).astype(np.int32) * half
    dist = np.abs(rel)
    large = max_exact + (np.log(np.maximum(dist, 1) / max_exact) / np.log(128 / max_exact) * (half - max_exact)).astype(np.int32)
    large = np.minimum(large, half - 1)
    return ret + np.where(dist < max_exact, dist, large)


def _static_tables():
    f32 = np.float32
    st = {}
    st["ident"] = np.eye(128, dtype=f32)
    st["aident"] = np.ascontiguousarray(np.eye(128, dtype=f32)[::-1])
    ob = np.zeros((128, 128), f32)
    ob[:64, :64] = 1
    ob[64:, 64:] = 1
    st["onesblk"] = ob
    j = np.arange(128)[:, None]
    i = np.arange(128)[None, :]
    mmat = np.zeros((128, 4, 128), f32)
    mmat[:, 0, :] = np.maximum(i - j, 0)
    mmat[:, 1, :] = (i >= j)
    mmat[:, 2, :] = np.maximum(j - i, 0)
    mmat[:, 3, :] = (j > i)
    st["mmat"] = mmat
    c = np.arange(512) % 128
    iot = np.zeros((128, 4, 512), f32)
    iot[:, 0, :] = c + 1
    iot[:, 1, :] = 128 - c
    iot[:, 2, :] = 127 - c
    iot[:, 3, :] = c
    st["iot"] = iot
    m = np.arange(640)
    rel = 255 - m
    bk = _t5_bucket(rel)
    oh = np.zeros((32, 640), f32)
    oh[bk, m] = 1
    st["oh"] = oh
    st["inwin"] = np.broadcast_to((np.abs(rel) <= 128).astype(f32)[None, :], (16, 640)).copy()
    return st


def _core_tables(is_prompt):
    f32 = np.float32
    seqlen = 4096 if is_prompt else 2048
    t = np.arange(NT) % seqlen
    d = np.arange(128) % 64
    pair = d // 2
    sgn = np.where(d % 2 == 0, -1.0, 1.0)
    quarter = 16
    freqs = (np.float32(10000.0) ** (-np.arange(quarter, dtype=f32) / quarter)).astype(f32)
    row = (t // 64).astype(f32)
    col = (t % 64).astype(f32)
    ang = np.concatenate([row[:, None] * freqs, col[:, None] * freqs], axis=-1).astype(f32)
    angd = ang[:, pair].T.astype(np.float64)
    tabA = np.stack([np.cos(angd), np.sin(angd) * sgn[:, None]]).astype(f32)
    half = 32
    freqs_b = (np.float32(10000.0) ** (-np.arange(half, dtype=f32) / half)).astype(f32)
    angb = (t.astype(f32)[:, None] * freqs_b).astype(f32)
    angbd = angb[:, pair].T.astype(np.float64)
    tabB = np.stack([np.cos(angbd), np.sin(angbd) * sgn[:, None]]).astype(f32)
    seq_of_blk = (np.arange(NB) * 128) // seqlen
    maskA = np.zeros((NCH, NB), f32)
    for C in range(NCH):
        sq = (C * CH) // seqlen
        maskA[C, :] = np.where(seq_of_blk == sq, 0.0, NEG)
    maskA = np.broadcast_to(maskA.reshape(1, -1), (128, NCH * NB)).copy()
    maskW = np.zeros((NB, 3), f32)
    for i in range(NB):
        for o in range(3):
            jb = i + o - 1
            if jb < 0 or jb >= NB or seq_of_blk[jb] != seq_of_blk[i]:
                maskW[i, o] = NEG
    maskW = np.broadcast_to(maskW.reshape(1, -1), (128, NB * 3)).copy()
    cps = seqlen // 128
    rf = np.array([0.0 if (n % cps == 0) else 1.0 for n in range(NB)], f32)
    rb = np.array([0.0 if (n % cps == cps - 1) else 1.0 for n in range(NB)], f32)
    rfb = np.broadcast_to(np.concatenate([rf, rb])[None, :], (128, 64)).copy()
    return {"tabA": tabA, "tabB": tabB, "maskA": maskA, "maskW": maskW, "rfb": rfb}


def _swap(cols):
    cols = np.asarray(cols)
    return cols ^ 1


def _prep_common(norm_g, w_in_ab, qk_norm_a, ret_decay, w_out_ab, w_in_c, sink_c, w_out_c, rel_bias, final_norm):
    f32 = np.float32
    W = np.asarray(w_in_ab[0], f32)
    qa = np.arange(0, 512)
    ka = np.arange(512, 640)
    va = np.arange(640, 768)
    ga = np.arange(768, 1280)
    qb = np.arange(1280, 1536)
    kb = np.arange(1536, 1792)
    vb = np.arange(1792, 2304)
    gb = np.arange(2304, 2816)
    kadup = np.concatenate([ka[0:64], ka[0:64], ka[64:128], ka[64:128]])
    cm = {}
    cm["wG"] = np.ascontiguousarray(W[:, np.concatenate([kadup, _swap(kadup), kb, _swap(kb), va, vb])])
    cm["wLB"] = np.ascontiguousarray(W[:, np.concatenate([qb, _swap(qb), kb, _swap(kb), gb, vb])])
    cm["wLA"] = np.ascontiguousarray(W[:, np.concatenate([qa, _swap(qa), ga])])
    cm["woab"] = np.ascontiguousarray(np.asarray(w_out_ab[0], f32))
    Wc = np.asarray(w_in_c[0], f32)
    kc = np.arange(1024, 1152)
    kcdup = np.concatenate([kc[0:64], kc[0:64], kc[64:128], kc[64:128]])
    cm["wG1"] = np.ascontiguousarray(Wc[:, np.concatenate([kcdup, np.arange(1152, 1280)])])
    cm["wL1"] = np.ascontiguousarray(Wc[:, np.concatenate([np.arange(0, 1024), np.arange(1280, 2304)])])
    cm["woc"] = np.ascontiguousarray(np.asarray(w_out_c[0], f32))
    ng = np.asarray(norm_g, f32)
    cm["gcol"] = np.ascontiguousarray(ng.reshape(2, 8, 128).transpose(2, 0, 1).reshape(128, 16))
    cm["fn"] = np.asarray(final_norm, f32).reshape(1, 1024).copy()
    g = np.asarray(qk_norm_a[0], f32)
    d = np.arange(128) % 64
    cm["gqk"] = np.stack([g[0][d], g[0][d ^ 1], g[1][d], g[1][d ^ 1]], axis=1).astype(f32).copy()
    rd = np.asarray(ret_decay[0], f32)
    hp = (np.arange(128) // 64)
    rdec = np.zeros((128, 12), f32)
    for p in range(2):
        rdec[:, p] = rd[0][2 * p + hp]
        rdec[:, 2 + p] = rd[1][2 * p + hp]
    for h in range(4):
        rdec[:, 4 + h] = rd[0][h]
        rdec[:, 8 + h] = rd[1][h]
    cm["rdec"] = rdec
    sk = np.asarray(sink_c[0], f32)
    sinkl = np.zeros((128, 8), f32)
    for t in range(8):
        sinkl[:, t] = sk[2 * t + hp]
    cm["sinkl"] = sinkl
    cm["relb"] = np.ascontiguousarray(np.asarray(rel_bias, f32))
    cm.update(_static_tables())
    return cm


_CACHE = {}


def kernel(x_prompt, x_sample, norm_g, w_in_ab, qk_norm_a, ret_decay, w_out_ab, w_in_c, sink_c, w_out_c, rel_bias, final_norm):
    xp = np.asarray(x_prompt, np.float32)
    xs = np.asarray(x_sample, np.float32)
    cm = _prep_common(norm_g, w_in_ab, qk_norm_a, ret_decay, w_out_ab, w_in_c, sink_c, w_out_c, rel_bias, final_norm)
    tp = _core_tables(True)
    tsm = _core_tables(False)
    in_maps = []
    for c in range(8):
        m = dict(cm)
        if c < 4:
            m["x"] = np.ascontiguousarray(xp[c])
            m.update(tp)
        else:
            m["x"] = np.ascontiguousarray(xs[2 * (c - 4):2 * (c - 4) + 2].reshape(NT, 1024))
            m.update(tsm)
        in_maps.append(m)
    if "nc" not in _CACHE:
        _CACHE["nc"] = build_program()[0]
    nc = _CACHE["nc"]
    res = run_bass_kernel_spmd(nc, in_maps, core_ids=list(range(8)))
    outs = [np.asarray(r["y"], np.float32) for r in res.results]
    y_prompt = np.stack(outs[0:4], axis=0)
    y_sample = np.stack(outs[4:8], axis=0).reshape(8, 2048, 1024)
    return (y_prompt, y_sample)
```

```python
import numpy as np
import concourse.bass as bass
import concourse.mybir as mybir
from concourse.bass_utils import run_bass_kernel_spmd

F32 = mybir.dt.float32
BF16 = mybir.dt.bfloat16
AF = mybir.ActivationFunctionType
ALU = mybir.AluOpType

NT = 4096
NB = 32
CH = 512
NCH = 8
EPS = 1e-6
NEG = -30000.0


class Prod:
    def __init__(self, sem, inc):
        self.sem = sem
        self.inc = inc
        self.cnt = 0


class Res:
    def __init__(self):
        self.w = {}
        self.r = {}
        self.excl = False


class V:
    def __init__(self, ap, res=None):
        self.ap = ap
        self.res = res if res is not None else Res()

    def __getitem__(self, k):
        return V(self.ap[k], self.res)

    def re(self, pat, **kw):
        return V(self.ap.rearrange(pat, **kw), self.res)

    def bc(self, shape):
        return V(self.ap.to_broadcast(shape), self.res)


class Ker:
    def __init__(self, nc):
        self.nc = nc
        self.eng = {"pe": nc.tensor, "act": nc.scalar, "dve": nc.vector, "pool": nc.gpsimd, "sp": nc.sync}
        self.prod = {}
        for n in ("pe", "act", "dve", "pool"):
            self.prod[n] = Prod(nc.alloc_semaphore("s_" + n), 1)
        self.seen = {n: {} for n in self.eng}
        self.nslot = 0
        self.ninstr = 0

    def slot(self):
        self.nslot += 1
        p = Prod(self.nc.alloc_semaphore("d%d" % self.nslot), 16)
        if hasattr(self, "slots"):
            self.slots.append(p)
        return p

    def _wait(self, en, reads, writes):
        deps = {}
        for v in reads:
            for p, i in v.res.w.items():
                deps[p] = max(deps.get(p, 0), i)
        for v in writes:
            for p, i in v.res.w.items():
                deps[p] = max(deps.get(p, 0), i)
            for p, i in v.res.r.items():
                deps[p] = max(deps.get(p, 0), i)
        e = self.eng[en]
        seen = self.seen[en]
        own = self.prod.get(en)
        for p, i in deps.items():
            if p is own and en == "pe":
                continue
            if seen.get(p, 0) >= i:
                continue
            e.wait_ge(p.sem, i)
            seen[p] = i

    def op(self, en, fn, reads, writes):
        writes = list(writes) + [r for r in reads if r.res.excl]
        self._wait(en, reads, writes)
        ins = fn(self.eng[en])
        p = self.prod[en]
        p.cnt += 1
        ins.then_inc(p.sem, 1)
        for v in reads:
            v.res.r[p] = p.cnt
        for v in writes:
            v.res.w[p] = p.cnt
        self.ninstr += 1

    def dma(self, q, out, in_, slot):
        self._wait(q, [in_], [out])
        ins = self.eng[q].dma_start(out=out.ap, in_=in_.ap)
        slot.cnt += 16
        ins.then_inc(slot.sem, 16)
        in_.res.r[slot] = slot.cnt
        out.res.w[slot] = slot.cnt

    def mm(self, out, lhsT, rhs, start=True, stop=True, tp=None):
        kw = {}
        if tp is not None:
            kw["tile_position"] = tp
        self.op("pe", lambda e: e.matmul(out.ap, lhsT.ap, rhs.ap, start=start, stop=stop, **kw), [lhsT, rhs], [out])

    def tr(self, out, in_, ident):
        self.op("pe", lambda e: e.transpose(out.ap, in_.ap, ident.ap), [in_, ident], [out])

    def act(self, out, in_, func, bias=None, scale=1.0, accum=None):
        reads = [in_]
        kw = {}
        if bias is not None:
            if isinstance(bias, V):
                reads.append(bias)
                kw["bias"] = bias.ap
            else:
                kw["bias"] = bias
        if isinstance(scale, V):
            reads.append(scale)
            kw["scale"] = scale.ap
        else:
            kw["scale"] = scale
        writes = [out]
        if accum is not None:
            writes.append(accum)
            kw["accum_out"] = accum.ap
        self.op("act", lambda e: e.activation(out.ap, in_.ap, func, **kw), reads, writes)

    def tt(self, en, out, a, b, op):
        self.op(en, lambda e: e.tensor_tensor(out.ap, a.ap, b.ap, op), [a, b], [out])

    def stt(self, en, out, in0, scalar, in1, op0, op1):
        reads = [in0, in1]
        s = scalar
        if isinstance(scalar, V):
            reads.append(scalar)
            s = scalar.ap
        self.op(en, lambda e: e.scalar_tensor_tensor(out.ap, in0.ap, s, in1.ap, op0, op1), reads, [out])

    def ts(self, en, out, in0, s1, op0, s2=None, op1=None):
        reads = [in0]
        a1 = s1
        if isinstance(s1, V):
            reads.append(s1)
            a1 = s1.ap
        a2 = s2
        if isinstance(s2, V):
            reads.append(s2)
            a2 = s2.ap
        if op1 is None:
            self.op(en, lambda e: e.tensor_scalar(out.ap, in0.ap, a1, None, op0), reads, [out])
        else:
            self.op(en, lambda e: e.tensor_scalar(out.ap, in0.ap, a1, a2, op0, op1), reads, [out])

    def cp(self, en, out, in_):
        if en == "act":
            self.op("act", lambda e: e.copy(out.ap, in_.ap), [in_], [out])
        else:
            self.op(en, lambda e: e.tensor_copy(out.ap, in_.ap), [in_], [out])

    def amul(self, out, in_, m):
        self.op("act", lambda e: e.mul(out.ap, in_.ap, m.ap), [in_, m], [out])

    def recip(self, out, in_):
        self.op("dve", lambda e: e.reciprocal(out.ap, in_.ap), [in_], [out])

    def memset(self, en, out, val):
        self.op(en, lambda e: e.memset(out.ap, val), [], [out])


class StopBuild(Exception):
    pass


import contextlib


class Scope(contextlib.ExitStack):
    def __init__(self, K):
        super().__init__()
        self.K = K
        self.tiles = []

    def __exit__(self, *a):
        fr = self.K.freed
        for v in self.tiles:
            for d in (v.res.w, v.res.r):
                for p, i in d.items():
                    fr[p] = max(fr.get(p, 0), i)
        self.tiles = []
        return super().__exit__(*a)

    def close(self):
        self.__exit__(None, None, None)


def build_program(stop=None, taps=()):
    nc = bass.Bass("TRN2", target_bir_lowering=False)
    K = Ker(nc)
    K.slots = []
    K.freed = {}
    K.tapped = {}

    def checkpoint(name):
        if stop == name:
            raise StopBuild()

    def tap(name, v, shape, dt=F32):
        if name not in taps or name in K.tapped:
            return
        d = V(nc.dram_tensor("dbg_" + name, list(shape), dt, kind="ExternalOutput").ap())
        K.tapped[name] = d
        K.dma("sp", d, v, K.slot())

    def din(name, shape, dt=F32):
        return V(nc.dram_tensor(name, list(shape), dt, kind="ExternalInput").ap())

    x_d = din("x", [NT, 1024])
    wG_d = din("wG", [1024, 1664])
    wLB_d = din("wLB", [1024, 2048])
    wLA_d = din("wLA", [1024, 1536])
    woab_d = din("woab", [1024, 1024])
    wG1_d = din("wG1", [1024, 384])
    wL1_d = din("wL1", [1024, 2048])
    woc_d = din("woc", [1024, 1024])
    gcol_d = din("gcol", [128, 16])
    fn_d = din("fn", [1, 1024])
    gqk_d = din("gqk", [128, 4])
    rdec_d = din("rdec", [128, 12])
    sink_d = din("sinkl", [128, 8])
    relb_d = din("relb", [32, 16])
    ident_d = din("ident", [128, 128])
    onesblk_d = din("onesblk", [128, 128])
    mm_d = din("mmat", [128, 4, 128])
    iot_d = din("iot", [128, 4, 512])
    oh_d = din("oh", [32, 640])
    inwin_d = din("inwin", [16, 640])
    tabA_d = din("tabA", [2, 128, NT])
    tabB_d = din("tabB", [2, 128, NT])
    maskA_d = din("maskA", [128, 256])
    maskW_d = din("maskW", [128, 96])
    rfb_d = din("rfb", [128, 64])
    y_d = V(nc.dram_tensor("y", [NT, 1024], F32, kind="ExternalOutput").ap())
    x1_d = V(nc.dram_tensor("x1s", [NT, 1024], F32, kind="Internal").ap())
    mixb_d = V(nc.dram_tensor("mixbs", [4, 128, NT], BF16, kind="Internal").ap())
    vec_h = nc.dram_tensor("vecs", [16, 640], BF16, kind="Internal")
    vec_d = V(vec_h.ap())
    aident_d = din("aident", [128, 128])
    xnT0_d = V(nc.dram_tensor("xnT0s", [NCH, 128, 8, CH], BF16, kind="Internal").ap())
    xnT1_d = V(nc.dram_tensor("xnT1s", [NCH, 128, 8, CH], BF16, kind="Internal").ap())

    es = Scope(K)
    uid = [0]

    def sb(name, shape, dt=F32, stack=None):
        uid[0] += 1
        st_ = stack if stack is not None else es
        t = st_.enter_context(nc.sbuf_tensor("sb%d_%s" % (uid[0], name), list(shape), dt))
        v = V(t[:])
        v.res.w = dict(K.freed)
        st_.tiles.append(v)
        return v

    def ps(name, shape, dt=F32, stack=None):
        uid[0] += 1
        st_ = stack if stack is not None else es
        t = st_.enter_context(nc.psum_tensor("ps%d_%s" % (uid[0], name), list(shape), dt))
        v = V(t[:])
        v.res.excl = True
        v.res.w = dict(K.freed)
        st_.tiles.append(v)
        return v

    try:
        with es:
            cslot = K.slot()
            consts = []

            def cload(name, src, shape, dt=F32, q="sp"):
                t = sb(name, shape, dt)
                K.dma(q, t, src, cslot)
                consts.append(t)
                return t

            gcol = cload("gcol", gcol_d, [128, 16])
            gqk = cload("gqk", gqk_d, [128, 4])
            rdec = cload("rdec", rdec_d, [128, 12])
            sinkl = cload("sinkl", sink_d, [128, 8])
            maskA = cload("maskA", maskA_d, [128, 256])
            maskW = cload("maskW", maskW_d, [128, 96])
            rfb = cload("rfb", rfb_d, [128, 64])
            ident32 = cload("ident32", ident_d, [128, 128])
            onesblk32 = cload("onesblk32", onesblk_d, [128, 128])
            for c in consts:
                c.res.w[cslot] = cslot.cnt
            ident = sb("ident", [128, 128], BF16)
            onesblk = sb("onesblk", [128, 128], BF16)
            ones = sb("ones", [128, 128], BF16)
            epsb = sb("epsb", [128, 1])
            K.cp("dve", ident, ident32)
            K.cp("dve", onesblk, onesblk32)
            K.memset("dve", ones, 1.0)
            K.memset("dve", epsb, EPS)

            checkpoint("c0")
            wbf = sb("wbf", [128, 8, 2048], BF16)
            wobf = sb("wobf", [128, 8, 1024], BF16)
            wst = [sb("wst%d" % i, [128, 1024]) for i in range(2)]
            wst_slot = [K.slot() for _ in range(2)]
            xnTs = [sb("xnT%d" % i, [128, 8, CH], BF16) for i in range(2)]
            xnT_slot = [K.slot() for _ in range(2)]
            cur = {"xnT": xnTs[0]}
            xch_slot = [K.slot() for _ in range(4)]
            xst_slot = [K.slot() for _ in range(2)]
            EBT = sb("EBT", [128, 16, 3, 128], BF16)
            esk = sb("esk", [128, 8], F32)
            K.act(esk, sinkl, AF.Exp)
            with Scope(K) as S1:
                relb = sb("relb", [32, 16], F32, S1)
                oh = sb("oh", [32, 640], F32, S1)
                inw = sb("inw", [16, 640], F32, S1)
                e_slot = K.slot()
                K.dma("sp", relb, relb_d, e_slot)
                K.dma("sp", oh, oh_d, e_slot)
                K.dma("sp", inw, inwin_d, e_slot)
                for t_ in (relb, oh, inw):
                    t_.res.w[e_slot] = e_slot.cnt
                pv = ps("pv", [16, 1024], F32, S1)[:, 0:640]
                vec = sb("vec", [16, 640], F32, S1)
                vecb = sb("vecb", [16, 640], BF16, S1)
                K.mm(pv[:, 0:512], relb, oh[:, 0:512])
                K.mm(pv[:, 512:640], relb, oh[:, 512:640])
                K.act(vec, pv, AF.Exp)
                K.tt("dve", vecb, vec, inw, ALU.mult)
                v_slot = K.slot()
                K.dma("sp", vec_d, vecb, v_slot)
                g_slot = K.slot()
                aid32 = sb("aid32", [128, 128], F32, S1)
                K.dma("sp", aid32, aident_d, g_slot)
                aid = sb("aid", [128, 128], BF16, S1)
                K.cp("dve", aid, aid32)
                TT = sb("TT", [128, 16 * 384], BF16, S1)
                src = V(bass.AP(vec_h, 0, [[1, 128], [640, 16], [1, 384]]), vec_d.res)
                K.dma("sp", TT.re("p (h j) -> p h j", h=16), src, g_slot)
                prev = ps("prev", [128, 2, CH], F32, S1)
                EBTf = EBT.re("p h o q -> p (h o q)")
                for n_ in range(12):
                    K.mm(prev[:, n_ % 2, :], aid, TT[:, n_ * 512:(n_ + 1) * 512])
                    K.cp("act" if n_ % 2 == 0 else "dve", EBTf[:, n_ * 512:(n_ + 1) * 512], prev[:, n_ % 2, :])
            wcount = [0]

            def handoff(srcs, dsts):
                for d_ in dsts:
                    for s_ in srcs:
                        for dd in (s_.res.w, s_.res.r):
                            for p_, i_ in dd.items():
                                d_.res.w[p_] = max(d_.res.w.get(p_, 0), i_)

            def subview(parent, ap):
                v = V(ap)
                v.res.excl = parent.res.excl
                v.res.w = dict(parent.res.w)
                return v

            class MX:
                pass

            def alloc_mx(scope, full=True):
                m = MX()
                m.xch = sb("xch", [128, 4, 1024], F32, scope)
                if full:
                    m.xn = [sb("xn%d" % i, [128, 1024], BF16, scope) for i in range(2)]
                    m.junk = sb("junk", [128, 1024], BF16, scope)
                    m.ss = sb("ss", [128, 4], F32, scope)
                    m.lnv4 = sb("lnv4", [128, 4], F32, scope)
                    m.rstd4 = sb("rstd4", [128, 4], F32, scope)
                return m

            def load_x(m, src_d, C):
                for b in range(4):
                    r0 = C * CH + b * 128
                    K.dma("sp", m.xch[:, b, :], src_d[r0:r0 + 128, :], xch_slot[b])

            def store_xnT(dst_d, C, slot_i):
                K.dma("pool", dst_d[C], cur["xnT"], xst_slot[slot_i])

            def load_xnT(src_d, C):
                i = C % 2
                K.dma("sp", xnTs[i], src_d[C], xnT_slot[i])

            def use_xnT(C):
                cur["xnT"] = xnTs[C % 2]

            def load_w_piece(dst, d0, src_d, kc, c0, c1, layer_g):
                i = wcount[0] % 2
                wcount[0] += 1
                n_ = c1 - c0
                K.dma("sp", wst[i][:, 0:n_], src_d[kc * 128:(kc + 1) * 128, c0:c1], wst_slot[i])
                en = "act" if (wcount[0] % 2 == 0) else "dve"
                if layer_g is None:
                    K.cp(en, dst[:, kc, d0:d0 + n_], wst[i][:, 0:n_])
                elif en == "act":
                    K.amul(dst[:, kc, d0:d0 + n_], wst[i][:, 0:n_], gcol[:, layer_g * 8 + kc:layer_g * 8 + kc + 1])
                else:
                    K.ts("dve", dst[:, kc, d0:d0 + n_], wst[i][:, 0:n_], gcol[:, layer_g * 8 + kc:layer_g * 8 + kc + 1], ALU.mult)

            def load_w(dst, src_d, ncols, layer_g):
                for kc in range(8):
                    for c0 in range(0, ncols, 1024):
                        c1 = min(ncols, c0 + 1024)
                        load_w_piece(dst, c0, src_d, kc, c0, c1, layer_g)

            wlo = V(wbf.ap[:, :, 0:1536])
            whi = V(wbf.ap[:, :, 1536:2048])
            wmode = {"split": False, "base": 0}

            def wcol(kc, c0, n_=128):
                c0 = c0 + wmode["base"]
                if not wmode["split"]:
                    return wbf[:, kc, c0:c0 + n_]
                if c0 + n_ <= 1536:
                    return wlo[:, kc, c0:c0 + n_]
                return whi[:, kc, c0 - 1536:c0 - 1536 + n_]

            def xnT_front(m, src_d, C):
                load_x(m, src_d, C)
                for b in range(4):
                    K.act(m.junk, m.xch[:, b, :], AF.Square, accum=m.ss[:, b:b + 1])
                K.act(m.lnv4, m.ss, AF.Ln, bias=epsb[:, 0:1], scale=1.0 / 1024.0)
                K.act(m.rstd4, m.lnv4, AF.Exp, scale=-0.5)

            def xnT_back(m, C, pTl):
                xnT = xnTs[C % 2]
                for b in range(4):
                    xb = m.xn[b % 2]
                    pT_ = pTl[b % len(pTl)]
                    K.ts("dve", xb, m.xch[:, b, :], m.rstd4[:, b:b + 1], ALU.mult)
                    for kc in range(8):
                        K.tr(pT_[:, kc, :], xb[:, kc * 128:(kc + 1) * 128], ident)
                    K.cp("act" if b % 2 == 0 else "dve", xnT[:, :, b * 128:(b + 1) * 128], pT_)

            def make_xnT(m, src_d, C, pT):
                use_xnT(C)
                xnT_front(m, src_d, C)
                xnT_back(m, C, pT if isinstance(pT, list) else [pT])

            def proj(dst, c0):
                for kc in range(8):
                    K.mm(dst, wcol(kc, c0), cur["xnT"][:, kc, :], start=(kc == 0), stop=(kc == 7))

            def rsq_bcast(dst, src_ps, nfeat, sq, psn, lnv, lhs_ones):
                K.act(sq, src_ps, AF.Square)
                K.mm(psn, lhs_ones, sq)
                K.act(lnv, psn, AF.Ln, bias=epsb[:, 0:1], scale=1.0 / nfeat)
                K.act(dst, lnv, AF.Exp, scale=-0.5)

            with Scope(K) as L0:
                LR = Scope(K)
                KaT = sb("KaT", [128, 2, NT], BF16, L0)
                Va = sb("Va", [128, NB, 128], BF16, L0)
                tabc = sb("tabc", [128, 2, CH], F32, L0)
                tab_slot = K.slot()
                tabd_slot = K.slot()
                sq = sb("sq", [128, CH], BF16, L0)
                lnv = sb("lnv", [128, CH], F32, L0)
                rs = sb("rs", [128, CH], F32, L0)
                t1 = sb("t1", [128, CH], F32, L0)
                t2 = sb("t2", [128, CH], F32, L0)
                tabg = sb("tabg", [128, 2, CH], F32, L0)
                SbAll = sb("SbAll", [128, 2, NB, 128], BF16, LR)
                tabd = sb("tabd", [128, 2, CH], F32, LR)
                vbtm = sb("vbtm", [128, 4, 512], BF16, LR)
                lg = sb("lg", [128, 12], F32, LR)
                K.act(lg, rdec, AF.Exp)
                K.ts("dve", lg, lg, -1.0, ALU.mult)
                cd = sb("cd", [128, 4], F32, LR)
                K.act(cd, lg[:, 0:4], AF.Exp, scale=128.0)
                cdr = sb("cdr", [128, 4, NB], F32, LR)
                for j in range(4):
                    off = 0 if j < 2 else 32
                    K.ts("dve", cdr[:, j, :], rfb[:, off:off + 32], cd[:, j:j + 1], ALU.mult)
                checkpoint("c1")
                QF4 = sb("QF4", [128, 2, CH], F32, LR)
                QB4 = sb("QB4", [128, 2, CH], F32, LR)
                KF4 = sb("KF4", [128, 2, CH], F32, LR)
                KB4 = sb("KB4", [128, 2, CH], F32, LR)
                DT = sb("DT", [128, 4, 128], F32, LR)
                with Scope(K) as S0:
                    iot = sb("iot", [128, 4, CH], F32, S0)
                    K.dma("sp", iot, iot_d, tabd_slot)
                    for p in range(2):
                        K.act(QF4[:, p, :], iot[:, 0, :], AF.Exp, scale=lg[:, p:p + 1])
                        K.act(QB4[:, p, :], iot[:, 1, :], AF.Exp, scale=lg[:, 2 + p:3 + p])
                        K.act(KF4[:, p, :], iot[:, 2, :], AF.Exp, scale=lg[:, p:p + 1])
                        K.act(KB4[:, p, :], iot[:, 3, :], AF.Exp, scale=lg[:, 2 + p:3 + p])
                    K.ts("dve", KF4, KF4, 0.125, ALU.mult)
                    K.ts("dve", KB4, KB4, 0.125, ALU.mult)
                    mmat = sb("mmat", [128, 4, 128], F32, S0)
                    K.dma("sp", mmat, mm_d, tab_slot)
                    d1 = sb("d1", [128, 128], F32, S0)
                    d2 = sb("d2", [128, 128], F32, S0)
                    for h in range(4):
                        checkpoint("d0")
                        K.act(d1, mmat[:, 0, :], AF.Exp, scale=lg[:, 4 + h:5 + h])
                        checkpoint("d1")
                        K.tt("dve", d1, d1, mmat[:, 1, :], ALU.mult)
                        checkpoint("d2")
                        K.act(d2, mmat[:, 2, :], AF.Exp, scale=lg[:, 8 + h:9 + h])
                        K.tt("dve", d2, d2, mmat[:, 3, :], ALU.mult)
                        K.tt("dve", d1, d1, d2, ALU.add)
                        checkpoint("d3")
                        K.ts("dve", DT[:, h, :], d1, 0.125, ALU.mult)
                        checkpoint("d4")

                def load_tab(dst, slot, src_d, C):
                    K.dma("sp", dst, V(src_d.ap[:, :, C * CH:(C + 1) * CH].rearrange("t p c -> p t c"), src_d.res), slot)

                def rope(psa, psb, tab, out32, ga=None, gb=None):
                    if ga is None:
                        K.tt("dve", t1, psa, tab[:, 0, :], ALU.mult)
                        K.tt("dve", t2, psb, tab[:, 1, :], ALU.mult)
                    else:
                        K.amul(tabg[:, 0, :], tab[:, 0, :], ga)
                        K.amul(tabg[:, 1, :], tab[:, 1, :], gb)
                        K.tt("dve", t1, psa, tabg[:, 0, :], ALU.mult)
                        K.tt("dve", t2, psb, tabg[:, 1, :], ALU.mult)
                    K.tt("pool", out32, t1, t2, ALU.add)

                checkpoint("setup0")
                load_w(wbf, wG_d, 1664, 0)
                with Scope(K) as PG:
                    pT = ps("pT", [128, 8, 128], BF16, PG)
                    pk = pT.re("p (c t) q -> p c t q", t=2)
                    pbig = ps("pbigG", [128, 7, CH], F32, PG)
                    bk = [subview(pbig, pbig.ap[:, i, :]) for i in range(7)]
                    pn = bk[4]
                    pT2g = V(bk[5].ap.bitcast(BF16).rearrange("p (k q) -> p k q", k=8), bk[5].res)
                    pkv = V(bk[6].ap[:, 0:256].rearrange("p (a b) -> p a b", a=2), bk[6].res)
                    kdbT = sb("kdbT", [128, 2, CH], BF16, PG)
                    kdbtm = sb("kdbtm", [128, 4, 2, 128], BF16, PG)
                    Rb = sb("Rb", [128, 2, 128], F32, PG)
                    mxg = alloc_mx(PG)
                    WS = [dict(sq=sq, lnv=lnv, rs=rs, t1=t1, t2=t2),
                          dict(sq=sb("wsq", [128, CH], BF16, PG), lnv=sb("wlnv", [128, CH], F32, PG),
                               rs=sb("wrs", [128, CH], F32, PG), t1=sb("wt1", [128, CH], F32, PG),
                               t2=sb("wt2", [128, CH], F32, PG))]
                    K.memset("dve", Rb, 0.0)
                    for C in range(NCH - 1, -1, -1):
                        make_xnT(mxg, x_d, C, [pT, pT2g])
                        store_xnT(xnT0_d, C, C % 2)
                        load_w_piece(wobf, 0, woab_d, C, 0, 1024, None)
                        load_tab(tabc, tab_slot, tabA_d, C)
                        load_tab(tabd, tabd_slot, tabB_d, C)
                        K.amul(tabg[:, 0, :], tabc[:, 0, :], gqk[:, 2:3])
                        K.amul(tabg[:, 1, :], tabc[:, 1, :], gqk[:, 3:4])
                        for t in range(2):
                            w_ = WS[t % 2]
                            pa_, pb_ = bk[2 * t], bk[2 * t + 1]
                            proj(pa_, t * 128)
                            proj(pb_, 256 + t * 128)
                            rsq_bcast(w_["rs"], pa_, 64.0, w_["sq"], pn, w_["lnv"], onesblk)
                            K.tt("dve", w_["t1"], pa_, tabg[:, 0, :], ALU.mult)
                            K.tt("dve", w_["t2"], pb_, tabg[:, 1, :], ALU.mult)
                            K.tt("pool", w_["t1"], w_["t1"], w_["t2"], ALU.add)
                            K.tt("pool", KaT[:, t, C * CH:(C + 1) * CH], w_["t1"], w_["rs"], ALU.mult)
                        for t in range(2):
                            w_ = WS[t % 2]
                            pa_, pb_ = bk[2 * t], bk[2 * t + 1]
                            proj(pa_, 512 + t * 128)
                            proj(pb_, 768 + t * 128)
                            K.tt("dve", w_["t1"], pa_, tabd[:, 0, :], ALU.mult)
                            K.tt("dve", w_["t2"], pb_, tabd[:, 1, :], ALU.mult)
                            K.tt("pool", w_["t1"], w_["t1"], w_["t2"], ALU.add)
                            K.tt("pool", kdbT[:, t, :], w_["t1"], KB4[:, t, :], ALU.mult)
                            for cj in range(4):
                                K.tr(pk[:, cj, t, :], kdbT[:, t, cj * 128:(cj + 1) * 128], ident)
                        K.cp("act", kdbtm, pk)
                        for b in range(4):
                            pva = (bk[4] if b % 2 == 0 else bk[2])[:, 0:128]
                            pvb = bk[5] if b % 2 == 0 else bk[3]
                            for kc in range(8):
                                K.mm(pva, cur["xnT"][:, kc, b * 128:(b + 1) * 128], wbf[:, kc, 1024:1152], start=(kc == 0), stop=(kc == 7))
                            for kc in range(8):
                                K.mm(pvb, cur["xnT"][:, kc, b * 128:(b + 1) * 128], wbf[:, kc, 1152:1664], start=(kc == 0), stop=(kc == 7))
                            K.cp("act", Va[:, C * 4 + b, :], pva)
                            K.cp("dve", vbtm[:, b, :], pvb)
                        for cj in range(3, -1, -1):
                            n = C * 4 + cj
                            for p in range(2):
                                K.mm(pkv[0:64, p, :], kdbtm[:, cj, p, 0:64], vbtm[:, cj, (2 * p) * 128:(2 * p + 1) * 128])
                                K.mm(pkv[64:128, p, :], kdbtm[:, cj, p, 64:128], vbtm[:, cj, (2 * p + 1) * 128:(2 * p + 2) * 128], tp=(0, 64))
                            K.ts("dve", SbAll[:, :, n, :], Rb, rfb[:, 32 + n:33 + n], ALU.mult)
                            for p in range(2):
                                K.ts("dve", Rb[:, p, :], Rb[:, p, :], cdr[:, 2 + p, n:n + 1], ALU.mult)
                                K.tt("dve", Rb[:, p, :], pkv[:, p, :], Rb[:, p, :], ALU.add)

                tap("KaT", KaT, [128, 2, NT], BF16)
                tap("Va", Va, [128, NB, 128], BF16)
                tap("SbAll", SbAll, [128, 2, NB, 128], BF16)
                checkpoint("G")
                load_w(wbf, wLB_d, 2048, 0)
                with Scope(K) as PB:
                    pT = ps("pT", [128, 8, 128], BF16, PB)
                    pk = pT.re("p (c t) q -> p c t q", t=2)
                    pbig = ps("pbigB", [128, 7, CH], F32, PB)
                    bk = [subview(pbig, pbig.ap[:, i, :]) for i in range(7)]
                    pa, pb, pss = bk[0], bk[1], bk[2]
                    po = subview(pbig, pbig.ap[:, 3:7, :])
                    qrT = sb("qrT", [128, 2, CH], BF16, PB)
                    qdf = sb("qdf", [128, 2, CH], BF16, PB)
                    qdb = sb("qdb", [128, 2, CH], BF16, PB)
                    krT = sb("krT", [128, 2, CH], BF16, PB)
                    kdfT = sb("kdfT", [128, 2, CH], BF16, PB)
                    kdftm = sb("kdftm", [128, 4, 2, 128], BF16, PB)
                    sg = sb("sg", [128, 4, CH], BF16, PB)
                    ATs = [sb("AT%d" % i, [128, 4, 128], BF16, PB) for i in range(2)]
                    Sfs = [sb("Sf%d" % i, [128, 2, 128], BF16, PB) for i in range(2)]
                    Rf = sb("Rf", [128, 2, 128], F32, PB)
                    mixBc = [sb("mixBc%d" % i, [128, 4, CH], BF16, PB) for i in range(1)]
                    mixB_slot = [K.slot() for _ in range(1)]
                    WS = [dict(sq=sq, lnv=lnv, rs=rs, t1=t1, t2=t2),
                          dict(sq=sb("wsq", [128, CH], BF16, PB), lnv=sb("wlnv", [128, CH], F32, PB),
                               rs=sb("wrs", [128, CH], F32, PB), t1=sb("wt1", [128, CH], F32, PB),
                               t2=sb("wt2", [128, CH], F32, PB))]
                    K.memset("dve", Rf, 0.0)
                    load_xnT(xnT0_d, 0)
                    pairs = [(bk[0], bk[1]), (bk[3], bk[4]), (bk[5], bk[6])]
                    for C in range(NCH):
                        use_xnT(C)
                        if C + 1 < NCH:
                            load_xnT(xnT0_d, C + 1)
                        load_tab(tabd, tabd_slot, tabB_d, C)
                        handoff([po], bk[3:7])
                        ip = 0
                        for t in range(2):
                            w_ = WS[ip % 2]
                            pa_, pb_ = pairs[ip % 3]
                            ip += 1
                            proj(pa_, t * 128)
                            proj(pb_, 256 + t * 128)
                            K.tt("dve", w_["t1"], pa_, tabd[:, 0, :], ALU.mult)
                            K.tt("dve", w_["t2"], pb_, tabd[:, 1, :], ALU.mult)
                            K.tt("pool", w_["t1"], w_["t1"], w_["t2"], ALU.add)
                            K.cp("act", qrT[:, t, :], w_["t1"])
                            K.tt("pool", qdf[:, t, :], w_["t1"], QF4[:, t, :], ALU.mult)
                            K.tt("pool", qdb[:, t, :], w_["t1"], QB4[:, t, :], ALU.mult)
                        for t in range(2):
                            w_ = WS[ip % 2]
                            pa_, pb_ = pairs[ip % 3]
                            ip += 1
                            proj(pa_, 512 + t * 128)
                            proj(pb_, 768 + t * 128)
                            K.tt("dve", w_["t1"], pa_, tabd[:, 0, :], ALU.mult)
                            K.tt("dve", w_["t2"], pb_, tabd[:, 1, :], ALU.mult)
                            K.tt("pool", w_["t1"], w_["t1"], w_["t2"], ALU.add)
                            K.cp("act", krT[:, t, :], w_["t1"])
                            K.tt("pool", kdfT[:, t, :], w_["t1"], KF4[:, t, :], ALU.mult)
                            for cj in range(4):
                                K.tr(pk[:, cj, t, :], kdfT[:, t, cj * 128:(cj + 1) * 128], ident)
                        K.cp("act", kdftm, pk)
                        for h in range(4):
                            pa_ = bk[3 + h]
                            proj(pa_, 1024 + h * 128)
                            K.act(sg[:, h, :], pa_, AF.Silu)
                        for b in range(4):
                            pv_ = bk[1 + b % 2]
                            for kc in range(8):
                                K.mm(pv_, cur["xnT"][:, kc, b * 128:(b + 1) * 128], wbf[:, kc, 1536:2048], start=(kc == 0), stop=(kc == 7))
                            K.cp("dve", vbtm[:, b, :], pv_)
                        handoff(bk[3:7], [po])
                        for cj in range(4):
                            n = C * 4 + cj
                            cs = slice(cj * 128, (cj + 1) * 128)
                            Sf = Sfs[cj % 2]
                            AT = ATs[cj % 2]
                            K.ts("dve", Sf, Rf, rfb[:, n:n + 1], ALU.mult)
                            for p in range(2):
                                K.mm(pa[0:64, p * 128:(p + 1) * 128], kdftm[:, cj, p, 0:64], vbtm[:, cj, (2 * p) * 128:(2 * p + 1) * 128])
                                K.mm(pa[64:128, p * 128:(p + 1) * 128], kdftm[:, cj, p, 64:128], vbtm[:, cj, (2 * p + 1) * 128:(2 * p + 2) * 128], tp=(0, 64))
                            for p in range(2):
                                K.ts("dve", Rf[:, p, :], Rf[:, p, :], cdr[:, p, n:n + 1], ALU.mult)
                                K.tt("dve", Rf[:, p, :], pa[:, p * 128:(p + 1) * 128], Rf[:, p, :], ALU.add)
                            for h in range(4):
                                t, r0 = h // 2, (h % 2) * 64
                                pdst = pss if (h % 2 == 0) else pb
                                K.mm(pdst[:, t * 128:(t + 1) * 128], krT[r0:r0 + 64, t, cs], qrT[r0:r0 + 64, t, cs])
                            ATv = AT.re("p (t hp) i -> p hp t i", hp=2)
                            DTv = DT.re("p (t hp) i -> p hp t i", hp=2)
                            K.tt("dve", ATv[:, 0, :, :], pss[:, 0:256].re("p (t i) -> p t i", t=2), DTv[:, 0, :, :], ALU.mult)
                            K.tt("dve", ATv[:, 1, :, :], pb[:, 0:256].re("p (t i) -> p t i", t=2), DTv[:, 1, :, :], ALU.mult)
                            for h in range(4):
                                t, r0 = h // 2, (h % 2) * 64
                                K.mm(po[:, h, cs], vbtm[:, cj, h * 128:(h + 1) * 128], AT[:, h, :], start=True, stop=False)
                                K.mm(po[:, h, cs], Sf[r0:r0 + 64, t, :], qdf[r0:r0 + 64, t, cs], start=False, stop=False)
                                K.mm(po[:, h, cs], SbAll[r0:r0 + 64, t, n, :], qdb[r0:r0 + 64, t, cs], start=False, stop=True)
                        mb = mixBc[0]
                        for h0 in (0, 2):
                            hs = (h0, h0 + 1)
                            for h in hs:
                                K.act(WS[h % 2]["sq"], po[:, h, :], AF.Square)
                            for h in hs:
                                K.mm(bk[h % 2], ones, WS[h % 2]["sq"])
                            for h in hs:
                                K.act(WS[h % 2]["lnv"], bk[h % 2], AF.Ln, bias=epsb[:, 0:1], scale=1.0 / 128.0)
                            for h in hs:
                                K.act(WS[h % 2]["rs"], WS[h % 2]["lnv"], AF.Exp, scale=-0.5)
                            for h in hs:
                                K.tt("dve", WS[h % 2]["t1"], po[:, h, :], WS[h % 2]["rs"], ALU.mult)
                                K.tt("pool", mb[:, h, :], WS[h % 2]["t1"], sg[:, h, :], ALU.mult)
                        K.dma("pool", V(mixb_d.ap[:, :, C * CH:(C + 1) * CH].rearrange("h p c -> p h c"), mixb_d.res), mb, mixB_slot[0])

                tap("mixb", mixb_d, [4, 128, NT], BF16)
                checkpoint("LB")
                LR.close()
                load_w(wbf, wLA_d, 1536, 0)
                handoff([wbf], [wlo, whi])
                wmode["split"] = True
                with Scope(K) as PA:
                    pbig = ps("pbig", [128, 8, CH], F32, PA)
                    psc = [subview(pbig, pbig.ap[:, 2 * i:2 * i + 2, :]) for i in range(3)]
                    pnum = subview(pbig, pbig.ap[:, 6, :])
                    pden = subview(pbig, pbig.ap[:, 7, :])
                    bk = [subview(pbig, pbig.ap[:, i, :]) for i in range(6)] + [pnum, pden]
                    WS = [dict(sq=sq, lnv=lnv, rs=rs, t1=t1, t2=t2),
                          dict(sq=sb("wsq", [128, CH], BF16, PA), lnv=sb("wlnv", [128, CH], F32, PA),
                               rs=sb("wrs", [128, CH], F32, PA), t1=sb("wt1", [128, CH], F32, PA),
                               t2=sb("wt2", [128, CH], F32, PA))]
                    qaT = sb("qaT", [128, 4, CH], BF16, PA)
                    sga = sb("sga", [128, 4, CH], BF16, PA)
                    mixAs = [sb("mixA%d" % i, [128, 4, CH], BF16, PA) for i in range(2)]
                    mixBls = [sb("mixBl%d" % i, [128, 4, CH], BF16, PA) for i in range(2)]
                    mixBl_slots = [K.slot() for _ in range(2)]
                    xblk = [sb("xblk%d" % i, [128, 1024], F32, PA) for i in range(2)]
                    xblk_slot = [K.slot() for _ in range(2)]
                    nxb = [0]
                    pTs = [sb("pTs%d" % i, [128, 2, CH], BF16, PA) for i in range(3)]
                    x1b = [sb("x1b%d" % i, [128, 1024], F32, PA) for i in range(2)]
                    dcp = sb("dcp", [128, CH], F32, PA)
                    ncp = sb("ncp", [128, CH], F32, PA)
                    x1b_slot = [K.slot() for _ in range(2)]
                    qaTs = [qaT, sb("qaT1", [128, 4, CH], BF16, PA)]
                    sgas = [sga, sb("sga1", [128, 4, CH], BF16, PA)]
                    tabcs = [tabc, sb("tabc1", [128, 2, CH], F32, PA)]
                    tabgs = [tabg, sb("tabg1", [128, 2, CH], F32, PA)]
                    tabsl = [tab_slot, K.slot()]
                    nbuf = [0]
                    npt = [0]

                    held = set()

                    def take_buf():
                        while True:
                            i_ = nbuf[0] % 3
                            nbuf[0] += 1
                            if i_ not in held:
                                return psc[i_]

                    def hold(b_):
                        held.add(psc.index(b_))

                    def release(b_):
                        held.discard(psc.index(b_))

                    def projx(dst, c0, xT):
                        for kc in range(8):
                            K.mm(dst, wcol(kc, c0), xT[:, kc, :], start=(kc == 0), stop=(kc == 7))

                    def proj_items(Cn):
                        q_, g_ = qaTs[Cn % 2], sgas[Cn % 2]
                        tc_, tg_ = tabcs[Cn % 2], tabgs[Cn % 2]
                        xT = xnTs[Cn % 2]

                        def prep():
                            load_tab(tc_, tabsl[Cn % 2], tabA_d, Cn)
                            K.amul(tg_[:, 0, :], tc_[:, 0, :], gqk[:, 0:1])
                            K.amul(tg_[:, 1, :], tc_[:, 1, :], gqk[:, 1:2])

                        items = []
                        for t in range(4):
                            def mk(t=t):
                                st = {}
                                w_ = WS[t % 2]

                                def s1():
                                    st["buf"] = take_buf()
                                    hold(st["buf"])
                                    projx(st["buf"][:, 0, :], t * 128, xT)
                                    projx(st["buf"][:, 1, :], 512 + t * 128, xT)

                                def s2():
                                    pa_, pb_ = st["buf"][:, 0, :], st["buf"][:, 1, :]
                                    K.tt("dve", w_["t1"], pa_, tg_[:, 0, :], ALU.mult)
                                    K.tt("dve", w_["t2"], pb_, tg_[:, 1, :], ALU.mult)
                                    K.act(w_["sq"], pa_, AF.Square)

                                def s3():
                                    K.mm(st["buf"][:, 1, :], onesblk, w_["sq"])

                                def s4():
                                    K.act(w_["lnv"], st["buf"][:, 1, :], AF.Ln, bias=epsb[:, 0:1], scale=1.0 / 64.0)
                                    K.act(w_["rs"], w_["lnv"], AF.Exp, scale=-0.5)
                                    K.tt("pool", w_["t1"], w_["t1"], w_["t2"], ALU.add)
                                    K.tt("pool", q_[:, t, :], w_["t1"], w_["rs"], ALU.mult)
                                    release(st["buf"])
                                return [(s1, 3), (s2, 1), (s3, 2), (s4, 0)]
                            items.append(mk())
                        for t2_ in range(2):
                            def mk(t2_=t2_):
                                st = {}

                                def s1():
                                    st["buf"] = take_buf()
                                    hold(st["buf"])
                                    for j in range(2):
                                        projx(st["buf"][:, j, :], 1024 + (2 * t2_ + j) * 128, xT)

                                def s2():
                                    for j in range(2):
                                        K.act(WS[j]["t1"], st["buf"][:, j, :], AF.Tanh, scale=0.5)

                                def s3():
                                    for j in range(2):
                                        K.ts("dve", WS[j]["t1"], WS[j]["t1"], 0.5, ALU.mult, 0.5, ALU.add)
                                        K.tt("dve", g_[:, 2 * t2_ + j, :], st["buf"][:, j, :], WS[j]["t1"], ALU.mult)
                                    release(st["buf"])
                                return [(s1, 3), (s2, 1), (s3, 0)]
                            items.append(mk())
                        return prep, items

                    def outproj_items(Cc):
                        mA, mB = mixAs[Cc % 2], mixBls[Cc % 2]
                        items = []
                        for b in range(4):
                            def mk(b=b):
                                st = {}
                                bs = slice(b * 128, (b + 1) * 128)
                                r0 = Cc * CH + b * 128

                                def s1():
                                    i_ = nxb[0] % 2
                                    nxb[0] += 1
                                    st["i"] = i_
                                    K.dma("sp", xblk[i_], x_d[r0:r0 + 128, :], xblk_slot[i_])
                                    st["buf"] = take_buf()
                                    hold(st["buf"])
                                    py = st["buf"]
                                    for half in range(2):
                                        for f in range(8):
                                            src = mA[:, f, bs] if f < 4 else mB[:, f - 4, bs]
                                            K.mm(py[:, half, :], src, wobf[:, f, half * 512:(half + 1) * 512], start=(f == 0), stop=(f == 7))

                                def s2():
                                    xo = x1b[st["i"]]
                                    K.tt("dve", xo, st["buf"].re("p a c -> p (a c)"), xblk[st["i"]], ALU.add)
                                    K.dma("pool", x1_d[r0:r0 + 128, :], xo, x1b_slot[st["i"]])
                                    release(st["buf"])
                                return [(s1, 4), (s2, 0)]
                            items.append(mk())
                        return items

                    load_xnT(xnT0_d, 0)
                    prep0, items0 = proj_items(0)
                    prep0()
                    for it_ in items0:
                        for st_fn, _d in it_:
                            st_fn()
                    carry = []
                    for C in range(NCH):
                        use_xnT(C)
                        qaT_c, sga_c = qaTs[C % 2], sgas[C % 2]
                        mixA = mixAs[C % 2]
                        load_w_piece(whi, 0, wG1_d, C, 0, 384, 1)
                        pending = list(carry)
                        carry = []
                        if C + 1 < NCH:
                            load_xnT(xnT0_d, C + 1)
                            prepn, pitems = proj_items(C + 1)
                            prepn()
                            pending = pending + pitems
                        K.dma("pool", mixBls[C % 2], V(mixb_d.ap[:, :, C * CH:(C + 1) * CH].rearrange("h p c -> p h c"), mixb_d.res), mixBl_slots[C % 2])
                        nit = 0
                        active = [None]
                        for t in range(4):
                            kv = t // 2
                            fifo = []

                            def qk(kb_):
                                sc_ = take_buf()
                                fifo.append(sc_)
                                ks = slice(kb_ * 128, (kb_ + 1) * 128)
                                K.mm(sc_[:, 0, :], KaT[0:64, kv, ks], qaT_c[0:64, t, :])
                                K.mm(sc_[:, 1, :], KaT[64:128, kv, ks], qaT_c[64:128, t, :])

                            qk(0)
                            qk(1)
                            for kb in range(NB):
                                sc = fifo.pop(0)
                                pt = pTs[npt[0] % 3]
                                npt[0] += 1
                                nit += 1
                                K.act(pt, sc, AF.Exp, bias=maskA[:, C * NB + kb:C * NB + kb + 1], scale=0.125)
                                if kb + 2 < NB:
                                    qk(kb + 2)
                                if active[0] is None and pending and nit % 12 == 3:
                                    active[0] = [pending.pop(0), 0, nit]
                                if active[0] is not None and nit >= active[0][2]:
                                    stages_, si_, _due = active[0]
                                    fn_, delay_ = stages_[si_]
                                    fn_()
                                    if si_ + 1 < len(stages_):
                                        active[0] = [stages_, si_ + 1, nit + delay_]
                                    else:
                                        active[0] = None
                                st, sp_ = (kb == 0), (kb == NB - 1)
                                K.mm(pnum[0:64, :], Va[:, kb, kv * 64:(kv + 1) * 64], pt[:, 0, :], start=st, stop=sp_)
                                K.mm(pnum[64:128, :], Va[:, kb, kv * 64:(kv + 1) * 64], pt[:, 1, :], start=st, stop=sp_, tp=(0, 64))
                                K.mm(pden[0:64, :], ones[:, 0:64], pt[:, 0, :], start=st, stop=sp_)
                                K.mm(pden[64:128, :], ones[:, 0:64], pt[:, 1, :], start=st, stop=sp_, tp=(0, 64))
                            K.cp("dve", dcp, pden)
                            K.cp("dve", ncp, pnum)
                            K.recip(dcp, dcp)
                            K.tt("dve", ncp, ncp, dcp, ALU.mult)
                            K.tt("pool", mixA[:, t, :], ncp, sga_c[:, t, :], ALU.mult)
                        while active[0] is not None or pending:
                            if active[0] is None:
                                active[0] = [pending.pop(0), 0, 0]
                            stages_, si_, _due = active[0]
                            stages_[si_][0]()
                            active[0] = [stages_, si_ + 1, 0] if si_ + 1 < len(stages_) else None
                        carry = outproj_items(C)
                        if C == NCH - 1:
                            for it_ in carry:
                                for st_fn, _d in it_:
                                    st_fn()
                            carry = []

            tap("x1", x1_d, [NT, 1024], F32)
            checkpoint("LA")
            with Scope(K) as L1:
                KcT = sb("KcT", [128, 2, (NB + 2) * 128], BF16, L1)
                Vc = sb("Vc", [128, NB + 2, 128], BF16, L1)
                K.memset("pool", KcT[:, :, 0:128], 0.0)
                K.memset("pool", KcT[:, :, (NB + 1) * 128:(NB + 2) * 128], 0.0)
                K.memset("pool", Vc[:, 0, :], 0.0)
                K.memset("pool", Vc[:, NB + 1, :], 0.0)
                tap("EBT", EBT, [128, 16, 3, 128], BF16)
                checkpoint("EBT")
                wmode["base"] = 1536
                with Scope(K) as PG1:
                    pT = ps("pT", [128, 8, 128], BF16, PG1)
                    pa = ps("pa", [128, CH], F32, PG1)
                    pva = ps("pva", [128, 512], F32, PG1)[:, 0:128]
                    mxs = [alloc_mx(PG1), alloc_mx(PG1)]
                    pT2 = ps("pT2", [128, 8, 128], BF16, PG1)
                    pa2 = ps("pa2", [128, CH], F32, PG1)
                    pva2 = ps("pva2", [128, 512], F32, PG1)[:, 0:128]
                    xnT_front(mxs[0], x1_d, 0)
                    xnT_back(mxs[0], 0, [pT, pT2])
                    for C in range(NCH):
                        use_xnT(C)
                        store_xnT(xnT1_d, C, C % 2)
                        if C + 1 < NCH:
                            xnT_front(mxs[(C + 1) % 2], x1_d, C + 1)
                        load_w_piece(wlo, 0, wL1_d, C, 0, 1024, 1)
                        load_w_piece(wlo, 1024, wL1_d, C, 1024, 1536, 1)
                        load_w_piece(wobf, 0, woc_d, C, 0, 1024, None)
                        for t in range(2):
                            pa_ = pa if t == 0 else pa2
                            proj(pa_, t * 128)
                            K.cp("act" if t == 0 else "dve", KcT[:, t, (C * 4 + 1) * 128:(C * 4 + 5) * 128], pa_)
                        for b in range(4):
                            pv_ = pva if b % 2 == 0 else pva2
                            for kc in range(8):
                                K.mm(pv_, cur["xnT"][:, kc, b * 128:(b + 1) * 128], wcol(kc, 256), start=(kc == 0), stop=(kc == 7))
                            K.cp("dve" if b % 2 == 0 else "act", Vc[:, C * 4 + b + 1, :], pv_)
                        if C + 1 < NCH:
                            xnT_back(mxs[(C + 1) % 2], C + 1, [pT, pT2])

                tap("KcT", KcT, [128, 2, (NB + 2) * 128], BF16)
                tap("Vc", Vc, [128, NB + 2, 128], BF16)
                checkpoint("G1")
                wmode["base"] = 0
                for kc_ in range(8):
                    load_w_piece(whi, 0, wL1_d, kc_, 1536, 2048, 1)
                with Scope(K) as PL1:
                    pbig = ps("pbig1", [128, 8, CH], F32, PL1)
                    pw = [subview(pbig, pbig.ap[:, 2 * i:2 * i + 2, :]) for i in range(2)]
                    pnum = subview(pbig, pbig.ap[:, 6, :])
                    pden = subview(pbig, pbig.ap[:, 7, :])
                    bk = [subview(pbig, pbig.ap[:, i, :]) for i in range(6)]
                    qcT = sb("qcT", [128, 8, CH], BF16, PL1)
                    sgc = sb("sgc", [128, 8, CH], BF16, PL1)
                    mixC = sb("mixC", [128, 8, CH], BF16, PL1)
                    pws = [sb("pws%d" % i, [128, 2, 3, 128], BF16, PL1) for i in range(2)]
                    pw2 = [sb("pw2%d" % i, [128, 2, 3, 128], BF16, PL1) for i in range(2)]
                    rs = sb("rs1", [128, CH], F32, PL1)
                    lnr = sb("lnr", [128, CH], F32, PL1)
                    t1 = sb("t11", [128, CH], F32, PL1)
                    EP = [dict(rs=rs, t1=t1, lnr=lnr),
                          dict(rs=sb("rs1b", [128, CH], F32, PL1), t1=sb("t11b", [128, CH], F32, PL1), lnr=sb("lnrb", [128, CH], F32, PL1))]
                    pnums = [pnum, bk[4]]
                    pdens = [pden, bk[5]]
                    x2 = [sb("x2%d" % i, [128, 1024], F32, PL1) for i in range(2)]
                    yo = [sb("yo%d" % i, [128, 1024], F32, PL1) for i in range(2)]
                    yo_slot = [K.slot() for _ in range(2)]
                    ss2 = sb("ss2", [128, 2], F32, PL1)
                    ln2 = sb("ln2", [128, 2], F32, PL1)
                    r2 = sb("r2", [128, 2], F32, PL1)
                    it = 0
                    mxl = alloc_mx(PL1, full=False)
                    xch = mxl.xch
                    junk = sb("junk1", [128, 1024], BF16, PL1)
                    fnbc = sb("fnbc", [128, 1024], F32, PL1)
                    fn_slot = K.slot()
                    K.dma("sp", fnbc, V(fn_d.ap.to_broadcast([128, 1024]), fn_d.res), fn_slot)
                    load_xnT(xnT1_d, 0)
                    for C in range(NCH):
                        use_xnT(C)
                        if C + 1 < NCH:
                            load_xnT(xnT1_d, C + 1)
                        load_x(mxl, x1_d, C)
                        handoff(pw, bk[0:4])
                        for t in range(8):
                            pa_ = bk[t % 6]
                            proj(pa_, t * 128)
                            K.cp("act" if t % 2 == 0 else "dve", qcT[:, t, :], pa_)
                        for t in range(8):
                            pa_ = bk[(t + 2) % 6]
                            proj(pa_, 1024 + t * 128)
                            K.act(sgc[:, t, :], pa_, AF.Silu)
                        handoff(bk[0:4], pw)
                        items = [(t, qi) for t in range(8) for qi in range(4)]

                        def wqk(t, qi, w):
                            kv = t // 4
                            i = C * 4 + qi
                            qs = slice(qi * 128, (qi + 1) * 128)
                            for o in range(3):
                                sl = 2 - o
                                ks = slice((i + o) * 128, (i + o + 1) * 128)
                                K.mm(w[:, 0, sl * 128:(sl + 1) * 128], KcT[0:64, kv, ks], qcT[0:64, t, qs])
                                K.mm(w[:, 1, sl * 128:(sl + 1) * 128], KcT[64:128, kv, ks], qcT[64:128, t, qs])

                        wqk(items[0][0], items[0][1], pw[it % 2])
                        deferred = []
                        for idx, (t, qi) in enumerate(items):
                            kv = t // 4
                            i = C * 4 + qi
                            qs = slice(qi * 128, (qi + 1) * 128)
                            w = pw[it % 2]
                            s1 = pws[it % 2]
                            s2 = pw2[it % 2]
                            it += 1
                            if i in (0, NB // 2 - 1, NB // 2, NB - 1):
                                for o in range(3):
                                    sl = 2 - o
                                    K.act(s1[:, :, sl, :], w[:, :, sl * 128:(sl + 1) * 128], AF.Exp, bias=maskW[:, i * 3 + o:i * 3 + o + 1], scale=0.125)
                            else:
                                K.act(s1, w[:, :, 0:384].re("p h (o q) -> p h o q", o=3), AF.Exp, scale=0.125)
                            K.tt("dve", s2, s1, EBT[:, 2 * t:2 * t + 2, :, :], ALU.mult)
                            if deferred:
                                deferred.pop(0)()
                            if idx + 1 < len(items):
                                wqk(items[idx + 1][0], items[idx + 1][1], pw[it % 2])
                            pnum_, pden_ = pnums[t % 2], pdens[t % 2]
                            for o in range(3):
                                sl = 2 - o
                                st, sp_ = (o == 0), (o == 2)
                                vv = Vc[:, i + o, kv * 64:(kv + 1) * 64]
                                K.mm(pnum_[0:64, qs], vv, s2[:, 0, sl, :], start=st, stop=sp_)
                                K.mm(pnum_[64:128, qs], vv, s2[:, 1, sl, :], start=st, stop=sp_, tp=(0, 64))
                                K.mm(pden_[0:64, qs], ones[:, 0:64], s2[:, 0, sl, :], start=st, stop=sp_)
                                K.mm(pden_[64:128, qs], ones[:, 0:64], s2[:, 1, sl, :], start=st, stop=sp_, tp=(0, 64))
                            if qi == 3:
                                def epi(t=t, pnum_=pnum_, pden_=pden_):
                                    e_ = EP[t % 2]
                                    K.ts("dve", e_["rs"], pden_, esk[:, t:t + 1], ALU.add)
                                    K.cp("dve", e_["t1"], pnum_)
                                    K.act(e_["lnr"], e_["rs"], AF.Ln)
                                    K.act(e_["rs"], e_["lnr"], AF.Exp, scale=-1.0)
                                    K.tt("pool", e_["t1"], e_["t1"], e_["rs"], ALU.mult)
                                    K.tt("pool", mixC[:, t, :], e_["t1"], sgc[:, t, :], ALU.mult)
                                deferred.append(epi)
                        while deferred:
                            deferred.pop(0)()
                        for b in range(4):
                            bs = slice(b * 128, (b + 1) * 128)
                            py = pw[b % 2]
                            for half in range(2):
                                for f in range(8):
                                    K.mm(py[:, half, :], mixC[:, f, bs], wobf[:, f, half * 512:(half + 1) * 512], start=(f == 0), stop=(f == 7))
                            xo = x2[b % 2]
                            K.tt("dve", xo, py.re("p a c -> p (a c)"), xch[:, b, :], ALU.add)
                            K.act(junk, xo, AF.Square, accum=ss2[:, b % 2:b % 2 + 1])
                            K.act(ln2[:, b % 2:b % 2 + 1], ss2[:, b % 2:b % 2 + 1], AF.Ln, bias=epsb[:, 0:1], scale=1.0 / 1024.0)
                            K.act(r2[:, b % 2:b % 2 + 1], ln2[:, b % 2:b % 2 + 1], AF.Exp, scale=-0.5)
                            yb = yo[b % 2]
                            K.ts("dve", yb, xo, r2[:, b % 2:b % 2 + 1], ALU.mult)
                            K.tt("pool", yb, yb, fnbc, ALU.mult)
                            r0 = C * CH + b * 128
                            K.dma("pool", y_d[r0:r0 + 128, :], yb, yo_slot[b % 2])
    except StopBuild:
        pass
    for s_ in K.slots:
        if s_.cnt:
            nc.gpsimd.wait_ge(s_.sem, s_.cnt)
    return nc, K


def _t5_bucket(rel):
    half = 16
    max_exact = 8
    ret = (rel > 0).astype(np.int32) * half
    dist = np.abs(rel)
    large = max_exact + (np.log(np.maximum(dist, 1) / max_exact) / np.log(128 / max_exact) * (half - max_exact)).astype(np.int32)
    large = np.minimum(large, half - 1)
    return ret + np.where(dist < max_exact, dist, large)


def _static_tables():
    f32 = np.float32
    st = {}
    st["ident"] = np.eye(128, dtype=f32)
    st["aident"] = np.ascontiguousarray(np.eye(128, dtype=f32)[::-1])
    ob = np.zeros((128, 128), f32)
    ob[:64, :64] = 1
    ob[64:, 64:] = 1
    st["onesblk"] = ob
    j = np.arange(128)[:, None]
    i = np.arange(128)[None, :]
    mmat = np.zeros((128, 4, 128), f32)
    mmat[:, 0, :] = np.maximum(i - j, 0)
    mmat[:, 1, :] = (i >= j)
    mmat[:, 2, :] = np.maximum(j - i, 0)
    mmat[:, 3, :] = (j > i)
    st["mmat"] = mmat
    c = np.arange(512) % 128
    iot = np.zeros((128, 4, 512), f32)
    iot[:, 0, :] = c + 1
    iot[:, 1, :] = 128 - c
    iot[:, 2, :] = 127 - c
    iot[:, 3, :] = c
    st["iot"] = iot
    m = np.arange(640)
    rel = 255 - m
    bk = _t5_bucket(rel)
    oh = np.zeros((32, 640), f32)
    oh[bk, m] = 1
    st["oh"] = oh
    st["inwin"] = np.broadcast_to((np.abs(rel) <= 128).astype(f32)[None, :], (16, 640)).copy()
    return st


def _core_tables(is_prompt):
    f32 = np.float32
    seqlen = 4096 if is_prompt else 2048
    t = np.arange(NT) % seqlen
    d = np.arange(128) % 64
    pair = d // 2
    sgn = np.where(d % 2 == 0, -1.0, 1.0)
    quarter = 16
    freqs = (np.float32(10000.0) ** (-np.arange(quarter, dtype=f32) / quarter)).astype(f32)
    row = (t // 64).astype(f32)
    col = (t % 64).astype(f32)
    ang = np.concatenate([row[:, None] * freqs, col[:, None] * freqs], axis=-1).astype(f32)
    angd = ang[:, pair].T.astype(np.float64)
    tabA = np.stack([np.cos(angd), np.sin(angd) * sgn[:, None]]).astype(f32)
    half = 32
    freqs_b = (np.float32(10000.0) ** (-np.arange(half, dtype=f32) / half)).astype(f32)
    angb = (t.astype(f32)[:, None] * freqs_b).astype(f32)
    angbd = angb[:, pair].T.astype(np.float64)
    tabB = np.stack([np.cos(angbd), np.sin(angbd) * sgn[:, None]]).astype(f32)
    seq_of_blk = (np.arange(NB) * 128) // seqlen
    maskA = np.zeros((NCH, NB), f32)
    for C in range(NCH):
        sq = (C * CH) // seqlen
        maskA[C, :] = np.where(seq_of_blk == sq, 0.0, NEG)
    maskA = np.broadcast_to(maskA.reshape(1, -1), (128, NCH * NB)).copy()
    maskW = np.zeros((NB, 3), f32)
    for i in range(NB):
        for o in range(3):
            jb = i + o - 1
            if jb < 0 or jb >= NB or seq_of_blk[jb] != seq_of_blk[i]:
                maskW[i, o] = NEG
    maskW = np.broadcast_to(maskW.reshape(1, -1), (128, NB * 3)).copy()
    cps = seqlen // 128
    rf = np.array([0.0 if (n % cps == 0) else 1.0 for n in range(NB)], f32)
    rb = np.array([0.0 if (n % cps == cps - 1) else 1.0 for n in range(NB)], f32)
    rfb = np.broadcast_to(np.concatenate([rf, rb])[None, :], (128, 64)).copy()
    return {"tabA": tabA, "tabB": tabB, "maskA": maskA, "maskW": maskW, "rfb": rfb}


def _swap(cols):
    cols = np.asarray(cols)
    return cols ^ 1


def _prep_common(norm_g, w_in_ab, qk_norm_a, ret_decay, w_out_ab, w_in_c, sink_c, w_out_c, rel_bias, final_norm):
    f32 = np.float32
    W = np.asarray(w_in_ab[0], f32)
    qa = np.arange(0, 512)
    ka = np.arange(512, 640)
    va = np.arange(640, 768)
    ga = np.arange(768, 1280)
    qb = np.arange(1280, 1536)
    kb = np.arange(1536, 1792)
    vb = np.arange(1792, 2304)
    gb = np.arange(2304, 2816)
    kadup = np.concatenate([ka[0:64], ka[0:64], ka[64:128], ka[64:128]])
    cm = {}
    cm["wG"] = np.ascontiguousarray(W[:, np.concatenate([kadup, _swap(kadup), kb, _swap(kb), va, vb])])
    cm["wLB"] = np.ascontiguousarray(W[:, np.concatenate([qb, _swap(qb), kb, _swap(kb), gb, vb])])
    cm["wLA"] = np.ascontiguousarray(W[:, np.concatenate([qa, _swap(qa), ga])])
    cm["woab"] = np.ascontiguousarray(np.asarray(w_out_ab[0], f32))
    Wc = np.asarray(w_in_c[0], f32)
    kc = np.arange(1024, 1152)
    kcdup = np.concatenate([kc[0:64], kc[0:64], kc[64:128], kc[64:128]])
    cm["wG1"] = np.ascontiguousarray(Wc[:, np.concatenate([kcdup, np.arange(1152, 1280)])])
    cm["wL1"] = np.ascontiguousarray(Wc[:, np.concatenate([np.arange(0, 1024), np.arange(1280, 2304)])])
    cm["woc"] = np.ascontiguousarray(np.asarray(w_out_c[0], f32))
    ng = np.asarray(norm_g, f32)
    cm["gcol"] = np.ascontiguousarray(ng.reshape(2, 8, 128).transpose(2, 0, 1).reshape(128, 16))
    cm["fn"] = np.asarray(final_norm, f32).reshape(1, 1024).copy()
    g = np.asarray(qk_norm_a[0], f32)
    d = np.arange(128) % 64
    cm["gqk"] = np.stack([g[0][d], g[0][d ^ 1], g[1][d], g[1][d ^ 1]], axis=1).astype(f32).copy()
    rd = np.asarray(ret_decay[0], f32)
    hp = (np.arange(128) // 64)
    rdec = np.zeros((128, 12), f32)
    for p in range(2):
        rdec[:, p] = rd[0][2 * p + hp]
        rdec[:, 2 + p] = rd[1][2 * p + hp]
    for h in range(4):
        rdec[:, 4 + h] = rd[0][h]
        rdec[:, 8 + h] = rd[1][h]
    cm["rdec"] = rdec
    sk = np.asarray(sink_c[0], f32)
    sinkl = np.zeros((128, 8), f32)
    for t in range(8):
        sinkl[:, t] = sk[2 * t + hp]
    cm["sinkl"] = sinkl
    cm["relb"] = np.ascontiguousarray(np.asarray(rel_bias, f32))
    cm.update(_static_tables())
    return cm


_CACHE = {}


def kernel(x_prompt, x_sample, norm_g, w_in_ab, qk_norm_a, ret_decay, w_out_ab, w_in_c, sink_c, w_out_c, rel_bias, final_norm):
    xp = np.asarray(x_prompt, np.float32)
    xs = np.asarray(x_sample, np.float32)
    cm = _prep_common(norm_g, w_in_ab, qk_norm_a, ret_decay, w_out_ab, w_in_c, sink_c, w_out_c, rel_bias, final_norm)
    tp = _core_tables(True)
    tsm = _core_tables(False)
    in_maps = []
    for c in range(8):
        m = dict(cm)
        if c < 4:
            m["x"] = np.ascontiguousarray(xp[c])
            m.update(tp)
        else:
            m["x"] = np.ascontiguousarray(xs[2 * (c - 4):2 * (c - 4) + 2].reshape(NT, 1024))
            m.update(tsm)
        in_maps.append(m)
    if "nc" not in _CACHE:
        _CACHE["nc"] = build_program()[0]
    nc = _CACHE["nc"]
    res = run_bass_kernel_spmd(nc, in_maps, core_ids=list(range(8)))
    outs = [np.asarray(r["y"], np.float32) for r in res.results]
    y_prompt = np.stack(outs[0:4], axis=0)
    y_sample = np.stack(outs[4:8], axis=0).reshape(8, 2048, 1024)
    return (y_prompt, y_sample)
```

```python
import numpy as np
import concourse.bass as bass
import concourse.mybir as mybir
from concourse.bass_utils import run_bass_kernel_spmd

F32 = mybir.dt.float32
BF16 = mybir.dt.bfloat16
AF = mybir.ActivationFunctionType
ALU = mybir.AluOpType

NT = 4096
NB = 32
CH = 512
NCH = 8
EPS = 1e-6
NEG = -30000.0


class Prod:
    def __init__(self, sem, inc):
        self.sem = sem
        self.inc = inc
        self.cnt = 0


class Res:
    def __init__(self):
        self.w = {}
        self.r = {}
        self.excl = False


class V:
    def __init__(self, ap, res=None):
        self.ap = ap
        self.res = res if res is not None else Res()

    def __getitem__(self, k):
        return V(self.ap[k], self.res)

    def re(self, pat, **kw):
        return V(self.ap.rearrange(pat, **kw), self.res)

    def bc(self, shape):
        return V(self.ap.to_broadcast(shape), self.res)


class Ker:
    def __init__(self, nc):
        self.nc = nc
        self.eng = {"pe": nc.tensor, "act": nc.scalar, "dve": nc.vector, "pool": nc.gpsimd, "sp": nc.sync}
        self.prod = {}
        for n in ("pe", "act", "dve", "pool"):
            self.prod[n] = Prod(nc.alloc_semaphore("s_" + n), 1)
        self.seen = {n: {} for n in self.eng}
        self.nslot = 0
        self.ninstr = 0

    def slot(self):
        self.nslot += 1
        p = Prod(self.nc.alloc_semaphore("d%d" % self.nslot), 16)
        if hasattr(self, "slots"):
            self.slots.append(p)
        return p

    def _wait(self, en, reads, writes):
        deps = {}
        for v in reads:
            for p, i in v.res.w.items():
                deps[p] = max(deps.get(p, 0), i)
        for v in writes:
            for p, i in v.res.w.items():
                deps[p] = max(deps.get(p, 0), i)
            for p, i in v.res.r.items():
                deps[p] = max(deps.get(p, 0), i)
        e = self.eng[en]
        seen = self.seen[en]
        own = self.prod.get(en)
        for p, i in deps.items():
            if p is own and en == "pe":
                continue
            if seen.get(p, 0) >= i:
                continue
            e.wait_ge(p.sem, i)
            seen[p] = i

    def op(self, en, fn, reads, writes):
        writes = list(writes) + [r for r in reads if r.res.excl]
        self._wait(en, reads, writes)
        ins = fn(self.eng[en])
        p = self.prod[en]
        p.cnt += 1
        ins.then_inc(p.sem, 1)
        for v in reads:
            v.res.r[p] = p.cnt
        for v in writes:
            v.res.w[p] = p.cnt
        self.ninstr += 1

    def dma(self, q, out, in_, slot):
        self._wait(q, [in_], [out])
        ins = self.eng[q].dma_start(out=out.ap, in_=in_.ap)
        slot.cnt += 16
        ins.then_inc(slot.sem, 16)
        in_.res.r[slot] = slot.cnt
        out.res.w[slot] = slot.cnt

    def mm(self, out, lhsT, rhs, start=True, stop=True, tp=None):
        kw = {}
        if tp is not None:
            kw["tile_position"] = tp
        self.op("pe", lambda e: e.matmul(out.ap, lhsT.ap, rhs.ap, start=start, stop=stop, **kw), [lhsT, rhs], [out])

    def tr(self, out, in_, ident):
        self.op("pe", lambda e: e.transpose(out.ap, in_.ap, ident.ap), [in_, ident], [out])

    def act(self, out, in_, func, bias=None, scale=1.0, accum=None):
        reads = [in_]
        kw = {}
        if bias is not None:
            if isinstance(bias, V):
                reads.append(bias)
                kw["bias"] = bias.ap
            else:
                kw["bias"] = bias
        if isinstance(scale, V):
            reads.append(scale)
            kw["scale"] = scale.ap
        else:
            kw["scale"] = scale
        writes = [out]
        if accum is not None:
            writes.append(accum)
            kw["accum_out"] = accum.ap
        self.op("act", lambda e: e.activation(out.ap, in_.ap, func, **kw), reads, writes)

    def tt(self, en, out, a, b, op):
        self.op(en, lambda e: e.tensor_tensor(out.ap, a.ap, b.ap, op), [a, b], [out])

    def stt(self, en, out, in0, scalar, in1, op0, op1):
        reads = [in0, in1]
        s = scalar
        if isinstance(scalar, V):
            reads.append(scalar)
            s = scalar.ap
        self.op(en, lambda e: e.scalar_tensor_tensor(out.ap, in0.ap, s, in1.ap, op0, op1), reads, [out])

    def ts(self, en, out, in0, s1, op0, s2=None, op1=None):
        reads = [in0]
        a1 = s1
        if isinstance(s1, V):
            reads.append(s1)
            a1 = s1.ap
        a2 = s2
        if isinstance(s2, V):
            reads.append(s2)
            a2 = s2.ap
        if op1 is None:
            self.op(en, lambda e: e.tensor_scalar(out.ap, in0.ap, a1, None, op0), reads, [out])
        else:
            self.op(en, lambda e: e.tensor_scalar(out.ap, in0.ap, a1, a2, op0, op1), reads, [out])

    def cp(self, en, out, in_):
        if en == "act":
            self.op("act", lambda e: e.copy(out.ap, in_.ap), [in_], [out])
        else:
            self.op(en, lambda e: e.tensor_copy(out.ap, in_.ap), [in_], [out])

    def amul(self, out, in_, m):
        self.op("act", lambda e: e.mul(out.ap, in_.ap, m.ap), [in_, m], [out])

    def recip(self, out, in_):
        self.op("dve", lambda e: e.reciprocal(out.ap, in_.ap), [in_], [out])

    def memset(self, en, out, val):
        self.op(en, lambda e: e.memset(out.ap, val), [], [out])


class StopBuild(Exception):
    pass


import contextlib


class Scope(contextlib.ExitStack):
    def __init__(self, K):
        super().__init__()
        self.K = K
        self.tiles = []

    def __exit__(self, *a):
        fr = self.K.freed
        for v in self.tiles:
            for d in (v.res.w, v.res.r):
                for p, i in d.items():
                    fr[p] = max(fr.get(p, 0), i)
        self.tiles = []
        return super().__exit__(*a)

    def close(self):
        self.__exit__(None, None, None)


def build_program(stop=None, taps=()):
    nc = bass.Bass("TRN2", target_bir_lowering=False)
    K = Ker(nc)
    K.slots = []
    K.freed = {}
    K.tapped = {}

    def checkpoint(name):
        if stop == name:
            raise StopBuild()

    def tap(name, v, shape, dt=F32):
        if name not in taps or name in K.tapped:
            return
        d = V(nc.dram_tensor("dbg_" + name, list(shape), dt, kind="ExternalOutput").ap())
        K.tapped[name] = d
        K.dma("sp", d, v, K.slot())

    def din(name, shape, dt=F32):
        return V(nc.dram_tensor(name, list(shape), dt, kind="ExternalInput").ap())

    x_d = din("x", [NT, 1024])
    wG_d = din("wG", [1024, 1664])
    wLB_d = din("wLB", [1024, 2048])
    wLA_d = din("wLA", [1024, 1536])
    woab_d = din("woab", [1024, 1024])
    wG1_d = din("wG1", [1024, 384])
    wL1_d = din("wL1", [1024, 2048])
    woc_d = din("woc", [1024, 1024])
    gcol_d = din("gcol", [128, 16])
    fn_d = din("fn", [1, 1024])
    gqk_d = din("gqk", [128, 4])
    rdec_d = din("rdec", [128, 12])
    sink_d = din("sinkl", [128, 8])
    relb_d = din("relb", [32, 16])
    ident_d = din("ident", [128, 128])
    onesblk_d = din("onesblk", [128, 128])
    mm_d = din("mmat", [128, 4, 128])
    iot_d = din("iot", [128, 4, 512])
    oh_d = din("oh", [32, 640])
    inwin_d = din("inwin", [16, 640])
    tabA_d = din("tabA", [2, 128, NT])
    tabB_d = din("tabB", [2, 128, NT])
    maskA_d = din("maskA", [128, 256])
    maskW_d = din("maskW", [128, 96])
    rfb_d = din("rfb", [128, 64])
    y_d = V(nc.dram_tensor("y", [NT, 1024], F32, kind="ExternalOutput").ap())
    x1_d = V(nc.dram_tensor("x1s", [NT, 1024], F32, kind="Internal").ap())
    mixb_d = V(nc.dram_tensor("mixbs", [4, 128, NT], BF16, kind="Internal").ap())
    vec_h = nc.dram_tensor("vecs", [16, 640], BF16, kind="Internal")
    vec_d = V(vec_h.ap())
    aident_d = din("aident", [128, 128])
    xnT0_d = V(nc.dram_tensor("xnT0s", [NCH, 128, 8, CH], BF16, kind="Internal").ap())
    xnT1_d = V(nc.dram_tensor("xnT1s", [NCH, 128, 8, CH], BF16, kind="Internal").ap())

    es = Scope(K)
    uid = [0]

    def sb(name, shape, dt=F32, stack=None):
        uid[0] += 1
        st_ = stack if stack is not None else es
        t = st_.enter_context(nc.sbuf_tensor("sb%d_%s" % (uid[0], name), list(shape), dt))
        v = V(t[:])
        v.res.w = dict(K.freed)
        st_.tiles.append(v)
        return v

    def ps(name, shape, dt=F32, stack=None):
        uid[0] += 1
        st_ = stack if stack is not None else es
        t = st_.enter_context(nc.psum_tensor("ps%d_%s" % (uid[0], name), list(shape), dt))
        v = V(t[:])
        v.res.excl = True
        v.res.w = dict(K.freed)
        st_.tiles.append(v)
        return v

    try:
        with es:
            cslot = K.slot()
            consts = []

            def cload(name, src, shape, dt=F32, q="sp"):
                t = sb(name, shape, dt)
                K.dma(q, t, src, cslot)
                consts.append(t)
                return t

            gcol = cload("gcol", gcol_d, [128, 16])
            gqk = cload("gqk", gqk_d, [128, 4])
            rdec = cload("rdec", rdec_d, [128, 12])
            sinkl = cload("sinkl", sink_d, [128, 8])
            maskA = cload("maskA", maskA_d, [128, 256])
            maskW = cload("maskW", maskW_d, [128, 96])
            rfb = cload("rfb", rfb_d, [128, 64])
            ident32 = cload("ident32", ident_d, [128, 128])
            onesblk32 = cload("onesblk32", onesblk_d, [128, 128])
            for c in consts:
                c.res.w[cslot] = cslot.cnt
            ident = sb("ident", [128, 128], BF16)
            onesblk = sb("onesblk", [128, 128], BF16)
            ones = sb("ones", [128, 128], BF16)
            epsb = sb("epsb", [128, 1])
            K.cp("dve", ident, ident32)
            K.cp("dve", onesblk, onesblk32)
            K.memset("dve", ones, 1.0)
            K.memset("dve", epsb, EPS)

            checkpoint("c0")
            wbf = sb("wbf", [128, 8, 2048], BF16)
            wobf = sb("wobf", [128, 8, 1024], BF16)
            wst = [sb("wst%d" % i, [128, 1024]) for i in range(2)]
            wst_slot = [K.slot() for _ in range(2)]
            xnTs = [sb("xnT%d" % i, [128, 8, CH], BF16) for i in range(2)]
            xnT_slot = [K.slot() for _ in range(2)]
            cur = {"xnT": xnTs[0]}
            xch_slot = [K.slot() for _ in range(4)]
            xst_slot = [K.slot() for _ in range(2)]
            EBT = sb("EBT", [128, 16, 3, 128], BF16)
            esk = sb("esk", [128, 8], F32)
            K.act(esk, sinkl, AF.Exp)
            with Scope(K) as S1:
                relb = sb("relb", [32, 16], F32, S1)
                oh = sb("oh", [32, 640], F32, S1)
                inw = sb("inw", [16, 640], F32, S1)
                e_slot = K.slot()
                K.dma("sp", relb, relb_d, e_slot)
                K.dma("sp", oh, oh_d, e_slot)
                K.dma("sp", inw, inwin_d, e_slot)
                for t_ in (relb, oh, inw):
                    t_.res.w[e_slot] = e_slot.cnt
                pv = ps("pv", [16, 1024], F32, S1)[:, 0:640]
                vec = sb("vec", [16, 640], F32, S1)
                vecb = sb("vecb", [16, 640], BF16, S1)
                K.mm(pv[:, 0:512], relb, oh[:, 0:512])
                K.mm(pv[:, 512:640], relb, oh[:, 512:640])
                K.act(vec, pv, AF.Exp)
                K.tt("dve", vecb, vec, inw, ALU.mult)
                v_slot = K.slot()
                K.dma("sp", vec_d, vecb, v_slot)
                g_slot = K.slot()
                aid32 = sb("aid32", [128, 128], F32, S1)
                K.dma("sp", aid32, aident_d, g_slot)
                aid = sb("aid", [128, 128], BF16, S1)
                K.cp("dve", aid, aid32)
                TT = sb("TT", [128, 16 * 384], BF16, S1)
                src = V(bass.AP(vec_h, 0, [[1, 128], [640, 16], [1, 384]]), vec_d.res)
                K.dma("sp", TT.re("p (h j) -> p h j", h=16), src, g_slot)
                prev = ps("prev", [128, 2, CH], F32, S1)
                EBTf = EBT.re("p h o q -> p (h o q)")
                for n_ in range(12):
                    K.mm(prev[:, n_ % 2, :], aid, TT[:, n_ * 512:(n_ + 1) * 512])
                    K.cp("act" if n_ % 2 == 0 else "dve", EBTf[:, n_ * 512:(n_ + 1) * 512], prev[:, n_ % 2, :])
            wcount = [0]

            def handoff(srcs, dsts):
                for d_ in dsts:
                    for s_ in srcs:
                        for dd in (s_.res.w, s_.res.r):
                            for p_, i_ in dd.items():
                                d_.res.w[p_] = max(d_.res.w.get(p_, 0), i_)

            def subview(parent, ap):
                v = V(ap)
                v.res.excl = parent.res.excl
                v.res.w = dict(parent.res.w)
                return v

            class MX:
                pass

            def alloc_mx(scope, full=True):
                m = MX()
                m.xch = sb("xch", [128, 4, 1024], F32, scope)
                if full:
                    m.xn = [sb("xn%d" % i, [128, 1024], BF16, scope) for i in range(2)]
                    m.junk = sb("junk", [128, 1024], BF16, scope)
                    m.ss = sb("ss", [128, 4], F32, scope)
                    m.lnv4 = sb("lnv4", [128, 4], F32, scope)
                    m.rstd4 = sb("rstd4", [128, 4], F32, scope)
                return m

            def load_x(m, src_d, C):
                for b in range(4):
                    r0 = C * CH + b * 128
                    K.dma("sp", m.xch[:, b, :], src_d[r0:r0 + 128, :], xch_slot[b])

            def store_xnT(dst_d, C, slot_i):
                K.dma("pool", dst_d[C], cur["xnT"], xst_slot[slot_i])

            def load_xnT(src_d, C):
                i = C % 2
                K.dma("sp", xnTs[i], src_d[C], xnT_slot[i])

            def use_xnT(C):
                cur["xnT"] = xnTs[C % 2]

            def load_w_piece(dst, d0, src_d, kc, c0, c1, layer_g):
                i = wcount[0] % 2
                wcount[0] += 1
                n_ = c1 - c0
                K.dma("sp", wst[i][:, 0:n_], src_d[kc * 128:(kc + 1) * 128, c0:c1], wst_slot[i])
                en = "act" if (wcount[0] % 2 == 0) else "dve"
                if layer_g is None:
                    K.cp(en, dst[:, kc, d0:d0 + n_], wst[i][:, 0:n_])
                elif en == "act":
                    K.amul(dst[:, kc, d0:d0 + n_], wst[i][:, 0:n_], gcol[:, layer_g * 8 + kc:layer_g * 8 + kc + 1])
                else:
                    K.ts("dve", dst[:, kc, d0:d0 + n_], wst[i][:, 0:n_], gcol[:, layer_g * 8 + kc:layer_g * 8 + kc + 1], ALU.mult)

            def load_w(dst, src_d, ncols, layer_g):
                for kc in range(8):
                    for c0 in range(0, ncols, 1024):
                        c1 = min(ncols, c0 + 1024)
                        load_w_piece(dst, c0, src_d, kc, c0, c1, layer_g)

            wlo = V(wbf.ap[:, :, 0:1536])
            whi = V(wbf.ap[:, :, 1536:2048])
            wmode = {"split": False, "base": 0}

            def wcol(kc, c0, n_=128):
                c0 = c0 + wmode["base"]
                if not wmode["split"]:
                    return wbf[:, kc, c0:c0 + n_]
                if c0 + n_ <= 1536:
                    return wlo[:, kc, c0:c0 + n_]
                return whi[:, kc, c0 - 1536:c0 - 1536 + n_]

            def xnT_front(m, src_d, C):
                load_x(m, src_d, C)
                for b in range(4):
                    K.act(m.junk, m.xch[:, b, :], AF.Square, accum=m.ss[:, b:b + 1])
                K.act(m.lnv4, m.ss, AF.Ln, bias=epsb[:, 0:1], scale=1.0 / 1024.0)
                K.act(m.rstd4, m.lnv4, AF.Exp, scale=-0.5)

            def xnT_back(m, C, pTl):
                xnT = xnTs[C % 2]
                for b in range(4):
                    xb = m.xn[b % 2]
                    pT_ = pTl[b % len(pTl)]
                    K.ts("dve", xb, m.xch[:, b, :], m.rstd4[:, b:b + 1], ALU.mult)
                    for kc in range(8):
                        K.tr(pT_[:, kc, :], xb[:, kc * 128:(kc + 1) * 128], ident)
                    K.cp("act" if b % 2 == 0 else "dve", xnT[:, :, b * 128:(b + 1) * 128], pT_)

            def make_xnT(m, src_d, C, pT):
                use_xnT(C)
                xnT_front(m, src_d, C)
                xnT_back(m, C, pT if isinstance(pT, list) else [pT])

            def proj(dst, c0):
                for kc in range(8):
                    K.mm(dst, wcol(kc, c0), cur["xnT"][:, kc, :], start=(kc == 0), stop=(kc == 7))

            def rsq_bcast(dst, src_ps, nfeat, sq, psn, lnv, lhs_ones):
                K.act(sq, src_ps, AF.Square)
                K.mm(psn, lhs_ones, sq)
                K.act(lnv, psn, AF.Ln, bias=epsb[:, 0:1], scale=1.0 / nfeat)
                K.act(dst, lnv, AF.Exp, scale=-0.5)

            with Scope(K) as L0:
                LR = Scope(K)
                KaT = sb("KaT", [128, 2, NT], BF16, L0)
                Va = sb("Va", [128, NB, 128], BF16, L0)
                tabc = sb("tabc", [128, 2, CH], F32, L0)
                tab_slot = K.slot()
                tabd_slot = K.slot()
                sq = sb("sq", [128, CH], BF16, L0)
                lnv = sb("lnv", [128, CH], F32, L0)
                rs = sb("rs", [128, CH], F32, L0)
                t1 = sb("t1", [128, CH], F32, L0)
                t2 = sb("t2", [128, CH], F32, L0)
                tabg = sb("tabg", [128, 2, CH], F32, L0)
                SbAll = sb("SbAll", [128, 2, NB, 128], BF16, LR)
                tabd = sb("tabd", [128, 2, CH], F32, LR)
                vbtm = sb("vbtm", [128, 4, 512], BF16, LR)
                lg = sb("lg", [128, 12], F32, LR)
                K.act(lg, rdec, AF.Exp)
                K.ts("dve", lg, lg, -1.0, ALU.mult)
                cd = sb("cd", [128, 4], F32, LR)
                K.act(cd, lg[:, 0:4], AF.Exp, scale=128.0)
                cdr = sb("cdr", [128, 4, NB], F32, LR)
                for j in range(4):
                    off = 0 if j < 2 else 32
                    K.ts("dve", cdr[:, j, :], rfb[:, off:off + 32], cd[:, j:j + 1], ALU.mult)
                checkpoint("c1")
                QF4 = sb("QF4", [128, 2, CH], F32, LR)
                QB4 = sb("QB4", [128, 2, CH], F32, LR)
                KF4 = sb("KF4", [128, 2, CH], F32, LR)
                KB4 = sb("KB4", [128, 2, CH], F32, LR)
                DT = sb("DT", [128, 4, 128], F32, LR)
                with Scope(K) as S0:
                    iot = sb("iot", [128, 4, CH], F32, S0)
                    K.dma("sp", iot, iot_d, tabd_slot)
                    for p in range(2):
                        K.act(QF4[:, p, :], iot[:, 0, :], AF.Exp, scale=lg[:, p:p + 1])
                        K.act(QB4[:, p, :], iot[:, 1, :], AF.Exp, scale=lg[:, 2 + p:3 + p])
                        K.act(KF4[:, p, :], iot[:, 2, :], AF.Exp, scale=lg[:, p:p + 1])
                        K.act(KB4[:, p, :], iot[:, 3, :], AF.Exp, scale=lg[:, 2 + p:3 + p])
                    K.ts("dve", KF4, KF4, 0.125, ALU.mult)
                    K.ts("dve", KB4, KB4, 0.125, ALU.mult)
                    mmat = sb("mmat", [128, 4, 128], F32, S0)
                    K.dma("sp", mmat, mm_d, tab_slot)
                    d1 = sb("d1", [128, 128], F32, S0)
                    d2 = sb("d2", [128, 128], F32, S0)
                    for h in range(4):
                        checkpoint("d0")
                        K.act(d1, mmat[:, 0, :], AF.Exp, scale=lg[:, 4 + h:5 + h])
                        checkpoint("d1")
                        K.tt("dve", d1, d1, mmat[:, 1, :], ALU.mult)
                        checkpoint("d2")
                        K.act(d2, mmat[:, 2, :], AF.Exp, scale=lg[:, 8 + h:9 + h])
                        K.tt("dve", d2, d2, mmat[:, 3, :], ALU.mult)
                        K.tt("dve", d1, d1, d2, ALU.add)
                        checkpoint("d3")
                        K.ts("dve", DT[:, h, :], d1, 0.125, ALU.mult)
                        checkpoint("d4")

                def load_tab(dst, slot, src_d, C):
                    K.dma("sp", dst, V(src_d.ap[:, :, C * CH:(C + 1) * CH].rearrange("t p c -> p t c"), src_d.res), slot)

                def rope(psa, psb, tab, out32, ga=None, gb=None):
                    if ga is None:
                        K.tt("dve", t1, psa, tab[:, 0, :], ALU.mult)
                        K.tt("dve", t2, psb, tab[:, 1, :], ALU.mult)
                    else:
                        K.amul(tabg[:, 0, :], tab[:, 0, :], ga)
                        K.amul(tabg[:, 1, :], tab[:, 1, :], gb)
                        K.tt("dve", t1, psa, tabg[:, 0, :], ALU.mult)
                        K.tt("dve", t2, psb, tabg[:, 1, :], ALU.mult)
                    K.tt("pool", out32, t1, t2, ALU.add)

                checkpoint("setup0")
                load_w(wbf, wG_d, 1664, 0)
                with Scope(K) as PG:
                    pT = ps("pT", [128, 8, 128], BF16, PG)
                    pk = pT.re("p (c t) q -> p c t q", t=2)
                    pbig = ps("pbigG", [128, 7, CH], F32, PG)
                    bk = [subview(pbig, pbig.ap[:, i, :]) for i in range(7)]
                    pn = bk[4]
                    pT2g = V(bk[5].ap.bitcast(BF16).rearrange("p (k q) -> p k q", k=8), bk[5].res)
                    pkv = V(bk[6].ap[:, 0:256].rearrange("p (a b) -> p a b", a=2), bk[6].res)
                    kdbT = sb("kdbT", [128, 2, CH], BF16, PG)
                    kdbtm = sb("kdbtm", [128, 4, 2, 128], BF16, PG)
                    Rb = sb("Rb", [128, 2, 128], F32, PG)
                    mxg = alloc_mx(PG)
                    WS = [dict(sq=sq, lnv=lnv, rs=rs, t1=t1, t2=t2),
                          dict(sq=sb("wsq", [128, CH], BF16, PG), lnv=sb("wlnv", [128, CH], F32, PG),
                               rs=sb("wrs", [128, CH], F32, PG), t1=sb("wt1", [128, CH], F32, PG),
                               t2=sb("wt2", [128, CH], F32, PG))]
                    K.memset("dve", Rb, 0.0)
                    for C in range(NCH - 1, -1, -1):
                        make_xnT(mxg, x_d, C, [pT, pT2g])
                        store_xnT(xnT0_d, C, C % 2)
                        load_w_piece(wobf, 0, woab_d, C, 0, 1024, None)
                        load_tab(tabc, tab_slot, tabA_d, C)
                        load_tab(tabd, tabd_slot, tabB_d, C)
                        K.amul(tabg[:, 0, :], tabc[:, 0, :], gqk[:, 2:3])
                        K.amul(tabg[:, 1, :], tabc[:, 1, :], gqk[:, 3:4])
                        for t in range(2):
                            w_ = WS[t % 2]
                            pa_, pb_ = bk[2 * t], bk[2 * t + 1]
                            proj(pa_, t * 128)
                            proj(pb_, 256 + t * 128)
                            rsq_bcast(w_["rs"], pa_, 64.0, w_["sq"], pn, w_["lnv"], onesblk)
                            K.tt("dve", w_["t1"], pa_, tabg[:, 0, :], ALU.mult)
                            K.tt("dve", w_["t2"], pb_, tabg[:, 1, :], ALU.mult)
                            K.tt("pool", w_["t1"], w_["t1"], w_["t2"], ALU.add)
                            K.tt("pool", KaT[:, t, C * CH:(C + 1) * CH], w_["t1"], w_["rs"], ALU.mult)
                        for t in range(2):
                            w_ = WS[t % 2]
                            pa_, pb_ = bk[2 * t], bk[2 * t + 1]
                            proj(pa_, 512 + t * 128)
                            proj(pb_, 768 + t * 128)
                            K.tt("dve", w_["t1"], pa_, tabd[:, 0, :], ALU.mult)
                            K.tt("dve", w_["t2"], pb_, tabd[:, 1, :], ALU.mult)
                            K.tt("pool", w_["t1"], w_["t1"], w_["t2"], ALU.add)
                            K.tt("pool", kdbT[:, t, :], w_["t1"], KB4[:, t, :], ALU.mult)
                        for b in range(4):
                            pva = (bk[4] if b % 2 == 0 else bk[2])[:, 0:128]
                            pvb = bk[5] if b % 2 == 0 else bk[3]
                            for kc in range(8):
                                K.mm(pva, cur["xnT"][:, kc, b * 128:(b + 1) * 128], wbf[:, kc, 1024:1152], start=(kc == 0), stop=(kc == 7))
                            for kc in range(8):
                                K.mm(pvb, cur["xnT"][:, kc, b * 128:(b + 1) * 128], wbf[:, kc, 1152:1664], start=(kc == 0), stop=(kc == 7))
                            K.cp("act", Va[:, C * 4 + b, :], pva)
                            K.cp("dve", vbtm[:, b, :], pvb)
                        for t in range(2):
                            for cj in range(4):
                                K.tr(pk[:, cj, t, :], kdbT[:, t, cj * 128:(cj + 1) * 128], ident)
                        K.cp("act", kdbtm, pk)
                        for cj in range(3, -1, -1):
                            n = C * 4 + cj
                            for p in range(2):
                                K.mm(pkv[0:64, p, :], kdbtm[:, cj, p, 0:64], vbtm[:, cj, (2 * p) * 128:(2 * p + 1) * 128])
                                K.mm(pkv[64:128, p, :], kdbtm[:, cj, p, 64:128], vbtm[:, cj, (2 * p + 1) * 128:(2 * p + 2) * 128], tp=(0, 64))
                            K.ts("dve", SbAll[:, :, n, :], Rb, rfb[:, 32 + n:33 + n], ALU.mult)
                            for p in range(2):
                                K.ts("dve", Rb[:, p, :], Rb[:, p, :], cdr[:, 2 + p, n:n + 1], ALU.mult)
                                K.tt("dve", Rb[:, p, :], pkv[:, p, :], Rb[:, p, :], ALU.add)

                tap("KaT", KaT, [128, 2, NT], BF16)
                tap("Va", Va, [128, NB, 128], BF16)
                tap("SbAll", SbAll, [128, 2, NB, 128], BF16)
                checkpoint("G")
                load_w(wbf, wLB_d, 2048, 0)
                with Scope(K) as PB:
                    pT = ps("pT", [128, 8, 128], BF16, PB)
                    pk = pT.re("p (c t) q -> p c t q", t=2)
                    pbig = ps("pbigB", [128, 7, CH], F32, PB)
                    bk = [subview(pbig, pbig.ap[:, i, :]) for i in range(7)]
                    pa, pb, pss = bk[0], bk[1], bk[2]
                    po = subview(pbig, pbig.ap[:, 3:7, :])
                    qrT = sb("qrT", [128, 2, CH], BF16, PB)
                    qdf = sb("qdf", [128, 2, CH], BF16, PB)
                    qdb = sb("qdb", [128, 2, CH], BF16, PB)
                    krT = sb("krT", [128, 2, CH], BF16, PB)
                    kdfT = sb("kdfT", [128, 2, CH], BF16, PB)
                    kdftm = sb("kdftm", [128, 4, 2, 128], BF16, PB)
                    sg = sb("sg", [128, 4, CH], BF16, PB)
                    ATs = [sb("AT%d" % i, [128, 4, 128], BF16, PB) for i in range(2)]
                    Sfs = [sb("Sf%d" % i, [128, 2, 128], BF16, PB) for i in range(2)]
                    Rf = sb("Rf", [128, 2, 128], F32, PB)
                    mixBc = [sb("mixBc%d" % i, [128, 4, CH], BF16, PB) for i in range(1)]
                    mixB_slot = [K.slot() for _ in range(1)]
                    WS = [dict(sq=sq, lnv=lnv, rs=rs, t1=t1, t2=t2),
                          dict(sq=sb("wsq", [128, CH], BF16, PB), lnv=sb("wlnv", [128, CH], F32, PB),
                               rs=sb("wrs", [128, CH], F32, PB), t1=sb("wt1", [128, CH], F32, PB),
                               t2=sb("wt2", [128, CH], F32, PB))]
                    K.memset("dve", Rf, 0.0)
                    load_xnT(xnT0_d, 0)
                    pairs = [(bk[0], bk[1]), (bk[3], bk[4]), (bk[5], bk[6])]
                    for C in range(NCH):
                        use_xnT(C)
                        if C + 1 < NCH:
                            load_xnT(xnT0_d, C + 1)
                        load_tab(tabd, tabd_slot, tabB_d, C)
                        handoff([po], bk[3:7])
                        ip = 0
                        for t in range(2):
                            w_ = WS[ip % 2]
                            pa_, pb_ = pairs[ip % 3]
                            ip += 1
                            proj(pa_, t * 128)
                            proj(pb_, 256 + t * 128)
                            K.tt("dve", w_["t1"], pa_, tabd[:, 0, :], ALU.mult)
                            K.tt("dve", w_["t2"], pb_, tabd[:, 1, :], ALU.mult)
                            K.tt("pool", w_["t1"], w_["t1"], w_["t2"], ALU.add)
                            K.cp("act", qrT[:, t, :], w_["t1"])
                            K.tt("pool", qdf[:, t, :], w_["t1"], QF4[:, t, :], ALU.mult)
                            K.tt("pool", qdb[:, t, :], w_["t1"], QB4[:, t, :], ALU.mult)
                        for t in range(2):
                            w_ = WS[ip % 2]
                            pa_, pb_ = pairs[ip % 3]
                            ip += 1
                            proj(pa_, 512 + t * 128)
                            proj(pb_, 768 + t * 128)
                            K.tt("dve", w_["t1"], pa_, tabd[:, 0, :], ALU.mult)
                            K.tt("dve", w_["t2"], pb_, tabd[:, 1, :], ALU.mult)
                            K.tt("pool", w_["t1"], w_["t1"], w_["t2"], ALU.add)
                            K.cp("act", krT[:, t, :], w_["t1"])
                            K.tt("pool", kdfT[:, t, :], w_["t1"], KF4[:, t, :], ALU.mult)
                        for h in range(4):
                            pa_ = bk[3 + h]
                            proj(pa_, 1024 + h * 128)
                            K.act(sg[:, h, :], pa_, AF.Silu)
                        for b in range(4):
                            pv_ = bk[1 + b % 2]
                            for kc in range(8):
                                K.mm(pv_, cur["xnT"][:, kc, b * 128:(b + 1) * 128], wbf[:, kc, 1536:2048], start=(kc == 0), stop=(kc == 7))
                            K.cp("dve", vbtm[:, b, :], pv_)
                        for t in range(2):
                            for cj in range(4):
                                K.tr(pk[:, cj, t, :], kdfT[:, t, cj * 128:(cj + 1) * 128], ident)
                        K.cp("act", kdftm, pk)
                        handoff(bk[3:7], [po])
                        for cj in range(4):
                            n = C * 4 + cj
                            cs = slice(cj * 128, (cj + 1) * 128)
                            Sf = Sfs[cj % 2]
                            AT = ATs[cj % 2]
                            K.ts("dve", Sf, Rf, rfb[:, n:n + 1], ALU.mult)
                            for p in range(2):
                                K.mm(pa[0:64, p * 128:(p + 1) * 128], kdftm[:, cj, p, 0:64], vbtm[:, cj, (2 * p) * 128:(2 * p + 1) * 128])
                                K.mm(pa[64:128, p * 128:(p + 1) * 128], kdftm[:, cj, p, 64:128], vbtm[:, cj, (2 * p + 1) * 128:(2 * p + 2) * 128], tp=(0, 64))
                            for p in range(2):
                                K.ts("dve", Rf[:, p, :], Rf[:, p, :], cdr[:, p, n:n + 1], ALU.mult)
                                K.tt("dve", Rf[:, p, :], pa[:, p * 128:(p + 1) * 128], Rf[:, p, :], ALU.add)
                            for h in range(4):
                                t, r0 = h // 2, (h % 2) * 64
                                pdst = pss if (h % 2 == 0) else pb
                                K.mm(pdst[:, t * 128:(t + 1) * 128], krT[r0:r0 + 64, t, cs], qrT[r0:r0 + 64, t, cs])
                            ATv = AT.re("p (t hp) i -> p hp t i", hp=2)
                            DTv = DT.re("p (t hp) i -> p hp t i", hp=2)
                            K.tt("dve", ATv[:, 0, :, :], pss[:, 0:256].re("p (t i) -> p t i", t=2), DTv[:, 0, :, :], ALU.mult)
                            K.tt("dve", ATv[:, 1, :, :], pb[:, 0:256].re("p (t i) -> p t i", t=2), DTv[:, 1, :, :], ALU.mult)
                            for h in range(4):
                                t, r0 = h // 2, (h % 2) * 64
                                K.mm(po[:, h, cs], vbtm[:, cj, h * 128:(h + 1) * 128], AT[:, h, :], start=True, stop=False)
                                K.mm(po[:, h, cs], Sf[r0:r0 + 64, t, :], qdf[r0:r0 + 64, t, cs], start=False, stop=False)
                                K.mm(po[:, h, cs], SbAll[r0:r0 + 64, t, n, :], qdb[r0:r0 + 64, t, cs], start=False, stop=True)
                        mb = mixBc[0]
                        for h0 in (0, 2):
                            hs = (h0, h0 + 1)
                            for h in hs:
                                K.act(WS[h % 2]["sq"], po[:, h, :], AF.Square)
                            for h in hs:
                                K.mm(bk[h % 2], ones, WS[h % 2]["sq"])
                            for h in hs:
                                K.act(WS[h % 2]["lnv"], bk[h % 2], AF.Ln, bias=epsb[:, 0:1], scale=1.0 / 128.0)
                            for h in hs:
                                K.act(WS[h % 2]["rs"], WS[h % 2]["lnv"], AF.Exp, scale=-0.5)
                            for h in hs:
                                K.tt("dve", WS[h % 2]["t1"], po[:, h, :], WS[h % 2]["rs"], ALU.mult)
                                K.tt("pool", mb[:, h, :], WS[h % 2]["t1"], sg[:, h, :], ALU.mult)
                        K.dma("pool", V(mixb_d.ap[:, :, C * CH:(C + 1) * CH].rearrange("h p c -> p h c"), mixb_d.res), mb, mixB_slot[0])

                tap("mixb", mixb_d, [4, 128, NT], BF16)
                checkpoint("LB")
                LR.close()
                load_w(wbf, wLA_d, 1536, 0)
                handoff([wbf], [wlo, whi])
                wmode["split"] = True
                with Scope(K) as PA:
                    pbig = ps("pbig", [128, 8, CH], F32, PA)
                    psc = [subview(pbig, pbig.ap[:, 2 * i:2 * i + 2, :]) for i in range(3)]
                    pnum = subview(pbig, pbig.ap[:, 6, :])
                    pden = subview(pbig, pbig.ap[:, 7, :])
                    bk = [subview(pbig, pbig.ap[:, i, :]) for i in range(6)] + [pnum, pden]
                    WS = [dict(sq=sq, lnv=lnv, rs=rs, t1=t1, t2=t2),
                          dict(sq=sb("wsq", [128, CH], BF16, PA), lnv=sb("wlnv", [128, CH], F32, PA),
                               rs=sb("wrs", [128, CH], F32, PA), t1=sb("wt1", [128, CH], F32, PA),
                               t2=sb("wt2", [128, CH], F32, PA))]
                    qaT = sb("qaT", [128, 4, CH], BF16, PA)
                    sga = sb("sga", [128, 4, CH], BF16, PA)
                    mixAs = [sb("mixA%d" % i, [128, 4, CH], BF16, PA) for i in range(2)]
                    mixBls = [sb("mixBl%d" % i, [128, 4, CH], BF16, PA) for i in range(2)]
                    mixBl_slots = [K.slot() for _ in range(2)]
                    xblk = [sb("xblk%d" % i, [128, 1024], F32, PA) for i in range(2)]
                    xblk_slot = [K.slot() for _ in range(2)]
                    nxb = [0]
                    pTs = [sb("pTs%d" % i, [128, 2, CH], BF16, PA) for i in range(3)]
                    x1b = [sb("x1b%d" % i, [128, 1024], F32, PA) for i in range(2)]
                    dcp = sb("dcp", [128, CH], F32, PA)
                    ncp = sb("ncp", [128, CH], F32, PA)
                    x1b_slot = [K.slot() for _ in range(2)]
                    qaTs = [qaT, sb("qaT1", [128, 4, CH], BF16, PA)]
                    sgas = [sga, sb("sga1", [128, 4, CH], BF16, PA)]
                    tabcs = [tabc, sb("tabc1", [128, 2, CH], F32, PA)]
                    tabgs = [tabg, sb("tabg1", [128, 2, CH], F32, PA)]
                    tabsl = [tab_slot, K.slot()]
                    nbuf = [0]
                    npt = [0]

                    held = set()

                    def take_buf():
                        while True:
                            i_ = nbuf[0] % 3
                            nbuf[0] += 1
                            if i_ not in held:
                                return psc[i_]

                    def hold(b_):
                        held.add(psc.index(b_))

                    def release(b_):
                        held.discard(psc.index(b_))

                    def projx(dst, c0, xT):
                        for kc in range(8):
                            K.mm(dst, wcol(kc, c0), xT[:, kc, :], start=(kc == 0), stop=(kc == 7))

                    def proj_items(Cn):
                        q_, g_ = qaTs[Cn % 2], sgas[Cn % 2]
                        tc_, tg_ = tabcs[Cn % 2], tabgs[Cn % 2]
                        xT = xnTs[Cn % 2]

                        def prep():
                            load_tab(tc_, tabsl[Cn % 2], tabA_d, Cn)
                            K.amul(tg_[:, 0, :], tc_[:, 0, :], gqk[:, 0:1])
                            K.amul(tg_[:, 1, :], tc_[:, 1, :], gqk[:, 1:2])

                        items = []
                        for t in range(4):
                            def mk(t=t):
                                st = {}
                                w_ = WS[t % 2]

                                def s1():
                                    st["buf"] = take_buf()
                                    hold(st["buf"])
                                    projx(st["buf"][:, 0, :], t * 128, xT)
                                    projx(st["buf"][:, 1, :], 512 + t * 128, xT)

                                def s2():
                                    pa_, pb_ = st["buf"][:, 0, :], st["buf"][:, 1, :]
                                    K.tt("dve", w_["t1"], pa_, tg_[:, 0, :], ALU.mult)
                                    K.tt("dve", w_["t2"], pb_, tg_[:, 1, :], ALU.mult)
                                    K.act(w_["sq"], pa_, AF.Square)

                                def s3():
                                    K.mm(st["buf"][:, 1, :], onesblk, w_["sq"])

                                def s4():
                                    K.act(w_["lnv"], st["buf"][:, 1, :], AF.Ln, bias=epsb[:, 0:1], scale=1.0 / 64.0)
                                    K.act(w_["rs"], w_["lnv"], AF.Exp, scale=-0.5)
                                    K.tt("pool", w_["t1"], w_["t1"], w_["t2"], ALU.add)
                                    K.tt("pool", q_[:, t, :], w_["t1"], w_["rs"], ALU.mult)
                                    release(st["buf"])
                                return [(s1, 3), (s2, 1), (s3, 2), (s4, 0)]
                            items.append(mk())
                        for t2_ in range(2):
                            def mk(t2_=t2_):
                                st = {}

                                def s1():
                                    st["buf"] = take_buf()
                                    hold(st["buf"])
                                    for j in range(2):
                                        projx(st["buf"][:, j, :], 1024 + (2 * t2_ + j) * 128, xT)

                                def s2():
                                    for j in range(2):
                                        K.act(WS[j]["t1"], st["buf"][:, j, :], AF.Tanh, scale=0.5)

                                def s3():
                                    for j in range(2):
                                        K.ts("dve", WS[j]["t1"], WS[j]["t1"], 0.5, ALU.mult, 0.5, ALU.add)
                                        K.tt("dve", g_[:, 2 * t2_ + j, :], st["buf"][:, j, :], WS[j]["t1"], ALU.mult)
                                    release(st["buf"])
                                return [(s1, 3), (s2, 1), (s3, 0)]
                            items.append(mk())
                        return prep, items

                    def outproj_items(Cc):
                        mA, mB = mixAs[Cc % 2], mixBls[Cc % 2]
                        items = []
                        for b in range(4):
                            def mk(b=b):
                                st = {}
                                bs = slice(b * 128, (b + 1) * 128)
                                r0 = Cc * CH + b * 128

                                def s1():
                                    i_ = nxb[0] % 2
                                    nxb[0] += 1
                                    st["i"] = i_
                                    K.dma("sp", xblk[i_], x_d[r0:r0 + 128, :], xblk_slot[i_])
                                    st["buf"] = take_buf()
                                    hold(st["buf"])
                                    py = st["buf"]
                                    for half in range(2):
                                        for f in range(8):
                                            src = mA[:, f, bs] if f < 4 else mB[:, f - 4, bs]
                                            K.mm(py[:, half, :], src, wobf[:, f, half * 512:(half + 1) * 512], start=(f == 0), stop=(f == 7))

                                def s2():
                                    xo = x1b[st["i"]]
                                    K.tt("dve", xo, st["buf"].re("p a c -> p (a c)"), xblk[st["i"]], ALU.add)
                                    K.dma("pool", x1_d[r0:r0 + 128, :], xo, x1b_slot[st["i"]])
                                    release(st["buf"])
                                return [(s1, 4), (s2, 0)]
                            items.append(mk())
                        return items

                    load_xnT(xnT0_d, 0)
                    prep0, items0 = proj_items(0)
                    prep0()
                    for it_ in items0:
                        for st_fn, _d in it_:
                            st_fn()
                    carry = []
                    for C in range(NCH):
                        use_xnT(C)
                        qaT_c, sga_c = qaTs[C % 2], sgas[C % 2]
                        mixA = mixAs[C % 2]
                        load_w_piece(whi, 0, wG1_d, C, 0, 384, 1)
                        pending = list(carry)
                        carry = []
                        if C + 1 < NCH:
                            load_xnT(xnT0_d, C + 1)
                            prepn, pitems = proj_items(C + 1)
                            prepn()
                            pending = pending + pitems
                        K.dma("pool", mixBls[C % 2], V(mixb_d.ap[:, :, C * CH:(C + 1) * CH].rearrange("h p c -> p h c"), mixb_d.res), mixBl_slots[C % 2])
                        nit = 0
                        active = [None]
                        for t in range(4):
                            kv = t // 2
                            fifo = []

                            def qk(kb_):
                                sc_ = take_buf()
                                fifo.append(sc_)
                                ks = slice(kb_ * 128, (kb_ + 1) * 128)
                                K.mm(sc_[:, 0, :], KaT[0:64, kv, ks], qaT_c[0:64, t, :])
                                K.mm(sc_[:, 1, :], KaT[64:128, kv, ks], qaT_c[64:128, t, :])

                            qk(0)
                            qk(1)
                            for kb in range(NB):
                                sc = fifo.pop(0)
                                pt = pTs[npt[0] % 3]
                                npt[0] += 1
                                nit += 1
                                K.act(pt, sc, AF.Exp, bias=maskA[:, C * NB + kb:C * NB + kb + 1], scale=0.125)
                                if kb + 2 < NB:
                                    qk(kb + 2)
                                if active[0] is None and pending and nit % 12 == 3:
                                    active[0] = [pending.pop(0), 0, nit]
                                if active[0] is not None and nit >= active[0][2]:
                                    stages_, si_, _due = active[0]
                                    fn_, delay_ = stages_[si_]
                                    fn_()
                                    if si_ + 1 < len(stages_):
                                        active[0] = [stages_, si_ + 1, nit + delay_]
                                    else:
                                        active[0] = None
                                st, sp_ = (kb == 0), (kb == NB - 1)
                                K.mm(pnum[0:64, :], Va[:, kb, kv * 64:(kv + 1) * 64], pt[:, 0, :], start=st, stop=sp_)
                                K.mm(pnum[64:128, :], Va[:, kb, kv * 64:(kv + 1) * 64], pt[:, 1, :], start=st, stop=sp_, tp=(0, 64))
                                K.mm(pden[0:64, :], ones[:, 0:64], pt[:, 0, :], start=st, stop=sp_)
                                K.mm(pden[64:128, :], ones[:, 0:64], pt[:, 1, :], start=st, stop=sp_, tp=(0, 64))
                            K.cp("dve", dcp, pden)
                            K.cp("dve", ncp, pnum)
                            K.recip(dcp, dcp)
                            K.tt("dve", ncp, ncp, dcp, ALU.mult)
                            K.tt("pool", mixA[:, t, :], ncp, sga_c[:, t, :], ALU.mult)
                        while active[0] is not None or pending:
                            if active[0] is None:
                                active[0] = [pending.pop(0), 0, 0]
                            stages_, si_, _due = active[0]
                            stages_[si_][0]()
                            active[0] = [stages_, si_ + 1, 0] if si_ + 1 < len(stages_) else None
                        carry = outproj_items(C)
                        if C == NCH - 1:
                            for it_ in carry:
                                for st_fn, _d in it_:
                                    st_fn()
                            carry = []

            tap("x1", x1_d, [NT, 1024], F32)
            checkpoint("LA")
            with Scope(K) as L1:
                KcT = sb("KcT", [128, 2, (NB + 2) * 128], BF16, L1)
                Vc = sb("Vc", [128, NB + 2, 128], BF16, L1)
                K.memset("pool", KcT[:, :, 0:128], 0.0)
                K.memset("pool", KcT[:, :, (NB + 1) * 128:(NB + 2) * 128], 0.0)
                K.memset("pool", Vc[:, 0, :], 0.0)
                K.memset("pool", Vc[:, NB + 1, :], 0.0)
                tap("EBT", EBT, [128, 16, 3, 128], BF16)
                checkpoint("EBT")
                wmode["base"] = 1536
                with Scope(K) as PG1:
                    pT = ps("pT", [128, 8, 128], BF16, PG1)
                    pa = ps("pa", [128, CH], F32, PG1)
                    pva = ps("pva", [128, 512], F32, PG1)[:, 0:128]
                    mxs = [alloc_mx(PG1), alloc_mx(PG1)]
                    pT2 = ps("pT2", [128, 8, 128], BF16, PG1)
                    pa2 = ps("pa2", [128, CH], F32, PG1)
                    pva2 = ps("pva2", [128, 512], F32, PG1)[:, 0:128]
                    xnT_front(mxs[0], x1_d, 0)
                    xnT_back(mxs[0], 0, [pT, pT2])
                    for C in range(NCH):
                        use_xnT(C)
                        store_xnT(xnT1_d, C, C % 2)
                        if C + 1 < NCH:
                            xnT_front(mxs[(C + 1) % 2], x1_d, C + 1)
                        load_w_piece(wlo, 0, wL1_d, C, 0, 1024, 1)
                        load_w_piece(wlo, 1024, wL1_d, C, 1024, 1536, 1)
                        load_w_piece(wobf, 0, woc_d, C, 0, 1024, None)
                        for t in range(2):
                            pa_ = pa if t == 0 else pa2
                            proj(pa_, t * 128)
                            K.cp("act" if t == 0 else "dve", KcT[:, t, (C * 4 + 1) * 128:(C * 4 + 5) * 128], pa_)
                        for b in range(4):
                            pv_ = pva if b % 2 == 0 else pva2
                            for kc in range(8):
                                K.mm(pv_, cur["xnT"][:, kc, b * 128:(b + 1) * 128], wcol(kc, 256), start=(kc == 0), stop=(kc == 7))
                            K.cp("dve" if b % 2 == 0 else "act", Vc[:, C * 4 + b + 1, :], pv_)
                        if C + 1 < NCH:
                            xnT_back(mxs[(C + 1) % 2], C + 1, [pT, pT2])

                tap("KcT", KcT, [128, 2, (NB + 2) * 128], BF16)
                tap("Vc", Vc, [128, NB + 2, 128], BF16)
                checkpoint("G1")
                wmode["base"] = 0
                for kc_ in range(8):
                    load_w_piece(whi, 0, wL1_d, kc_, 1536, 2048, 1)
                with Scope(K) as PL1:
                    pbig = ps("pbig1", [128, 8, CH], F32, PL1)
                    pw = [subview(pbig, pbig.ap[:, 2 * i:2 * i + 2, :]) for i in range(2)]
                    pnum = subview(pbig, pbig.ap[:, 6, :])
                    pden = subview(pbig, pbig.ap[:, 7, :])
                    bk = [subview(pbig, pbig.ap[:, i, :]) for i in range(6)]
                    qcT = sb("qcT", [128, 8, CH], BF16, PL1)
                    sgc = sb("sgc", [128, 8, CH], BF16, PL1)
                    mixC = sb("mixC", [128, 8, CH], BF16, PL1)
                    pws = [sb("pws%d" % i, [128, 2, 3, 128], BF16, PL1) for i in range(2)]
                    pw2 = [sb("pw2%d" % i, [128, 2, 3, 128], BF16, PL1) for i in range(2)]
                    rs = sb("rs1", [128, CH], F32, PL1)
                    lnr = sb("lnr", [128, CH], F32, PL1)
                    t1 = sb("t11", [128, CH], F32, PL1)
                    EP = [dict(rs=rs, t1=t1, lnr=lnr),
                          dict(rs=sb("rs1b", [128, CH], F32, PL1), t1=sb("t11b", [128, CH], F32, PL1), lnr=sb("lnrb", [128, CH], F32, PL1))]
                    pnums = [pnum, bk[4]]
                    pdens = [pden, bk[5]]
                    x2 = [sb("x2%d" % i, [128, 1024], F32, PL1) for i in range(2)]
                    yo = [sb("yo%d" % i, [128, 1024], F32, PL1) for i in range(2)]
                    yo_slot = [K.slot() for _ in range(2)]
                    ss2 = sb("ss2", [128, 2], F32, PL1)
                    ln2 = sb("ln2", [128, 2], F32, PL1)
                    r2 = sb("r2", [128, 2], F32, PL1)
                    it = 0
                    mxl = alloc_mx(PL1, full=False)
                    xch = mxl.xch
                    junk = sb("junk1", [128, 1024], BF16, PL1)
                    fnbc = sb("fnbc", [128, 1024], F32, PL1)
                    fn_slot = K.slot()
                    K.dma("sp", fnbc, V(fn_d.ap.to_broadcast([128, 1024]), fn_d.res), fn_slot)
                    load_xnT(xnT1_d, 0)
                    for C in range(NCH):
                        use_xnT(C)
                        if C + 1 < NCH:
                            load_xnT(xnT1_d, C + 1)
                        load_x(mxl, x1_d, C)
                        handoff(pw, bk[0:4])
                        for t in range(8):
                            pa_ = bk[t % 6]
                            proj(pa_, t * 128)
                            K.cp("act" if t % 2 == 0 else "dve", qcT[:, t, :], pa_)
                        for t in range(8):
                            pa_ = bk[(t + 2) % 6]
                            proj(pa_, 1024 + t * 128)
                            K.act(sgc[:, t, :], pa_, AF.Silu)
                        handoff(bk[0:4], pw)
                        items = [(t, qi) for t in range(8) for qi in range(4)]

                        def wqk(t, qi, w):
                            kv = t // 4
                            i = C * 4 + qi
                            qs = slice(qi * 128, (qi + 1) * 128)
                            for o in range(3):
                                sl = 2 - o
                                ks = slice((i + o) * 128, (i + o + 1) * 128)
                                K.mm(w[:, 0, sl * 128:(sl + 1) * 128], KcT[0:64, kv, ks], qcT[0:64, t, qs])
                                K.mm(w[:, 1, sl * 128:(sl + 1) * 128], KcT[64:128, kv, ks], qcT[64:128, t, qs])

                        wqk(items[0][0], items[0][1], pw[it % 2])
                        deferred = []
                        for idx, (t, qi) in enumerate(items):
                            kv = t // 4
                            i = C * 4 + qi
                            qs = slice(qi * 128, (qi + 1) * 128)
                            w = pw[it % 2]
                            s1 = pws[it % 2]
                            s2 = pw2[it % 2]
                            it += 1
                            if i in (0, NB // 2 - 1, NB // 2, NB - 1):
                                for o in range(3):
                                    sl = 2 - o
                                    K.act(s1[:, :, sl, :], w[:, :, sl * 128:(sl + 1) * 128], AF.Exp, bias=maskW[:, i * 3 + o:i * 3 + o + 1], scale=0.125)
                            else:
                                K.act(s1, w[:, :, 0:384].re("p h (o q) -> p h o q", o=3), AF.Exp, scale=0.125)
                            K.tt("dve", s2, s1, EBT[:, 2 * t:2 * t + 2, :, :], ALU.mult)
                            if deferred:
                                deferred.pop(0)()
                            if idx + 1 < len(items):
                                wqk(items[idx + 1][0], items[idx + 1][1], pw[it % 2])
                            pnum_, pden_ = pnums[t % 2], pdens[t % 2]
                            for o in range(3):
                                sl = 2 - o
                                st, sp_ = (o == 0), (o == 2)
                                vv = Vc[:, i + o, kv * 64:(kv + 1) * 64]
                                K.mm(pnum_[0:64, qs], vv, s2[:, 0, sl, :], start=st, stop=sp_)
                                K.mm(pnum_[64:128, qs], vv, s2[:, 1, sl, :], start=st, stop=sp_, tp=(0, 64))
                                K.mm(pden_[0:64, qs], ones[:, 0:64], s2[:, 0, sl, :], start=st, stop=sp_)
                                K.mm(pden_[64:128, qs], ones[:, 0:64], s2[:, 1, sl, :], start=st, stop=sp_, tp=(0, 64))
                            if qi == 3:
                                def epi(t=t, pnum_=pnum_, pden_=pden_):
                                    e_ = EP[t % 2]
                                    K.ts("dve", e_["rs"], pden_, esk[:, t:t + 1], ALU.add)
                                    K.cp("dve", e_["t1"], pnum_)
                                    K.act(e_["lnr"], e_["rs"], AF.Ln)
                                    K.act(e_["rs"], e_["lnr"], AF.Exp, scale=-1.0)
                                    K.tt("pool", e_["t1"], e_["t1"], e_["rs"], ALU.mult)
                                    K.tt("pool", mixC[:, t, :], e_["t1"], sgc[:, t, :], ALU.mult)
                                deferred.append(epi)
                        while deferred:
                            deferred.pop(0)()
                        for b in range(4):
                            bs = slice(b * 128, (b + 1) * 128)
                            py = pw[b % 2]
                            for half in range(2):
                                for f in range(8):
                                    K.mm(py[:, half, :], mixC[:, f, bs], wobf[:, f, half * 512:(half + 1) * 512], start=(f == 0), stop=(f == 7))
                            xo = x2[b % 2]
                            K.tt("dve", xo, py.re("p a c -> p (a c)"), xch[:, b, :], ALU.add)
                            K.act(junk, xo, AF.Square, accum=ss2[:, b % 2:b % 2 + 1])
                            K.act(ln2[:, b % 2:b % 2 + 1], ss2[:, b % 2:b % 2 + 1], AF.Ln, bias=epsb[:, 0:1], scale=1.0 / 1024.0)
                            K.act(r2[:, b % 2:b % 2 + 1], ln2[:, b % 2:b % 2 + 1], AF.Exp, scale=-0.5)
                            yb = yo[b % 2]
                            K.ts("dve", yb, xo, r2[:, b % 2:b % 2 + 1], ALU.mult)
                            K.tt("pool", yb, yb, fnbc, ALU.mult)
                            r0 = C * CH + b * 128
                            K.dma("pool", y_d[r0:r0 + 128, :], yb, yo_slot[b % 2])
    except StopBuild:
        pass
    for s_ in K.slots:
        if s_.cnt:
            nc.gpsimd.wait_ge(s_.sem, s_.cnt)
    return nc, K


def _t5_bucket(rel):
    half = 16
    max_exact = 8
    ret = (rel > 0).astype(np.int32) * half
    dist = np.abs(rel)
    large = max_exact + (np.log(np.maximum(dist, 1) / max_exact) / np.log(128 / max_exact) * (half - max_exact)).astype(np.int32)
    large = np.minimum(large, half - 1)
    return ret + np.where(dist < max_exact, dist, large)


def _static_tables():
    f32 = np.float32
    st = {}
    st["ident"] = np.eye(128, dtype=f32)
    st["aident"] = np.ascontiguousarray(np.eye(128, dtype=f32)[::-1])
    ob = np.zeros((128, 128), f32)
    ob[:64, :64] = 1
    ob[64:, 64:] = 1
    st["onesblk"] = ob
    j = np.arange(128)[:, None]
    i = np.arange(128)[None, :]
    mmat = np.zeros((128, 4, 128), f32)
    mmat[:, 0, :] = np.maximum(i - j, 0)
    mmat[:, 1, :] = (i >= j)
    mmat[:, 2, :] = np.maximum(j - i, 0)
    mmat[:, 3, :] = (j > i)
    st["mmat"] = mmat
    c = np.arange(512) % 128
    iot = np.zeros((128, 4, 512), f32)
    iot[:, 0, :] = c + 1
    iot[:, 1, :] = 128 - c
    iot[:, 2, :] = 127 - c
    iot[:, 3, :] = c
    st["iot"] = iot
    m = np.arange(640)
    rel = 255 - m
    bk = _t5_bucket(rel)
    oh = np.zeros((32, 640), f32)
    oh[bk, m] = 1
    st["oh"] = oh
    st["inwin"] = np.broadcast_to((np.abs(rel) <= 128).astype(f32)[None, :], (16, 640)).copy()
    return st


def _core_tables(is_prompt):
    f32 = np.float32
    seqlen = 4096 if is_prompt else 2048
    t = np.arange(NT) % seqlen
    d = np.arange(128) % 64
    pair = d // 2
    sgn = np.where(d % 2 == 0, -1.0, 1.0)
    quarter = 16
    freqs = (np.float32(10000.0) ** (-np.arange(quarter, dtype=f32) / quarter)).astype(f32)
    row = (t // 64).astype(f32)
    col = (t % 64).astype(f32)
    ang = np.concatenate([row[:, None] * freqs, col[:, None] * freqs], axis=-1).astype(f32)
    angd = ang[:, pair].T.astype(np.float64)
    tabA = np.stack([np.cos(angd), np.sin(angd) * sgn[:, None]]).astype(f32)
    half = 32
    freqs_b = (np.float32(10000.0) ** (-np.arange(half, dtype=f32) / half)).astype(f32)
    angb = (t.astype(f32)[:, None] * freqs_b).astype(f32)
    angbd = angb[:, pair].T.astype(np.float64)
    tabB = np.stack([np.cos(angbd), np.sin(angbd) * sgn[:, None]]).astype(f32)
    seq_of_blk = (np.arange(NB) * 128) // seqlen
    maskA = np.zeros((NCH, NB), f32)
    for C in range(NCH):
        sq = (C * CH) // seqlen
        maskA[C, :] = np.where(seq_of_blk == sq, 0.0, NEG)
    maskA = np.broadcast_to(maskA.reshape(1, -1), (128, NCH * NB)).copy()
    maskW = np.zeros((NB, 3), f32)
    for i in range(NB):
        for o in range(3):
            jb = i + o - 1
            if jb < 0 or jb >= NB or seq_of_blk[jb] != seq_of_blk[i]:
                maskW[i, o] = NEG
    maskW = np.broadcast_to(maskW.reshape(1, -1), (128, NB * 3)).copy()
    cps = seqlen // 128
    rf = np.array([0.0 if (n % cps == 0) else 1.0 for n in range(NB)], f32)
    rb = np.array([0.0 if (n % cps == cps - 1) else 1.0 for n in range(NB)], f32)
    rfb = np.broadcast_to(np.concatenate([rf, rb])[None, :], (128, 64)).copy()
    return {"tabA": tabA, "tabB": tabB, "maskA": maskA, "maskW": maskW, "rfb": rfb}


def _swap(cols):
    cols = np.asarray(cols)
    return cols ^ 1


def _prep_common(norm_g, w_in_ab, qk_norm_a, ret_decay, w_out_ab, w_in_c, sink_c, w_out_c, rel_bias, final_norm):
    f32 = np.float32
    W = np.asarray(w_in_ab[0], f32)
    qa = np.arange(0, 512)
    ka = np.arange(512, 640)
    va = np.arange(640, 768)
    ga = np.arange(768, 1280)
    qb = np.arange(1280, 1536)
    kb = np.arange(1536, 1792)
    vb = np.arange(1792, 2304)
    gb = np.arange(2304, 2816)
    kadup = np.concatenate([ka[0:64], ka[0:64], ka[64:128], ka[64:128]])
    cm = {}
    cm["wG"] = np.ascontiguousarray(W[:, np.concatenate([kadup, _swap(kadup), kb, _swap(kb), va, vb])])
    cm["wLB"] = np.ascontiguousarray(W[:, np.concatenate([qb, _swap(qb), kb, _swap(kb), gb, vb])])
    cm["wLA"] = np.ascontiguousarray(W[:, np.concatenate([qa, _swap(qa), ga])])
    cm["woab"] = np.ascontiguousarray(np.asarray(w_out_ab[0], f32))
    Wc = np.asarray(w_in_c[0], f32)
    kc = np.arange(1024, 1152)
    kcdup = np.concatenate([kc[0:64], kc[0:64], kc[64:128], kc[64:128]])
    cm["wG1"] = np.ascontiguousarray(Wc[:, np.concatenate([kcdup, np.arange(1152, 1280)])])
    cm["wL1"] = np.ascontiguousarray(Wc[:, np.concatenate([np.arange(0, 1024), np.arange(1280, 2304)])])
    cm["woc"] = np.ascontiguousarray(np.asarray(w_out_c[0], f32))
    ng = np.asarray(norm_g, f32)
    cm["gcol"] = np.ascontiguousarray(ng.reshape(2, 8, 128).transpose(2, 0, 1).reshape(128, 16))
    cm["fn"] = np.asarray(final_norm, f32).reshape(1, 1024).copy()
    g = np.asarray(qk_norm_a[0], f32)
    d = np.arange(128) % 64
    cm["gqk"] = np.stack([g[0][d], g[0][d ^ 1], g[1][d], g[1][d ^ 1]], axis=1).astype(f32).copy()
    rd = np.asarray(ret_decay[0], f32)
    hp = (np.arange(128) // 64)
    rdec = np.zeros((128, 12), f32)
    for p in range(2):
        rdec[:, p] = rd[0][2 * p + hp]
        rdec[:, 2 + p] = rd[1][2 * p + hp]
    for h in range(4):
        rdec[:, 4 + h] = rd[0][h]
        rdec[:, 8 + h] = rd[1][h]
    cm["rdec"] = rdec
    sk = np.asarray(sink_c[0], f32)
    sinkl = np.zeros((128, 8), f32)
    for t in range(8):
        sinkl[:, t] = sk[2 * t + hp]
    cm["sinkl"] = sinkl
    cm["relb"] = np.ascontiguousarray(np.asarray(rel_bias, f32))
    cm.update(_static_tables())
    return cm


_CACHE = {}


def kernel(x_prompt, x_sample, norm_g, w_in_ab, qk_norm_a, ret_decay, w_out_ab, w_in_c, sink_c, w_out_c, rel_bias, final_norm):
    xp = np.asarray(x_prompt, np.float32)
    xs = np.asarray(x_sample, np.float32)
    cm = _prep_common(norm_g, w_in_ab, qk_norm_a, ret_decay, w_out_ab, w_in_c, sink_c, w_out_c, rel_bias, final_norm)
    tp = _core_tables(True)
    tsm = _core_tables(False)
    in_maps = []
    for c in range(8):
        m = dict(cm)
        if c < 4:
            m["x"] = np.ascontiguousarray(xp[c])
            m.update(tp)
        else:
            m["x"] = np.ascontiguousarray(xs[2 * (c - 4):2 * (c - 4) + 2].reshape(NT, 1024))
            m.update(tsm)
        in_maps.append(m)
    if "nc" not in _CACHE:
        _CACHE["nc"] = build_program()[0]
    nc = _CACHE["nc"]
    res = run_bass_kernel_spmd(nc, in_maps, core_ids=list(range(8)))
    outs = [np.asarray(r["y"], np.float32) for r in res.results]
    y_prompt = np.stack(outs[0:4], axis=0)
    y_sample = np.stack(outs[4:8], axis=0).reshape(8, 2048, 1024)
    return (y_prompt, y_sample)
```

```python
import numpy as np
import concourse.bass as bass
import concourse.mybir as mybir
from concourse.bass_utils import run_bass_kernel_spmd

F32 = mybir.dt.float32
BF16 = mybir.dt.bfloat16
AF = mybir.ActivationFunctionType
ALU = mybir.AluOpType

NT = 4096
NB = 32
CH = 512
NCH = 8
EPS = 1e-6
NEG = -30000.0


class Prod:
    def __init__(self, sem, inc):
        self.sem = sem
        self.inc = inc
        self.cnt = 0


class Res:
    def __init__(self):
        self.w = {}
        self.r = {}
        self.excl = False


class V:
    def __init__(self, ap, res=None):
        self.ap = ap
        self.res = res if res is not None else Res()

    def __getitem__(self, k):
        return V(self.ap[k], self.res)

    def re(self, pat, **kw):
        return V(self.ap.rearrange(pat, **kw), self.res)

    def bc(self, shape):
        return V(self.ap.to_broadcast(shape), self.res)


class Ker:
    def __init__(self, nc):
        self.nc = nc
        self.eng = {"pe": nc.tensor, "act": nc.scalar, "dve": nc.vector, "pool": nc.gpsimd, "sp": nc.sync}
        self.prod = {}
        for n in ("pe", "act", "dve", "pool"):
            self.prod[n] = Prod(nc.alloc_semaphore("s_" + n), 1)
        self.seen = {n: {} for n in self.eng}
        self.nslot = 0
        self.ninstr = 0

    def slot(self):
        self.nslot += 1
        p = Prod(self.nc.alloc_semaphore("d%d" % self.nslot), 16)
        if hasattr(self, "slots"):
            self.slots.append(p)
        return p

    def _wait(self, en, reads, writes):
        deps = {}
        for v in reads:
            for p, i in v.res.w.items():
                deps[p] = max(deps.get(p, 0), i)
        for v in writes:
            for p, i in v.res.w.items():
                deps[p] = max(deps.get(p, 0), i)
            for p, i in v.res.r.items():
                deps[p] = max(deps.get(p, 0), i)
        e = self.eng[en]
        seen = self.seen[en]
        own = self.prod.get(en)
        for p, i in deps.items():
            if p is own and en == "pe":
                continue
            if seen.get(p, 0) >= i:
                continue
            e.wait_ge(p.sem, i)
            seen[p] = i

    def op(self, en, fn, reads, writes):
        writes = list(writes) + [r for r in reads if r.res.excl]
        self._wait(en, reads, writes)
        ins = fn(self.eng[en])
        p = self.prod[en]
        p.cnt += 1
        ins.then_inc(p.sem, 1)
        for v in reads:
            v.res.r[p] = p.cnt
        for v in writes:
            v.res.w[p] = p.cnt
        self.ninstr += 1

    def dma(self, q, out, in_, slot):
        self._wait(q, [in_], [out])
        ins = self.eng[q].dma_start(out=out.ap, in_=in_.ap)
        slot.cnt += 16
        ins.then_inc(slot.sem, 16)
        in_.res.r[slot] = slot.cnt
        out.res.w[slot] = slot.cnt

    def mm(self, out, lhsT, rhs, start=True, stop=True, tp=None):
        kw = {}
        if tp is not None:
            kw["tile_position"] = tp
        self.op("pe", lambda e: e.matmul(out.ap, lhsT.ap, rhs.ap, start=start, stop=stop, **kw), [lhsT, rhs], [out])

    def tr(self, out, in_, ident):
        self.op("pe", lambda e: e.transpose(out.ap, in_.ap, ident.ap), [in_, ident], [out])

    def act(self, out, in_, func, bias=None, scale=1.0, accum=None):
        reads = [in_]
        kw = {}
        if bias is not None:
            if isinstance(bias, V):
                reads.append(bias)
                kw["bias"] = bias.ap
            else:
                kw["bias"] = bias
        if isinstance(scale, V):
            reads.append(scale)
            kw["scale"] = scale.ap
        else:
            kw["scale"] = scale
        writes = [out]
        if accum is not None:
            writes.append(accum)
            kw["accum_out"] = accum.ap
        self.op("act", lambda e: e.activation(out.ap, in_.ap, func, **kw), reads, writes)

    def tt(self, en, out, a, b, op):
        self.op(en, lambda e: e.tensor_tensor(out.ap, a.ap, b.ap, op), [a, b], [out])

    def stt(self, en, out, in0, scalar, in1, op0, op1):
        reads = [in0, in1]
        s = scalar
        if isinstance(scalar, V):
            reads.append(scalar)
            s = scalar.ap
        self.op(en, lambda e: e.scalar_tensor_tensor(out.ap, in0.ap, s, in1.ap, op0, op1), reads, [out])

    def ts(self, en, out, in0, s1, op0, s2=None, op1=None):
        reads = [in0]
        a1 = s1
        if isinstance(s1, V):
            reads.append(s1)
            a1 = s1.ap
        a2 = s2
        if isinstance(s2, V):
            reads.append(s2)
            a2 = s2.ap
        if op1 is None:
            self.op(en, lambda e: e.tensor_scalar(out.ap, in0.ap, a1, None, op0), reads, [out])
        else:
            self.op(en, lambda e: e.tensor_scalar(out.ap, in0.ap, a1, a2, op0, op1), reads, [out])

    def cp(self, en, out, in_):
        if en == "act":
            self.op("act", lambda e: e.copy(out.ap, in_.ap), [in_], [out])
        else:
            self.op(en, lambda e: e.tensor_copy(out.ap, in_.ap), [in_], [out])

    def amul(self, out, in_, m):
        self.op("act", lambda e: e.mul(out.ap, in_.ap, m.ap), [in_, m], [out])

    def recip(self, out, in_):
        self.op("dve", lambda e: e.reciprocal(out.ap, in_.ap), [in_], [out])

    def memset(self, en, out, val):
        self.op(en, lambda e: e.memset(out.ap, val), [], [out])


class StopBuild(Exception):
    pass


import contextlib


class Scope(contextlib.ExitStack):
    def __init__(self, K):
        super().__init__()
        self.K = K
        self.tiles = []

    def __exit__(self, *a):
        fr = self.K.freed
        for v in self.tiles:
            for d in (v.res.w, v.res.r):
                for p, i in d.items():
                    fr[p] = max(fr.get(p, 0), i)
        self.tiles = []
        return super().__exit__(*a)

    def close(self):
        self.__exit__(None, None, None)


def build_program(stop=None, taps=()):
    nc = bass.Bass("TRN2", target_bir_lowering=False)
    K = Ker(nc)
    K.slots = []
    K.freed = {}
    K.tapped = {}

    def checkpoint(name):
        if stop == name:
            raise StopBuild()

    def tap(name, v, shape, dt=F32):
        if name not in taps or name in K.tapped:
            return
        d = V(nc.dram_tensor("dbg_" + name, list(shape), dt, kind="ExternalOutput").ap())
        K.tapped[name] = d
        K.dma("sp", d, v, K.slot())

    def din(name, shape, dt=F32):
        return V(nc.dram_tensor(name, list(shape), dt, kind="ExternalInput").ap())

    x_d = din("x", [NT, 1024])
    wG_d = din("wG", [1024, 1664])
    wLB_d = din("wLB", [1024, 2048])
    wLA_d = din("wLA", [1024, 1536])
    woab_d = din("woab", [1024, 1024])
    wG1_d = din("wG1", [1024, 384])
    wL1_d = din("wL1", [1024, 2048])
    woc_d = din("woc", [1024, 1024])
    gcol_d = din("gcol", [128, 16])
    fn_d = din("fn", [1, 1024])
    gqk_d = din("gqk", [128, 4])
    rdec_d = din("rdec", [128, 12])
    sink_d = din("sinkl", [128, 8])
    relb_d = din("relb", [32, 16])
    ident_d = din("ident", [128, 128])
    onesblk_d = din("onesblk", [128, 128])
    mm_d = din("mmat", [128, 4, 128])
    iot_d = din("iot", [128, 4, 512])
    oh_d = din("oh", [32, 640])
    inwin_d = din("inwin", [16, 640])
    tabA_d = din("tabA", [2, 128, NT])
    tabB_d = din("tabB", [2, 128, NT])
    maskA_d = din("maskA", [128, 256])
    maskW_d = din("maskW", [128, 96])
    rfb_d = din("rfb", [128, 64])
    y_d = V(nc.dram_tensor("y", [NT, 1024], F32, kind="ExternalOutput").ap())
    x1_d = V(nc.dram_tensor("x1s", [NT, 1024], F32, kind="Internal").ap())
    mixb_d = V(nc.dram_tensor("mixbs", [4, 128, NT], BF16, kind="Internal").ap())
    vec_h = nc.dram_tensor("vecs", [16, 640], BF16, kind="Internal")
    vec_d = V(vec_h.ap())
    aident_d = din("aident", [128, 128])
    xnT0_d = V(nc.dram_tensor("xnT0s", [NCH, 128, 8, CH], BF16, kind="Internal").ap())
    xnT1_d = V(nc.dram_tensor("xnT1s", [NCH, 128, 8, CH], BF16, kind="Internal").ap())

    es = Scope(K)
    uid = [0]

    def sb(name, shape, dt=F32, stack=None):
        uid[0] += 1
        st_ = stack if stack is not None else es
        t = st_.enter_context(nc.sbuf_tensor("sb%d_%s" % (uid[0], name), list(shape), dt))
        v = V(t[:])
        v.res.w = dict(K.freed)
        st_.tiles.append(v)
        return v

    def ps(name, shape, dt=F32, stack=None):
        uid[0] += 1
        st_ = stack if stack is not None else es
        t = st_.enter_context(nc.psum_tensor("ps%d_%s" % (uid[0], name), list(shape), dt))
        v = V(t[:])
        v.res.excl = True
        v.res.w = dict(K.freed)
        st_.tiles.append(v)
        return v

    try:
        with es:
            cslot = K.slot()
            consts = []

            def cload(name, src, shape, dt=F32, q="sp"):
                t = sb(name, shape, dt)
                K.dma(q, t, src, cslot)
                consts.append(t)
                return t

            gcol = cload("gcol", gcol_d, [128, 16])
            gqk = cload("gqk", gqk_d, [128, 4])
            rdec = cload("rdec", rdec_d, [128, 12])
            sinkl = cload("sinkl", sink_d, [128, 8])
            maskA = cload("maskA", maskA_d, [128, 256])
            maskW = cload("maskW", maskW_d, [128, 96])
            rfb = cload("rfb", rfb_d, [128, 64])
            ident32 = cload("ident32", ident_d, [128, 128])
            onesblk32 = cload("onesblk32", onesblk_d, [128, 128])
            for c in consts:
                c.res.w[cslot] = cslot.cnt
            ident = sb("ident", [128, 128], BF16)
            onesblk = sb("onesblk", [128, 128], BF16)
            ones = sb("ones", [128, 128], BF16)
            epsb = sb("epsb", [128, 1])
            K.cp("dve", ident, ident32)
            K.cp("dve", onesblk, onesblk32)
            K.memset("dve", ones, 1.0)
            K.memset("dve", epsb, EPS)

            checkpoint("c0")
            wbf = sb("wbf", [128, 8, 2048], BF16)
            wobf = sb("wobf", [128, 8, 1024], BF16)
            wst = [sb("wst%d" % i, [128, 1024]) for i in range(2)]
            wst_slot = [K.slot() for _ in range(2)]
            xnTs = [sb("xnT%d" % i, [128, 8, CH], BF16) for i in range(2)]
            xnT_slot = [K.slot() for _ in range(2)]
            cur = {"xnT": xnTs[0]}
            xch_slot = [K.slot() for _ in range(4)]
            xst_slot = [K.slot() for _ in range(2)]
            EBT = sb("EBT", [128, 16, 3, 128], BF16)
            esk = sb("esk", [128, 8], F32)
            K.act(esk, sinkl, AF.Exp)
            with Scope(K) as S1:
                relb = sb("relb", [32, 16], F32, S1)
                oh = sb("oh", [32, 640], F32, S1)
                inw = sb("inw", [16, 640], F32, S1)
                e_slot = K.slot()
                K.dma("sp", relb, relb_d, e_slot)
                K.dma("sp", oh, oh_d, e_slot)
                K.dma("sp", inw, inwin_d, e_slot)
                for t_ in (relb, oh, inw):
                    t_.res.w[e_slot] = e_slot.cnt
                pv = ps("pv", [16, 1024], F32, S1)[:, 0:640]
                vec = sb("vec", [16, 640], F32, S1)
                vecb = sb("vecb", [16, 640], BF16, S1)
                K.mm(pv[:, 0:512], relb, oh[:, 0:512])
                K.mm(pv[:, 512:640], relb, oh[:, 512:640])
                K.act(vec, pv, AF.Exp)
                K.tt("dve", vecb, vec, inw, ALU.mult)
                v_slot = K.slot()
                K.dma("sp", vec_d, vecb, v_slot)
                g_slot = K.slot()
                aid32 = sb("aid32", [128, 128], F32, S1)
                K.dma("sp", aid32, aident_d, g_slot)
                aid = sb("aid", [128, 128], BF16, S1)
                K.cp("dve", aid, aid32)
                TT = sb("TT", [128, 16 * 384], BF16, S1)
                src = V(bass.AP(vec_h, 0, [[1, 128], [640, 16], [1, 384]]), vec_d.res)
                K.dma("sp", TT.re("p (h j) -> p h j", h=16), src, g_slot)
                prev = ps("prev", [128, 2, CH], F32, S1)
                EBTf = EBT.re("p h o q -> p (h o q)")
                for n_ in range(12):
                    K.mm(prev[:, n_ % 2, :], aid, TT[:, n_ * 512:(n_ + 1) * 512])
                    K.cp("act" if n_ % 2 == 0 else "dve", EBTf[:, n_ * 512:(n_ + 1) * 512], prev[:, n_ % 2, :])
            wcount = [0]

            def handoff(srcs, dsts):
                for d_ in dsts:
                    for s_ in srcs:
                        for dd in (s_.res.w, s_.res.r):
                            for p_, i_ in dd.items():
                                d_.res.w[p_] = max(d_.res.w.get(p_, 0), i_)

            def subview(parent, ap):
                v = V(ap)
                v.res.excl = parent.res.excl
                v.res.w = dict(parent.res.w)
                return v

            class MX:
                pass

            def alloc_mx(scope, full=True):
                m = MX()
                m.xch = sb("xch", [128, 4, 1024], F32, scope)
                if full:
                    m.xn = [sb("xn%d" % i, [128, 1024], BF16, scope) for i in range(2)]
                    m.junk = sb("junk", [128, 1024], BF16, scope)
                    m.ss = sb("ss", [128, 4], F32, scope)
                    m.lnv4 = sb("lnv4", [128, 4], F32, scope)
                    m.rstd4 = sb("rstd4", [128, 4], F32, scope)
                return m

            def load_x(m, src_d, C):
                for b in range(4):
                    r0 = C * CH + b * 128
                    K.dma("sp", m.xch[:, b, :], src_d[r0:r0 + 128, :], xch_slot[b])

            def store_xnT(dst_d, C, slot_i):
                K.dma("pool", dst_d[C], cur["xnT"], xst_slot[slot_i])

            def load_xnT(src_d, C):
                i = C % 2
                K.dma("sp", xnTs[i], src_d[C], xnT_slot[i])

            def use_xnT(C):
                cur["xnT"] = xnTs[C % 2]

            def load_w_piece(dst, d0, src_d, kc, c0, c1, layer_g):
                i = wcount[0] % 2
                wcount[0] += 1
                n_ = c1 - c0
                K.dma("sp", wst[i][:, 0:n_], src_d[kc * 128:(kc + 1) * 128, c0:c1], wst_slot[i])
                en = "act" if (wcount[0] % 2 == 0) else "dve"
                if layer_g is None:
                    K.cp(en, dst[:, kc, d0:d0 + n_], wst[i][:, 0:n_])
                elif en == "act":
                    K.amul(dst[:, kc, d0:d0 + n_], wst[i][:, 0:n_], gcol[:, layer_g * 8 + kc:layer_g * 8 + kc + 1])
                else:
                    K.ts("dve", dst[:, kc, d0:d0 + n_], wst[i][:, 0:n_], gcol[:, layer_g * 8 + kc:layer_g * 8 + kc + 1], ALU.mult)

            def load_w(dst, src_d, ncols, layer_g):
                for kc in range(8):
                    for c0 in range(0, ncols, 1024):
                        c1 = min(ncols, c0 + 1024)
                        load_w_piece(dst, c0, src_d, kc, c0, c1, layer_g)

            wlo = V(wbf.ap[:, :, 0:1536])
            whi = V(wbf.ap[:, :, 1536:2048])
            wmode = {"split": False, "base": 0}

            def wcol(kc, c0, n_=128):
                c0 = c0 + wmode["base"]
                if not wmode["split"]:
                    return wbf[:, kc, c0:c0 + n_]
                if c0 + n_ <= 1536:
                    return wlo[:, kc, c0:c0 + n_]
                return whi[:, kc, c0 - 1536:c0 - 1536 + n_]

            def xnT_front(m, src_d, C):
                load_x(m, src_d, C)
                for b in range(4):
                    K.act(m.junk, m.xch[:, b, :], AF.Square, accum=m.ss[:, b:b + 1])
                K.act(m.lnv4, m.ss, AF.Ln, bias=epsb[:, 0:1], scale=1.0 / 1024.0)
                K.act(m.rstd4, m.lnv4, AF.Exp, scale=-0.5)

            def xnT_back(m, C, pTl):
                xnT = xnTs[C % 2]
                for b in range(4):
                    xb = m.xn[b % 2]
                    pT_ = pTl[b % len(pTl)]
                    K.ts("dve", xb, m.xch[:, b, :], m.rstd4[:, b:b + 1], ALU.mult)
                    for kc in range(8):
                        K.tr(pT_[:, kc, :], xb[:, kc * 128:(kc + 1) * 128], ident)
                    K.cp("act" if b % 2 == 0 else "dve", xnT[:, :, b * 128:(b + 1) * 128], pT_)

            def make_xnT(m, src_d, C, pT):
                use_xnT(C)
                xnT_front(m, src_d, C)
                xnT_back(m, C, pT if isinstance(pT, list) else [pT])

            def proj(dst, c0):
                for kc in range(8):
                    K.mm(dst, wcol(kc, c0), cur["xnT"][:, kc, :], start=(kc == 0), stop=(kc == 7))

            def rsq_bcast(dst, src_ps, nfeat, sq, psn, lnv, lhs_ones):
                K.act(sq, src_ps, AF.Square)
                K.mm(psn, lhs_ones, sq)
                K.act(lnv, psn, AF.Ln, bias=epsb[:, 0:1], scale=1.0 / nfeat)
                K.act(dst, lnv, AF.Exp, scale=-0.5)

            with Scope(K) as L0:
                LR = Scope(K)
                KaT = sb("KaT", [128, 2, NT], BF16, L0)
                Va = sb("Va", [128, NB, 128], BF16, L0)
                tabc = sb("tabc", [128, 2, CH], F32, L0)
                tab_slot = K.slot()
                tabd_slot = K.slot()
                sq = sb("sq", [128, CH], BF16, L0)
                lnv = sb("lnv", [128, CH], F32, L0)
                rs = sb("rs", [128, CH], F32, L0)
                t1 = sb("t1", [128, CH], F32, L0)
                t2 = sb("t2", [128, CH], F32, L0)
                tabg = sb("tabg", [128, 2, CH], F32, L0)
                SbAll = sb("SbAll", [128, 2, NB, 128], BF16, LR)
                tabd = sb("tabd", [128, 2, CH], F32, LR)
                vbtm = sb("vbtm", [128, 4, 512], BF16, LR)
                lg = sb("lg", [128, 12], F32, LR)
                K.act(lg, rdec, AF.Exp)
                K.ts("dve", lg, lg, -1.0, ALU.mult)
                cd = sb("cd", [128, 4], F32, LR)
                K.act(cd, lg[:, 0:4], AF.Exp, scale=128.0)
                cdr = sb("cdr", [128, 4, NB], F32, LR)
                for j in range(4):
                    off = 0 if j < 2 else 32
                    K.ts("dve", cdr[:, j, :], rfb[:, off:off + 32], cd[:, j:j + 1], ALU.mult)
                checkpoint("c1")
                QF4 = sb("QF4", [128, 2, CH], F32, LR)
                QB4 = sb("QB4", [128, 2, CH], F32, LR)
                KF4 = sb("KF4", [128, 2, CH], F32, LR)
                KB4 = sb("KB4", [128, 2, CH], F32, LR)
                DT = sb("DT", [128, 4, 128], F32, LR)
                with Scope(K) as S0:
                    iot = sb("iot", [128, 4, CH], F32, S0)
                    K.dma("sp", iot, iot_d, tabd_slot)
                    for p in range(2):
                        K.act(QF4[:, p, :], iot[:, 0, :], AF.Exp, scale=lg[:, p:p + 1])
                        K.act(QB4[:, p, :], iot[:, 1, :], AF.Exp, scale=lg[:, 2 + p:3 + p])
                        K.act(KF4[:, p, :], iot[:, 2, :], AF.Exp, scale=lg[:, p:p + 1])
                        K.act(KB4[:, p, :], iot[:, 3, :], AF.Exp, scale=lg[:, 2 + p:3 + p])
                    K.ts("dve", KF4, KF4, 0.125, ALU.mult)
                    K.ts("dve", KB4, KB4, 0.125, ALU.mult)
                    mmat = sb("mmat", [128, 4, 128], F32, S0)
                    K.dma("sp", mmat, mm_d, tab_slot)
                    d1 = sb("d1", [128, 128], F32, S0)
                    d2 = sb("d2", [128, 128], F32, S0)
                    for h in range(4):
                        checkpoint("d0")
                        K.act(d1, mmat[:, 0, :], AF.Exp, scale=lg[:, 4 + h:5 + h])
                        checkpoint("d1")
                        K.tt("dve", d1, d1, mmat[:, 1, :], ALU.mult)
                        checkpoint("d2")
                        K.act(d2, mmat[:, 2, :], AF.Exp, scale=lg[:, 8 + h:9 + h])
                        K.tt("dve", d2, d2, mmat[:, 3, :], ALU.mult)
                        K.tt("dve", d1, d1, d2, ALU.add)
                        checkpoint("d3")
                        K.ts("dve", DT[:, h, :], d1, 0.125, ALU.mult)
                        checkpoint("d4")

                def load_tab(dst, slot, src_d, C):
                    K.dma("sp", dst, V(src_d.ap[:, :, C * CH:(C + 1) * CH].rearrange("t p c -> p t c"), src_d.res), slot)

                def rope(psa, psb, tab, out32, ga=None, gb=None):
                    if ga is None:
                        K.tt("dve", t1, psa, tab[:, 0, :], ALU.mult)
                        K.tt("dve", t2, psb, tab[:, 1, :], ALU.mult)
                    else:
                        K.amul(tabg[:, 0, :], tab[:, 0, :], ga)
                        K.amul(tabg[:, 1, :], tab[:, 1, :], gb)
                        K.tt("dve", t1, psa, tabg[:, 0, :], ALU.mult)
                        K.tt("dve", t2, psb, tabg[:, 1, :], ALU.mult)
                    K.tt("pool", out32, t1, t2, ALU.add)

                checkpoint("setup0")
                load_w(wbf, wG_d, 1664, 0)
                with Scope(K) as PG:
                    pT = ps("pT", [128, 8, 128], BF16, PG)
                    pk = pT.re("p (c t) q -> p c t q", t=2)
                    pbig = ps("pbigG", [128, 7, CH], F32, PG)
                    bk = [subview(pbig, pbig.ap[:, i, :]) for i in range(7)]
                    pn = bk[4]
                    pT2g = V(bk[5].ap.bitcast(BF16).rearrange("p (k q) -> p k q", k=8), bk[5].res)
                    pkv = V(bk[6].ap[:, 0:256].rearrange("p (a b) -> p a b", a=2), bk[6].res)
                    kdbT = sb("kdbT", [128, 2, CH], BF16, PG)
                    kdbtm = sb("kdbtm", [128, 4, 2, 128], BF16, PG)
                    Rb = sb("Rb", [128, 2, 128], F32, PG)
                    mxg = alloc_mx(PG)
                    WS = [dict(sq=sq, lnv=lnv, rs=rs, t1=t1, t2=t2),
                          dict(sq=sb("wsq", [128, CH], BF16, PG), lnv=sb("wlnv", [128, CH], F32, PG),
                               rs=sb("wrs", [128, CH], F32, PG), t1=sb("wt1", [128, CH], F32, PG),
                               t2=sb("wt2", [128, CH], F32, PG))]
                    K.memset("dve", Rb, 0.0)
                    for C in range(NCH - 1, -1, -1):
                        make_xnT(mxg, x_d, C, [pT, pT2g])
                        store_xnT(xnT0_d, C, C % 2)
                        load_w_piece(wobf, 0, woab_d, C, 0, 1024, None)
                        load_tab(tabc, tab_slot, tabA_d, C)
                        load_tab(tabd, tabd_slot, tabB_d, C)
                        K.amul(tabg[:, 0, :], tabc[:, 0, :], gqk[:, 2:3])
                        K.amul(tabg[:, 1, :], tabc[:, 1, :], gqk[:, 3:4])
                        for t in range(2):
                            w_ = WS[t % 2]
                            pa_, pb_ = bk[2 * t], bk[2 * t + 1]
                            proj(pa_, t * 128)
                            proj(pb_, 256 + t * 128)
                            rsq_bcast(w_["rs"], pa_, 64.0, w_["sq"], pn, w_["lnv"], onesblk)
                            K.tt("dve", w_["t1"], pa_, tabg[:, 0, :], ALU.mult)
                            K.tt("dve", w_["t2"], pb_, tabg[:, 1, :], ALU.mult)
                            K.tt("pool", w_["t1"], w_["t1"], w_["t2"], ALU.add)
                            K.tt("pool", KaT[:, t, C * CH:(C + 1) * CH], w_["t1"], w_["rs"], ALU.mult)
                        for t in range(2):
                            w_ = WS[t % 2]
                            pa_, pb_ = bk[2 * t], bk[2 * t + 1]
                            proj(pa_, 512 + t * 128)
                            proj(pb_, 768 + t * 128)
                            K.tt("dve", w_["t1"], pa_, tabd[:, 0, :], ALU.mult)
                            K.tt("dve", w_["t2"], pb_, tabd[:, 1, :], ALU.mult)
                            K.tt("pool", w_["t1"], w_["t1"], w_["t2"], ALU.add)
                            K.tt("pool", kdbT[:, t, :], w_["t1"], KB4[:, t, :], ALU.mult)
                        for b in range(4):
                            pva = (bk[4] if b % 2 == 0 else bk[2])[:, 0:128]
                            pvb = bk[5] if b % 2 == 0 else bk[3]
                            for kc in range(8):
                                K.mm(pva, cur["xnT"][:, kc, b * 128:(b + 1) * 128], wbf[:, kc, 1024:1152], start=(kc == 0), stop=(kc == 7))
                            for kc in range(8):
                                K.mm(pvb, cur["xnT"][:, kc, b * 128:(b + 1) * 128], wbf[:, kc, 1152:1664], start=(kc == 0), stop=(kc == 7))
                            K.cp("act", Va[:, C * 4 + b, :], pva)
                            K.cp("dve", vbtm[:, b, :], pvb)
                        for t in range(2):
                            for cj in range(4):
                                K.tr(pk[:, cj, t, :], kdbT[:, t, cj * 128:(cj + 1) * 128], ident)
                        K.cp("act", kdbtm, pk)
                        for cj in range(3, -1, -1):
                            n = C * 4 + cj
                            for p in range(2):
                                K.mm(pkv[0:64, p, :], kdbtm[:, cj, p, 0:64], vbtm[:, cj, (2 * p) * 128:(2 * p + 1) * 128])
                                K.mm(pkv[64:128, p, :], kdbtm[:, cj, p, 64:128], vbtm[:, cj, (2 * p + 1) * 128:(2 * p + 2) * 128], tp=(0, 64))
                            K.ts("dve", SbAll[:, :, n, :], Rb, rfb[:, 32 + n:33 + n], ALU.mult)
                            for p in range(2):
                                K.ts("dve", Rb[:, p, :], Rb[:, p, :], cdr[:, 2 + p, n:n + 1], ALU.mult)
                                K.tt("dve", Rb[:, p, :], pkv[:, p, :], Rb[:, p, :], ALU.add)

                tap("KaT", KaT, [128, 2, NT], BF16)
                tap("Va", Va, [128, NB, 128], BF16)
                tap("SbAll", SbAll, [128, 2, NB, 128], BF16)
                checkpoint("G")
                load_w(wbf, wLB_d, 2048, 0)
                with Scope(K) as PB:
                    pT = ps("pT", [128, 8, 128], BF16, PB)
                    pk = pT.re("p (c t) q -> p c t q", t=2)
                    pbig = ps("pbigB", [128, 7, CH], F32, PB)
                    bk = [subview(pbig, pbig.ap[:, i, :]) for i in range(7)]
                    pa, pb, pss = bk[0], bk[1], bk[2]
                    po = subview(pbig, pbig.ap[:, 3:7, :])
                    qrT = sb("qrT", [128, 2, CH], BF16, PB)
                    qdf = sb("qdf", [128, 2, CH], BF16, PB)
                    qdb = sb("qdb", [128, 2, CH], BF16, PB)
                    krT = sb("krT", [128, 2, CH], BF16, PB)
                    kdfT = sb("kdfT", [128, 2, CH], BF16, PB)
                    kdftm = sb("kdftm", [128, 4, 2, 128], BF16, PB)
                    sg = sb("sg", [128, 4, CH], BF16, PB)
                    ATs = [sb("AT%d" % i, [128, 4, 128], BF16, PB) for i in range(2)]
                    Sfs = [sb("Sf%d" % i, [128, 2, 128], BF16, PB) for i in range(2)]
                    Rf = sb("Rf", [128, 2, 128], F32, PB)
                    mixBc = [sb("mixBc%d" % i, [128, 4, CH], BF16, PB) for i in range(1)]
                    mixB_slot = [K.slot() for _ in range(1)]
                    WS = [dict(sq=sq, lnv=lnv, rs=rs, t1=t1, t2=t2),
                          dict(sq=sb("wsq", [128, CH], BF16, PB), lnv=sb("wlnv", [128, CH], F32, PB),
                               rs=sb("wrs", [128, CH], F32, PB), t1=sb("wt1", [128, CH], F32, PB),
                               t2=sb("wt2", [128, CH], F32, PB))]
                    K.memset("dve", Rf, 0.0)
                    load_xnT(xnT0_d, 0)
                    pairs = [(bk[0], bk[1]), (bk[3], bk[4]), (bk[5], bk[6])]
                    for C in range(NCH):
                        use_xnT(C)
                        if C + 1 < NCH:
                            load_xnT(xnT0_d, C + 1)
                        load_tab(tabd, tabd_slot, tabB_d, C)
                        handoff([po], bk[3:7])
                        ip = 0
                        for t in range(2):
                            w_ = WS[ip % 2]
                            pa_, pb_ = pairs[ip % 3]
                            ip += 1
                            proj(pa_, t * 128)
                            proj(pb_, 256 + t * 128)
                            K.tt("dve", w_["t1"], pa_, tabd[:, 0, :], ALU.mult)
                            K.tt("dve", w_["t2"], pb_, tabd[:, 1, :], ALU.mult)
                            K.tt("pool", w_["t1"], w_["t1"], w_["t2"], ALU.add)
                            K.cp("act", qrT[:, t, :], w_["t1"])
                            K.tt("pool", qdf[:, t, :], w_["t1"], QF4[:, t, :], ALU.mult)
                            K.tt("pool", qdb[:, t, :], w_["t1"], QB4[:, t, :], ALU.mult)
                        for t in range(2):
                            w_ = WS[ip % 2]
                            pa_, pb_ = pairs[ip % 3]
                            ip += 1
                            proj(pa_, 512 + t * 128)
                            proj(pb_, 768 + t * 128)
                            K.tt("dve", w_["t1"], pa_, tabd[:, 0, :], ALU.mult)
                            K.tt("dve", w_["t2"], pb_, tabd[:, 1, :], ALU.mult)
                            K.tt("pool", w_["t1"], w_["t1"], w_["t2"], ALU.add)
                            K.cp("act", krT[:, t, :], w_["t1"])
                            K.tt("pool", kdfT[:, t, :], w_["t1"], KF4[:, t, :], ALU.mult)
                        for h in range(4):
                            pa_ = bk[3 + h]
                            proj(pa_, 1024 + h * 128)
                            K.act(sg[:, h, :], pa_, AF.Silu)
                        for b in range(4):
                            pv_ = bk[1 + b % 2]
                            for kc in range(8):
                                K.mm(pv_, cur["xnT"][:, kc, b * 128:(b + 1) * 128], wbf[:, kc, 1536:2048], start=(kc == 0), stop=(kc == 7))
                            K.cp("dve", vbtm[:, b, :], pv_)
                        for t in range(2):
                            for cj in range(4):
                                K.tr(pk[:, cj, t, :], kdfT[:, t, cj * 128:(cj + 1) * 128], ident)
                        K.cp("act", kdftm, pk)
                        handoff(bk[3:7], [po])
                        for cj in range(4):
                            n = C * 4 + cj
                            cs = slice(cj * 128, (cj + 1) * 128)
                            Sf = Sfs[cj % 2]
                            AT = ATs[cj % 2]
                            K.ts("dve", Sf, Rf, rfb[:, n:n + 1], ALU.mult)
                            for p in range(2):
                                K.mm(pa[0:64, p * 128:(p + 1) * 128], kdftm[:, cj, p, 0:64], vbtm[:, cj, (2 * p) * 128:(2 * p + 1) * 128])
                                K.mm(pa[64:128, p * 128:(p + 1) * 128], kdftm[:, cj, p, 64:128], vbtm[:, cj, (2 * p + 1) * 128:(2 * p + 2) * 128], tp=(0, 64))
                            for p in range(2):
                                K.ts("dve", Rf[:, p, :], Rf[:, p, :], cdr[:, p, n:n + 1], ALU.mult)
                                K.tt("dve", Rf[:, p, :], pa[:, p * 128:(p + 1) * 128], Rf[:, p, :], ALU.add)
                            for h in range(4):
                                t, r0 = h // 2, (h % 2) * 64
                                pdst = pss if (h % 2 == 0) else pb
                                K.mm(pdst[:, t * 128:(t + 1) * 128], krT[r0:r0 + 64, t, cs], qrT[r0:r0 + 64, t, cs])
                            ATv = AT.re("p (t hp) i -> p hp t i", hp=2)
                            DTv = DT.re("p (t hp) i -> p hp t i", hp=2)
                            K.tt("dve", ATv[:, 0, :, :], pss[:, 0:256].re("p (t i) -> p t i", t=2), DTv[:, 0, :, :], ALU.mult)
                            K.tt("dve", ATv[:, 1, :, :], pb[:, 0:256].re("p (t i) -> p t i", t=2), DTv[:, 1, :, :], ALU.mult)
                            for h in range(4):
                                t, r0 = h // 2, (h % 2) * 64
                                K.mm(po[:, h, cs], vbtm[:, cj, h * 128:(h + 1) * 128], AT[:, h, :], start=True, stop=False)
                                K.mm(po[:, h, cs], Sf[r0:r0 + 64, t, :], qdf[r0:r0 + 64, t, cs], start=False, stop=False)
                                K.mm(po[:, h, cs], SbAll[r0:r0 + 64, t, n, :], qdb[r0:r0 + 64, t, cs], start=False, stop=True)
                        mb = mixBc[0]
                        for h0 in (0, 2):
                            hs = (h0, h0 + 1)
                            for h in hs:
                                K.act(WS[h % 2]["sq"], po[:, h, :], AF.Square)
                            for h in hs:
                                K.mm(bk[h % 2], ones, WS[h % 2]["sq"])
                            for h in hs:
                                K.act(WS[h % 2]["lnv"], bk[h % 2], AF.Ln, bias=epsb[:, 0:1], scale=1.0 / 128.0)
                            for h in hs:
                                K.act(WS[h % 2]["rs"], WS[h % 2]["lnv"], AF.Exp, scale=-0.5)
                            for h in hs:
                                K.tt("dve", WS[h % 2]["t1"], po[:, h, :], WS[h % 2]["rs"], ALU.mult)
                                K.tt("pool", mb[:, h, :], WS[h % 2]["t1"], sg[:, h, :], ALU.mult)
                        K.dma("pool", V(mixb_d.ap[:, :, C * CH:(C + 1) * CH].rearrange("h p c -> p h c"), mixb_d.res), mb, mixB_slot[0])

                tap("mixb", mixb_d, [4, 128, NT], BF16)
                checkpoint("LB")
                LR.close()
                load_w(wbf, wLA_d, 1536, 0)
                handoff([wbf], [wlo, whi])
                wmode["split"] = True
                with Scope(K) as PA:
                    pbig = ps("pbig", [128, 8, CH], F32, PA)
                    psc = [subview(pbig, pbig.ap[:, 2 * i:2 * i + 2, :]) for i in range(3)]
                    pnum = subview(pbig, pbig.ap[:, 6, :])
                    pden = subview(pbig, pbig.ap[:, 7, :])
                    bk = [subview(pbig, pbig.ap[:, i, :]) for i in range(6)] + [pnum, pden]
                    WS = [dict(sq=sq, lnv=lnv, rs=rs, t1=t1, t2=t2),
                          dict(sq=sb("wsq", [128, CH], BF16, PA), lnv=sb("wlnv", [128, CH], F32, PA),
                               rs=sb("wrs", [128, CH], F32, PA), t1=sb("wt1", [128, CH], F32, PA),
                               t2=sb("wt2", [128, CH], F32, PA))]
                    qaT = sb("qaT", [128, 4, CH], BF16, PA)
                    sga = sb("sga", [128, 4, CH], BF16, PA)
                    mixAs = [sb("mixA%d" % i, [128, 4, CH], BF16, PA) for i in range(2)]
                    mixBls = [sb("mixBl%d" % i, [128, 4, CH], BF16, PA) for i in range(2)]
                    mixBl_slots = [K.slot() for _ in range(2)]
                    xblk = [sb("xblk%d" % i, [128, 1024], F32, PA) for i in range(2)]
                    xblk_slot = [K.slot() for _ in range(2)]
                    nxb = [0]
                    pTs = [sb("pTs%d" % i, [128, 2, CH], BF16, PA) for i in range(3)]
                    x1b = [sb("x1b%d" % i, [128, 1024], F32, PA) for i in range(2)]
                    dcp = sb("dcp", [128, CH], F32, PA)
                    ncp = sb("ncp", [128, CH], F32, PA)
                    x1b_slot = [K.slot() for _ in range(2)]
                    qaTs = [qaT, sb("qaT1", [128, 4, CH], BF16, PA)]
                    sgas = [sga, sb("sga1", [128, 4, CH], BF16, PA)]
                    tabcs = [tabc, sb("tabc1", [128, 2, CH], F32, PA)]
                    tabgs = [tabg, sb("tabg1", [128, 2, CH], F32, PA)]
                    tabsl = [tab_slot, K.slot()]
                    nbuf = [0]
                    npt = [0]

                    held = set()

                    def take_buf():
                        while True:
                            i_ = nbuf[0] % 3
                            nbuf[0] += 1
                            if i_ not in held:
                                return psc[i_]

                    def hold(b_):
                        held.add(psc.index(b_))

                    def release(b_):
                        held.discard(psc.index(b_))

                    def projx(dst, c0, xT):
                        for kc in range(8):
                            K.mm(dst, wcol(kc, c0), xT[:, kc, :], start=(kc == 0), stop=(kc == 7))

                    def proj_items(Cn):
                        q_, g_ = qaTs[Cn % 2], sgas[Cn % 2]
                        tc_, tg_ = tabcs[Cn % 2], tabgs[Cn % 2]
                        xT = xnTs[Cn % 2]

                        def prep():
                            load_tab(tc_, tabsl[Cn % 2], tabA_d, Cn)
                            K.amul(tg_[:, 0, :], tc_[:, 0, :], gqk[:, 0:1])
                            K.amul(tg_[:, 1, :], tc_[:, 1, :], gqk[:, 1:2])

                        items = []
                        for t in range(4):
                            def mk(t=t):
                                st = {}
                                w_ = WS[t % 2]

                                def s1():
                                    st["buf"] = take_buf()
                                    hold(st["buf"])
                                    projx(st["buf"][:, 0, :], t * 128, xT)
                                    projx(st["buf"][:, 1, :], 512 + t * 128, xT)

                                def s2():
                                    pa_, pb_ = st["buf"][:, 0, :], st["buf"][:, 1, :]
                                    K.tt("dve", w_["t1"], pa_, tg_[:, 0, :], ALU.mult)
                                    K.tt("dve", w_["t2"], pb_, tg_[:, 1, :], ALU.mult)
                                    K.act(w_["sq"], pa_, AF.Square)

                                def s3():
                                    K.mm(st["buf"][:, 1, :], onesblk, w_["sq"])

                                def s4():
                                    K.act(w_["lnv"], st["buf"][:, 1, :], AF.Ln, bias=epsb[:, 0:1], scale=1.0 / 64.0)
                                    K.act(w_["rs"], w_["lnv"], AF.Exp, scale=-0.5)
                                    K.tt("pool", w_["t1"], w_["t1"], w_["t2"], ALU.add)
                                    K.tt("pool", q_[:, t, :], w_["t1"], w_["rs"], ALU.mult)
                                    release(st["buf"])
                                return [(s1, 3), (s2, 1), (s3, 2), (s4, 0)]
                            items.append(mk())
                        for t2_ in range(2):
                            def mk(t2_=t2_):
                                st = {}

                                def s1():
                                    st["buf"] = take_buf()
                                    hold(st["buf"])
                                    for j in range(2):
                                        projx(st["buf"][:, j, :], 1024 + (2 * t2_ + j) * 128, xT)

                                def s2():
                                    for j in range(2):
                                        K.act(WS[j]["t1"], st["buf"][:, j, :], AF.Tanh, scale=0.5)

                                def s3():
                                    for j in range(2):
                                        K.ts("dve", WS[j]["t1"], WS[j]["t1"], 0.5, ALU.mult, 0.5, ALU.add)
                                        K.tt("dve", g_[:, 2 * t2_ + j, :], st["buf"][:, j, :], WS[j]["t1"], ALU.mult)
                                    release(st["buf"])
                                return [(s1, 3), (s2, 1), (s3, 0)]
                            items.append(mk())
                        return prep, items

                    def outproj_items(Cc):
                        mA, mB = mixAs[Cc % 2], mixBls[Cc % 2]
                        items = []
                        for b in range(4):
                            def mk(b=b):
                                st = {}
                                bs = slice(b * 128, (b + 1) * 128)
                                r0 = Cc * CH + b * 128

                                def s1():
                                    i_ = nxb[0] % 2
                                    nxb[0] += 1
                                    st["i"] = i_
                                    K.dma("sp", xblk[i_], x_d[r0:r0 + 128, :], xblk_slot[i_])
                                    st["buf"] = take_buf()
                                    hold(st["buf"])
                                    py = st["buf"]
                                    for half in range(2):
                                        for f in range(8):
                                            src = mA[:, f, bs] if f < 4 else mB[:, f - 4, bs]
                                            K.mm(py[:, half, :], src, wobf[:, f, half * 512:(half + 1) * 512], start=(f == 0), stop=(f == 7))

                                def s2():
                                    xo = x1b[st["i"]]
                                    K.tt("dve", xo, st["buf"].re("p a c -> p (a c)"), xblk[st["i"]], ALU.add)
                                    K.dma("pool", x1_d[r0:r0 + 128, :], xo, x1b_slot[st["i"]])
                                    release(st["buf"])
                                return [(s1, 4), (s2, 0)]
                            items.append(mk())
                        return items

                    load_xnT(xnT0_d, 0)
                    prep0, items0 = proj_items(0)
                    prep0()
                    for it_ in items0:
                        for st_fn, _d in it_:
                            st_fn()
                    carry = []
                    for C in range(NCH):
                        use_xnT(C)
                        qaT_c, sga_c = qaTs[C % 2], sgas[C % 2]
                        mixA = mixAs[C % 2]
                        load_w_piece(whi, 0, wG1_d, C, 0, 384, 1)
                        pending = list(carry)
                        carry = []
                        if C + 1 < NCH:
                            load_xnT(xnT0_d, C + 1)
                            prepn, pitems = proj_items(C + 1)
                            prepn()
                            pending = pending + pitems
                        K.dma("pool", mixBls[C % 2], V(mixb_d.ap[:, :, C * CH:(C + 1) * CH].rearrange("h p c -> p h c"), mixb_d.res), mixBl_slots[C % 2])
                        nit = 0
                        active = [None]
                        for t in range(4):
                            kv = t // 2
                            fifo = []

                            def qk(kb_):
                                sc_ = take_buf()
                                fifo.append(sc_)
                                ks = slice(kb_ * 128, (kb_ + 1) * 128)
                                K.mm(sc_[:, 0, :], KaT[0:64, kv, ks], qaT_c[0:64, t, :])
                                K.mm(sc_[:, 1, :], KaT[64:128, kv, ks], qaT_c[64:128, t, :])

                            qk(0)
                            qk(1)
                            for kb in range(NB):
                                sc = fifo.pop(0)
                                pt = pTs[npt[0] % 3]
                                npt[0] += 1
                                nit += 1
                                K.act(pt, sc, AF.Exp, bias=maskA[:, C * NB + kb:C * NB + kb + 1], scale=0.125)
                                if kb + 2 < NB:
                                    qk(kb + 2)
                                if active[0] is None and pending and nit % 12 == 3:
                                    active[0] = [pending.pop(0), 0, nit]
                                if active[0] is not None and nit >= active[0][2]:
                                    stages_, si_, _due = active[0]
                                    fn_, delay_ = stages_[si_]
                                    fn_()
                                    if si_ + 1 < len(stages_):
                                        active[0] = [stages_, si_ + 1, nit + delay_]
                                    else:
                                        active[0] = None
                                st, sp_ = (kb == 0), (kb == NB - 1)
                                K.mm(pnum[0:64, :], Va[:, kb, kv * 64:(kv + 1) * 64], pt[:, 0, :], start=st, stop=sp_)
                                K.mm(pnum[64:128, :], Va[:, kb, kv * 64:(kv + 1) * 64], pt[:, 1, :], start=st, stop=sp_, tp=(0, 64))
                                K.mm(pden[0:64, :], ones[:, 0:64], pt[:, 0, :], start=st, stop=sp_)
                                K.mm(pden[64:128, :], ones[:, 0:64], pt[:, 1, :], start=st, stop=sp_, tp=(0, 64))
                            K.cp("dve", dcp, pden)
                            K.cp("dve", ncp, pnum)
                            K.recip(dcp, dcp)
                            K.tt("dve", ncp, ncp, dcp, ALU.mult)
                            K.tt("pool", mixA[:, t, :], ncp, sga_c[:, t, :], ALU.mult)
                        while active[0] is not None or pending:
                            if active[0] is None:
                                active[0] = [pending.pop(0), 0, 0]
                            stages_, si_, _due = active[0]
                            stages_[si_][0]()
                            active[0] = [stages_, si_ + 1, 0] if si_ + 1 < len(stages_) else None
                        carry = outproj_items(C)
                        if C == NCH - 1:
                            for it_ in carry:
                                for st_fn, _d in it_:
                                    st_fn()
                            carry = []

            tap("x1", x1_d, [NT, 1024], F32)
            checkpoint("LA")
            with Scope(K) as L1:
                KcT = sb("KcT", [128, 2, (NB + 2) * 128], BF16, L1)
                Vc = sb("Vc", [128, NB + 2, 128], BF16, L1)
                K.memset("pool", KcT[:, :, 0:128], 0.0)
                K.memset("pool", KcT[:, :, (NB + 1) * 128:(NB + 2) * 128], 0.0)
                K.memset("pool", Vc[:, 0, :], 0.0)
                K.memset("pool", Vc[:, NB + 1, :], 0.0)
                tap("EBT", EBT, [128, 16, 3, 128], BF16)
                checkpoint("EBT")
                wmode["base"] = 1536
                with Scope(K) as PG1:
                    pT = ps("pT", [128, 8, 128], BF16, PG1)
                    pa = ps("pa", [128, CH], F32, PG1)
                    pva = ps("pva", [128, 512], F32, PG1)[:, 0:128]
                    mxs = [alloc_mx(PG1), alloc_mx(PG1)]
                    pT2 = ps("pT2", [128, 8, 128], BF16, PG1)
                    pa2 = ps("pa2", [128, CH], F32, PG1)
                    pva2 = ps("pva2", [128, 512], F32, PG1)[:, 0:128]
                    xnT_front(mxs[0], x1_d, 0)
                    xnT_back(mxs[0], 0, [pT, pT2])
                    for C in range(NCH):
                        use_xnT(C)
                        store_xnT(xnT1_d, C, C % 2)
                        if C + 1 < NCH:
                            xnT_front(mxs[(C + 1) % 2], x1_d, C + 1)
                        load_w_piece(wlo, 0, wL1_d, C, 0, 1024, 1)
                        load_w_piece(wlo, 1024, wL1_d, C, 1024, 1536, 1)
                        load_w_piece(wobf, 0, woc_d, C, 0, 1024, None)
                        for t in range(2):
                            pa_ = pa if t == 0 else pa2
                            proj(pa_, t * 128)
                            K.cp("act" if t == 0 else "dve", KcT[:, t, (C * 4 + 1) * 128:(C * 4 + 5) * 128], pa_)
                        for b in range(4):
                            pv_ = pva if b % 2 == 0 else pva2
                            for kc in range(8):
                                K.mm(pv_, cur["xnT"][:, kc, b * 128:(b + 1) * 128], wcol(kc, 256), start=(kc == 0), stop=(kc == 7))
                            K.cp("dve" if b % 2 == 0 else "act", Vc[:, C * 4 + b + 1, :], pv_)
                        if C + 1 < NCH:
                            xnT_back(mxs[(C + 1) % 2], C + 1, [pT, pT2])

                tap("KcT", KcT, [128, 2, (NB + 2) * 128], BF16)
                tap("Vc", Vc, [128, NB + 2, 128], BF16)
                checkpoint("G1")
                wmode["base"] = 0
                for kc_ in range(8):
                    load_w_piece(whi, 0, wL1_d, kc_, 1536, 2048, 1)
                with Scope(K) as PL1:
                    pbig = ps("pbig1", [128, 8, CH], F32, PL1)
                    pw = [subview(pbig, pbig.ap[:, 2 * i:2 * i + 2, :]) for i in range(3)]
                    pnum = subview(pbig, pbig.ap[:, 6, :])
                    pden = subview(pbig, pbig.ap[:, 7, :])
                    bk = [subview(pbig, pbig.ap[:, i, :]) for i in range(6)]
                    qcT = sb("qcT", [128, 8, CH], BF16, PL1)
                    sgc = sb("sgc", [128, 8, CH], BF16, PL1)
                    mixC = sb("mixC", [128, 8, CH], BF16, PL1)
                    pws = [sb("pws%d" % i, [128, 2, 3, 128], BF16, PL1) for i in range(3)]
                    pw2 = [sb("pw2%d" % i, [128, 2, 3, 128], BF16, PL1) for i in range(3)]
                    rs = sb("rs1", [128, CH], F32, PL1)
                    lnr = sb("lnr", [128, CH], F32, PL1)
                    t1 = sb("t11", [128, CH], F32, PL1)
                    EP = [dict(rs=rs, t1=t1, lnr=lnr),
                          dict(rs=sb("rs1b", [128, CH], F32, PL1), t1=sb("t11b", [128, CH], F32, PL1), lnr=sb("lnrb", [128, CH], F32, PL1))]
                    pnums = [pnum, pnum]
                    pdens = [pden, pden]
                    x2 = [sb("x2%d" % i, [128, 1024], F32, PL1) for i in range(2)]
                    yo = [sb("yo%d" % i, [128, 1024], F32, PL1) for i in range(2)]
                    yo_slot = [K.slot() for _ in range(2)]
                    ss2 = sb("ss2", [128, 2], F32, PL1)
                    ln2 = sb("ln2", [128, 2], F32, PL1)
                    r2 = sb("r2", [128, 2], F32, PL1)
                    it = 0
                    mxl = alloc_mx(PL1, full=False)
                    xch = mxl.xch
                    junk = sb("junk1", [128, 1024], BF16, PL1)
                    fnbc = sb("fnbc", [128, 1024], F32, PL1)
                    fn_slot = K.slot()
                    K.dma("sp", fnbc, V(fn_d.ap.to_broadcast([128, 1024]), fn_d.res), fn_slot)
                    load_xnT(xnT1_d, 0)
                    for C in range(NCH):
                        use_xnT(C)
                        if C + 1 < NCH:
                            load_xnT(xnT1_d, C + 1)
                        load_x(mxl, x1_d, C)
                        handoff(pw, bk[0:6])
                        for t in range(8):
                            pa_ = bk[t % 6]
                            proj(pa_, t * 128)
                            K.cp("act" if t % 2 == 0 else "dve", qcT[:, t, :], pa_)
                        for t in range(8):
                            pa_ = bk[(t + 2) % 6]
                            proj(pa_, 1024 + t * 128)
                            K.act(sgc[:, t, :], pa_, AF.Silu)
                        handoff(bk[0:6], pw)
                        items = [(t, qi) for t in range(8) for qi in range(4)]

                        def wqk(t, qi, w):
                            kv = t // 4
                            i = C * 4 + qi
                            qs = slice(qi * 128, (qi + 1) * 128)
                            for o in range(3):
                                sl = 2 - o
                                ks = slice((i + o) * 128, (i + o + 1) * 128)
                                K.mm(w[:, 0, sl * 128:(sl + 1) * 128], KcT[0:64, kv, ks], qcT[0:64, t, qs])
                                K.mm(w[:, 1, sl * 128:(sl + 1) * 128], KcT[64:128, kv, ks], qcT[64:128, t, qs])

                        wqk(items[0][0], items[0][1], pw[it % 3])
                        wqk(items[1][0], items[1][1], pw[(it + 1) % 3])
                        deferred = []
                        for idx, (t, qi) in enumerate(items):
                            kv = t // 4
                            i = C * 4 + qi
                            qs = slice(qi * 128, (qi + 1) * 128)
                            w = pw[it % 3]
                            s1 = pws[it % 3]
                            s2 = pw2[it % 3]
                            it += 1
                            if i in (0, NB // 2 - 1, NB // 2, NB - 1):
                                for o in range(3):
                                    sl = 2 - o
                                    K.act(s1[:, :, sl, :], w[:, :, sl * 128:(sl + 1) * 128], AF.Exp, bias=maskW[:, i * 3 + o:i * 3 + o + 1], scale=0.125)
                            else:
                                K.act(s1, w[:, :, 0:384].re("p h (o q) -> p h o q", o=3), AF.Exp, scale=0.125)
                            K.tt("dve", s2, s1, EBT[:, 2 * t:2 * t + 2, :, :], ALU.mult)
                            if deferred:
                                deferred.pop(0)()
                            if idx + 2 < len(items):
                                wqk(items[idx + 2][0], items[idx + 2][1], pw[(it + 1) % 3])
                            pnum_, pden_ = pnums[t % 2], pdens[t % 2]
                            for o in range(3):
                                sl = 2 - o
                                st, sp_ = (o == 0), (o == 2)
                                vv = Vc[:, i + o, kv * 64:(kv + 1) * 64]
                                K.mm(pnum_[0:64, qs], vv, s2[:, 0, sl, :], start=st, stop=sp_)
                                K.mm(pnum_[64:128, qs], vv, s2[:, 1, sl, :], start=st, stop=sp_, tp=(0, 64))
                                K.mm(pden_[0:64, qs], ones[:, 0:64], s2[:, 0, sl, :], start=st, stop=sp_)
                                K.mm(pden_[64:128, qs], ones[:, 0:64], s2[:, 1, sl, :], start=st, stop=sp_, tp=(0, 64))
                            if qi == 3:
                                def epi(t=t, pnum_=pnum_, pden_=pden_):
                                    e_ = EP[t % 2]
                                    K.ts("dve", e_["rs"], pden_, esk[:, t:t + 1], ALU.add)
                                    K.cp("dve", e_["t1"], pnum_)
                                    K.act(e_["lnr"], e_["rs"], AF.Ln)
                                    K.act(e_["rs"], e_["lnr"], AF.Exp, scale=-1.0)
                                    K.tt("pool", e_["t1"], e_["t1"], e_["rs"], ALU.mult)
                                    K.tt("pool", mixC[:, t, :], e_["t1"], sgc[:, t, :], ALU.mult)
                                deferred.append(epi)
                        while deferred:
                            deferred.pop(0)()
                        for b in range(4):
                            bs = slice(b * 128, (b + 1) * 128)
                            py = pw[b % 2]
                            for half in range(2):
                                for f in range(8):
                                    K.mm(py[:, half, :], mixC[:, f, bs], wobf[:, f, half * 512:(half + 1) * 512], start=(f == 0), stop=(f == 7))
                            xo = x2[b % 2]
                            K.tt("dve", xo, py.re("p a c -> p (a c)"), xch[:, b, :], ALU.add)
                            K.act(junk, xo, AF.Square, accum=ss2[:, b % 2:b % 2 + 1])
                            K.act(ln2[:, b % 2:b % 2 + 1], ss2[:, b % 2:b % 2 + 1], AF.Ln, bias=epsb[:, 0:1], scale=1.0 / 1024.0)
                            K.act(r2[:, b % 2:b % 2 + 1], ln2[:, b % 2:b % 2 + 1], AF.Exp, scale=-0.5)
                            yb = yo[b % 2]
                            K.ts("dve", yb, xo, r2[:, b % 2:b % 2 + 1], ALU.mult)
                            K.tt("pool", yb, yb, fnbc, ALU.mult)
                            r0 = C * CH + b * 128
                            K.dma("pool", y_d[r0:r0 + 128, :], yb, yo_slot[b % 2])
    except StopBuild:
        pass
    for s_ in K.slots:
        if s_.cnt:
            nc.gpsimd.wait_ge(s_.sem, s_.cnt)
    return nc, K


def _t5_bucket(rel):
    half = 16
    max_exact = 8
    ret = (rel > 0).astype(np.int32) * half
    dist = np.abs(rel)
    large = max_exact + (np.log(np.maximum(dist, 1) / max_exact) / np.log(128 / max_exact) * (half - max_exact)).astype(np.int32)
    large = np.minimum(large, half - 1)
    return ret + np.where(dist < max_exact, dist, large)


def _static_tables():
    f32 = np.float32
    st = {}
    st["ident"] = np.eye(128, dtype=f32)
    st["aident"] = np.ascontiguousarray(np.eye(128, dtype=f32)[::-1])
    ob = np.zeros((128, 128), f32)
    ob[:64, :64] = 1
    ob[64:, 64:] = 1
    st["onesblk"] = ob
    j = np.arange(128)[:, None]
    i = np.arange(128)[None, :]
    mmat = np.zeros((128, 4, 128), f32)
    mmat[:, 0, :] = np.maximum(i - j, 0)
    mmat[:, 1, :] = (i >= j)
    mmat[:, 2, :] = np.maximum(j - i, 0)
    mmat[:, 3, :] = (j > i)
    st["mmat"] = mmat
    c = np.arange(512) % 128
    iot = np.zeros((128, 4, 512), f32)
    iot[:, 0, :] = c + 1
    iot[:, 1, :] = 128 - c
    iot[:, 2, :] = 127 - c
    iot[:, 3, :] = c
    st["iot"] = iot
    m = np.arange(640)
    rel = 255 - m
    bk = _t5_bucket(rel)
    oh = np.zeros((32, 640), f32)
    oh[bk, m] = 1
    st["oh"] = oh
    st["inwin"] = np.broadcast_to((np.abs(rel) <= 128).astype(f32)[None, :], (16, 640)).copy()
    return st


def _core_tables(is_prompt):
    f32 = np.float32
    seqlen = 4096 if is_prompt else 2048
    t = np.arange(NT) % seqlen
    d = np.arange(128) % 64
    pair = d // 2
    sgn = np.where(d % 2 == 0, -1.0, 1.0)
    quarter = 16
    freqs = (np.float32(10000.0) ** (-np.arange(quarter, dtype=f32) / quarter)).astype(f32)
    row = (t // 64).astype(f32)
    col = (t % 64).astype(f32)
    ang = np.concatenate([row[:, None] * freqs, col[:, None] * freqs], axis=-1).astype(f32)
    angd = ang[:, pair].T.astype(np.float64)
    tabA = np.stack([np.cos(angd), np.sin(angd) * sgn[:, None]]).astype(f32)
    half = 32
    freqs_b = (np.float32(10000.0) ** (-np.arange(half, dtype=f32) / half)).astype(f32)
    angb = (t.astype(f32)[:, None] * freqs_b).astype(f32)
    angbd = angb[:, pair].T.astype(np.float64)
    tabB = np.stack([np.cos(angbd), np.sin(angbd) * sgn[:, None]]).astype(f32)
    seq_of_blk = (np.arange(NB) * 128) // seqlen
    maskA = np.zeros((NCH, NB), f32)
    for C in range(NCH):
        sq = (C * CH) // seqlen
        maskA[C, :] = np.where(seq_of_blk == sq, 0.0, NEG)
    maskA = np.broadcast_to(maskA.reshape(1, -1), (128, NCH * NB)).copy()
    maskW = np.zeros((NB, 3), f32)
    for i in range(NB):
        for o in range(3):
            jb = i + o - 1
            if jb < 0 or jb >= NB or seq_of_blk[jb] != seq_of_blk[i]:
                maskW[i, o] = NEG
    maskW = np.broadcast_to(maskW.reshape(1, -1), (128, NB * 3)).copy()
    cps = seqlen // 128
    rf = np.array([0.0 if (n % cps == 0) else 1.0 for n in range(NB)], f32)
    rb = np.array([0.0 if (n % cps == cps - 1) else 1.0 for n in range(NB)], f32)
    rfb = np.broadcast_to(np.concatenate([rf, rb])[None, :], (128, 64)).copy()
    return {"tabA": tabA, "tabB": tabB, "maskA": maskA, "maskW": maskW, "rfb": rfb}


def _swap(cols):
    cols = np.asarray(cols)
    return cols ^ 1


def _prep_common(norm_g, w_in_ab, qk_norm_a, ret_decay, w_out_ab, w_in_c, sink_c, w_out_c, rel_bias, final_norm):
    f32 = np.float32
    W = np.asarray(w_in_ab[0], f32)
    qa = np.arange(0, 512)
    ka = np.arange(512, 640)
    va = np.arange(640, 768)
    ga = np.arange(768, 1280)
    qb = np.arange(1280, 1536)
    kb = np.arange(1536, 1792)
    vb = np.arange(1792, 2304)
    gb = np.arange(2304, 2816)
    kadup = np.concatenate([ka[0:64], ka[0:64], ka[64:128], ka[64:128]])
    cm = {}
    cm["wG"] = np.ascontiguousarray(W[:, np.concatenate([kadup, _swap(kadup), kb, _swap(kb), va, vb])])
    cm["wLB"] = np.ascontiguousarray(W[:, np.concatenate([qb, _swap(qb), kb, _swap(kb), gb, vb])])
    cm["wLA"] = np.ascontiguousarray(W[:, np.concatenate([qa, _swap(qa), ga])])
    cm["woab"] = np.ascontiguousarray(np.asarray(w_out_ab[0], f32))
    Wc = np.asarray(w_in_c[0], f32)
    kc = np.arange(1024, 1152)
    kcdup = np.concatenate([kc[0:64], kc[0:64], kc[64:128], kc[64:128]])
    cm["wG1"] = np.ascontiguousarray(Wc[:, np.concatenate([kcdup, np.arange(1152, 1280)])])
    cm["wL1"] = np.ascontiguousarray(Wc[:, np.concatenate([np.arange(0, 1024), np.arange(1280, 2304)])])
    cm["woc"] = np.ascontiguousarray(np.asarray(w_out_c[0], f32))
    ng = np.asarray(norm_g, f32)
    cm["gcol"] = np.ascontiguousarray(ng.reshape(2, 8, 128).transpose(2, 0, 1).reshape(128, 16))
    cm["fn"] = np.asarray(final_norm, f32).reshape(1, 1024).copy()
    g = np.asarray(qk_norm_a[0], f32)
    d = np.arange(128) % 64
    cm["gqk"] = np.stack([g[0][d], g[0][d ^ 1], g[1][d], g[1][d ^ 1]], axis=1).astype(f32).copy()
    rd = np.asarray(ret_decay[0], f32)
    hp = (np.arange(128) // 64)
    rdec = np.zeros((128, 12), f32)
    for p in range(2):
        rdec[:, p] = rd[0][2 * p + hp]
        rdec[:, 2 + p] = rd[1][2 * p + hp]
    for h in range(4):
        rdec[:, 4 + h] = rd[0][h]
        rdec[:, 8 + h] = rd[1][h]
    cm["rdec"] = rdec
    sk = np.asarray(sink_c[0], f32)
    sinkl = np.zeros((128, 8), f32)
    for t in range(8):
        sinkl[:, t] = sk[2 * t + hp]
    cm["sinkl"] = sinkl
    cm["relb"] = np.ascontiguousarray(np.asarray(rel_bias, f32))
    cm.update(_static_tables())
    return cm


_CACHE = {}


def kernel(x_prompt, x_sample, norm_g, w_in_ab, qk_norm_a, ret_decay, w_out_ab, w_in_c, sink_c, w_out_c, rel_bias, final_norm):
    xp = np.asarray(x_prompt, np.float32)
    xs = np.asarray(x_sample, np.float32)
    cm = _prep_common(norm_g, w_in_ab, qk_norm_a, ret_decay, w_out_ab, w_in_c, sink_c, w_out_c, rel_bias, final_norm)
    tp = _core_tables(True)
    tsm = _core_tables(False)
    in_maps = []
    for c in range(8):
        m = dict(cm)
        if c < 4:
            m["x"] = np.ascontiguousarray(xp[c])
            m.update(tp)
        else:
            m["x"] = np.ascontiguousarray(xs[2 * (c - 4):2 * (c - 4) + 2].reshape(NT, 1024))
            m.update(tsm)
        in_maps.append(m)
    if "nc" not in _CACHE:
        _CACHE["nc"] = build_program()[0]
    nc = _CACHE["nc"]
    res = run_bass_kernel_spmd(nc, in_maps, core_ids=list(range(8)))
    outs = [np.asarray(r["y"], np.float32) for r in res.results]
    y_prompt = np.stack(outs[0:4], axis=0)
    y_sample = np.stack(outs[4:8], axis=0).reshape(8, 2048, 1024)
    return (y_prompt, y_sample)
```

```python
import numpy as np
import concourse.bass as bass
import concourse.mybir as mybir
from concourse.bass_utils import run_bass_kernel_spmd

F32 = mybir.dt.float32
BF16 = mybir.dt.bfloat16
AF = mybir.ActivationFunctionType
ALU = mybir.AluOpType

NT = 4096
NB = 32
CH = 512
NCH = 8
EPS = 1e-6
NEG = -30000.0


class Prod:
    def __init__(self, sem, inc):
        self.sem = sem
        self.inc = inc
        self.cnt = 0


class Res:
    def __init__(self):
        self.w = {}
        self.r = {}
        self.excl = False


class V:
    def __init__(self, ap, res=None):
        self.ap = ap
        self.res = res if res is not None else Res()

    def __getitem__(self, k):
        return V(self.ap[k], self.res)

    def re(self, pat, **kw):
        return V(self.ap.rearrange(pat, **kw), self.res)

    def bc(self, shape):
        return V(self.ap.to_broadcast(shape), self.res)


class Ker:
    def __init__(self, nc):
        self.nc = nc
        self.eng = {"pe": nc.tensor, "act": nc.scalar, "dve": nc.vector, "pool": nc.gpsimd, "sp": nc.sync}
        self.prod = {}
        for n in ("pe", "act", "dve", "pool"):
            self.prod[n] = Prod(nc.alloc_semaphore("s_" + n), 1)
        self.seen = {n: {} for n in self.eng}
        self.nslot = 0
        self.ninstr = 0

    def slot(self):
        self.nslot += 1
        p = Prod(self.nc.alloc_semaphore("d%d" % self.nslot), 16)
        if hasattr(self, "slots"):
            self.slots.append(p)
        return p

    def _wait(self, en, reads, writes):
        deps = {}
        for v in reads:
            for p, i in v.res.w.items():
                deps[p] = max(deps.get(p, 0), i)
        for v in writes:
            for p, i in v.res.w.items():
                deps[p] = max(deps.get(p, 0), i)
            for p, i in v.res.r.items():
                deps[p] = max(deps.get(p, 0), i)
        e = self.eng[en]
        seen = self.seen[en]
        own = self.prod.get(en)
        for p, i in deps.items():
            if p is own and en == "pe":
                continue
            if seen.get(p, 0) >= i:
                continue
            e.wait_ge(p.sem, i)
            seen[p] = i

    def op(self, en, fn, reads, writes):
        writes = list(writes) + [r for r in reads if r.res.excl]
        self._wait(en, reads, writes)
        ins = fn(self.eng[en])
        p = self.prod[en]
        p.cnt += 1
        ins.then_inc(p.sem, 1)
        for v in reads:
            v.res.r[p] = p.cnt
        for v in writes:
            v.res.w[p] = p.cnt
        self.ninstr += 1

    def dma(self, q, out, in_, slot):
        self._wait(q, [in_], [out])
        ins = self.eng[q].dma_start(out=out.ap, in_=in_.ap)
        slot.cnt += 16
        ins.then_inc(slot.sem, 16)
        in_.res.r[slot] = slot.cnt
        out.res.w[slot] = slot.cnt

    def mm(self, out, lhsT, rhs, start=True, stop=True, tp=None):
        kw = {}
        if tp is not None:
            kw["tile_position"] = tp
        self.op("pe", lambda e: e.matmul(out.ap, lhsT.ap, rhs.ap, start=start, stop=stop, **kw), [lhsT, rhs], [out])

    def tr(self, out, in_, ident):
        self.op("pe", lambda e: e.transpose(out.ap, in_.ap, ident.ap), [in_, ident], [out])

    def act(self, out, in_, func, bias=None, scale=1.0, accum=None):
        reads = [in_]
        kw = {}
        if bias is not None:
            if isinstance(bias, V):
                reads.append(bias)
                kw["bias"] = bias.ap
            else:
                kw["bias"] = bias
        if isinstance(scale, V):
            reads.append(scale)
            kw["scale"] = scale.ap
        else:
            kw["scale"] = scale
        writes = [out]
        if accum is not None:
            writes.append(accum)
            kw["accum_out"] = accum.ap
        self.op("act", lambda e: e.activation(out.ap, in_.ap, func, **kw), reads, writes)

    def tt(self, en, out, a, b, op):
        self.op(en, lambda e: e.tensor_tensor(out.ap, a.ap, b.ap, op), [a, b], [out])

    def stt(self, en, out, in0, scalar, in1, op0, op1):
        reads = [in0, in1]
        s = scalar
        if isinstance(scalar, V):
            reads.append(scalar)
            s = scalar.ap
        self.op(en, lambda e: e.scalar_tensor_tensor(out.ap, in0.ap, s, in1.ap, op0, op1), reads, [out])

    def ts(self, en, out, in0, s1, op0, s2=None, op1=None):
        reads = [in0]
        a1 = s1
        if isinstance(s1, V):
            reads.append(s1)
            a1 = s1.ap
        a2 = s2
        if isinstance(s2, V):
            reads.append(s2)
            a2 = s2.ap
        if op1 is None:
            self.op(en, lambda e: e.tensor_scalar(out.ap, in0.ap, a1, None, op0), reads, [out])
        else:
            self.op(en, lambda e: e.tensor_scalar(out.ap, in0.ap, a1, a2, op0, op1), reads, [out])

    def cp(self, en, out, in_):
        if en == "act":
            self.op("act", lambda e: e.copy(out.ap, in_.ap), [in_], [out])
        else:
            self.op(en, lambda e: e.tensor_copy(out.ap, in_.ap), [in_], [out])

    def amul(self, out, in_, m):
        self.op("act", lambda e: e.mul(out.ap, in_.ap, m.ap), [in_, m], [out])

    def recip(self, out, in_):
        self.op("dve", lambda e: e.reciprocal(out.ap, in_.ap), [in_], [out])

    def memset(self, en, out, val):
        self.op(en, lambda e: e.memset(out.ap, val), [], [out])


class StopBuild(Exception):
    pass


import contextlib


class Scope(contextlib.ExitStack):
    def __init__(self, K):
        super().__init__()
        self.K = K
        self.tiles = []

    def __exit__(self, *a):
        fr = self.K.freed
        for v in self.tiles:
            for d in (v.res.w, v.res.r):
                for p, i in d.items():
                    fr[p] = max(fr.get(p, 0), i)
        self.tiles = []
        return super().__exit__(*a)

    def close(self):
        self.__exit__(None, None, None)


def build_program(stop=None, taps=()):
    nc = bass.Bass("TRN2", target_bir_lowering=False)
    K = Ker(nc)
    K.slots = []
    K.freed = {}
    K.tapped = {}

    def checkpoint(name):
        if stop == name:
            raise StopBuild()

    def tap(name, v, shape, dt=F32):
        if name not in taps or name in K.tapped:
            return
        d = V(nc.dram_tensor("dbg_" + name, list(shape), dt, kind="ExternalOutput").ap())
        K.tapped[name] = d
        K.dma("sp", d, v, K.slot())

    def din(name, shape, dt=F32):
        return V(nc.dram_tensor(name, list(shape), dt, kind="ExternalInput").ap())

    x_d = din("x", [NT, 1024])
    wG_d = din("wG", [1024, 1664])
    wLB_d = din("wLB", [1024, 1024])
    wLA_d = din("wLA", [1024, 1536])
    woab_d = din("woab", [1024, 1024])
    wG1_d = din("wG1", [1024, 384])
    wL1_d = din("wL1", [1024, 2048])
    woc_d = din("woc", [1024, 1024])
    gcol_d = din("gcol", [128, 16])
    fn_d = din("fn", [1, 1024])
    gqk_d = din("gqk", [128, 4])
    rdec_d = din("rdec", [128, 12])
    sink_d = din("sinkl", [128, 8])
    relb_d = din("relb", [32, 16])
    ident_d = din("ident", [128, 128])
    onesblk_d = din("onesblk", [128, 128])
    mm_d = din("mmat", [128, 4, 128])
    iot_d = din("iot", [128, 4, 512])
    oh_d = din("oh", [32, 640])
    inwin_d = din("inwin", [16, 640])
    tabA_d = din("tabA", [2, 128, NT])
    tabB_d = din("tabB", [2, 128, NT])
    maskA_d = din("maskA", [128, 256])
    maskW_d = din("maskW", [128, 96])
    rfb_d = din("rfb", [128, 64])
    y_d = V(nc.dram_tensor("y", [NT, 1024], F32, kind="ExternalOutput").ap())
    x1_d = V(nc.dram_tensor("x1s", [NT, 1024], F32, kind="Internal").ap())
    mixb_d = V(nc.dram_tensor("mixbs", [4, 128, NT], BF16, kind="Internal").ap())
    vec_h = nc.dram_tensor("vecs", [16, 640], BF16, kind="Internal")
    vec_d = V(vec_h.ap())
    aident_d = din("aident", [128, 128])
    xnT0_d = V(nc.dram_tensor("xnT0s", [NCH, 128, 8, CH], BF16, kind="Internal").ap())
    xnT1_d = V(nc.dram_tensor("xnT1s", [NCH, 128, 8, CH], BF16, kind="Internal").ap())
    kr_d = V(nc.dram_tensor("krs", [NCH, 128, 2, CH], F32, kind="Internal").ap())
    vb_d = V(nc.dram_tensor("vbs", [NCH, 128, 4, 512], BF16, kind="Internal").ap())

    es = Scope(K)
    uid = [0]

    def sb(name, shape, dt=F32, stack=None):
        uid[0] += 1
        st_ = stack if stack is not None else es
        t = st_.enter_context(nc.sbuf_tensor("sb%d_%s" % (uid[0], name), list(shape), dt))
        v = V(t[:])
        v.res.w = dict(K.freed)
        st_.tiles.append(v)
        return v

    def ps(name, shape, dt=F32, stack=None):
        uid[0] += 1
        st_ = stack if stack is not None else es
        t = st_.enter_context(nc.psum_tensor("ps%d_%s" % (uid[0], name), list(shape), dt))
        v = V(t[:])
        v.res.excl = True
        v.res.w = dict(K.freed)
        st_.tiles.append(v)
        return v

    try:
        with es:
            cslot = K.slot()
            consts = []

            def cload(name, src, shape, dt=F32, q="sp"):
                t = sb(name, shape, dt)
                K.dma(q, t, src, cslot)
                consts.append(t)
                return t

            gcol = cload("gcol", gcol_d, [128, 16])
            gqk = cload("gqk", gqk_d, [128, 4])
            rdec = cload("rdec", rdec_d, [128, 12])
            sinkl = cload("sinkl", sink_d, [128, 8])
            maskA = cload("maskA", maskA_d, [128, 256])
            maskW = cload("maskW", maskW_d, [128, 96])
            rfb = cload("rfb", rfb_d, [128, 64])
            ident32 = cload("ident32", ident_d, [128, 128])
            onesblk32 = cload("onesblk32", onesblk_d, [128, 128])
            for c in consts:
                c.res.w[cslot] = cslot.cnt
            ident = sb("ident", [128, 128], BF16)
            onesblk = sb("onesblk", [128, 128], BF16)
            ones = sb("ones", [128, 128], BF16)
            epsb = sb("epsb", [128, 1])
            K.cp("dve", ident, ident32)
            K.cp("dve", onesblk, onesblk32)
            K.memset("dve", ones, 1.0)
            K.memset("dve", epsb, EPS)

            checkpoint("c0")
            wbf = sb("wbf", [128, 8, 2048], BF16)
            wobf = sb("wobf", [128, 8, 1024], BF16)
            wst = [sb("wst%d" % i, [128, 1024]) for i in range(2)]
            wst_slot = [K.slot() for _ in range(2)]
            xnTs = [sb("xnT%d" % i, [128, 8, CH], BF16) for i in range(2)]
            xnT_slot = [K.slot() for _ in range(2)]
            cur = {"xnT": xnTs[0]}
            xch_slot = [K.slot() for _ in range(4)]
            xst_slot = [K.slot() for _ in range(2)]
            EBT = sb("EBT", [128, 16, 3, 128], BF16)
            esk = sb("esk", [128, 8], F32)
            K.act(esk, sinkl, AF.Exp)
            with Scope(K) as S1:
                relb = sb("relb", [32, 16], F32, S1)
                oh = sb("oh", [32, 640], F32, S1)
                inw = sb("inw", [16, 640], F32, S1)
                e_slot = K.slot()
                K.dma("sp", relb, relb_d, e_slot)
                K.dma("sp", oh, oh_d, e_slot)
                K.dma("sp", inw, inwin_d, e_slot)
                for t_ in (relb, oh, inw):
                    t_.res.w[e_slot] = e_slot.cnt
                pv = ps("pv", [16, 1024], F32, S1)[:, 0:640]
                vec = sb("vec", [16, 640], F32, S1)
                vecb = sb("vecb", [16, 640], BF16, S1)
                K.mm(pv[:, 0:512], relb, oh[:, 0:512])
                K.mm(pv[:, 512:640], relb, oh[:, 512:640])
                K.act(vec, pv, AF.Exp)
                K.tt("dve", vecb, vec, inw, ALU.mult)
                v_slot = K.slot()
                K.dma("sp", vec_d, vecb, v_slot)
                g_slot = K.slot()
                aid32 = sb("aid32", [128, 128], F32, S1)
                K.dma("sp", aid32, aident_d, g_slot)
                aid = sb("aid", [128, 128], BF16, S1)
                K.cp("dve", aid, aid32)
                TT = sb("TT", [128, 16 * 384], BF16, S1)
                src = V(bass.AP(vec_h, 0, [[1, 128], [640, 16], [1, 384]]), vec_d.res)
                K.dma("sp", TT.re("p (h j) -> p h j", h=16), src, g_slot)
                prev = ps("prev", [128, 2, CH], F32, S1)
                EBTf = EBT.re("p h o q -> p (h o q)")
                for n_ in range(12):
                    K.mm(prev[:, n_ % 2, :], aid, TT[:, n_ * 512:(n_ + 1) * 512])
                    K.cp("act" if n_ % 2 == 0 else "dve", EBTf[:, n_ * 512:(n_ + 1) * 512], prev[:, n_ % 2, :])
            wcount = [0]

            def handoff(srcs, dsts):
                for d_ in dsts:
                    for s_ in srcs:
                        for dd in (s_.res.w, s_.res.r):
                            for p_, i_ in dd.items():
                                d_.res.w[p_] = max(d_.res.w.get(p_, 0), i_)

            def subview(parent, ap):
                v = V(ap)
                v.res.excl = parent.res.excl
                v.res.w = dict(parent.res.w)
                return v

            class MX:
                pass

            def alloc_mx(scope, full=True):
                m = MX()
                m.xch = sb("xch", [128, 4, 1024], F32, scope)
                if full:
                    m.xn = [sb("xn%d" % i, [128, 1024], BF16, scope) for i in range(2)]
                    m.junk = sb("junk", [128, 1024], BF16, scope)
                    m.ss = sb("ss", [128, 4], F32, scope)
                    m.lnv4 = sb("lnv4", [128, 4], F32, scope)
                    m.rstd4 = sb("rstd4", [128, 4], F32, scope)
                return m

            def load_x(m, src_d, C):
                for b in range(4):
                    r0 = C * CH + b * 128
                    K.dma("sp", m.xch[:, b, :], src_d[r0:r0 + 128, :], xch_slot[b])

            def store_xnT(dst_d, C, slot_i):
                K.dma("pool", dst_d[C], cur["xnT"], xst_slot[slot_i])

            def load_xnT(src_d, C):
                i = C % 2
                K.dma("sp", xnTs[i], src_d[C], xnT_slot[i])

            def use_xnT(C):
                cur["xnT"] = xnTs[C % 2]

            def load_w_piece(dst, d0, src_d, kc, c0, c1, layer_g):
                i = wcount[0] % 2
                wcount[0] += 1
                n_ = c1 - c0
                K.dma("sp", wst[i][:, 0:n_], src_d[kc * 128:(kc + 1) * 128, c0:c1], wst_slot[i])
                en = "act" if (wcount[0] % 2 == 0) else "dve"
                if layer_g is None:
                    K.cp(en, dst[:, kc, d0:d0 + n_], wst[i][:, 0:n_])
                elif en == "act":
                    K.amul(dst[:, kc, d0:d0 + n_], wst[i][:, 0:n_], gcol[:, layer_g * 8 + kc:layer_g * 8 + kc + 1])
                else:
                    K.ts("dve", dst[:, kc, d0:d0 + n_], wst[i][:, 0:n_], gcol[:, layer_g * 8 + kc:layer_g * 8 + kc + 1], ALU.mult)

            def load_w(dst, src_d, ncols, layer_g):
                for kc in range(8):
                    for c0 in range(0, ncols, 1024):
                        c1 = min(ncols, c0 + 1024)
                        load_w_piece(dst, c0, src_d, kc, c0, c1, layer_g)

            wlo = V(wbf.ap[:, :, 0:1536])
            whi = V(wbf.ap[:, :, 1536:2048])
            wmode = {"split": False, "base": 0}

            def wcol(kc, c0, n_=128):
                c0 = c0 + wmode["base"]
                if not wmode["split"]:
                    return wbf[:, kc, c0:c0 + n_]
                if c0 + n_ <= 1536:
                    return wlo[:, kc, c0:c0 + n_]
                return whi[:, kc, c0 - 1536:c0 - 1536 + n_]

            def xnT_front(m, src_d, C):
                load_x(m, src_d, C)
                for b in range(4):
                    K.act(m.junk, m.xch[:, b, :], AF.Square, accum=m.ss[:, b:b + 1])
                K.act(m.lnv4, m.ss, AF.Ln, bias=epsb[:, 0:1], scale=1.0 / 1024.0)
                K.act(m.rstd4, m.lnv4, AF.Exp, scale=-0.5)

            def xnT_back(m, C, pTl):
                xnT = xnTs[C % 2]
                for b in range(4):
                    xb = m.xn[b % 2]
                    pT_ = pTl[b % len(pTl)]
                    K.ts("dve", xb, m.xch[:, b, :], m.rstd4[:, b:b + 1], ALU.mult)
                    for kc in range(8):
                        K.tr(pT_[:, kc, :], xb[:, kc * 128:(kc + 1) * 128], ident)
                    K.cp("act" if b % 2 == 0 else "dve", xnT[:, :, b * 128:(b + 1) * 128], pT_)

            def make_xnT(m, src_d, C, pT):
                use_xnT(C)
                xnT_front(m, src_d, C)
                xnT_back(m, C, pT if isinstance(pT, list) else [pT])

            def proj(dst, c0):
                for kc in range(8):
                    K.mm(dst, wcol(kc, c0), cur["xnT"][:, kc, :], start=(kc == 0), stop=(kc == 7))

            def rsq_bcast(dst, src_ps, nfeat, sq, psn, lnv, lhs_ones):
                K.act(sq, src_ps, AF.Square)
                K.mm(psn, lhs_ones, sq)
                K.act(lnv, psn, AF.Ln, bias=epsb[:, 0:1], scale=1.0 / nfeat)
                K.act(dst, lnv, AF.Exp, scale=-0.5)

            with Scope(K) as L0:
                LR = Scope(K)
                KaT = sb("KaT", [128, 2, NT], BF16, L0)
                Va = sb("Va", [128, NB, 128], BF16, L0)
                tabc = sb("tabc", [128, 2, CH], F32, L0)
                tab_slot = K.slot()
                tabd_slot = K.slot()
                sq = sb("sq", [128, CH], BF16, L0)
                lnv = sb("lnv", [128, CH], F32, L0)
                rs = sb("rs", [128, CH], F32, L0)
                t1 = sb("t1", [128, CH], F32, L0)
                t2 = sb("t2", [128, CH], F32, L0)
                tabg = sb("tabg", [128, 2, CH], F32, L0)
                SbAll = sb("SbAll", [128, 2, NB, 128], BF16, LR)
                tabd = sb("tabd", [128, 2, CH], F32, LR)
                vbtm = sb("vbtm", [128, 4, 512], BF16, LR)
                lg = sb("lg", [128, 12], F32, LR)
                K.act(lg, rdec, AF.Exp)
                K.ts("dve", lg, lg, -1.0, ALU.mult)
                cd = sb("cd", [128, 4], F32, LR)
                K.act(cd, lg[:, 0:4], AF.Exp, scale=128.0)
                cdr = sb("cdr", [128, 4, NB], F32, LR)
                for j in range(4):
                    off = 0 if j < 2 else 32
                    K.ts("dve", cdr[:, j, :], rfb[:, off:off + 32], cd[:, j:j + 1], ALU.mult)
                checkpoint("c1")
                QF4 = sb("QF4", [128, 2, CH], F32, LR)
                QB4 = sb("QB4", [128, 2, CH], F32, LR)
                KF4 = sb("KF4", [128, 2, CH], F32, LR)
                KB4 = sb("KB4", [128, 2, CH], F32, LR)
                DT = sb("DT", [128, 4, 128], F32, LR)
                with Scope(K) as S0:
                    iot = sb("iot", [128, 4, CH], F32, S0)
                    K.dma("sp", iot, iot_d, tabd_slot)
                    for p in range(2):
                        K.act(QF4[:, p, :], iot[:, 0, :], AF.Exp, scale=lg[:, p:p + 1])
                        K.act(QB4[:, p, :], iot[:, 1, :], AF.Exp, scale=lg[:, 2 + p:3 + p])
                        K.act(KF4[:, p, :], iot[:, 2, :], AF.Exp, scale=lg[:, p:p + 1])
                        K.act(KB4[:, p, :], iot[:, 3, :], AF.Exp, scale=lg[:, 2 + p:3 + p])
                    K.ts("dve", KF4, KF4, 0.125, ALU.mult)
                    K.ts("dve", KB4, KB4, 0.125, ALU.mult)
                    mmat = sb("mmat", [128, 4, 128], F32, S0)
                    K.dma("sp", mmat, mm_d, tab_slot)
                    d1 = sb("d1", [128, 128], F32, S0)
                    d2 = sb("d2", [128, 128], F32, S0)
                    for h in range(4):
                        checkpoint("d0")
                        K.act(d1, mmat[:, 0, :], AF.Exp, scale=lg[:, 4 + h:5 + h])
                        checkpoint("d1")
                        K.tt("dve", d1, d1, mmat[:, 1, :], ALU.mult)
                        checkpoint("d2")
                        K.act(d2, mmat[:, 2, :], AF.Exp, scale=lg[:, 8 + h:9 + h])
                        K.tt("dve", d2, d2, mmat[:, 3, :], ALU.mult)
                        K.tt("dve", d1, d1, d2, ALU.add)
                        checkpoint("d3")
                        K.ts("dve", DT[:, h, :], d1, 0.125, ALU.mult)
                        checkpoint("d4")

                def load_tab(dst, slot, src_d, C):
                    K.dma("sp", dst, V(src_d.ap[:, :, C * CH:(C + 1) * CH].rearrange("t p c -> p t c"), src_d.res), slot)

                def rope(psa, psb, tab, out32, ga=None, gb=None):
                    if ga is None:
                        K.tt("dve", t1, psa, tab[:, 0, :], ALU.mult)
                        K.tt("dve", t2, psb, tab[:, 1, :], ALU.mult)
                    else:
                        K.amul(tabg[:, 0, :], tab[:, 0, :], ga)
                        K.amul(tabg[:, 1, :], tab[:, 1, :], gb)
                        K.tt("dve", t1, psa, tabg[:, 0, :], ALU.mult)
                        K.tt("dve", t2, psb, tabg[:, 1, :], ALU.mult)
                    K.tt("pool", out32, t1, t2, ALU.add)

                checkpoint("setup0")
                load_w(wbf, wG_d, 1664, 0)
                with Scope(K) as PG:
                    pT = ps("pT", [128, 8, 128], BF16, PG)
                    pk = pT.re("p (c t) q -> p c t q", t=2)
                    pbig = ps("pbigG", [128, 7, CH], F32, PG)
                    bk = [subview(pbig, pbig.ap[:, i, :]) for i in range(7)]
                    pn = bk[4]
                    pT2g = V(bk[5].ap.bitcast(BF16).rearrange("p (k q) -> p k q", k=8), bk[5].res)
                    pkv = V(bk[6].ap[:, 0:256].rearrange("p (a b) -> p a b", a=2), bk[6].res)
                    kdbT = sb("kdbT", [128, 2, CH], BF16, PG)
                    kdbtm = sb("kdbtm", [128, 4, 2, 128], BF16, PG)
                    Rb = sb("Rb", [128, 2, 128], F32, PG)
                    mxg = alloc_mx(PG)
                    WS = [dict(sq=sq, lnv=lnv, rs=rs, t1=t1, t2=t2),
                          dict(sq=sb("wsq", [128, CH], BF16, PG), lnv=sb("wlnv", [128, CH], F32, PG),
                               rs=sb("wrs", [128, CH], F32, PG), t1=sb("wt1", [128, CH], F32, PG),
                               t2=sb("wt2", [128, CH], F32, PG))]
                    K.memset("dve", Rb, 0.0)
                    kr_slot = [K.slot() for _ in range(2)]
                    vb_slot = K.slot()
                    for C in range(NCH - 1, -1, -1):
                        make_xnT(mxg, x_d, C, [pT, pT2g])
                        store_xnT(xnT0_d, C, C % 2)
                        load_w_piece(wobf, 0, woab_d, C, 0, 1024, None)
                        load_tab(tabc, tab_slot, tabA_d, C)
                        load_tab(tabd, tabd_slot, tabB_d, C)
                        K.amul(tabg[:, 0, :], tabc[:, 0, :], gqk[:, 2:3])
                        K.amul(tabg[:, 1, :], tabc[:, 1, :], gqk[:, 3:4])
                        for t in range(2):
                            w_ = WS[t % 2]
                            pa_, pb_ = bk[2 * t], bk[2 * t + 1]
                            proj(pa_, t * 128)
                            proj(pb_, 256 + t * 128)
                            rsq_bcast(w_["rs"], pa_, 64.0, w_["sq"], pn, w_["lnv"], onesblk)
                            K.tt("dve", w_["t1"], pa_, tabg[:, 0, :], ALU.mult)
                            K.tt("dve", w_["t2"], pb_, tabg[:, 1, :], ALU.mult)
                            K.tt("pool", w_["t1"], w_["t1"], w_["t2"], ALU.add)
                            K.tt("pool", KaT[:, t, C * CH:(C + 1) * CH], w_["t1"], w_["rs"], ALU.mult)
                        for t in range(2):
                            w_ = WS[t % 2]
                            pa_, pb_ = bk[2 * t], bk[2 * t + 1]
                            proj(pa_, 512 + t * 128)
                            proj(pb_, 768 + t * 128)
                            K.tt("dve", w_["t1"], pa_, tabd[:, 0, :], ALU.mult)
                            K.tt("dve", w_["t2"], pb_, tabd[:, 1, :], ALU.mult)
                            K.tt("pool", w_["t1"], w_["t1"], w_["t2"], ALU.add)
                            K.dma("pool", kr_d[C][:, t, :], w_["t1"], kr_slot[t % 2])
                            K.tt("pool", kdbT[:, t, :], w_["t1"], KB4[:, t, :], ALU.mult)
                        for b in range(4):
                            pva = (bk[4] if b % 2 == 0 else bk[2])[:, 0:128]
                            pvb = bk[5] if b % 2 == 0 else bk[3]
                            for kc in range(8):
                                K.mm(pva, cur["xnT"][:, kc, b * 128:(b + 1) * 128], wbf[:, kc, 1024:1152], start=(kc == 0), stop=(kc == 7))
                            for kc in range(8):
                                K.mm(pvb, cur["xnT"][:, kc, b * 128:(b + 1) * 128], wbf[:, kc, 1152:1664], start=(kc == 0), stop=(kc == 7))
                            K.cp("act", Va[:, C * 4 + b, :], pva)
                            K.cp("dve", vbtm[:, b, :], pvb)
                        K.dma("pool", vb_d[C], vbtm, vb_slot)
                        for t in range(2):
                            for cj in range(4):
                                K.tr(pk[:, cj, t, :], kdbT[:, t, cj * 128:(cj + 1) * 128], ident)
                        K.cp("act", kdbtm, pk)
                        for cj in range(3, -1, -1):
                            n = C * 4 + cj
                            for p in range(2):
                                K.mm(pkv[0:64, p, :], kdbtm[:, cj, p, 0:64], vbtm[:, cj, (2 * p) * 128:(2 * p + 1) * 128])
                                K.mm(pkv[64:128, p, :], kdbtm[:, cj, p, 64:128], vbtm[:, cj, (2 * p + 1) * 128:(2 * p + 2) * 128], tp=(0, 64))
                            K.ts("dve", SbAll[:, :, n, :], Rb, rfb[:, 32 + n:33 + n], ALU.mult)
                            for p in range(2):
                                K.ts("dve", Rb[:, p, :], Rb[:, p, :], cdr[:, 2 + p, n:n + 1], ALU.mult)
                                K.tt("dve", Rb[:, p, :], pkv[:, p, :], Rb[:, p, :], ALU.add)

                tap("KaT", KaT, [128, 2, NT], BF16)
                tap("Va", Va, [128, NB, 128], BF16)
                tap("SbAll", SbAll, [128, 2, NB, 128], BF16)
                checkpoint("G")
                load_w(wbf, wLB_d, 1024, 0)
                with Scope(K) as PB:
                    pT = ps("pT", [128, 8, 128], BF16, PB)
                    pk = pT.re("p (c t) q -> p c t q", t=2)
                    pbig = ps("pbigB", [128, 7, CH], F32, PB)
                    bk = [subview(pbig, pbig.ap[:, i, :]) for i in range(7)]
                    pa, pb, pss = bk[0], bk[1], bk[2]
                    po = subview(pbig, pbig.ap[:, 3:7, :])
                    qrT = sb("qrT", [128, 2, CH], BF16, PB)
                    qdf = sb("qdf", [128, 2, CH], BF16, PB)
                    qdb = sb("qdb", [128, 2, CH], BF16, PB)
                    krT = sb("krT", [128, 2, CH], BF16, PB)
                    kdfT = sb("kdfT", [128, 2, CH], BF16, PB)
                    kdftm = sb("kdftm", [128, 4, 2, 128], BF16, PB)
                    sg = sb("sg", [128, 4, CH], BF16, PB)
                    ATs = [sb("AT%d" % i, [128, 4, 128], BF16, PB) for i in range(2)]
                    Sfs = [sb("Sf%d" % i, [128, 2, 128], BF16, PB) for i in range(2)]
                    Rf = sb("Rf", [128, 2, 128], F32, PB)
                    mixBc = [sb("mixBc%d" % i, [128, 4, CH], BF16, PB) for i in range(1)]
                    mixB_slot = [K.slot() for _ in range(1)]
                    WS = [dict(sq=sq, lnv=lnv, rs=rs, t1=t1, t2=t2),
                          dict(sq=sb("wsq", [128, CH], BF16, PB), lnv=sb("wlnv", [128, CH], F32, PB),
                               rs=sb("wrs", [128, CH], F32, PB), t1=sb("wt1", [128, CH], F32, PB),
                               t2=sb("wt2", [128, CH], F32, PB))]
                    K.memset("dve", Rf, 0.0)
                    krl_slot = [K.slot() for _ in range(2)]
                    vbl_slot = K.slot()
                    load_xnT(xnT0_d, 0)
                    pairs = [(bk[0], bk[1]), (bk[3], bk[4]), (bk[5], bk[6])]
                    for C in range(NCH):
                        use_xnT(C)
                        if C + 1 < NCH:
                            load_xnT(xnT0_d, C + 1)
                        load_tab(tabd, tabd_slot, tabB_d, C)
                        handoff([po], bk[3:7])
                        ip = 0
                        for t in range(2):
                            w_ = WS[ip % 2]
                            pa_, pb_ = pairs[ip % 3]
                            ip += 1
                            proj(pa_, t * 128)
                            proj(pb_, 256 + t * 128)
                            K.tt("dve", w_["t1"], pa_, tabd[:, 0, :], ALU.mult)
                            K.tt("dve", w_["t2"], pb_, tabd[:, 1, :], ALU.mult)
                            K.tt("pool", w_["t1"], w_["t1"], w_["t2"], ALU.add)
                            K.cp("act", qrT[:, t, :], w_["t1"])
                            K.tt("pool", qdf[:, t, :], w_["t1"], QF4[:, t, :], ALU.mult)
                            K.tt("pool", qdb[:, t, :], w_["t1"], QB4[:, t, :], ALU.mult)
                        K.dma("sp", vbtm, vb_d[C], vbl_slot)
                        for t in range(2):
                            kr_t = WS[t]["t2"]
                            K.dma("sp", kr_t, kr_d[C][:, t, :], krl_slot[t])
                            K.cp("act", krT[:, t, :], kr_t)
                            K.tt("pool", kdfT[:, t, :], kr_t, KF4[:, t, :], ALU.mult)
                        for h in range(4):
                            pa_ = bk[3 + h]
                            proj(pa_, 512 + h * 128)
                            K.act(sg[:, h, :], pa_, AF.Silu)
                        for t in range(2):
                            for cj in range(4):
                                K.tr(pk[:, cj, t, :], kdfT[:, t, cj * 128:(cj + 1) * 128], ident)
                        K.cp("act", kdftm, pk)
                        handoff(bk[3:7], [po])
                        for cj in range(4):
                            n = C * 4 + cj
                            cs = slice(cj * 128, (cj + 1) * 128)
                            Sf = Sfs[cj % 2]
                            AT = ATs[cj % 2]
                            K.ts("dve", Sf, Rf, rfb[:, n:n + 1], ALU.mult)
                            for p in range(2):
                                K.mm(pa[0:64, p * 128:(p + 1) * 128], kdftm[:, cj, p, 0:64], vbtm[:, cj, (2 * p) * 128:(2 * p + 1) * 128])
                                K.mm(pa[64:128, p * 128:(p + 1) * 128], kdftm[:, cj, p, 64:128], vbtm[:, cj, (2 * p + 1) * 128:(2 * p + 2) * 128], tp=(0, 64))
                            for p in range(2):
                                K.ts("dve", Rf[:, p, :], Rf[:, p, :], cdr[:, p, n:n + 1], ALU.mult)
                                K.tt("dve", Rf[:, p, :], pa[:, p * 128:(p + 1) * 128], Rf[:, p, :], ALU.add)
                            for h in range(4):
                                t, r0 = h // 2, (h % 2) * 64
                                pdst = pss if (h % 2 == 0) else pb
                                K.mm(pdst[:, t * 128:(t + 1) * 128], krT[r0:r0 + 64, t, cs], qrT[r0:r0 + 64, t, cs])
                            ATv = AT.re("p (t hp) i -> p hp t i", hp=2)
                            DTv = DT.re("p (t hp) i -> p hp t i", hp=2)
                            K.tt("dve", ATv[:, 0, :, :], pss[:, 0:256].re("p (t i) -> p t i", t=2), DTv[:, 0, :, :], ALU.mult)
                            K.tt("dve", ATv[:, 1, :, :], pb[:, 0:256].re("p (t i) -> p t i", t=2), DTv[:, 1, :, :], ALU.mult)
                            for h in range(4):
                                t, r0 = h // 2, (h % 2) * 64
                                K.mm(po[:, h, cs], vbtm[:, cj, h * 128:(h + 1) * 128], AT[:, h, :], start=True, stop=False)
                                K.mm(po[:, h, cs], Sf[r0:r0 + 64, t, :], qdf[r0:r0 + 64, t, cs], start=False, stop=False)
                                K.mm(po[:, h, cs], SbAll[r0:r0 + 64, t, n, :], qdb[r0:r0 + 64, t, cs], start=False, stop=True)
                        mb = mixBc[0]
                        for h0 in (0, 2):
                            hs = (h0, h0 + 1)
                            for h in hs:
                                K.act(WS[h % 2]["sq"], po[:, h, :], AF.Square)
                            for h in hs:
                                K.mm(bk[h % 2], ones, WS[h % 2]["sq"])
                            for h in hs:
                                K.act(WS[h % 2]["lnv"], bk[h % 2], AF.Ln, bias=epsb[:, 0:1], scale=1.0 / 128.0)
                            for h in hs:
                                K.act(WS[h % 2]["rs"], WS[h % 2]["lnv"], AF.Exp, scale=-0.5)
                            for h in hs:
                                K.tt("dve", WS[h % 2]["t1"], po[:, h, :], WS[h % 2]["rs"], ALU.mult)
                                K.tt("pool", mb[:, h, :], WS[h % 2]["t1"], sg[:, h, :], ALU.mult)
                        K.dma("pool", V(mixb_d.ap[:, :, C * CH:(C + 1) * CH].rearrange("h p c -> p h c"), mixb_d.res), mb, mixB_slot[0])

                tap("mixb", mixb_d, [4, 128, NT], BF16)
                checkpoint("LB")
                LR.close()
                load_w(wbf, wLA_d, 1536, 0)
                handoff([wbf], [wlo, whi])
                wmode["split"] = True
                with Scope(K) as PA:
                    pbig = ps("pbig", [128, 8, CH], F32, PA)
                    psc = [subview(pbig, pbig.ap[:, 2 * i:2 * i + 2, :]) for i in range(3)]
                    pnum = subview(pbig, pbig.ap[:, 6, :])
                    pden = subview(pbig, pbig.ap[:, 7, :])
                    bk = [subview(pbig, pbig.ap[:, i, :]) for i in range(6)] + [pnum, pden]
                    WS = [dict(sq=sq, lnv=lnv, rs=rs, t1=t1, t2=t2),
                          dict(sq=sb("wsq", [128, CH], BF16, PA), lnv=sb("wlnv", [128, CH], F32, PA),
                               rs=sb("wrs", [128, CH], F32, PA), t1=sb("wt1", [128, CH], F32, PA),
                               t2=sb("wt2", [128, CH], F32, PA))]
                    qaT = sb("qaT", [128, 4, CH], BF16, PA)
                    sga = sb("sga", [128, 4, CH], BF16, PA)
                    mixAs = [sb("mixA%d" % i, [128, 4, CH], BF16, PA) for i in range(2)]
                    mixBls = [sb("mixBl%d" % i, [128, 4, CH], BF16, PA) for i in range(2)]
                    mixBl_slots = [K.slot() for _ in range(2)]
                    xblk = [sb("xblk%d" % i, [128, 1024], F32, PA) for i in range(2)]
                    xblk_slot = [K.slot() for _ in range(2)]
                    nxb = [0]
                    pTs = [sb("pTs%d" % i, [128, 2, CH], BF16, PA) for i in range(3)]
                    x1b = [sb("x1b%d" % i, [128, 1024], F32, PA) for i in range(2)]
                    dcp = sb("dcp", [128, CH], F32, PA)
                    ncp = sb("ncp", [128, CH], F32, PA)
                    x1b_slot = [K.slot() for _ in range(2)]
                    qaTs = [qaT, sb("qaT1", [128, 4, CH], BF16, PA)]
                    sgas = [sga, sb("sga1", [128, 4, CH], BF16, PA)]
                    tabcs = [tabc, sb("tabc1", [128, 2, CH], F32, PA)]
                    tabgs = [tabg, sb("tabg1", [128, 2, CH], F32, PA)]
                    tabsl = [tab_slot, K.slot()]
                    nbuf = [0]
                    npt = [0]

                    held = set()

                    def take_buf():
                        while True:
                            i_ = nbuf[0] % 3
                            nbuf[0] += 1
                            if i_ not in held:
                                return psc[i_]

                    def hold(b_):
                        held.add(psc.index(b_))

                    def release(b_):
                        held.discard(psc.index(b_))

                    def projx(dst, c0, xT):
                        for kc in range(8):
                            K.mm(dst, wcol(kc, c0), xT[:, kc, :], start=(kc == 0), stop=(kc == 7))

                    def proj_items(Cn):
                        q_, g_ = qaTs[Cn % 2], sgas[Cn % 2]
                        tc_, tg_ = tabcs[Cn % 2], tabgs[Cn % 2]
                        xT = xnTs[Cn % 2]

                        def prep():
                            load_tab(tc_, tabsl[Cn % 2], tabA_d, Cn)
                            K.amul(tg_[:, 0, :], tc_[:, 0, :], gqk[:, 0:1])
                            K.amul(tg_[:, 1, :], tc_[:, 1, :], gqk[:, 1:2])

                        items = []
                        for t in range(4):
                            def mk(t=t):
                                st = {}
                                w_ = WS[t % 2]

                                def s1():
                                    st["buf"] = take_buf()
                                    hold(st["buf"])
                                    projx(st["buf"][:, 0, :], t * 128, xT)
                                    projx(st["buf"][:, 1, :], 512 + t * 128, xT)

                                def s2():
                                    pa_, pb_ = st["buf"][:, 0, :], st["buf"][:, 1, :]
                                    K.tt("dve", w_["t1"], pa_, tg_[:, 0, :], ALU.mult)
                                    K.tt("dve", w_["t2"], pb_, tg_[:, 1, :], ALU.mult)
                                    K.act(w_["sq"], pa_, AF.Square)

                                def s3():
                                    K.mm(st["buf"][:, 1, :], onesblk, w_["sq"])

                                def s4():
                                    K.act(w_["lnv"], st["buf"][:, 1, :], AF.Ln, bias=epsb[:, 0:1], scale=1.0 / 64.0)
                                    K.act(w_["rs"], w_["lnv"], AF.Exp, scale=-0.5)
                                    K.tt("pool", w_["t1"], w_["t1"], w_["t2"], ALU.add)
                                    K.tt("pool", q_[:, t, :], w_["t1"], w_["rs"], ALU.mult)
                                    release(st["buf"])
                                return [(s1, 3), (s2, 1), (s3, 2), (s4, 0)]
                            items.append(mk())
                        for t2_ in range(2):
                            def mk(t2_=t2_):
                                st = {}

                                def s1():
                                    st["buf"] = take_buf()
                                    hold(st["buf"])
                                    for j in range(2):
                                        projx(st["buf"][:, j, :], 1024 + (2 * t2_ + j) * 128, xT)

                                def s2():
                                    for j in range(2):
                                        K.act(WS[j]["t1"], st["buf"][:, j, :], AF.Tanh, scale=0.5)

                                def s3():
                                    for j in range(2):
                                        K.ts("dve", WS[j]["t1"], WS[j]["t1"], 0.5, ALU.mult, 0.5, ALU.add)
                                        K.tt("dve", g_[:, 2 * t2_ + j, :], st["buf"][:, j, :], WS[j]["t1"], ALU.mult)
                                    release(st["buf"])
                                return [(s1, 3), (s2, 1), (s3, 0)]
                            items.append(mk())
                        return prep, items

                    def outproj_items(Cc):
                        mA, mB = mixAs[Cc % 2], mixBls[Cc % 2]
                        items = []
                        for b in range(4):
                            def mk(b=b):
                                st = {}
                                bs = slice(b * 128, (b + 1) * 128)
                                r0 = Cc * CH + b * 128

                                def s1():
                                    i_ = nxb[0] % 2
                                    nxb[0] += 1
                                    st["i"] = i_
                                    K.dma("sp", xblk[i_], x_d[r0:r0 + 128, :], xblk_slot[i_])
                                    st["buf"] = take_buf()
                                    hold(st["buf"])
                                    py = st["buf"]
                                    for half in range(2):
                                        for f in range(8):
                                            src = mA[:, f, bs] if f < 4 else mB[:, f - 4, bs]
                                            K.mm(py[:, half, :], src, wobf[:, f, half * 512:(half + 1) * 512], start=(f == 0), stop=(f == 7))

                                def s2():
                                    xo = x1b[st["i"]]
                                    K.tt("dve", xo, st["buf"].re("p a c -> p (a c)"), xblk[st["i"]], ALU.add)
                                    K.dma("pool", x1_d[r0:r0 + 128, :], xo, x1b_slot[st["i"]])
                                    release(st["buf"])
                                return [(s1, 4), (s2, 0)]
                            items.append(mk())
                        return items

                    load_xnT(xnT0_d, 0)
                    prep0, items0 = proj_items(0)
                    prep0()
                    for it_ in items0:
                        for st_fn, _d in it_:
                            st_fn()
                    carry = []
                    for C in range(NCH):
                        use_xnT(C)
                        qaT_c, sga_c = qaTs[C % 2], sgas[C % 2]
                        mixA = mixAs[C % 2]
                        load_w_piece(whi, 0, wG1_d, C, 0, 384, 1)
                        pending = list(carry)
                        carry = []
                        if C + 1 < NCH:
                            load_xnT(xnT0_d, C + 1)
                            prepn, pitems = proj_items(C + 1)
                            prepn()
                            pending = pending + pitems
                        K.dma("pool", mixBls[C % 2], V(mixb_d.ap[:, :, C * CH:(C + 1) * CH].rearrange("h p c -> p h c"), mixb_d.res), mixBl_slots[C % 2])
                        nit = 0
                        active = [None]
                        for t in range(4):
                            kv = t // 2
                            fifo = []

                            def qk(kb_):
                                sc_ = take_buf()
                                fifo.append(sc_)
                                ks = slice(kb_ * 128, (kb_ + 1) * 128)
                                K.mm(sc_[:, 0, :], KaT[0:64, kv, ks], qaT_c[0:64, t, :])
                                K.mm(sc_[:, 1, :], KaT[64:128, kv, ks], qaT_c[64:128, t, :])

                            qk(0)
                            qk(1)
                            for kb in range(NB):
                                sc = fifo.pop(0)
                                pt = pTs[npt[0] % 3]
                                npt[0] += 1
                                nit += 1
                                K.act(pt, sc, AF.Exp, bias=maskA[:, C * NB + kb:C * NB + kb + 1], scale=0.125)
                                if kb + 2 < NB:
                                    qk(kb + 2)
                                if active[0] is None and pending and nit % 12 == 3:
                                    active[0] = [pending.pop(0), 0, nit]
                                if active[0] is not None and nit >= active[0][2]:
                                    stages_, si_, _due = active[0]
                                    fn_, delay_ = stages_[si_]
                                    fn_()
                                    if si_ + 1 < len(stages_):
                                        active[0] = [stages_, si_ + 1, nit + delay_]
                                    else:
                                        active[0] = None
                                st, sp_ = (kb == 0), (kb == NB - 1)
                                K.mm(pnum[0:64, :], Va[:, kb, kv * 64:(kv + 1) * 64], pt[:, 0, :], start=st, stop=sp_)
                                K.mm(pnum[64:128, :], Va[:, kb, kv * 64:(kv + 1) * 64], pt[:, 1, :], start=st, stop=sp_, tp=(0, 64))
                                K.mm(pden[0:64, :], ones[:, 0:64], pt[:, 0, :], start=st, stop=sp_)
                                K.mm(pden[64:128, :], ones[:, 0:64], pt[:, 1, :], start=st, stop=sp_, tp=(0, 64))
                            K.cp("dve", dcp, pden)
                            K.cp("dve", ncp, pnum)
                            K.recip(dcp, dcp)
                            K.tt("dve", ncp, ncp, dcp, ALU.mult)
                            K.tt("pool", mixA[:, t, :], ncp, sga_c[:, t, :], ALU.mult)
                        while active[0] is not None or pending:
                            if active[0] is None:
                                active[0] = [pending.pop(0), 0, 0]
                            stages_, si_, _due = active[0]
                            stages_[si_][0]()
                            active[0] = [stages_, si_ + 1, 0] if si_ + 1 < len(stages_) else None
                        carry = outproj_items(C)
                        if C == NCH - 1:
                            for it_ in carry:
                                for st_fn, _d in it_:
                                    st_fn()
                            carry = []

            tap("x1", x1_d, [NT, 1024], F32)
            checkpoint("LA")
            with Scope(K) as L1:
                KcT = sb("KcT", [128, 2, (NB + 2) * 128], BF16, L1)
                Vc = sb("Vc", [128, NB + 2, 128], BF16, L1)
                K.memset("pool", KcT[:, :, 0:128], 0.0)
                K.memset("pool", KcT[:, :, (NB + 1) * 128:(NB + 2) * 128], 0.0)
                K.memset("pool", Vc[:, 0, :], 0.0)
                K.memset("pool", Vc[:, NB + 1, :], 0.0)
                tap("EBT", EBT, [128, 16, 3, 128], BF16)
                checkpoint("EBT")
                wmode["base"] = 1536
                with Scope(K) as PG1:
                    pT = ps("pT", [128, 8, 128], BF16, PG1)
                    pa = ps("pa", [128, CH], F32, PG1)
                    pva = ps("pva", [128, 512], F32, PG1)[:, 0:128]
                    mxs = [alloc_mx(PG1), alloc_mx(PG1)]
                    pT2 = ps("pT2", [128, 8, 128], BF16, PG1)
                    pa2 = ps("pa2", [128, CH], F32, PG1)
                    pva2 = ps("pva2", [128, 512], F32, PG1)[:, 0:128]
                    xnT_front(mxs[0], x1_d, 0)
                    xnT_back(mxs[0], 0, [pT, pT2])
                    for C in range(NCH):
                        use_xnT(C)
                        store_xnT(xnT1_d, C, C % 2)
                        if C + 1 < NCH:
                            xnT_front(mxs[(C + 1) % 2], x1_d, C + 1)
                        load_w_piece(wlo, 0, wL1_d, C, 0, 1024, 1)
                        load_w_piece(wlo, 1024, wL1_d, C, 1024, 1536, 1)
                        load_w_piece(wobf, 0, woc_d, C, 0, 1024, None)
                        for t in range(2):
                            pa_ = pa if t == 0 else pa2
                            proj(pa_, t * 128)
                            K.cp("act" if t == 0 else "dve", KcT[:, t, (C * 4 + 1) * 128:(C * 4 + 5) * 128], pa_)
                        for b in range(4):
                            pv_ = pva if b % 2 == 0 else pva2
                            for kc in range(8):
                                K.mm(pv_, cur["xnT"][:, kc, b * 128:(b + 1) * 128], wcol(kc, 256), start=(kc == 0), stop=(kc == 7))
                            K.cp("dve" if b % 2 == 0 else "act", Vc[:, C * 4 + b + 1, :], pv_)
                        if C + 1 < NCH:
                            xnT_back(mxs[(C + 1) % 2], C + 1, [pT, pT2])

                tap("KcT", KcT, [128, 2, (NB + 2) * 128], BF16)
                tap("Vc", Vc, [128, NB + 2, 128], BF16)
                checkpoint("G1")
                wmode["base"] = 0
                for kc_ in range(8):
                    load_w_piece(whi, 0, wL1_d, kc_, 1536, 2048, 1)
                with Scope(K) as PL1:
                    pbig = ps("pbig1", [128, 8, CH], F32, PL1)
                    pw = [subview(pbig, pbig.ap[:, 2 * i:2 * i + 2, :]) for i in range(3)]
                    pnum = subview(pbig, pbig.ap[:, 6, :])
                    pden = subview(pbig, pbig.ap[:, 7, :])
                    bk = [subview(pbig, pbig.ap[:, i, :]) for i in range(6)]
                    qcT = sb("qcT", [128, 8, CH], BF16, PL1)
                    sgc = sb("sgc", [128, 8, CH], BF16, PL1)
                    mixC = sb("mixC", [128, 8, CH], BF16, PL1)
                    pws = [sb("pws%d" % i, [128, 2, 3, 128], BF16, PL1) for i in range(3)]
                    pw2 = [sb("pw2%d" % i, [128, 2, 3, 128], BF16, PL1) for i in range(3)]
                    rs = sb("rs1", [128, CH], F32, PL1)
                    lnr = sb("lnr", [128, CH], F32, PL1)
                    t1 = sb("t11", [128, CH], F32, PL1)
                    EP = [dict(rs=rs, t1=t1, lnr=lnr),
                          dict(rs=sb("rs1b", [128, CH], F32, PL1), t1=sb("t11b", [128, CH], F32, PL1), lnr=sb("lnrb", [128, CH], F32, PL1))]
                    pnums = [pnum, pnum]
                    pdens = [pden, pden]
                    x2 = [sb("x2%d" % i, [128, 1024], F32, PL1) for i in range(2)]
                    yo = [sb("yo%d" % i, [128, 1024], F32, PL1) for i in range(2)]
                    yo_slot = [K.slot() for _ in range(2)]
                    ss2 = sb("ss2", [128, 2], F32, PL1)
                    ln2 = sb("ln2", [128, 2], F32, PL1)
                    r2 = sb("r2", [128, 2], F32, PL1)
                    it = 0
                    mxl = alloc_mx(PL1, full=False)
                    xch = mxl.xch
                    junk = sb("junk1", [128, 1024], BF16, PL1)
                    fnbc = sb("fnbc", [128, 1024], F32, PL1)
                    fn_slot = K.slot()
                    K.dma("sp", fnbc, V(fn_d.ap.to_broadcast([128, 1024]), fn_d.res), fn_slot)
                    load_xnT(xnT1_d, 0)
                    for C in range(NCH):
                        use_xnT(C)
                        if C + 1 < NCH:
                            load_xnT(xnT1_d, C + 1)
                        load_x(mxl, x1_d, C)
                        handoff(pw, bk[0:6])
                        for t in range(8):
                            pa_ = bk[t % 6]
                            proj(pa_, t * 128)
                            K.cp("act" if t % 2 == 0 else "dve", qcT[:, t, :], pa_)
                        for t in range(8):
                            pa_ = bk[(t + 2) % 6]
                            proj(pa_, 1024 + t * 128)
                            K.act(sgc[:, t, :], pa_, AF.Silu)
                        handoff(bk[0:6], pw)
                        items = [(t, qi) for t in range(8) for qi in range(4)]

                        def wqk(t, qi, w):
                            kv = t // 4
                            i = C * 4 + qi
                            qs = slice(qi * 128, (qi + 1) * 128)
                            for o in range(3):
                                sl = 2 - o
                                ks = slice((i + o) * 128, (i + o + 1) * 128)
                                K.mm(w[:, 0, sl * 128:(sl + 1) * 128], KcT[0:64, kv, ks], qcT[0:64, t, qs])
                                K.mm(w[:, 1, sl * 128:(sl + 1) * 128], KcT[64:128, kv, ks], qcT[64:128, t, qs])

                        wqk(items[0][0], items[0][1], pw[it % 3])
                        wqk(items[1][0], items[1][1], pw[(it + 1) % 3])
                        deferred = []
                        for idx, (t, qi) in enumerate(items):
                            kv = t // 4
                            i = C * 4 + qi
                            qs = slice(qi * 128, (qi + 1) * 128)
                            w = pw[it % 3]
                            s1 = pws[it % 3]
                            s2 = pw2[it % 3]
                            it += 1
                            if i in (0, NB // 2 - 1, NB // 2, NB - 1):
                                for o in range(3):
                                    sl = 2 - o
                                    K.act(s1[:, :, sl, :], w[:, :, sl * 128:(sl + 1) * 128], AF.Exp, bias=maskW[:, i * 3 + o:i * 3 + o + 1], scale=0.125)
                            else:
                                K.act(s1, w[:, :, 0:384].re("p h (o q) -> p h o q", o=3), AF.Exp, scale=0.125)
                            K.tt("dve", s2, s1, EBT[:, 2 * t:2 * t + 2, :, :], ALU.mult)
                            if deferred:
                                deferred.pop(0)()
                            if idx + 2 < len(items):
                                wqk(items[idx + 2][0], items[idx + 2][1], pw[(it + 1) % 3])
                            pnum_, pden_ = pnums[t % 2], pdens[t % 2]
                            for o in range(3):
                                sl = 2 - o
                                st, sp_ = (o == 0), (o == 2)
                                vv = Vc[:, i + o, kv * 64:(kv + 1) * 64]
                                K.mm(pnum_[0:64, qs], vv, s2[:, 0, sl, :], start=st, stop=sp_)
                                K.mm(pnum_[64:128, qs], vv, s2[:, 1, sl, :], start=st, stop=sp_, tp=(0, 64))
                                K.mm(pden_[0:64, qs], ones[:, 0:64], s2[:, 0, sl, :], start=st, stop=sp_)
                                K.mm(pden_[64:128, qs], ones[:, 0:64], s2[:, 1, sl, :], start=st, stop=sp_, tp=(0, 64))
                            if qi == 3:
                                def epi(t=t, pnum_=pnum_, pden_=pden_):
                                    e_ = EP[t % 2]
                                    K.ts("dve", e_["rs"], pden_, esk[:, t:t + 1], ALU.add)
                                    K.cp("dve", e_["t1"], pnum_)
                                    K.act(e_["lnr"], e_["rs"], AF.Ln)
                                    K.act(e_["rs"], e_["lnr"], AF.Exp, scale=-1.0)
                                    K.tt("pool", e_["t1"], e_["t1"], e_["rs"], ALU.mult)
                                    K.tt("pool", mixC[:, t, :], e_["t1"], sgc[:, t, :], ALU.mult)
                                deferred.append(epi)
                        while deferred:
                            deferred.pop(0)()
                        for b in range(4):
                            bs = slice(b * 128, (b + 1) * 128)
                            py = pw[b % 2]
                            for half in range(2):
                                for f in range(8):
                                    K.mm(py[:, half, :], mixC[:, f, bs], wobf[:, f, half * 512:(half + 1) * 512], start=(f == 0), stop=(f == 7))
                            xo = x2[b % 2]
                            K.tt("dve", xo, py.re("p a c -> p (a c)"), xch[:, b, :], ALU.add)
                            K.act(junk, xo, AF.Square, accum=ss2[:, b % 2:b % 2 + 1])
                            K.act(ln2[:, b % 2:b % 2 + 1], ss2[:, b % 2:b % 2 + 1], AF.Ln, bias=epsb[:, 0:1], scale=1.0 / 1024.0)
                            K.act(r2[:, b % 2:b % 2 + 1], ln2[:, b % 2:b % 2 + 1], AF.Exp, scale=-0.5)
                            yb = yo[b % 2]
                            K.ts("dve", yb, xo, r2[:, b % 2:b % 2 + 1], ALU.mult)
                            K.tt("pool", yb, yb, fnbc, ALU.mult)
                            r0 = C * CH + b * 128
                            K.dma("pool", y_d[r0:r0 + 128, :], yb, yo_slot[b % 2])
    except StopBuild:
        pass
    for s_ in K.slots:
        if s_.cnt:
            nc.gpsimd.wait_ge(s_.sem, s_.cnt)
    return nc, K


def _t5_bucket(rel):
    half = 16
    max_exact = 8
    ret = (rel > 0).astype(np.int32) * half
    dist = np.abs(rel)
    large = max_exact + (np.log(np.maximum(dist, 1) / max_exact) / np.log(128 / max_exact) * (half - max_exact)).astype(np.int32)
    large = np.minimum(large, half - 1)
    return ret + np.where(dist < max_exact, dist, large)


def _static_tables():
    f32 = np.float32
    st = {}
    st["ident"] = np.eye(128, dtype=f32)
    st["aident"] = np.ascontiguousarray(np.eye(128, dtype=f32)[::-1])
    ob = np.zeros((128, 128), f32)
    ob[:64, :64] = 1
    ob[64:, 64:] = 1
    st["onesblk"] = ob
    j = np.arange(128)[:, None]
    i = np.arange(128)[None, :]
    mmat = np.zeros((128, 4, 128), f32)
    mmat[:, 0, :] = np.maximum(i - j, 0)
    mmat[:, 1, :] = (i >= j)
    mmat[:, 2, :] = np.maximum(j - i, 0)
    mmat[:, 3, :] = (j > i)
    st["mmat"] = mmat
    c = np.arange(512) % 128
    iot = np.zeros((128, 4, 512), f32)
    iot[:, 0, :] = c + 1
    iot[:, 1, :] = 128 - c
    iot[:, 2, :] = 127 - c
    iot[:, 3, :] = c
    st["iot"] = iot
    m = np.arange(640)
    rel = 255 - m
    bk = _t5_bucket(rel)
    oh = np.zeros((32, 640), f32)
    oh[bk, m] = 1
    st["oh"] = oh
    st["inwin"] = np.broadcast_to((np.abs(rel) <= 128).astype(f32)[None, :], (16, 640)).copy()
    return st


def _core_tables(is_prompt):
    f32 = np.float32
    seqlen = 4096 if is_prompt else 2048
    t = np.arange(NT) % seqlen
    d = np.arange(128) % 64
    pair = d // 2
    sgn = np.where(d % 2 == 0, -1.0, 1.0)
    quarter = 16
    freqs = (np.float32(10000.0) ** (-np.arange(quarter, dtype=f32) / quarter)).astype(f32)
    row = (t // 64).astype(f32)
    col = (t % 64).astype(f32)
    ang = np.concatenate([row[:, None] * freqs, col[:, None] * freqs], axis=-1).astype(f32)
    angd = ang[:, pair].T.astype(np.float64)
    tabA = np.stack([np.cos(angd), np.sin(angd) * sgn[:, None]]).astype(f32)
    half = 32
    freqs_b = (np.float32(10000.0) ** (-np.arange(half, dtype=f32) / half)).astype(f32)
    angb = (t.astype(f32)[:, None] * freqs_b).astype(f32)
    angbd = angb[:, pair].T.astype(np.float64)
    tabB = np.stack([np.cos(angbd), np.sin(angbd) * sgn[:, None]]).astype(f32)
    seq_of_blk = (np.arange(NB) * 128) // seqlen
    maskA = np.zeros((NCH, NB), f32)
    for C in range(NCH):
        sq = (C * CH) // seqlen
        maskA[C, :] = np.where(seq_of_blk == sq, 0.0, NEG)
    maskA = np.broadcast_to(maskA.reshape(1, -1), (128, NCH * NB)).copy()
    maskW = np.zeros((NB, 3), f32)
    for i in range(NB):
        for o in range(3):
            jb = i + o - 1
            if jb < 0 or jb >= NB or seq_of_blk[jb] != seq_of_blk[i]:
                maskW[i, o] = NEG
    maskW = np.broadcast_to(maskW.reshape(1, -1), (128, NB * 3)).copy()
    cps = seqlen // 128
    rf = np.array([0.0 if (n % cps == 0) else 1.0 for n in range(NB)], f32)
    rb = np.array([0.0 if (n % cps == cps - 1) else 1.0 for n in range(NB)], f32)
    rfb = np.broadcast_to(np.concatenate([rf, rb])[None, :], (128, 64)).copy()
    return {"tabA": tabA, "tabB": tabB, "maskA": maskA, "maskW": maskW, "rfb": rfb}


def _swap(cols):
    cols = np.asarray(cols)
    return cols ^ 1


def _prep_common(norm_g, w_in_ab, qk_norm_a, ret_decay, w_out_ab, w_in_c, sink_c, w_out_c, rel_bias, final_norm):
    f32 = np.float32
    W = np.asarray(w_in_ab[0], f32)
    qa = np.arange(0, 512)
    ka = np.arange(512, 640)
    va = np.arange(640, 768)
    ga = np.arange(768, 1280)
    qb = np.arange(1280, 1536)
    kb = np.arange(1536, 1792)
    vb = np.arange(1792, 2304)
    gb = np.arange(2304, 2816)
    kadup = np.concatenate([ka[0:64], ka[0:64], ka[64:128], ka[64:128]])
    cm = {}
    cm["wG"] = np.ascontiguousarray(W[:, np.concatenate([kadup, _swap(kadup), kb, _swap(kb), va, vb])])
    cm["wLB"] = np.ascontiguousarray(W[:, np.concatenate([qb, _swap(qb), gb])])
    cm["wLA"] = np.ascontiguousarray(W[:, np.concatenate([qa, _swap(qa), ga])])
    cm["woab"] = np.ascontiguousarray(np.asarray(w_out_ab[0], f32))
    Wc = np.asarray(w_in_c[0], f32)
    kc = np.arange(1024, 1152)
    kcdup = np.concatenate([kc[0:64], kc[0:64], kc[64:128], kc[64:128]])
    cm["wG1"] = np.ascontiguousarray(Wc[:, np.concatenate([kcdup, np.arange(1152, 1280)])])
    cm["wL1"] = np.ascontiguousarray(Wc[:, np.concatenate([np.arange(0, 1024), np.arange(1280, 2304)])])
    cm["woc"] = np.ascontiguousarray(np.asarray(w_out_c[0], f32))
    ng = np.asarray(norm_g, f32)
    cm["gcol"] = np.ascontiguousarray(ng.reshape(2, 8, 128).transpose(2, 0, 1).reshape(128, 16))
    cm["fn"] = np.asarray(final_norm, f32).reshape(1, 1024).copy()
    g = np.asarray(qk_norm_a[0], f32)
    d = np.arange(128) % 64
    cm["gqk"] = np.stack([g[0][d], g[0][d ^ 1], g[1][d], g[1][d ^ 1]], axis=1).astype(f32).copy()
    rd = np.asarray(ret_decay[0], f32)
    hp = (np.arange(128) // 64)
    rdec = np.zeros((128, 12), f32)
    for p in range(2):
        rdec[:, p] = rd[0][2 * p + hp]
        rdec[:, 2 + p] = rd[1][2 * p + hp]
    for h in range(4):
        rdec[:, 4 + h] = rd[0][h]
        rdec[:, 8 + h] = rd[1][h]
    cm["rdec"] = rdec
    sk = np.asarray(sink_c[0], f32)
    sinkl = np.zeros((128, 8), f32)
    for t in range(8):
        sinkl[:, t] = sk[2 * t + hp]
    cm["sinkl"] = sinkl
    cm["relb"] = np.ascontiguousarray(np.asarray(rel_bias, f32))
    cm.update(_static_tables())
    return cm


_CACHE = {}


def kernel(x_prompt, x_sample, norm_g, w_in_ab, qk_norm_a, ret_decay, w_out_ab, w_in_c, sink_c, w_out_c, rel_bias, final_norm):
    xp = np.asarray(x_prompt, np.float32)
    xs = np.asarray(x_sample, np.float32)
    cm = _prep_common(norm_g, w_in_ab, qk_norm_a, ret_decay, w_out_ab, w_in_c, sink_c, w_out_c, rel_bias, final_norm)
    tp = _core_tables(True)
    tsm = _core_tables(False)
    in_maps = []
    for c in range(8):
        m = dict(cm)
        if c < 4:
            m["x"] = np.ascontiguousarray(xp[c])
            m.update(tp)
        else:
            m["x"] = np.ascontiguousarray(xs[2 * (c - 4):2 * (c - 4) + 2].reshape(NT, 1024))
            m.update(tsm)
        in_maps.append(m)
    if "nc" not in _CACHE:
        _CACHE["nc"] = build_program()[0]
    nc = _CACHE["nc"]
    res = run_bass_kernel_spmd(nc, in_maps, core_ids=list(range(8)))
    outs = [np.asarray(r["y"], np.float32) for r in res.results]
    y_prompt = np.stack(outs[0:4], axis=0)
    y_sample = np.stack(outs[4:8], axis=0).reshape(8, 2048, 1024)
    return (y_prompt, y_sample)
```

```python
import numpy as np
import concourse.bass as bass
import concourse.mybir as mybir
from concourse.bass_utils import run_bass_kernel_spmd

F32 = mybir.dt.float32
BF16 = mybir.dt.bfloat16
AF = mybir.ActivationFunctionType
ALU = mybir.AluOpType

NT = 4096
NB = 32
CH = 512
NCH = 8
EPS = 1e-6
NEG = -30000.0


class Prod:
    def __init__(self, sem, inc):
        self.sem = sem
        self.inc = inc
        self.cnt = 0


class Res:
    def __init__(self):
        self.w = {}
        self.r = {}
        self.excl = False


class V:
    def __init__(self, ap, res=None):
        self.ap = ap
        self.res = res if res is not None else Res()

    def __getitem__(self, k):
        return V(self.ap[k], self.res)

    def re(self, pat, **kw):
        return V(self.ap.rearrange(pat, **kw), self.res)

    def bc(self, shape):
        return V(self.ap.to_broadcast(shape), self.res)


class Ker:
    def __init__(self, nc):
        self.nc = nc
        self.eng = {"pe": nc.tensor, "act": nc.scalar, "dve": nc.vector, "pool": nc.gpsimd, "sp": nc.sync}
        self.prod = {}
        for n in ("pe", "act", "dve", "pool"):
            self.prod[n] = Prod(nc.alloc_semaphore("s_" + n), 1)
        self.seen = {n: {} for n in self.eng}
        self.nslot = 0
        self.ninstr = 0

    def slot(self):
        self.nslot += 1
        p = Prod(self.nc.alloc_semaphore("d%d" % self.nslot), 16)
        if hasattr(self, "slots"):
            self.slots.append(p)
        return p

    def _wait(self, en, reads, writes):
        deps = {}
        for v in reads:
            for p, i in v.res.w.items():
                deps[p] = max(deps.get(p, 0), i)
        for v in writes:
            for p, i in v.res.w.items():
                deps[p] = max(deps.get(p, 0), i)
            for p, i in v.res.r.items():
                deps[p] = max(deps.get(p, 0), i)
        e = self.eng[en]
        seen = self.seen[en]
        own = self.prod.get(en)
        for p, i in deps.items():
            if p is own and en == "pe":
                continue
            if seen.get(p, 0) >= i:
                continue
            e.wait_ge(p.sem, i)
            seen[p] = i

    def op(self, en, fn, reads, writes):
        writes = list(writes) + [r for r in reads if r.res.excl]
        self._wait(en, reads, writes)
        ins = fn(self.eng[en])
        p = self.prod[en]
        if en == "pe" and getattr(self, "batching", False):
            idx = p.cnt + 1
            self.batch_last = ins
        else:
            p.cnt += 1
            idx = p.cnt
            ins.then_inc(p.sem, 1)
        for v in reads:
            v.res.r[p] = idx
        for v in writes:
            v.res.w[p] = idx
        self.ninstr += 1

    def pe_batch(self):
        ker = self

        class _B:
            def __enter__(self_):
                ker.batching = True
                ker.batch_last = None

            def __exit__(self_, *a):
                ker.batching = False
                if ker.batch_last is not None:
                    p = ker.prod["pe"]
                    p.cnt += 1
                    ker.batch_last.then_inc(p.sem, 1)
                    ker.batch_last = None
                return False
        return _B()

    def dma(self, q, out, in_, slot):
        self._wait(q, [in_], [out])
        ins = self.eng[q].dma_start(out=out.ap, in_=in_.ap)
        slot.cnt += 16
        ins.then_inc(slot.sem, 16)
        in_.res.r[slot] = slot.cnt
        out.res.w[slot] = slot.cnt

    def mm(self, out, lhsT, rhs, start=True, stop=True, tp=None):
        kw = {}
        if tp is not None:
            kw["tile_position"] = tp
        self.op("pe", lambda e: e.matmul(out.ap, lhsT.ap, rhs.ap, start=start, stop=stop, **kw), [lhsT, rhs], [out])

    def tr(self, out, in_, ident):
        self.op("pe", lambda e: e.transpose(out.ap, in_.ap, ident.ap), [in_, ident], [out])

    def act(self, out, in_, func, bias=None, scale=1.0, accum=None):
        reads = [in_]
        kw = {}
        if bias is not None:
            if isinstance(bias, V):
                reads.append(bias)
                kw["bias"] = bias.ap
            else:
                kw["bias"] = bias
        if isinstance(scale, V):
            reads.append(scale)
            kw["scale"] = scale.ap
        else:
            kw["scale"] = scale
        writes = [out]
        if accum is not None:
            writes.append(accum)
            kw["accum_out"] = accum.ap
        self.op("act", lambda e: e.activation(out.ap, in_.ap, func, **kw), reads, writes)

    def tt(self, en, out, a, b, op):
        self.op(en, lambda e: e.tensor_tensor(out.ap, a.ap, b.ap, op), [a, b], [out])

    def stt(self, en, out, in0, scalar, in1, op0, op1):
        reads = [in0, in1]
        s = scalar
        if isinstance(scalar, V):
            reads.append(scalar)
            s = scalar.ap
        self.op(en, lambda e: e.scalar_tensor_tensor(out.ap, in0.ap, s, in1.ap, op0, op1), reads, [out])

    def ts(self, en, out, in0, s1, op0, s2=None, op1=None):
        reads = [in0]
        a1 = s1
        if isinstance(s1, V):
            reads.append(s1)
            a1 = s1.ap
        a2 = s2
        if isinstance(s2, V):
            reads.append(s2)
            a2 = s2.ap
        if op1 is None:
            self.op(en, lambda e: e.tensor_scalar(out.ap, in0.ap, a1, None, op0), reads, [out])
        else:
            self.op(en, lambda e: e.tensor_scalar(out.ap, in0.ap, a1, a2, op0, op1), reads, [out])

    def cp(self, en, out, in_):
        if en == "act":
            self.op("act", lambda e: e.copy(out.ap, in_.ap), [in_], [out])
        else:
            self.op(en, lambda e: e.tensor_copy(out.ap, in_.ap), [in_], [out])

    def amul(self, out, in_, m):
        self.op("act", lambda e: e.mul(out.ap, in_.ap, m.ap), [in_, m], [out])

    def recip(self, out, in_):
        self.op("dve", lambda e: e.reciprocal(out.ap, in_.ap), [in_], [out])

    def memset(self, en, out, val):
        self.op(en, lambda e: e.memset(out.ap, val), [], [out])


class StopBuild(Exception):
    pass


import contextlib


class Scope(contextlib.ExitStack):
    def __init__(self, K):
        super().__init__()
        self.K = K
        self.tiles = []

    def __exit__(self, *a):
        fr = self.K.freed
        for v in self.tiles:
            for d in (v.res.w, v.res.r):
                for p, i in d.items():
                    fr[p] = max(fr.get(p, 0), i)
        self.tiles = []
        return super().__exit__(*a)

    def close(self):
        self.__exit__(None, None, None)


def build_program(stop=None, taps=()):
    nc = bass.Bass("TRN2", target_bir_lowering=False)
    K = Ker(nc)
    K.slots = []
    K.freed = {}
    K.tapped = {}

    def checkpoint(name):
        if stop == name:
            raise StopBuild()

    def tap(name, v, shape, dt=F32):
        if name not in taps or name in K.tapped:
            return
        d = V(nc.dram_tensor("dbg_" + name, list(shape), dt, kind="ExternalOutput").ap())
        K.tapped[name] = d
        K.dma("sp", d, v, K.slot())

    def din(name, shape, dt=F32):
        return V(nc.dram_tensor(name, list(shape), dt, kind="ExternalInput").ap())

    x_d = din("x", [NT, 1024])
    wG_d = din("wG", [1024, 1664])
    wLB_d = din("wLB", [1024, 1024])
    wLA_d = din("wLA", [1024, 1536])
    woab_d = din("woab", [1024, 1024])
    wG1_d = din("wG1", [1024, 384])
    wL1_d = din("wL1", [1024, 2048])
    woc_d = din("woc", [1024, 1024])
    gcol_d = din("gcol", [128, 16])
    fn_d = din("fn", [1, 1024])
    gqk_d = din("gqk", [128, 4])
    rdec_d = din("rdec", [128, 12])
    sink_d = din("sinkl", [128, 8])
    relb_d = din("relb", [32, 16])
    ident_d = din("ident", [128, 128])
    onesblk_d = din("onesblk", [128, 128])
    mm_d = din("mmat", [128, 4, 128])
    iot_d = din("iot", [128, 4, 512])
    oh_d = din("oh", [32, 640])
    inwin_d = din("inwin", [16, 640])
    tabA_d = din("tabA", [2, 128, NT])
    tabB_d = din("tabB", [2, 128, NT])
    maskA_d = din("maskA", [128, 256])
    maskW_d = din("maskW", [128, 96])
    rfb_d = din("rfb", [128, 64])
    y_d = V(nc.dram_tensor("y", [NT, 1024], F32, kind="ExternalOutput").ap())
    x1_d = V(nc.dram_tensor("x1s", [NT, 1024], F32, kind="Internal").ap())
    mixb_d = V(nc.dram_tensor("mixbs", [4, 128, NT], BF16, kind="Internal").ap())
    vec_h = nc.dram_tensor("vecs", [16, 640], BF16, kind="Internal")
    vec_d = V(vec_h.ap())
    aident_d = din("aident", [128, 128])
    xnT0_d = V(nc.dram_tensor("xnT0s", [NCH, 128, 8, CH], BF16, kind="Internal").ap())
    xnT1_d = V(nc.dram_tensor("xnT1s", [NCH, 128, 8, CH], BF16, kind="Internal").ap())
    kr_d = V(nc.dram_tensor("krs", [NCH, 128, 2, CH], F32, kind="Internal").ap())
    vb_d = V(nc.dram_tensor("vbs", [NCH, 128, 4, 512], BF16, kind="Internal").ap())

    es = Scope(K)
    uid = [0]

    def sb(name, shape, dt=F32, stack=None):
        uid[0] += 1
        st_ = stack if stack is not None else es
        t = st_.enter_context(nc.sbuf_tensor("sb%d_%s" % (uid[0], name), list(shape), dt))
        v = V(t[:])
        v.res.w = dict(K.freed)
        st_.tiles.append(v)
        return v

    def ps(name, shape, dt=F32, stack=None):
        uid[0] += 1
        st_ = stack if stack is not None else es
        t = st_.enter_context(nc.psum_tensor("ps%d_%s" % (uid[0], name), list(shape), dt))
        v = V(t[:])
        v.res.excl = True
        v.res.w = dict(K.freed)
        st_.tiles.append(v)
        return v

    try:
        with es:
            cslot = K.slot()
            consts = []

            def cload(name, src, shape, dt=F32, q="sp"):
                t = sb(name, shape, dt)
                K.dma(q, t, src, cslot)
                consts.append(t)
                return t

            gcol = cload("gcol", gcol_d, [128, 16])
            gqk = cload("gqk", gqk_d, [128, 4])
            rdec = cload("rdec", rdec_d, [128, 12])
            sinkl = cload("sinkl", sink_d, [128, 8])
            maskA = cload("maskA", maskA_d, [128, 256])
            maskW = cload("maskW", maskW_d, [128, 96])
            rfb = cload("rfb", rfb_d, [128, 64])
            ident32 = cload("ident32", ident_d, [128, 128])
            onesblk32 = cload("onesblk32", onesblk_d, [128, 128])
            for c in consts:
                c.res.w[cslot] = cslot.cnt
            ident = sb("ident", [128, 128], BF16)
            onesblk = sb("onesblk", [128, 128], BF16)
            ones = sb("ones", [128, 128], BF16)
            epsb = sb("epsb", [128, 1])
            K.cp("dve", ident, ident32)
            K.cp("dve", onesblk, onesblk32)
            K.memset("dve", ones, 1.0)
            K.memset("dve", epsb, EPS)

            checkpoint("c0")
            wbf = sb("wbf", [128, 8, 2048], BF16)
            wobf = sb("wobf", [128, 8, 1024], BF16)
            wst = [sb("wst%d" % i, [128, 1024]) for i in range(2)]
            wst_slot = [K.slot() for _ in range(2)]
            xnTs = [sb("xnT%d" % i, [128, 8, CH], BF16) for i in range(2)]
            xnT_slot = [K.slot() for _ in range(2)]
            cur = {"xnT": xnTs[0]}
            xch_slot = [K.slot() for _ in range(4)]
            xst_slot = [K.slot() for _ in range(2)]
            EBT = sb("EBT", [128, 16, 3, 128], BF16)
            esk = sb("esk", [128, 8], F32)
            K.act(esk, sinkl, AF.Exp)
            with Scope(K) as S1:
                relb = sb("relb", [32, 16], F32, S1)
                oh = sb("oh", [32, 640], F32, S1)
                inw = sb("inw", [16, 640], F32, S1)
                e_slot = K.slot()
                K.dma("sp", relb, relb_d, e_slot)
                K.dma("sp", oh, oh_d, e_slot)
                K.dma("sp", inw, inwin_d, e_slot)
                for t_ in (relb, oh, inw):
                    t_.res.w[e_slot] = e_slot.cnt
                pv = ps("pv", [16, 1024], F32, S1)[:, 0:640]
                vec = sb("vec", [16, 640], F32, S1)
                vecb = sb("vecb", [16, 640], BF16, S1)
                K.mm(pv[:, 0:512], relb, oh[:, 0:512])
                K.mm(pv[:, 512:640], relb, oh[:, 512:640])
                K.act(vec, pv, AF.Exp)
                K.tt("dve", vecb, vec, inw, ALU.mult)
                v_slot = K.slot()
                K.dma("sp", vec_d, vecb, v_slot)
                g_slot = K.slot()
                aid32 = sb("aid32", [128, 128], F32, S1)
                K.dma("sp", aid32, aident_d, g_slot)
                aid = sb("aid", [128, 128], BF16, S1)
                K.cp("dve", aid, aid32)
                TT = sb("TT", [128, 16 * 384], BF16, S1)
                src = V(bass.AP(vec_h, 0, [[1, 128], [640, 16], [1, 384]]), vec_d.res)
                K.dma("sp", TT.re("p (h j) -> p h j", h=16), src, g_slot)
                prev = ps("prev", [128, 2, CH], F32, S1)
                EBTf = EBT.re("p h o q -> p (h o q)")
                for n_ in range(12):
                    K.mm(prev[:, n_ % 2, :], aid, TT[:, n_ * 512:(n_ + 1) * 512])
                    K.cp("act" if n_ % 2 == 0 else "dve", EBTf[:, n_ * 512:(n_ + 1) * 512], prev[:, n_ % 2, :])
            wcount = [0]

            def handoff(srcs, dsts):
                for d_ in dsts:
                    for s_ in srcs:
                        for dd in (s_.res.w, s_.res.r):
                            for p_, i_ in dd.items():
                                d_.res.w[p_] = max(d_.res.w.get(p_, 0), i_)

            def subview(parent, ap):
                v = V(ap)
                v.res.excl = parent.res.excl
                v.res.w = dict(parent.res.w)
                return v

            class MX:
                pass

            def alloc_mx(scope, full=True):
                m = MX()
                m.xch = sb("xch", [128, 4, 1024], F32, scope)
                if full:
                    m.xn = [sb("xn%d" % i, [128, 1024], BF16, scope) for i in range(2)]
                    m.junk = sb("junk", [128, 1024], BF16, scope)
                    m.ss = sb("ss", [128, 4], F32, scope)
                    m.lnv4 = sb("lnv4", [128, 4], F32, scope)
                    m.rstd4 = sb("rstd4", [128, 4], F32, scope)
                return m

            def load_x(m, src_d, C):
                for b in range(4):
                    r0 = C * CH + b * 128
                    K.dma("sp", m.xch[:, b, :], src_d[r0:r0 + 128, :], xch_slot[b])

            def store_xnT(dst_d, C, slot_i):
                K.dma("pool", dst_d[C], cur["xnT"], xst_slot[slot_i])

            def load_xnT(src_d, C):
                i = C % 2
                K.dma("sp", xnTs[i], src_d[C], xnT_slot[i])

            def use_xnT(C):
                cur["xnT"] = xnTs[C % 2]

            def load_w_piece(dst, d0, src_d, kc, c0, c1, layer_g):
                i = wcount[0] % 2
                wcount[0] += 1
                n_ = c1 - c0
                K.dma("sp", wst[i][:, 0:n_], src_d[kc * 128:(kc + 1) * 128, c0:c1], wst_slot[i])
                en = "act" if (wcount[0] % 2 == 0) else "dve"
                if layer_g is None:
                    K.cp(en, dst[:, kc, d0:d0 + n_], wst[i][:, 0:n_])
                elif en == "act":
                    K.amul(dst[:, kc, d0:d0 + n_], wst[i][:, 0:n_], gcol[:, layer_g * 8 + kc:layer_g * 8 + kc + 1])
                else:
                    K.ts("dve", dst[:, kc, d0:d0 + n_], wst[i][:, 0:n_], gcol[:, layer_g * 8 + kc:layer_g * 8 + kc + 1], ALU.mult)

            def load_w(dst, src_d, ncols, layer_g):
                for kc in range(8):
                    for c0 in range(0, ncols, 1024):
                        c1 = min(ncols, c0 + 1024)
                        load_w_piece(dst, c0, src_d, kc, c0, c1, layer_g)

            wlo = V(wbf.ap[:, :, 0:1536])
            whi = V(wbf.ap[:, :, 1536:2048])
            wmode = {"split": False, "base": 0}

            def wcol(kc, c0, n_=128):
                c0 = c0 + wmode["base"]
                if not wmode["split"]:
                    return wbf[:, kc, c0:c0 + n_]
                if c0 + n_ <= 1536:
                    return wlo[:, kc, c0:c0 + n_]
                return whi[:, kc, c0 - 1536:c0 - 1536 + n_]

            def xnT_front(m, src_d, C):
                load_x(m, src_d, C)
                for b in range(4):
                    K.act(m.junk, m.xch[:, b, :], AF.Square, accum=m.ss[:, b:b + 1])
                K.act(m.lnv4, m.ss, AF.Ln, bias=epsb[:, 0:1], scale=1.0 / 1024.0)
                K.act(m.rstd4, m.lnv4, AF.Exp, scale=-0.5)

            def xnT_back(m, C, pTl):
                xnT = xnTs[C % 2]
                for b in range(4):
                    xb = m.xn[b % 2]
                    pT_ = pTl[b % len(pTl)]
                    K.ts("dve", xb, m.xch[:, b, :], m.rstd4[:, b:b + 1], ALU.mult)
                    for kc in range(8):
                        K.tr(pT_[:, kc, :], xb[:, kc * 128:(kc + 1) * 128], ident)
                    K.cp("act" if b % 2 == 0 else "dve", xnT[:, :, b * 128:(b + 1) * 128], pT_)

            def make_xnT(m, src_d, C, pT):
                use_xnT(C)
                xnT_front(m, src_d, C)
                xnT_back(m, C, pT if isinstance(pT, list) else [pT])

            def proj(dst, c0):
                with K.pe_batch():
                    for kc in range(8):
                        K.mm(dst, wcol(kc, c0), cur["xnT"][:, kc, :], start=(kc == 0), stop=(kc == 7))

            def rsq_bcast(dst, src_ps, nfeat, sq, psn, lnv, lhs_ones):
                K.act(sq, src_ps, AF.Square)
                K.mm(psn, lhs_ones, sq)
                K.act(lnv, psn, AF.Ln, bias=epsb[:, 0:1], scale=1.0 / nfeat)
                K.act(dst, lnv, AF.Exp, scale=-0.5)

            with Scope(K) as L0:
                LR = Scope(K)
                KaT = sb("KaT", [128, 2, NT], BF16, L0)
                Va = sb("Va", [128, NB, 128], BF16, L0)
                tabc = sb("tabc", [128, 2, CH], F32, L0)
                tab_slot = K.slot()
                tabd_slot = K.slot()
                sq = sb("sq", [128, CH], BF16, L0)
                lnv = sb("lnv", [128, CH], F32, L0)
                rs = sb("rs", [128, CH], F32, L0)
                t1 = sb("t1", [128, CH], F32, L0)
                t2 = sb("t2", [128, CH], F32, L0)
                tabg = sb("tabg", [128, 2, CH], F32, L0)
                SbAll = sb("SbAll", [128, 2, NB, 128], BF16, LR)
                tabd = sb("tabd", [128, 2, CH], F32, LR)
                vbtm = sb("vbtm", [128, 4, 512], BF16, LR)
                lg = sb("lg", [128, 12], F32, LR)
                K.act(lg, rdec, AF.Exp)
                K.ts("dve", lg, lg, -1.0, ALU.mult)
                cd = sb("cd", [128, 4], F32, LR)
                K.act(cd, lg[:, 0:4], AF.Exp, scale=128.0)
                cdr = sb("cdr", [128, 4, NB], F32, LR)
                for j in range(4):
                    off = 0 if j < 2 else 32
                    K.ts("dve", cdr[:, j, :], rfb[:, off:off + 32], cd[:, j:j + 1], ALU.mult)
                checkpoint("c1")
                QF4 = sb("QF4", [128, 2, CH], F32, LR)
                QB4 = sb("QB4", [128, 2, CH], F32, LR)
                KF4 = sb("KF4", [128, 2, CH], F32, LR)
                KB4 = sb("KB4", [128, 2, CH], F32, LR)
                DT = sb("DT", [128, 4, 128], F32, LR)
                with Scope(K) as S0:
                    iot = sb("iot", [128, 4, CH], F32, S0)
                    K.dma("sp", iot, iot_d, tabd_slot)
                    for p in range(2):
                        K.act(QF4[:, p, :], iot[:, 0, :], AF.Exp, scale=lg[:, p:p + 1])
                        K.act(QB4[:, p, :], iot[:, 1, :], AF.Exp, scale=lg[:, 2 + p:3 + p])
                        K.act(KF4[:, p, :], iot[:, 2, :], AF.Exp, scale=lg[:, p:p + 1])
                        K.act(KB4[:, p, :], iot[:, 3, :], AF.Exp, scale=lg[:, 2 + p:3 + p])
                    K.ts("dve", KF4, KF4, 0.125, ALU.mult)
                    K.ts("dve", KB4, KB4, 0.125, ALU.mult)
                    mmat = sb("mmat", [128, 4, 128], F32, S0)
                    K.dma("sp", mmat, mm_d, tab_slot)
                    d1 = sb("d1", [128, 128], F32, S0)
                    d2 = sb("d2", [128, 128], F32, S0)
                    for h in range(4):
                        checkpoint("d0")
                        K.act(d1, mmat[:, 0, :], AF.Exp, scale=lg[:, 4 + h:5 + h])
                        checkpoint("d1")
                        K.tt("dve", d1, d1, mmat[:, 1, :], ALU.mult)
                        checkpoint("d2")
                        K.act(d2, mmat[:, 2, :], AF.Exp, scale=lg[:, 8 + h:9 + h])
                        K.tt("dve", d2, d2, mmat[:, 3, :], ALU.mult)
                        K.tt("dve", d1, d1, d2, ALU.add)
                        checkpoint("d3")
                        K.ts("dve", DT[:, h, :], d1, 0.125, ALU.mult)
                        checkpoint("d4")

                def load_tab(dst, slot, src_d, C):
                    K.dma("sp", dst, V(src_d.ap[:, :, C * CH:(C + 1) * CH].rearrange("t p c -> p t c"), src_d.res), slot)

                def rope(psa, psb, tab, out32, ga=None, gb=None):
                    if ga is None:
                        K.tt("dve", t1, psa, tab[:, 0, :], ALU.mult)
                        K.tt("dve", t2, psb, tab[:, 1, :], ALU.mult)
                    else:
                        K.amul(tabg[:, 0, :], tab[:, 0, :], ga)
                        K.amul(tabg[:, 1, :], tab[:, 1, :], gb)
                        K.tt("dve", t1, psa, tabg[:, 0, :], ALU.mult)
                        K.tt("dve", t2, psb, tabg[:, 1, :], ALU.mult)
                    K.tt("pool", out32, t1, t2, ALU.add)

                checkpoint("setup0")
                load_w(wbf, wG_d, 1664, 0)
                with Scope(K) as PG:
                    pT = ps("pT", [128, 8, 128], BF16, PG)
                    pk = pT.re("p (c t) q -> p c t q", t=2)
                    pbig = ps("pbigG", [128, 7, CH], F32, PG)
                    bk = [subview(pbig, pbig.ap[:, i, :]) for i in range(7)]
                    pn = bk[4]
                    pT2g = V(bk[5].ap.bitcast(BF16).rearrange("p (k q) -> p k q", k=8), bk[5].res)
                    pkv = V(bk[6].ap[:, 0:256].rearrange("p (a b) -> p a b", a=2), bk[6].res)
                    kdbT = sb("kdbT", [128, 2, CH], BF16, PG)
                    kdbtm = sb("kdbtm", [128, 4, 2, 128], BF16, PG)
                    Rb = sb("Rb", [128, 2, 128], F32, PG)
                    mxg = alloc_mx(PG)
                    WS = [dict(sq=sq, lnv=lnv, rs=rs, t1=t1, t2=t2),
                          dict(sq=sb("wsq", [128, CH], BF16, PG), lnv=sb("wlnv", [128, CH], F32, PG),
                               rs=sb("wrs", [128, CH], F32, PG), t1=sb("wt1", [128, CH], F32, PG),
                               t2=sb("wt2", [128, CH], F32, PG))]
                    K.memset("dve", Rb, 0.0)
                    kr_slot = [K.slot() for _ in range(2)]
                    vb_slot = K.slot()
                    for C in range(NCH - 1, -1, -1):
                        make_xnT(mxg, x_d, C, [pT, pT2g])
                        store_xnT(xnT0_d, C, C % 2)
                        load_w_piece(wobf, 0, woab_d, C, 0, 1024, None)
                        load_tab(tabc, tab_slot, tabA_d, C)
                        load_tab(tabd, tabd_slot, tabB_d, C)
                        K.amul(tabg[:, 0, :], tabc[:, 0, :], gqk[:, 2:3])
                        K.amul(tabg[:, 1, :], tabc[:, 1, :], gqk[:, 3:4])
                        for t in range(2):
                            w_ = WS[t % 2]
                            pa_, pb_ = bk[2 * t], bk[2 * t + 1]
                            proj(pa_, t * 128)
                            proj(pb_, 256 + t * 128)
                            rsq_bcast(w_["rs"], pa_, 64.0, w_["sq"], pn, w_["lnv"], onesblk)
                            K.tt("dve", w_["t1"], pa_, tabg[:, 0, :], ALU.mult)
                            K.tt("dve", w_["t2"], pb_, tabg[:, 1, :], ALU.mult)
                            K.tt("pool", w_["t1"], w_["t1"], w_["t2"], ALU.add)
                            K.tt("pool", KaT[:, t, C * CH:(C + 1) * CH], w_["t1"], w_["rs"], ALU.mult)
                        for t in range(2):
                            w_ = WS[t % 2]
                            pa_, pb_ = bk[2 * t], bk[2 * t + 1]
                            proj(pa_, 512 + t * 128)
                            proj(pb_, 768 + t * 128)
                            K.tt("dve", w_["t1"], pa_, tabd[:, 0, :], ALU.mult)
                            K.tt("dve", w_["t2"], pb_, tabd[:, 1, :], ALU.mult)
                            K.tt("pool", w_["t1"], w_["t1"], w_["t2"], ALU.add)
                            K.dma("pool", kr_d[C][:, t, :], w_["t1"], kr_slot[t % 2])
                            K.tt("pool", kdbT[:, t, :], w_["t1"], KB4[:, t, :], ALU.mult)
                        for b in range(4):
                            pva = (bk[4] if b % 2 == 0 else bk[2])[:, 0:128]
                            pvb = bk[5] if b % 2 == 0 else bk[3]
                            for kc in range(8):
                                K.mm(pva, cur["xnT"][:, kc, b * 128:(b + 1) * 128], wbf[:, kc, 1024:1152], start=(kc == 0), stop=(kc == 7))
                            for kc in range(8):
                                K.mm(pvb, cur["xnT"][:, kc, b * 128:(b + 1) * 128], wbf[:, kc, 1152:1664], start=(kc == 0), stop=(kc == 7))
                            K.cp("act", Va[:, C * 4 + b, :], pva)
                            K.cp("dve", vbtm[:, b, :], pvb)
                        K.dma("pool", vb_d[C], vbtm, vb_slot)
                        for t in range(2):
                            for cj in range(4):
                                K.tr(pk[:, cj, t, :], kdbT[:, t, cj * 128:(cj + 1) * 128], ident)
                        K.cp("act", kdbtm, pk)
                        for cj in range(3, -1, -1):
                            n = C * 4 + cj
                            for p in range(2):
                                K.mm(pkv[0:64, p, :], kdbtm[:, cj, p, 0:64], vbtm[:, cj, (2 * p) * 128:(2 * p + 1) * 128])
                                K.mm(pkv[64:128, p, :], kdbtm[:, cj, p, 64:128], vbtm[:, cj, (2 * p + 1) * 128:(2 * p + 2) * 128], tp=(0, 64))
                            K.ts("dve", SbAll[:, :, n, :], Rb, rfb[:, 32 + n:33 + n], ALU.mult)
                            for p in range(2):
                                K.ts("dve", Rb[:, p, :], Rb[:, p, :], cdr[:, 2 + p, n:n + 1], ALU.mult)
                                K.tt("dve", Rb[:, p, :], pkv[:, p, :], Rb[:, p, :], ALU.add)

                tap("KaT", KaT, [128, 2, NT], BF16)
                tap("Va", Va, [128, NB, 128], BF16)
                tap("SbAll", SbAll, [128, 2, NB, 128], BF16)
                checkpoint("G")
                load_w(wbf, wLB_d, 1024, 0)
                with Scope(K) as PB:
                    pT = ps("pT", [128, 8, 128], BF16, PB)
                    pk = pT.re("p (c t) q -> p c t q", t=2)
                    pbig = ps("pbigB", [128, 7, CH], F32, PB)
                    bk = [subview(pbig, pbig.ap[:, i, :]) for i in range(7)]
                    pa, pb, pss = bk[0], bk[1], bk[2]
                    po = subview(pbig, pbig.ap[:, 3:7, :])
                    qrT = sb("qrT", [128, 2, CH], BF16, PB)
                    qdf = sb("qdf", [128, 2, CH], BF16, PB)
                    qdb = sb("qdb", [128, 2, CH], BF16, PB)
                    krT = sb("krT", [128, 2, CH], BF16, PB)
                    kdfT = sb("kdfT", [128, 2, CH], BF16, PB)
                    kdftm = sb("kdftm", [128, 4, 2, 128], BF16, PB)
                    sg = sb("sg", [128, 4, CH], BF16, PB)
                    ATs = [sb("AT%d" % i, [128, 4, 128], BF16, PB) for i in range(2)]
                    Sfs = [sb("Sf%d" % i, [128, 2, 128], BF16, PB) for i in range(2)]
                    Rf = sb("Rf", [128, 2, 128], F32, PB)
                    mixBc = [sb("mixBc%d" % i, [128, 4, CH], BF16, PB) for i in range(1)]
                    mixB_slot = [K.slot() for _ in range(1)]
                    WS = [dict(sq=sq, lnv=lnv, rs=rs, t1=t1, t2=t2),
                          dict(sq=sb("wsq", [128, CH], BF16, PB), lnv=sb("wlnv", [128, CH], F32, PB),
                               rs=sb("wrs", [128, CH], F32, PB), t1=sb("wt1", [128, CH], F32, PB),
                               t2=sb("wt2", [128, CH], F32, PB))]
                    K.memset("dve", Rf, 0.0)
                    krl_slot = [K.slot() for _ in range(2)]
                    vbl_slot = K.slot()
                    load_xnT(xnT0_d, 0)
                    pairs = [(bk[0], bk[1]), (bk[3], bk[4]), (bk[5], bk[6])]
                    for C in range(NCH):
                        use_xnT(C)
                        if C + 1 < NCH:
                            load_xnT(xnT0_d, C + 1)
                        load_tab(tabd, tabd_slot, tabB_d, C)
                        handoff([po], bk[3:7])
                        ip = 0
                        for t in range(2):
                            w_ = WS[ip % 2]
                            pa_, pb_ = pairs[ip % 3]
                            ip += 1
                            proj(pa_, t * 128)
                            proj(pb_, 256 + t * 128)
                            K.tt("dve", w_["t1"], pa_, tabd[:, 0, :], ALU.mult)
                            K.tt("dve", w_["t2"], pb_, tabd[:, 1, :], ALU.mult)
                            K.tt("pool", w_["t1"], w_["t1"], w_["t2"], ALU.add)
                            K.cp("act", qrT[:, t, :], w_["t1"])
                            K.tt("pool", qdf[:, t, :], w_["t1"], QF4[:, t, :], ALU.mult)
                            K.tt("pool", qdb[:, t, :], w_["t1"], QB4[:, t, :], ALU.mult)
                        K.dma("sp", vbtm, vb_d[C], vbl_slot)
                        for t in range(2):
                            kr_t = WS[t]["t2"]
                            K.dma("sp", kr_t, kr_d[C][:, t, :], krl_slot[t])
                            K.cp("act", krT[:, t, :], kr_t)
                            K.tt("pool", kdfT[:, t, :], kr_t, KF4[:, t, :], ALU.mult)
                        for h in range(4):
                            pa_ = bk[3 + h]
                            proj(pa_, 512 + h * 128)
                            K.act(sg[:, h, :], pa_, AF.Silu)
                        for t in range(2):
                            for cj in range(4):
                                K.tr(pk[:, cj, t, :], kdfT[:, t, cj * 128:(cj + 1) * 128], ident)
                        K.cp("act", kdftm, pk)
                        handoff(bk[3:7], [po])
                        for cj in range(4):
                            n = C * 4 + cj
                            cs = slice(cj * 128, (cj + 1) * 128)
                            Sf = Sfs[cj % 2]
                            AT = ATs[cj % 2]
                            K.ts("dve", Sf, Rf, rfb[:, n:n + 1], ALU.mult)
                            for p in range(2):
                                K.mm(pa[0:64, p * 128:(p + 1) * 128], kdftm[:, cj, p, 0:64], vbtm[:, cj, (2 * p) * 128:(2 * p + 1) * 128])
                                K.mm(pa[64:128, p * 128:(p + 1) * 128], kdftm[:, cj, p, 64:128], vbtm[:, cj, (2 * p + 1) * 128:(2 * p + 2) * 128], tp=(0, 64))
                            for p in range(2):
                                K.ts("dve", Rf[:, p, :], Rf[:, p, :], cdr[:, p, n:n + 1], ALU.mult)
                                K.tt("dve", Rf[:, p, :], pa[:, p * 128:(p + 1) * 128], Rf[:, p, :], ALU.add)
                            for h in range(4):
                                t, r0 = h // 2, (h % 2) * 64
                                pdst = pss if (h % 2 == 0) else pb
                                K.mm(pdst[:, t * 128:(t + 1) * 128], krT[r0:r0 + 64, t, cs], qrT[r0:r0 + 64, t, cs])
                            ATv = AT.re("p (t hp) i -> p hp t i", hp=2)
                            DTv = DT.re("p (t hp) i -> p hp t i", hp=2)
                            K.tt("dve", ATv[:, 0, :, :], pss[:, 0:256].re("p (t i) -> p t i", t=2), DTv[:, 0, :, :], ALU.mult)
                            K.tt("dve", ATv[:, 1, :, :], pb[:, 0:256].re("p (t i) -> p t i", t=2), DTv[:, 1, :, :], ALU.mult)
                            for h in range(4):
                                t, r0 = h // 2, (h % 2) * 64
                                K.mm(po[:, h, cs], vbtm[:, cj, h * 128:(h + 1) * 128], AT[:, h, :], start=True, stop=False)
                                K.mm(po[:, h, cs], Sf[r0:r0 + 64, t, :], qdf[r0:r0 + 64, t, cs], start=False, stop=False)
                                K.mm(po[:, h, cs], SbAll[r0:r0 + 64, t, n, :], qdb[r0:r0 + 64, t, cs], start=False, stop=True)
                        mb = mixBc[0]
                        for h0 in (0, 2):
                            hs = (h0, h0 + 1)
                            for h in hs:
                                K.act(WS[h % 2]["sq"], po[:, h, :], AF.Square)
                            for h in hs:
                                K.mm(bk[h % 2], ones, WS[h % 2]["sq"])
                            for h in hs:
                                K.act(WS[h % 2]["lnv"], bk[h % 2], AF.Ln, bias=epsb[:, 0:1], scale=1.0 / 128.0)
                            for h in hs:
                                K.act(WS[h % 2]["rs"], WS[h % 2]["lnv"], AF.Exp, scale=-0.5)
                            for h in hs:
                                K.tt("dve", WS[h % 2]["t1"], po[:, h, :], WS[h % 2]["rs"], ALU.mult)
                                K.tt("pool", mb[:, h, :], WS[h % 2]["t1"], sg[:, h, :], ALU.mult)
                        K.dma("pool", V(mixb_d.ap[:, :, C * CH:(C + 1) * CH].rearrange("h p c -> p h c"), mixb_d.res), mb, mixB_slot[0])

                tap("mixb", mixb_d, [4, 128, NT], BF16)
                checkpoint("LB")
                LR.close()
                load_w(wbf, wLA_d, 1536, 0)
                handoff([wbf], [wlo, whi])
                wmode["split"] = True
                with Scope(K) as PA:
                    pbig = ps("pbig", [128, 8, CH], F32, PA)
                    psc = [subview(pbig, pbig.ap[:, 2 * i:2 * i + 2, :]) for i in range(3)]
                    pnum = subview(pbig, pbig.ap[:, 6, :])
                    pden = subview(pbig, pbig.ap[:, 7, :])
                    bk = [subview(pbig, pbig.ap[:, i, :]) for i in range(6)] + [pnum, pden]
                    WS = [dict(sq=sq, lnv=lnv, rs=rs, t1=t1, t2=t2),
                          dict(sq=sb("wsq", [128, CH], BF16, PA), lnv=sb("wlnv", [128, CH], F32, PA),
                               rs=sb("wrs", [128, CH], F32, PA), t1=sb("wt1", [128, CH], F32, PA),
                               t2=sb("wt2", [128, CH], F32, PA))]
                    qaT = sb("qaT", [128, 4, CH], BF16, PA)
                    sga = sb("sga", [128, 4, CH], BF16, PA)
                    mixAs = [sb("mixA%d" % i, [128, 4, CH], BF16, PA) for i in range(2)]
                    mixBls = [sb("mixBl%d" % i, [128, 4, CH], BF16, PA) for i in range(2)]
                    mixBl_slots = [K.slot() for _ in range(2)]
                    xblk = [sb("xblk%d" % i, [128, 1024], F32, PA) for i in range(2)]
                    xblk_slot = [K.slot() for _ in range(2)]
                    nxb = [0]
                    pTs = [sb("pTs%d" % i, [128, 2, CH], BF16, PA) for i in range(3)]
                    x1b = [sb("x1b%d" % i, [128, 1024], F32, PA) for i in range(2)]
                    dcp = sb("dcp", [128, CH], F32, PA)
                    ncp = sb("ncp", [128, CH], F32, PA)
                    x1b_slot = [K.slot() for _ in range(2)]
                    qaTs = [qaT, sb("qaT1", [128, 4, CH], BF16, PA)]
                    sgas = [sga, sb("sga1", [128, 4, CH], BF16, PA)]
                    tabcs = [tabc, sb("tabc1", [128, 2, CH], F32, PA)]
                    tabgs = [tabg, sb("tabg1", [128, 2, CH], F32, PA)]
                    tabsl = [tab_slot, K.slot()]
                    nbuf = [0]
                    npt = [0]

                    held = set()

                    def take_buf():
                        while True:
                            i_ = nbuf[0] % 3
                            nbuf[0] += 1
                            if i_ not in held:
                                return psc[i_]

                    def hold(b_):
                        held.add(psc.index(b_))

                    def release(b_):
                        held.discard(psc.index(b_))

                    def projx(dst, c0, xT):
                        with K.pe_batch():
                            for kc in range(8):
                                K.mm(dst, wcol(kc, c0), xT[:, kc, :], start=(kc == 0), stop=(kc == 7))

                    def proj_items(Cn):
                        q_, g_ = qaTs[Cn % 2], sgas[Cn % 2]
                        tc_, tg_ = tabcs[Cn % 2], tabgs[Cn % 2]
                        xT = xnTs[Cn % 2]

                        def prep():
                            load_tab(tc_, tabsl[Cn % 2], tabA_d, Cn)
                            K.amul(tg_[:, 0, :], tc_[:, 0, :], gqk[:, 0:1])
                            K.amul(tg_[:, 1, :], tc_[:, 1, :], gqk[:, 1:2])

                        items = []
                        for t in range(4):
                            def mk(t=t):
                                st = {}
                                w_ = WS[t % 2]

                                def s1():
                                    st["buf"] = take_buf()
                                    hold(st["buf"])
                                    projx(st["buf"][:, 0, :], t * 128, xT)
                                    projx(st["buf"][:, 1, :], 512 + t * 128, xT)

                                def s2():
                                    pa_, pb_ = st["buf"][:, 0, :], st["buf"][:, 1, :]
                                    K.tt("dve", w_["t1"], pa_, tg_[:, 0, :], ALU.mult)
                                    K.tt("dve", w_["t2"], pb_, tg_[:, 1, :], ALU.mult)
                                    K.act(w_["sq"], pa_, AF.Square)

                                def s3():
                                    K.mm(st["buf"][:, 1, :], onesblk, w_["sq"])

                                def s4():
                                    K.act(w_["lnv"], st["buf"][:, 1, :], AF.Ln, bias=epsb[:, 0:1], scale=1.0 / 64.0)
                                    K.act(w_["rs"], w_["lnv"], AF.Exp, scale=-0.5)
                                    K.tt("pool", w_["t1"], w_["t1"], w_["t2"], ALU.add)
                                    K.tt("pool", q_[:, t, :], w_["t1"], w_["rs"], ALU.mult)
                                    release(st["buf"])
                                return [(s1, 3), (s2, 1), (s3, 2), (s4, 0)]
                            items.append(mk())
                        for t2_ in range(2):
                            def mk(t2_=t2_):
                                st = {}

                                def s1():
                                    st["buf"] = take_buf()
                                    hold(st["buf"])
                                    for j in range(2):
                                        projx(st["buf"][:, j, :], 1024 + (2 * t2_ + j) * 128, xT)

                                def s2():
                                    for j in range(2):
                                        K.act(WS[j]["t1"], st["buf"][:, j, :], AF.Tanh, scale=0.5)

                                def s3():
                                    for j in range(2):
                                        K.ts("dve", WS[j]["t1"], WS[j]["t1"], 0.5, ALU.mult, 0.5, ALU.add)
                                        K.tt("dve", g_[:, 2 * t2_ + j, :], st["buf"][:, j, :], WS[j]["t1"], ALU.mult)
                                    release(st["buf"])
                                return [(s1, 3), (s2, 1), (s3, 0)]
                            items.append(mk())
                        return prep, items

                    def outproj_items(Cc):
                        mA, mB = mixAs[Cc % 2], mixBls[Cc % 2]
                        items = []
                        for b in range(4):
                            def mk(b=b):
                                st = {}
                                bs = slice(b * 128, (b + 1) * 128)
                                r0 = Cc * CH + b * 128

                                def s1():
                                    i_ = nxb[0] % 2
                                    nxb[0] += 1
                                    st["i"] = i_
                                    K.dma("sp", xblk[i_], x_d[r0:r0 + 128, :], xblk_slot[i_])
                                    st["buf"] = take_buf()
                                    hold(st["buf"])
                                    py = st["buf"]
                                    for half in range(2):
                                        for f in range(8):
                                            src = mA[:, f, bs] if f < 4 else mB[:, f - 4, bs]
                                            K.mm(py[:, half, :], src, wobf[:, f, half * 512:(half + 1) * 512], start=(f == 0), stop=(f == 7))

                                def s2():
                                    xo = x1b[st["i"]]
                                    K.tt("dve", xo, st["buf"].re("p a c -> p (a c)"), xblk[st["i"]], ALU.add)
                                    K.dma("pool", x1_d[r0:r0 + 128, :], xo, x1b_slot[st["i"]])
                                    release(st["buf"])
                                return [(s1, 4), (s2, 0)]
                            items.append(mk())
                        return items

                    load_xnT(xnT0_d, 0)
                    prep0, items0 = proj_items(0)
                    prep0()
                    for it_ in items0:
                        for st_fn, _d in it_:
                            st_fn()
                    carry = []
                    for C in range(NCH):
                        use_xnT(C)
                        qaT_c, sga_c = qaTs[C % 2], sgas[C % 2]
                        mixA = mixAs[C % 2]
                        load_w_piece(whi, 0, wG1_d, C, 0, 384, 1)
                        pending = list(carry)
                        carry = []
                        if C + 1 < NCH:
                            load_xnT(xnT0_d, C + 1)
                            prepn, pitems = proj_items(C + 1)
                            prepn()
                            pending = pending + pitems
                        K.dma("pool", mixBls[C % 2], V(mixb_d.ap[:, :, C * CH:(C + 1) * CH].rearrange("h p c -> p h c"), mixb_d.res), mixBl_slots[C % 2])
                        nit = 0
                        active = [None]
                        for t in range(4):
                            kv = t // 2
                            fifo = []

                            def qk(kb_):
                                sc_ = take_buf()
                                fifo.append(sc_)
                                ks = slice(kb_ * 128, (kb_ + 1) * 128)
                                with K.pe_batch():
                                    K.mm(sc_[:, 0, :], KaT[0:64, kv, ks], qaT_c[0:64, t, :])
                                    K.mm(sc_[:, 1, :], KaT[64:128, kv, ks], qaT_c[64:128, t, :])

                            qk(0)
                            qk(1)
                            for kb in range(NB):
                                sc = fifo.pop(0)
                                pt = pTs[npt[0] % 3]
                                npt[0] += 1
                                nit += 1
                                K.act(pt, sc, AF.Exp, bias=maskA[:, C * NB + kb:C * NB + kb + 1], scale=0.125)
                                if kb + 2 < NB:
                                    qk(kb + 2)
                                if active[0] is None and pending and nit % 12 == 3:
                                    active[0] = [pending.pop(0), 0, nit]
                                if active[0] is not None and nit >= active[0][2]:
                                    stages_, si_, _due = active[0]
                                    fn_, delay_ = stages_[si_]
                                    fn_()
                                    if si_ + 1 < len(stages_):
                                        active[0] = [stages_, si_ + 1, nit + delay_]
                                    else:
                                        active[0] = None
                                st, sp_ = (kb == 0), (kb == NB - 1)
                                with K.pe_batch():
                                    K.mm(pnum[0:64, :], Va[:, kb, kv * 64:(kv + 1) * 64], pt[:, 0, :], start=st, stop=sp_)
                                    K.mm(pnum[64:128, :], Va[:, kb, kv * 64:(kv + 1) * 64], pt[:, 1, :], start=st, stop=sp_, tp=(0, 64))
                                    K.mm(pden[0:64, :], ones[:, 0:64], pt[:, 0, :], start=st, stop=sp_)
                                    K.mm(pden[64:128, :], ones[:, 0:64], pt[:, 1, :], start=st, stop=sp_, tp=(0, 64))
                            K.cp("dve", dcp, pden)
                            K.cp("dve", ncp, pnum)
                            K.recip(dcp, dcp)
                            K.tt("dve", ncp, ncp, dcp, ALU.mult)
                            K.tt("pool", mixA[:, t, :], ncp, sga_c[:, t, :], ALU.mult)
                        while active[0] is not None or pending:
                            if active[0] is None:
                                active[0] = [pending.pop(0), 0, 0]
                            stages_, si_, _due = active[0]
                            stages_[si_][0]()
                            active[0] = [stages_, si_ + 1, 0] if si_ + 1 < len(stages_) else None
                        carry = outproj_items(C)
                        if C == NCH - 1:
                            for it_ in carry:
                                for st_fn, _d in it_:
                                    st_fn()
                            carry = []

            tap("x1", x1_d, [NT, 1024], F32)
            checkpoint("LA")
            with Scope(K) as L1:
                KcT = sb("KcT", [128, 2, (NB + 2) * 128], BF16, L1)
                Vc = sb("Vc", [128, NB + 2, 128], BF16, L1)
                K.memset("pool", KcT[:, :, 0:128], 0.0)
                K.memset("pool", KcT[:, :, (NB + 1) * 128:(NB + 2) * 128], 0.0)
                K.memset("pool", Vc[:, 0, :], 0.0)
                K.memset("pool", Vc[:, NB + 1, :], 0.0)
                tap("EBT", EBT, [128, 16, 3, 128], BF16)
                checkpoint("EBT")
                wmode["base"] = 1536
                with Scope(K) as PG1:
                    pT = ps("pT", [128, 8, 128], BF16, PG1)
                    pa = ps("pa", [128, CH], F32, PG1)
                    pva = ps("pva", [128, 512], F32, PG1)[:, 0:128]
                    mxs = [alloc_mx(PG1), alloc_mx(PG1)]
                    pT2 = ps("pT2", [128, 8, 128], BF16, PG1)
                    pa2 = ps("pa2", [128, CH], F32, PG1)
                    pva2 = ps("pva2", [128, 512], F32, PG1)[:, 0:128]
                    xnT_front(mxs[0], x1_d, 0)
                    xnT_back(mxs[0], 0, [pT, pT2])
                    for C in range(NCH):
                        use_xnT(C)
                        store_xnT(xnT1_d, C, C % 2)
                        if C + 1 < NCH:
                            xnT_front(mxs[(C + 1) % 2], x1_d, C + 1)
                        load_w_piece(wlo, 0, wL1_d, C, 0, 1024, 1)
                        load_w_piece(wlo, 1024, wL1_d, C, 1024, 1536, 1)
                        load_w_piece(wobf, 0, woc_d, C, 0, 1024, None)
                        for t in range(2):
                            pa_ = pa if t == 0 else pa2
                            proj(pa_, t * 128)
                            K.cp("act" if t == 0 else "dve", KcT[:, t, (C * 4 + 1) * 128:(C * 4 + 5) * 128], pa_)
                        for b in range(4):
                            pv_ = pva if b % 2 == 0 else pva2
                            for kc in range(8):
                                K.mm(pv_, cur["xnT"][:, kc, b * 128:(b + 1) * 128], wcol(kc, 256), start=(kc == 0), stop=(kc == 7))
                            K.cp("dve" if b % 2 == 0 else "act", Vc[:, C * 4 + b + 1, :], pv_)
                        if C + 1 < NCH:
                            xnT_back(mxs[(C + 1) % 2], C + 1, [pT, pT2])

                tap("KcT", KcT, [128, 2, (NB + 2) * 128], BF16)
                tap("Vc", Vc, [128, NB + 2, 128], BF16)
                checkpoint("G1")
                wmode["base"] = 0
                for kc_ in range(8):
                    load_w_piece(whi, 0, wL1_d, kc_, 1536, 2048, 1)
                with Scope(K) as PL1:
                    pbig = ps("pbig1", [128, 8, CH], F32, PL1)
                    pw = [subview(pbig, pbig.ap[:, 2 * i:2 * i + 2, :]) for i in range(3)]
                    pnum = subview(pbig, pbig.ap[:, 6, :])
                    pden = subview(pbig, pbig.ap[:, 7, :])
                    bk = [subview(pbig, pbig.ap[:, i, :]) for i in range(6)]
                    qcT = sb("qcT", [128, 8, CH], BF16, PL1)
                    sgc = sb("sgc", [128, 8, CH], BF16, PL1)
                    mixC = sb("mixC", [128, 8, CH], BF16, PL1)
                    pws = [sb("pws%d" % i, [128, 2, 3, 128], BF16, PL1) for i in range(3)]
                    pw2 = [sb("pw2%d" % i, [128, 2, 3, 128], BF16, PL1) for i in range(3)]
                    rs = sb("rs1", [128, CH], F32, PL1)
                    lnr = sb("lnr", [128, CH], F32, PL1)
                    t1 = sb("t11", [128, CH], F32, PL1)
                    EP = [dict(rs=rs, t1=t1, lnr=lnr),
                          dict(rs=sb("rs1b", [128, CH], F32, PL1), t1=sb("t11b", [128, CH], F32, PL1), lnr=sb("lnrb", [128, CH], F32, PL1))]
                    pnums = [pnum, pnum]
                    pdens = [pden, pden]
                    x2 = [sb("x2%d" % i, [128, 1024], F32, PL1) for i in range(2)]
                    yo = [sb("yo%d" % i, [128, 1024], F32, PL1) for i in range(2)]
                    yo_slot = [K.slot() for _ in range(2)]
                    ss2 = sb("ss2", [128, 2], F32, PL1)
                    ln2 = sb("ln2", [128, 2], F32, PL1)
                    r2 = sb("r2", [128, 2], F32, PL1)
                    it = 0
                    mxl = alloc_mx(PL1, full=False)
                    xch = mxl.xch
                    junk = sb("junk1", [128, 1024], BF16, PL1)
                    fnbc = sb("fnbc", [128, 1024], F32, PL1)
                    fn_slot = K.slot()
                    K.dma("sp", fnbc, V(fn_d.ap.to_broadcast([128, 1024]), fn_d.res), fn_slot)
                    load_xnT(xnT1_d, 0)
                    for C in range(NCH):
                        use_xnT(C)
                        if C + 1 < NCH:
                            load_xnT(xnT1_d, C + 1)
                        load_x(mxl, x1_d, C)
                        handoff(pw, bk[0:6])
                        for t in range(8):
                            pa_ = bk[t % 6]
                            proj(pa_, t * 128)
                            K.cp("act" if t % 2 == 0 else "dve", qcT[:, t, :], pa_)
                        for t in range(8):
                            pa_ = bk[(t + 2) % 6]
                            proj(pa_, 1024 + t * 128)
                            K.act(sgc[:, t, :], pa_, AF.Silu)
                        handoff(bk[0:6], pw)
                        items = [(t, qi) for t in range(8) for qi in range(4)]

                        def wqk(t, qi, w):
                            kv = t // 4
                            i = C * 4 + qi
                            qs = slice(qi * 128, (qi + 1) * 128)
                            with K.pe_batch():
                                for o in range(3):
                                    sl = 2 - o
                                    ks = slice((i + o) * 128, (i + o + 1) * 128)
                                    K.mm(w[:, 0, sl * 128:(sl + 1) * 128], KcT[0:64, kv, ks], qcT[0:64, t, qs])
                                    K.mm(w[:, 1, sl * 128:(sl + 1) * 128], KcT[64:128, kv, ks], qcT[64:128, t, qs])

                        wqk(items[0][0], items[0][1], pw[it % 3])
                        wqk(items[1][0], items[1][1], pw[(it + 1) % 3])
                        deferred = []
                        for idx, (t, qi) in enumerate(items):
                            kv = t // 4
                            i = C * 4 + qi
                            qs = slice(qi * 128, (qi + 1) * 128)
                            w = pw[it % 3]
                            s1 = pws[it % 3]
                            s2 = pw2[it % 3]
                            it += 1
                            if i in (0, NB // 2 - 1, NB // 2, NB - 1):
                                for o in range(3):
                                    sl = 2 - o
                                    K.act(s1[:, :, sl, :], w[:, :, sl * 128:(sl + 1) * 128], AF.Exp, bias=maskW[:, i * 3 + o:i * 3 + o + 1], scale=0.125)
                            else:
                                K.act(s1, w[:, :, 0:384].re("p h (o q) -> p h o q", o=3), AF.Exp, scale=0.125)
                            K.tt("dve", s2, s1, EBT[:, 2 * t:2 * t + 2, :, :], ALU.mult)
                            if deferred:
                                deferred.pop(0)()
                            if idx + 2 < len(items):
                                wqk(items[idx + 2][0], items[idx + 2][1], pw[(it + 1) % 3])
                            pnum_, pden_ = pnums[t % 2], pdens[t % 2]
                            with K.pe_batch():
                                for o in range(3):
                                    sl = 2 - o
                                    st, sp_ = (o == 0), (o == 2)
                                    vv = Vc[:, i + o, kv * 64:(kv + 1) * 64]
                                    K.mm(pnum_[0:64, qs], vv, s2[:, 0, sl, :], start=st, stop=sp_)
                                    K.mm(pnum_[64:128, qs], vv, s2[:, 1, sl, :], start=st, stop=sp_, tp=(0, 64))
                                    K.mm(pden_[0:64, qs], ones[:, 0:64], s2[:, 0, sl, :], start=st, stop=sp_)
                                    K.mm(pden_[64:128, qs], ones[:, 0:64], s2[:, 1, sl, :], start=st, stop=sp_, tp=(0, 64))
                            if qi == 3:
                                def epi(t=t, pnum_=pnum_, pden_=pden_):
                                    e_ = EP[t % 2]
                                    K.ts("dve", e_["rs"], pden_, esk[:, t:t + 1], ALU.add)
                                    K.cp("dve", e_["t1"], pnum_)
                                    K.act(e_["lnr"], e_["rs"], AF.Ln)
                                    K.act(e_["rs"], e_["lnr"], AF.Exp, scale=-1.0)
                                    K.tt("pool", e_["t1"], e_["t1"], e_["rs"], ALU.mult)
                                    K.tt("pool", mixC[:, t, :], e_["t1"], sgc[:, t, :], ALU.mult)
                                deferred.append(epi)
                        while deferred:
                            deferred.pop(0)()
                        for b in range(4):
                            bs = slice(b * 128, (b + 1) * 128)
                            py = pw[b % 2]
                            for half in range(2):
                                for f in range(8):
                                    K.mm(py[:, half, :], mixC[:, f, bs], wobf[:, f, half * 512:(half + 1) * 512], start=(f == 0), stop=(f == 7))
                            xo = x2[b % 2]
                            K.tt("dve", xo, py.re("p a c -> p (a c)"), xch[:, b, :], ALU.add)
                            K.act(junk, xo, AF.Square, accum=ss2[:, b % 2:b % 2 + 1])
                            K.act(ln2[:, b % 2:b % 2 + 1], ss2[:, b % 2:b % 2 + 1], AF.Ln, bias=epsb[:, 0:1], scale=1.0 / 1024.0)
                            K.act(r2[:, b % 2:b % 2 + 1], ln2[:, b % 2:b % 2 + 1], AF.Exp, scale=-0.5)
                            yb = yo[b % 2]
                            K.ts("dve", yb, xo, r2[:, b % 2:b % 2 + 1], ALU.mult)
                            K.tt("pool", yb, yb, fnbc, ALU.mult)
                            r0 = C * CH + b * 128
                            K.dma("pool", y_d[r0:r0 + 128, :], yb, yo_slot[b % 2])
    except StopBuild:
        pass
    for s_ in K.slots:
        if s_.cnt:
            nc.gpsimd.wait_ge(s_.sem, s_.cnt)
    return nc, K


def _t5_bucket(rel):
    half = 16
    max_exact = 8
    ret = (rel > 0).astype(np.int32) * half
    dist = np.abs(rel)
    large = max_exact + (np.log(np.maximum(dist, 1) / max_exact) / np.log(128 / max_exact) * (half - max_exact)).astype(np.int32)
    large = np.minimum(large, half - 1)
    return ret + np.where(dist < max_exact, dist, large)


def _static_tables():
    f32 = np.float32
    st = {}
    st["ident"] = np.eye(128, dtype=f32)
    st["aident"] = np.ascontiguousarray(np.eye(128, dtype=f32)[::-1])
    ob = np.zeros((128, 128), f32)
    ob[:64, :64] = 1
    ob[64:, 64:] = 1
    st["onesblk"] = ob
    j = np.arange(128)[:, None]
    i = np.arange(128)[None, :]
    mmat = np.zeros((128, 4, 128), f32)
    mmat[:, 0, :] = np.maximum(i - j, 0)
    mmat[:, 1, :] = (i >= j)
    mmat[:, 2, :] = np.maximum(j - i, 0)
    mmat[:, 3, :] = (j > i)
    st["mmat"] = mmat
    c = np.arange(512) % 128
    iot = np.zeros((128, 4, 512), f32)
    iot[:, 0, :] = c + 1
    iot[:, 1, :] = 128 - c
    iot[:, 2, :] = 127 - c
    iot[:, 3, :] = c
    st["iot"] = iot
    m = np.arange(640)
    rel = 255 - m
    bk = _t5_bucket(rel)
    oh = np.zeros((32, 640), f32)
    oh[bk, m] = 1
    st["oh"] = oh
    st["inwin"] = np.broadcast_to((np.abs(rel) <= 128).astype(f32)[None, :], (16, 640)).copy()
    return st


def _core_tables(is_prompt):
    f32 = np.float32
    seqlen = 4096 if is_prompt else 2048
    t = np.arange(NT) % seqlen
    d = np.arange(128) % 64
    pair = d // 2
    sgn = np.where(d % 2 == 0, -1.0, 1.0)
    quarter = 16
    freqs = (np.float32(10000.0) ** (-np.arange(quarter, dtype=f32) / quarter)).astype(f32)
    row = (t // 64).astype(f32)
    col = (t % 64).astype(f32)
    ang = np.concatenate([row[:, None] * freqs, col[:, None] * freqs], axis=-1).astype(f32)
    angd = ang[:, pair].T.astype(np.float64)
    tabA = np.stack([np.cos(angd), np.sin(angd) * sgn[:, None]]).astype(f32)
    half = 32
    freqs_b = (np.float32(10000.0) ** (-np.arange(half, dtype=f32) / half)).astype(f32)
    angb = (t.astype(f32)[:, None] * freqs_b).astype(f32)
    angbd = angb[:, pair].T.astype(np.float64)
    tabB = np.stack([np.cos(angbd), np.sin(angbd) * sgn[:, None]]).astype(f32)
    seq_of_blk = (np.arange(NB) * 128) // seqlen
    maskA = np.zeros((NCH, NB), f32)
    for C in range(NCH):
        sq = (C * CH) // seqlen
        maskA[C, :] = np.where(seq_of_blk == sq, 0.0, NEG)
    maskA = np.broadcast_to(maskA.reshape(1, -1), (128, NCH * NB)).copy()
    maskW = np.zeros((NB, 3), f32)
    for i in range(NB):
        for o in range(3):
            jb = i + o - 1
            if jb < 0 or jb >= NB or seq_of_blk[jb] != seq_of_blk[i]:
                maskW[i, o] = NEG
    maskW = np.broadcast_to(maskW.reshape(1, -1), (128, NB * 3)).copy()
    cps = seqlen // 128
    rf = np.array([0.0 if (n % cps == 0) else 1.0 for n in range(NB)], f32)
    rb = np.array([0.0 if (n % cps == cps - 1) else 1.0 for n in range(NB)], f32)
    rfb = np.broadcast_to(np.concatenate([rf, rb])[None, :], (128, 64)).copy()
    return {"tabA": tabA, "tabB": tabB, "maskA": maskA, "maskW": maskW, "rfb": rfb}


def _swap(cols):
    cols = np.asarray(cols)
    return cols ^ 1


def _prep_common(norm_g, w_in_ab, qk_norm_a, ret_decay, w_out_ab, w_in_c, sink_c, w_out_c, rel_bias, final_norm):
    f32 = np.float32
    W = np.asarray(w_in_ab[0], f32)
    qa = np.arange(0, 512)
    ka = np.arange(512, 640)
    va = np.arange(640, 768)
    ga = np.arange(768, 1280)
    qb = np.arange(1280, 1536)
    kb = np.arange(1536, 1792)
    vb = np.arange(1792, 2304)
    gb = np.arange(2304, 2816)
    kadup = np.concatenate([ka[0:64], ka[0:64], ka[64:128], ka[64:128]])
    cm = {}
    cm["wG"] = np.ascontiguousarray(W[:, np.concatenate([kadup, _swap(kadup), kb, _swap(kb), va, vb])])
    cm["wLB"] = np.ascontiguousarray(W[:, np.concatenate([qb, _swap(qb), gb])])
    cm["wLA"] = np.ascontiguousarray(W[:, np.concatenate([qa, _swap(qa), ga])])
    cm["woab"] = np.ascontiguousarray(np.asarray(w_out_ab[0], f32))
    Wc = np.asarray(w_in_c[0], f32)
    kc = np.arange(1024, 1152)
    kcdup = np.concatenate([kc[0:64], kc[0:64], kc[64:128], kc[64:128]])
    cm["wG1"] = np.ascontiguousarray(Wc[:, np.concatenate([kcdup, np.arange(1152, 1280)])])
    cm["wL1"] = np.ascontiguousarray(Wc[:, np.concatenate([np.arange(0, 1024), np.arange(1280, 2304)])])
    cm["woc"] = np.ascontiguousarray(np.asarray(w_out_c[0], f32))
    ng = np.asarray(norm_g, f32)
    cm["gcol"] = np.ascontiguousarray(ng.reshape(2, 8, 128).transpose(2, 0, 1).reshape(128, 16))
    cm["fn"] = np.asarray(final_norm, f32).reshape(1, 1024).copy()
    g = np.asarray(qk_norm_a[0], f32)
    d = np.arange(128) % 64
    cm["gqk"] = np.stack([g[0][d], g[0][d ^ 1], g[1][d], g[1][d ^ 1]], axis=1).astype(f32).copy()
    rd = np.asarray(ret_decay[0], f32)
    hp = (np.arange(128) // 64)
    rdec = np.zeros((128, 12), f32)
    for p in range(2):
        rdec[:, p] = rd[0][2 * p + hp]
        rdec[:, 2 + p] = rd[1][2 * p + hp]
    for h in range(4):
        rdec[:, 4 + h] = rd[0][h]
        rdec[:, 8 + h] = rd[1][h]
    cm["rdec"] = rdec
    sk = np.asarray(sink_c[0], f32)
    sinkl = np.zeros((128, 8), f32)
    for t in range(8):
        sinkl[:, t] = sk[2 * t + hp]
    cm["sinkl"] = sinkl
    cm["relb"] = np.ascontiguousarray(np.asarray(rel_bias, f32))
    cm.update(_static_tables())
    return cm


_CACHE = {}


def kernel(x_prompt, x_sample, norm_g, w_in_ab, qk_norm_a, ret_decay, w_out_ab, w_in_c, sink_c, w_out_c, rel_bias, final_norm):
    xp = np.asarray(x_prompt, np.float32)
    xs = np.asarray(x_sample, np.float32)
    cm = _prep_common(norm_g, w_in_ab, qk_norm_a, ret_decay, w_out_ab, w_in_c, sink_c, w_out_c, rel_bias, final_norm)
    tp = _core_tables(True)
    tsm = _core_tables(False)
    in_maps = []
    for c in range(8):
        m = dict(cm)
        if c < 4:
            m["x"] = np.ascontiguousarray(xp[c])
            m.update(tp)
        else:
            m["x"] = np.ascontiguousarray(xs[2 * (c - 4):2 * (c - 4) + 2].reshape(NT, 1024))
            m.update(tsm)
        in_maps.append(m)
    if "nc" not in _CACHE:
        _CACHE["nc"] = build_program()[0]
    nc = _CACHE["nc"]
    res = run_bass_kernel_spmd(nc, in_maps, core_ids=list(range(8)))
    outs = [np.asarray(r["y"], np.float32) for r in res.results]
    y_prompt = np.stack(outs[0:4], axis=0)
    y_sample = np.stack(outs[4:8], axis=0).reshape(8, 2048, 1024)
    return (y_prompt, y_sample)
```

```python
import numpy as np
import concourse.bass as bass
import concourse.mybir as mybir
from concourse.bass_utils import run_bass_kernel_spmd

F32 = mybir.dt.float32
BF16 = mybir.dt.bfloat16
AF = mybir.ActivationFunctionType
ALU = mybir.AluOpType

NT = 4096
NB = 32
CH = 512
NCH = 8
EPS = 1e-6
NEG = -30000.0


class Prod:
    def __init__(self, sem, inc):
        self.sem = sem
        self.inc = inc
        self.cnt = 0


class Res:
    def __init__(self):
        self.w = {}
        self.r = {}
        self.excl = False


class V:
    def __init__(self, ap, res=None):
        self.ap = ap
        self.res = res if res is not None else Res()

    def __getitem__(self, k):
        return V(self.ap[k], self.res)

    def re(self, pat, **kw):
        return V(self.ap.rearrange(pat, **kw), self.res)

    def bc(self, shape):
        return V(self.ap.to_broadcast(shape), self.res)


class Ker:
    def __init__(self, nc):
        self.nc = nc
        self.eng = {"pe": nc.tensor, "act": nc.scalar, "dve": nc.vector, "pool": nc.gpsimd, "sp": nc.sync}
        self.prod = {}
        for n in ("pe", "act", "dve", "pool"):
            self.prod[n] = Prod(nc.alloc_semaphore("s_" + n), 1)
        self.seen = {n: {} for n in self.eng}
        self.nslot = 0
        self.ninstr = 0

    def slot(self):
        self.nslot += 1
        p = Prod(self.nc.alloc_semaphore("d%d" % self.nslot), 16)
        if hasattr(self, "slots"):
            self.slots.append(p)
        return p

    def _wait(self, en, reads, writes):
        deps = {}
        for v in reads:
            for p, i in v.res.w.items():
                deps[p] = max(deps.get(p, 0), i)
        for v in writes:
            for p, i in v.res.w.items():
                deps[p] = max(deps.get(p, 0), i)
            for p, i in v.res.r.items():
                deps[p] = max(deps.get(p, 0), i)
        e = self.eng[en]
        seen = self.seen[en]
        own = self.prod.get(en)
        for p, i in deps.items():
            if p is own and en == "pe":
                continue
            if seen.get(p, 0) >= i:
                continue
            e.wait_ge(p.sem, i)
            seen[p] = i

    def op(self, en, fn, reads, writes):
        writes = list(writes) + [r for r in reads if r.res.excl]
        self._wait(en, reads, writes)
        ins = fn(self.eng[en])
        p = self.prod[en]
        if en == "pe" and getattr(self, "batching", False):
            idx = p.cnt + 1
            self.batch_last = ins
        else:
            p.cnt += 1
            idx = p.cnt
            ins.then_inc(p.sem, 1)
        for v in reads:
            v.res.r[p] = idx
        for v in writes:
            v.res.w[p] = idx
        self.ninstr += 1

    def pe_batch(self):
        ker = self

        class _B:
            def __enter__(self_):
                ker.batching = True
                ker.batch_last = None

            def __exit__(self_, *a):
                ker.batching = False
                if ker.batch_last is not None:
                    p = ker.prod["pe"]
                    p.cnt += 1
                    ker.batch_last.then_inc(p.sem, 1)
                    ker.batch_last = None
                return False
        return _B()

    def dma(self, q, out, in_, slot):
        self._wait(q, [in_], [out])
        ins = self.eng[q].dma_start(out=out.ap, in_=in_.ap)
        slot.cnt += 16
        ins.then_inc(slot.sem, 16)
        in_.res.r[slot] = slot.cnt
        out.res.w[slot] = slot.cnt

    def mm(self, out, lhsT, rhs, start=True, stop=True, tp=None):
        kw = {}
        if tp is not None:
            kw["tile_position"] = tp
        self.op("pe", lambda e: e.matmul(out.ap, lhsT.ap, rhs.ap, start=start, stop=stop, **kw), [lhsT, rhs], [out])

    def tr(self, out, in_, ident):
        self.op("pe", lambda e: e.transpose(out.ap, in_.ap, ident.ap), [in_, ident], [out])

    def act(self, out, in_, func, bias=None, scale=1.0, accum=None):
        reads = [in_]
        kw = {}
        if bias is not None:
            if isinstance(bias, V):
                reads.append(bias)
                kw["bias"] = bias.ap
            else:
                kw["bias"] = bias
        if isinstance(scale, V):
            reads.append(scale)
            kw["scale"] = scale.ap
        else:
            kw["scale"] = scale
        writes = [out]
        if accum is not None:
            writes.append(accum)
            kw["accum_out"] = accum.ap
        self.op("act", lambda e: e.activation(out.ap, in_.ap, func, **kw), reads, writes)

    def tt(self, en, out, a, b, op):
        self.op(en, lambda e: e.tensor_tensor(out.ap, a.ap, b.ap, op), [a, b], [out])

    def stt(self, en, out, in0, scalar, in1, op0, op1):
        reads = [in0, in1]
        s = scalar
        if isinstance(scalar, V):
            reads.append(scalar)
            s = scalar.ap
        self.op(en, lambda e: e.scalar_tensor_tensor(out.ap, in0.ap, s, in1.ap, op0, op1), reads, [out])

    def ts(self, en, out, in0, s1, op0, s2=None, op1=None):
        reads = [in0]
        a1 = s1
        if isinstance(s1, V):
            reads.append(s1)
            a1 = s1.ap
        a2 = s2
        if isinstance(s2, V):
            reads.append(s2)
            a2 = s2.ap
        if op1 is None:
            self.op(en, lambda e: e.tensor_scalar(out.ap, in0.ap, a1, None, op0), reads, [out])
        else:
            self.op(en, lambda e: e.tensor_scalar(out.ap, in0.ap, a1, a2, op0, op1), reads, [out])

    def cp(self, en, out, in_):
        if en == "act":
            self.op("act", lambda e: e.copy(out.ap, in_.ap), [in_], [out])
        else:
            self.op(en, lambda e: e.tensor_copy(out.ap, in_.ap), [in_], [out])

    def amul(self, out, in_, m):
        self.op("act", lambda e: e.mul(out.ap, in_.ap, m.ap), [in_, m], [out])

    def recip(self, out, in_):
        self.op("dve", lambda e: e.reciprocal(out.ap, in_.ap), [in_], [out])

    def memset(self, en, out, val):
        self.op(en, lambda e: e.memset(out.ap, val), [], [out])


class StopBuild(Exception):
    pass


import contextlib


class Scope(contextlib.ExitStack):
    def __init__(self, K):
        super().__init__()
        self.K = K
        self.tiles = []

    def __exit__(self, *a):
        fr = self.K.freed
        for v in self.tiles:
            for d in (v.res.w, v.res.r):
                for p, i in d.items():
                    fr[p] = max(fr.get(p, 0), i)
        self.tiles = []
        return super().__exit__(*a)

    def close(self):
        self.__exit__(None, None, None)


def build_program(stop=None, taps=()):
    nc = bass.Bass("TRN2", target_bir_lowering=False)
    K = Ker(nc)
    K.slots = []
    K.freed = {}
    K.tapped = {}

    def checkpoint(name):
        if stop == name:
            raise StopBuild()

    def tap(name, v, shape, dt=F32):
        if name not in taps or name in K.tapped:
            return
        d = V(nc.dram_tensor("dbg_" + name, list(shape), dt, kind="ExternalOutput").ap())
        K.tapped[name] = d
        K.dma("sp", d, v, K.slot())

    def din(name, shape, dt=F32):
        return V(nc.dram_tensor(name, list(shape), dt, kind="ExternalInput").ap())

    x_d = din("x", [NT, 1024])
    wG_d = din("wG", [1024, 1664])
    wLB_d = din("wLB", [1024, 1024])
    wLA_d = din("wLA", [1024, 1536])
    woab_d = din("woab", [1024, 1024])
    wG1_d = din("wG1", [1024, 384])
    wL1_d = din("wL1", [1024, 2048])
    woc_d = din("woc", [1024, 1024])
    gcol_d = din("gcol", [128, 16])
    fn_d = din("fn", [1, 1024])
    gqk_d = din("gqk", [128, 4])
    rdec_d = din("rdec", [128, 12])
    sink_d = din("sinkl", [128, 8])
    relb_d = din("relb", [32, 16])
    ident_d = din("ident", [128, 128])
    onesblk_d = din("onesblk", [128, 128])
    mm_d = din("mmat", [128, 4, 128])
    iot_d = din("iot", [128, 4, 512])
    oh_d = din("oh", [32, 640])
    inwin_d = din("inwin", [16, 640])
    tabA_d = din("tabA", [2, 128, NT])
    tabB_d = din("tabB", [2, 128, NT])
    maskA_d = din("maskA", [128, 256])
    maskW_d = din("maskW", [128, 96])
    rfb_d = din("rfb", [128, 64])
    y_d = V(nc.dram_tensor("y", [NT, 1024], F32, kind="ExternalOutput").ap())
    x1_d = V(nc.dram_tensor("x1s", [NT, 1024], F32, kind="Internal").ap())
    mixb_d = V(nc.dram_tensor("mixbs", [4, 128, NT], BF16, kind="Internal").ap())
    vec_h = nc.dram_tensor("vecs", [16, 640], BF16, kind="Internal")
    vec_d = V(vec_h.ap())
    aident_d = din("aident", [128, 128])
    xnT0_d = V(nc.dram_tensor("xnT0s", [NCH, 128, 8, CH], BF16, kind="Internal").ap())
    xnT1_d = V(nc.dram_tensor("xnT1s", [NCH, 128, 8, CH], BF16, kind="Internal").ap())
    kr_d = V(nc.dram_tensor("krs", [NCH, 128, 2, CH], F32, kind="Internal").ap())
    vb_d = V(nc.dram_tensor("vbs", [NCH, 128, 4, 512], BF16, kind="Internal").ap())

    es = Scope(K)
    uid = [0]

    def sb(name, shape, dt=F32, stack=None):
        uid[0] += 1
        st_ = stack if stack is not None else es
        t = st_.enter_context(nc.sbuf_tensor("sb%d_%s" % (uid[0], name), list(shape), dt))
        v = V(t[:])
        v.res.w = dict(K.freed)
        st_.tiles.append(v)
        return v

    def ps(name, shape, dt=F32, stack=None):
        uid[0] += 1
        st_ = stack if stack is not None else es
        t = st_.enter_context(nc.psum_tensor("ps%d_%s" % (uid[0], name), list(shape), dt))
        v = V(t[:])
        v.res.excl = True
        v.res.w = dict(K.freed)
        st_.tiles.append(v)
        return v

    try:
        with es:
            cslot = K.slot()
            consts = []

            def cload(name, src, shape, dt=F32, q="sp"):
                t = sb(name, shape, dt)
                K.dma(q, t, src, cslot)
                consts.append(t)
                return t

            gcol = cload("gcol", gcol_d, [128, 16])
            gqk = cload("gqk", gqk_d, [128, 4])
            rdec = cload("rdec", rdec_d, [128, 12])
            sinkl = cload("sinkl", sink_d, [128, 8])
            maskA = cload("maskA", maskA_d, [128, 256])
            maskW = cload("maskW", maskW_d, [128, 96])
            rfb = cload("rfb", rfb_d, [128, 64])
            ident32 = cload("ident32", ident_d, [128, 128])
            onesblk32 = cload("onesblk32", onesblk_d, [128, 128])
            for c in consts:
                c.res.w[cslot] = cslot.cnt
            ident = sb("ident", [128, 128], BF16)
            onesblk = sb("onesblk", [128, 128], BF16)
            ones = sb("ones", [128, 128], BF16)
            epsb = sb("epsb", [128, 1])
            K.cp("dve", ident, ident32)
            K.cp("dve", onesblk, onesblk32)
            K.memset("dve", ones, 1.0)
            K.memset("dve", epsb, EPS)

            checkpoint("c0")
            wbf = sb("wbf", [128, 8, 2048], BF16)
            wobf = sb("wobf", [128, 8, 1024], BF16)
            wst = [sb("wst%d" % i, [128, 1024]) for i in range(2)]
            wst_slot = [K.slot() for _ in range(2)]
            xnTs = [sb("xnT%d" % i, [128, 8, CH], BF16) for i in range(2)]
            xnT_slot = [K.slot() for _ in range(2)]
            cur = {"xnT": xnTs[0]}
            xch_slot = [K.slot() for _ in range(4)]
            xst_slot = [K.slot() for _ in range(2)]
            EBT = sb("EBT", [128, 16, 3, 128], BF16)
            esk = sb("esk", [128, 8], F32)
            K.act(esk, sinkl, AF.Exp)
            with Scope(K) as S1:
                relb = sb("relb", [32, 16], F32, S1)
                oh = sb("oh", [32, 640], F32, S1)
                inw = sb("inw", [16, 640], F32, S1)
                e_slot = K.slot()
                K.dma("sp", relb, relb_d, e_slot)
                K.dma("sp", oh, oh_d, e_slot)
                K.dma("sp", inw, inwin_d, e_slot)
                for t_ in (relb, oh, inw):
                    t_.res.w[e_slot] = e_slot.cnt
                pv = ps("pv", [16, 1024], F32, S1)[:, 0:640]
                vec = sb("vec", [16, 640], F32, S1)
                vecb = sb("vecb", [16, 640], BF16, S1)
                K.mm(pv[:, 0:512], relb, oh[:, 0:512])
                K.mm(pv[:, 512:640], relb, oh[:, 512:640])
                K.act(vec, pv, AF.Exp)
                K.tt("dve", vecb, vec, inw, ALU.mult)
                v_slot = K.slot()
                K.dma("sp", vec_d, vecb, v_slot)
                g_slot = K.slot()
                aid32 = sb("aid32", [128, 128], F32, S1)
                K.dma("sp", aid32, aident_d, g_slot)
                aid = sb("aid", [128, 128], BF16, S1)
                K.cp("dve", aid, aid32)
                TT = sb("TT", [128, 16 * 384], BF16, S1)
                src = V(bass.AP(vec_h, 0, [[1, 128], [640, 16], [1, 384]]), vec_d.res)
                K.dma("sp", TT.re("p (h j) -> p h j", h=16), src, g_slot)
                prev = ps("prev", [128, 2, CH], F32, S1)
                EBTf = EBT.re("p h o q -> p (h o q)")
                for n_ in range(12):
                    K.mm(prev[:, n_ % 2, :], aid, TT[:, n_ * 512:(n_ + 1) * 512])
                    K.cp("act" if n_ % 2 == 0 else "dve", EBTf[:, n_ * 512:(n_ + 1) * 512], prev[:, n_ % 2, :])
            wcount = [0]

            def handoff(srcs, dsts):
                for d_ in dsts:
                    for s_ in srcs:
                        for dd in (s_.res.w, s_.res.r):
                            for p_, i_ in dd.items():
                                d_.res.w[p_] = max(d_.res.w.get(p_, 0), i_)

            def subview(parent, ap):
                v = V(ap)
                v.res.excl = parent.res.excl
                v.res.w = dict(parent.res.w)
                return v

            class MX:
                pass

            def alloc_mx(scope, full=True):
                m = MX()
                m.xch = sb("xch", [128, 4, 1024], F32, scope)
                if full:
                    m.xn = [sb("xn%d" % i, [128, 1024], BF16, scope) for i in range(2)]
                    m.junk = sb("junk", [128, 1024], BF16, scope)
                    m.ss = sb("ss", [128, 4], F32, scope)
                    m.lnv4 = sb("lnv4", [128, 4], F32, scope)
                    m.rstd4 = sb("rstd4", [128, 4], F32, scope)
                return m

            def load_x(m, src_d, C):
                for b in range(4):
                    r0 = C * CH + b * 128
                    K.dma("sp", m.xch[:, b, :], src_d[r0:r0 + 128, :], xch_slot[b])

            def store_xnT(dst_d, C, slot_i):
                K.dma("pool", dst_d[C], cur["xnT"], xst_slot[slot_i])

            def load_xnT(src_d, C):
                i = C % 2
                K.dma("sp", xnTs[i], src_d[C], xnT_slot[i])

            def use_xnT(C):
                cur["xnT"] = xnTs[C % 2]

            def load_w_piece(dst, d0, src_d, kc, c0, c1, layer_g):
                i = wcount[0] % 2
                wcount[0] += 1
                n_ = c1 - c0
                K.dma("sp", wst[i][:, 0:n_], src_d[kc * 128:(kc + 1) * 128, c0:c1], wst_slot[i])
                en = "act" if (wcount[0] % 2 == 0) else "dve"
                if layer_g is None:
                    K.cp(en, dst[:, kc, d0:d0 + n_], wst[i][:, 0:n_])
                elif en == "act":
                    K.amul(dst[:, kc, d0:d0 + n_], wst[i][:, 0:n_], gcol[:, layer_g * 8 + kc:layer_g * 8 + kc + 1])
                else:
                    K.ts("dve", dst[:, kc, d0:d0 + n_], wst[i][:, 0:n_], gcol[:, layer_g * 8 + kc:layer_g * 8 + kc + 1], ALU.mult)

            def load_w(dst, src_d, ncols, layer_g):
                for kc in range(8):
                    for c0 in range(0, ncols, 1024):
                        c1 = min(ncols, c0 + 1024)
                        load_w_piece(dst, c0, src_d, kc, c0, c1, layer_g)

            wlo = V(wbf.ap[:, :, 0:1536])
            whi = V(wbf.ap[:, :, 1536:2048])
            wmode = {"split": False, "base": 0}

            def wcol(kc, c0, n_=128):
                c0 = c0 + wmode["base"]
                if not wmode["split"]:
                    return wbf[:, kc, c0:c0 + n_]
                if c0 + n_ <= 1536:
                    return wlo[:, kc, c0:c0 + n_]
                return whi[:, kc, c0 - 1536:c0 - 1536 + n_]

            def xnT_front(m, src_d, C):
                load_x(m, src_d, C)
                for b in range(4):
                    K.act(m.junk, m.xch[:, b, :], AF.Square, accum=m.ss[:, b:b + 1])
                K.act(m.lnv4, m.ss, AF.Ln, bias=epsb[:, 0:1], scale=1.0 / 1024.0)
                K.act(m.rstd4, m.lnv4, AF.Exp, scale=-0.5)

            def xnT_back(m, C, pTl):
                xnT = xnTs[C % 2]
                for b in range(4):
                    xb = m.xn[b % 2]
                    pT_ = pTl[b % len(pTl)]
                    K.ts("dve", xb, m.xch[:, b, :], m.rstd4[:, b:b + 1], ALU.mult)
                    for kc in range(8):
                        K.tr(pT_[:, kc, :], xb[:, kc * 128:(kc + 1) * 128], ident)
                    K.cp("act" if b % 2 == 0 else "dve", xnT[:, :, b * 128:(b + 1) * 128], pT_)

            def make_xnT(m, src_d, C, pT):
                use_xnT(C)
                xnT_front(m, src_d, C)
                xnT_back(m, C, pT if isinstance(pT, list) else [pT])

            def proj(dst, c0):
                with K.pe_batch():
                    for kc in range(8):
                        K.mm(dst, wcol(kc, c0), cur["xnT"][:, kc, :], start=(kc == 0), stop=(kc == 7))

            def rsq_bcast(dst, src_ps, nfeat, sq, psn, lnv, lhs_ones):
                K.act(sq, src_ps, AF.Square)
                K.mm(psn, lhs_ones, sq)
                K.act(lnv, psn, AF.Ln, bias=epsb[:, 0:1], scale=1.0 / nfeat)
                K.act(dst, lnv, AF.Exp, scale=-0.5)

            with Scope(K) as L0:
                LR = Scope(K)
                KaT = sb("KaT", [128, 2, NT], BF16, L0)
                Va = sb("Va", [128, NB, 128], BF16, L0)
                tabc = sb("tabc", [128, 2, CH], F32, L0)
                tab_slot = K.slot()
                tabd_slot = K.slot()
                sq = sb("sq", [128, CH], BF16, L0)
                lnv = sb("lnv", [128, CH], F32, L0)
                rs = sb("rs", [128, CH], F32, L0)
                t1 = sb("t1", [128, CH], F32, L0)
                t2 = sb("t2", [128, CH], F32, L0)
                tabg = sb("tabg", [128, 2, CH], F32, L0)
                SbAll = sb("SbAll", [128, 2, NB, 128], BF16, LR)
                tabd = sb("tabd", [128, 2, CH], F32, LR)
                vbtm = sb("vbtm", [128, 4, 512], BF16, LR)
                lg = sb("lg", [128, 12], F32, LR)
                K.act(lg, rdec, AF.Exp)
                K.ts("dve", lg, lg, -1.0, ALU.mult)
                cd = sb("cd", [128, 4], F32, LR)
                K.act(cd, lg[:, 0:4], AF.Exp, scale=128.0)
                cdr = sb("cdr", [128, 4, NB], F32, LR)
                for j in range(4):
                    off = 0 if j < 2 else 32
                    K.ts("dve", cdr[:, j, :], rfb[:, off:off + 32], cd[:, j:j + 1], ALU.mult)
                checkpoint("c1")
                QF4 = sb("QF4", [128, 2, CH], F32, LR)
                QB4 = sb("QB4", [128, 2, CH], F32, LR)
                KF4 = sb("KF4", [128, 2, CH], F32, LR)
                KB4 = sb("KB4", [128, 2, CH], F32, LR)
                DT = sb("DT", [128, 4, 128], F32, LR)
                with Scope(K) as S0:
                    iot = sb("iot", [128, 4, CH], F32, S0)
                    K.dma("sp", iot, iot_d, tabd_slot)
                    for p in range(2):
                        K.act(QF4[:, p, :], iot[:, 0, :], AF.Exp, scale=lg[:, p:p + 1])
                        K.act(QB4[:, p, :], iot[:, 1, :], AF.Exp, scale=lg[:, 2 + p:3 + p])
                        K.act(KF4[:, p, :], iot[:, 2, :], AF.Exp, scale=lg[:, p:p + 1])
                        K.act(KB4[:, p, :], iot[:, 3, :], AF.Exp, scale=lg[:, 2 + p:3 + p])
                    K.ts("dve", KF4, KF4, 0.125, ALU.mult)
                    K.ts("dve", KB4, KB4, 0.125, ALU.mult)
                    mmat = sb("mmat", [128, 4, 128], F32, S0)
                    K.dma("sp", mmat, mm_d, tab_slot)
                    d1 = sb("d1", [128, 128], F32, S0)
                    d2 = sb("d2", [128, 128], F32, S0)
                    for h in range(4):
                        checkpoint("d0")
                        K.act(d1, mmat[:, 0, :], AF.Exp, scale=lg[:, 4 + h:5 + h])
                        checkpoint("d1")
                        K.tt("dve", d1, d1, mmat[:, 1, :], ALU.mult)
                        checkpoint("d2")
                        K.act(d2, mmat[:, 2, :], AF.Exp, scale=lg[:, 8 + h:9 + h])
                        K.tt("dve", d2, d2, mmat[:, 3, :], ALU.mult)
                        K.tt("dve", d1, d1, d2, ALU.add)
                        checkpoint("d3")
                        K.ts("dve", DT[:, h, :], d1, 0.125, ALU.mult)
                        checkpoint("d4")

                def load_tab(dst, slot, src_d, C):
                    K.dma("sp", dst, V(src_d.ap[:, :, C * CH:(C + 1) * CH].rearrange("t p c -> p t c"), src_d.res), slot)

                def rope(psa, psb, tab, out32, ga=None, gb=None):
                    if ga is None:
                        K.tt("dve", t1, psa, tab[:, 0, :], ALU.mult)
                        K.tt("dve", t2, psb, tab[:, 1, :], ALU.mult)
                    else:
                        K.amul(tabg[:, 0, :], tab[:, 0, :], ga)
                        K.amul(tabg[:, 1, :], tab[:, 1, :], gb)
                        K.tt("dve", t1, psa, tabg[:, 0, :], ALU.mult)
                        K.tt("dve", t2, psb, tabg[:, 1, :], ALU.mult)
                    K.tt("pool", out32, t1, t2, ALU.add)

                checkpoint("setup0")
                load_w(wbf, wG_d, 1664, 0)
                with Scope(K) as PG:
                    pT = ps("pT", [128, 8, 128], BF16, PG)
                    pk = pT.re("p (c t) q -> p c t q", t=2)
                    pbig = ps("pbigG", [128, 7, CH], F32, PG)
                    bk = [subview(pbig, pbig.ap[:, i, :]) for i in range(7)]
                    pn = bk[4]
                    pT2g = V(bk[5].ap.bitcast(BF16).rearrange("p (k q) -> p k q", k=8), bk[5].res)
                    pkv = V(bk[6].ap[:, 0:256].rearrange("p (a b) -> p a b", a=2), bk[6].res)
                    kdbT = sb("kdbT", [128, 2, CH], BF16, PG)
                    kdbtm = sb("kdbtm", [128, 4, 2, 128], BF16, PG)
                    Rb = sb("Rb", [128, 2, 128], F32, PG)
                    mxg = alloc_mx(PG)
                    WS = [dict(sq=sq, lnv=lnv, rs=rs, t1=t1, t2=t2),
                          dict(sq=sb("wsq", [128, CH], BF16, PG), lnv=sb("wlnv", [128, CH], F32, PG),
                               rs=sb("wrs", [128, CH], F32, PG), t1=sb("wt1", [128, CH], F32, PG),
                               t2=sb("wt2", [128, CH], F32, PG))]
                    K.memset("dve", Rb, 0.0)
                    kr_slot = [K.slot() for _ in range(2)]
                    vb_slot = K.slot()
                    for C in range(NCH - 1, -1, -1):
                        make_xnT(mxg, x_d, C, [pT, pT2g])
                        store_xnT(xnT0_d, C, C % 2)
                        load_w_piece(wobf, 0, woab_d, C, 0, 1024, None)
                        load_tab(tabc, tab_slot, tabA_d, C)
                        load_tab(tabd, tabd_slot, tabB_d, C)
                        K.amul(tabg[:, 0, :], tabc[:, 0, :], gqk[:, 2:3])
                        K.amul(tabg[:, 1, :], tabc[:, 1, :], gqk[:, 3:4])
                        for t in range(2):
                            w_ = WS[t % 2]
                            pa_, pb_ = bk[2 * t], bk[2 * t + 1]
                            proj(pa_, t * 128)
                            proj(pb_, 256 + t * 128)
                            rsq_bcast(w_["rs"], pa_, 64.0, w_["sq"], pn, w_["lnv"], onesblk)
                            K.tt("dve", w_["t1"], pa_, tabg[:, 0, :], ALU.mult)
                            K.tt("dve", w_["t2"], pb_, tabg[:, 1, :], ALU.mult)
                            K.tt("pool", w_["t1"], w_["t1"], w_["t2"], ALU.add)
                            K.tt("pool", KaT[:, t, C * CH:(C + 1) * CH], w_["t1"], w_["rs"], ALU.mult)
                        for t in range(2):
                            w_ = WS[t % 2]
                            pa_, pb_ = bk[2 * t], bk[2 * t + 1]
                            proj(pa_, 512 + t * 128)
                            proj(pb_, 768 + t * 128)
                            K.tt("dve", w_["t1"], pa_, tabd[:, 0, :], ALU.mult)
                            K.tt("dve", w_["t2"], pb_, tabd[:, 1, :], ALU.mult)
                            K.tt("pool", w_["t1"], w_["t1"], w_["t2"], ALU.add)
                            K.dma("pool", kr_d[C][:, t, :], w_["t1"], kr_slot[t % 2])
                            K.tt("pool", kdbT[:, t, :], w_["t1"], KB4[:, t, :], ALU.mult)
                        for b in range(4):
                            pva = (bk[4] if b % 2 == 0 else bk[2])[:, 0:128]
                            pvb = bk[5] if b % 2 == 0 else bk[3]
                            for kc in range(8):
                                K.mm(pva, cur["xnT"][:, kc, b * 128:(b + 1) * 128], wbf[:, kc, 1024:1152], start=(kc == 0), stop=(kc == 7))
                            for kc in range(8):
                                K.mm(pvb, cur["xnT"][:, kc, b * 128:(b + 1) * 128], wbf[:, kc, 1152:1664], start=(kc == 0), stop=(kc == 7))
                            K.cp("act", Va[:, C * 4 + b, :], pva)
                            K.cp("dve", vbtm[:, b, :], pvb)
                        K.dma("pool", vb_d[C], vbtm, vb_slot)
                        for t in range(2):
                            for cj in range(4):
                                K.tr(pk[:, cj, t, :], kdbT[:, t, cj * 128:(cj + 1) * 128], ident)
                        K.cp("act", kdbtm, pk)
                        for cj in range(3, -1, -1):
                            n = C * 4 + cj
                            for p in range(2):
                                K.mm(pkv[0:64, p, :], kdbtm[:, cj, p, 0:64], vbtm[:, cj, (2 * p) * 128:(2 * p + 1) * 128])
                                K.mm(pkv[64:128, p, :], kdbtm[:, cj, p, 64:128], vbtm[:, cj, (2 * p + 1) * 128:(2 * p + 2) * 128], tp=(0, 64))
                            K.ts("dve", SbAll[:, :, n, :], Rb, rfb[:, 32 + n:33 + n], ALU.mult)
                            for p in range(2):
                                K.ts("dve", Rb[:, p, :], Rb[:, p, :], cdr[:, 2 + p, n:n + 1], ALU.mult)
                                K.tt("dve", Rb[:, p, :], pkv[:, p, :], Rb[:, p, :], ALU.add)

                tap("KaT", KaT, [128, 2, NT], BF16)
                tap("Va", Va, [128, NB, 128], BF16)
                tap("SbAll", SbAll, [128, 2, NB, 128], BF16)
                checkpoint("G")
                load_w(wbf, wLB_d, 1024, 0)
                with Scope(K) as PB:
                    pT = ps("pT", [128, 8, 128], BF16, PB)
                    pk = pT.re("p (c t) q -> p c t q", t=2)
                    pbig = ps("pbigB", [128, 7, CH], F32, PB)
                    bk = [subview(pbig, pbig.ap[:, i, :]) for i in range(7)]
                    pa, pb, pss = bk[0], bk[1], bk[2]
                    po = subview(pbig, pbig.ap[:, 3:7, :])
                    qrT = sb("qrT", [128, 2, CH], BF16, PB)
                    qdf = sb("qdf", [128, 2, CH], BF16, PB)
                    qdb = sb("qdb", [128, 2, CH], BF16, PB)
                    krT = sb("krT", [128, 2, CH], BF16, PB)
                    kdfT = sb("kdfT", [128, 2, CH], BF16, PB)
                    kdftm = sb("kdftm", [128, 4, 2, 128], BF16, PB)
                    sg = sb("sg", [128, 4, CH], BF16, PB)
                    ATs = [sb("AT%d" % i, [128, 4, 128], BF16, PB) for i in range(2)]
                    Sfs = [sb("Sf%d" % i, [128, 2, 128], BF16, PB) for i in range(2)]
                    Rf = sb("Rf", [128, 2, 128], F32, PB)
                    mixBc = [sb("mixBc%d" % i, [128, 4, CH], BF16, PB) for i in range(1)]
                    mixB_slot = [K.slot() for _ in range(1)]
                    WS = [dict(sq=sq, lnv=lnv, rs=rs, t1=t1, t2=t2),
                          dict(sq=sb("wsq", [128, CH], BF16, PB), lnv=sb("wlnv", [128, CH], F32, PB),
                               rs=sb("wrs", [128, CH], F32, PB), t1=sb("wt1", [128, CH], F32, PB),
                               t2=sb("wt2", [128, CH], F32, PB))]
                    K.memset("dve", Rf, 0.0)
                    krl_slot = [K.slot() for _ in range(2)]
                    vbl_slot = K.slot()
                    load_xnT(xnT0_d, 0)
                    pairs = [(bk[0], bk[1]), (bk[3], bk[4]), (bk[5], bk[6])]
                    for C in range(NCH):
                        use_xnT(C)
                        if C + 1 < NCH:
                            load_xnT(xnT0_d, C + 1)
                        load_tab(tabd, tabd_slot, tabB_d, C)
                        handoff([po], bk[3:7])
                        ip = 0
                        for t in range(2):
                            w_ = WS[ip % 2]
                            pa_, pb_ = pairs[ip % 3]
                            ip += 1
                            proj(pa_, t * 128)
                            proj(pb_, 256 + t * 128)
                            K.tt("dve", w_["t1"], pa_, tabd[:, 0, :], ALU.mult)
                            K.tt("dve", w_["t2"], pb_, tabd[:, 1, :], ALU.mult)
                            K.tt("pool", w_["t1"], w_["t1"], w_["t2"], ALU.add)
                            K.cp("act", qrT[:, t, :], w_["t1"])
                            K.tt("pool", qdf[:, t, :], w_["t1"], QF4[:, t, :], ALU.mult)
                            K.tt("pool", qdb[:, t, :], w_["t1"], QB4[:, t, :], ALU.mult)
                        K.dma("sp", vbtm, vb_d[C], vbl_slot)
                        for t in range(2):
                            kr_t = WS[t]["t2"]
                            K.dma("sp", kr_t, kr_d[C][:, t, :], krl_slot[t])
                            K.cp("act", krT[:, t, :], kr_t)
                            K.tt("pool", kdfT[:, t, :], kr_t, KF4[:, t, :], ALU.mult)
                        for h in range(4):
                            pa_ = bk[3 + h]
                            proj(pa_, 512 + h * 128)
                            K.act(sg[:, h, :], pa_, AF.Silu)
                        for t in range(2):
                            for cj in range(4):
                                K.tr(pk[:, cj, t, :], kdfT[:, t, cj * 128:(cj + 1) * 128], ident)
                        K.cp("act", kdftm, pk)
                        handoff(bk[3:7], [po])
                        for cj in range(4):
                            n = C * 4 + cj
                            cs = slice(cj * 128, (cj + 1) * 128)
                            Sf = Sfs[cj % 2]
                            AT = ATs[cj % 2]
                            K.ts("dve", Sf, Rf, rfb[:, n:n + 1], ALU.mult)
                            for p in range(2):
                                K.mm(pa[0:64, p * 128:(p + 1) * 128], kdftm[:, cj, p, 0:64], vbtm[:, cj, (2 * p) * 128:(2 * p + 1) * 128])
                                K.mm(pa[64:128, p * 128:(p + 1) * 128], kdftm[:, cj, p, 64:128], vbtm[:, cj, (2 * p + 1) * 128:(2 * p + 2) * 128], tp=(0, 64))
                            for p in range(2):
                                K.ts("dve", Rf[:, p, :], Rf[:, p, :], cdr[:, p, n:n + 1], ALU.mult)
                                K.tt("dve", Rf[:, p, :], pa[:, p * 128:(p + 1) * 128], Rf[:, p, :], ALU.add)
                            for h in range(4):
                                t, r0 = h // 2, (h % 2) * 64
                                pdst = pss if (h % 2 == 0) else pb
                                K.mm(pdst[:, t * 128:(t + 1) * 128], krT[r0:r0 + 64, t, cs], qrT[r0:r0 + 64, t, cs])
                            ATv = AT.re("p (t hp) i -> p hp t i", hp=2)
                            DTv = DT.re("p (t hp) i -> p hp t i", hp=2)
                            K.tt("dve", ATv[:, 0, :, :], pss[:, 0:256].re("p (t i) -> p t i", t=2), DTv[:, 0, :, :], ALU.mult)
                            K.tt("dve", ATv[:, 1, :, :], pb[:, 0:256].re("p (t i) -> p t i", t=2), DTv[:, 1, :, :], ALU.mult)
                            for h in range(4):
                                t, r0 = h // 2, (h % 2) * 64
                                K.mm(po[:, h, cs], vbtm[:, cj, h * 128:(h + 1) * 128], AT[:, h, :], start=True, stop=False)
                                K.mm(po[:, h, cs], Sf[r0:r0 + 64, t, :], qdf[r0:r0 + 64, t, cs], start=False, stop=False)
                                K.mm(po[:, h, cs], SbAll[r0:r0 + 64, t, n, :], qdb[r0:r0 + 64, t, cs], start=False, stop=True)
                        mb = mixBc[0]
                        for h0 in (0, 2):
                            hs = (h0, h0 + 1)
                            for h in hs:
                                K.act(WS[h % 2]["sq"], po[:, h, :], AF.Square)
                            for h in hs:
                                K.mm(bk[h % 2], ones, WS[h % 2]["sq"])
                            for h in hs:
                                K.act(WS[h % 2]["lnv"], bk[h % 2], AF.Ln, bias=epsb[:, 0:1], scale=1.0 / 128.0)
                            for h in hs:
                                K.act(WS[h % 2]["rs"], WS[h % 2]["lnv"], AF.Exp, scale=-0.5)
                            for h in hs:
                                K.tt("dve", WS[h % 2]["t1"], po[:, h, :], WS[h % 2]["rs"], ALU.mult)
                                K.tt("pool", mb[:, h, :], WS[h % 2]["t1"], sg[:, h, :], ALU.mult)
                        K.dma("pool", V(mixb_d.ap[:, :, C * CH:(C + 1) * CH].rearrange("h p c -> p h c"), mixb_d.res), mb, mixB_slot[0])

                tap("mixb", mixb_d, [4, 128, NT], BF16)
                checkpoint("LB")
                LR.close()
                load_w(wbf, wLA_d, 1536, 0)
                handoff([wbf], [wlo, whi])
                wmode["split"] = True
                with Scope(K) as PA:
                    pbig = ps("pbig", [128, 8, CH], F32, PA)
                    psc = [subview(pbig, pbig.ap[:, 2 * i:2 * i + 2, :]) for i in range(3)]
                    pnum = subview(pbig, pbig.ap[:, 6, :])
                    pden = subview(pbig, pbig.ap[:, 7, :])
                    bk = [subview(pbig, pbig.ap[:, i, :]) for i in range(6)] + [pnum, pden]
                    WS = [dict(sq=sq, lnv=lnv, rs=rs, t1=t1, t2=t2),
                          dict(sq=sb("wsq", [128, CH], BF16, PA), lnv=sb("wlnv", [128, CH], F32, PA),
                               rs=sb("wrs", [128, CH], F32, PA), t1=sb("wt1", [128, CH], F32, PA),
                               t2=sb("wt2", [128, CH], F32, PA))]
                    qaT = sb("qaT", [128, 4, CH], BF16, PA)
                    sga = sb("sga", [128, 4, CH], BF16, PA)
                    mixAs = [sb("mixA%d" % i, [128, 4, CH], BF16, PA) for i in range(2)]
                    mixBls = [sb("mixBl%d" % i, [128, 4, CH], BF16, PA) for i in range(2)]
                    mixBl_slots = [K.slot() for _ in range(2)]
                    xblk = [sb("xblk%d" % i, [128, 1024], F32, PA) for i in range(2)]
                    xblk_slot = [K.slot() for _ in range(2)]
                    nxb = [0]
                    pTs = [sb("pTs%d" % i, [128, 2, CH], BF16, PA) for i in range(3)]
                    x1b = [sb("x1b%d" % i, [128, 1024], F32, PA) for i in range(2)]
                    dcp = sb("dcp", [128, CH], F32, PA)
                    ncp = sb("ncp", [128, CH], F32, PA)
                    x1b_slot = [K.slot() for _ in range(2)]
                    qaTs = [qaT, sb("qaT1", [128, 4, CH], BF16, PA)]
                    sgas = [sga, sb("sga1", [128, 4, CH], BF16, PA)]
                    tabcs = [tabc, sb("tabc1", [128, 2, CH], F32, PA)]
                    tabgs = [tabg, sb("tabg1", [128, 2, CH], F32, PA)]
                    tabsl = [tab_slot, K.slot()]
                    nbuf = [0]
                    npt = [0]

                    held = set()

                    def take_buf():
                        while True:
                            i_ = nbuf[0] % 3
                            nbuf[0] += 1
                            if i_ not in held:
                                return psc[i_]

                    def hold(b_):
                        held.add(psc.index(b_))

                    def release(b_):
                        held.discard(psc.index(b_))

                    def projx(dst, c0, xT):
                        with K.pe_batch():
                            for kc in range(8):
                                K.mm(dst, wcol(kc, c0), xT[:, kc, :], start=(kc == 0), stop=(kc == 7))

                    def proj_items(Cn):
                        q_, g_ = qaTs[Cn % 2], sgas[Cn % 2]
                        tc_, tg_ = tabcs[Cn % 2], tabgs[Cn % 2]
                        xT = xnTs[Cn % 2]

                        def prep():
                            load_tab(tc_, tabsl[Cn % 2], tabA_d, Cn)
                            K.amul(tg_[:, 0, :], tc_[:, 0, :], gqk[:, 0:1])
                            K.amul(tg_[:, 1, :], tc_[:, 1, :], gqk[:, 1:2])

                        items = []
                        for t in range(4):
                            def mk(t=t):
                                st = {}
                                w_ = WS[t % 2]

                                def s1():
                                    st["buf"] = take_buf()
                                    hold(st["buf"])
                                    projx(st["buf"][:, 0, :], t * 128, xT)
                                    projx(st["buf"][:, 1, :], 512 + t * 128, xT)

                                def s2():
                                    pa_, pb_ = st["buf"][:, 0, :], st["buf"][:, 1, :]
                                    K.tt("dve", w_["t1"], pa_, tg_[:, 0, :], ALU.mult)
                                    K.tt("dve", w_["t2"], pb_, tg_[:, 1, :], ALU.mult)
                                    K.act(w_["sq"], pa_, AF.Square)

                                def s3():
                                    K.mm(st["buf"][:, 1, :], onesblk, w_["sq"])

                                def s4():
                                    K.act(w_["lnv"], st["buf"][:, 1, :], AF.Ln, bias=epsb[:, 0:1], scale=1.0 / 64.0)
                                    K.act(w_["rs"], w_["lnv"], AF.Exp, scale=-0.5)
                                    K.tt("pool", w_["t1"], w_["t1"], w_["t2"], ALU.add)
                                    K.tt("pool", q_[:, t, :], w_["t1"], w_["rs"], ALU.mult)
                                    release(st["buf"])
                                return [(s1, 3), (s2, 1), (s3, 2), (s4, 0)]
                            items.append(mk())
                        for t2_ in range(2):
                            def mk(t2_=t2_):
                                st = {}

                                def s1():
                                    st["buf"] = take_buf()
                                    hold(st["buf"])
                                    for j in range(2):
                                        projx(st["buf"][:, j, :], 1024 + (2 * t2_ + j) * 128, xT)

                                def s2():
                                    for j in range(2):
                                        K.act(WS[j]["t1"], st["buf"][:, j, :], AF.Tanh, scale=0.5)

                                def s3():
                                    for j in range(2):
                                        K.ts("dve", WS[j]["t1"], WS[j]["t1"], 0.5, ALU.mult, 0.5, ALU.add)
                                        K.tt("dve", g_[:, 2 * t2_ + j, :], st["buf"][:, j, :], WS[j]["t1"], ALU.mult)
                                    release(st["buf"])
                                return [(s1, 3), (s2, 1), (s3, 0)]
                            items.append(mk())
                        return prep, items

                    def outproj_items(Cc):
                        mA, mB = mixAs[Cc % 2], mixBls[Cc % 2]
                        items = []
                        for b in range(4):
                            def mk(b=b):
                                st = {}
                                bs = slice(b * 128, (b + 1) * 128)
                                r0 = Cc * CH + b * 128

                                def s1():
                                    i_ = nxb[0] % 2
                                    nxb[0] += 1
                                    st["i"] = i_
                                    K.dma("sp", xblk[i_], x_d[r0:r0 + 128, :], xblk_slot[i_])
                                    st["buf"] = take_buf()
                                    hold(st["buf"])
                                    py = st["buf"]
                                    for half in range(2):
                                        for f in range(8):
                                            src = mA[:, f, bs] if f < 4 else mB[:, f - 4, bs]
                                            K.mm(py[:, half, :], src, wobf[:, f, half * 512:(half + 1) * 512], start=(f == 0), stop=(f == 7))

                                def s2():
                                    xo = x1b[st["i"]]
                                    K.tt("dve", xo, st["buf"].re("p a c -> p (a c)"), xblk[st["i"]], ALU.add)
                                    K.dma("pool", x1_d[r0:r0 + 128, :], xo, x1b_slot[st["i"]])
                                    release(st["buf"])
                                return [(s1, 4), (s2, 0)]
                            items.append(mk())
                        return items

                    load_xnT(xnT0_d, 0)
                    prep0, items0 = proj_items(0)
                    prep0()
                    def run_interleaved(items_, width=2):
                        for g_ in range(0, len(items_), width):
                            grp = items_[g_:g_ + width]
                            for si_ in range(max(len(x_) for x_ in grp)):
                                for x_ in grp:
                                    if si_ < len(x_):
                                        x_[si_][0]()

                    run_interleaved(items0[0:4])
                    run_interleaved(items0[4:], width=1)
                    carry = []
                    for C in range(NCH):
                        use_xnT(C)
                        qaT_c, sga_c = qaTs[C % 2], sgas[C % 2]
                        mixA = mixAs[C % 2]
                        load_w_piece(whi, 0, wG1_d, C, 0, 384, 1)
                        pending = list(carry)
                        carry = []
                        if C + 1 < NCH:
                            load_xnT(xnT0_d, C + 1)
                            prepn, pitems = proj_items(C + 1)
                            prepn()
                            pending = pending + pitems
                        K.dma("pool", mixBls[C % 2], V(mixb_d.ap[:, :, C * CH:(C + 1) * CH].rearrange("h p c -> p h c"), mixb_d.res), mixBl_slots[C % 2])
                        nit = 0
                        active = [None]
                        for t in range(4):
                            kv = t // 2
                            fifo = []

                            def qk(kb_):
                                sc_ = take_buf()
                                fifo.append(sc_)
                                ks = slice(kb_ * 128, (kb_ + 1) * 128)
                                with K.pe_batch():
                                    K.mm(sc_[:, 0, :], KaT[0:64, kv, ks], qaT_c[0:64, t, :])
                                    K.mm(sc_[:, 1, :], KaT[64:128, kv, ks], qaT_c[64:128, t, :])

                            qk(0)
                            qk(1)
                            for kb in range(NB):
                                sc = fifo.pop(0)
                                pt = pTs[npt[0] % 3]
                                npt[0] += 1
                                nit += 1
                                K.act(pt, sc, AF.Exp, bias=maskA[:, C * NB + kb:C * NB + kb + 1], scale=0.125)
                                if kb + 2 < NB:
                                    qk(kb + 2)
                                if active[0] is None and pending and nit % 12 == 3:
                                    active[0] = [pending.pop(0), 0, nit]
                                if active[0] is not None and nit >= active[0][2]:
                                    stages_, si_, _due = active[0]
                                    fn_, delay_ = stages_[si_]
                                    fn_()
                                    if si_ + 1 < len(stages_):
                                        active[0] = [stages_, si_ + 1, nit + delay_]
                                    else:
                                        active[0] = None
                                st, sp_ = (kb == 0), (kb == NB - 1)
                                with K.pe_batch():
                                    K.mm(pnum[0:64, :], Va[:, kb, kv * 64:(kv + 1) * 64], pt[:, 0, :], start=st, stop=sp_)
                                    K.mm(pnum[64:128, :], Va[:, kb, kv * 64:(kv + 1) * 64], pt[:, 1, :], start=st, stop=sp_, tp=(0, 64))
                                    K.mm(pden[0:64, :], ones[:, 0:64], pt[:, 0, :], start=st, stop=sp_)
                                    K.mm(pden[64:128, :], ones[:, 0:64], pt[:, 1, :], start=st, stop=sp_, tp=(0, 64))
                            K.cp("dve", dcp, pden)
                            K.cp("dve", ncp, pnum)
                            K.recip(dcp, dcp)
                            K.tt("dve", ncp, ncp, dcp, ALU.mult)
                            K.tt("pool", mixA[:, t, :], ncp, sga_c[:, t, :], ALU.mult)
                        while active[0] is not None or pending:
                            if active[0] is None:
                                active[0] = [pending.pop(0), 0, 0]
                            stages_, si_, _due = active[0]
                            stages_[si_][0]()
                            active[0] = [stages_, si_ + 1, 0] if si_ + 1 < len(stages_) else None
                        carry = outproj_items(C)
                        if C == NCH - 1:
                            run_interleaved(carry)
                            carry = []

            tap("x1", x1_d, [NT, 1024], F32)
            checkpoint("LA")
            with Scope(K) as L1:
                KcT = sb("KcT", [128, 2, (NB + 2) * 128], BF16, L1)
                Vc = sb("Vc", [128, NB + 2, 128], BF16, L1)
                K.memset("pool", KcT[:, :, 0:128], 0.0)
                K.memset("pool", KcT[:, :, (NB + 1) * 128:(NB + 2) * 128], 0.0)
                K.memset("pool", Vc[:, 0, :], 0.0)
                K.memset("pool", Vc[:, NB + 1, :], 0.0)
                tap("EBT", EBT, [128, 16, 3, 128], BF16)
                checkpoint("EBT")
                wmode["base"] = 1536
                with Scope(K) as PG1:
                    pT = ps("pT", [128, 8, 128], BF16, PG1)
                    pa = ps("pa", [128, CH], F32, PG1)
                    pva = ps("pva", [128, 512], F32, PG1)[:, 0:128]
                    mxs = [alloc_mx(PG1), alloc_mx(PG1)]
                    pT2 = ps("pT2", [128, 8, 128], BF16, PG1)
                    pa2 = ps("pa2", [128, CH], F32, PG1)
                    pva2 = ps("pva2", [128, 512], F32, PG1)[:, 0:128]
                    xnT_front(mxs[0], x1_d, 0)
                    xnT_back(mxs[0], 0, [pT, pT2])
                    for C in range(NCH):
                        use_xnT(C)
                        store_xnT(xnT1_d, C, C % 2)
                        if C + 1 < NCH:
                            xnT_front(mxs[(C + 1) % 2], x1_d, C + 1)
                        load_w_piece(wlo, 0, wL1_d, C, 0, 1024, 1)
                        load_w_piece(wlo, 1024, wL1_d, C, 1024, 1536, 1)
                        load_w_piece(wobf, 0, woc_d, C, 0, 1024, None)
                        for t in range(2):
                            pa_ = pa if t == 0 else pa2
                            proj(pa_, t * 128)
                            K.cp("act" if t == 0 else "dve", KcT[:, t, (C * 4 + 1) * 128:(C * 4 + 5) * 128], pa_)
                        for b in range(4):
                            pv_ = pva if b % 2 == 0 else pva2
                            for kc in range(8):
                                K.mm(pv_, cur["xnT"][:, kc, b * 128:(b + 1) * 128], wcol(kc, 256), start=(kc == 0), stop=(kc == 7))
                            K.cp("dve" if b % 2 == 0 else "act", Vc[:, C * 4 + b + 1, :], pv_)
                        if C + 1 < NCH:
                            xnT_back(mxs[(C + 1) % 2], C + 1, [pT, pT2])

                tap("KcT", KcT, [128, 2, (NB + 2) * 128], BF16)
                tap("Vc", Vc, [128, NB + 2, 128], BF16)
                checkpoint("G1")
                wmode["base"] = 0
                for kc_ in range(8):
                    load_w_piece(whi, 0, wL1_d, kc_, 1536, 2048, 1)
                with Scope(K) as PL1:
                    pbig = ps("pbig1", [128, 8, CH], F32, PL1)
                    pw = [subview(pbig, pbig.ap[:, 2 * i:2 * i + 2, :]) for i in range(3)]
                    pnum = subview(pbig, pbig.ap[:, 6, :])
                    pden = subview(pbig, pbig.ap[:, 7, :])
                    bk = [subview(pbig, pbig.ap[:, i, :]) for i in range(6)]
                    qcT = sb("qcT", [128, 8, CH], BF16, PL1)
                    sgc = sb("sgc", [128, 8, CH], BF16, PL1)
                    mixC = sb("mixC", [128, 8, CH], BF16, PL1)
                    pws = [sb("pws%d" % i, [128, 2, 3, 128], BF16, PL1) for i in range(3)]
                    pw2 = [sb("pw2%d" % i, [128, 2, 3, 128], BF16, PL1) for i in range(3)]
                    rs = sb("rs1", [128, CH], F32, PL1)
                    lnr = sb("lnr", [128, CH], F32, PL1)
                    t1 = sb("t11", [128, CH], F32, PL1)
                    EP = [dict(rs=rs, t1=t1, lnr=lnr),
                          dict(rs=sb("rs1b", [128, CH], F32, PL1), t1=sb("t11b", [128, CH], F32, PL1), lnr=sb("lnrb", [128, CH], F32, PL1))]
                    pnums = [pnum, pnum]
                    pdens = [pden, pden]
                    x2 = [sb("x2%d" % i, [128, 1024], F32, PL1) for i in range(2)]
                    yo = [sb("yo%d" % i, [128, 1024], F32, PL1) for i in range(2)]
                    yo_slot = [K.slot() for _ in range(2)]
                    ss2 = sb("ss2", [128, 2], F32, PL1)
                    ln2 = sb("ln2", [128, 2], F32, PL1)
                    r2 = sb("r2", [128, 2], F32, PL1)
                    it = 0
                    mxl = alloc_mx(PL1, full=False)
                    xch = mxl.xch
                    junk = sb("junk1", [128, 1024], BF16, PL1)
                    fnbc = sb("fnbc", [128, 1024], F32, PL1)
                    fn_slot = K.slot()
                    K.dma("sp", fnbc, V(fn_d.ap.to_broadcast([128, 1024]), fn_d.res), fn_slot)
                    load_xnT(xnT1_d, 0)
                    for C in range(NCH):
                        use_xnT(C)
                        if C + 1 < NCH:
                            load_xnT(xnT1_d, C + 1)
                        load_x(mxl, x1_d, C)
                        handoff(pw, bk[0:6])
                        for t in range(8):
                            pa_ = bk[t % 6]
                            proj(pa_, t * 128)
                            K.cp("act" if t % 2 == 0 else "dve", qcT[:, t, :], pa_)
                        for t in range(8):
                            pa_ = bk[(t + 2) % 6]
                            proj(pa_, 1024 + t * 128)
                            K.act(sgc[:, t, :], pa_, AF.Silu)
                        handoff(bk[0:6], pw)
                        items = [(t, qi) for t in range(8) for qi in range(4)]

                        def wqk(t, qi, w):
                            kv = t // 4
                            i = C * 4 + qi
                            qs = slice(qi * 128, (qi + 1) * 128)
                            with K.pe_batch():
                                for o in range(3):
                                    sl = 2 - o
                                    ks = slice((i + o) * 128, (i + o + 1) * 128)
                                    K.mm(w[:, 0, sl * 128:(sl + 1) * 128], KcT[0:64, kv, ks], qcT[0:64, t, qs])
                                    K.mm(w[:, 1, sl * 128:(sl + 1) * 128], KcT[64:128, kv, ks], qcT[64:128, t, qs])

                        wqk(items[0][0], items[0][1], pw[it % 3])
                        wqk(items[1][0], items[1][1], pw[(it + 1) % 3])
                        deferred = []
                        for idx, (t, qi) in enumerate(items):
                            kv = t // 4
                            i = C * 4 + qi
                            qs = slice(qi * 128, (qi + 1) * 128)
                            w = pw[it % 3]
                            s1 = pws[it % 3]
                            s2 = pw2[it % 3]
                            it += 1
                            if i in (0, NB // 2 - 1, NB // 2, NB - 1):
                                for o in range(3):
                                    sl = 2 - o
                                    K.act(s1[:, :, sl, :], w[:, :, sl * 128:(sl + 1) * 128], AF.Exp, bias=maskW[:, i * 3 + o:i * 3 + o + 1], scale=0.125)
                            else:
                                K.act(s1, w[:, :, 0:384].re("p h (o q) -> p h o q", o=3), AF.Exp, scale=0.125)
                            K.tt("dve", s2, s1, EBT[:, 2 * t:2 * t + 2, :, :], ALU.mult)
                            if deferred:
                                deferred.pop(0)()
                            if idx + 2 < len(items):
                                wqk(items[idx + 2][0], items[idx + 2][1], pw[(it + 1) % 3])
                            pnum_, pden_ = pnums[t % 2], pdens[t % 2]
                            with K.pe_batch():
                                for o in range(3):
                                    sl = 2 - o
                                    st, sp_ = (o == 0), (o == 2)
                                    vv = Vc[:, i + o, kv * 64:(kv + 1) * 64]
                                    K.mm(pnum_[0:64, qs], vv, s2[:, 0, sl, :], start=st, stop=sp_)
                                    K.mm(pnum_[64:128, qs], vv, s2[:, 1, sl, :], start=st, stop=sp_, tp=(0, 64))
                                    K.mm(pden_[0:64, qs], ones[:, 0:64], s2[:, 0, sl, :], start=st, stop=sp_)
                                    K.mm(pden_[64:128, qs], ones[:, 0:64], s2[:, 1, sl, :], start=st, stop=sp_, tp=(0, 64))
                            if qi == 3:
                                def epi(t=t, pnum_=pnum_, pden_=pden_):
                                    e_ = EP[t % 2]
                                    K.ts("dve", e_["rs"], pden_, esk[:, t:t + 1], ALU.add)
                                    K.cp("dve", e_["t1"], pnum_)
                                    K.act(e_["lnr"], e_["rs"], AF.Ln)
                                    K.act(e_["rs"], e_["lnr"], AF.Exp, scale=-1.0)
                                    K.tt("pool", e_["t1"], e_["t1"], e_["rs"], ALU.mult)
                                    K.tt("pool", mixC[:, t, :], e_["t1"], sgc[:, t, :], ALU.mult)
                                deferred.append(epi)
                        while deferred:
                            deferred.pop(0)()
                        for b in range(4):
                            bs = slice(b * 128, (b + 1) * 128)
                            py = pw[b % 2]
                            for half in range(2):
                                for f in range(8):
                                    K.mm(py[:, half, :], mixC[:, f, bs], wobf[:, f, half * 512:(half + 1) * 512], start=(f == 0), stop=(f == 7))
                            xo = x2[b % 2]
                            K.tt("dve", xo, py.re("p a c -> p (a c)"), xch[:, b, :], ALU.add)
                            K.act(junk, xo, AF.Square, accum=ss2[:, b % 2:b % 2 + 1])
                            K.act(ln2[:, b % 2:b % 2 + 1], ss2[:, b % 2:b % 2 + 1], AF.Ln, bias=epsb[:, 0:1], scale=1.0 / 1024.0)
                            K.act(r2[:, b % 2:b % 2 + 1], ln2[:, b % 2:b % 2 + 1], AF.Exp, scale=-0.5)
                            yb = yo[b % 2]
                            K.ts("dve", yb, xo, r2[:, b % 2:b % 2 + 1], ALU.mult)
                            K.tt("pool", yb, yb, fnbc, ALU.mult)
                            r0 = C * CH + b * 128
                            K.dma("pool", y_d[r0:r0 + 128, :], yb, yo_slot[b % 2])
    except StopBuild:
        pass
    for s_ in K.slots:
        if s_.cnt:
            nc.gpsimd.wait_ge(s_.sem, s_.cnt)
    return nc, K


def _t5_bucket(rel):
    half = 16
    max_exact = 8
    ret = (rel > 0).astype(np.int32) * half
    dist = np.abs(rel)
    large = max_exact + (np.log(np.maximum(dist, 1) / max_exact) / np.log(128 / max_exact) * (half - max_exact)).astype(np.int32)
    large = np.minimum(large, half - 1)
    return ret + np.where(dist < max_exact, dist, large)


def _static_tables():
    f32 = np.float32
    st = {}
    st["ident"] = np.eye(128, dtype=f32)
    st["aident"] = np.ascontiguousarray(np.eye(128, dtype=f32)[::-1])
    ob = np.zeros((128, 128), f32)
    ob[:64, :64] = 1
    ob[64:, 64:] = 1
    st["onesblk"] = ob
    j = np.arange(128)[:, None]
    i = np.arange(128)[None, :]
    mmat = np.zeros((128, 4, 128), f32)
    mmat[:, 0, :] = np.maximum(i - j, 0)
    mmat[:, 1, :] = (i >= j)
    mmat[:, 2, :] = np.maximum(j - i, 0)
    mmat[:, 3, :] = (j > i)
    st["mmat"] = mmat
    c = np.arange(512) % 128
    iot = np.zeros((128, 4, 512), f32)
    iot[:, 0, :] = c + 1
    iot[:, 1, :] = 128 - c
    iot[:, 2, :] = 127 - c
    iot[:, 3, :] = c
    st["iot"] = iot
    m = np.arange(640)
    rel = 255 - m
    bk = _t5_bucket(rel)
    oh = np.zeros((32, 640), f32)
    oh[bk, m] = 1
    st["oh"] = oh
    st["inwin"] = np.broadcast_to((np.abs(rel) <= 128).astype(f32)[None, :], (16, 640)).copy()
    return st


def _core_tables(is_prompt):
    f32 = np.float32
    seqlen = 4096 if is_prompt else 2048
    t = np.arange(NT) % seqlen
    d = np.arange(128) % 64
    pair = d // 2
    sgn = np.where(d % 2 == 0, -1.0, 1.0)
    quarter = 16
    freqs = (np.float32(10000.0) ** (-np.arange(quarter, dtype=f32) / quarter)).astype(f32)
    row = (t // 64).astype(f32)
    col = (t % 64).astype(f32)
    ang = np.concatenate([row[:, None] * freqs, col[:, None] * freqs], axis=-1).astype(f32)
    angd = ang[:, pair].T.astype(np.float64)
    tabA = np.stack([np.cos(angd), np.sin(angd) * sgn[:, None]]).astype(f32)
    half = 32
    freqs_b = (np.float32(10000.0) ** (-np.arange(half, dtype=f32) / half)).astype(f32)
    angb = (t.astype(f32)[:, None] * freqs_b).astype(f32)
    angbd = angb[:, pair].T.astype(np.float64)
    tabB = np.stack([np.cos(angbd), np.sin(angbd) * sgn[:, None]]).astype(f32)
    seq_of_blk = (np.arange(NB) * 128) // seqlen
    maskA = np.zeros((NCH, NB), f32)
    for C in range(NCH):
        sq = (C * CH) // seqlen
        maskA[C, :] = np.where(seq_of_blk == sq, 0.0, NEG)
    maskA = np.broadcast_to(maskA.reshape(1, -1), (128, NCH * NB)).copy()
    maskW = np.zeros((NB, 3), f32)
    for i in range(NB):
        for o in range(3):
            jb = i + o - 1
            if jb < 0 or jb >= NB or seq_of_blk[jb] != seq_of_blk[i]:
                maskW[i, o] = NEG
    maskW = np.broadcast_to(maskW.reshape(1, -1), (128, NB * 3)).copy()
    cps = seqlen // 128
    rf = np.array([0.0 if (n % cps == 0) else 1.0 for n in range(NB)], f32)
    rb = np.array([0.0 if (n % cps == cps - 1) else 1.0 for n in range(NB)], f32)
    rfb = np.broadcast_to(np.concatenate([rf, rb])[None, :], (128, 64)).copy()
    return {"tabA": tabA, "tabB": tabB, "maskA": maskA, "maskW": maskW, "rfb": rfb}


def _swap(cols):
    cols = np.asarray(cols)
    return cols ^ 1


def _prep_common(norm_g, w_in_ab, qk_norm_a, ret_decay, w_out_ab, w_in_c, sink_c, w_out_c, rel_bias, final_norm):
    f32 = np.float32
    W = np.asarray(w_in_ab[0], f32)
    qa = np.arange(0, 512)
    ka = np.arange(512, 640)
    va = np.arange(640, 768)
    ga = np.arange(768, 1280)
    qb = np.arange(1280, 1536)
    kb = np.arange(1536, 1792)
    vb = np.arange(1792, 2304)
    gb = np.arange(2304, 2816)
    kadup = np.concatenate([ka[0:64], ka[0:64], ka[64:128], ka[64:128]])
    cm = {}
    cm["wG"] = np.ascontiguousarray(W[:, np.concatenate([kadup, _swap(kadup), kb, _swap(kb), va, vb])])
    cm["wLB"] = np.ascontiguousarray(W[:, np.concatenate([qb, _swap(qb), gb])])
    cm["wLA"] = np.ascontiguousarray(W[:, np.concatenate([qa, _swap(qa), ga])])
    cm["woab"] = np.ascontiguousarray(np.asarray(w_out_ab[0], f32))
    Wc = np.asarray(w_in_c[0], f32)
    kc = np.arange(1024, 1152)
    kcdup = np.concatenate([kc[0:64], kc[0:64], kc[64:128], kc[64:128]])
    cm["wG1"] = np.ascontiguousarray(Wc[:, np.concatenate([kcdup, np.arange(1152, 1280)])])
    cm["wL1"] = np.ascontiguousarray(Wc[:, np.concatenate([np.arange(0, 1024), np.arange(1280, 2304)])])
    cm["woc"] = np.ascontiguousarray(np.asarray(w_out_c[0], f32))
    ng = np.asarray(norm_g, f32)
    cm["gcol"] = np.ascontiguousarray(ng.reshape(2, 8, 128).transpose(2, 0, 1).reshape(128, 16))
    cm["fn"] = np.asarray(final_norm, f32).reshape(1, 1024).copy()
    g = np.asarray(qk_norm_a[0], f32)
    d = np.arange(128) % 64
    cm["gqk"] = np.stack([g[0][d], g[0][d ^ 1], g[1][d], g[1][d ^ 1]], axis=1).astype(f32).copy()
    rd = np.asarray(ret_decay[0], f32)
    hp = (np.arange(128) // 64)
    rdec = np.zeros((128, 12), f32)
    for p in range(2):
        rdec[:, p] = rd[0][2 * p + hp]
        rdec[:, 2 + p] = rd[1][2 * p + hp]
    for h in range(4):
        rdec[:, 4 + h] = rd[0][h]
        rdec[:, 8 + h] = rd[1][h]
    cm["rdec"] = rdec
    sk = np.asarray(sink_c[0], f32)
    sinkl = np.zeros((128, 8), f32)
    for t in range(8):
        sinkl[:, t] = sk[2 * t + hp]
    cm["sinkl"] = sinkl
    cm["relb"] = np.ascontiguousarray(np.asarray(rel_bias, f32))
    cm.update(_static_tables())
    return cm


_CACHE = {}


def kernel(x_prompt, x_sample, norm_g, w_in_ab, qk_norm_a, ret_decay, w_out_ab, w_in_c, sink_c, w_out_c, rel_bias, final_norm):
    xp = np.asarray(x_prompt, np.float32)
    xs = np.asarray(x_sample, np.float32)
    cm = _prep_common(norm_g, w_in_ab, qk_norm_a, ret_decay, w_out_ab, w_in_c, sink_c, w_out_c, rel_bias, final_norm)
    tp = _core_tables(True)
    tsm = _core_tables(False)
    in_maps = []
    for c in range(8):
        m = dict(cm)
        if c < 4:
            m["x"] = np.ascontiguousarray(xp[c])
            m.update(tp)
        else:
            m["x"] = np.ascontiguousarray(xs[2 * (c - 4):2 * (c - 4) + 2].reshape(NT, 1024))
            m.update(tsm)
        in_maps.append(m)
    if "nc" not in _CACHE:
        _CACHE["nc"] = build_program()[0]
    nc = _CACHE["nc"]
    res = run_bass_kernel_spmd(nc, in_maps, core_ids=list(range(8)))
    outs = [np.asarray(r["y"], np.float32) for r in res.results]
    y_prompt = np.stack(outs[0:4], axis=0)
    y_sample = np.stack(outs[4:8], axis=0).reshape(8, 2048, 1024)
    return (y_prompt, y_sample)
```
